# Optimizing a Trainium2 kernel written in Bass

```python
import math
import jax
import jax.numpy as jnp
from jax import lax
import numpy as np


D_MODEL = 1024
BATCH = 8
SEQ = 4096
DEPTH = 2

GRID_W = 64
CTX_LEN = 256
EPS = 1e-6
ROPE_BASE = 10000.0

BR_WIDTH = D_MODEL // 2
A_DK = 128
A_DV = 128
A_HEADS = BR_WIDTH // A_DK
A_WIDTH = A_HEADS * A_DV
A_CONV = 5
A_CHUNK = 64
B_HD = 64
B_Q_HEADS = BR_WIDTH // B_HD
B_KV_HEADS = B_Q_HEADS // 4
B_WIDTH = B_Q_HEADS * B_HD
WINDOW = 128
B_BLOCK = 128
C_HD = 128
C_HEADS = BR_WIDTH // C_HD
C_WIDTH = C_HEADS * C_HD
C_CHUNK = 64
N_BRANCH = 3

SPLIT_SIZES = (3 * A_WIDTH, A_WIDTH, 2 * A_HEADS, 2 * A_HEADS,
               B_Q_HEADS * B_HD, 2 * B_KV_HEADS * B_HD, B_WIDTH,
               3 * C_WIDTH, C_WIDTH, N_BRANCH * D_MODEL)
SPLIT_POINTS = tuple(int(v) for v in np.cumsum(SPLIT_SIZES)[:-1])
IN_WIDTH = int(sum(SPLIT_SIZES))

kernel_name = 'hybrid_delta_window_retention_dit'


def rms_norm(x, w):
    xf = x.astype(jnp.float32)
    y = xf * lax.rsqrt(jnp.mean(xf * xf, axis=-1, keepdims=True) + EPS)
    return (y * w.astype(jnp.float32)).astype(x.dtype)


def l2_normalize(x):
    xf = x.astype(jnp.float32)
    return xf * lax.rsqrt(jnp.sum(xf * xf, axis=-1, keepdims=True) + EPS)


def to_heads(x, h):
    b, t, _ = x.shape
    return x.reshape(b, t, h, -1).transpose(0, 2, 1, 3)


def from_heads(x):
    b, h, t, d = x.shape
    return x.transpose(0, 2, 1, 3).reshape(b, t, h * d)


def flip_t(a):
    return jnp.flip(a, axis=2)


def rope_angles(pos, n_freq):
    inv = ROPE_BASE ** (-jnp.arange(n_freq, dtype=jnp.float32) / n_freq)
    return pos[:, None] * inv[None, :]


def apply_rot(x, ang):
    x1, x2 = jnp.split(x, 2, axis=-1)
    cos = jnp.cos(ang).astype(x.dtype)
    sin = jnp.sin(ang).astype(x.dtype)
    return jnp.concatenate([x1 * cos - x2 * sin, x1 * sin + x2 * cos], axis=-1)


def apply_axial(x, ang_r, ang_c):
    half = x.shape[-1] // 2
    return jnp.concatenate([apply_rot(x[..., :half], ang_r), apply_rot(x[..., half:], ang_c)], axis=-1)


def short_conv(x, w):
    k = w.shape[0]
    p = k // 2
    return lax.conv_general_dilated(x, w[:, None, :].astype(x.dtype), window_strides=(1,),
                                    padding=[(p, p)], dimension_numbers=('NWC', 'WIO', 'NWC'),
                                    feature_group_count=x.shape[-1])


def gated_delta_chunk(q, k, v, g, beta, s0):
    b, h, t, dk = q.shape
    dv = v.shape[-1]
    c = A_CHUNK
    n = t // c
    q = q.reshape(b, h, n, c, dk)
    k = k.reshape(b, h, n, c, dk)
    v = v.reshape(b, h, n, c, dv)
    g = jnp.cumsum(g.reshape(b, h, n, c), axis=-1)
    beta = beta.reshape(b, h, n, c)
    incl = jnp.tril(jnp.ones((c, c), dtype=bool))
    strict = jnp.tril(jnp.ones((c, c), dtype=bool), -1)
    decay = jnp.exp(jnp.where(incl, g[..., :, None] - g[..., None, :], -jnp.inf))
    kb = k * beta[..., None]
    lmat = jnp.where(strict, jnp.einsum('bhnid,bhnjd->bhnij', kb, k) * decay, 0.0)
    eye = jnp.eye(c, dtype=jnp.float32)
    tinv = lax.linalg.triangular_solve(lmat + eye, jnp.broadcast_to(eye, lmat.shape),
                                       left_side=True, lower=True)
    u = jnp.einsum('bhnij,bhnjd->bhnid', tinv, v * beta[..., None])
    w = jnp.einsum('bhnij,bhnjd->bhnid', tinv, kb * jnp.exp(g)[..., None])
    qk = jnp.einsum('bhnid,bhnjd->bhnij', q, k) * decay
    q_dec = q * jnp.exp(g)[..., None]
    g_last = g[..., -1]
    k_tail = k * jnp.exp(g_last[..., None] - g)[..., None]

    def step(state, inp):
        u_n, w_n, qk_n, qd_n, kt_n, gl_n = inp
        v_new = u_n - jnp.einsum('bhcd,bhde->bhce', w_n, state)
        o_n = jnp.einsum('bhcd,bhde->bhce', qd_n, state) + jnp.einsum('bhij,bhje->bhie', qk_n, v_new)
        state = state * jnp.exp(gl_n)[..., None, None] + jnp.einsum('bhcd,bhce->bhde', kt_n, v_new)
        return state, o_n

    mv = lambda a: jnp.moveaxis(a, 2, 0)
    s_fin, o = lax.scan(step, s0, (mv(u), mv(w), mv(qk), mv(q_dec), mv(k_tail), mv(g_last)))
    return jnp.moveaxis(o, 0, 2).reshape(b, h, t, dv), s_fin


def delta_prep(qkv, b_raw, a_raw, conv_w, a_log, dt_bias):
    qkv = jax.nn.silu(short_conv(qkv, conv_w))
    q, k, v = jnp.split(qkv, 3, axis=-1)
    q = l2_normalize(to_heads(q, A_HEADS)) * (A_DK ** -0.5)
    k = l2_normalize(to_heads(k, A_HEADS))
    v = to_heads(v, A_HEADS).astype(jnp.float32)
    beta = jax.nn.sigmoid(b_raw.astype(jnp.float32)).transpose(0, 2, 1)
    g = -jnp.exp(a_log.astype(jnp.float32))[None, :, None] * jax.nn.softplus(
        (a_raw.astype(jnp.float32) + dt_bias.astype(jnp.float32)).transpose(0, 2, 1))
    return q, k, v, beta, g


def bidir_delta(q, k, v, beta, g, s0_f, s0_b):
    h = A_HEADS
    o_f, s_f = gated_delta_chunk(q, k, v, g[:, :h], beta[:, :h], s0_f)
    o_b, s_b = gated_delta_chunk(flip_t(q), flip_t(k), flip_t(v), flip_t(g[:, h:]), flip_t(beta[:, h:]), s0_b)
    return o_f + flip_t(o_b), s_f, s_b


def gated_head_rmsnorm(o, z, w):
    o = o * lax.rsqrt(jnp.mean(o * o, axis=-1, keepdims=True) + EPS) * w.astype(jnp.float32)
    return (from_heads(o) * jax.nn.silu(z.astype(jnp.float32))).astype(z.dtype)


def window_attention(q, k, v, k_ctx, v_ctx, sink):
    b, t, hq, dh = q.shape
    hkv = k.shape[2]
    grp = hq // hkv
    blk = B_BLOCK
    nb = t // blk
    l = k_ctx.shape[1]
    scale = dh ** -0.5
    qb = q.reshape(b, nb, blk, hkv, grp, dh)
    pad = ((0, 0), (blk, blk), (0, 0), (0, 0))
    kp = jnp.pad(k, pad).reshape(b, nb + 2, blk, hkv, dh)
    vp = jnp.pad(v, pad).reshape(b, nb + 2, blk, hkv, dh)
    kw = jnp.concatenate([kp[:, :-2], kp[:, 1:-1], kp[:, 2:]], axis=2)
    vw = jnp.concatenate([vp[:, :-2], vp[:, 1:-1], vp[:, 2:]], axis=2)
    s_win = jnp.einsum('bnqhgd,bnkhd->bnhgqk', qb, kw).astype(jnp.float32) * scale
    s_ctx = jnp.einsum('bnqhgd,blhd->bnhgql', qb, k_ctx).astype(jnp.float32) * scale
    rel = jnp.arange(3 * blk)[None, :] - blk - jnp.arange(blk)[:, None]
    kpos = jnp.arange(nb)[:, None] * blk - blk + jnp.arange(3 * blk)[None, :]
    valid = (jnp.abs(rel) <= WINDOW)[None] & ((kpos >= 0) & (kpos < t))[:, None, :]
    s_win = jnp.where(valid[None, :, None, None], s_win, -jnp.inf)
    sink_s = jnp.broadcast_to(sink.astype(jnp.float32).reshape(1, 1, hkv, grp, 1, 1), s_win.shape[:-1] + (1,))
    p = jax.nn.softmax(jnp.concatenate([sink_s, s_ctx, s_win], axis=-1), axis=-1)
    p_ctx = p[..., 1:1 + l].astype(v.dtype)
    p_win = p[..., 1 + l:].astype(v.dtype)
    o = (jnp.einsum('bnhgql,blhd->bnqhgd', p_ctx, v_ctx)
         + jnp.einsum('bnhgqk,bnkhd->bnqhgd', p_win, vw))
    return o.reshape(b, t, hq * dh)


def context_attention(q, k, v, sink):
    b, l, hq, dh = q.shape
    hkv = k.shape[2]
    grp = hq // hkv
    qg = q.reshape(b, l, hkv, grp, dh)
    s = jnp.einsum('bqhgd,bkhd->bhgqk', qg, k).astype(jnp.float32) * (dh ** -0.5)
    sink_s = jnp.broadcast_to(sink.astype(jnp.float32).reshape(1, hkv, grp, 1, 1), s.shape[:-1] + (1,))
    p = jax.nn.softmax(jnp.concatenate([sink_s, s], axis=-1), axis=-1)[..., 1:]
    o = jnp.einsum('bhgqk,bkhd->bqhgd', p.astype(v.dtype), v)
    return o.reshape(b, l, hq * dh)


def retention_chunk(q, k, v, log_gamma, s0):
    b, h, t, dk = q.shape
    dv = v.shape[-1]
    c = C_CHUNK
    n = t // c
    q = q.reshape(b, h, n, c, dk)
    k = k.reshape(b, h, n, c, dk)
    v = v.reshape(b, h, n, c, dv)
    idx = jnp.arange(c, dtype=jnp.float32)
    lg = log_gamma.astype(jnp.float32)[:, None]
    incl = jnp.tril(jnp.ones((c, c), dtype=bool))
    dmask = jnp.exp(jnp.where(incl, (idx[:, None] - idx[None, :]) * lg[:, :, None], -jnp.inf))
    o_inner = jnp.einsum('bhnij,bhnje->bhnie',
                         jnp.einsum('bhnid,bhnjd->bhnij', q, k) * dmask[None, :, None], v)
    q_dec = q * jnp.exp((idx + 1.0) * lg)[None, :, None, :, None]
    k_dec = k * jnp.exp((c - 1.0 - idx) * lg)[None, :, None, :, None]
    chunk_dec = jnp.exp(c * lg)[None, :, :, None]

    def step(state, inp):
        qd_n, kd_n, v_n = inp
        o_n = jnp.einsum('bhcd,bhde->bhce', qd_n, state)
        state = state * chunk_dec + jnp.einsum('bhcd,bhce->bhde', kd_n, v_n)
        return state, o_n

    mv = lambda a: jnp.moveaxis(a, 2, 0)
    s_fin, o_cross = lax.scan(step, s0, (mv(q_dec), mv(k_dec), mv(v)))
    o = o_inner + jnp.moveaxis(o_cross, 0, 2)
    return o.reshape(b, h, t, dv), s_fin


def retention_prep(qkv, ang):
    q, k, v = jnp.split(qkv, 3, axis=-1)
    q = to_heads(q, C_HEADS).astype(jnp.float32)
    k = to_heads(k, C_HEADS).astype(jnp.float32) * (C_HD ** -0.5)
    v = to_heads(v, C_HEADS).astype(jnp.float32)
    if ang is not None:
        q = apply_rot(q, ang)
        k = apply_rot(k, ang)
    return q, k, v


def bidir_retention(q, k, v, log_gamma, s0_f, s0_b):
    h = C_HEADS
    o_f, s_f = retention_chunk(q, k, v, log_gamma[:h], s0_f)
    o_b, s_b = retention_chunk(flip_t(q), flip_t(k), flip_t(v), log_gamma[h:], s0_b)
    return o_f + flip_t(o_b), s_f, s_b


def retention_out(o, z, w):
    mu = jnp.mean(o, axis=-1, keepdims=True)
    var = jnp.mean(jnp.square(o - mu), axis=-1, keepdims=True)
    o = from_heads((o - mu) * lax.rsqrt(var + EPS)) * w.astype(jnp.float32)
    return (o * jax.nn.silu(z.astype(jnp.float32))).astype(z.dtype)


def merge_branches(y_a, y_b, y_c, logits, w_br, w_o):
    g_a, g_b, g_c = jnp.split(jax.nn.sigmoid(logits), 3, axis=-1)
    merged = g_a * (y_a @ w_br[0]) + g_b * (y_b @ w_br[1]) + g_c * (y_c @ w_br[2])
    return merged @ w_o


def setup_inputs(seed: int = 0) -> dict:
    key = jax.random.key(seed)
    ks = jax.random.split(key, 18)
    f32 = jnp.float32

    def nrm(k, shape, s):
        return jax.random.normal(k, shape, f32) * s

    x = nrm(ks[0], (BATCH, SEQ, D_MODEL), 1.0)
    c = nrm(ks[1], (BATCH, D_MODEL), 1.0)
    ctx = nrm(ks[2], (BATCH, CTX_LEN, D_MODEL), 1.0)
    c_ctx = nrm(ks[3], (D_MODEL,), 1.0)
    w_ada = nrm(ks[4], (DEPTH, D_MODEL, 3 * D_MODEL), 0.5 * D_MODEL ** -0.5)
    b_ada = nrm(ks[5], (DEPTH, 3 * D_MODEL), 0.02)
    norm_w = 1.0 + nrm(ks[6], (DEPTH, D_MODEL), 0.02)
    w_in = nrm(ks[7], (DEPTH, D_MODEL, IN_WIDTH), D_MODEL ** -0.5)
    a_conv_w = nrm(ks[8], (DEPTH, A_CONV, 3 * A_WIDTH), A_CONV ** -0.5)
    a_log = jnp.log(jax.random.uniform(ks[9], (DEPTH, 2 * A_HEADS), f32, 1.0, 16.0))
    dt = jnp.exp(jax.random.uniform(ks[10], (DEPTH, 2 * A_HEADS), f32, math.log(1e-3), math.log(1e-1)))
    a_dt_bias = dt + jnp.log(-jnp.expm1(-dt))
    a_norm_w = 1.0 + nrm(ks[11], (DEPTH, A_DV), 0.02)
    b_sink = nrm(ks[12], (DEPTH, B_Q_HEADS), 0.5)
    gam = 1.0 - 2.0 ** (-5.0 - np.arange(C_HEADS, dtype=np.float32))
    base = jnp.asarray(np.log(gam) - np.log1p(-gam), f32)
    c_decay = jnp.tile(base, 2)[None, :] + nrm(ks[13], (DEPTH, 2 * C_HEADS), 0.1)
    c_norm_w = 1.0 + nrm(ks[14], (DEPTH, C_WIDTH), 0.02)
    w_branch = nrm(ks[15], (DEPTH, N_BRANCH, BR_WIDTH, D_MODEL), BR_WIDTH ** -0.5)
    w_out = nrm(ks[16], (DEPTH, D_MODEL, D_MODEL), D_MODEL ** -0.5)
    final_norm_w = 1.0 + nrm(ks[17], (D_MODEL,), 0.02)
    return {'x': x, 'c': c, 'ctx': ctx, 'c_ctx': c_ctx, 'w_ada': w_ada, 'b_ada': b_ada,
            'norm_w': norm_w, 'w_in': w_in, 'a_conv_w': a_conv_w, 'a_log': a_log,
            'a_dt_bias': a_dt_bias, 'a_norm_w': a_norm_w, 'b_sink': b_sink, 'c_decay': c_decay,
            'c_norm_w': c_norm_w, 'w_branch': w_branch, 'w_out': w_out, 'final_norm_w': final_norm_w}


def reference(x, c, ctx, c_ctx, w_ada, b_ada, norm_w, w_in, a_conv_w, a_log, a_dt_bias, a_norm_w,
              b_sink, c_decay, c_norm_w, w_branch, w_out, final_norm_w):
    b, t, d = x.shape
    l = ctx.shape[1]
    rows_n = t // GRID_W
    rows = jnp.repeat(jnp.arange(rows_n, dtype=jnp.float32), GRID_W)
    cols = jnp.tile(jnp.arange(GRID_W, dtype=jnp.float32), rows_n)
    n_ax = B_HD // 4
    ang_r = rope_angles(rows, n_ax)[:, None, :]
    ang_c = rope_angles(cols, n_ax)[:, None, :]
    ang_ret = rope_angles(jnp.arange(t, dtype=jnp.float32), C_HD // 2)
    zeros_a = jnp.zeros((b, A_HEADS, A_DK, A_DV), jnp.float32)
    zeros_c = jnp.zeros((b, C_HEADS, C_HD, C_HD), jnp.float32)

    for layer in range(DEPTH):
        mod = jax.nn.silu(c) @ w_ada[layer] + b_ada[layer]
        shift, scale, gate = [m[:, None, :] for m in jnp.split(mod, 3, axis=-1)]
        mod_c = jax.nn.silu(c_ctx) @ w_ada[layer] + b_ada[layer]
        shift_c, scale_c, gate_c = jnp.split(mod_c, 3, axis=-1)
        h = rms_norm(x, norm_w[layer]) * (1.0 + scale) + shift
        hc = rms_norm(ctx, norm_w[layer]) * (1.0 + scale_c) + shift_c
        (a_qkv, a_z, a_beta, a_alpha, b_q, b_kv, b_z, c_qkv, c_z, merge) = jnp.split(
            h @ w_in[layer], SPLIT_POINTS, axis=-1)
        (a_qkv_c, a_z_c, a_beta_c, a_alpha_c, b_q_c, b_kv_c, b_z_c, c_qkv_c, c_z_c, merge_c) = jnp.split(
            hc @ w_in[layer], SPLIT_POINTS, axis=-1)

        qa, ka, va, beta_a, g_a = delta_prep(a_qkv, a_beta, a_alpha, a_conv_w[layer], a_log[layer], a_dt_bias[layer])
        qa_c, ka_c, va_c, beta_ac, g_ac = delta_prep(a_qkv_c, a_beta_c, a_alpha_c, a_conv_w[layer],
                                                     a_log[layer], a_dt_bias[layer])
        o_ac, s_af, s_ab = bidir_delta(qa_c, ka_c, va_c, beta_ac, g_ac, zeros_a, zeros_a)
        o_a, _, _ = bidir_delta(qa, ka, va, beta_a, g_a, s_af, s_ab)
        y_a = gated_head_rmsnorm(o_a, a_z, a_norm_w[layer]).astype(x.dtype)

        q_b = apply_axial(b_q.reshape(b, t, B_Q_HEADS, B_HD), ang_r, ang_c)
        k_b, v_b = jnp.split(b_kv.reshape(b, t, 2 * B_KV_HEADS, B_HD), 2, axis=2)
        k_b = apply_axial(k_b, ang_r, ang_c)
        q_bc = b_q_c.reshape(b, l, B_Q_HEADS, B_HD)
        k_bc, v_bc = jnp.split(b_kv_c.reshape(b, l, 2 * B_KV_HEADS, B_HD), 2, axis=2)
        y_b = (window_attention(q_b, k_b, v_b, k_bc, v_bc, b_sink[layer]) * jax.nn.silu(b_z)).astype(x.dtype)

        log_gamma = jax.nn.log_sigmoid(c_decay[layer].astype(jnp.float32))
        q_c, k_c, v_c = retention_prep(c_qkv, ang_ret)
        q_cc, k_cc, v_cc = retention_prep(c_qkv_c, None)
        o_cc, s_cf, s_cb = bidir_retention(q_cc, k_cc, v_cc, log_gamma, zeros_c, zeros_c)
        o_c, _, _ = bidir_retention(q_c, k_c, v_c, log_gamma, s_cf, s_cb)
        y_c = retention_out(o_c, c_z, c_norm_w[layer]).astype(x.dtype)

        out = merge_branches(y_a, y_b, y_c, merge, w_branch[layer], w_out[layer])
        x_new = x + gate * out

        if layer < DEPTH - 1:
            y_ac = gated_head_rmsnorm(o_ac, a_z_c, a_norm_w[layer]).astype(ctx.dtype)
            y_bc = (context_attention(q_bc, k_bc, v_bc, b_sink[layer]) * jax.nn.silu(b_z_c)).astype(ctx.dtype)
            y_cc = retention_out(o_cc, c_z_c, c_norm_w[layer]).astype(ctx.dtype)
            out_c = merge_branches(y_ac, y_bc, y_cc, merge_c, w_branch[layer], w_out[layer])
            ctx = ctx + gate_c * out_c
        x = x_new

    return rms_norm(x, final_norm_w)
```

```python
from contextlib import ExitStack
import numpy as np
import concourse.bass as bass
import concourse.mybir as mybir
from concourse.bass_utils import run_bass_kernel_spmd

F32 = mybir.dt.float32
BF16 = mybir.dt.bfloat16
AF = mybir.ActivationFunctionType
ALU = mybir.AluOpType
AX = mybir.AxisListType

T = 4096
LC = 256
D = 1024
NT = 34
NTOK = 4352
INW = 8464
CH = 128
EPS = 1e-6
NEG = -1.0e5
O_AQ, O_AK, O_AV, O_AZ, O_AB, O_BQ, O_BKV, O_BZ, O_CQ, O_CK, O_CV, O_CZ, O_MG = (
    0, 512, 1024, 1536, 2048, 2064, 2576, 2832, 3344, 3856, 4368, 4880, 5392)


class Buf:
    __slots__ = ("name", "w", "r", "parts")

    def __init__(self, name):
        self.name = name
        self.w = None
        self.r = {}
        self.parts = {}

    def sub(self, p):
        return Sub(self, p)


class Sub:
    __slots__ = ("parent", "p", "name")

    def __init__(self, parent, p):
        self.parent = parent
        self.p = p
        self.name = "%s[%s]" % (parent.name, p)

    def _slot(self):
        return self.parent.parts.setdefault(self.p, [None, {}])


class Eng:
    def __init__(self, key, e, sem):
        self.key = key
        self.e = e
        self.sem = sem
        self.count = 0
        self.waited = {}


class Sched:
    def __init__(self, nc, n_dma_sems=40):
        self.nc = nc
        self.sems = {}
        self.engs = {}
        for key, e in (("pe", nc.tensor), ("act", nc.scalar), ("dve", nc.vector), ("pool", nc.gpsimd), ("sp", nc.sync)):
            s = nc.alloc_semaphore("sem_" + key)
            self.sems[key] = s
            self.engs[key] = Eng(key, e, s)
        self.dma_sems = []
        for i in range(n_dma_sems):
            k = "dma%d" % i
            self.sems[k] = nc.alloc_semaphore("sem_" + k)
            self.dma_sems.append([k, 0])
        self.dma_rr = 0
        self.nops = 0
        self.clocks = {}
        self.psum = []
        self.ps_rr = 0
        for i in range(8):
            self.psum.append((nc.alloc_psum_tensor("psb%d" % i, [128, 512], F32), Buf("psb%d" % i)))

    def ps(self, pool=None):
        if pool is not None:
            banks, st = pool
            r = self.psum[banks[st[0] % len(banks)]]
            st[0] += 1
            return r
        r = self.psum[self.ps_rr]
        self.ps_rr = (self.ps_rr + 1) % 8
        return r

    def _deps(self, eng, reads, writes, is_dma):
        deps = {}

        def add(tok, same_ok):
            if tok is None:
                return
            k, v = tok
            if k == eng.key and not same_ok:
                return
            if deps.get(k, 0) < v:
                deps[k] = v

        same = is_dma or eng.key != "pe"
        for b in reads:
            if isinstance(b, Sub):
                add(b.parent.w, True)
                add(b._slot()[0], True)
            else:
                add(b.w, True)
                for pw, pr in b.parts.values():
                    add(pw, True)
        for b in writes:
            if isinstance(b, Sub):
                add(b.parent.w, same)
                for k, v in b.parent.r.items():
                    add((k, v), same)
                pw, pr = b._slot()
                add(pw, same)
                for k, v in pr.items():
                    add((k, v), same)
            else:
                add(b.w, same)
                for k, v in b.r.items():
                    add((k, v), same)
                for pw, pr in b.parts.values():
                    add(pw, same)
                    for k, v in pr.items():
                        add((k, v), same)
        for k, v in sorted(deps.items(), key=lambda kv: -kv[1]):
            self._need(eng, k, v)

    def _record(self, key, val, reads, writes):
        for b in reads:
            r = b._slot()[1] if isinstance(b, Sub) else b.r
            if r.get(key, 0) < val:
                r[key] = val
        for b in writes:
            if isinstance(b, Sub):
                sl = b._slot()
                sl[0] = (key, val)
                sl[1] = {}
            else:
                b.w = (key, val)
                b.r = {}
                b.parts = {}

    def _need(self, eng, k, v):
        if eng.waited.get(k, 0) >= v:
            return
        eng.e.wait_ge(self.sems[k], v)
        eng.waited[k] = v
        clk = self.clocks.get((k, v))
        if clk:
            w = eng.waited
            for k2, v2 in clk.items():
                if w.get(k2, 0) < v2:
                    w[k2] = v2

    def op(self, ek, fn, reads=(), writes=()):
        eng = self.engs[ek]
        self._deps(eng, reads, writes, False)
        ins = fn(eng.e)
        self.nops += 1
        eng.count += 1
        ins.then_inc(eng.sem, 1)
        clk = dict(eng.waited)
        clk.pop(eng.key, None)
        self.clocks[(eng.key, eng.count)] = clk
        self._record(eng.key, eng.count, reads, writes)
        return ins

    def dma(self, ek, out, in_, reads=(), writes=(), **kw):
        eng = self.engs[ek]
        self._deps(eng, reads, writes, True)
        slot = self.dma_sems[self.dma_rr]
        self.dma_rr = (self.dma_rr + 1) % len(self.dma_sems)
        k, uses = slot
        if uses > 0:
            self._need(eng, k, 16 * uses)
        ins = eng.e.dma_start(out=out, in_=in_, **kw)
        self.nops += 1
        slot[1] = uses + 1
        val = 16 * (uses + 1)
        ins.then_inc(self.sems[k], 16)
        clk = dict(eng.waited)
        clk.pop(eng.key, None)
        self.clocks[(k, val)] = clk
        self._record(k, val, reads, writes)
        return ins

    def wait_all(self, ek, bufs):
        eng = self.engs[ek]
        for b in bufs:
            toks = []
            if b.w is not None:
                toks.append(b.w)
            toks.extend(b.r.items())
            for pw, pr in b.parts.values():
                if pw is not None:
                    toks.append(pw)
                toks.extend(pr.items())
            for k, v in toks:
                if eng.waited.get(k, 0) < v:
                    eng.e.wait_ge(self.sems[k], v)
                    eng.waited[k] = v

    def barrier(self):
        for eng in self.engs.values():
            for o in self.engs.values():
                if o.key != eng.key and o.count > 0 and eng.waited.get(o.key, 0) < o.count:
                    eng.e.wait_ge(self.sems[o.key], o.count)
                    eng.waited[o.key] = o.count
            for k, uses in self.dma_sems:
                if uses > 0 and eng.waited.get(k, 0) < 16 * uses:
                    eng.e.wait_ge(self.sems[k], 16 * uses)
                    eng.waited[k] = 16 * uses


class Rot:
    def __init__(self, alloc, name, shape, dtype, n=2):
        self.items = [(alloc("%s%d" % (name, i), shape, dtype), Buf("%s%d" % (name, i))) for i in range(n)]
        self.i = 0

    def next(self):
        r = self.items[self.i]
        self.i = (self.i + 1) % len(self.items)
        return r


def host_consts():
    f = np.float32
    j = np.arange(128)[:, None]
    i = np.arange(128)[None, :]
    c = {}
    c["k_ident"] = np.eye(128, dtype=f)
    c["k_ones"] = np.ones((128, 128), f)
    am = np.zeros((4, 128, 128), f)
    am[0] = np.where(i >= j, 0.0, NEG)
    am[1] = np.where(i > j, 0.0, NEG)
    am[2] = np.where(i <= j, 0.0, NEG)
    am[3] = np.where(i < j, 0.0, NEG)
    c["k_amask"] = am
    tri = np.zeros((2, 128, 128), f)
    tri[0] = (j <= i)
    tri[1] = (j >= i)
    c["k_tri"] = tri
    bm = np.zeros((2, 128, 512), f)
    bm[0] = np.tile((j >= i).astype(f), (1, 4))
    bm[1] = np.tile((j <= i).astype(f), (1, 4))
    c["k_bmask"] = bm
    cm = np.zeros((6, 128, 128), f)
    cm[0] = np.maximum(i - j, 0)
    cm[1] = np.maximum(j - i, 0)
    cm[2] = (i > j)
    cm[3] = (j > i)
    cm[4] = np.broadcast_to(i + 1, (128, 128))
    cm[5] = np.broadcast_to(CH - i, (128, 128))
    c["k_cm"] = cm
    nm = np.zeros((14, 128, 128), f)
    for d in range(2):
        for lev in range(7):
            b = 1 << lev
            same = (j // (2 * b)) == (i // (2 * b))
            if d == 0:
                m = same & ((j % (2 * b)) < b) & ((i % (2 * b)) >= b)
            else:
                m = same & ((i % (2 * b)) < b) & ((j % (2 * b)) >= b)
            nm[d * 7 + lev] = -m.astype(f)
    c["k_nm"] = nm
    cj = np.zeros((128, 8), f)
    cj[:, 0:4] = (CH - 1 - np.arange(128))[:, None]
    cj[:, 4:8] = np.arange(128)[:, None]
    c["k_cj"] = cj
    t = np.arange(T)
    inv16 = (f(10000.0) ** (-np.arange(16, dtype=f) / f(16))).astype(f)
    ar = (t // 64).astype(f)[:, None] * inv16[None, :]
    ac = (t % 64).astype(f)[:, None] * inv16[None, :]
    ab = np.concatenate([ar, ac], axis=1).astype(f)
    c["k_cosb"] = np.tile(np.cos(ab).astype(f), (1, 8))
    c["k_sinb"] = np.tile(np.sin(ab).astype(f), (1, 8))
    inv64 = (f(10000.0) ** (-np.arange(64, dtype=f) / f(64))).astype(f)
    ang = t.astype(f)[:, None] * inv64[None, :]
    c["k_cosc"] = np.cos(ang).astype(f)
    c["k_sinc"] = np.sin(ang).astype(f)
    return c


CONST_SHAPES = {"k_ident": [128, 128], "k_ones": [128, 128], "k_amask": [4, 128, 128], "k_tri": [2, 128, 128],
                "k_bmask": [2, 128, 512], "k_cm": [6, 128, 128], "k_cj": [128, 8], "k_nm": [14, 128, 128],
                "k_cosb": [T, 256], "k_sinb": [T, 256], "k_cosc": [T, 64], "k_sinc": [T, 64]}

IN_SHAPES = {"x": [T, D], "c": [D], "ctx": [LC, D], "c_ctx": [D], "w_ada": [2, D, 3 * D], "b_ada": [2, 3 * D],
             "norm_w": [2, D], "w_in": [2, D, INW], "a_conv_w": [2, 5, 1536], "a_log": [2, 8], "a_dt_bias": [2, 8],
             "a_norm_w": [2, 128], "b_sink": [2, 8], "c_decay": [2, 8], "c_norm_w": [2, 512],
             "w_branch": [2, 3, 512, D], "w_out": [2, D, D], "final_norm_w": [D]}

SCRATCH = {"XS": ([T, D], F32), "CTXS": ([LC, D], F32),
           "QA_FM": ([4, 128, NTOK], BF16), "KA_FM": ([4, 128, NTOK], BF16),
           "KA_TM": ([4, NTOK, 128], BF16), "VA_TM": ([4, NTOK, 128], BF16),
           "ZA": ([NTOK, 512], BF16), "ZB": ([NTOK, 512], BF16), "ZC": ([NTOK, 512], BF16),
           "OA": ([NTOK, 512], F32),
           "QB_FM": ([8, 128, NTOK], BF16),
           "QC_FM": ([4, 128, NTOK], BF16), "KC_FM": ([4, 128, NTOK], BF16),
           "KC_TM": ([NTOK, 512], BF16), "VC_TM": ([NTOK, 512], BF16),
           "SCB": ([NT, 128, 512], BF16), "SCF": ([NT, 128, 512], BF16),
           "KB_FM": ([2, 128, NTOK], BF16), "VB_TM": ([NTOK, 2, 65], BF16),
           "GM_FM": ([3 * D, NTOK], BF16), "Y_FM": ([3, 512, NTOK], BF16)}


class MK:
    def __init__(self, nlayers=2, dbg=(), stop=None):
        nc = bass.Bass("TRN2", target_bir_lowering=False, dynamic_dma_scratch_size=1024)
        self.nc = nc
        self.S = Sched(nc)
        self.nlayers = nlayers
        self.stop = stop
        self.din = {}
        for n, shp in list(IN_SHAPES.items()) + list(CONST_SHAPES.items()):
            self.din[n] = nc.dram_tensor(n, shp, F32, kind="ExternalInput").ap()
        self.out = nc.dram_tensor("out", [T, D], F32, kind="ExternalOutput").ap()
        self.scr = {}
        self._db = {}
        for n, (shp, dt) in SCRATCH.items():
            kind = "ExternalOutput" if n in dbg else "Internal"
            self.scr[n] = nc.dram_tensor(n, shp, dt, kind=kind).ap()
        self.dbg = dbg
        self._tcount = 0
        self.alloc()

    def db(self, name, tt):
        k = (name, tt)
        if k not in self._db:
            self._db[k] = Buf("%s_%d" % k)
        return self._db[k]

    def dball(self, name):
        return [self.db(name, tt) for tt in range(NT)]

    def sb(self, name, shape, dtype=F32):
        return self.nc.alloc_sbuf_tensor(name, shape, dtype), Buf(name)

    def rot(self, name, shape, dtype=F32, n=2):
        return Rot(lambda nm, sh, dt: self.nc.alloc_sbuf_tensor(nm, sh, dt), name, shape, dtype, n)

    def _talloc(self, name, shape, dtype):
        self._tcount += 1
        return self.es.enter_context(self.nc.sbuf_tensor("%s_t%d" % (name, self._tcount), shape, dtype))

    def tsb(self, name, shape, dtype=F32):
        return self._talloc(name, shape, dtype), Buf(name)

    def trot(self, name, shape, dtype=F32, n=2):
        return Rot(self._talloc, name, shape, dtype, n)

    def alloc(self):
        nc, S, din = self.nc, self.S, self.din
        self.ident_f, self.b_ident_f = self.sb("ident_f", [128, 128])
        self.ones_f, self.b_ones_f = self.sb("ones_f", [128, 128])
        self.ident_b, self.b_ident_b = self.sb("ident_b", [128, 128], BF16)
        self.ones_b, self.b_ones_b = self.sb("ones_b", [128, 128], BF16)
        self.amask, self.b_amask = self.sb("amask", [128, 4, 128])
        self.tri, self.b_tri = self.sb("tri", [128, 2, 128])
        self.bmask, self.b_bmask = self.sb("bmask", [128, 2, 512], BF16)
        self.cm, self.b_cm = self.sb("cm", [128, 6, 128])
        self.cj, self.b_cj = self.sb("cj", [128, 8])
        S.dma("sp", self.ident_f[:], din["k_ident"], writes=[self.b_ident_f])
        S.dma("sp", self.ones_f[:], din["k_ones"], writes=[self.b_ones_f])
        S.dma("sp", self.amask[:], din["k_amask"].rearrange("a p n -> p a n"), writes=[self.b_amask])
        S.dma("sp", self.tri[:], din["k_tri"].rearrange("a p n -> p a n"), writes=[self.b_tri])
        S.dma("sp", self.cm[:], din["k_cm"].rearrange("a p n -> p a n"), writes=[self.b_cm])
        S.dma("sp", self.cj[:], din["k_cj"], writes=[self.b_cj])
        S.op("dve", lambda e: e.tensor_copy(out=self.ident_b[:], in_=self.ident_f[:]), reads=[self.b_ident_f], writes=[self.b_ident_b])
        S.op("dve", lambda e: e.tensor_copy(out=self.ones_b[:], in_=self.ones_f[:]), reads=[self.b_ones_f], writes=[self.b_ones_b])
        self.R1, self.b_R1 = self.sb("R1", [128, 8 * NTOK], BF16)
        self.hT = self.R1[:].rearrange("p (k t) -> p k t", k=8)
        self.b_hT = [Buf("hT%d" % i) for i in range(NT)]
        self.wst = self.rot("wst", [128, 8, 512], F32, 1)
        self.wb = self.rot("wb", [128, 8, 512], BF16, 4)
        self.scol, self.b_scol = self.sb("scol", [128, 8, 2])
        self.mod, self.b_mod = self.sb("mod", [128, 24, 2])
        self.nwcol, self.b_nwcol = self.sb("nwcol", [128, 8])
        self.badacol, self.b_badacol = self.sb("badacol", [128, 24])
        self.Afm, self.b_Afm = self.sb("Afm", [128, 8, 2])
        self.convw, self.b_convw = self.sb("convw", [128, 12, 5])
        self.rowtmp = self.rot("rowtmp", [128, 128], F32, 2)
        for it in self.rowtmp.items:
            S.op("pool", lambda e: e.memset(it[0][:], 0.0), writes=[it[1]])
        self.gdiag = self.rot("gdiag", [128, 512], F32, 2)
        self.SCR, self.b_SCR = self.sb("SCR", [128, NT, 16])
        self.asc = {}
        for n in ("BETA", "GC", "NEGG", "LNBMG", "NEGEG", "ETAIL", "EGL"):
            self.asc[n] = self.sb("asc_" + n, [128, NT, 8])
        self.SH2, self.b_SH2 = self.sb("SH2", [128, NT, 8])
        self.kmx, self.b_kmx = self.sb("kmx", [128, 4])
        self.par = {}
        for n, w in (("a_log", 8), ("a_dt_bias", 8), ("b_sink", 8), ("c_decay", 8), ("a_norm_w", 128), ("c_norm_w", 512)):
            self.par[n] = self.sb("par_" + n, [128, w])
        self.epsb, self.b_epsb = self.sb("epsb", [128, 4])
        for col, val in ((0, EPS), (1, -0.5 * float(np.log(128.0))), (2, 0.0), (3, 1.0)):
            S.op("pool", lambda e: e.memset(self.epsb[:, col:col + 1], val), writes=[self.b_epsb])

    def load_cols(self, src_rows, n, dst, bdst, func=None):
        S = self.S
        rt, brt = self.rowtmp.next()
        S.dma("sp", rt[0:n, :], src_rows, writes=[brt])
        if func is not None:
            S.op("act", lambda e: e.activation(out=rt[0:n, :], in_=rt[0:n, :], func=func), reads=[brt], writes=[brt])
        ps, bps = S.ps()
        S.op("pe", lambda e: e.transpose(out=ps[:, 0:128], in_=rt[:, :], identity=self.ident_f[:]),
             reads=[brt, self.b_ident_f], writes=[bps])
        S.op("dve", lambda e: e.tensor_copy(out=dst, in_=ps[:, 0:n]), reads=[bps], writes=[bdst])

    def w_plan(self, src2d, groups):
        self._wsrc = src2d
        self._wgroups = list(groups)
        self._wi = 0
        self._wq = []
        self._w_issue()

    def _w_issue(self):
        if self._wi < len(self._wgroups):
            c0, ncols = self._wgroups[self._wi]
            self._wi += 1
            self._wq.append(((c0, ncols), self._load_w_raw(self._wsrc, c0, ncols)))

    def load_w(self, src2d, c0, ncols, nk=8):
        if getattr(self, "_wq", None):
            key, val = self._wq.pop(0)
            assert key == (c0, ncols), (key, c0, ncols)
            self._w_issue()
            return val
        return self._load_w_raw(src2d, c0, ncols, nk)

    def _load_w_raw(self, src2d, c0, ncols, nk=8):
        S = self.S
        st, bst = self.wst.next()
        wb, bwb = self.wb.next()
        S.dma("sp", st[:, 0:nk, 0:ncols], src2d.rearrange("(k p) n -> p k n", p=128)[:, :, c0:c0 + ncols], writes=[bst])
        S.op("pool", lambda e: e.tensor_copy(out=wb[:, 0:nk, 0:ncols], in_=st[:, 0:nk, 0:ncols]), reads=[bst], writes=[bwb])
        return wb, bwb

    def phase0(self, l):
        S, din = self.S, self.din
        self.es = ExitStack()
        for n in self.par:
            t, b = self.par[n]
            S.dma("sp", t[:], din[n][l].partition_broadcast(128), writes=[b])
        self.load_cols(din["c"].rearrange("(k p) -> k p", p=128), 8, self.scol[:, :, 0], self.b_scol, AF.Silu)
        self.load_cols(din["c_ctx"].rearrange("(k p) -> k p", p=128), 8, self.scol[:, :, 1], self.b_scol, AF.Silu)
        self.load_cols(din["norm_w"][l].rearrange("(k p) -> k p", p=128), 8, self.nwcol[:], self.b_nwcol)
        self.load_cols(din["b_ada"][l].rearrange("(k p) -> k p", p=128), 24, self.badacol[:], self.b_badacol)
        cw, bcw = self.tsb("cwrows", [128, 1536])
        S.op("pool", lambda e: e.memset(cw[:], 0.0), writes=[bcw])
        S.dma("sp", cw[0:5, :], din["a_conv_w"][l], writes=[bcw])
        for ct in range(12):
            ps, bps = S.ps()
            S.op("pe", lambda e: e.transpose(out=ps[:, 0:128], in_=cw[:, ct * 128:(ct + 1) * 128], identity=self.ident_f[:]),
                 reads=[bcw, self.b_ident_f], writes=[bps])
            S.op("dve", lambda e: e.tensor_copy(out=self.convw[:, ct, :], in_=ps[:, 0:5]), reads=[bps], writes=[self.b_convw])
        psm, bpsm = S.ps()
        for g in range(6):
            st, bst = self.wst.next()
            S.dma("sp", st[:], din["w_ada"][l].rearrange("(k p) n -> p k n", p=128)[:, :, g * 512:(g + 1) * 512], writes=[bst])
            for jl in range(4):
                j = g * 4 + jl
                for kc in range(8):
                    S.op("pe", lambda e: e.matmul(out=psm[:, 2 * j:2 * j + 2], lhsT=st[:, kc, jl * 128:(jl + 1) * 128], rhs=self.scol[:, kc, :],
                                                  start=(kc == 0), stop=(kc == 7)),
                         reads=[bst, self.b_scol], writes=[bpsm])
        S.op("dve", lambda e: e.tensor_tensor(out=self.mod[:], in0=psm[:, 0:48].rearrange("p (j s) -> p j s", s=2),
                                              in1=self.badacol[:].unsqueeze(2).to_broadcast([128, 24, 2]), op=ALU.add),
             reads=[bpsm, self.b_badacol], writes=[self.b_mod])
        S.op("dve", lambda e: e.scalar_tensor_tensor(out=self.Afm[:], in0=self.mod[:, 8:16, :], scalar=1.0,
                                                     in1=self.nwcol[:].unsqueeze(2).to_broadcast([128, 8, 2]), op0=ALU.add, op1=ALU.mult),
             reads=[self.b_mod, self.b_nwcol], writes=[self.b_Afm])
        S.barrier()
        self.es.close()

    def bc_rows(self, dst_fn, bdst, col_fn, bcol, nchunks):
        S = self.S
        for half in range(nchunks // 4):
            t, bt = self.gdiag.next()
            for q in range(4):
                kc = half * 4 + q
                S.op("dve", lambda e: e.tensor_scalar(out=t[:, q * 128:(q + 1) * 128], in0=self.ident_f[:], scalar1=col_fn(kc),
                                                      scalar2=None, op0=ALU.mult),
                     reads=[self.b_ident_f, bcol], writes=[bt])
            ps, bps = S.ps()
            S.op("pe", lambda e: e.matmul(out=ps[:], lhsT=self.ones_f[:], rhs=t[:], start=True, stop=True),
                 reads=[self.b_ones_f, bt], writes=[bps])
            S.op("act", lambda e: e.copy(out=dst_fn(half), in_=ps[:]), reads=[bps], writes=[bdst])

    def phase1(self, l):
        S, din = self.S, self.din
        self.es = ExitStack()
        self.p1_xt = self.trot("p1_xt", [128, 1024], F32, 2)
        self.p1_sq, self.b_p1_sq = self.tsb("p1_sq", [128, 1024])
        self.p1_st = self.trot("p1_st", [128, 4], F32, 2)
        for tt in range(NT):
            s = 1 if tt < 2 else 0
            if tt < 2:
                src = (din["ctx"] if l == 0 else self.scr["CTXS"])[tt * 128:(tt + 1) * 128, :]
                rd = [] if l == 0 else [self.db("CTXS", tt)]
            else:
                src = (din["x"] if l == 0 else self.scr["XS"])[(tt - 2) * 128:(tt - 1) * 128, :]
                rd = [] if l == 0 else [self.db("XS", tt)]
            xt, bxt = self.p1_xt.next()
            st, bst = self.p1_st.next()
            S.dma("sp", xt[:], src, reads=rd, writes=[bxt])
            S.op("act", lambda e: e.activation(out=self.p1_sq[:], in_=xt[:], func=AF.Square, accum_out=st[:, 0:1]),
                 reads=[bxt], writes=[self.b_p1_sq, bst])
            S.op("dve", lambda e: e.tensor_scalar(out=st[:, 1:2], in0=st[:, 0:1], scalar1=1.0 / D, scalar2=EPS, op0=ALU.mult, op1=ALU.add),
                 reads=[bst], writes=[bst])
            S.op("act", lambda e: e.activation(out=st[:, 2:3], in_=st[:, 1:2], func=AF.Ln), reads=[bst], writes=[bst])
            S.op("act", lambda e: e.activation(out=st[:, 3:4], in_=st[:, 2:3], func=AF.Exp, scale=-0.5), reads=[bst], writes=[bst])
            S.op("act", lambda e: e.activation(out=xt[:], in_=xt[:], func=AF.Copy, scale=st[:, 3:4]),
                 reads=[bst, bxt], writes=[bxt])
            for half in range(2):
                ps, bps = S.ps()
                for q in range(4):
                    kc = half * 4 + q
                    S.op("pe", lambda e: e.transpose(out=ps[:, q * 128:(q + 1) * 128], in_=xt[:, kc * 128:(kc + 1) * 128], identity=self.ident_f[:]),
                         reads=[bxt, self.b_ident_f], writes=[bps])
                for q in range(4):
                    kc = half * 4 + q
                    dst = self.hT[:, kc, tt * 128:(tt + 1) * 128]
                    if q % 2 == 0:
                        S.op("dve", lambda e: e.tensor_scalar(out=dst, in0=ps[:, q * 128:(q + 1) * 128], scalar1=self.Afm[:, kc, s:s + 1],
                                                              scalar2=self.mod[:, kc, s:s + 1], op0=ALU.mult, op1=ALU.add),
                             reads=[bps, self.b_Afm, self.b_mod], writes=[self.b_hT[tt]])
                    else:
                        S.op("act", lambda e: e.activation(out=dst, in_=ps[:, q * 128:(q + 1) * 128], func=AF.Identity,
                                                           scale=self.Afm[:, kc, s:s + 1], bias=self.mod[:, kc, s:s + 1]),
                             reads=[bps, self.b_Afm, self.b_mod], writes=[self.b_hT[tt]])
        S.barrier()
        self.es.close()

    def tok_groups(self):
        g = [(0, 256, 0, 2)]
        for i in range(8):
            g.append((256 + i * 512, 512, 2 + 4 * i, 4))
        return g

    class WStream:
        def __init__(self, mk, src2d, groups, bufs):
            self.mk, self.src, self.groups, self.bufs = mk, src2d, list(groups), bufs
            self.i = 0
            self.q = []
            self._issue()

        def _issue(self):
            if self.i < len(self.groups):
                c0, ncols = self.groups[self.i]
                wb, bwb = self.bufs[self.i % len(self.bufs)]
                S = self.mk.S
                st, bst = self.mk.wst.next()
                S.dma("sp", st[:, :, 0:ncols], self.src.rearrange("(k p) n -> p k n", p=128)[:, :, c0:c0 + ncols], writes=[bst])
                S.op("pool", lambda e: e.tensor_copy(out=wb[:, :, 0:ncols], in_=st[:, :, 0:ncols]), reads=[bst], writes=[bwb])
                self.q.append(((c0, ncols), (wb, bwb)))
                self.i += 1

        def get(self, c0, ncols):
            key, val = self.q.pop(0)
            assert key == (c0, ncols), (key, c0, ncols)
            self._issue()
            return val

    def proj_tm_gen(self, l, c0, ncols, handler, ws):
        S = self.S
        if hasattr(self, "marks"):
            self.marks.append(("  L%d tm@%d" % (l, c0), {k: v.count for k, v in S.engs.items()}))
        wb, bwb = ws.get(c0, ncols)
        for tt in range(NT):
            ps, bps = S.ps()
            for kc in range(8):
                S.op("pe", lambda e: e.matmul(out=ps[:, 0:ncols], lhsT=self.hT[:, kc, tt * 128:(tt + 1) * 128], rhs=wb[:, kc, 0:ncols],
                                              start=(kc == 0), stop=(kc == 7)),
                     reads=[self.b_hT[tt], bwb], writes=[bps])
            handler(tt, ps, bps)
            yield

    def proj_tm(self, l, c0, ncols, handler, ws):
        for _ in self.proj_tm_gen(l, c0, ncols, handler, ws):
            pass

    @staticmethod
    def run_pair(g1, g2):
        gens = [g1, g2]
        while gens:
            for g_ in list(gens):
                try:
                    next(g_)
                except StopIteration:
                    gens.remove(g_)

    def transpose_out(self, src_fn, n, rows, dst, dst_buf_list, tag, pool=None):
        S = self.S
        ps, bps = S.ps(pool)
        psb = ps[:].bitcast(BF16)
        for i in range(n):
            src, bsrc = src_fn(i)
            S.op("pe", lambda e: e.transpose(out=psb[0:rows, i * 128:(i + 1) * 128], in_=src, identity=self.ident_b[:]),
                 reads=[bsrc, self.b_ident_b], writes=[bps])
        S.op("act", lambda e: e.copy(out=dst, in_=psb[0:rows, 0:n * 128]), reads=[bps], writes=dst_buf_list)

    def phase2(self, l):
        S, din, scr = self.S, self.din, self.scr
        last = (l == self.nlayers - 1)
        self.es = ExitStack()
        self.zt = self.trot("zt", [128, 512], BF16, 2)
        self.tmpA = self.trot("tmpA", [128, 512], F32, 2)
        self.tmpB = self.trot("tmpB", [128, 512], F32, 2)
        self.tmo = self.trot("tmo", [128, 512], BF16, 3)
        self.fmo = self.trot("fmo", [128, 512], BF16, 3)
        self.qa = self.trot("qa", [128, 8, 128], BF16, 2)
        self.ka = self.trot("ka", [128, 2, 128], BF16, 2)
        self.vb = self.trot("vb", [128, 2, 65], BF16, 2)
        self.qaT = self.trot("qaT", [128, 8, 128], BF16, 2)
        self.kaT = self.trot("kaT", [128, 2, 128], BF16, 2)
        self.csc = self.trot("csc", [128, 2, 64], F32, 3)
        self.csb = self.trot("csb", [128, 2, 256], F32, 3)
        self.st8 = self.trot("st8", [128, 24], F32, 4)
        self.tmp8 = self.trot("tmp8", [128, NT, 8], F32, 4)
        for n in ("LNB", "GRAW", "GT"):
            self.asc[n] = self.tsb("asc_" + n, [128, NT, 8])
        self.KM, self.b_KM = self.tsb("KM", [128, 2])
        self.half8, self.b_half8 = self.tsb("half8", [128, 8])
        S.op("pool", lambda e: e.memset(self.half8[:], 0.5), writes=[self.b_half8])
        self.rowbuf, self.b_rowbuf = self.tsb("rowbuf", [128, 4360])
        self.slrow, self.b_slrow = self.tsb("slrow", [128, NTOK])
        S.op("pool", lambda e: e.memset(self.rowbuf[:], 0.0), writes=[self.b_rowbuf])
        for it in self.vb.items:
            S.op("pool", lambda e: e.memset(it[0][:], 1.0), writes=[it[1]])
        for it in self.ka.items + self.qa.items:
            S.op("pool", lambda e: e.memset(it[0][:], 0.0), writes=[it[1]])
        for it in self.ka.items:
            S.op("pool", lambda e: e.memset(it[0][:, :, 64:65], 1.0), writes=[it[1]])
        S.op("pool", lambda e: e.memset(self.KM[:], 0.0), writes=[self.b_KM])

        def silu_out(name):
            def h(tt, ps, bps):
                z, bz = self.zt.next()
                S.op("act", lambda e: e.activation(out=z[:], in_=ps[:], func=AF.Silu), reads=[bps], writes=[bz])
                S.dma("act", scr[name][tt * 128:(tt + 1) * 128, :], z[:], reads=[bz], writes=[self.db(name, tt)])
            return h

        wsrc = din["w_in"][l]
        bufsA, bufsB = self.wb.items[0:2], self.wb.items[2:4]
        ws1 = self.WStream(self, wsrc, [(O_AZ, 512), (O_AB, 16), (O_BKV, 256)], bufsA)
        self.proj_tm(l, O_AZ, 512, silu_out("ZA"), ws1)
        if self.stop == "p2a":
            S.barrier()
            self.es.close()
            return

        def h_ab(tt, ps, bps):
            S.op("act", lambda e: e.copy(out=self.SCR[:, tt, :], in_=ps[:, 0:16]), reads=[bps], writes=[self.b_SCR])
        self.proj_tm(l, O_AB, 16, h_ab, ws1)
        if self.stop == "p2b1":
            S.barrier()
            self.es.close()
            return
        self.a_scalars(l)
        if self.stop in ("p2b", "p2b2"):
            S.barrier()
            self.es.close()
            return

        def load_cs(tt, which):
            rotp, cn, sn, w = (self.csc, "k_cosc", "k_sinc", 64) if which == "c" else (self.csb, "k_cosb", "k_sinb", 256)
            cs, bcs = rotp.next()
            r0 = (tt - 2) * 128
            S.dma("sp", cs[:, 0, :], din[cn][r0:r0 + 128, :], writes=[bcs])
            S.dma("sp", cs[:, 1, :], din[sn][r0:r0 + 128, :], writes=[bcs])
            return cs, bcs

        def rope(x1, x2, cosb, sinb, o1, o2, shape_fn, bps, bcs, bout, scale=None):
            ta, bta = self.tmpA.next()
            tb, btb = self.tmpB.next()
            ta1, ta2 = shape_fn(ta[:, 0:256]), shape_fn(ta[:, 256:512])
            tb1, tb2 = shape_fn(tb[:, 0:256]), shape_fn(tb[:, 256:512])
            if scale is None:
                mul = lambda o, a, b: (lambda e: e.tensor_tensor(out=o, in0=a, in1=b, op=ALU.mult))
            else:
                mul = lambda o, a, b: (lambda e: e.scalar_tensor_tensor(out=o, in0=a, scalar=scale, in1=b, op0=ALU.mult, op1=ALU.mult))
            S.op("dve", mul(ta1, x1, cosb), reads=[bps, bcs], writes=[bta])
            S.op("dve", mul(tb1, x2, sinb), reads=[bps, bcs], writes=[btb])
            S.op("dve", mul(ta2, x1, sinb), reads=[bps, bcs], writes=[bta])
            S.op("dve", mul(tb2, x2, cosb), reads=[bps, bcs], writes=[btb])
            S.op("pool", lambda e: e.tensor_tensor(out=o1, in0=ta1, in1=tb1, op=ALU.subtract), reads=[bta, btb], writes=[bout])
            S.op("pool", lambda e: e.tensor_tensor(out=o2, in0=ta2, in1=tb2, op=ALU.add), reads=[bta, btb], writes=[bout])

        def rope_b(ps_ap, nha, tt, dst, bdst, bps, nh):
            o, bo = self.tmo.next()
            if tt >= 2:
                cs, bcs = load_cs(tt, "b")
                pv = ps_ap.rearrange("p (g f k) -> p g f k", f=2, k=16)
                ov = o[:, 0:nha * 32].rearrange("p (g f k) -> p g f k", f=2, k=16)
                cosb = cs[:, 0, 0:nha * 16].rearrange("p (g k) -> p g k", k=16)
                sinb = cs[:, 1, 0:nha * 16].rearrange("p (g k) -> p g k", k=16)
                rope(pv[:, :, 0, :], pv[:, :, 1, :], cosb, sinb, ov[:, :, 0, :], ov[:, :, 1, :],
                     lambda a: a[:, 0:nha * 16].rearrange("p (g k) -> p g k", k=16), bps, bcs, bo)
                S.op("act", lambda e: e.copy(out=dst[:, :, 0:64], in_=o[:, 0:nha * 32].rearrange("p (h k) -> p h k", h=nh)), reads=[bo], writes=[bdst])
            else:
                S.op("act", lambda e: e.copy(out=dst[:, :, 0:64], in_=ps_ap.rearrange("p (h k) -> p h k", h=nh)), reads=[bps], writes=[bdst])

        def h_bkv(tt, ps, bps):
            ka, bka = self.ka.next()
            rope_b(ps[:, 0:128], 4, tt, ka, bka, bps, 2)
            ta, bta = self.tmpA.next()
            st, bst = self.st8.next()
            S.op("act", lambda e: e.activation(out=ta[:, 0:128], in_=ps[:, 0:128], func=AF.Square), reads=[bps], writes=[bta])
            S.op("dve", lambda e: e.tensor_reduce(out=st[:, 0:2], in_=ta[:, 0:128].rearrange("p (h k) -> p h k", h=2), axis=AX.X, op=ALU.add),
                 reads=[bta], writes=[bst])
            S.op("dve", lambda e: e.tensor_tensor(out=self.KM[:], in0=self.KM[:], in1=st[:, 0:2], op=ALU.max), reads=[bst, self.b_KM], writes=[self.b_KM])
            vb, bvb = self.vb.next()
            S.op("dve", lambda e: e.tensor_copy(out=vb[:, :, 0:64], in_=ps[:, 128:256].rearrange("p (h k) -> p h k", h=2)),
                 reads=[bps], writes=[bvb])
            S.dma("act", scr["VB_TM"][tt * 128:(tt + 1) * 128, :, :], vb[:], reads=[bvb], writes=[self.db("VB_TM", tt)])
            kT, bkT = self.kaT.next()
            self.transpose_out(lambda i: (ka[:, i, :], bka), 2, 128, kT[:].rearrange("r h t -> r (h t)"), [bkT], "kbt")
            S.dma("act", scr["KB_FM"].rearrange("h r t -> r h t")[:, :, tt * 128:(tt + 1) * 128], kT[:], reads=[bkT], writes=[self.db("KB_FM", tt)])
        self.proj_tm(l, O_BKV, 256, h_bkv, ws1)
        if self.stop == "p2c":
            S.barrier()
            self.es.close()
            return
        S.op("dve", lambda e: e.tensor_reduce(out=self.kmx[:, 0:1], in_=self.KM[:], axis=AX.X, op=ALU.max), reads=[self.b_KM], writes=[self.b_kmx])
        dgk, bdgk = self.gdiag.next()
        S.op("dve", lambda e: e.tensor_scalar(out=dgk[:, 0:128], in0=self.ident_f[:], scalar1=self.kmx[:, 0:1], scalar2=None, op0=ALU.mult),
             reads=[self.b_ident_f, self.b_kmx], writes=[bdgk])
        ps, bps = S.ps()
        S.op("pe", lambda e: e.matmul(out=ps[:, 0:128], lhsT=self.ones_f[:], rhs=dgk[:, 0:128], start=True, stop=True),
             reads=[self.b_ones_f, bdgk], writes=[bps])
        kr, bkr = self.tsb("kmrow", [128, 4])
        S.op("dve", lambda e: e.tensor_reduce(out=kr[:, 0:1], in_=ps[:, 0:128], axis=AX.X, op=ALU.max), reads=[bps], writes=[bkr])
        S.op("act", lambda e: e.activation(out=kr[:, 1:2], in_=kr[:, 0:1], func=AF.Ln), reads=[bkr], writes=[bkr])
        S.op("act", lambda e: e.activation(out=kr[:, 2:3], in_=kr[:, 1:2], func=AF.Exp, scale=0.5), reads=[bkr], writes=[bkr])
        S.op("dve", lambda e: e.tensor_scalar(out=self.kmx[:, 1:2], in0=kr[:, 2:3], scalar1=-1.0, scalar2=None, op0=ALU.mult), reads=[bkr], writes=[self.b_kmx])
        S.op("dve", lambda e: e.tensor_scalar(out=self.kmx[:, 2:3], in0=kr[:, 2:3], scalar1=-0.125, scalar2=None, op0=ALU.mult), reads=[bkr], writes=[self.b_kmx])

        def h_bq(tt, ps, bps):
            qa, bqa = self.qa.next()
            ta, bta = self.tmpA.next()
            st, bst = self.st8.next()
            S.op("act", lambda e: e.activation(out=ta[:], in_=ps[:], func=AF.Square), reads=[bps], writes=[bta])
            S.op("dve", lambda e: e.tensor_reduce(out=st[:, 0:8], in_=ta[:].rearrange("p (h k) -> p h k", h=8), axis=AX.X, op=ALU.add),
                 reads=[bta], writes=[bst])
            S.op("pool", lambda e: e.tensor_tensor(out=st[:, 16:24], in0=st[:, 0:8], in1=self.half8[:], op=ALU.pow), reads=[bst, self.b_half8], writes=[bst])
            rope_b(ps[:], 16, tt, qa, bqa, bps, 8)
            S.op("dve", lambda e: e.tensor_scalar(out=qa[:, :, 64], in0=st[:, 16:24], scalar1=self.kmx[:, 1:2], scalar2=None, op0=ALU.mult),
                 reads=[bst, self.b_kmx], writes=[bqa])
            S.op("dve", lambda e: e.scalar_tensor_tensor(out=self.SH2[:, tt, :], in0=st[:, 16:24], scalar=self.kmx[:, 2:3], in1=self.par["b_sink"][0][:],
                                                         op0=ALU.mult, op1=ALU.add),
                 reads=[bst, self.b_kmx, self.par["b_sink"][1]], writes=[self.b_SH2])
            qT, bqT = self.qaT.next()
            self.transpose_out(lambda i: (qa[:, i, :], bqa), 8, 128, qT[:].rearrange("r h t -> r (h t)"), [bqT], "qbt")
            S.dma("act", scr["QB_FM"].rearrange("h r t -> r h t")[:, :, tt * 128:(tt + 1) * 128], qT[:], reads=[bqT], writes=[self.db("QB_FM", tt)])
        if self.stop == "p2d":
            S.barrier()
            self.es.close()
            return

        def h_cqk(name_fm, name_tm, scale):
            def h(tt, ps, bps):
                o, bo = self.tmo.next()
                if tt >= 2:
                    cs, bcs = load_cs(tt, "c")
                    pv = ps[:].rearrange("p (h f k) -> p h f k", h=4, f=2)
                    ov = o[:].rearrange("p (h f k) -> p h f k", h=4, f=2)
                    cosb = cs[:, 0, :].unsqueeze(1).to_broadcast([128, 4, 64])
                    sinb = cs[:, 1, :].unsqueeze(1).to_broadcast([128, 4, 64])
                    rope(pv[:, :, 0, :], pv[:, :, 1, :], cosb, sinb, ov[:, :, 0, :], ov[:, :, 1, :],
                         lambda a: a.rearrange("p (h k) -> p h k", h=4), bps, bcs, bo, scale=scale)
                else:
                    S.op("act", lambda e: e.activation(out=o[:], in_=ps[:], func=AF.Copy, scale=(1.0 if scale is None else scale)), reads=[bps], writes=[bo])
                if name_tm is not None:
                    S.dma("act", scr[name_tm][tt * 128:(tt + 1) * 128, :], o[:], reads=[bo], writes=[self.db(name_tm, tt)])
                f, bf = self.fmo.next()
                self.transpose_out(lambda i: (o[:, i * 128:(i + 1) * 128], bo), 4, 128, f[:], [bf], "cfm")
                S.dma("act", scr[name_fm].rearrange("h p t -> p h t")[:, :, tt * 128:(tt + 1) * 128], f[:].rearrange("p (h t) -> p h t", h=4),
                      reads=[bf], writes=[self.db(name_fm, tt)])
            return h

        def h_cv(tt, ps, bps):
            o, bo = self.tmo.next()
            S.op("act", lambda e: e.copy(out=o[:], in_=ps[:]), reads=[bps], writes=[bo])
            S.dma("act", scr["VC_TM"][tt * 128:(tt + 1) * 128, :], o[:], reads=[bo], writes=[self.db("VC_TM", tt)])
        wsZ = self.WStream(self, wsrc, [(O_BZ, 512), (O_CZ, 512)], bufsB)
        self.proj_tm(l, O_BZ, 512, silu_out("ZB"), wsZ)
        self.proj_tm(l, O_CZ, 512, silu_out("ZC"), wsZ)
        groups = self.tok_groups()
        wsH = self.WStream(self, wsrc, [(O_BQ, 512), (O_CQ, 512), (O_CK, 512)] + [(g * 512, 512) for g in range(3)], bufsA)
        wsL = self.WStream(self, wsrc, [(O_MG + g * 512, 512) for g in range(6)] + [(O_CV, 512)], bufsB)
        wsF = wsH
        wsM = wsL

        def heavy():
            yield from self.proj_tm_gen(l, O_BQ, 512, h_bq, wsH)
            yield from self.proj_tm_gen(l, O_CQ, 512, h_cqk("QC_FM", None, None), wsH)
            yield from self.proj_tm_gen(l, O_CK, 512, h_cqk("KC_FM", "KC_TM", float(CH) ** -0.5), wsH)

        def light():
            yield from self.proj_tm_gen(l, O_CV, 512, h_cv, wsL)
        if self.stop == "p2e":
            S.barrier()
            self.es.close()
            return


        def afm_gen():
            for g3 in range(3):
                self.marks.append(("  L%d afm%d" % (l, g3), {k: v.count for k, v in S.engs.items()}))
                wb, bwb = wsF.get(g3 * 512, 512)
                for cl in range(4):
                    ct = g3 * 4 + cl
                    head = cl
                    for (t0, n, tile0, ntile) in groups:
                        ps, bps = S.ps()
                        for kc in range(8):
                            S.op("pe", lambda e: e.matmul(out=ps[:, 0:n], lhsT=wb[:, kc, cl * 128:(cl + 1) * 128], rhs=self.hT[:, kc, t0:t0 + n],
                                                          start=(kc == 0), stop=(kc == 7)),
                                 reads=[self.b_hT[tile0 + i] for i in range(ntile)] + [bwb], writes=[bps])
                        off = 2 + t0 if t0 == 0 else 6 + t0
                        S.op("act", lambda e: e.copy(out=self.rowbuf[:, off:off + n], in_=ps[:, 0:n]), reads=[bps], writes=[self.b_rowbuf.sub(t0)])
                        yield
                    for (t0, n, tile0, ntile) in groups:
                        off = 2 + t0 if t0 == 0 else 6 + t0
                        cv, bcv = self.tmpA.next()
                        for k in range(5):
                            src = self.rowbuf[:, off + k - 2:off + k - 2 + n]
                            if k == 0:
                                S.op("dve", lambda e: e.tensor_scalar(out=cv[:, 0:n], in0=src, scalar1=self.convw[:, ct, 0:1], scalar2=None, op0=ALU.mult),
                                     reads=[self.b_rowbuf, self.b_convw], writes=[bcv])
                            else:
                                S.op("dve", lambda e: e.scalar_tensor_tensor(out=cv[:, 0:n], in0=src, scalar=self.convw[:, ct, k:k + 1], in1=cv[:, 0:n],
                                                                             op0=ALU.mult, op1=ALU.add),
                                     reads=[self.b_rowbuf, self.b_convw, bcv], writes=[bcv])
                        if g3 < 2:
                            S.op("act", lambda e: e.activation(out=self.slrow[:, t0:t0 + n], in_=cv[:, 0:n], func=AF.Silu), reads=[bcv], writes=[self.b_slrow.sub(t0)])
                            yield
                        else:
                            o, bo = self.tmo.next()
                            S.op("act", lambda e: e.activation(out=o[:, 0:n], in_=cv[:, 0:n], func=AF.Silu), reads=[bcv], writes=[bo])
                            f, bf = self.fmo.next()
                            self.transpose_out(lambda i: (o[:, i * 128:(i + 1) * 128], bo), ntile, 128, f[:, 0:n], [bf], "va")
                            S.dma("act", scr["VA_TM"][head].rearrange("(t p) c -> p t c", p=128)[:, tile0:tile0 + ntile, :],
                                  f[:, 0:n].rearrange("p (t c) -> p t c", c=128), reads=[bf], writes=[self.db("VA_TM", tile0 + i) for i in range(ntile)])
                            yield
                    if g3 == 2:
                        continue
                    qscale_bias = -0.5 * float(np.log(128.0)) if g3 == 0 else 0.0
                    name = "QA_FM" if g3 == 0 else "KA_FM"
                    for (t0, n, tile0, ntile) in groups:
                        sq, bsq = self.tmo.next()
                        S.op("act", lambda e: e.activation(out=sq[:, 0:n], in_=self.slrow[:, t0:t0 + n], func=AF.Square), reads=[self.b_slrow.sub(t0)], writes=[bsq])
                        ps, bps = S.ps()
                        S.op("pe", lambda e: e.matmul(out=ps[:, 0:n], lhsT=self.ones_b[:], rhs=sq[:, 0:n], start=True, stop=True),
                             reads=[self.b_ones_b, bsq], writes=[bps])
                        ta, bta = self.tmpB.next()
                        S.op("act", lambda e: e.activation(out=ta[:, 0:n], in_=ps[:, 0:n], func=AF.Ln, bias=self.epsb[:, 0:1]), reads=[bps, self.b_epsb], writes=[bta])
                        S.op("act", lambda e: e.activation(out=ta[:, 0:n], in_=ta[:, 0:n], func=AF.Exp, scale=-0.5, bias=self.epsb[:, 1 + g3:2 + g3]),
                             reads=[bta, self.b_epsb], writes=[bta])
                        o, bo = self.fmo.next()
                        S.op("dve", lambda e: e.tensor_tensor(out=o[:, 0:n], in0=self.slrow[:, t0:t0 + n], in1=ta[:, 0:n], op=ALU.mult),
                             reads=[self.b_slrow.sub(t0), bta], writes=[bo])
                        S.dma("act", scr[name][head][:, t0:t0 + n], o[:, 0:n], reads=[bo], writes=[self.db(name, tile0 + i) for i in range(ntile)])
                        if g3 == 1:
                            f, bf = self.zt.next()
                            self.transpose_out(lambda i: (o[:, i * 128:(i + 1) * 128], bo), ntile, 128, f[:, 0:n], [bf], "ka")
                            S.dma("act", scr["KA_TM"][head].rearrange("(t p) c -> p t c", p=128)[:, tile0:tile0 + ntile, :],
                                  f[:, 0:n].rearrange("p (t c) -> p t c", c=128), reads=[bf], writes=[self.db("KA_TM", tile0 + i) for i in range(ntile)])
                        yield


        def merge_gen():
            for g6 in range(6):
                self.marks.append(("  L%d mg%d" % (l, g6), {k: v.count for k, v in S.engs.items()}))
                wb, bwb = wsM.get(O_MG + g6 * 512, 512)
                for cl in range(4):
                    ct = g6 * 4 + cl
                    for (t0, n, tile0, ntile) in groups:
                        if last and t0 == 0:
                            continue
                        ps, bps = S.ps()
                        for kc in range(8):
                            S.op("pe", lambda e: e.matmul(out=ps[:, 0:n], lhsT=wb[:, kc, cl * 128:(cl + 1) * 128], rhs=self.hT[:, kc, t0:t0 + n],
                                                          start=(kc == 0), stop=(kc == 7)),
                                 reads=[self.b_hT[tile0 + i] for i in range(ntile)] + [bwb], writes=[bps])
                        o, bo = self.fmo.next()
                        S.op("act", lambda e: e.activation(out=o[:, 0:n], in_=ps[:, 0:n], func=AF.Sigmoid), reads=[bps], writes=[bo])
                        S.dma("act", scr["GM_FM"][ct * 128:(ct + 1) * 128, t0:t0 + n], o[:, 0:n], reads=[bo],
                              writes=[self.db("GM_FM%d" % ct, tile0 + i) for i in range(ntile)])
                        yield
        self.run_pair(heavy(), merge_gen())
        self.run_pair(afm_gen(), light())
        if self.stop == "p2":
            self.dump_p2()
        S.barrier()
        self.es.close()

    def a_scalars(self, l):
        S = self.S
        A = self.asc
        braw = self.SCR[:, :, 0:8]
        araw = self.SCR[:, :, 8:16]
        rs = [self.b_SCR]
        one = self.epsb[:, 3:4]

        def softplus_parts(x_ap, xb, neg):
            t1, b1 = self.tmp8.next()
            t2, b2 = self.tmp8.next()
            S.op("act", lambda e: e.activation(out=t1[:], in_=x_ap, func=AF.Abs), reads=xb, writes=[b1])
            S.op("act", lambda e: e.activation(out=t1[:], in_=t1[:], func=AF.Exp, scale=-1.0), reads=[b1], writes=[b1])
            S.op("act", lambda e: e.activation(out=t1[:], in_=t1[:], func=AF.Ln, bias=one), reads=[b1, self.b_epsb], writes=[b1])
            S.op("dve", lambda e: e.tensor_scalar(out=t2[:], in0=x_ap, scalar1=(-1.0 if neg else 1.0), scalar2=0.0, op0=ALU.mult, op1=ALU.max),
                 reads=xb, writes=[b2])
            return (t2, b2), (t1, b1)

        (m, bm), (l1, bl1) = softplus_parts(braw, rs, True)
        LNB, bLNB = A["LNB"]
        S.op("dve", lambda e: e.scalar_tensor_tensor(out=LNB[:], in0=m[:], scalar=-1.0, in1=l1[:], op0=ALU.mult, op1=ALU.subtract),
             reads=[bm, bl1], writes=[bLNB])
        BETA, bBETA = A["BETA"]
        S.op("act", lambda e: e.activation(out=BETA[:], in_=LNB[:], func=AF.Exp), reads=[bLNB], writes=[bBETA])
        xa, bxa = self.tmp8.next()
        dtb, bdtb = self.par["a_dt_bias"]
        S.op("dve", lambda e: e.tensor_tensor(out=xa[:], in0=araw, in1=dtb[:].unsqueeze(1).to_broadcast([128, NT, 8]), op=ALU.add),
             reads=rs + [bdtb], writes=[bxa])
        (m2, bm2), (l2, bl2) = softplus_parts(xa[:], [bxa], False)
        S.op("dve", lambda e: e.tensor_tensor(out=m2[:], in0=m2[:], in1=l2[:], op=ALU.add), reads=[bm2, bl2], writes=[bm2])
        alog, balog = self.par["a_log"]
        nea, bnea = self.st8.next()
        S.op("act", lambda e: e.activation(out=nea[:, 0:8], in_=alog[:], func=AF.Exp), reads=[balog], writes=[bnea])
        GRAW, bGRAW = A["GRAW"]
        S.op("dve", lambda e: e.scalar_tensor_tensor(out=GRAW[:], in0=m2[:], scalar=-1.0, in1=nea[:, 0:8].unsqueeze(1).to_broadcast([128, NT, 8]),
                                                     op0=ALU.mult, op1=ALU.mult),
             reads=[bm2, bnea], writes=[bGRAW])
        GC, bGC = A["GC"]
        GT, bGT = A["GT"]
        if self.stop == "p2b2":
            return
        gflat = GRAW[:].rearrange("p t c -> p (t c)")
        res = []
        for lhs, blhs in ((self.tri[:, 0, :], self.b_tri), (self.tri[:, 1, :], self.b_tri), (self.ones_f[:], self.b_ones_f)):
            ps, bps = S.ps()
            S.op("pe", lambda e: e.matmul(out=ps[:, 0:NT * 8], lhsT=lhs, rhs=gflat, start=True, stop=True), reads=[blhs, bGRAW], writes=[bps])
            res.append((ps[:, 0:NT * 8].rearrange("p (t c) -> p t c", c=8), bps))
        S.op("dve", lambda e: e.tensor_copy(out=GC[:, :, 0:4], in_=res[0][0][:, :, 0:4]), reads=[res[0][1]], writes=[bGC])
        S.op("dve", lambda e: e.tensor_copy(out=GC[:, :, 4:8], in_=res[1][0][:, :, 4:8]), reads=[res[1][1]], writes=[bGC])
        S.op("act", lambda e: e.copy(out=GT[:], in_=res[2][0]), reads=[res[2][1]], writes=[bGT])
        NEGG, bNEGG = A["NEGG"]
        S.op("dve", lambda e: e.tensor_scalar(out=NEGG[:], in0=GC[:], scalar1=-1.0, scalar2=None, op0=ALU.mult), reads=[bGC], writes=[bNEGG])
        LNBMG, bLNBMG = A["LNBMG"]
        S.op("dve", lambda e: e.tensor_tensor(out=LNBMG[:], in0=LNB[:], in1=GC[:], op=ALU.subtract), reads=[bLNB, bGC], writes=[bLNBMG])
        NEGEG, bNEGEG = A["NEGEG"]
        S.op("act", lambda e: e.activation(out=NEGEG[:], in_=GC[:], func=AF.Exp), reads=[bGC], writes=[bNEGEG])
        S.op("dve", lambda e: e.tensor_scalar(out=NEGEG[:], in0=NEGEG[:], scalar1=-1.0, scalar2=None, op0=ALU.mult), reads=[bNEGEG], writes=[bNEGEG])
        ETAIL, bETAIL = A["ETAIL"]
        S.op("dve", lambda e: e.tensor_tensor(out=ETAIL[:], in0=GT[:], in1=GC[:], op=ALU.subtract), reads=[bGT, bGC], writes=[bETAIL])
        S.op("act", lambda e: e.activation(out=ETAIL[:], in_=ETAIL[:], func=AF.Exp), reads=[bETAIL], writes=[bETAIL])
        EGL, bEGL = A["EGL"]
        S.op("act", lambda e: e.activation(out=EGL[:], in_=GT[:], func=AF.Exp), reads=[bGT], writes=[bEGL])

    def core_b(self, l, defer=False):
        S, scr, din = self.S, self.scr, self.din
        last = (l == self.nlayers - 1)
        if not defer:
            self.es = ExitStack()
        KBT = self.R1[:, 0:2 * NTOK].rearrange("p (g t) -> p g t", g=2)
        VBR = self.R1[:, 2 * NTOK:2 * NTOK + NT * 130].rearrange("p (t g c) -> p t g c", g=2, c=65)
        bKBT, bVBR = Buf("KBT"), Buf("VBR")
        S.dma("sp", KBT, scr["KB_FM"].rearrange("g r t -> r g t"), reads=self.dball("KB_FM"), writes=[bKBT])
        S.dma("sp", VBR, scr["VB_TM"].rearrange("(t p) g c -> p t g c", p=128), reads=self.dball("VB_TM"), writes=[bVBR])
        if l == 0:
            bst, bbst = self.tsb("bmst", [128, 2, 512])
            S.dma("sp", bst[:], din["k_bmask"].rearrange("a p n -> p a n"), writes=[bbst])
            S.op("pool", lambda e: e.tensor_copy(out=self.bmask[:], in_=bst[:]), reads=[bbst], writes=[self.b_bmask])
        qTr = self.trot("b_qT", [128, 4, 128], BF16, 2)
        pTr = self.trot("b_pT", [128, 5, 512], BF16, 2)
        zbr = self.trot("b_zb", [128, 512], BF16, 2)
        ybr = self.trot("b_yb", [128, 512], BF16, 2)
        obr = self.trot("b_ob", [128, 256], F32, 2)
        str_ = self.trot("b_st", [128, 16], F32, 6)
        fmo = self.trot("b_fmo", [128, 512], BF16, 2)
        qtiles = list(range(2, NT)) if last else list(range(NT))
        def b_gen():
            for qt in qtiles:
                zb, bzb = zbr.next()
                S.dma("sp", zb[:], scr["ZB"][qt * 128:(qt + 1) * 128, :], reads=[self.db("ZB", qt)], writes=[bzb])
                yb, byb = ybr.next()
                for g in range(2):
                    qT, bqT = qTr.next()
                    S.dma("sp", qT[:], scr["QB_FM"][g * 4:(g + 1) * 4].rearrange("h r t -> r h t")[:, :, qt * 128:(qt + 1) * 128],
                          reads=[self.db("QB_FM", qt)], writes=[bqT])
                    keys = [(0, None), (1, None)]
                    if qt >= 2:
                        if qt - 1 >= 2:
                            keys.append((qt - 1, 0))
                        keys.append((qt, None))
                        if qt + 1 < NT:
                            keys.append((qt + 1, 1))
                    pT, bpT = pTr.next()
                    for idx, (kt, m) in enumerate(keys):
                        ps, bps = S.ps()
                        S.op("pe", lambda e: e.matmul(out=ps[:], lhsT=KBT[:, g, kt * 128:(kt + 1) * 128], rhs=qT[:].rearrange("p h t -> p (h t)"),
                                                      start=True, stop=True), reads=[bKBT, bqT], writes=[bps])
                        S.op("act", lambda e: e.activation(out=pT[:, idx, :], in_=ps[:], func=AF.Exp, scale=0.125), reads=[bps], writes=[bpT.sub(idx)])
                        if m is not None:
                            S.op("pool", lambda e: e.tensor_tensor(out=pT[:, idx, :], in0=pT[:, idx, :], in1=self.bmask[:, m, :], op=ALU.mult),
                                 reads=[bpT.sub(idx), self.b_bmask], writes=[bpT.sub(idx)])
                    po, bpo = S.ps()
                    for h in range(4):
                        for idx, (kt, m) in enumerate(keys):
                            S.op("pe", lambda e: e.matmul(out=po[:, h * 65:(h + 1) * 65], lhsT=pT[:, idx, h * 128:(h + 1) * 128], rhs=VBR[:, kt, g, :],
                                                          start=(idx == 0), stop=(idx == len(keys) - 1)), reads=[bpT.sub(idx), bVBR], writes=[bpo])
                    st, bst_ = str_.next()
                    pov = po[:, 0:260].rearrange("p (h c) -> p h c", c=65)
                    S.op("act", lambda e: e.activation(out=st[:, 0:4], in_=self.SH2[:, qt, g * 4:(g + 1) * 4], func=AF.Exp), reads=[self.b_SH2], writes=[bst_])
                    S.op("dve", lambda e: e.tensor_tensor(out=st[:, 4:8], in0=pov[:, :, 64], in1=st[:, 0:4], op=ALU.add), reads=[bpo, bst_], writes=[bst_])
                    S.op("dve", lambda e: e.reciprocal(out=st[:, 8:12], in_=st[:, 4:8]), reads=[bst_], writes=[bst_])
                    ob, bob = obr.next()
                    S.op("dve", lambda e: e.tensor_tensor(out=ob[:].rearrange("p (h c) -> p h c", c=64), in0=pov[:, :, 0:64],
                                                          in1=st[:, 8:12].unsqueeze(2).to_broadcast([128, 4, 64]), op=ALU.mult),
                         reads=[bpo, bst_], writes=[bob])
                    S.op("pool", lambda e: e.tensor_tensor(out=yb[:, g * 256:(g + 1) * 256], in0=ob[:], in1=zb[:, g * 256:(g + 1) * 256], op=ALU.mult),
                         reads=[bob, bzb], writes=[byb.sub(g)])
                    yield
                self.y_out(1, qt, yb, byb, fmo)
                yield
        if defer:
            return b_gen()
        for _ in b_gen():
            pass
        S.barrier()
        self.es.close()

    def core_bc(self, l):
        self.es = ExitStack()
        gb = self.core_b(l, defer=True)
        gc = self.core_c(l, defer=True)
        self.run_pair(gb, gc)
        self.S.barrier()
        self.es.close()

    def y_out(self, br, tt, y, by, fmo, pool=None):
        S = self.S
        f, bf = fmo.next()
        self.transpose_out(lambda i: (y[:, i * 128:(i + 1) * 128], by), 4, 128, f[:], [bf], "y", pool=pool)
        S.dma("act", self.scr["Y_FM"][br].rearrange("(k p) t -> p k t", p=128)[:, :, tt * 128:(tt + 1) * 128],
              f[:].rearrange("p (k t) -> p k t", k=4), reads=[bf], writes=[self.db("Y_FM%d" % br, tt)])

    def core_c(self, l, defer=False):
        S, scr = self.S, self.scr
        last = (l == self.nlayers - 1)
        if not defer:
            self.es = ExitStack()
        one = self.epsb[:, 3:4]
        cd, bcd = self.par["c_decay"]
        c8 = self.trot("c_c8", [128, 8], F32, 6)
        t1, b1 = c8.next()
        t2, b2 = c8.next()
        LG, bLG = c8.next()
        S.op("act", lambda e: e.activation(out=t1[:], in_=cd[:], func=AF.Abs), reads=[bcd], writes=[b1])
        S.op("act", lambda e: e.activation(out=t1[:], in_=t1[:], func=AF.Exp, scale=-1.0), reads=[b1], writes=[b1])
        S.op("act", lambda e: e.activation(out=t1[:], in_=t1[:], func=AF.Ln, bias=one), reads=[b1, self.b_epsb], writes=[b1])
        S.op("dve", lambda e: e.tensor_scalar(out=t2[:], in0=cd[:], scalar1=-1.0, scalar2=0.0, op0=ALU.mult, op1=ALU.max), reads=[bcd], writes=[b2])
        S.op("dve", lambda e: e.scalar_tensor_tensor(out=LG[:], in0=t2[:], scalar=-1.0, in1=t1[:], op0=ALU.mult, op1=ALU.subtract),
             reads=[b1, b2], writes=[bLG])
        GAMC, bGAMC = c8.next()
        S.op("act", lambda e: e.activation(out=GAMC[:], in_=LG[:], func=AF.Exp, scale=float(CH)), reads=[bLG], writes=[bGAMC])
        KDEC, bKDEC = c8.next()
        S.op("dve", lambda e: e.tensor_tensor(out=KDEC[:], in0=LG[:], in1=self.cj[:], op=ALU.mult), reads=[bLG, self.b_cj], writes=[bKDEC])
        S.op("act", lambda e: e.activation(out=KDEC[:], in_=KDEC[:], func=AF.Exp), reads=[bKDEC], writes=[bKDEC])
        DM, bDM = self.tsb("c_DM", [128, 512])
        QDF, bQDF = self.tsb("c_QDF", [128, 512], BF16)
        QDB, bQDB = self.tsb("c_QDB", [128, 512], BF16)
        tm = self.trot("c_tm", [128, 128], F32, 2)
        for h in range(4):
            ta, bta = tm.next()
            tb, btb = tm.next()
            S.op("act", lambda e: e.activation(out=ta[:], in_=self.cm[:, 0, :], func=AF.Exp, scale=LG[:, h:h + 1]), reads=[self.b_cm, bLG], writes=[bta])
            S.op("dve", lambda e: e.tensor_tensor(out=ta[:], in0=ta[:], in1=self.cm[:, 2, :], op=ALU.mult), reads=[bta, self.b_cm], writes=[bta])
            S.op("act", lambda e: e.activation(out=tb[:], in_=self.cm[:, 1, :], func=AF.Exp, scale=LG[:, 4 + h:5 + h]), reads=[self.b_cm, bLG], writes=[btb])
            S.op("dve", lambda e: e.tensor_tensor(out=tb[:], in0=tb[:], in1=self.cm[:, 3, :], op=ALU.mult), reads=[btb, self.b_cm], writes=[btb])
            S.op("dve", lambda e: e.tensor_tensor(out=ta[:], in0=ta[:], in1=tb[:], op=ALU.add), reads=[bta, btb], writes=[bta])
            S.op("dve", lambda e: e.scalar_tensor_tensor(out=DM[:, h * 128:(h + 1) * 128], in0=self.ident_f[:], scalar=2.0, in1=ta[:], op0=ALU.mult, op1=ALU.add),
                 reads=[bta, self.b_ident_f], writes=[bDM])
            S.op("act", lambda e: e.activation(out=QDF[:, h * 128:(h + 1) * 128], in_=self.cm[:, 4, :], func=AF.Exp, scale=LG[:, h:h + 1]),
                 reads=[self.b_cm, bLG], writes=[bQDF])
            S.op("act", lambda e: e.activation(out=QDB[:, h * 128:(h + 1) * 128], in_=self.cm[:, 5, :], func=AF.Exp, scale=LG[:, 4 + h:5 + h]),
                 reads=[self.b_cm, bLG], writes=[bQDB])
        kTMr = [self.trot("c_kTM%d" % d, [128, 512], BF16, 2) for d in range(2)]
        vr = [self.trot("c_v%d" % d, [128, 512], BF16, 2) for d in range(3)]
        kdr = [self.trot("c_kd%d" % d, [128, 512], BF16, 2) for d in range(2)]
        sbfr = [self.trot("c_sbf%d" % d, [128, 512], BF16, 2) for d in range(2)]
        S32 = [self.tsb("c_S32_%d" % d, [128, 512]) for d in range(2)]
        SCN = ["SCF", "SCB"]

        def state_update(d, cc, kTM, bkTM, v, bv):
            kd, bkd = kdr[d].next()
            for h in range(4):
                S.op("act", lambda e: e.activation(out=kd[:, h * 128:(h + 1) * 128], in_=kTM[:, h * 128:(h + 1) * 128], func=AF.Copy,
                                                   scale=KDEC[:, d * 4 + h:d * 4 + h + 1]),
                     reads=[bkTM, bKDEC], writes=[bkd.sub(h)])
            ps, bps = S.ps()
            for h in range(4):
                S.op("pe", lambda e: e.matmul(out=ps[:, h * 128:(h + 1) * 128], lhsT=kd[:, h * 128:(h + 1) * 128], rhs=v[:, h * 128:(h + 1) * 128],
                                              start=True, stop=True), reads=[bkd, bv], writes=[bps])
            s32, bs32 = S32[d]
            for h in range(4):
                S.op("dve", lambda e: e.scalar_tensor_tensor(out=s32[:, h * 128:(h + 1) * 128], in0=s32[:, h * 128:(h + 1) * 128],
                                                             scalar=GAMC[:, d * 4 + h:d * 4 + h + 1], in1=ps[:, h * 128:(h + 1) * 128],
                                                             op0=ALU.mult, op1=ALU.add),
                     reads=[bs32.sub(h), bGAMC, bps], writes=[bs32.sub(h)])

        def state_pass(d):
            S.op("pool", lambda e: e.memset(S32[d][0][:], 0.0), writes=[S32[d][1]])
            order = list(range(NT)) if d == 0 else [1, 0] + list(range(NT - 1, 1, -1))
            for cc in order:
                sbf, bsbf = sbfr[d].next()
                S.op("act", lambda e: e.copy(out=sbf[:], in_=S32[d][0][:]), reads=[S32[d][1]], writes=[bsbf])
                S.dma("act", scr[SCN[d]][cc], sbf[:], reads=[bsbf], writes=[self.db(SCN[d], cc)])
                kTM, bkTM = kTMr[d].next()
                v, bv = vr[d].next()
                S.dma("sp", kTM[:], scr["KC_TM"][cc * 128:(cc + 1) * 128, :], reads=[self.db("KC_TM", cc)], writes=[bkTM])
                S.dma("sp", v[:], scr["VC_TM"][cc * 128:(cc + 1) * 128, :], reads=[self.db("VC_TM", cc)], writes=[bv])
                yield
                state_update(d, cc, kTM, bkTM, v, bv)
                yield

        qTr = self.trot("c_qT", [128, 512], BF16, 2)
        kTr = self.trot("c_kT", [128, 512], BF16, 2)
        sbr = self.trot("c_sb", [128, 512], BF16, 2)
        sfr = self.trot("c_sf", [128, 512], BF16, 2)
        zcr = self.trot("c_zc", [128, 512], BF16, 2)
        qkr = self.trot("c_qk", [128, 512], BF16, 2)
        qdr = self.trot("c_qd", [128, 512], BF16, 2)
        ofr = self.trot("c_of", [128, 512], F32, 2)
        t5r = self.trot("c_t5", [128, 512], F32, 1)
        ycr = self.trot("c_yc", [128, 512], BF16, 2)
        stc = self.trot("c_st", [128, 24], F32, 3)
        fmo = self.trot("c_fmo", [128, 512], BF16, 2)
        cnw, bcnw = self.par["c_norm_w"]
        eps = self.epsb[:, 0:1]
        def out_pass():
            for cc in range(NT):
                need_out = (cc >= 2) or (not last)
                if need_out:
                    v, bv = vr[2].next()
                    S.dma("sp", v[:], scr["VC_TM"][cc * 128:(cc + 1) * 128, :], reads=[self.db("VC_TM", cc)], writes=[bv])
                    qT, bqT = qTr.next()
                    kT, bkT = kTr.next()
                    sb, bsb = sbr.next()
                    zc, bzc = zcr.next()
                    S.dma("sp", qT[:].rearrange("p (h t) -> p h t", h=4), scr["QC_FM"].rearrange("h p t -> p h t")[:, :, cc * 128:(cc + 1) * 128],
                          reads=[self.db("QC_FM", cc)], writes=[bqT])
                    S.dma("sp", kT[:].rearrange("p (h t) -> p h t", h=4), scr["KC_FM"].rearrange("h p t -> p h t")[:, :, cc * 128:(cc + 1) * 128],
                          reads=[self.db("KC_FM", cc)], writes=[bkT])
                    S.dma("sp", sb[:], scr["SCB"][cc], reads=[self.db("SCB", cc)], writes=[bsb])
                    S.dma("sp", zc[:], scr["ZC"][cc * 128:(cc + 1) * 128, :], reads=[self.db("ZC", cc)], writes=[bzc])
                    sbf, bsbf = sfr.next()
                    S.dma("sp", sbf[:], scr["SCF"][cc], reads=[self.db("SCF", cc)], writes=[bsbf])
                    ps1, bps1 = S.ps()
                    for h in range(4):
                        S.op("pe", lambda e: e.matmul(out=ps1[:, h * 128:(h + 1) * 128], lhsT=kT[:, h * 128:(h + 1) * 128], rhs=qT[:, h * 128:(h + 1) * 128],
                                                      start=True, stop=True), reads=[bkT, bqT], writes=[bps1])
                    qk, bqk = qkr.next()
                    S.op("dve", lambda e: e.tensor_tensor(out=qk[:], in0=ps1[:], in1=DM[:], op=ALU.mult), reads=[bps1, bDM], writes=[bqk])
                    qdf, bqdf = qdr.next()
                    qdb, bqdb = qdr.next()
                    S.op("pool", lambda e: e.tensor_tensor(out=qdf[:], in0=qT[:], in1=QDF[:], op=ALU.mult), reads=[bqT, bQDF], writes=[bqdf])
                    S.op("pool", lambda e: e.tensor_tensor(out=qdb[:], in0=qT[:], in1=QDB[:], op=ALU.mult), reads=[bqT, bQDB], writes=[bqdb])
                    po, bpo = S.ps()
                    for h in range(4):
                        sl = slice(h * 128, (h + 1) * 128)
                        S.op("pe", lambda e: e.matmul(out=po[:, sl], lhsT=qk[:, sl], rhs=v[:, sl], start=True, stop=False), reads=[bqk, bv], writes=[bpo])
                        S.op("pe", lambda e: e.matmul(out=po[:, sl], lhsT=qdf[:, sl], rhs=sbf[:, sl], start=False, stop=False), reads=[bqdf, bsbf], writes=[bpo])
                        S.op("pe", lambda e: e.matmul(out=po[:, sl], lhsT=qdb[:, sl], rhs=sb[:, sl], start=False, stop=True), reads=[bqdb, bsb], writes=[bpo])
                    of, bof = ofr.next()
                    t5, bt5 = t5r.next()
                    st, bst = stc.next()
                    S.op("act", lambda e: e.copy(out=of[:], in_=po[:]), reads=[bpo], writes=[bof])
                    S.op("act", lambda e: e.activation(out=t5[:], in_=po[:], func=AF.Square), reads=[bpo], writes=[bt5])
                    S.op("dve", lambda e: e.tensor_reduce(out=st[:, 0:4], in_=of[:].rearrange("p (h c) -> p h c", h=4), axis=AX.X, op=ALU.add), reads=[bof], writes=[bst])
                    S.op("dve", lambda e: e.tensor_reduce(out=st[:, 4:8], in_=t5[:].rearrange("p (h c) -> p h c", h=4), axis=AX.X, op=ALU.add), reads=[bt5], writes=[bst])
                    S.op("dve", lambda e: e.tensor_scalar(out=st[:, 8:12], in0=st[:, 0:4], scalar1=1.0 / 128, scalar2=None, op0=ALU.mult), reads=[bst], writes=[bst])
                    S.op("dve", lambda e: e.tensor_tensor(out=st[:, 12:16], in0=st[:, 8:12], in1=st[:, 8:12], op=ALU.mult), reads=[bst], writes=[bst])
                    S.op("dve", lambda e: e.scalar_tensor_tensor(out=st[:, 16:20], in0=st[:, 4:8], scalar=1.0 / 128, in1=st[:, 12:16], op0=ALU.mult, op1=ALU.subtract),
                         reads=[bst], writes=[bst])
                    S.op("act", lambda e: e.activation(out=st[:, 20:24], in_=st[:, 16:20], func=AF.Ln, bias=eps), reads=[bst, self.b_epsb], writes=[bst])
                    S.op("act", lambda e: e.activation(out=st[:, 20:24], in_=st[:, 20:24], func=AF.Exp, scale=-0.5), reads=[bst], writes=[bst])
                    ofv = of[:].rearrange("p (h c) -> p h c", h=4)
                    S.op("dve", lambda e: e.tensor_tensor(out=ofv, in0=ofv, in1=st[:, 8:12].unsqueeze(2).to_broadcast([128, 4, 128]), op=ALU.subtract),
                         reads=[bof, bst], writes=[bof])
                    S.op("dve", lambda e: e.tensor_tensor(out=ofv, in0=ofv, in1=st[:, 20:24].unsqueeze(2).to_broadcast([128, 4, 128]), op=ALU.mult),
                         reads=[bof, bst], writes=[bof])
                    S.op("pool", lambda e: e.tensor_tensor(out=of[:], in0=of[:], in1=cnw[:], op=ALU.mult), reads=[bof, bcnw], writes=[bof])
                    yc, byc = ycr.next()
                    S.op("pool", lambda e: e.tensor_tensor(out=yc[:], in0=of[:], in1=zc[:], op=ALU.mult), reads=[bof, bzc], writes=[byc])
                    self.y_out(2, cc, yc, byc, fmo)
                yield

        def c_gen():
            gens = [state_pass(0), state_pass(1)]
            while gens:
                for g_ in list(gens):
                    try:
                        next(g_)
                        yield
                    except StopIteration:
                        gens.remove(g_)
            yield from out_pass()
        if defer:
            return c_gen()
        for _ in c_gen():
            pass
        S.barrier()
        self.es.close()

    def core_a(self, l):
        S, scr = self.S, self.scr
        last = (l == self.nlayers - 1)
        self.es = ExitStack()
        A = self.asc
        import os
        KPRE = int(os.environ.get("KPRE", "2"))
        r1_next = [0]

        def mkrot(name, k, use_r1=True):
            items = []
            for i in range(k):
                if use_r1 and r1_next[0] < 68:
                    j = r1_next[0]
                    r1_next[0] += 1
                    items.append((self.R1[:, j * 512:(j + 1) * 512], Buf("%s%d" % (name, i))))
                else:
                    t = self._talloc("a_" + name, [128, 512], BF16)
                    items.append((t[:], Buf("%s%d" % (name, i))))
            r = Rot.__new__(Rot)
            r.items = items
            r.i = 0
            return r

        def R(n, dt=BF16, k=2):
            r = self.trot("a_" + n, [128, 512], dt, k)
            r.items = [(t[:], b) for t, b in r.items]
            return r
        ofr, t5r = R("of", F32, 3), R("t5", F32, 2)
        zar, yar, fmo = R("za"), R("ya"), R("fmo")
        sta = self.trot("a_st", [128, 16], F32, 4)
        nmask, bnmask = self.tsb("a_nmask", [128, 14, 128], BF16)
        nmst, bnmst = self.wst.next()
        nmv = nmst[:].rearrange("p k n -> p (k n)")[:, 0:14 * 128].rearrange("p (a n) -> p a n", a=14)
        S.dma("sp", nmv, self.din["k_nm"].rearrange("a p n -> p a n"), writes=[bnmst])
        S.op("pool", lambda e: e.tensor_copy(out=nmask[:], in_=nmv), reads=[bnmst], writes=[bnmask])
        anw, banw = self.par["a_norm_w"]
        eps = self.epsb[:, 0:1]
        H = [slice(h * 128, (h + 1) * 128) for h in range(4)]
        v4 = lambda t: t[:].rearrange("p (h c) -> p h c", h=4)
        orders = [list(range(NT)), [1, 0] + list(range(NT - 1, 1, -1))]
        oa_written = set()
        from collections import deque
        free_banks = deque(range(8))

        def acq():
            while not free_banks:
                yield
            bk = free_banks.popleft()
            ps, bps = S.psum[bk]
            return ps, bps, bk

        def rel(bk):
            free_banks.append(bk)

        def mm4(lhs, blhs, rhs, brhs):
            ps, bps, bk = yield from acq()
            for h in range(4):
                S.op("pe", lambda e: e.matmul(out=ps[:, H[h]], lhsT=lhs[:, H[h]], rhs=rhs[:, H[h]], start=True, stop=True), reads=[blhs, brhs], writes=[bps])
            return ps, bps, bk

        def tr4(src, bsrc):
            ps, bps, bk = yield from acq()
            psb = ps[:].bitcast(BF16)
            for h in range(4):
                S.op("pe", lambda e: e.transpose(out=psb[:, H[h]], in_=src[:, H[h]], identity=self.ident_b[:]), reads=[bsrc, self.b_ident_b], writes=[bps])
            return psb, bps, bk

        class DirBufs:
            pass
        DB = []
        for d in range(2):
            o = DirBufs()
            for n in ("kT", "kTM", "vTM", "qT", "qk", "qd", "kt", "Pf"):
                setattr(o, n, mkrot("%s_%d" % (n, d), KPRE + 1))
            o.tsets = []
            for ts in range(KPRE):
                tsd = {n: mkrot("%s_%d_%d" % (n, d, ts), 1).items[0] for n in ("eg", "Ma", "Ml", "Pa", "Pb", "W1", "X", "T")}
                for n in ("F0", "F1", "F2"):
                    tsd[n] = (self._talloc("a_%s_%d_%d" % (n, d, ts), [128, 512], F32)[:], Buf("%s_%d_%d" % (n, d, ts)))
                o.tsets.append(tsd)
            for n in ("Y", "vn", "sbf"):
                setattr(o, n, R("%s_%d" % (n, d)))
            o.S32 = self.tsb("a_S32_%d" % d, [128, 512])
            DB.append(o)

        def prep(d, cc, out, ts):
            B = DB[d]
            TS = B.tsets[ts]
            need_out = (cc >= 2) or (not last)
            out["need_out"] = need_out
            col = lambda name, h: A[name][0][:, cc, d * 4 + h:d * 4 + h + 1]
            tok = slice(cc * 128, (cc + 1) * 128)
            m_incl = self.amask[:, 2 * d, :]
            m_strict = self.amask[:, 2 * d + 1, :]
            nmb = lambda lev: nmask[:, d * 7 + lev, :].unsqueeze(1).to_broadcast([128, 4, 128])
            kT, bkT = B.kT.next()
            kTM, bkTM = B.kTM.next()
            vTM, bvTM = B.vTM.next()
            S.dma("sp", kT.rearrange("p (h t) -> p h t", h=4), scr["KA_FM"].rearrange("h p t -> p h t")[:, :, tok], reads=[self.db("KA_FM", cc)], writes=[bkT])
            S.dma("sp", kTM.rearrange("p (h c) -> p h c", h=4), scr["KA_TM"].rearrange("h t c -> t h c")[tok, :, :], reads=[self.db("KA_TM", cc)], writes=[bkTM])
            S.dma("sp", vTM.rearrange("p (h c) -> p h c", h=4), scr["VA_TM"].rearrange("h t c -> t h c")[tok, :, :], reads=[self.db("VA_TM", cc)], writes=[bvTM])
            out.update(kT=(kT, bkT), vTM=(vTM, bvTM))
            if need_out:
                qT, bqT = B.qT.next()
                S.dma("sp", qT.rearrange("p (h t) -> p h t", h=4), scr["QA_FM"].rearrange("h p t -> p h t")[:, :, tok], reads=[self.db("QA_FM", cc)], writes=[bqT])
            yield
            dg, bdg = TS["F0"]
            for h in range(4):
                S.op("dve", lambda e: e.tensor_scalar(out=dg[:, H[h]], in0=self.ident_f[:], scalar1=col("GC", h), scalar2=None, op0=ALU.mult),
                     reads=[self.b_ident_f, A["GC"][1]], writes=[bdg])
            p3, bp3, k3 = yield from acq()
            S.op("pe", lambda e: e.matmul(out=p3[:], lhsT=self.ones_f[:], rhs=dg[:], start=True, stop=True), reads=[self.b_ones_f, bdg], writes=[bp3])
            yield
            Dm2, bDm2 = TS["F1"]
            for h in range(4):
                S.op("dve", lambda e: e.scalar_tensor_tensor(out=Dm2[:, H[h]], in0=p3[:, H[h]], scalar=col("LNBMG", h), in1=m_strict, op0=ALU.add, op1=ALU.add),
                     reads=[bp3, A["LNBMG"][1], self.b_amask], writes=[bDm2])
            if need_out:
                Dm, bDm = TS["F2"]
                for h in range(4):
                    S.op("dve", lambda e: e.scalar_tensor_tensor(out=Dm[:, H[h]], in0=p3[:, H[h]], scalar=col("NEGG", h), in1=m_incl, op0=ALU.add, op1=ALU.add),
                         reads=[bp3, A["NEGG"][1], self.b_amask], writes=[bDm])
                eg, beg = TS["eg"]
                S.op("act", lambda e: e.activation(out=eg, in_=p3[:], func=AF.Exp), reads=[bp3, bDm, bDm2], writes=[beg])
            rel(k3)
            yield
            decb, bdecb = TS["F0"]
            S.op("act", lambda e: e.activation(out=decb[:], in_=Dm2[:], func=AF.Exp), reads=[bDm2], writes=[bdecb])
            p1, bp1, k1 = yield from mm4(kT, bkT, kT, bkT)
            yield
            Ma, bMa = TS["Ma"]
            S.op("dve", lambda e: e.tensor_tensor(out=Ma, in0=p1[:], in1=decb[:], op=ALU.mult), reads=[bp1, bdecb], writes=[bMa])
            rel(k1)
            yield
            psb, bps, kb = yield from tr4(Ma, bMa)
            Ml, bMl = TS["Ml"]
            S.op("act", lambda e: e.copy(out=Ml, in_=psb[:, 0:512]), reads=[bps], writes=[bMl])
            rel(kb)
            P, bP = TS["Pa"]
            S.op("pool", lambda e: e.tensor_tensor(out=v4(P), in0=v4(Ma), in1=nmb(0), op=ALU.mult), reads=[bMa, bnmask], writes=[bP])
            S.op("pool", lambda e: e.tensor_tensor(out=v4(P), in0=v4(P), in1=self.ident_b[:].unsqueeze(1).to_broadcast([128, 4, 128]), op=ALU.add),
                 reads=[bP, self.b_ident_b], writes=[bP])
            yield
            if need_out:
                dec, bdec = TS["F1"]
                S.op("act", lambda e: e.activation(out=dec[:], in_=Dm[:], func=AF.Exp), reads=[bDm], writes=[bdec])
                p2, bp2, k2 = yield from mm4(kT, bkT, qT, bqT)
                yield
                qk, bqk = B.qk.next()
                S.op("dve", lambda e: e.tensor_tensor(out=qk, in0=p2[:], in1=dec[:], op=ALU.mult), reads=[bp2, bdec], writes=[bqk])
                rel(k2)
                qd, bqd = B.qd.next()
                S.op("pool", lambda e: e.tensor_tensor(out=qd, in0=qT, in1=eg, op=ALU.mult), reads=[bqT, beg], writes=[bqd])
                out.update(qk=(qk, bqk), qd=(qd, bqd))
                yield
            kt, bkt = B.kt.next()
            for h in range(4):
                S.op("act", lambda e: e.activation(out=kt[:, H[h]], in_=kTM[:, H[h]], func=AF.Copy, scale=col("ETAIL", h)),
                     reads=[bkTM, A["ETAIL"][1]], writes=[bkt])
            out.update(kt=(kt, bkt))
            yield
            for lev in range(1, 7):
                psw, bpsw, kw = yield from mm4(Ml, bMl, P, bP)
                psb, bps, kb = yield from tr4(P, bP)
                yield
                W1, bW1 = TS["W1"]
                S.op("act", lambda e: e.copy(out=W1, in_=psw[:]), reads=[bpsw], writes=[bW1])
                rel(kw)
                X, bX = TS["X"]
                S.op("dve", lambda e: e.tensor_copy(out=X, in_=psb[:, 0:512]), reads=[bps], writes=[bX])
                rel(kb)
                yield
                ps2, bps2, k2 = yield from mm4(X, bX, W1, bW1)
                yield
                tm_, btm_ = TS["T"]
                S.op("dve", lambda e: e.tensor_tensor(out=v4(tm_), in0=ps2[:].rearrange("p (h c) -> p h c", h=4), in1=nmb(lev), op=ALU.mult),
                     reads=[bps2, bnmask], writes=[btm_])
                rel(k2)
                Pn, bPn = (B.Pf.next() if lev == 6 else TS["Pb" if lev % 2 == 1 else "Pa"])
                S.op("pool", lambda e: e.tensor_tensor(out=Pn, in0=P, in1=tm_, op=ALU.add), reads=[bP, btm_], writes=[bPn])
                P, bP = Pn, bPn
                yield
            out.update(P=(P, bP))

        def scan(d, cc, ops, st):
            B = DB[d]
            need_out = ops["need_out"]
            col = lambda name, h: A[name][0][:, cc, d * 4 + h:d * 4 + h + 1]
            tok = slice(cc * 128, (cc + 1) * 128)
            kT, bkT = ops["kT"]
            vTM, bvTM = ops["vTM"]
            kt, bkt = ops["kt"]
            P, bP = ops["P"]
            sbf, bsbf = st["sbf"]
            s32, bs32 = B.S32
            px, bpx, kx = yield from mm4(kT, bkT, sbf, bsbf)
            yield
            Y, bY = B.Y.next()
            for h in range(4):
                S.op("dve", lambda e: e.scalar_tensor_tensor(out=Y[:, H[h]], in0=px[:, H[h]], scalar=col("NEGEG", h), in1=vTM[:, H[h]], op0=ALU.mult, op1=ALU.add),
                     reads=[bpx, A["NEGEG"][1], bvTM], writes=[bY])
            rel(kx)
            yield
            pz, bpz, kz = yield from mm4(P, bP, Y, bY)
            yield
            vn, bvn = B.vn.next()
            for h in range(4):
                S.op("act", lambda e: e.activation(out=vn[:, H[h]], in_=pz[:, H[h]], func=AF.Copy, scale=col("BETA", h)), reads=[bpz, A["BETA"][1]], writes=[bvn])
            rel(kz)
            yield
            pS, bpS, kS = yield from mm4(kt, bkt, vn, bvn)
            if need_out:
                qk, bqk = ops["qk"]
                qd, bqd = ops["qd"]
                po, bpo, ko = yield from acq()
                for h in range(4):
                    S.op("pe", lambda e: e.matmul(out=po[:, H[h]], lhsT=qd[:, H[h]], rhs=sbf[:, H[h]], start=True, stop=False), reads=[bqd, bsbf], writes=[bpo])
                    S.op("pe", lambda e: e.matmul(out=po[:, H[h]], lhsT=qk[:, H[h]], rhs=vn[:, H[h]], start=False, stop=True), reads=[bqk, bvn], writes=[bpo])
            yield
            for h in range(4):
                S.op("dve", lambda e: e.scalar_tensor_tensor(out=s32[:, H[h]], in0=s32[:, H[h]], scalar=col("EGL", h), in1=pS[:, H[h]], op0=ALU.mult, op1=ALU.add),
                     reads=[bs32, A["EGL"][1], bpS], writes=[bs32])
            rel(kS)
            sbf2, bsbf2 = B.sbf.next()
            S.op("act", lambda e: e.copy(out=sbf2, in_=s32[:]), reads=[bs32], writes=[bsbf2])
            st["sbf"] = (sbf2, bsbf2)
            yield
            if need_out:
                first = cc not in oa_written
                oa_written.add(cc)
                of, bof = ofr.next()
                if first:
                    S.op("act", lambda e: e.copy(out=of[:], in_=po[:]), reads=[bpo], writes=[bof])
                    rel(ko)
                    S.dma("act", scr["OA"][tok, :], of[:], reads=[bof], writes=[self.db("OA", cc)])
                    yield
                else:
                    S.dma("sp", of[:], scr["OA"][tok, :], reads=[self.db("OA", cc)], writes=[bof])
                    za, bza = zar.next()
                    S.dma("sp", za, scr["ZA"][tok, :], reads=[self.db("ZA", cc)], writes=[bza])
                    yield
                    S.op("dve", lambda e: e.tensor_tensor(out=of[:], in0=po[:], in1=of[:], op=ALU.add), reads=[bpo, bof], writes=[bof])
                    rel(ko)
                    t5, bt5 = t5r.next()
                    st_, bst = sta.next()
                    S.op("act", lambda e: e.activation(out=t5[:], in_=of[:], func=AF.Square), reads=[bof], writes=[bt5])
                    yield
                    S.op("dve", lambda e: e.tensor_reduce(out=st_[:, 0:4], in_=t5[:].rearrange("p (h c) -> p h c", h=4), axis=AX.X, op=ALU.add), reads=[bt5], writes=[bst])
                    S.op("act", lambda e: e.activation(out=st_[:, 4:8], in_=st_[:, 0:4], func=AF.Ln, scale=1.0 / 128, bias=eps), reads=[bst, self.b_epsb], writes=[bst])
                    S.op("act", lambda e: e.activation(out=st_[:, 8:12], in_=st_[:, 4:8], func=AF.Exp, scale=-0.5), reads=[bst], writes=[bst])
                    yield
                    ofv = of[:].rearrange("p (h c) -> p h c", h=4)
                    S.op("dve", lambda e: e.tensor_tensor(out=ofv, in0=ofv, in1=st_[:, 8:12].unsqueeze(2).to_broadcast([128, 4, 128]), op=ALU.mult), reads=[bof, bst], writes=[bof])
                    S.op("pool", lambda e: e.tensor_tensor(out=ofv, in0=ofv, in1=anw[:].unsqueeze(1).to_broadcast([128, 4, 128]), op=ALU.mult), reads=[bof, banw], writes=[bof])
                    yield
                    ya, bya = yar.next()
                    S.op("pool", lambda e: e.tensor_tensor(out=ya, in0=of[:], in1=za, op=ALU.mult), reads=[bof, bza], writes=[bya])
                    f, bf = fmo.next()
                    psb, bps, kb = yield from tr4(ya, bya)
                    S.op("act", lambda e: e.copy(out=f, in_=psb[:, 0:512]), reads=[bps], writes=[bf])
                    rel(kb)
                    S.dma("act", scr["Y_FM"][0].rearrange("(k p) t -> p k t", p=128)[:, :, tok], f.rearrange("p (k t) -> p k t", k=4),
                          reads=[bf], writes=[self.db("Y_FM0", cc)])
                    yield

        def chain(d):
            B = DB[d]
            s32, bs32 = B.S32
            S.op("pool", lambda e: e.memset(s32[:], 0.0), writes=[bs32])
            sbf, bsbf = B.sbf.next()
            S.op("pool", lambda e: e.memset(sbf, 0.0), writes=[bsbf])
            st = {"sbf": (sbf, bsbf)}
            order = orders[d]
            n = len(order)
            outs = [dict() for _ in range(n)]
            started = 0
            active = []
            done = set()

            def start_upto(j):
                nonlocal started
                while started <= min(j, n - 1):
                    active.append((started, prep(d, order[started], outs[started], started % KPRE)))
                    started += 1

            def step_preps():
                for item in list(active):
                    try:
                        next(item[1])
                    except StopIteration:
                        active.remove(item)
                        done.add(item[0])
            start_upto(0)
            while 0 not in done:
                step_preps()
                yield
            for i, cc in enumerate(order):
                start_upto(i + KPRE)
                sc = scan(d, cc, outs[i], st)
                sc_done = False
                while not sc_done or (i + 1 < n and (i + 1) not in done):
                    if not sc_done:
                        try:
                            next(sc)
                        except StopIteration:
                            sc_done = True
                    step_preps()
                    yield

        gens = [chain(0), chain(1)]
        while gens:
            for g_ in list(gens):
                try:
                    next(g_)
                except StopIteration:
                    gens.remove(g_)
        S.barrier()
        self.es.close()

    def phase5(self, l):
        S, scr, din = self.S, self.scr, self.din
        last = (l == self.nlayers - 1)
        self.es = ExitStack()
        wbr = self.R1[:, 14336:26624].rearrange("p (r n) -> p r n", n=1024)
        wo = self.R1[:, 26624:34816].rearrange("p (r n) -> p r n", n=1024)
        bwbr, bwo = Buf("wbr"), Buf("wo")

        def load_into(src_view, dst, bdst, nk):
            st, bst = self.wst.next()
            S.dma("sp", st[:, 0:nk, :], src_view, writes=[bst])
            S.op("pool", lambda e: e.tensor_copy(out=dst, in_=st[:, 0:nk, :]), reads=[bst], writes=[bdst])
        wbsrc = din["w_branch"][l].rearrange("b (k p) n -> p (b k) n", p=128)
        for half in range(2):
            for r0, nk in ((0, 8), (8, 4)):
                load_into(wbsrc[:, r0:r0 + nk, half * 512:(half + 1) * 512], wbr[:, r0:r0 + nk, half * 512:(half + 1) * 512], bwbr, nk)
        wosrc = din["w_out"][l].rearrange("(k p) n -> p k n", p=128)
        for half in range(2):
            load_into(wosrc[:, :, half * 512:(half + 1) * 512], wo[:, :, half * 512:(half + 1) * 512], bwo, 8)
        gate_bc, bgate = self.tsb("gate_bc", [128, 2, 1024])
        for s_ in range(2):
            if last and s_ == 1:
                continue
            self.bc_rows(lambda half: gate_bc[:, s_, half * 512:(half + 1) * 512], bgate, lambda kc: self.mod[:, 16 + kc, s_:s_ + 1], self.b_mod, 8)
        if last:
            fnw, bfnw = self.tsb("fnw_bc", [128, 1024])
            S.dma("sp", fnw[:], din["final_norm_w"].partition_broadcast(128), writes=[bfnw])
        yTr = self.trot("p5_yT", [128, 12, 512], BF16, 1)
        gmr = self.trot("p5_gm", [128, 512], BF16, 3)
        accr = self.trot("p5_acc", [128, 512], F32, 2)
        tmr = self.trot("p5_tm", [128, 512], F32, 2)
        mTr = self.trot("p5_mT", [128, 8, 512], BF16, 2)
        xtr = self.trot("p5_xt", [128, 1024], F32, 2)
        t1r = self.trot("p5_t1", [128, 1024], F32, 2)
        sqr, bsqr = self.tsb("p5_sq", [128, 1024])
        st5 = self.trot("p5_st", [128, 4], F32, 3)
        for (t0, n, tile0, ntile) in self.tok_groups():
            if last and t0 == 0:
                continue
            s_ = 1 if t0 == 0 else 0
            yT, byT = yTr.next()
            for br in range(3):
                S.dma("sp", yT[:, br * 4:(br + 1) * 4, 0:n], scr["Y_FM"][br].rearrange("(k p) t -> p k t", p=128)[:, :, t0:t0 + n],
                      reads=[self.db("Y_FM%d" % br, tile0 + i) for i in range(ntile)], writes=[byT.sub(br)])
            mT, bmT = mTr.next()
            for dt in range(8):
                acc, bacc = accr.next()
                for br in range(3):
                    ct = br * 8 + dt
                    gm, bgm = gmr.next()
                    S.dma("sp", gm[:, 0:n], scr["GM_FM"][ct * 128:(ct + 1) * 128, t0:t0 + n],
                          reads=[self.db("GM_FM%d" % ct, tile0 + i) for i in range(ntile)], writes=[bgm])
                    ps, bps = S.ps()
                    for kc in range(4):
                        S.op("pe", lambda e: e.matmul(out=ps[:, 0:n], lhsT=wbr[:, br * 4 + kc, dt * 128:(dt + 1) * 128], rhs=yT[:, br * 4 + kc, 0:n],
                                                      start=(kc == 0), stop=(kc == 3)), reads=[bwbr, byT.sub(br)], writes=[bps])
                    if br == 0:
                        S.op("dve", lambda e: e.tensor_tensor(out=acc[:, 0:n], in0=ps[:, 0:n], in1=gm[:, 0:n], op=ALU.mult), reads=[bps, bgm], writes=[bacc])
                    else:
                        tm, btm = tmr.next()
                        S.op("dve", lambda e: e.tensor_tensor(out=tm[:, 0:n], in0=ps[:, 0:n], in1=gm[:, 0:n], op=ALU.mult), reads=[bps, bgm], writes=[btm])
                        if br == 1:
                            S.op("pool", lambda e: e.tensor_tensor(out=acc[:, 0:n], in0=acc[:, 0:n], in1=tm[:, 0:n], op=ALU.add), reads=[bacc, btm], writes=[bacc])
                        else:
                            S.op("pool", lambda e: e.tensor_tensor(out=mT[:, dt, 0:n], in0=acc[:, 0:n], in1=tm[:, 0:n], op=ALU.add), reads=[bacc, btm], writes=[bmT.sub(dt)])
            for ti in range(ntile):
                tt = tile0 + ti
                xt, bxt = xtr.next()
                if tt < 2:
                    src = (din["ctx"] if l == 0 else scr["CTXS"])[tt * 128:(tt + 1) * 128, :]
                    rd = [] if l == 0 else [self.db("CTXS", tt)]
                else:
                    src = (din["x"] if l == 0 else scr["XS"])[(tt - 2) * 128:(tt - 1) * 128, :]
                    rd = [] if l == 0 else [self.db("XS", tt)]
                S.dma("sp", xt[:], src, reads=rd, writes=[bxt])
                t1, bt1 = t1r.next()
                for cg in range(2):
                    ps, bps = S.ps()
                    for kc in range(8):
                        S.op("pe", lambda e: e.matmul(out=ps[:], lhsT=mT[:, kc, ti * 128:(ti + 1) * 128], rhs=wo[:, kc, cg * 512:(cg + 1) * 512],
                                                      start=(kc == 0), stop=(kc == 7)), reads=[bmT, bwo], writes=[bps])
                    S.op("dve", lambda e: e.tensor_tensor(out=t1[:, cg * 512:(cg + 1) * 512], in0=ps[:], in1=gate_bc[:, s_, cg * 512:(cg + 1) * 512], op=ALU.mult),
                         reads=[bps, bgate], writes=[bt1.sub(cg)])
                S.op("pool", lambda e: e.tensor_tensor(out=t1[:], in0=t1[:], in1=xt[:], op=ALU.add), reads=[bt1, bxt], writes=[bt1])
                if not last:
                    if tt < 2:
                        S.dma("act", scr["CTXS"][tt * 128:(tt + 1) * 128, :], t1[:], reads=[bt1], writes=[self.db("CTXS", tt)])
                    else:
                        S.dma("act", scr["XS"][(tt - 2) * 128:(tt - 1) * 128, :], t1[:], reads=[bt1], writes=[self.db("XS", tt)])
                else:
                    st, bst = st5.next()
                    S.op("act", lambda e: e.activation(out=sqr[:], in_=t1[:], func=AF.Square, accum_out=st[:, 0:1]), reads=[bt1], writes=[bsqr, bst])
                    S.op("dve", lambda e: e.tensor_scalar(out=st[:, 1:2], in0=st[:, 0:1], scalar1=1.0 / D, scalar2=EPS, op0=ALU.mult, op1=ALU.add), reads=[bst], writes=[bst])
                    S.op("act", lambda e: e.activation(out=st[:, 2:3], in_=st[:, 1:2], func=AF.Ln), reads=[bst], writes=[bst])
                    S.op("act", lambda e: e.activation(out=st[:, 3:4], in_=st[:, 2:3], func=AF.Exp, scale=-0.5), reads=[bst], writes=[bst])
                    S.op("dve", lambda e: e.scalar_tensor_tensor(out=xt[:], in0=t1[:], scalar=st[:, 3:4], in1=fnw[:], op0=ALU.mult, op1=ALU.mult),
                         reads=[bt1, bst, bfnw, bxt], writes=[bxt])
                    S.dma("act", self.out[(tt - 2) * 128:(tt - 1) * 128, :], xt[:], reads=[bxt], writes=[self.db("OUT", tt)])
        S.barrier()
        self.es.close()

    def dump(self, name, ap, reads, shape, dtype=F32):
        o = self.nc.dram_tensor("dbg_" + name, shape, dtype, kind="ExternalOutput").ap()
        b = Buf("dbg_" + name)
        self.S.dma("sp", o, ap, reads=reads, writes=[b])
        self._dbgbufs.append(b)

    def dump_p2(self):
        self.dump("mod", self.mod[:], [self.b_mod], [128, 24, 2])
        self.dump("SCR", self.SCR[:], [self.b_SCR], [128, NT, 16])
        for n in self.asc:
            self.dump(n, self.asc[n][0][:], [self.asc[n][1]], [128, NT, 8])
        self.dump("SH2", self.SH2[:], [self.b_SH2], [128, NT, 8])
        self.dump("kmx", self.kmx[:], [self.b_kmx], [128, 4])
        self.dump("hT", self.hT, self.b_hT, [128, 8, NTOK], BF16)

    def program(self):
        S = self.S
        self._dbgbufs = []
        self.marks = []
        mark = lambda n: self.marks.append((n, {k: v.count for k, v in S.engs.items()}))
        for l in range(self.nlayers):
            mark("L%d start" % l)
            self.phase0(l)
            mark("L%d p0 done" % l)
            if self.stop == "p0":
                self.dump("mod", self.mod[:], [self.b_mod], [128, 24, 2])
                self.dump("Afm", self.Afm[:], [self.b_Afm], [128, 8, 2])
                self.dump("convw", self.convw[:], [self.b_convw], [128, 12, 5])
                self.dump("scol", self.scol[:], [self.b_scol], [128, 8, 2])
                break
            self.phase1(l)
            mark("L%d p1 done" % l)
            if self.stop == "p1":
                self.dump("hT", self.hT, self.b_hT, [128, 8, NTOK], BF16)
                break
            self.phase2(l)
            if self.stop is not None and self.stop.startswith("p2"):
                break
            mark("L%d p2 done" % l)
            if self.stop == "b":
                self.core_b(l)
                break
            self.core_bc(l)
            mark("L%d B done" % l)
            mark("L%d C done" % l)
            if self.stop == "c":
                break
            self.core_a(l)
            mark("L%d A done" % l)
            if self.stop == "a":
                break
            self.phase5(l)
            mark("L%d p5 done" % l)
            if self.stop == "p5":
                break
        S.barrier()
        return self.nc


def shard_inputs(inputs, b):
    m = {}
    for n in IN_SHAPES:
        a = np.asarray(inputs[n], dtype=np.float32)
        if n in ("x", "c", "ctx"):
            a = a[b]
        m[n] = np.ascontiguousarray(a)
    return m


_CACHE = {}


def kernel(**inputs):
    if "nc" not in _CACHE:
        _CACHE["nc"] = MK().program()
        _CACHE["consts"] = host_consts()
    nc = _CACHE["nc"]
    in_maps = []
    for b in range(8):
        m = shard_inputs(inputs, b)
        m.update(_CACHE["consts"])
        in_maps.append(m)
    res = run_bass_kernel_spmd(nc, in_maps, core_ids=list(range(8)))
    return np.stack([np.asarray(r["out"], dtype=np.float32) for r in res.results], axis=0)
```

```python
from contextlib import ExitStack
import numpy as np
import concourse.bass as bass
import concourse.mybir as mybir
from concourse.bass_utils import run_bass_kernel_spmd

F32 = mybir.dt.float32
BF16 = mybir.dt.bfloat16
AF = mybir.ActivationFunctionType
ALU = mybir.AluOpType
AX = mybir.AxisListType

T = 4096
LC = 256
D = 1024
NT = 34
NTOK = 4352
INW = 8464
CH = 128
EPS = 1e-6
NEG = -1.0e5
O_AQ, O_AK, O_AV, O_AZ, O_AB, O_BQ, O_BKV, O_BZ, O_CQ, O_CK, O_CV, O_CZ, O_MG = (
    0, 512, 1024, 1536, 2048, 2064, 2576, 2832, 3344, 3856, 4368, 4880, 5392)


class Buf:
    __slots__ = ("name", "w", "r", "parts")

    def __init__(self, name):
        self.name = name
        self.w = None
        self.r = {}
        self.parts = {}

    def sub(self, p):
        return Sub(self, p)


class Sub:
    __slots__ = ("parent", "p", "name")

    def __init__(self, parent, p):
        self.parent = parent
        self.p = p
        self.name = "%s[%s]" % (parent.name, p)

    def _slot(self):
        return self.parent.parts.setdefault(self.p, [None, {}])


class Eng:
    def __init__(self, key, e, sem):
        self.key = key
        self.e = e
        self.sem = sem
        self.count = 0
        self.waited = {}


class Sched:
    def __init__(self, nc, n_dma_sems=40):
        self.nc = nc
        self.sems = {}
        self.engs = {}
        for key, e in (("pe", nc.tensor), ("act", nc.scalar), ("dve", nc.vector), ("pool", nc.gpsimd), ("sp", nc.sync)):
            s = nc.alloc_semaphore("sem_" + key)
            self.sems[key] = s
            self.engs[key] = Eng(key, e, s)
        self.dma_sems = []
        for i in range(n_dma_sems):
            k = "dma%d" % i
            self.sems[k] = nc.alloc_semaphore("sem_" + k)
            self.dma_sems.append([k, 0])
        self.dma_rr = 0
        self.nops = 0
        self.clocks = {}
        self.psum = []
        self.ps_rr = 0
        for i in range(8):
            self.psum.append((nc.alloc_psum_tensor("psb%d" % i, [128, 512], F32), Buf("psb%d" % i)))

    def ps(self, pool=None):
        if pool is not None:
            banks, st = pool
            r = self.psum[banks[st[0] % len(banks)]]
            st[0] += 1
            return r
        r = self.psum[self.ps_rr]
        self.ps_rr = (self.ps_rr + 1) % 8
        return r

    def _deps(self, eng, reads, writes, is_dma):
        deps = {}

        def add(tok, same_ok):
            if tok is None:
                return
            k, v = tok
            if k == eng.key and not same_ok:
                return
            if deps.get(k, 0) < v:
                deps[k] = v

        same = is_dma or eng.key != "pe"
        for b in reads:
            if isinstance(b, Sub):
                add(b.parent.w, True)
                add(b._slot()[0], True)
            else:
                add(b.w, True)
                for pw, pr in b.parts.values():
                    add(pw, True)
        for b in writes:
            if isinstance(b, Sub):
                add(b.parent.w, same)
                for k, v in b.parent.r.items():
                    add((k, v), same)
                pw, pr = b._slot()
                add(pw, same)
                for k, v in pr.items():
                    add((k, v), same)
            else:
                add(b.w, same)
                for k, v in b.r.items():
                    add((k, v), same)
                for pw, pr in b.parts.values():
                    add(pw, same)
                    for k, v in pr.items():
                        add((k, v), same)
        for k, v in sorted(deps.items(), key=lambda kv: -kv[1]):
            self._need(eng, k, v)

    def _record(self, key, val, reads, writes):
        for b in reads:
            r = b._slot()[1] if isinstance(b, Sub) else b.r
            if r.get(key, 0) < val:
                r[key] = val
        for b in writes:
            if isinstance(b, Sub):
                sl = b._slot()
                sl[0] = (key, val)
                sl[1] = {}
            else:
                b.w = (key, val)
                b.r = {}
                b.parts = {}

    def _need(self, eng, k, v):
        if eng.waited.get(k, 0) >= v:
            return
        eng.e.wait_ge(self.sems[k], v)
        eng.waited[k] = v
        clk = self.clocks.get((k, v))
        if clk:
            w = eng.waited
            for k2, v2 in clk.items():
                if w.get(k2, 0) < v2:
                    w[k2] = v2

    def op(self, ek, fn, reads=(), writes=()):
        eng = self.engs[ek]
        self._deps(eng, reads, writes, False)
        ins = fn(eng.e)
        self.nops += 1
        eng.count += 1
        ins.then_inc(eng.sem, 1)
        clk = dict(eng.waited)
        clk.pop(eng.key, None)
        self.clocks[(eng.key, eng.count)] = clk
        self._record(eng.key, eng.count, reads, writes)
        return ins

    def dma(self, ek, out, in_, reads=(), writes=(), **kw):
        eng = self.engs[ek]
        self._deps(eng, reads, writes, True)
        slot = self.dma_sems[self.dma_rr]
        self.dma_rr = (self.dma_rr + 1) % len(self.dma_sems)
        k, uses = slot
        if uses > 0:
            self._need(eng, k, 16 * uses)
        ins = eng.e.dma_start(out=out, in_=in_, **kw)
        self.nops += 1
        slot[1] = uses + 1
        val = 16 * (uses + 1)
        ins.then_inc(self.sems[k], 16)
        clk = dict(eng.waited)
        clk.pop(eng.key, None)
        self.clocks[(k, val)] = clk
        self._record(k, val, reads, writes)
        return ins

    def wait_all(self, ek, bufs):
        eng = self.engs[ek]
        for b in bufs:
            toks = []
            if b.w is not None:
                toks.append(b.w)
            toks.extend(b.r.items())
            for pw, pr in b.parts.values():
                if pw is not None:
                    toks.append(pw)
                toks.extend(pr.items())
            for k, v in toks:
                if eng.waited.get(k, 0) < v:
                    eng.e.wait_ge(self.sems[k], v)
                    eng.waited[k] = v

    def barrier(self):
        for eng in self.engs.values():
            for o in self.engs.values():
                if o.key != eng.key and o.count > 0 and eng.waited.get(o.key, 0) < o.count:
                    eng.e.wait_ge(self.sems[o.key], o.count)
                    eng.waited[o.key] = o.count
            for k, uses in self.dma_sems:
                if uses > 0 and eng.waited.get(k, 0) < 16 * uses:
                    eng.e.wait_ge(self.sems[k], 16 * uses)
                    eng.waited[k] = 16 * uses


class Rot:
    def __init__(self, alloc, name, shape, dtype, n=2):
        self.items = [(alloc("%s%d" % (name, i), shape, dtype), Buf("%s%d" % (name, i))) for i in range(n)]
        self.i = 0

    def next(self):
        r = self.items[self.i]
        self.i = (self.i + 1) % len(self.items)
        return r


def host_consts():
    f = np.float32
    j = np.arange(128)[:, None]
    i = np.arange(128)[None, :]
    c = {}
    c["k_ident"] = np.eye(128, dtype=f)
    c["k_ones"] = np.ones((128, 128), f)
    am = np.zeros((4, 128, 128), f)
    am[0] = np.where(i >= j, 0.0, NEG)
    am[1] = np.where(i > j, 0.0, NEG)
    am[2] = np.where(i <= j, 0.0, NEG)
    am[3] = np.where(i < j, 0.0, NEG)
    c["k_amask"] = am
    tri = np.zeros((2, 128, 128), f)
    tri[0] = (j <= i)
    tri[1] = (j >= i)
    c["k_tri"] = tri
    bm = np.zeros((2, 128, 512), f)
    bm[0] = np.tile((j >= i).astype(f), (1, 4))
    bm[1] = np.tile((j <= i).astype(f), (1, 4))
    c["k_bmask"] = bm
    cm = np.zeros((6, 128, 128), f)
    cm[0] = np.maximum(i - j, 0)
    cm[1] = np.maximum(j - i, 0)
    cm[2] = (i > j)
    cm[3] = (j > i)
    cm[4] = np.broadcast_to(i + 1, (128, 128))
    cm[5] = np.broadcast_to(CH - i, (128, 128))
    c["k_cm"] = cm
    nm = np.zeros((14, 128, 128), f)
    for d in range(2):
        for lev in range(7):
            b = 1 << lev
            same = (j // (2 * b)) == (i // (2 * b))
            if d == 0:
                m = same & ((j % (2 * b)) < b) & ((i % (2 * b)) >= b)
            else:
                m = same & ((i % (2 * b)) < b) & ((j % (2 * b)) >= b)
            nm[d * 7 + lev] = -m.astype(f)
    c["k_nm"] = nm
    cj = np.zeros((128, 8), f)
    cj[:, 0:4] = (CH - 1 - np.arange(128))[:, None]
    cj[:, 4:8] = np.arange(128)[:, None]
    c["k_cj"] = cj
    t = np.arange(T)
    inv16 = (f(10000.0) ** (-np.arange(16, dtype=f) / f(16))).astype(f)
    ar = (t // 64).astype(f)[:, None] * inv16[None, :]
    ac = (t % 64).astype(f)[:, None] * inv16[None, :]
    ab = np.concatenate([ar, ac], axis=1).astype(f)
    c["k_cosb"] = np.tile(np.cos(ab).astype(f), (1, 8))
    c["k_sinb"] = np.tile(np.sin(ab).astype(f), (1, 8))
    inv64 = (f(10000.0) ** (-np.arange(64, dtype=f) / f(64))).astype(f)
    ang = t.astype(f)[:, None] * inv64[None, :]
    c["k_cosc"] = np.cos(ang).astype(f)
    c["k_sinc"] = np.sin(ang).astype(f)
    return c


CONST_SHAPES = {"k_ident": [128, 128], "k_ones": [128, 128], "k_amask": [4, 128, 128], "k_tri": [2, 128, 128],
                "k_bmask": [2, 128, 512], "k_cm": [6, 128, 128], "k_cj": [128, 8], "k_nm": [14, 128, 128],
                "k_cosb": [T, 256], "k_sinb": [T, 256], "k_cosc": [T, 64], "k_sinc": [T, 64]}

IN_SHAPES = {"x": [T, D], "c": [D], "ctx": [LC, D], "c_ctx": [D], "w_ada": [2, D, 3 * D], "b_ada": [2, 3 * D],
             "norm_w": [2, D], "w_in": [2, D, INW], "a_conv_w": [2, 5, 1536], "a_log": [2, 8], "a_dt_bias": [2, 8],
             "a_norm_w": [2, 128], "b_sink": [2, 8], "c_decay": [2, 8], "c_norm_w": [2, 512],
             "w_branch": [2, 3, 512, D], "w_out": [2, D, D], "final_norm_w": [D]}

SCRATCH = {"XS": ([T, D], F32), "CTXS": ([LC, D], F32),
           "QA_FM": ([4, 128, NTOK], BF16), "KA_FM": ([4, 128, NTOK], BF16),
           "KA_TM": ([4, NTOK, 128], BF16), "VA_TM": ([4, NTOK, 128], BF16),
           "ZA": ([NTOK, 512], BF16), "ZB": ([NTOK, 512], BF16), "ZC": ([NTOK, 512], BF16),
           "OA": ([NTOK, 512], F32),
           "QB_FM": ([8, 128, NTOK], BF16),
           "QC_FM": ([4, 128, NTOK], BF16), "KC_FM": ([4, 128, NTOK], BF16),
           "KC_TM": ([NTOK, 512], BF16), "VC_TM": ([NTOK, 512], BF16),
           "SCB": ([NT, 128, 512], BF16), "SCF": ([NT, 128, 512], BF16),
           "KB_FM": ([2, 128, NTOK], BF16), "VB_TM": ([NTOK, 2, 65], BF16),
           "GM_FM": ([3 * D, NTOK], BF16), "Y_FM": ([3, 512, NTOK], BF16)}


class MK:
    def __init__(self, nlayers=2, dbg=(), stop=None):
        nc = bass.Bass("TRN2", target_bir_lowering=False, dynamic_dma_scratch_size=1024)
        self.nc = nc
        self.S = Sched(nc)
        self.nlayers = nlayers
        self.stop = stop
        self.din = {}
        for n, shp in list(IN_SHAPES.items()) + list(CONST_SHAPES.items()):
            self.din[n] = nc.dram_tensor(n, shp, F32, kind="ExternalInput").ap()
        self.out = nc.dram_tensor("out", [T, D], F32, kind="ExternalOutput").ap()
        self.scr = {}
        self._db = {}
        for n, (shp, dt) in SCRATCH.items():
            kind = "ExternalOutput" if n in dbg else "Internal"
            self.scr[n] = nc.dram_tensor(n, shp, dt, kind=kind).ap()
        self.dbg = dbg
        self._tcount = 0
        self.alloc()

    def db(self, name, tt):
        k = (name, tt)
        if k not in self._db:
            self._db[k] = Buf("%s_%d" % k)
        return self._db[k]

    def dball(self, name):
        return [self.db(name, tt) for tt in range(NT)]

    def sb(self, name, shape, dtype=F32):
        return self.nc.alloc_sbuf_tensor(name, shape, dtype), Buf(name)

    def rot(self, name, shape, dtype=F32, n=2):
        return Rot(lambda nm, sh, dt: self.nc.alloc_sbuf_tensor(nm, sh, dt), name, shape, dtype, n)

    def _talloc(self, name, shape, dtype):
        self._tcount += 1
        return self.es.enter_context(self.nc.sbuf_tensor("%s_t%d" % (name, self._tcount), shape, dtype))

    def tsb(self, name, shape, dtype=F32):
        return self._talloc(name, shape, dtype), Buf(name)

    def trot(self, name, shape, dtype=F32, n=2):
        return Rot(self._talloc, name, shape, dtype, n)

    def alloc(self):
        nc, S, din = self.nc, self.S, self.din
        self.ident_f, self.b_ident_f = self.sb("ident_f", [128, 128])
        self.ones_f, self.b_ones_f = self.sb("ones_f", [128, 128])
        self.ident_b, self.b_ident_b = self.sb("ident_b", [128, 128], BF16)
        self.ones_b, self.b_ones_b = self.sb("ones_b", [128, 128], BF16)
        self.amask, self.b_amask = self.sb("amask", [128, 4, 128])
        self.tri, self.b_tri = self.sb("tri", [128, 2, 128])
        self.bmask, self.b_bmask = self.sb("bmask", [128, 2, 512], BF16)
        self.cm, self.b_cm = self.sb("cm", [128, 6, 128])
        self.cj, self.b_cj = self.sb("cj", [128, 8])
        S.dma("sp", self.ident_f[:], din["k_ident"], writes=[self.b_ident_f])
        S.dma("sp", self.ones_f[:], din["k_ones"], writes=[self.b_ones_f])
        S.dma("sp", self.amask[:], din["k_amask"].rearrange("a p n -> p a n"), writes=[self.b_amask])
        S.dma("sp", self.tri[:], din["k_tri"].rearrange("a p n -> p a n"), writes=[self.b_tri])
        S.dma("sp", self.cm[:], din["k_cm"].rearrange("a p n -> p a n"), writes=[self.b_cm])
        S.dma("sp", self.cj[:], din["k_cj"], writes=[self.b_cj])
        S.op("dve", lambda e: e.tensor_copy(out=self.ident_b[:], in_=self.ident_f[:]), reads=[self.b_ident_f], writes=[self.b_ident_b])
        S.op("dve", lambda e: e.tensor_copy(out=self.ones_b[:], in_=self.ones_f[:]), reads=[self.b_ones_f], writes=[self.b_ones_b])
        self.R1, self.b_R1 = self.sb("R1", [128, 8 * NTOK], BF16)
        self.hT = self.R1[:].rearrange("p (k t) -> p k t", k=8)
        self.b_hT = [Buf("hT%d" % i) for i in range(NT)]
        self.wst = self.rot("wst", [128, 8, 512], F32, 1)
        self.wb = self.rot("wb", [128, 8, 512], BF16, 4)
        self.scol, self.b_scol = self.sb("scol", [128, 8, 2])
        self.mod, self.b_mod = self.sb("mod", [128, 24, 2])
        self.nwcol, self.b_nwcol = self.sb("nwcol", [128, 8])
        self.badacol, self.b_badacol = self.sb("badacol", [128, 24])
        self.Afm, self.b_Afm = self.sb("Afm", [128, 8, 2])
        self.convw, self.b_convw = self.sb("convw", [128, 12, 5])
        self.rowtmp = self.rot("rowtmp", [128, 128], F32, 2)
        for it in self.rowtmp.items:
            S.op("pool", lambda e: e.memset(it[0][:], 0.0), writes=[it[1]])
        self.gdiag = self.rot("gdiag", [128, 512], F32, 2)
        self.SCR, self.b_SCR = self.sb("SCR", [128, NT, 16])
        self.asc = {}
        for n in ("BETA", "GC", "NEGG", "LNBMG", "NEGEG", "ETAIL", "EGL"):
            self.asc[n] = self.sb("asc_" + n, [128, NT, 8])
        self.SH2, self.b_SH2 = self.sb("SH2", [128, NT, 8])
        self.kmx, self.b_kmx = self.sb("kmx", [128, 4])
        self.par = {}
        for n, w in (("a_log", 8), ("a_dt_bias", 8), ("b_sink", 8), ("c_decay", 8), ("a_norm_w", 128), ("c_norm_w", 512)):
            self.par[n] = self.sb("par_" + n, [128, w])
        self.epsb, self.b_epsb = self.sb("epsb", [128, 4])
        for col, val in ((0, EPS), (1, -0.5 * float(np.log(128.0))), (2, 0.0), (3, 1.0)):
            S.op("pool", lambda e: e.memset(self.epsb[:, col:col + 1], val), writes=[self.b_epsb])

    def load_cols(self, src_rows, n, dst, bdst, func=None):
        S = self.S
        rt, brt = self.rowtmp.next()
        S.dma("sp", rt[0:n, :], src_rows, writes=[brt])
        if func is not None:
            S.op("act", lambda e: e.activation(out=rt[0:n, :], in_=rt[0:n, :], func=func), reads=[brt], writes=[brt])
        ps, bps = S.ps()
        S.op("pe", lambda e: e.transpose(out=ps[:, 0:128], in_=rt[:, :], identity=self.ident_f[:]),
             reads=[brt, self.b_ident_f], writes=[bps])
        S.op("dve", lambda e: e.tensor_copy(out=dst, in_=ps[:, 0:n]), reads=[bps], writes=[bdst])

    def w_plan(self, src2d, groups):
        self._wsrc = src2d
        self._wgroups = list(groups)
        self._wi = 0
        self._wq = []
        self._w_issue()

    def _w_issue(self):
        if self._wi < len(self._wgroups):
            c0, ncols = self._wgroups[self._wi]
            self._wi += 1
            self._wq.append(((c0, ncols), self._load_w_raw(self._wsrc, c0, ncols)))

    def load_w(self, src2d, c0, ncols, nk=8):
        if getattr(self, "_wq", None):
            key, val = self._wq.pop(0)
            assert key == (c0, ncols), (key, c0, ncols)
            self._w_issue()
            return val
        return self._load_w_raw(src2d, c0, ncols, nk)

    def _load_w_raw(self, src2d, c0, ncols, nk=8):
        S = self.S
        st, bst = self.wst.next()
        wb, bwb = self.wb.next()
        S.dma("sp", st[:, 0:nk, 0:ncols], src2d.rearrange("(k p) n -> p k n", p=128)[:, :, c0:c0 + ncols], writes=[bst])
        S.op("pool", lambda e: e.tensor_copy(out=wb[:, 0:nk, 0:ncols], in_=st[:, 0:nk, 0:ncols]), reads=[bst], writes=[bwb])
        return wb, bwb

    def phase0(self, l):
        S, din = self.S, self.din
        self.es = ExitStack()
        for n in self.par:
            t, b = self.par[n]
            S.dma("sp", t[:], din[n][l].partition_broadcast(128), writes=[b])
        self.load_cols(din["c"].rearrange("(k p) -> k p", p=128), 8, self.scol[:, :, 0], self.b_scol, AF.Silu)
        self.load_cols(din["c_ctx"].rearrange("(k p) -> k p", p=128), 8, self.scol[:, :, 1], self.b_scol, AF.Silu)
        self.load_cols(din["norm_w"][l].rearrange("(k p) -> k p", p=128), 8, self.nwcol[:], self.b_nwcol)
        self.load_cols(din["b_ada"][l].rearrange("(k p) -> k p", p=128), 24, self.badacol[:], self.b_badacol)
        cw, bcw = self.tsb("cwrows", [128, 1536])
        S.op("pool", lambda e: e.memset(cw[:], 0.0), writes=[bcw])
        S.dma("sp", cw[0:5, :], din["a_conv_w"][l], writes=[bcw])
        for ct in range(12):
            ps, bps = S.ps()
            S.op("pe", lambda e: e.transpose(out=ps[:, 0:128], in_=cw[:, ct * 128:(ct + 1) * 128], identity=self.ident_f[:]),
                 reads=[bcw, self.b_ident_f], writes=[bps])
            S.op("dve", lambda e: e.tensor_copy(out=self.convw[:, ct, :], in_=ps[:, 0:5]), reads=[bps], writes=[self.b_convw])
        psm, bpsm = S.ps()
        for g in range(6):
            st, bst = self.wst.next()
            S.dma("sp", st[:], din["w_ada"][l].rearrange("(k p) n -> p k n", p=128)[:, :, g * 512:(g + 1) * 512], writes=[bst])
            for jl in range(4):
                j = g * 4 + jl
                for kc in range(8):
                    S.op("pe", lambda e: e.matmul(out=psm[:, 2 * j:2 * j + 2], lhsT=st[:, kc, jl * 128:(jl + 1) * 128], rhs=self.scol[:, kc, :],
                                                  start=(kc == 0), stop=(kc == 7)),
                         reads=[bst, self.b_scol], writes=[bpsm])
        S.op("dve", lambda e: e.tensor_tensor(out=self.mod[:], in0=psm[:, 0:48].rearrange("p (j s) -> p j s", s=2),
                                              in1=self.badacol[:].unsqueeze(2).to_broadcast([128, 24, 2]), op=ALU.add),
             reads=[bpsm, self.b_badacol], writes=[self.b_mod])
        S.op("dve", lambda e: e.scalar_tensor_tensor(out=self.Afm[:], in0=self.mod[:, 8:16, :], scalar=1.0,
                                                     in1=self.nwcol[:].unsqueeze(2).to_broadcast([128, 8, 2]), op0=ALU.add, op1=ALU.mult),
             reads=[self.b_mod, self.b_nwcol], writes=[self.b_Afm])
        S.barrier()
        self.es.close()

    def bc_rows(self, dst_fn, bdst, col_fn, bcol, nchunks):
        S = self.S
        for half in range(nchunks // 4):
            t, bt = self.gdiag.next()
            for q in range(4):
                kc = half * 4 + q
                S.op("dve", lambda e: e.tensor_scalar(out=t[:, q * 128:(q + 1) * 128], in0=self.ident_f[:], scalar1=col_fn(kc),
                                                      scalar2=None, op0=ALU.mult),
                     reads=[self.b_ident_f, bcol], writes=[bt])
            ps, bps = S.ps()
            S.op("pe", lambda e: e.matmul(out=ps[:], lhsT=self.ones_f[:], rhs=t[:], start=True, stop=True),
                 reads=[self.b_ones_f, bt], writes=[bps])
            S.op("act", lambda e: e.copy(out=dst_fn(half), in_=ps[:]), reads=[bps], writes=[bdst])

    def phase1(self, l):
        S, din = self.S, self.din
        self.es = ExitStack()
        self.p1_xt = self.trot("p1_xt", [128, 1024], F32, 2)
        self.p1_sq, self.b_p1_sq = self.tsb("p1_sq", [128, 1024])
        self.p1_st = self.trot("p1_st", [128, 4], F32, 2)
        for tt in range(NT):
            s = 1 if tt < 2 else 0
            if tt < 2:
                src = (din["ctx"] if l == 0 else self.scr["CTXS"])[tt * 128:(tt + 1) * 128, :]
                rd = [] if l == 0 else [self.db("CTXS", tt)]
            else:
                src = (din["x"] if l == 0 else self.scr["XS"])[(tt - 2) * 128:(tt - 1) * 128, :]
                rd = [] if l == 0 else [self.db("XS", tt)]
            xt, bxt = self.p1_xt.next()
            st, bst = self.p1_st.next()
            S.dma("sp", xt[:], src, reads=rd, writes=[bxt])
            S.op("act", lambda e: e.activation(out=self.p1_sq[:], in_=xt[:], func=AF.Square, accum_out=st[:, 0:1]),
                 reads=[bxt], writes=[self.b_p1_sq, bst])
            S.op("dve", lambda e: e.tensor_scalar(out=st[:, 1:2], in0=st[:, 0:1], scalar1=1.0 / D, scalar2=EPS, op0=ALU.mult, op1=ALU.add),
                 reads=[bst], writes=[bst])
            S.op("act", lambda e: e.activation(out=st[:, 2:3], in_=st[:, 1:2], func=AF.Ln), reads=[bst], writes=[bst])
            S.op("act", lambda e: e.activation(out=st[:, 3:4], in_=st[:, 2:3], func=AF.Exp, scale=-0.5), reads=[bst], writes=[bst])
            S.op("act", lambda e: e.activation(out=xt[:], in_=xt[:], func=AF.Copy, scale=st[:, 3:4]),
                 reads=[bst, bxt], writes=[bxt])
            for half in range(2):
                ps, bps = S.ps()
                for q in range(4):
                    kc = half * 4 + q
                    S.op("pe", lambda e: e.transpose(out=ps[:, q * 128:(q + 1) * 128], in_=xt[:, kc * 128:(kc + 1) * 128], identity=self.ident_f[:]),
                         reads=[bxt, self.b_ident_f], writes=[bps])
                for q in range(4):
                    kc = half * 4 + q
                    dst = self.hT[:, kc, tt * 128:(tt + 1) * 128]
                    if q % 2 == 0:
                        S.op("dve", lambda e: e.tensor_scalar(out=dst, in0=ps[:, q * 128:(q + 1) * 128], scalar1=self.Afm[:, kc, s:s + 1],
                                                              scalar2=self.mod[:, kc, s:s + 1], op0=ALU.mult, op1=ALU.add),
                             reads=[bps, self.b_Afm, self.b_mod], writes=[self.b_hT[tt]])
                    else:
                        S.op("act", lambda e: e.activation(out=dst, in_=ps[:, q * 128:(q + 1) * 128], func=AF.Identity,
                                                           scale=self.Afm[:, kc, s:s + 1], bias=self.mod[:, kc, s:s + 1]),
                             reads=[bps, self.b_Afm, self.b_mod], writes=[self.b_hT[tt]])
        S.barrier()
        self.es.close()

    def tok_groups(self):
        g = [(0, 256, 0, 2)]
        for i in range(8):
            g.append((256 + i * 512, 512, 2 + 4 * i, 4))
        return g

    class WStream:
        def __init__(self, mk, src2d, groups, bufs):
            self.mk, self.src, self.groups, self.bufs = mk, src2d, list(groups), bufs
            self.i = 0
            self.q = []
            self._issue()

        def _issue(self):
            if self.i < len(self.groups):
                c0, ncols = self.groups[self.i]
                wb, bwb = self.bufs[self.i % len(self.bufs)]
                S = self.mk.S
                st, bst = self.mk.wst.next()
                S.dma("sp", st[:, :, 0:ncols], self.src.rearrange("(k p) n -> p k n", p=128)[:, :, c0:c0 + ncols], writes=[bst])
                S.op("pool", lambda e: e.tensor_copy(out=wb[:, :, 0:ncols], in_=st[:, :, 0:ncols]), reads=[bst], writes=[bwb])
                self.q.append(((c0, ncols), (wb, bwb)))
                self.i += 1

        def get(self, c0, ncols):
            key, val = self.q.pop(0)
            assert key == (c0, ncols), (key, c0, ncols)
            self._issue()
            return val

    def proj_tm_gen(self, l, c0, ncols, handler, ws):
        S = self.S
        if hasattr(self, "marks"):
            self.marks.append(("  L%d tm@%d" % (l, c0), {k: v.count for k, v in S.engs.items()}))
        wb, bwb = ws.get(c0, ncols)
        for tt in range(NT):
            ps, bps = S.ps()
            for kc in range(8):
                S.op("pe", lambda e: e.matmul(out=ps[:, 0:ncols], lhsT=self.hT[:, kc, tt * 128:(tt + 1) * 128], rhs=wb[:, kc, 0:ncols],
                                              start=(kc == 0), stop=(kc == 7)),
                     reads=[self.b_hT[tt], bwb], writes=[bps])
            handler(tt, ps, bps)
            yield

    def proj_tm(self, l, c0, ncols, handler, ws):
        for _ in self.proj_tm_gen(l, c0, ncols, handler, ws):
            pass

    @staticmethod
    def run_pair(g1, g2):
        gens = [g1, g2]
        while gens:
            for g_ in list(gens):
                try:
                    next(g_)
                except StopIteration:
                    gens.remove(g_)

    def transpose_out(self, src_fn, n, rows, dst, dst_buf_list, tag, pool=None):
        S = self.S
        ps, bps = S.ps(pool)
        psb = ps[:].bitcast(BF16)
        for i in range(n):
            src, bsrc = src_fn(i)
            S.op("pe", lambda e: e.transpose(out=psb[0:rows, i * 128:(i + 1) * 128], in_=src, identity=self.ident_b[:]),
                 reads=[bsrc, self.b_ident_b], writes=[bps])
        S.op("act", lambda e: e.copy(out=dst, in_=psb[0:rows, 0:n * 128]), reads=[bps], writes=dst_buf_list)

    def phase2(self, l):
        S, din, scr = self.S, self.din, self.scr
        last = (l == self.nlayers - 1)
        self.es = ExitStack()
        self.zt = self.trot("zt", [128, 512], BF16, 2)
        self.tmpA = self.trot("tmpA", [128, 512], F32, 2)
        self.tmpB = self.trot("tmpB", [128, 512], F32, 2)
        self.tmo = self.trot("tmo", [128, 512], BF16, 3)
        self.fmo = self.trot("fmo", [128, 512], BF16, 3)
        self.qa = self.trot("qa", [128, 8, 128], BF16, 2)
        self.ka = self.trot("ka", [128, 2, 128], BF16, 2)
        self.vb = self.trot("vb", [128, 2, 65], BF16, 2)
        self.qaT = self.trot("qaT", [128, 8, 128], BF16, 2)
        self.kaT = self.trot("kaT", [128, 2, 128], BF16, 2)
        self.csc = self.trot("csc", [128, 2, 64], F32, 3)
        self.csb = self.trot("csb", [128, 2, 256], F32, 3)
        self.st8 = self.trot("st8", [128, 24], F32, 4)
        self.tmp8 = self.trot("tmp8", [128, NT, 8], F32, 4)
        for n in ("LNB", "GRAW", "GT"):
            self.asc[n] = self.tsb("asc_" + n, [128, NT, 8])
        self.KM, self.b_KM = self.tsb("KM", [128, 2])
        self.half8, self.b_half8 = self.tsb("half8", [128, 8])
        S.op("pool", lambda e: e.memset(self.half8[:], 0.5), writes=[self.b_half8])
        self.rowbuf, self.b_rowbuf = self.tsb("rowbuf", [128, 4360])
        self.slrow, self.b_slrow = self.tsb("slrow", [128, NTOK])
        S.op("pool", lambda e: e.memset(self.rowbuf[:], 0.0), writes=[self.b_rowbuf])
        for it in self.vb.items:
            S.op("pool", lambda e: e.memset(it[0][:], 1.0), writes=[it[1]])
        for it in self.ka.items + self.qa.items:
            S.op("pool", lambda e: e.memset(it[0][:], 0.0), writes=[it[1]])
        for it in self.ka.items:
            S.op("pool", lambda e: e.memset(it[0][:, :, 64:65], 1.0), writes=[it[1]])
        S.op("pool", lambda e: e.memset(self.KM[:], 0.0), writes=[self.b_KM])

        def silu_out(name):
            def h(tt, ps, bps):
                z, bz = self.zt.next()
                S.op("act", lambda e: e.activation(out=z[:], in_=ps[:], func=AF.Silu), reads=[bps], writes=[bz])
                S.dma("act", scr[name][tt * 128:(tt + 1) * 128, :], z[:], reads=[bz], writes=[self.db(name, tt)])
            return h

        wsrc = din["w_in"][l]
        bufsA, bufsB = self.wb.items[0:2], self.wb.items[2:4]
        ws1 = self.WStream(self, wsrc, [(O_AZ, 512), (O_AB, 16), (O_BKV, 256)], bufsA)
        self.proj_tm(l, O_AZ, 512, silu_out("ZA"), ws1)
        if self.stop == "p2a":
            S.barrier()
            self.es.close()
            return

        def h_ab(tt, ps, bps):
            S.op("act", lambda e: e.copy(out=self.SCR[:, tt, :], in_=ps[:, 0:16]), reads=[bps], writes=[self.b_SCR])
        self.proj_tm(l, O_AB, 16, h_ab, ws1)
        if self.stop == "p2b1":
            S.barrier()
            self.es.close()
            return
        self.a_scalars(l)
        if self.stop in ("p2b", "p2b2"):
            S.barrier()
            self.es.close()
            return

        def load_cs(tt, which):
            rotp, cn, sn, w = (self.csc, "k_cosc", "k_sinc", 64) if which == "c" else (self.csb, "k_cosb", "k_sinb", 256)
            cs, bcs = rotp.next()
            r0 = (tt - 2) * 128
            S.dma("sp", cs[:, 0, :], din[cn][r0:r0 + 128, :], writes=[bcs])
            S.dma("sp", cs[:, 1, :], din[sn][r0:r0 + 128, :], writes=[bcs])
            return cs, bcs

        def rope(x1, x2, cosb, sinb, o1, o2, shape_fn, bps, bcs, bout, scale=None):
            ta, bta = self.tmpA.next()
            tb, btb = self.tmpB.next()
            ta1, ta2 = shape_fn(ta[:, 0:256]), shape_fn(ta[:, 256:512])
            tb1, tb2 = shape_fn(tb[:, 0:256]), shape_fn(tb[:, 256:512])
            if scale is None:
                mul = lambda o, a, b: (lambda e: e.tensor_tensor(out=o, in0=a, in1=b, op=ALU.mult))
            else:
                mul = lambda o, a, b: (lambda e: e.scalar_tensor_tensor(out=o, in0=a, scalar=scale, in1=b, op0=ALU.mult, op1=ALU.mult))
            S.op("dve", mul(ta1, x1, cosb), reads=[bps, bcs], writes=[bta])
            S.op("dve", mul(tb1, x2, sinb), reads=[bps, bcs], writes=[btb])
            S.op("dve", mul(ta2, x1, sinb), reads=[bps, bcs], writes=[bta])
            S.op("dve", mul(tb2, x2, cosb), reads=[bps, bcs], writes=[btb])
            S.op("pool", lambda e: e.tensor_tensor(out=o1, in0=ta1, in1=tb1, op=ALU.subtract), reads=[bta, btb], writes=[bout])
            S.op("pool", lambda e: e.tensor_tensor(out=o2, in0=ta2, in1=tb2, op=ALU.add), reads=[bta, btb], writes=[bout])

        def rope_b(ps_ap, nha, tt, dst, bdst, bps, nh):
            o, bo = self.tmo.next()
            if tt >= 2:
                cs, bcs = load_cs(tt, "b")
                pv = ps_ap.rearrange("p (g f k) -> p g f k", f=2, k=16)
                ov = o[:, 0:nha * 32].rearrange("p (g f k) -> p g f k", f=2, k=16)
                cosb = cs[:, 0, 0:nha * 16].rearrange("p (g k) -> p g k", k=16)
                sinb = cs[:, 1, 0:nha * 16].rearrange("p (g k) -> p g k", k=16)
                rope(pv[:, :, 0, :], pv[:, :, 1, :], cosb, sinb, ov[:, :, 0, :], ov[:, :, 1, :],
                     lambda a: a[:, 0:nha * 16].rearrange("p (g k) -> p g k", k=16), bps, bcs, bo)
                S.op("act", lambda e: e.copy(out=dst[:, :, 0:64], in_=o[:, 0:nha * 32].rearrange("p (h k) -> p h k", h=nh)), reads=[bo], writes=[bdst])
            else:
                S.op("act", lambda e: e.copy(out=dst[:, :, 0:64], in_=ps_ap.rearrange("p (h k) -> p h k", h=nh)), reads=[bps], writes=[bdst])

        def h_bkv(tt, ps, bps):
            ka, bka = self.ka.next()
            rope_b(ps[:, 0:128], 4, tt, ka, bka, bps, 2)
            ta, bta = self.tmpA.next()
            st, bst = self.st8.next()
            S.op("act", lambda e: e.activation(out=ta[:, 0:128], in_=ps[:, 0:128], func=AF.Square), reads=[bps], writes=[bta])
            S.op("dve", lambda e: e.tensor_reduce(out=st[:, 0:2], in_=ta[:, 0:128].rearrange("p (h k) -> p h k", h=2), axis=AX.X, op=ALU.add),
                 reads=[bta], writes=[bst])
            S.op("dve", lambda e: e.tensor_tensor(out=self.KM[:], in0=self.KM[:], in1=st[:, 0:2], op=ALU.max), reads=[bst, self.b_KM], writes=[self.b_KM])
            vb, bvb = self.vb.next()
            S.op("dve", lambda e: e.tensor_copy(out=vb[:, :, 0:64], in_=ps[:, 128:256].rearrange("p (h k) -> p h k", h=2)),
                 reads=[bps], writes=[bvb])
            S.dma("act", scr["VB_TM"][tt * 128:(tt + 1) * 128, :, :], vb[:], reads=[bvb], writes=[self.db("VB_TM", tt)])
            kT, bkT = self.kaT.next()
            self.transpose_out(lambda i: (ka[:, i, :], bka), 2, 128, kT[:].rearrange("r h t -> r (h t)"), [bkT], "kbt")
            S.dma("act", scr["KB_FM"].rearrange("h r t -> r h t")[:, :, tt * 128:(tt + 1) * 128], kT[:], reads=[bkT], writes=[self.db("KB_FM", tt)])
        self.proj_tm(l, O_BKV, 256, h_bkv, ws1)
        if self.stop == "p2c":
            S.barrier()
            self.es.close()
            return
        S.op("dve", lambda e: e.tensor_reduce(out=self.kmx[:, 0:1], in_=self.KM[:], axis=AX.X, op=ALU.max), reads=[self.b_KM], writes=[self.b_kmx])
        dgk, bdgk = self.gdiag.next()
        S.op("dve", lambda e: e.tensor_scalar(out=dgk[:, 0:128], in0=self.ident_f[:], scalar1=self.kmx[:, 0:1], scalar2=None, op0=ALU.mult),
             reads=[self.b_ident_f, self.b_kmx], writes=[bdgk])
        ps, bps = S.ps()
        S.op("pe", lambda e: e.matmul(out=ps[:, 0:128], lhsT=self.ones_f[:], rhs=dgk[:, 0:128], start=True, stop=True),
             reads=[self.b_ones_f, bdgk], writes=[bps])
        kr, bkr = self.tsb("kmrow", [128, 4])
        S.op("dve", lambda e: e.tensor_reduce(out=kr[:, 0:1], in_=ps[:, 0:128], axis=AX.X, op=ALU.max), reads=[bps], writes=[bkr])
        S.op("act", lambda e: e.activation(out=kr[:, 1:2], in_=kr[:, 0:1], func=AF.Ln), reads=[bkr], writes=[bkr])
        S.op("act", lambda e: e.activation(out=kr[:, 2:3], in_=kr[:, 1:2], func=AF.Exp, scale=0.5), reads=[bkr], writes=[bkr])
        S.op("dve", lambda e: e.tensor_scalar(out=self.kmx[:, 1:2], in0=kr[:, 2:3], scalar1=-1.0, scalar2=None, op0=ALU.mult), reads=[bkr], writes=[self.b_kmx])
        S.op("dve", lambda e: e.tensor_scalar(out=self.kmx[:, 2:3], in0=kr[:, 2:3], scalar1=-0.125, scalar2=None, op0=ALU.mult), reads=[bkr], writes=[self.b_kmx])

        def h_bq(tt, ps, bps):
            qa, bqa = self.qa.next()
            ta, bta = self.tmpA.next()
            st, bst = self.st8.next()
            S.op("act", lambda e: e.activation(out=ta[:], in_=ps[:], func=AF.Square), reads=[bps], writes=[bta])
            S.op("dve", lambda e: e.tensor_reduce(out=st[:, 0:8], in_=ta[:].rearrange("p (h k) -> p h k", h=8), axis=AX.X, op=ALU.add),
                 reads=[bta], writes=[bst])
            S.op("pool", lambda e: e.tensor_tensor(out=st[:, 16:24], in0=st[:, 0:8], in1=self.half8[:], op=ALU.pow), reads=[bst, self.b_half8], writes=[bst])
            rope_b(ps[:], 16, tt, qa, bqa, bps, 8)
            S.op("dve", lambda e: e.tensor_scalar(out=qa[:, :, 64], in0=st[:, 16:24], scalar1=self.kmx[:, 1:2], scalar2=None, op0=ALU.mult),
                 reads=[bst, self.b_kmx], writes=[bqa])
            S.op("dve", lambda e: e.scalar_tensor_tensor(out=self.SH2[:, tt, :], in0=st[:, 16:24], scalar=self.kmx[:, 2:3], in1=self.par["b_sink"][0][:],
                                                         op0=ALU.mult, op1=ALU.add),
                 reads=[bst, self.b_kmx, self.par["b_sink"][1]], writes=[self.b_SH2])
            qT, bqT = self.qaT.next()
            self.transpose_out(lambda i: (qa[:, i, :], bqa), 8, 128, qT[:].rearrange("r h t -> r (h t)"), [bqT], "qbt")
            S.dma("act", scr["QB_FM"].rearrange("h r t -> r h t")[:, :, tt * 128:(tt + 1) * 128], qT[:], reads=[bqT], writes=[self.db("QB_FM", tt)])
        if self.stop == "p2d":
            S.barrier()
            self.es.close()
            return

        def h_cqk(name_fm, name_tm, scale):
            def h(tt, ps, bps):
                o, bo = self.tmo.next()
                if tt >= 2:
                    cs, bcs = load_cs(tt, "c")
                    pv = ps[:].rearrange("p (h f k) -> p h f k", h=4, f=2)
                    ov = o[:].rearrange("p (h f k) -> p h f k", h=4, f=2)
                    cosb = cs[:, 0, :].unsqueeze(1).to_broadcast([128, 4, 64])
                    sinb = cs[:, 1, :].unsqueeze(1).to_broadcast([128, 4, 64])
                    rope(pv[:, :, 0, :], pv[:, :, 1, :], cosb, sinb, ov[:, :, 0, :], ov[:, :, 1, :],
                         lambda a: a.rearrange("p (h k) -> p h k", h=4), bps, bcs, bo, scale=scale)
                else:
                    S.op("act", lambda e: e.activation(out=o[:], in_=ps[:], func=AF.Copy, scale=(1.0 if scale is None else scale)), reads=[bps], writes=[bo])
                if name_tm is not None:
                    S.dma("act", scr[name_tm][tt * 128:(tt + 1) * 128, :], o[:], reads=[bo], writes=[self.db(name_tm, tt)])
                f, bf = self.fmo.next()
                self.transpose_out(lambda i: (o[:, i * 128:(i + 1) * 128], bo), 4, 128, f[:], [bf], "cfm")
                S.dma("act", scr[name_fm].rearrange("h p t -> p h t")[:, :, tt * 128:(tt + 1) * 128], f[:].rearrange("p (h t) -> p h t", h=4),
                      reads=[bf], writes=[self.db(name_fm, tt)])
            return h

        def h_cv(tt, ps, bps):
            o, bo = self.tmo.next()
            S.op("act", lambda e: e.copy(out=o[:], in_=ps[:]), reads=[bps], writes=[bo])
            S.dma("act", scr["VC_TM"][tt * 128:(tt + 1) * 128, :], o[:], reads=[bo], writes=[self.db("VC_TM", tt)])
        wsZ = self.WStream(self, wsrc, [(O_BZ, 512), (O_CZ, 512)], bufsB)
        self.proj_tm(l, O_BZ, 512, silu_out("ZB"), wsZ)
        self.proj_tm(l, O_CZ, 512, silu_out("ZC"), wsZ)
        groups = self.tok_groups()
        wsH = self.WStream(self, wsrc, [(O_BQ, 512), (O_CQ, 512), (O_CK, 512)] + [(g * 512, 512) for g in range(3)], bufsA)
        wsL = self.WStream(self, wsrc, [(O_MG + g * 512, 512) for g in range(6)] + [(O_CV, 512)], bufsB)
        wsF = wsH
        wsM = wsL

        def heavy():
            yield from self.proj_tm_gen(l, O_BQ, 512, h_bq, wsH)
            yield from self.proj_tm_gen(l, O_CQ, 512, h_cqk("QC_FM", None, None), wsH)
            yield from self.proj_tm_gen(l, O_CK, 512, h_cqk("KC_FM", "KC_TM", float(CH) ** -0.5), wsH)

        def light():
            yield from self.proj_tm_gen(l, O_CV, 512, h_cv, wsL)
        if self.stop == "p2e":
            S.barrier()
            self.es.close()
            return


        def afm_gen():
            for g3 in range(3):
                self.marks.append(("  L%d afm%d" % (l, g3), {k: v.count for k, v in S.engs.items()}))
                wb, bwb = wsF.get(g3 * 512, 512)
                for cl in range(4):
                    ct = g3 * 4 + cl
                    head = cl
                    for (t0, n, tile0, ntile) in groups:
                        ps, bps = S.ps()
                        for kc in range(8):
                            S.op("pe", lambda e: e.matmul(out=ps[:, 0:n], lhsT=wb[:, kc, cl * 128:(cl + 1) * 128], rhs=self.hT[:, kc, t0:t0 + n],
                                                          start=(kc == 0), stop=(kc == 7)),
                                 reads=[self.b_hT[tile0 + i] for i in range(ntile)] + [bwb], writes=[bps])
                        off = 2 + t0 if t0 == 0 else 6 + t0
                        S.op("act", lambda e: e.copy(out=self.rowbuf[:, off:off + n], in_=ps[:, 0:n]), reads=[bps], writes=[self.b_rowbuf.sub(t0)])
                        yield
                    for (t0, n, tile0, ntile) in groups:
                        off = 2 + t0 if t0 == 0 else 6 + t0
                        cv, bcv = self.tmpA.next()
                        for k in range(5):
                            src = self.rowbuf[:, off + k - 2:off + k - 2 + n]
                            if k == 0:
                                S.op("dve", lambda e: e.tensor_scalar(out=cv[:, 0:n], in0=src, scalar1=self.convw[:, ct, 0:1], scalar2=None, op0=ALU.mult),
                                     reads=[self.b_rowbuf, self.b_convw], writes=[bcv])
                            else:
                                S.op("dve", lambda e: e.scalar_tensor_tensor(out=cv[:, 0:n], in0=src, scalar=self.convw[:, ct, k:k + 1], in1=cv[:, 0:n],
                                                                             op0=ALU.mult, op1=ALU.add),
                                     reads=[self.b_rowbuf, self.b_convw, bcv], writes=[bcv])
                        if g3 < 2:
                            S.op("act", lambda e: e.activation(out=self.slrow[:, t0:t0 + n], in_=cv[:, 0:n], func=AF.Silu), reads=[bcv], writes=[self.b_slrow.sub(t0)])
                            yield
                        else:
                            o, bo = self.tmo.next()
                            S.op("act", lambda e: e.activation(out=o[:, 0:n], in_=cv[:, 0:n], func=AF.Silu), reads=[bcv], writes=[bo])
                            f, bf = self.fmo.next()
                            self.transpose_out(lambda i: (o[:, i * 128:(i + 1) * 128], bo), ntile, 128, f[:, 0:n], [bf], "va")
                            S.dma("act", scr["VA_TM"][head].rearrange("(t p) c -> p t c", p=128)[:, tile0:tile0 + ntile, :],
                                  f[:, 0:n].rearrange("p (t c) -> p t c", c=128), reads=[bf], writes=[self.db("VA_TM", tile0 + i) for i in range(ntile)])
                            yield
                    if g3 == 2:
                        continue
                    qscale_bias = -0.5 * float(np.log(128.0)) if g3 == 0 else 0.0
                    name = "QA_FM" if g3 == 0 else "KA_FM"
                    for (t0, n, tile0, ntile) in groups:
                        sq, bsq = self.tmo.next()
                        S.op("act", lambda e: e.activation(out=sq[:, 0:n], in_=self.slrow[:, t0:t0 + n], func=AF.Square), reads=[self.b_slrow.sub(t0)], writes=[bsq])
                        ps, bps = S.ps()
                        S.op("pe", lambda e: e.matmul(out=ps[:, 0:n], lhsT=self.ones_b[:], rhs=sq[:, 0:n], start=True, stop=True),
                             reads=[self.b_ones_b, bsq], writes=[bps])
                        ta, bta = self.tmpB.next()
                        S.op("act", lambda e: e.activation(out=ta[:, 0:n], in_=ps[:, 0:n], func=AF.Ln, bias=self.epsb[:, 0:1]), reads=[bps, self.b_epsb], writes=[bta])
                        S.op("act", lambda e: e.activation(out=ta[:, 0:n], in_=ta[:, 0:n], func=AF.Exp, scale=-0.5, bias=self.epsb[:, 1 + g3:2 + g3]),
                             reads=[bta, self.b_epsb], writes=[bta])
                        o, bo = self.fmo.next()
                        S.op("dve", lambda e: e.tensor_tensor(out=o[:, 0:n], in0=self.slrow[:, t0:t0 + n], in1=ta[:, 0:n], op=ALU.mult),
                             reads=[self.b_slrow.sub(t0), bta], writes=[bo])
                        S.dma("act", scr[name][head][:, t0:t0 + n], o[:, 0:n], reads=[bo], writes=[self.db(name, tile0 + i) for i in range(ntile)])
                        if g3 == 1:
                            f, bf = self.zt.next()
                            self.transpose_out(lambda i: (o[:, i * 128:(i + 1) * 128], bo), ntile, 128, f[:, 0:n], [bf], "ka")
                            S.dma("act", scr["KA_TM"][head].rearrange("(t p) c -> p t c", p=128)[:, tile0:tile0 + ntile, :],
                                  f[:, 0:n].rearrange("p (t c) -> p t c", c=128), reads=[bf], writes=[self.db("KA_TM", tile0 + i) for i in range(ntile)])
                        yield


        def merge_gen():
            for g6 in range(6):
                self.marks.append(("  L%d mg%d" % (l, g6), {k: v.count for k, v in S.engs.items()}))
                wb, bwb = wsM.get(O_MG + g6 * 512, 512)
                for cl in range(4):
                    ct = g6 * 4 + cl
                    for (t0, n, tile0, ntile) in groups:
                        if last and t0 == 0:
                            continue
                        ps, bps = S.ps()
                        for kc in range(8):
                            S.op("pe", lambda e: e.matmul(out=ps[:, 0:n], lhsT=wb[:, kc, cl * 128:(cl + 1) * 128], rhs=self.hT[:, kc, t0:t0 + n],
                                                          start=(kc == 0), stop=(kc == 7)),
                                 reads=[self.b_hT[tile0 + i] for i in range(ntile)] + [bwb], writes=[bps])
                        o, bo = self.fmo.next()
                        S.op("act", lambda e: e.activation(out=o[:, 0:n], in_=ps[:, 0:n], func=AF.Sigmoid), reads=[bps], writes=[bo])
                        S.dma("act", scr["GM_FM"][ct * 128:(ct + 1) * 128, t0:t0 + n], o[:, 0:n], reads=[bo],
                              writes=[self.db("GM_FM%d" % ct, tile0 + i) for i in range(ntile)])
                        yield
        self.run_pair(heavy(), merge_gen())
        self.run_pair(afm_gen(), light())
        if self.stop == "p2":
            self.dump_p2()
        S.barrier()
        self.es.close()

    def a_scalars(self, l):
        S = self.S
        A = self.asc
        braw = self.SCR[:, :, 0:8]
        araw = self.SCR[:, :, 8:16]
        rs = [self.b_SCR]
        one = self.epsb[:, 3:4]

        def softplus_parts(x_ap, xb, neg):
            t1, b1 = self.tmp8.next()
            t2, b2 = self.tmp8.next()
            S.op("act", lambda e: e.activation(out=t1[:], in_=x_ap, func=AF.Abs), reads=xb, writes=[b1])
            S.op("act", lambda e: e.activation(out=t1[:], in_=t1[:], func=AF.Exp, scale=-1.0), reads=[b1], writes=[b1])
            S.op("act", lambda e: e.activation(out=t1[:], in_=t1[:], func=AF.Ln, bias=one), reads=[b1, self.b_epsb], writes=[b1])
            S.op("dve", lambda e: e.tensor_scalar(out=t2[:], in0=x_ap, scalar1=(-1.0 if neg else 1.0), scalar2=0.0, op0=ALU.mult, op1=ALU.max),
                 reads=xb, writes=[b2])
            return (t2, b2), (t1, b1)

        (m, bm), (l1, bl1) = softplus_parts(braw, rs, True)
        LNB, bLNB = A["LNB"]
        S.op("dve", lambda e: e.scalar_tensor_tensor(out=LNB[:], in0=m[:], scalar=-1.0, in1=l1[:], op0=ALU.mult, op1=ALU.subtract),
             reads=[bm, bl1], writes=[bLNB])
        BETA, bBETA = A["BETA"]
        S.op("act", lambda e: e.activation(out=BETA[:], in_=LNB[:], func=AF.Exp), reads=[bLNB], writes=[bBETA])
        xa, bxa = self.tmp8.next()
        dtb, bdtb = self.par["a_dt_bias"]
        S.op("dve", lambda e: e.tensor_tensor(out=xa[:], in0=araw, in1=dtb[:].unsqueeze(1).to_broadcast([128, NT, 8]), op=ALU.add),
             reads=rs + [bdtb], writes=[bxa])
        (m2, bm2), (l2, bl2) = softplus_parts(xa[:], [bxa], False)
        S.op("dve", lambda e: e.tensor_tensor(out=m2[:], in0=m2[:], in1=l2[:], op=ALU.add), reads=[bm2, bl2], writes=[bm2])
        alog, balog = self.par["a_log"]
        nea, bnea = self.st8.next()
        S.op("act", lambda e: e.activation(out=nea[:, 0:8], in_=alog[:], func=AF.Exp), reads=[balog], writes=[bnea])
        GRAW, bGRAW = A["GRAW"]
        S.op("dve", lambda e: e.scalar_tensor_tensor(out=GRAW[:], in0=m2[:], scalar=-1.0, in1=nea[:, 0:8].unsqueeze(1).to_broadcast([128, NT, 8]),
                                                     op0=ALU.mult, op1=ALU.mult),
             reads=[bm2, bnea], writes=[bGRAW])
        GC, bGC = A["GC"]
        GT, bGT = A["GT"]
        if self.stop == "p2b2":
            return
        gflat = GRAW[:].rearrange("p t c -> p (t c)")
        res = []
        for lhs, blhs in ((self.tri[:, 0, :], self.b_tri), (self.tri[:, 1, :], self.b_tri), (self.ones_f[:], self.b_ones_f)):
            ps, bps = S.ps()
            S.op("pe", lambda e: e.matmul(out=ps[:, 0:NT * 8], lhsT=lhs, rhs=gflat, start=True, stop=True), reads=[blhs, bGRAW], writes=[bps])
            res.append((ps[:, 0:NT * 8].rearrange("p (t c) -> p t c", c=8), bps))
        S.op("dve", lambda e: e.tensor_copy(out=GC[:, :, 0:4], in_=res[0][0][:, :, 0:4]), reads=[res[0][1]], writes=[bGC])
        S.op("dve", lambda e: e.tensor_copy(out=GC[:, :, 4:8], in_=res[1][0][:, :, 4:8]), reads=[res[1][1]], writes=[bGC])
        S.op("act", lambda e: e.copy(out=GT[:], in_=res[2][0]), reads=[res[2][1]], writes=[bGT])
        NEGG, bNEGG = A["NEGG"]
        S.op("dve", lambda e: e.tensor_scalar(out=NEGG[:], in0=GC[:], scalar1=-1.0, scalar2=None, op0=ALU.mult), reads=[bGC], writes=[bNEGG])
        LNBMG, bLNBMG = A["LNBMG"]
        S.op("dve", lambda e: e.tensor_tensor(out=LNBMG[:], in0=LNB[:], in1=GC[:], op=ALU.subtract), reads=[bLNB, bGC], writes=[bLNBMG])
        NEGEG, bNEGEG = A["NEGEG"]
        S.op("act", lambda e: e.activation(out=NEGEG[:], in_=GC[:], func=AF.Exp), reads=[bGC], writes=[bNEGEG])
        S.op("dve", lambda e: e.tensor_scalar(out=NEGEG[:], in0=NEGEG[:], scalar1=-1.0, scalar2=None, op0=ALU.mult), reads=[bNEGEG], writes=[bNEGEG])
        ETAIL, bETAIL = A["ETAIL"]
        S.op("dve", lambda e: e.tensor_tensor(out=ETAIL[:], in0=GT[:], in1=GC[:], op=ALU.subtract), reads=[bGT, bGC], writes=[bETAIL])
        S.op("act", lambda e: e.activation(out=ETAIL[:], in_=ETAIL[:], func=AF.Exp), reads=[bETAIL], writes=[bETAIL])
        EGL, bEGL = A["EGL"]
        S.op("act", lambda e: e.activation(out=EGL[:], in_=GT[:], func=AF.Exp), reads=[bGT], writes=[bEGL])

    def core_b(self, l, defer=False):
        S, scr, din = self.S, self.scr, self.din
        last = (l == self.nlayers - 1)
        if not defer:
            self.es = ExitStack()
        KBT = self.R1[:, 0:2 * NTOK].rearrange("p (g t) -> p g t", g=2)
        VBR = self.R1[:, 2 * NTOK:2 * NTOK + NT * 130].rearrange("p (t g c) -> p t g c", g=2, c=65)
        bKBT, bVBR = Buf("KBT"), Buf("VBR")
        S.dma("sp", KBT, scr["KB_FM"].rearrange("g r t -> r g t"), reads=self.dball("KB_FM"), writes=[bKBT])
        S.dma("sp", VBR, scr["VB_TM"].rearrange("(t p) g c -> p t g c", p=128), reads=self.dball("VB_TM"), writes=[bVBR])
        if l == 0:
            bst, bbst = self.tsb("bmst", [128, 2, 512])
            S.dma("sp", bst[:], din["k_bmask"].rearrange("a p n -> p a n"), writes=[bbst])
            S.op("pool", lambda e: e.tensor_copy(out=self.bmask[:], in_=bst[:]), reads=[bbst], writes=[self.b_bmask])
        qTr = self.trot("b_qT", [128, 4, 128], BF16, 2)
        pTr = self.trot("b_pT", [128, 5, 512], BF16, 2)
        zbr = self.trot("b_zb", [128, 512], BF16, 2)
        ybr = self.trot("b_yb", [128, 512], BF16, 2)
        obr = self.trot("b_ob", [128, 256], F32, 2)
        str_ = self.trot("b_st", [128, 16], F32, 6)
        fmo = self.trot("b_fmo", [128, 512], BF16, 2)
        qtiles = list(range(2, NT)) if last else list(range(NT))
        def b_gen():
            for qt in qtiles:
                zb, bzb = zbr.next()
                S.dma("sp", zb[:], scr["ZB"][qt * 128:(qt + 1) * 128, :], reads=[self.db("ZB", qt)], writes=[bzb])
                yb, byb = ybr.next()
                for g in range(2):
                    qT, bqT = qTr.next()
                    S.dma("sp", qT[:], scr["QB_FM"][g * 4:(g + 1) * 4].rearrange("h r t -> r h t")[:, :, qt * 128:(qt + 1) * 128],
                          reads=[self.db("QB_FM", qt)], writes=[bqT])
                    keys = [(0, None), (1, None)]
                    if qt >= 2:
                        if qt - 1 >= 2:
                            keys.append((qt - 1, 0))
                        keys.append((qt, None))
                        if qt + 1 < NT:
                            keys.append((qt + 1, 1))
                    pT, bpT = pTr.next()
                    for idx, (kt, m) in enumerate(keys):
                        ps, bps = S.ps()
                        S.op("pe", lambda e: e.matmul(out=ps[:], lhsT=KBT[:, g, kt * 128:(kt + 1) * 128], rhs=qT[:].rearrange("p h t -> p (h t)"),
                                                      start=True, stop=True), reads=[bKBT, bqT], writes=[bps])
                        S.op("act", lambda e: e.activation(out=pT[:, idx, :], in_=ps[:], func=AF.Exp, scale=0.125), reads=[bps], writes=[bpT.sub(idx)])
                        if m is not None:
                            S.op("pool", lambda e: e.tensor_tensor(out=pT[:, idx, :], in0=pT[:, idx, :], in1=self.bmask[:, m, :], op=ALU.mult),
                                 reads=[bpT.sub(idx), self.b_bmask], writes=[bpT.sub(idx)])
                    po, bpo = S.ps()
                    for h in range(4):
                        for idx, (kt, m) in enumerate(keys):
                            S.op("pe", lambda e: e.matmul(out=po[:, h * 65:(h + 1) * 65], lhsT=pT[:, idx, h * 128:(h + 1) * 128], rhs=VBR[:, kt, g, :],
                                                          start=(idx == 0), stop=(idx == len(keys) - 1)), reads=[bpT.sub(idx), bVBR], writes=[bpo])
                    st, bst_ = str_.next()
                    pov = po[:, 0:260].rearrange("p (h c) -> p h c", c=65)
                    S.op("act", lambda e: e.activation(out=st[:, 0:4], in_=self.SH2[:, qt, g * 4:(g + 1) * 4], func=AF.Exp), reads=[self.b_SH2], writes=[bst_])
                    S.op("dve", lambda e: e.tensor_tensor(out=st[:, 4:8], in0=pov[:, :, 64], in1=st[:, 0:4], op=ALU.add), reads=[bpo, bst_], writes=[bst_])
                    S.op("dve", lambda e: e.reciprocal(out=st[:, 8:12], in_=st[:, 4:8]), reads=[bst_], writes=[bst_])
                    ob, bob = obr.next()
                    S.op("dve", lambda e: e.tensor_tensor(out=ob[:].rearrange("p (h c) -> p h c", c=64), in0=pov[:, :, 0:64],
                                                          in1=st[:, 8:12].unsqueeze(2).to_broadcast([128, 4, 64]), op=ALU.mult),
                         reads=[bpo, bst_], writes=[bob])
                    S.op("pool", lambda e: e.tensor_tensor(out=yb[:, g * 256:(g + 1) * 256], in0=ob[:], in1=zb[:, g * 256:(g + 1) * 256], op=ALU.mult),
                         reads=[bob, bzb], writes=[byb.sub(g)])
                    yield
                self.y_out(1, qt, yb, byb, fmo)
                yield
        if defer:
            return b_gen()
        for _ in b_gen():
            pass
        S.barrier()
        self.es.close()

    def core_bc(self, l):
        self.es = ExitStack()
        gb = self.core_b(l, defer=True)
        gc = self.core_c(l, defer=True)
        self.run_pair(gb, gc)
        self.S.barrier()
        self.es.close()

    def y_out(self, br, tt, y, by, fmo, pool=None):
        S = self.S
        f, bf = fmo.next()
        self.transpose_out(lambda i: (y[:, i * 128:(i + 1) * 128], by), 4, 128, f[:], [bf], "y", pool=pool)
        S.dma("act", self.scr["Y_FM"][br].rearrange("(k p) t -> p k t", p=128)[:, :, tt * 128:(tt + 1) * 128],
              f[:].rearrange("p (k t) -> p k t", k=4), reads=[bf], writes=[self.db("Y_FM%d" % br, tt)])

    def core_c(self, l, defer=False):
        S, scr = self.S, self.scr
        last = (l == self.nlayers - 1)
        if not defer:
            self.es = ExitStack()
        one = self.epsb[:, 3:4]
        cd, bcd = self.par["c_decay"]
        c8 = self.trot("c_c8", [128, 8], F32, 6)
        t1, b1 = c8.next()
        t2, b2 = c8.next()
        LG, bLG = c8.next()
        S.op("act", lambda e: e.activation(out=t1[:], in_=cd[:], func=AF.Abs), reads=[bcd], writes=[b1])
        S.op("act", lambda e: e.activation(out=t1[:], in_=t1[:], func=AF.Exp, scale=-1.0), reads=[b1], writes=[b1])
        S.op("act", lambda e: e.activation(out=t1[:], in_=t1[:], func=AF.Ln, bias=one), reads=[b1, self.b_epsb], writes=[b1])
        S.op("dve", lambda e: e.tensor_scalar(out=t2[:], in0=cd[:], scalar1=-1.0, scalar2=0.0, op0=ALU.mult, op1=ALU.max), reads=[bcd], writes=[b2])
        S.op("dve", lambda e: e.scalar_tensor_tensor(out=LG[:], in0=t2[:], scalar=-1.0, in1=t1[:], op0=ALU.mult, op1=ALU.subtract),
             reads=[b1, b2], writes=[bLG])
        GAMC, bGAMC = c8.next()
        S.op("act", lambda e: e.activation(out=GAMC[:], in_=LG[:], func=AF.Exp, scale=float(CH)), reads=[bLG], writes=[bGAMC])
        KDEC, bKDEC = c8.next()
        S.op("dve", lambda e: e.tensor_tensor(out=KDEC[:], in0=LG[:], in1=self.cj[:], op=ALU.mult), reads=[bLG, self.b_cj], writes=[bKDEC])
        S.op("act", lambda e: e.activation(out=KDEC[:], in_=KDEC[:], func=AF.Exp), reads=[bKDEC], writes=[bKDEC])
        DM, bDM = self.tsb("c_DM", [128, 512])
        QDF, bQDF = self.tsb("c_QDF", [128, 512], BF16)
        QDB, bQDB = self.tsb("c_QDB", [128, 512], BF16)
        tm = self.trot("c_tm", [128, 128], F32, 2)
        for h in range(4):
            ta, bta = tm.next()
            tb, btb = tm.next()
            S.op("act", lambda e: e.activation(out=ta[:], in_=self.cm[:, 0, :], func=AF.Exp, scale=LG[:, h:h + 1]), reads=[self.b_cm, bLG], writes=[bta])
            S.op("dve", lambda e: e.tensor_tensor(out=ta[:], in0=ta[:], in1=self.cm[:, 2, :], op=ALU.mult), reads=[bta, self.b_cm], writes=[bta])
            S.op("act", lambda e: e.activation(out=tb[:], in_=self.cm[:, 1, :], func=AF.Exp, scale=LG[:, 4 + h:5 + h]), reads=[self.b_cm, bLG], writes=[btb])
            S.op("dve", lambda e: e.tensor_tensor(out=tb[:], in0=tb[:], in1=self.cm[:, 3, :], op=ALU.mult), reads=[btb, self.b_cm], writes=[btb])
            S.op("dve", lambda e: e.tensor_tensor(out=ta[:], in0=ta[:], in1=tb[:], op=ALU.add), reads=[bta, btb], writes=[bta])
            S.op("dve", lambda e: e.scalar_tensor_tensor(out=DM[:, h * 128:(h + 1) * 128], in0=self.ident_f[:], scalar=2.0, in1=ta[:], op0=ALU.mult, op1=ALU.add),
                 reads=[bta, self.b_ident_f], writes=[bDM])
            S.op("act", lambda e: e.activation(out=QDF[:, h * 128:(h + 1) * 128], in_=self.cm[:, 4, :], func=AF.Exp, scale=LG[:, h:h + 1]),
                 reads=[self.b_cm, bLG], writes=[bQDF])
            S.op("act", lambda e: e.activation(out=QDB[:, h * 128:(h + 1) * 128], in_=self.cm[:, 5, :], func=AF.Exp, scale=LG[:, 4 + h:5 + h]),
                 reads=[self.b_cm, bLG], writes=[bQDB])
        kTMr = [self.trot("c_kTM%d" % d, [128, 512], BF16, 2) for d in range(2)]
        vr = [self.trot("c_v%d" % d, [128, 512], BF16, 2) for d in range(3)]
        kdr = [self.trot("c_kd%d" % d, [128, 512], BF16, 2) for d in range(2)]
        sbfr = [self.trot("c_sbf%d" % d, [128, 512], BF16, 2) for d in range(2)]
        S32 = [self.tsb("c_S32_%d" % d, [128, 512]) for d in range(2)]
        SCN = ["SCF", "SCB"]

        def state_update(d, cc, kTM, bkTM, v, bv):
            kd, bkd = kdr[d].next()
            for h in range(4):
                S.op("act", lambda e: e.activation(out=kd[:, h * 128:(h + 1) * 128], in_=kTM[:, h * 128:(h + 1) * 128], func=AF.Copy,
                                                   scale=KDEC[:, d * 4 + h:d * 4 + h + 1]),
                     reads=[bkTM, bKDEC], writes=[bkd.sub(h)])
            ps, bps = S.ps()
            for h in range(4):
                S.op("pe", lambda e: e.matmul(out=ps[:, h * 128:(h + 1) * 128], lhsT=kd[:, h * 128:(h + 1) * 128], rhs=v[:, h * 128:(h + 1) * 128],
                                              start=True, stop=True), reads=[bkd, bv], writes=[bps])
            s32, bs32 = S32[d]
            for h in range(4):
                S.op("dve", lambda e: e.scalar_tensor_tensor(out=s32[:, h * 128:(h + 1) * 128], in0=s32[:, h * 128:(h + 1) * 128],
                                                             scalar=GAMC[:, d * 4 + h:d * 4 + h + 1], in1=ps[:, h * 128:(h + 1) * 128],
                                                             op0=ALU.mult, op1=ALU.add),
                     reads=[bs32.sub(h), bGAMC, bps], writes=[bs32.sub(h)])

        def state_pass(d):
            S.op("pool", lambda e: e.memset(S32[d][0][:], 0.0), writes=[S32[d][1]])
            order = list(range(NT)) if d == 0 else [1, 0] + list(range(NT - 1, 1, -1))
            for cc in order:
                sbf, bsbf = sbfr[d].next()
                S.op("act", lambda e: e.copy(out=sbf[:], in_=S32[d][0][:]), reads=[S32[d][1]], writes=[bsbf])
                S.dma("act", scr[SCN[d]][cc], sbf[:], reads=[bsbf], writes=[self.db(SCN[d], cc)])
                kTM, bkTM = kTMr[d].next()
                v, bv = vr[d].next()
                S.dma("sp", kTM[:], scr["KC_TM"][cc * 128:(cc + 1) * 128, :], reads=[self.db("KC_TM", cc)], writes=[bkTM])
                S.dma("sp", v[:], scr["VC_TM"][cc * 128:(cc + 1) * 128, :], reads=[self.db("VC_TM", cc)], writes=[bv])
                yield
                state_update(d, cc, kTM, bkTM, v, bv)
                yield

        qTr = self.trot("c_qT", [128, 512], BF16, 2)
        kTr = self.trot("c_kT", [128, 512], BF16, 2)
        sbr = self.trot("c_sb", [128, 512], BF16, 2)
        sfr = self.trot("c_sf", [128, 512], BF16, 2)
        zcr = self.trot("c_zc", [128, 512], BF16, 2)
        qkr = self.trot("c_qk", [128, 512], BF16, 2)
        qdr = self.trot("c_qd", [128, 512], BF16, 2)
        ofr = self.trot("c_of", [128, 512], F32, 2)
        t5r = self.trot("c_t5", [128, 512], F32, 1)
        ycr = self.trot("c_yc", [128, 512], BF16, 2)
        stc = self.trot("c_st", [128, 24], F32, 3)
        fmo = self.trot("c_fmo", [128, 512], BF16, 2)
        cnw, bcnw = self.par["c_norm_w"]
        eps = self.epsb[:, 0:1]
        def out_pass():
            for cc in range(NT):
                need_out = (cc >= 2) or (not last)
                if need_out:
                    v, bv = vr[2].next()
                    S.dma("sp", v[:], scr["VC_TM"][cc * 128:(cc + 1) * 128, :], reads=[self.db("VC_TM", cc)], writes=[bv])
                    qT, bqT = qTr.next()
                    kT, bkT = kTr.next()
                    sb, bsb = sbr.next()
                    zc, bzc = zcr.next()
                    S.dma("sp", qT[:].rearrange("p (h t) -> p h t", h=4), scr["QC_FM"].rearrange("h p t -> p h t")[:, :, cc * 128:(cc + 1) * 128],
                          reads=[self.db("QC_FM", cc)], writes=[bqT])
                    S.dma("sp", kT[:].rearrange("p (h t) -> p h t", h=4), scr["KC_FM"].rearrange("h p t -> p h t")[:, :, cc * 128:(cc + 1) * 128],
                          reads=[self.db("KC_FM", cc)], writes=[bkT])
                    S.dma("sp", sb[:], scr["SCB"][cc], reads=[self.db("SCB", cc)], writes=[bsb])
                    S.dma("sp", zc[:], scr["ZC"][cc * 128:(cc + 1) * 128, :], reads=[self.db("ZC", cc)], writes=[bzc])
                    sbf, bsbf = sfr.next()
                    S.dma("sp", sbf[:], scr["SCF"][cc], reads=[self.db("SCF", cc)], writes=[bsbf])
                    ps1, bps1 = S.ps()
                    for h in range(4):
                        S.op("pe", lambda e: e.matmul(out=ps1[:, h * 128:(h + 1) * 128], lhsT=kT[:, h * 128:(h + 1) * 128], rhs=qT[:, h * 128:(h + 1) * 128],
                                                      start=True, stop=True), reads=[bkT, bqT], writes=[bps1])
                    qk, bqk = qkr.next()
                    S.op("dve", lambda e: e.tensor_tensor(out=qk[:], in0=ps1[:], in1=DM[:], op=ALU.mult), reads=[bps1, bDM], writes=[bqk])
                    qdf, bqdf = qdr.next()
                    qdb, bqdb = qdr.next()
                    S.op("pool", lambda e: e.tensor_tensor(out=qdf[:], in0=qT[:], in1=QDF[:], op=ALU.mult), reads=[bqT, bQDF], writes=[bqdf])
                    S.op("pool", lambda e: e.tensor_tensor(out=qdb[:], in0=qT[:], in1=QDB[:], op=ALU.mult), reads=[bqT, bQDB], writes=[bqdb])
                    po, bpo = S.ps()
                    for h in range(4):
                        sl = slice(h * 128, (h + 1) * 128)
                        S.op("pe", lambda e: e.matmul(out=po[:, sl], lhsT=qk[:, sl], rhs=v[:, sl], start=True, stop=False), reads=[bqk, bv], writes=[bpo])
                        S.op("pe", lambda e: e.matmul(out=po[:, sl], lhsT=qdf[:, sl], rhs=sbf[:, sl], start=False, stop=False), reads=[bqdf, bsbf], writes=[bpo])
                        S.op("pe", lambda e: e.matmul(out=po[:, sl], lhsT=qdb[:, sl], rhs=sb[:, sl], start=False, stop=True), reads=[bqdb, bsb], writes=[bpo])
                    of, bof = ofr.next()
                    t5, bt5 = t5r.next()
                    st, bst = stc.next()
                    S.op("act", lambda e: e.copy(out=of[:], in_=po[:]), reads=[bpo], writes=[bof])
                    S.op("act", lambda e: e.activation(out=t5[:], in_=po[:], func=AF.Square), reads=[bpo], writes=[bt5])
                    S.op("dve", lambda e: e.tensor_reduce(out=st[:, 0:4], in_=of[:].rearrange("p (h c) -> p h c", h=4), axis=AX.X, op=ALU.add), reads=[bof], writes=[bst])
                    S.op("dve", lambda e: e.tensor_reduce(out=st[:, 4:8], in_=t5[:].rearrange("p (h c) -> p h c", h=4), axis=AX.X, op=ALU.add), reads=[bt5], writes=[bst])
                    S.op("dve", lambda e: e.tensor_scalar(out=st[:, 8:12], in0=st[:, 0:4], scalar1=1.0 / 128, scalar2=None, op0=ALU.mult), reads=[bst], writes=[bst])
                    S.op("dve", lambda e: e.tensor_tensor(out=st[:, 12:16], in0=st[:, 8:12], in1=st[:, 8:12], op=ALU.mult), reads=[bst], writes=[bst])
                    S.op("dve", lambda e: e.scalar_tensor_tensor(out=st[:, 16:20], in0=st[:, 4:8], scalar=1.0 / 128, in1=st[:, 12:16], op0=ALU.mult, op1=ALU.subtract),
                         reads=[bst], writes=[bst])
                    S.op("act", lambda e: e.activation(out=st[:, 20:24], in_=st[:, 16:20], func=AF.Ln, bias=eps), reads=[bst, self.b_epsb], writes=[bst])
                    S.op("act", lambda e: e.activation(out=st[:, 20:24], in_=st[:, 20:24], func=AF.Exp, scale=-0.5), reads=[bst], writes=[bst])
                    ofv = of[:].rearrange("p (h c) -> p h c", h=4)
                    S.op("dve", lambda e: e.tensor_tensor(out=ofv, in0=ofv, in1=st[:, 8:12].unsqueeze(2).to_broadcast([128, 4, 128]), op=ALU.subtract),
                         reads=[bof, bst], writes=[bof])
                    S.op("dve", lambda e: e.tensor_tensor(out=ofv, in0=ofv, in1=st[:, 20:24].unsqueeze(2).to_broadcast([128, 4, 128]), op=ALU.mult),
                         reads=[bof, bst], writes=[bof])
                    S.op("pool", lambda e: e.tensor_tensor(out=of[:], in0=of[:], in1=cnw[:], op=ALU.mult), reads=[bof, bcnw], writes=[bof])
                    yc, byc = ycr.next()
                    S.op("pool", lambda e: e.tensor_tensor(out=yc[:], in0=of[:], in1=zc[:], op=ALU.mult), reads=[bof, bzc], writes=[byc])
                    self.y_out(2, cc, yc, byc, fmo)
                yield

        def c_gen():
            gens = [state_pass(0), state_pass(1)]
            while gens:
                for g_ in list(gens):
                    try:
                        next(g_)
                        yield
                    except StopIteration:
                        gens.remove(g_)
            yield from out_pass()
        if defer:
            return c_gen()
        for _ in c_gen():
            pass
        S.barrier()
        self.es.close()

    def core_a(self, l):
        S, scr = self.S, self.scr
        last = (l == self.nlayers - 1)
        self.es = ExitStack()
        A = self.asc
        import os
        KPRE = int(os.environ.get("KPRE", "2"))
        r1_next = [0]

        def mkrot(name, k, use_r1=True):
            items = []
            for i in range(k):
                if use_r1 and r1_next[0] < 68:
                    j = r1_next[0]
                    r1_next[0] += 1
                    items.append((self.R1[:, j * 512:(j + 1) * 512], Buf("%s%d" % (name, i))))
                else:
                    t = self._talloc("a_" + name, [128, 512], BF16)
                    items.append((t[:], Buf("%s%d" % (name, i))))
            r = Rot.__new__(Rot)
            r.items = items
            r.i = 0
            return r

        def R(n, dt=BF16, k=2):
            r = self.trot("a_" + n, [128, 512], dt, k)
            r.items = [(t[:], b) for t, b in r.items]
            return r
        ofr, t5r = R("of", F32, 3), R("t5", F32, 2)
        zar, yar, fmo = R("za"), R("ya"), R("fmo")
        sta = self.trot("a_st", [128, 16], F32, 4)
        nmask, bnmask = self.tsb("a_nmask", [128, 14, 128], BF16)
        nmst, bnmst = self.wst.next()
        nmv = nmst[:].rearrange("p k n -> p (k n)")[:, 0:14 * 128].rearrange("p (a n) -> p a n", a=14)
        S.dma("sp", nmv, self.din["k_nm"].rearrange("a p n -> p a n"), writes=[bnmst])
        S.op("pool", lambda e: e.tensor_copy(out=nmask[:], in_=nmv), reads=[bnmst], writes=[bnmask])
        anw, banw = self.par["a_norm_w"]
        eps = self.epsb[:, 0:1]
        H = [slice(h * 128, (h + 1) * 128) for h in range(4)]
        v4 = lambda t: t[:].rearrange("p (h c) -> p h c", h=4)
        orders = [list(range(NT)), [1, 0] + list(range(NT - 1, 1, -1))]
        oa_written = set()
        from collections import deque
        free_banks = deque(range(8))

        def acq():
            while not free_banks:
                yield
            bk = free_banks.popleft()
            ps, bps = S.psum[bk]
            return ps, bps, bk

        def rel(bk):
            free_banks.append(bk)

        def mm4(lhs, blhs, rhs, brhs):
            ps, bps, bk = yield from acq()
            for h in range(4):
                S.op("pe", lambda e: e.matmul(out=ps[:, H[h]], lhsT=lhs[:, H[h]], rhs=rhs[:, H[h]], start=True, stop=True), reads=[blhs, brhs], writes=[bps])
            return ps, bps, bk

        def tr4(src, bsrc):
            ps, bps, bk = yield from acq()
            psb = ps[:].bitcast(BF16)
            for h in range(4):
                S.op("pe", lambda e: e.transpose(out=psb[:, H[h]], in_=src[:, H[h]], identity=self.ident_b[:]), reads=[bsrc, self.b_ident_b], writes=[bps])
            return psb, bps, bk

        class DirBufs:
            pass
        DB = []
        for d in range(2):
            o = DirBufs()
            for n in ("kT", "kTM", "vTM", "qT", "qk", "qd", "kt", "Pf"):
                setattr(o, n, mkrot("%s_%d" % (n, d), KPRE + 1))
            o.tsets = []
            for ts in range(KPRE):
                tsd = {n: mkrot("%s_%d_%d" % (n, d, ts), 1).items[0] for n in ("eg", "Ma", "Ml", "Pa", "Pb", "W1", "X", "MlmA", "MlmB")}
                for n in ("F0", "F1", "F2"):
                    tsd[n] = (self._talloc("a_%s_%d_%d" % (n, d, ts), [128, 512], F32)[:], Buf("%s_%d_%d" % (n, d, ts)))
                o.tsets.append(tsd)
            for n in ("Y", "vn", "sbf"):
                setattr(o, n, R("%s_%d" % (n, d)))
            o.S32 = self.tsb("a_S32_%d" % d, [128, 512])
            DB.append(o)

        def prep(d, cc, out, ts):
            B = DB[d]
            TS = B.tsets[ts]
            need_out = (cc >= 2) or (not last)
            out["need_out"] = need_out
            col = lambda name, h: A[name][0][:, cc, d * 4 + h:d * 4 + h + 1]
            tok = slice(cc * 128, (cc + 1) * 128)
            m_incl = self.amask[:, 2 * d, :]
            m_strict = self.amask[:, 2 * d + 1, :]
            nmb = lambda lev: nmask[:, d * 7 + lev, :].unsqueeze(1).to_broadcast([128, 4, 128])
            kT, bkT = B.kT.next()
            kTM, bkTM = B.kTM.next()
            vTM, bvTM = B.vTM.next()
            S.dma("sp", kT.rearrange("p (h t) -> p h t", h=4), scr["KA_FM"].rearrange("h p t -> p h t")[:, :, tok], reads=[self.db("KA_FM", cc)], writes=[bkT])
            S.dma("sp", kTM.rearrange("p (h c) -> p h c", h=4), scr["KA_TM"].rearrange("h t c -> t h c")[tok, :, :], reads=[self.db("KA_TM", cc)], writes=[bkTM])
            S.dma("sp", vTM.rearrange("p (h c) -> p h c", h=4), scr["VA_TM"].rearrange("h t c -> t h c")[tok, :, :], reads=[self.db("VA_TM", cc)], writes=[bvTM])
            out.update(kT=(kT, bkT), vTM=(vTM, bvTM))
            if need_out:
                qT, bqT = B.qT.next()
                S.dma("sp", qT.rearrange("p (h t) -> p h t", h=4), scr["QA_FM"].rearrange("h p t -> p h t")[:, :, tok], reads=[self.db("QA_FM", cc)], writes=[bqT])
            yield
            dg, bdg = TS["F0"]
            for h in range(4):
                S.op("dve", lambda e: e.tensor_scalar(out=dg[:, H[h]], in0=self.ident_f[:], scalar1=col("GC", h), scalar2=None, op0=ALU.mult),
                     reads=[self.b_ident_f, A["GC"][1]], writes=[bdg.sub(h)])
            p3, bp3, k3 = yield from acq()
            S.op("pe", lambda e: e.matmul(out=p3[:], lhsT=self.ones_f[:], rhs=dg[:], start=True, stop=True), reads=[self.b_ones_f, bdg], writes=[bp3])
            yield
            Dm2, bDm2 = TS["F1"]
            for h in range(4):
                S.op("dve", lambda e: e.scalar_tensor_tensor(out=Dm2[:, H[h]], in0=p3[:, H[h]], scalar=col("LNBMG", h), in1=m_strict, op0=ALU.add, op1=ALU.add),
                     reads=[bp3, A["LNBMG"][1], self.b_amask], writes=[bDm2.sub(h)])
            if need_out:
                Dm, bDm = TS["F2"]
                for h in range(4):
                    S.op("dve", lambda e: e.scalar_tensor_tensor(out=Dm[:, H[h]], in0=p3[:, H[h]], scalar=col("NEGG", h), in1=m_incl, op0=ALU.add, op1=ALU.add),
                         reads=[bp3, A["NEGG"][1], self.b_amask], writes=[bDm.sub(h)])
                eg, beg = TS["eg"]
                S.op("act", lambda e: e.activation(out=eg, in_=p3[:], func=AF.Exp), reads=[bp3, bDm, bDm2], writes=[beg])
            rel(k3)
            yield
            decb, bdecb = TS["F0"]
            S.op("act", lambda e: e.activation(out=decb[:], in_=Dm2[:], func=AF.Exp), reads=[bDm2], writes=[bdecb])
            p1, bp1, k1 = yield from mm4(kT, bkT, kT, bkT)
            yield
            Ma, bMa = TS["Ma"]
            S.op("dve", lambda e: e.tensor_tensor(out=Ma, in0=p1[:], in1=decb[:], op=ALU.mult), reads=[bp1, bdecb], writes=[bMa])
            rel(k1)
            yield
            psb, bps, kb = yield from tr4(Ma, bMa)
            nml = lambda lev: nmask[:, (1 - d) * 7 + lev, :].unsqueeze(1).to_broadcast([128, 4, 128])
            mlm = lambda lev: TS["MlmA" if lev % 2 else "MlmB"]
            S.op("dve", lambda e: e.tensor_tensor(out=v4(mlm(1)[0]), in0=psb[:, 0:512].rearrange("p (h c) -> p h c", h=4), in1=nml(1), op=ALU.mult),
                 reads=[bps, bnmask], writes=[mlm(1)[1]])
            Ml, bMl = TS["Ml"]
            S.op("act", lambda e: e.copy(out=Ml, in_=psb[:, 0:512]), reads=[bps, mlm(1)[1]], writes=[bMl])
            rel(kb)
            P, bP = TS["Pa"]
            S.op("pool", lambda e: e.tensor_tensor(out=v4(P), in0=v4(Ma), in1=nmb(0), op=ALU.mult), reads=[bMa, bnmask], writes=[bP])
            S.op("pool", lambda e: e.tensor_tensor(out=v4(P), in0=v4(P), in1=self.ident_b[:].unsqueeze(1).to_broadcast([128, 4, 128]), op=ALU.add),
                 reads=[bP, self.b_ident_b], writes=[bP])
            yield
            if need_out:
                dec, bdec = TS["F1"]
                S.op("act", lambda e: e.activation(out=dec[:], in_=Dm[:], func=AF.Exp), reads=[bDm], writes=[bdec])
                p2, bp2, k2 = yield from mm4(kT, bkT, qT, bqT)
                yield
                qk, bqk = B.qk.next()
                S.op("dve", lambda e: e.tensor_tensor(out=qk, in0=p2[:], in1=dec[:], op=ALU.mult), reads=[bp2, bdec], writes=[bqk])
                rel(k2)
                qd, bqd = B.qd.next()
                S.op("pool", lambda e: e.tensor_tensor(out=qd, in0=qT, in1=eg, op=ALU.mult), reads=[bqT, beg], writes=[bqd])
                out.update(qk=(qk, bqk), qd=(qd, bqd))
                yield
            kt, bkt = B.kt.next()
            for h in range(4):
                S.op("act", lambda e: e.activation(out=kt[:, H[h]], in_=kTM[:, H[h]], func=AF.Copy, scale=col("ETAIL", h)),
                     reads=[bkTM, A["ETAIL"][1]], writes=[bkt.sub(h)])
            out.update(kt=(kt, bkt))
            yield
            for lev in range(1, 7):
                cur, bcur = mlm(lev)
                psw, bpsw, kw = yield from mm4(cur, bcur, P, bP)
                psb, bps, kb = yield from tr4(P, bP)
                if lev < 6:
                    nxt_, bnxt_ = mlm(lev + 1)
                    S.op("pool", lambda e: e.tensor_tensor(out=v4(nxt_), in0=v4(Ml), in1=nml(lev + 1), op=ALU.mult), reads=[bMl, bnmask], writes=[bnxt_])
                yield
                W1, bW1 = TS["W1"]
                S.op("act", lambda e: e.copy(out=W1, in_=psw[:]), reads=[bpsw], writes=[bW1])
                rel(kw)
                X, bX = TS["X"]
                S.op("dve", lambda e: e.tensor_copy(out=X, in_=psb[:, 0:512]), reads=[bps], writes=[bX])
                rel(kb)
                yield
                ps2, bps2, k2 = yield from mm4(X, bX, W1, bW1)
                yield
                Pn, bPn = (B.Pf.next() if lev == 6 else TS["Pb" if lev % 2 == 1 else "Pa"])
                S.op("dve", lambda e: e.tensor_tensor(out=Pn, in0=ps2[:], in1=P, op=ALU.add), reads=[bps2, bP], writes=[bPn])
                rel(k2)
                P, bP = Pn, bPn
                yield
            out.update(P=(P, bP))

        def scan(d, cc, ops, st):
            B = DB[d]
            need_out = ops["need_out"]
            col = lambda name, h: A[name][0][:, cc, d * 4 + h:d * 4 + h + 1]
            tok = slice(cc * 128, (cc + 1) * 128)
            kT, bkT = ops["kT"]
            vTM, bvTM = ops["vTM"]
            kt, bkt = ops["kt"]
            P, bP = ops["P"]
            sbf, bsbf = st["sbf"]
            s32, bs32 = B.S32
            px, bpx, kx = yield from mm4(kT, bkT, sbf, bsbf)
            yield
            Y, bY = B.Y.next()
            for h in range(4):
                S.op("dve", lambda e: e.scalar_tensor_tensor(out=Y[:, H[h]], in0=px[:, H[h]], scalar=col("NEGEG", h), in1=vTM[:, H[h]], op0=ALU.mult, op1=ALU.add),
                     reads=[bpx, A["NEGEG"][1], bvTM], writes=[bY.sub(h)])
            rel(kx)
            yield
            pz, bpz, kz = yield from mm4(P, bP, Y, bY)
            yield
            vn, bvn = B.vn.next()
            for h in range(4):
                S.op("act", lambda e: e.activation(out=vn[:, H[h]], in_=pz[:, H[h]], func=AF.Copy, scale=col("BETA", h)), reads=[bpz, A["BETA"][1]], writes=[bvn.sub(h)])
            rel(kz)
            yield
            pS, bpS, kS = yield from mm4(kt, bkt, vn, bvn)
            if need_out:
                qk, bqk = ops["qk"]
                qd, bqd = ops["qd"]
                po, bpo, ko = yield from acq()
                for h in range(4):
                    S.op("pe", lambda e: e.matmul(out=po[:, H[h]], lhsT=qd[:, H[h]], rhs=sbf[:, H[h]], start=True, stop=False), reads=[bqd, bsbf], writes=[bpo])
                    S.op("pe", lambda e: e.matmul(out=po[:, H[h]], lhsT=qk[:, H[h]], rhs=vn[:, H[h]], start=False, stop=True), reads=[bqk, bvn], writes=[bpo])
            yield
            for h in range(4):
                S.op("dve", lambda e: e.scalar_tensor_tensor(out=s32[:, H[h]], in0=s32[:, H[h]], scalar=col("EGL", h), in1=pS[:, H[h]], op0=ALU.mult, op1=ALU.add),
                     reads=[bs32.sub(h), A["EGL"][1], bpS], writes=[bs32.sub(h)])
            rel(kS)
            sbf2, bsbf2 = B.sbf.next()
            S.op("act", lambda e: e.copy(out=sbf2, in_=s32[:]), reads=[bs32], writes=[bsbf2])
            st["sbf"] = (sbf2, bsbf2)
            yield
            if need_out:
                first = cc not in oa_written
                oa_written.add(cc)
                of, bof = ofr.next()
                if first:
                    S.op("act", lambda e: e.copy(out=of[:], in_=po[:]), reads=[bpo], writes=[bof])
                    rel(ko)
                    S.dma("act", scr["OA"][tok, :], of[:], reads=[bof], writes=[self.db("OA", cc)])
                    yield
                else:
                    S.dma("sp", of[:], scr["OA"][tok, :], reads=[self.db("OA", cc)], writes=[bof])
                    za, bza = zar.next()
                    S.dma("sp", za, scr["ZA"][tok, :], reads=[self.db("ZA", cc)], writes=[bza])
                    yield
                    S.op("dve", lambda e: e.tensor_tensor(out=of[:], in0=po[:], in1=of[:], op=ALU.add), reads=[bpo, bof], writes=[bof])
                    rel(ko)
                    t5, bt5 = t5r.next()
                    st_, bst = sta.next()
                    S.op("act", lambda e: e.activation(out=t5[:], in_=of[:], func=AF.Square), reads=[bof], writes=[bt5])
                    yield
                    S.op("dve", lambda e: e.tensor_reduce(out=st_[:, 0:4], in_=t5[:].rearrange("p (h c) -> p h c", h=4), axis=AX.X, op=ALU.add), reads=[bt5], writes=[bst])
                    S.op("act", lambda e: e.activation(out=st_[:, 4:8], in_=st_[:, 0:4], func=AF.Ln, scale=1.0 / 128, bias=eps), reads=[bst, self.b_epsb], writes=[bst])
                    S.op("act", lambda e: e.activation(out=st_[:, 8:12], in_=st_[:, 4:8], func=AF.Exp, scale=-0.5), reads=[bst], writes=[bst])
                    yield
                    ofv = of[:].rearrange("p (h c) -> p h c", h=4)
                    S.op("dve", lambda e: e.tensor_tensor(out=ofv, in0=ofv, in1=st_[:, 8:12].unsqueeze(2).to_broadcast([128, 4, 128]), op=ALU.mult), reads=[bof, bst], writes=[bof])
                    S.op("pool", lambda e: e.tensor_tensor(out=ofv, in0=ofv, in1=anw[:].unsqueeze(1).to_broadcast([128, 4, 128]), op=ALU.mult), reads=[bof, banw], writes=[bof])
                    yield
                    ya, bya = yar.next()
                    S.op("pool", lambda e: e.tensor_tensor(out=ya, in0=of[:], in1=za, op=ALU.mult), reads=[bof, bza], writes=[bya])
                    f, bf = fmo.next()
                    psb, bps, kb = yield from tr4(ya, bya)
                    S.op("act", lambda e: e.copy(out=f, in_=psb[:, 0:512]), reads=[bps], writes=[bf])
                    rel(kb)
                    S.dma("act", scr["Y_FM"][0].rearrange("(k p) t -> p k t", p=128)[:, :, tok], f.rearrange("p (k t) -> p k t", k=4),
                          reads=[bf], writes=[self.db("Y_FM0", cc)])
                    yield

        def chain(d):
            B = DB[d]
            s32, bs32 = B.S32
            S.op("pool", lambda e: e.memset(s32[:], 0.0), writes=[bs32])
            sbf, bsbf = B.sbf.next()
            S.op("pool", lambda e: e.memset(sbf, 0.0), writes=[bsbf])
            st = {"sbf": (sbf, bsbf)}
            order = orders[d]
            n = len(order)
            outs = [dict() for _ in range(n)]
            started = 0
            active = []
            done = set()

            def start_upto(j):
                nonlocal started
                while started <= min(j, n - 1):
                    active.append((started, prep(d, order[started], outs[started], started % KPRE)))
                    started += 1

            def step_preps():
                for item in list(active):
                    try:
                        next(item[1])
                    except StopIteration:
                        active.remove(item)
                        done.add(item[0])
            start_upto(0)
            while 0 not in done:
                step_preps()
                yield
            for i, cc in enumerate(order):
                start_upto(i + KPRE)
                sc = scan(d, cc, outs[i], st)
                sc_done = False
                while not sc_done or (i + 1 < n and (i + 1) not in done):
                    if not sc_done:
                        try:
                            next(sc)
                        except StopIteration:
                            sc_done = True
                    step_preps()
                    yield

        gens = [chain(0), chain(1)]
        while gens:
            for g_ in list(gens):
                try:
                    next(g_)
                except StopIteration:
                    gens.remove(g_)
        S.barrier()
        self.es.close()

    def phase5(self, l):
        S, scr, din = self.S, self.scr, self.din
        last = (l == self.nlayers - 1)
        self.es = ExitStack()
        wbr = self.R1[:, 14336:26624].rearrange("p (r n) -> p r n", n=1024)
        wo = self.R1[:, 26624:34816].rearrange("p (r n) -> p r n", n=1024)
        bwbr, bwo = Buf("wbr"), Buf("wo")

        def load_into(src_view, dst, bdst, nk):
            st, bst = self.wst.next()
            S.dma("sp", st[:, 0:nk, :], src_view, writes=[bst])
            S.op("pool", lambda e: e.tensor_copy(out=dst, in_=st[:, 0:nk, :]), reads=[bst], writes=[bdst])
        wbsrc = din["w_branch"][l].rearrange("b (k p) n -> p (b k) n", p=128)
        for half in range(2):
            for r0, nk in ((0, 8), (8, 4)):
                load_into(wbsrc[:, r0:r0 + nk, half * 512:(half + 1) * 512], wbr[:, r0:r0 + nk, half * 512:(half + 1) * 512], bwbr, nk)
        wosrc = din["w_out"][l].rearrange("(k p) n -> p k n", p=128)
        for half in range(2):
            load_into(wosrc[:, :, half * 512:(half + 1) * 512], wo[:, :, half * 512:(half + 1) * 512], bwo, 8)
        gate_bc, bgate = self.tsb("gate_bc", [128, 2, 1024])
        for s_ in range(2):
            if last and s_ == 1:
                continue
            self.bc_rows(lambda half: gate_bc[:, s_, half * 512:(half + 1) * 512], bgate, lambda kc: self.mod[:, 16 + kc, s_:s_ + 1], self.b_mod, 8)
        if last:
            fnw, bfnw = self.tsb("fnw_bc", [128, 1024])
            S.dma("sp", fnw[:], din["final_norm_w"].partition_broadcast(128), writes=[bfnw])
        yTr = self.trot("p5_yT", [128, 12, 512], BF16, 1)
        gmr = self.trot("p5_gm", [128, 512], BF16, 3)
        accr = self.trot("p5_acc", [128, 512], F32, 2)
        tmr = self.trot("p5_tm", [128, 512], F32, 2)
        mTr = self.trot("p5_mT", [128, 8, 512], BF16, 2)
        xtr = self.trot("p5_xt", [128, 1024], F32, 2)
        t1r = self.trot("p5_t1", [128, 1024], F32, 2)
        sqr, bsqr = self.tsb("p5_sq", [128, 1024])
        st5 = self.trot("p5_st", [128, 4], F32, 3)
        for (t0, n, tile0, ntile) in self.tok_groups():
            if last and t0 == 0:
                continue
            s_ = 1 if t0 == 0 else 0
            yT, byT = yTr.next()
            for br in range(3):
                S.dma("sp", yT[:, br * 4:(br + 1) * 4, 0:n], scr["Y_FM"][br].rearrange("(k p) t -> p k t", p=128)[:, :, t0:t0 + n],
                      reads=[self.db("Y_FM%d" % br, tile0 + i) for i in range(ntile)], writes=[byT.sub(br)])
            mT, bmT = mTr.next()
            for dt in range(8):
                acc, bacc = accr.next()
                for br in range(3):
                    ct = br * 8 + dt
                    gm, bgm = gmr.next()
                    S.dma("sp", gm[:, 0:n], scr["GM_FM"][ct * 128:(ct + 1) * 128, t0:t0 + n],
                          reads=[self.db("GM_FM%d" % ct, tile0 + i) for i in range(ntile)], writes=[bgm])
                    ps, bps = S.ps()
                    for kc in range(4):
                        S.op("pe", lambda e: e.matmul(out=ps[:, 0:n], lhsT=wbr[:, br * 4 + kc, dt * 128:(dt + 1) * 128], rhs=yT[:, br * 4 + kc, 0:n],
                                                      start=(kc == 0), stop=(kc == 3)), reads=[bwbr, byT.sub(br)], writes=[bps])
                    if br == 0:
                        S.op("dve", lambda e: e.tensor_tensor(out=acc[:, 0:n], in0=ps[:, 0:n], in1=gm[:, 0:n], op=ALU.mult), reads=[bps, bgm], writes=[bacc])
                    else:
                        tm, btm = tmr.next()
                        S.op("dve", lambda e: e.tensor_tensor(out=tm[:, 0:n], in0=ps[:, 0:n], in1=gm[:, 0:n], op=ALU.mult), reads=[bps, bgm], writes=[btm])
                        if br == 1:
                            S.op("pool", lambda e: e.tensor_tensor(out=acc[:, 0:n], in0=acc[:, 0:n], in1=tm[:, 0:n], op=ALU.add), reads=[bacc, btm], writes=[bacc])
                        else:
                            S.op("pool", lambda e: e.tensor_tensor(out=mT[:, dt, 0:n], in0=acc[:, 0:n], in1=tm[:, 0:n], op=ALU.add), reads=[bacc, btm], writes=[bmT.sub(dt)])
            for ti in range(ntile):
                tt = tile0 + ti
                xt, bxt = xtr.next()
                if tt < 2:
                    src = (din["ctx"] if l == 0 else scr["CTXS"])[tt * 128:(tt + 1) * 128, :]
                    rd = [] if l == 0 else [self.db("CTXS", tt)]
                else:
                    src = (din["x"] if l == 0 else scr["XS"])[(tt - 2) * 128:(tt - 1) * 128, :]
                    rd = [] if l == 0 else [self.db("XS", tt)]
                S.dma("sp", xt[:], src, reads=rd, writes=[bxt])
                t1, bt1 = t1r.next()
                for cg in range(2):
                    ps, bps = S.ps()
                    for kc in range(8):
                        S.op("pe", lambda e: e.matmul(out=ps[:], lhsT=mT[:, kc, ti * 128:(ti + 1) * 128], rhs=wo[:, kc, cg * 512:(cg + 1) * 512],
                                                      start=(kc == 0), stop=(kc == 7)), reads=[bmT, bwo], writes=[bps])
                    S.op("dve", lambda e: e.tensor_tensor(out=t1[:, cg * 512:(cg + 1) * 512], in0=ps[:], in1=gate_bc[:, s_, cg * 512:(cg + 1) * 512], op=ALU.mult),
                         reads=[bps, bgate], writes=[bt1.sub(cg)])
                S.op("pool", lambda e: e.tensor_tensor(out=t1[:], in0=t1[:], in1=xt[:], op=ALU.add), reads=[bt1, bxt], writes=[bt1])
                if not last:
                    if tt < 2:
                        S.dma("act", scr["CTXS"][tt * 128:(tt + 1) * 128, :], t1[:], reads=[bt1], writes=[self.db("CTXS", tt)])
                    else:
                        S.dma("act", scr["XS"][(tt - 2) * 128:(tt - 1) * 128, :], t1[:], reads=[bt1], writes=[self.db("XS", tt)])
                else:
                    st, bst = st5.next()
                    S.op("act", lambda e: e.activation(out=sqr[:], in_=t1[:], func=AF.Square, accum_out=st[:, 0:1]), reads=[bt1], writes=[bsqr, bst])
                    S.op("dve", lambda e: e.tensor_scalar(out=st[:, 1:2], in0=st[:, 0:1], scalar1=1.0 / D, scalar2=EPS, op0=ALU.mult, op1=ALU.add), reads=[bst], writes=[bst])
                    S.op("act", lambda e: e.activation(out=st[:, 2:3], in_=st[:, 1:2], func=AF.Ln), reads=[bst], writes=[bst])
                    S.op("act", lambda e: e.activation(out=st[:, 3:4], in_=st[:, 2:3], func=AF.Exp, scale=-0.5), reads=[bst], writes=[bst])
                    S.op("dve", lambda e: e.scalar_tensor_tensor(out=xt[:], in0=t1[:], scalar=st[:, 3:4], in1=fnw[:], op0=ALU.mult, op1=ALU.mult),
                         reads=[bt1, bst, bfnw, bxt], writes=[bxt])
                    S.dma("act", self.out[(tt - 2) * 128:(tt - 1) * 128, :], xt[:], reads=[bxt], writes=[self.db("OUT", tt)])
        S.barrier()
        self.es.close()

    def dump(self, name, ap, reads, shape, dtype=F32):
        o = self.nc.dram_tensor("dbg_" + name, shape, dtype, kind="ExternalOutput").ap()
        b = Buf("dbg_" + name)
        self.S.dma("sp", o, ap, reads=reads, writes=[b])
        self._dbgbufs.append(b)

    def dump_p2(self):
        self.dump("mod", self.mod[:], [self.b_mod], [128, 24, 2])
        self.dump("SCR", self.SCR[:], [self.b_SCR], [128, NT, 16])
        for n in self.asc:
            self.dump(n, self.asc[n][0][:], [self.asc[n][1]], [128, NT, 8])
        self.dump("SH2", self.SH2[:], [self.b_SH2], [128, NT, 8])
        self.dump("kmx", self.kmx[:], [self.b_kmx], [128, 4])
        self.dump("hT", self.hT, self.b_hT, [128, 8, NTOK], BF16)

    def program(self):
        S = self.S
        self._dbgbufs = []
        self.marks = []
        mark = lambda n: self.marks.append((n, {k: v.count for k, v in S.engs.items()}))
        for l in range(self.nlayers):
            mark("L%d start" % l)
            self.phase0(l)
            mark("L%d p0 done" % l)
            if self.stop == "p0":
                self.dump("mod", self.mod[:], [self.b_mod], [128, 24, 2])
                self.dump("Afm", self.Afm[:], [self.b_Afm], [128, 8, 2])
                self.dump("convw", self.convw[:], [self.b_convw], [128, 12, 5])
                self.dump("scol", self.scol[:], [self.b_scol], [128, 8, 2])
                break
            self.phase1(l)
            mark("L%d p1 done" % l)
            if self.stop == "p1":
                self.dump("hT", self.hT, self.b_hT, [128, 8, NTOK], BF16)
                break
            self.phase2(l)
            if self.stop is not None and self.stop.startswith("p2"):
                break
            mark("L%d p2 done" % l)
            if self.stop == "b":
                self.core_b(l)
                break
            self.core_bc(l)
            mark("L%d B done" % l)
            mark("L%d C done" % l)
            if self.stop == "c":
                break
            self.core_a(l)
            mark("L%d A done" % l)
            if self.stop == "a":
                break
            self.phase5(l)
            mark("L%d p5 done" % l)
            if self.stop == "p5":
                break
        S.barrier()
        return self.nc


def shard_inputs(inputs, b):
    m = {}
    for n in IN_SHAPES:
        a = np.asarray(inputs[n], dtype=np.float32)
        if n in ("x", "c", "ctx"):
            a = a[b]
        m[n] = np.ascontiguousarray(a)
    return m


_CACHE = {}


def kernel(**inputs):
    if "nc" not in _CACHE:
        _CACHE["nc"] = MK().program()
        _CACHE["consts"] = host_consts()
    nc = _CACHE["nc"]
    in_maps = []
    for b in range(8):
        m = shard_inputs(inputs, b)
        m.update(_CACHE["consts"])
        in_maps.append(m)
    res = run_bass_kernel_spmd(nc, in_maps, core_ids=list(range(8)))
    return np.stack([np.asarray(r["out"], dtype=np.float32) for r in res.results], axis=0)
```

```python
from contextlib import ExitStack
import numpy as np
import concourse.bass as bass
import concourse.mybir as mybir
from concourse.bass_utils import run_bass_kernel_spmd

F32 = mybir.dt.float32
BF16 = mybir.dt.bfloat16
AF = mybir.ActivationFunctionType
ALU = mybir.AluOpType
AX = mybir.AxisListType

T = 4096
LC = 256
D = 1024
NT = 34
NTOK = 4352
INW = 8464
CH = 128
EPS = 1e-6
NEG = -1.0e5
O_AQ, O_AK, O_AV, O_AZ, O_AB, O_BQ, O_BKV, O_BZ, O_CQ, O_CK, O_CV, O_CZ, O_MG = (
    0, 512, 1024, 1536, 2048, 2064, 2576, 2832, 3344, 3856, 4368, 4880, 5392)


class Buf:
    __slots__ = ("name", "w", "r", "parts")

    def __init__(self, name):
        self.name = name
        self.w = None
        self.r = {}
        self.parts = {}

    def sub(self, p):
        return Sub(self, p)


class Sub:
    __slots__ = ("parent", "p", "name")

    def __init__(self, parent, p):
        self.parent = parent
        self.p = p
        self.name = "%s[%s]" % (parent.name, p)

    def _slot(self):
        return self.parent.parts.setdefault(self.p, [None, {}])


class Eng:
    def __init__(self, key, e, sem):
        self.key = key
        self.e = e
        self.sem = sem
        self.count = 0
        self.waited = {}


class Sched:
    def __init__(self, nc, n_dma_sems=40):
        self.nc = nc
        self.sems = {}
        self.engs = {}
        for key, e in (("pe", nc.tensor), ("act", nc.scalar), ("dve", nc.vector), ("pool", nc.gpsimd), ("sp", nc.sync)):
            s = nc.alloc_semaphore("sem_" + key)
            self.sems[key] = s
            self.engs[key] = Eng(key, e, s)
        self.dma_sems = []
        for i in range(n_dma_sems):
            k = "dma%d" % i
            self.sems[k] = nc.alloc_semaphore("sem_" + k)
            self.dma_sems.append([k, 0])
        self.dma_rr = 0
        self.nops = 0
        self.clocks = {}
        self.psum = []
        self.ps_rr = 0
        for i in range(8):
            self.psum.append((nc.alloc_psum_tensor("psb%d" % i, [128, 512], F32), Buf("psb%d" % i)))

    def ps(self, pool=None):
        if pool is not None:
            banks, st = pool
            r = self.psum[banks[st[0] % len(banks)]]
            st[0] += 1
            return r
        r = self.psum[self.ps_rr]
        self.ps_rr = (self.ps_rr + 1) % 8
        return r

    def _deps(self, eng, reads, writes, is_dma):
        deps = {}

        def add(tok, same_ok):
            if tok is None:
                return
            k, v = tok
            if k == eng.key and not same_ok:
                return
            if deps.get(k, 0) < v:
                deps[k] = v

        same = is_dma or eng.key != "pe"
        for b in reads:
            if isinstance(b, Sub):
                add(b.parent.w, True)
                add(b._slot()[0], True)
            else:
                add(b.w, True)
                for pw, pr in b.parts.values():
                    add(pw, True)
        for b in writes:
            if isinstance(b, Sub):
                add(b.parent.w, same)
                for k, v in b.parent.r.items():
                    add((k, v), same)
                pw, pr = b._slot()
                add(pw, same)
                for k, v in pr.items():
                    add((k, v), same)
            else:
                add(b.w, same)
                for k, v in b.r.items():
                    add((k, v), same)
                for pw, pr in b.parts.values():
                    add(pw, same)
                    for k, v in pr.items():
                        add((k, v), same)
        for k, v in sorted(deps.items(), key=lambda kv: -kv[1]):
            self._need(eng, k, v)

    def _record(self, key, val, reads, writes):
        for b in reads:
            r = b._slot()[1] if isinstance(b, Sub) else b.r
            if r.get(key, 0) < val:
                r[key] = val
        for b in writes:
            if isinstance(b, Sub):
                sl = b._slot()
                sl[0] = (key, val)
                sl[1] = {}
            else:
                b.w = (key, val)
                b.r = {}
                b.parts = {}

    def _need(self, eng, k, v):
        if eng.waited.get(k, 0) >= v:
            return
        eng.e.wait_ge(self.sems[k], v)
        eng.waited[k] = v
        clk = self.clocks.get((k, v))
        if clk:
            w = eng.waited
            for k2, v2 in clk.items():
                if w.get(k2, 0) < v2:
                    w[k2] = v2

    def op(self, ek, fn, reads=(), writes=()):
        eng = self.engs[ek]
        self._deps(eng, reads, writes, False)
        ins = fn(eng.e)
        self.nops += 1
        eng.count += 1
        ins.then_inc(eng.sem, 1)
        clk = dict(eng.waited)
        clk.pop(eng.key, None)
        self.clocks[(eng.key, eng.count)] = clk
        self._record(eng.key, eng.count, reads, writes)
        return ins

    def dma(self, ek, out, in_, reads=(), writes=(), **kw):
        eng = self.engs[ek]
        self._deps(eng, reads, writes, True)
        slot = self.dma_sems[self.dma_rr]
        self.dma_rr = (self.dma_rr + 1) % len(self.dma_sems)
        k, uses = slot
        if uses > 0:
            self._need(eng, k, 16 * uses)
        ins = eng.e.dma_start(out=out, in_=in_, **kw)
        self.nops += 1
        slot[1] = uses + 1
        val = 16 * (uses + 1)
        ins.then_inc(self.sems[k], 16)
        clk = dict(eng.waited)
        clk.pop(eng.key, None)
        self.clocks[(k, val)] = clk
        self._record(k, val, reads, writes)
        return ins

    def wait_all(self, ek, bufs):
        eng = self.engs[ek]
        for b in bufs:
            toks = []
            if b.w is not None:
                toks.append(b.w)
            toks.extend(b.r.items())
            for pw, pr in b.parts.values():
                if pw is not None:
                    toks.append(pw)
                toks.extend(pr.items())
            for k, v in toks:
                if eng.waited.get(k, 0) < v:
                    eng.e.wait_ge(self.sems[k], v)
                    eng.waited[k] = v

    def barrier(self):
        for eng in self.engs.values():
            for o in self.engs.values():
                if o.key != eng.key and o.count > 0 and eng.waited.get(o.key, 0) < o.count:
                    eng.e.wait_ge(self.sems[o.key], o.count)
                    eng.waited[o.key] = o.count
            for k, uses in self.dma_sems:
                if uses > 0 and eng.waited.get(k, 0) < 16 * uses:
                    eng.e.wait_ge(self.sems[k], 16 * uses)
                    eng.waited[k] = 16 * uses


class Rot:
    def __init__(self, alloc, name, shape, dtype, n=2):
        self.items = [(alloc("%s%d" % (name, i), shape, dtype), Buf("%s%d" % (name, i))) for i in range(n)]
        self.i = 0

    def next(self):
        r = self.items[self.i]
        self.i = (self.i + 1) % len(self.items)
        return r


def host_consts():
    f = np.float32
    j = np.arange(128)[:, None]
    i = np.arange(128)[None, :]
    c = {}
    c["k_ident"] = np.eye(128, dtype=f)
    c["k_ones"] = np.ones((128, 128), f)
    am = np.zeros((4, 128, 128), f)
    am[0] = np.where(i >= j, 0.0, NEG)
    am[1] = np.where(i > j, 0.0, NEG)
    am[2] = np.where(i <= j, 0.0, NEG)
    am[3] = np.where(i < j, 0.0, NEG)
    c["k_amask"] = am
    tri = np.zeros((2, 128, 128), f)
    tri[0] = (j <= i)
    tri[1] = (j >= i)
    c["k_tri"] = tri
    bm = np.zeros((2, 128, 512), f)
    bm[0] = np.tile((j >= i).astype(f), (1, 4))
    bm[1] = np.tile((j <= i).astype(f), (1, 4))
    c["k_bmask"] = bm
    cm = np.zeros((6, 128, 128), f)
    cm[0] = np.maximum(i - j, 0)
    cm[1] = np.maximum(j - i, 0)
    cm[2] = (i > j)
    cm[3] = (j > i)
    cm[4] = np.broadcast_to(i + 1, (128, 128))
    cm[5] = np.broadcast_to(CH - i, (128, 128))
    c["k_cm"] = cm
    nm = np.zeros((14, 128, 128), f)
    for d in range(2):
        for lev in range(7):
            b = 1 << lev
            same = (j // (2 * b)) == (i // (2 * b))
            if d == 0:
                m = same & ((j % (2 * b)) < b) & ((i % (2 * b)) >= b)
            else:
                m = same & ((i % (2 * b)) < b) & ((j % (2 * b)) >= b)
            nm[d * 7 + lev] = -m.astype(f)
    c["k_nm"] = nm
    cj = np.zeros((128, 8), f)
    cj[:, 0:4] = (CH - 1 - np.arange(128))[:, None]
    cj[:, 4:8] = np.arange(128)[:, None]
    c["k_cj"] = cj
    t = np.arange(T)
    inv16 = (f(10000.0) ** (-np.arange(16, dtype=f) / f(16))).astype(f)
    ar = (t // 64).astype(f)[:, None] * inv16[None, :]
    ac = (t % 64).astype(f)[:, None] * inv16[None, :]
    ab = np.concatenate([ar, ac], axis=1).astype(f)
    c["k_cosb"] = np.tile(np.cos(ab).astype(f), (1, 8))
    c["k_sinb"] = np.tile(np.sin(ab).astype(f), (1, 8))
    inv64 = (f(10000.0) ** (-np.arange(64, dtype=f) / f(64))).astype(f)
    ang = t.astype(f)[:, None] * inv64[None, :]
    c["k_cosc"] = np.cos(ang).astype(f)
    c["k_sinc"] = np.sin(ang).astype(f)
    return c


CONST_SHAPES = {"k_ident": [128, 128], "k_ones": [128, 128], "k_amask": [4, 128, 128], "k_tri": [2, 128, 128],
                "k_bmask": [2, 128, 512], "k_cm": [6, 128, 128], "k_cj": [128, 8], "k_nm": [14, 128, 128],
                "k_cosb": [T, 256], "k_sinb": [T, 256], "k_cosc": [T, 64], "k_sinc": [T, 64]}

IN_SHAPES = {"x": [T, D], "c": [D], "ctx": [LC, D], "c_ctx": [D], "w_ada": [2, D, 3 * D], "b_ada": [2, 3 * D],
             "norm_w": [2, D], "w_in": [2, D, INW], "a_conv_w": [2, 5, 1536], "a_log": [2, 8], "a_dt_bias": [2, 8],
             "a_norm_w": [2, 128], "b_sink": [2, 8], "c_decay": [2, 8], "c_norm_w": [2, 512],
             "w_branch": [2, 3, 512, D], "w_out": [2, D, D], "final_norm_w": [D]}

SCRATCH = {"XS": ([T, D], F32), "CTXS": ([LC, D], F32),
           "QA_FM": ([4, 128, NTOK], BF16), "KA_FM": ([4, 128, NTOK], BF16),
           "KA_TM": ([4, NTOK, 128], BF16), "VA_TM": ([4, NTOK, 128], BF16),
           "ZA": ([NTOK, 512], BF16), "ZB": ([NTOK, 512], BF16), "ZC": ([NTOK, 512], BF16),
           "OA": ([NTOK, 512], F32),
           "QB_FM": ([8, 128, NTOK], BF16),
           "QC_FM": ([4, 128, NTOK], BF16), "KC_FM": ([4, 128, NTOK], BF16),
           "KC_TM": ([NTOK, 512], BF16), "VC_TM": ([NTOK, 512], BF16),
           "SCB": ([NT, 128, 512], BF16), "SCF": ([NT, 128, 512], BF16),
           "KB_FM": ([2, 128, NTOK], BF16), "VB_TM": ([NTOK, 2, 65], BF16),
           "GM_FM": ([3 * D, NTOK], BF16), "Y_FM": ([3, 512, NTOK], BF16)}


class MK:
    def __init__(self, nlayers=2, dbg=(), stop=None):
        nc = bass.Bass("TRN2", target_bir_lowering=False, dynamic_dma_scratch_size=1024)
        self.nc = nc
        self.S = Sched(nc)
        self.nlayers = nlayers
        self.stop = stop
        self.din = {}
        for n, shp in list(IN_SHAPES.items()) + list(CONST_SHAPES.items()):
            self.din[n] = nc.dram_tensor(n, shp, F32, kind="ExternalInput").ap()
        self.out = nc.dram_tensor("out", [T, D], F32, kind="ExternalOutput").ap()
        self.scr = {}
        self._db = {}
        for n, (shp, dt) in SCRATCH.items():
            kind = "ExternalOutput" if n in dbg else "Internal"
            self.scr[n] = nc.dram_tensor(n, shp, dt, kind=kind).ap()
        self.dbg = dbg
        self._tcount = 0
        self.alloc()

    def db(self, name, tt):
        k = (name, tt)
        if k not in self._db:
            self._db[k] = Buf("%s_%d" % k)
        return self._db[k]

    def dball(self, name):
        return [self.db(name, tt) for tt in range(NT)]

    def sb(self, name, shape, dtype=F32):
        return self.nc.alloc_sbuf_tensor(name, shape, dtype), Buf(name)

    def rot(self, name, shape, dtype=F32, n=2):
        return Rot(lambda nm, sh, dt: self.nc.alloc_sbuf_tensor(nm, sh, dt), name, shape, dtype, n)

    def _talloc(self, name, shape, dtype):
        self._tcount += 1
        return self.es.enter_context(self.nc.sbuf_tensor("%s_t%d" % (name, self._tcount), shape, dtype))

    def tsb(self, name, shape, dtype=F32):
        return self._talloc(name, shape, dtype), Buf(name)

    def trot(self, name, shape, dtype=F32, n=2):
        return Rot(self._talloc, name, shape, dtype, n)

    def alloc(self):
        nc, S, din = self.nc, self.S, self.din
        self.ident_f, self.b_ident_f = self.sb("ident_f", [128, 128])
        self.ones_f, self.b_ones_f = self.sb("ones_f", [128, 128])
        self.ident_b, self.b_ident_b = self.sb("ident_b", [128, 128], BF16)
        self.ones_b, self.b_ones_b = self.sb("ones_b", [128, 128], BF16)
        self.amask, self.b_amask = self.sb("amask", [128, 4, 128])
        self.tri, self.b_tri = self.sb("tri", [128, 2, 128])
        self.bmask, self.b_bmask = self.sb("bmask", [128, 2, 512], BF16)
        self.cm, self.b_cm = self.sb("cm", [128, 6, 128])
        self.cj, self.b_cj = self.sb("cj", [128, 8])
        S.dma("sp", self.ident_f[:], din["k_ident"], writes=[self.b_ident_f])
        S.dma("sp", self.ones_f[:], din["k_ones"], writes=[self.b_ones_f])
        S.dma("sp", self.amask[:], din["k_amask"].rearrange("a p n -> p a n"), writes=[self.b_amask])
        S.dma("sp", self.tri[:], din["k_tri"].rearrange("a p n -> p a n"), writes=[self.b_tri])
        S.dma("sp", self.cm[:], din["k_cm"].rearrange("a p n -> p a n"), writes=[self.b_cm])
        S.dma("sp", self.cj[:], din["k_cj"], writes=[self.b_cj])
        S.op("dve", lambda e: e.tensor_copy(out=self.ident_b[:], in_=self.ident_f[:]), reads=[self.b_ident_f], writes=[self.b_ident_b])
        S.op("dve", lambda e: e.tensor_copy(out=self.ones_b[:], in_=self.ones_f[:]), reads=[self.b_ones_f], writes=[self.b_ones_b])
        self.R1, self.b_R1 = self.sb("R1", [128, 8 * NTOK], BF16)
        self.hT = self.R1[:].rearrange("p (k t) -> p k t", k=8)
        self.b_hT = [Buf("hT%d" % i) for i in range(NT)]
        self.wst = self.rot("wst", [128, 8, 512], F32, 1)
        self.wb = self.rot("wb", [128, 8, 512], BF16, 4)
        self.scol, self.b_scol = self.sb("scol", [128, 8, 2])
        self.mod, self.b_mod = self.sb("mod", [128, 24, 2])
        self.nwcol, self.b_nwcol = self.sb("nwcol", [128, 8])
        self.badacol, self.b_badacol = self.sb("badacol", [128, 24])
        self.Afm, self.b_Afm = self.sb("Afm", [128, 8, 2])
        self.convw, self.b_convw = self.sb("convw", [128, 12, 5])
        self.rowtmp = self.rot("rowtmp", [128, 128], F32, 2)
        for it in self.rowtmp.items:
            S.op("pool", lambda e: e.memset(it[0][:], 0.0), writes=[it[1]])
        self.gdiag = self.rot("gdiag", [128, 512], F32, 2)
        self.SCR, self.b_SCR = self.sb("SCR", [128, NT, 16])
        self.asc = {}
        for n in ("BETA", "GC", "NEGG", "LNBMG", "NEGEG", "ETAIL", "EGL"):
            self.asc[n] = self.sb("asc_" + n, [128, NT, 8])
        self.SH2, self.b_SH2 = self.sb("SH2", [128, NT, 8])
        self.kmx, self.b_kmx = self.sb("kmx", [128, 4])
        self.par = {}
        for n, w in (("a_log", 8), ("a_dt_bias", 8), ("b_sink", 8), ("c_decay", 8), ("a_norm_w", 128), ("c_norm_w", 512)):
            self.par[n] = self.sb("par_" + n, [128, w])
        self.epsb, self.b_epsb = self.sb("epsb", [128, 4])
        for col, val in ((0, EPS), (1, -0.5 * float(np.log(128.0))), (2, 0.0), (3, 1.0)):
            S.op("pool", lambda e: e.memset(self.epsb[:, col:col + 1], val), writes=[self.b_epsb])

    def load_cols(self, src_rows, n, dst, bdst, func=None):
        S = self.S
        rt, brt = self.rowtmp.next()
        S.dma("sp", rt[0:n, :], src_rows, writes=[brt])
        if func is not None:
            S.op("act", lambda e: e.activation(out=rt[0:n, :], in_=rt[0:n, :], func=func), reads=[brt], writes=[brt])
        ps, bps = S.ps()
        S.op("pe", lambda e: e.transpose(out=ps[:, 0:128], in_=rt[:, :], identity=self.ident_f[:]),
             reads=[brt, self.b_ident_f], writes=[bps])
        S.op("dve", lambda e: e.tensor_copy(out=dst, in_=ps[:, 0:n]), reads=[bps], writes=[bdst])

    def w_plan(self, src2d, groups):
        self._wsrc = src2d
        self._wgroups = list(groups)
        self._wi = 0
        self._wq = []
        self._w_issue()

    def _w_issue(self):
        if self._wi < len(self._wgroups):
            c0, ncols = self._wgroups[self._wi]
            self._wi += 1
            self._wq.append(((c0, ncols), self._load_w_raw(self._wsrc, c0, ncols)))

    def load_w(self, src2d, c0, ncols, nk=8):
        if getattr(self, "_wq", None):
            key, val = self._wq.pop(0)
            assert key == (c0, ncols), (key, c0, ncols)
            self._w_issue()
            return val
        return self._load_w_raw(src2d, c0, ncols, nk)

    def _load_w_raw(self, src2d, c0, ncols, nk=8):
        S = self.S
        st, bst = self.wst.next()
        wb, bwb = self.wb.next()
        S.dma("sp", st[:, 0:nk, 0:ncols], src2d.rearrange("(k p) n -> p k n", p=128)[:, :, c0:c0 + ncols], writes=[bst])
        S.op("pool", lambda e: e.tensor_copy(out=wb[:, 0:nk, 0:ncols], in_=st[:, 0:nk, 0:ncols]), reads=[bst], writes=[bwb])
        return wb, bwb

    def phase0(self, l):
        S, din = self.S, self.din
        self.es = ExitStack()
        for n in self.par:
            t, b = self.par[n]
            S.dma("sp", t[:], din[n][l].partition_broadcast(128), writes=[b])
        self.load_cols(din["c"].rearrange("(k p) -> k p", p=128), 8, self.scol[:, :, 0], self.b_scol, AF.Silu)
        self.load_cols(din["c_ctx"].rearrange("(k p) -> k p", p=128), 8, self.scol[:, :, 1], self.b_scol, AF.Silu)
        self.load_cols(din["norm_w"][l].rearrange("(k p) -> k p", p=128), 8, self.nwcol[:], self.b_nwcol)
        self.load_cols(din["b_ada"][l].rearrange("(k p) -> k p", p=128), 24, self.badacol[:], self.b_badacol)
        cw, bcw = self.tsb("cwrows", [128, 1536])
        S.op("pool", lambda e: e.memset(cw[:], 0.0), writes=[bcw])
        S.dma("sp", cw[0:5, :], din["a_conv_w"][l], writes=[bcw])
        for ct in range(12):
            ps, bps = S.ps()
            S.op("pe", lambda e: e.transpose(out=ps[:, 0:128], in_=cw[:, ct * 128:(ct + 1) * 128], identity=self.ident_f[:]),
                 reads=[bcw, self.b_ident_f], writes=[bps])
            S.op("dve", lambda e: e.tensor_copy(out=self.convw[:, ct, :], in_=ps[:, 0:5]), reads=[bps], writes=[self.b_convw])
        psm, bpsm = S.ps()
        for g in range(6):
            st, bst = self.wst.next()
            S.dma("sp", st[:], din["w_ada"][l].rearrange("(k p) n -> p k n", p=128)[:, :, g * 512:(g + 1) * 512], writes=[bst])
            for jl in range(4):
                j = g * 4 + jl
                for kc in range(8):
                    S.op("pe", lambda e: e.matmul(out=psm[:, 2 * j:2 * j + 2], lhsT=st[:, kc, jl * 128:(jl + 1) * 128], rhs=self.scol[:, kc, :],
                                                  start=(kc == 0), stop=(kc == 7)),
                         reads=[bst, self.b_scol], writes=[bpsm])
        S.op("dve", lambda e: e.tensor_tensor(out=self.mod[:], in0=psm[:, 0:48].rearrange("p (j s) -> p j s", s=2),
                                              in1=self.badacol[:].unsqueeze(2).to_broadcast([128, 24, 2]), op=ALU.add),
             reads=[bpsm, self.b_badacol], writes=[self.b_mod])
        S.op("dve", lambda e: e.scalar_tensor_tensor(out=self.Afm[:], in0=self.mod[:, 8:16, :], scalar=1.0,
                                                     in1=self.nwcol[:].unsqueeze(2).to_broadcast([128, 8, 2]), op0=ALU.add, op1=ALU.mult),
             reads=[self.b_mod, self.b_nwcol], writes=[self.b_Afm])
        S.barrier()
        self.es.close()

    def bc_rows(self, dst_fn, bdst, col_fn, bcol, nchunks):
        S = self.S
        for half in range(nchunks // 4):
            t, bt = self.gdiag.next()
            for q in range(4):
                kc = half * 4 + q
                S.op("dve", lambda e: e.tensor_scalar(out=t[:, q * 128:(q + 1) * 128], in0=self.ident_f[:], scalar1=col_fn(kc),
                                                      scalar2=None, op0=ALU.mult),
                     reads=[self.b_ident_f, bcol], writes=[bt])
            ps, bps = S.ps()
            S.op("pe", lambda e: e.matmul(out=ps[:], lhsT=self.ones_f[:], rhs=t[:], start=True, stop=True),
                 reads=[self.b_ones_f, bt], writes=[bps])
            S.op("act", lambda e: e.copy(out=dst_fn(half), in_=ps[:]), reads=[bps], writes=[bdst])

    def phase1(self, l):
        S, din = self.S, self.din
        self.es = ExitStack()
        self.p1_xt = self.trot("p1_xt", [128, 1024], F32, 2)
        self.p1_sq, self.b_p1_sq = self.tsb("p1_sq", [128, 1024])
        self.p1_st = self.trot("p1_st", [128, 4], F32, 2)
        for tt in range(NT):
            s = 1 if tt < 2 else 0
            if tt < 2:
                src = (din["ctx"] if l == 0 else self.scr["CTXS"])[tt * 128:(tt + 1) * 128, :]
                rd = [] if l == 0 else [self.db("CTXS", tt)]
            else:
                src = (din["x"] if l == 0 else self.scr["XS"])[(tt - 2) * 128:(tt - 1) * 128, :]
                rd = [] if l == 0 else [self.db("XS", tt)]
            xt, bxt = self.p1_xt.next()
            st, bst = self.p1_st.next()
            S.dma("sp", xt[:], src, reads=rd, writes=[bxt])
            S.op("act", lambda e: e.activation(out=self.p1_sq[:], in_=xt[:], func=AF.Square, accum_out=st[:, 0:1]),
                 reads=[bxt], writes=[self.b_p1_sq, bst])
            S.op("dve", lambda e: e.tensor_scalar(out=st[:, 1:2], in0=st[:, 0:1], scalar1=1.0 / D, scalar2=EPS, op0=ALU.mult, op1=ALU.add),
                 reads=[bst], writes=[bst])
            S.op("act", lambda e: e.activation(out=st[:, 2:3], in_=st[:, 1:2], func=AF.Ln), reads=[bst], writes=[bst])
            S.op("act", lambda e: e.activation(out=st[:, 3:4], in_=st[:, 2:3], func=AF.Exp, scale=-0.5), reads=[bst], writes=[bst])
            S.op("act", lambda e: e.activation(out=xt[:], in_=xt[:], func=AF.Copy, scale=st[:, 3:4]),
                 reads=[bst, bxt], writes=[bxt])
            for half in range(2):
                ps, bps = S.ps()
                for q in range(4):
                    kc = half * 4 + q
                    S.op("pe", lambda e: e.transpose(out=ps[:, q * 128:(q + 1) * 128], in_=xt[:, kc * 128:(kc + 1) * 128], identity=self.ident_f[:]),
                         reads=[bxt, self.b_ident_f], writes=[bps])
                for q in range(4):
                    kc = half * 4 + q
                    dst = self.hT[:, kc, tt * 128:(tt + 1) * 128]
                    if q % 2 == 0:
                        S.op("dve", lambda e: e.tensor_scalar(out=dst, in0=ps[:, q * 128:(q + 1) * 128], scalar1=self.Afm[:, kc, s:s + 1],
                                                              scalar2=self.mod[:, kc, s:s + 1], op0=ALU.mult, op1=ALU.add),
                             reads=[bps, self.b_Afm, self.b_mod], writes=[self.b_hT[tt]])
                    else:
                        S.op("act", lambda e: e.activation(out=dst, in_=ps[:, q * 128:(q + 1) * 128], func=AF.Identity,
                                                           scale=self.Afm[:, kc, s:s + 1], bias=self.mod[:, kc, s:s + 1]),
                             reads=[bps, self.b_Afm, self.b_mod], writes=[self.b_hT[tt]])
        S.barrier()
        self.es.close()

    def tok_groups(self):
        g = [(0, 256, 0, 2)]
        for i in range(8):
            g.append((256 + i * 512, 512, 2 + 4 * i, 4))
        return g

    class WStream:
        def __init__(self, mk, src2d, groups, bufs):
            self.mk, self.src, self.groups, self.bufs = mk, src2d, list(groups), bufs
            self.i = 0
            self.q = []
            self._issue()

        def _issue(self):
            if self.i < len(self.groups):
                c0, ncols = self.groups[self.i]
                wb, bwb = self.bufs[self.i % len(self.bufs)]
                S = self.mk.S
                st, bst = self.mk.wst.next()
                S.dma("sp", st[:, :, 0:ncols], self.src.rearrange("(k p) n -> p k n", p=128)[:, :, c0:c0 + ncols], writes=[bst])
                S.op("pool", lambda e: e.tensor_copy(out=wb[:, :, 0:ncols], in_=st[:, :, 0:ncols]), reads=[bst], writes=[bwb])
                self.q.append(((c0, ncols), (wb, bwb)))
                self.i += 1

        def get(self, c0, ncols):
            key, val = self.q.pop(0)
            assert key == (c0, ncols), (key, c0, ncols)
            self._issue()
            return val

    def proj_tm_gen(self, l, c0, ncols, handler, ws):
        S = self.S
        if hasattr(self, "marks"):
            self.marks.append(("  L%d tm@%d" % (l, c0), {k: v.count for k, v in S.engs.items()}))
        wb, bwb = ws.get(c0, ncols)
        for tt in range(NT):
            ps, bps = S.ps()
            for kc in range(8):
                S.op("pe", lambda e: e.matmul(out=ps[:, 0:ncols], lhsT=self.hT[:, kc, tt * 128:(tt + 1) * 128], rhs=wb[:, kc, 0:ncols],
                                              start=(kc == 0), stop=(kc == 7)),
                     reads=[self.b_hT[tt], bwb], writes=[bps])
            handler(tt, ps, bps)
            yield

    def proj_tm(self, l, c0, ncols, handler, ws):
        for _ in self.proj_tm_gen(l, c0, ncols, handler, ws):
            pass

    @staticmethod
    def run_pair(g1, g2):
        gens = [g1, g2]
        while gens:
            for g_ in list(gens):
                try:
                    next(g_)
                except StopIteration:
                    gens.remove(g_)

    def transpose_out(self, src_fn, n, rows, dst, dst_buf_list, tag, pool=None):
        S = self.S
        ps, bps = S.ps(pool)
        psb = ps[:].bitcast(BF16)
        for i in range(n):
            src, bsrc = src_fn(i)
            S.op("pe", lambda e: e.transpose(out=psb[0:rows, i * 128:(i + 1) * 128], in_=src, identity=self.ident_b[:]),
                 reads=[bsrc, self.b_ident_b], writes=[bps])
        S.op("act", lambda e: e.copy(out=dst, in_=psb[0:rows, 0:n * 128]), reads=[bps], writes=dst_buf_list)

    def phase2(self, l):
        S, din, scr = self.S, self.din, self.scr
        last = (l == self.nlayers - 1)
        self.es = ExitStack()
        self.zt = self.trot("zt", [128, 512], BF16, 2)
        self.tmpA = self.trot("tmpA", [128, 512], F32, 2)
        self.tmpB = self.trot("tmpB", [128, 512], F32, 2)
        self.tmo = self.trot("tmo", [128, 512], BF16, 3)
        self.fmo = self.trot("fmo", [128, 512], BF16, 3)
        self.qa = self.trot("qa", [128, 8, 128], BF16, 2)
        self.ka = self.trot("ka", [128, 2, 128], BF16, 2)
        self.vb = self.trot("vb", [128, 2, 65], BF16, 2)
        self.qaT = self.trot("qaT", [128, 8, 128], BF16, 2)
        self.kaT = self.trot("kaT", [128, 2, 128], BF16, 2)
        self.csc = self.trot("csc", [128, 2, 64], F32, 3)
        self.csb = self.trot("csb", [128, 2, 256], F32, 3)
        self.st8 = self.trot("st8", [128, 24], F32, 4)
        self.tmp8 = self.trot("tmp8", [128, NT, 8], F32, 4)
        for n in ("LNB", "GRAW", "GT"):
            self.asc[n] = self.tsb("asc_" + n, [128, NT, 8])
        self.KM, self.b_KM = self.tsb("KM", [128, 2])
        self.half8, self.b_half8 = self.tsb("half8", [128, 8])
        S.op("pool", lambda e: e.memset(self.half8[:], 0.5), writes=[self.b_half8])
        self.rowbuf, self.b_rowbuf = self.tsb("rowbuf", [128, 4360])
        self.slrow, self.b_slrow = self.tsb("slrow", [128, NTOK])
        S.op("pool", lambda e: e.memset(self.rowbuf[:], 0.0), writes=[self.b_rowbuf])
        for it in self.vb.items:
            S.op("pool", lambda e: e.memset(it[0][:], 1.0), writes=[it[1]])
        for it in self.ka.items + self.qa.items:
            S.op("pool", lambda e: e.memset(it[0][:], 0.0), writes=[it[1]])
        for it in self.ka.items:
            S.op("pool", lambda e: e.memset(it[0][:, :, 64:65], 1.0), writes=[it[1]])
        S.op("pool", lambda e: e.memset(self.KM[:], 0.0), writes=[self.b_KM])

        def silu_out(name):
            def h(tt, ps, bps):
                z, bz = self.zt.next()
                S.op("act", lambda e: e.activation(out=z[:], in_=ps[:], func=AF.Silu), reads=[bps], writes=[bz])
                S.dma("act", scr[name][tt * 128:(tt + 1) * 128, :], z[:], reads=[bz], writes=[self.db(name, tt)])
            return h

        wsrc = din["w_in"][l]
        bufsA, bufsB = self.wb.items[0:2], self.wb.items[2:4]
        ws1 = self.WStream(self, wsrc, [(O_AZ, 512), (O_AB, 16), (O_BKV, 256)], bufsA)
        self.proj_tm(l, O_AZ, 512, silu_out("ZA"), ws1)
        if self.stop == "p2a":
            S.barrier()
            self.es.close()
            return

        def h_ab(tt, ps, bps):
            S.op("act", lambda e: e.copy(out=self.SCR[:, tt, :], in_=ps[:, 0:16]), reads=[bps], writes=[self.b_SCR])
        self.proj_tm(l, O_AB, 16, h_ab, ws1)
        if self.stop == "p2b1":
            S.barrier()
            self.es.close()
            return
        self.a_scalars(l)
        if self.stop in ("p2b", "p2b2"):
            S.barrier()
            self.es.close()
            return

        def load_cs(tt, which):
            rotp, cn, sn, w = (self.csc, "k_cosc", "k_sinc", 64) if which == "c" else (self.csb, "k_cosb", "k_sinb", 256)
            cs, bcs = rotp.next()
            r0 = (tt - 2) * 128
            S.dma("sp", cs[:, 0, :], din[cn][r0:r0 + 128, :], writes=[bcs])
            S.dma("sp", cs[:, 1, :], din[sn][r0:r0 + 128, :], writes=[bcs])
            return cs, bcs

        def rope(x1, x2, cosb, sinb, o1, o2, shape_fn, bps, bcs, bout, scale=None):
            ta, bta = self.tmpA.next()
            tb, btb = self.tmpB.next()
            ta1, ta2 = shape_fn(ta[:, 0:256]), shape_fn(ta[:, 256:512])
            tb1, tb2 = shape_fn(tb[:, 0:256]), shape_fn(tb[:, 256:512])
            if scale is None:
                mul = lambda o, a, b: (lambda e: e.tensor_tensor(out=o, in0=a, in1=b, op=ALU.mult))
            else:
                mul = lambda o, a, b: (lambda e: e.scalar_tensor_tensor(out=o, in0=a, scalar=scale, in1=b, op0=ALU.mult, op1=ALU.mult))
            S.op("dve", mul(ta1, x1, cosb), reads=[bps, bcs], writes=[bta])
            S.op("dve", mul(tb1, x2, sinb), reads=[bps, bcs], writes=[btb])
            S.op("dve", mul(ta2, x1, sinb), reads=[bps, bcs], writes=[bta])
            S.op("dve", mul(tb2, x2, cosb), reads=[bps, bcs], writes=[btb])
            S.op("pool", lambda e: e.tensor_tensor(out=o1, in0=ta1, in1=tb1, op=ALU.subtract), reads=[bta, btb], writes=[bout])
            S.op("pool", lambda e: e.tensor_tensor(out=o2, in0=ta2, in1=tb2, op=ALU.add), reads=[bta, btb], writes=[bout])

        def rope_b(ps_ap, nha, tt, dst, bdst, bps, nh):
            o, bo = self.tmo.next()
            if tt >= 2:
                cs, bcs = load_cs(tt, "b")
                pv = ps_ap.rearrange("p (g f k) -> p g f k", f=2, k=16)
                ov = o[:, 0:nha * 32].rearrange("p (g f k) -> p g f k", f=2, k=16)
                cosb = cs[:, 0, 0:nha * 16].rearrange("p (g k) -> p g k", k=16)
                sinb = cs[:, 1, 0:nha * 16].rearrange("p (g k) -> p g k", k=16)
                rope(pv[:, :, 0, :], pv[:, :, 1, :], cosb, sinb, ov[:, :, 0, :], ov[:, :, 1, :],
                     lambda a: a[:, 0:nha * 16].rearrange("p (g k) -> p g k", k=16), bps, bcs, bo)
                S.op("act", lambda e: e.copy(out=dst[:, :, 0:64], in_=o[:, 0:nha * 32].rearrange("p (h k) -> p h k", h=nh)), reads=[bo], writes=[bdst])
            else:
                S.op("act", lambda e: e.copy(out=dst[:, :, 0:64], in_=ps_ap.rearrange("p (h k) -> p h k", h=nh)), reads=[bps], writes=[bdst])

        def h_bkv(tt, ps, bps):
            ka, bka = self.ka.next()
            rope_b(ps[:, 0:128], 4, tt, ka, bka, bps, 2)
            ta, bta = self.tmpA.next()
            st, bst = self.st8.next()
            S.op("act", lambda e: e.activation(out=ta[:, 0:128], in_=ps[:, 0:128], func=AF.Square), reads=[bps], writes=[bta])
            S.op("dve", lambda e: e.tensor_reduce(out=st[:, 0:2], in_=ta[:, 0:128].rearrange("p (h k) -> p h k", h=2), axis=AX.X, op=ALU.add),
                 reads=[bta], writes=[bst])
            S.op("dve", lambda e: e.tensor_tensor(out=self.KM[:], in0=self.KM[:], in1=st[:, 0:2], op=ALU.max), reads=[bst, self.b_KM], writes=[self.b_KM])
            vb, bvb = self.vb.next()
            S.op("dve", lambda e: e.tensor_copy(out=vb[:, :, 0:64], in_=ps[:, 128:256].rearrange("p (h k) -> p h k", h=2)),
                 reads=[bps], writes=[bvb])
            S.dma("act", scr["VB_TM"][tt * 128:(tt + 1) * 128, :, :], vb[:], reads=[bvb], writes=[self.db("VB_TM", tt)])
            kT, bkT = self.kaT.next()
            self.transpose_out(lambda i: (ka[:, i, :], bka), 2, 128, kT[:].rearrange("r h t -> r (h t)"), [bkT], "kbt")
            S.dma("act", scr["KB_FM"].rearrange("h r t -> r h t")[:, :, tt * 128:(tt + 1) * 128], kT[:], reads=[bkT], writes=[self.db("KB_FM", tt)])
        self.proj_tm(l, O_BKV, 256, h_bkv, ws1)
        if self.stop == "p2c":
            S.barrier()
            self.es.close()
            return
        S.op("dve", lambda e: e.tensor_reduce(out=self.kmx[:, 0:1], in_=self.KM[:], axis=AX.X, op=ALU.max), reads=[self.b_KM], writes=[self.b_kmx])
        dgk, bdgk = self.gdiag.next()
        S.op("dve", lambda e: e.tensor_scalar(out=dgk[:, 0:128], in0=self.ident_f[:], scalar1=self.kmx[:, 0:1], scalar2=None, op0=ALU.mult),
             reads=[self.b_ident_f, self.b_kmx], writes=[bdgk])
        ps, bps = S.ps()
        S.op("pe", lambda e: e.matmul(out=ps[:, 0:128], lhsT=self.ones_f[:], rhs=dgk[:, 0:128], start=True, stop=True),
             reads=[self.b_ones_f, bdgk], writes=[bps])
        kr, bkr = self.tsb("kmrow", [128, 4])
        S.op("dve", lambda e: e.tensor_reduce(out=kr[:, 0:1], in_=ps[:, 0:128], axis=AX.X, op=ALU.max), reads=[bps], writes=[bkr])
        S.op("act", lambda e: e.activation(out=kr[:, 1:2], in_=kr[:, 0:1], func=AF.Ln), reads=[bkr], writes=[bkr])
        S.op("act", lambda e: e.activation(out=kr[:, 2:3], in_=kr[:, 1:2], func=AF.Exp, scale=0.5), reads=[bkr], writes=[bkr])
        S.op("dve", lambda e: e.tensor_scalar(out=self.kmx[:, 1:2], in0=kr[:, 2:3], scalar1=-1.0, scalar2=None, op0=ALU.mult), reads=[bkr], writes=[self.b_kmx])
        S.op("dve", lambda e: e.tensor_scalar(out=self.kmx[:, 2:3], in0=kr[:, 2:3], scalar1=-0.125, scalar2=None, op0=ALU.mult), reads=[bkr], writes=[self.b_kmx])

        def h_bq(tt, ps, bps):
            qa, bqa = self.qa.next()
            ta, bta = self.tmpA.next()
            st, bst = self.st8.next()
            S.op("act", lambda e: e.activation(out=ta[:], in_=ps[:], func=AF.Square), reads=[bps], writes=[bta])
            S.op("dve", lambda e: e.tensor_reduce(out=st[:, 0:8], in_=ta[:].rearrange("p (h k) -> p h k", h=8), axis=AX.X, op=ALU.add),
                 reads=[bta], writes=[bst])
            S.op("pool", lambda e: e.tensor_tensor(out=st[:, 16:24], in0=st[:, 0:8], in1=self.half8[:], op=ALU.pow), reads=[bst, self.b_half8], writes=[bst])
            rope_b(ps[:], 16, tt, qa, bqa, bps, 8)
            S.op("dve", lambda e: e.tensor_scalar(out=qa[:, :, 64], in0=st[:, 16:24], scalar1=self.kmx[:, 1:2], scalar2=None, op0=ALU.mult),
                 reads=[bst, self.b_kmx], writes=[bqa])
            S.op("dve", lambda e: e.scalar_tensor_tensor(out=self.SH2[:, tt, :], in0=st[:, 16:24], scalar=self.kmx[:, 2:3], in1=self.par["b_sink"][0][:],
                                                         op0=ALU.mult, op1=ALU.add),
                 reads=[bst, self.b_kmx, self.par["b_sink"][1]], writes=[self.b_SH2])
            qT, bqT = self.qaT.next()
            self.transpose_out(lambda i: (qa[:, i, :], bqa), 8, 128, qT[:].rearrange("r h t -> r (h t)"), [bqT], "qbt")
            S.dma("act", scr["QB_FM"].rearrange("h r t -> r h t")[:, :, tt * 128:(tt + 1) * 128], qT[:], reads=[bqT], writes=[self.db("QB_FM", tt)])
        if self.stop == "p2d":
            S.barrier()
            self.es.close()
            return

        def h_cqk(name_fm, name_tm, scale):
            def h(tt, ps, bps):
                o, bo = self.tmo.next()
                if tt >= 2:
                    cs, bcs = load_cs(tt, "c")
                    pv = ps[:].rearrange("p (h f k) -> p h f k", h=4, f=2)
                    ov = o[:].rearrange("p (h f k) -> p h f k", h=4, f=2)
                    cosb = cs[:, 0, :].unsqueeze(1).to_broadcast([128, 4, 64])
                    sinb = cs[:, 1, :].unsqueeze(1).to_broadcast([128, 4, 64])
                    rope(pv[:, :, 0, :], pv[:, :, 1, :], cosb, sinb, ov[:, :, 0, :], ov[:, :, 1, :],
                         lambda a: a.rearrange("p (h k) -> p h k", h=4), bps, bcs, bo, scale=scale)
                else:
                    S.op("act", lambda e: e.activation(out=o[:], in_=ps[:], func=AF.Copy, scale=(1.0 if scale is None else scale)), reads=[bps], writes=[bo])
                if name_tm is not None:
                    S.dma("act", scr[name_tm][tt * 128:(tt + 1) * 128, :], o[:], reads=[bo], writes=[self.db(name_tm, tt)])
                f, bf = self.fmo.next()
                self.transpose_out(lambda i: (o[:, i * 128:(i + 1) * 128], bo), 4, 128, f[:], [bf], "cfm")
                S.dma("act", scr[name_fm].rearrange("h p t -> p h t")[:, :, tt * 128:(tt + 1) * 128], f[:].rearrange("p (h t) -> p h t", h=4),
                      reads=[bf], writes=[self.db(name_fm, tt)])
            return h

        def h_cv(tt, ps, bps):
            o, bo = self.tmo.next()
            S.op("act", lambda e: e.copy(out=o[:], in_=ps[:]), reads=[bps], writes=[bo])
            S.dma("act", scr["VC_TM"][tt * 128:(tt + 1) * 128, :], o[:], reads=[bo], writes=[self.db("VC_TM", tt)])
        wsZ = self.WStream(self, wsrc, [(O_BZ, 512), (O_CZ, 512)], bufsB)
        self.proj_tm(l, O_BZ, 512, silu_out("ZB"), wsZ)
        self.proj_tm(l, O_CZ, 512, silu_out("ZC"), wsZ)
        groups = self.tok_groups()
        wsH = self.WStream(self, wsrc, [(O_BQ, 512), (O_CQ, 512), (O_CK, 512)] + [(g * 512, 512) for g in range(3)], bufsA)
        wsL = self.WStream(self, wsrc, [(O_MG + g * 512, 512) for g in range(6)] + [(O_CV, 512)], bufsB)
        wsF = wsH
        wsM = wsL

        def heavy():
            yield from self.proj_tm_gen(l, O_BQ, 512, h_bq, wsH)
            yield from self.proj_tm_gen(l, O_CQ, 512, h_cqk("QC_FM", None, None), wsH)
            yield from self.proj_tm_gen(l, O_CK, 512, h_cqk("KC_FM", "KC_TM", float(CH) ** -0.5), wsH)

        def light():
            yield from self.proj_tm_gen(l, O_CV, 512, h_cv, wsL)
        if self.stop == "p2e":
            S.barrier()
            self.es.close()
            return


        def afm_gen():
            for g3 in range(3):
                self.marks.append(("  L%d afm%d" % (l, g3), {k: v.count for k, v in S.engs.items()}))
                wb, bwb = wsF.get(g3 * 512, 512)
                for cl in range(4):
                    ct = g3 * 4 + cl
                    head = cl
                    for (t0, n, tile0, ntile) in groups:
                        ps, bps = S.ps()
                        for kc in range(8):
                            S.op("pe", lambda e: e.matmul(out=ps[:, 0:n], lhsT=wb[:, kc, cl * 128:(cl + 1) * 128], rhs=self.hT[:, kc, t0:t0 + n],
                                                          start=(kc == 0), stop=(kc == 7)),
                                 reads=[self.b_hT[tile0 + i] for i in range(ntile)] + [bwb], writes=[bps])
                        off = 2 + t0 if t0 == 0 else 6 + t0
                        S.op("act", lambda e: e.copy(out=self.rowbuf[:, off:off + n], in_=ps[:, 0:n]), reads=[bps], writes=[self.b_rowbuf.sub(t0)])
                        yield
                    for gi, (t0, n, tile0, ntile) in enumerate(groups):
                        off = 2 + t0 if t0 == 0 else 6 + t0
                        cv, bcv = self.tmpA.next()
                        nb = [gi] if gi == 0 else [j for j in (gi - 1, gi, gi + 1) if 1 <= j < len(groups)]
                        rb_reads = [self.b_rowbuf.sub(groups[j][0]) for j in nb]
                        for k in range(5):
                            src = self.rowbuf[:, off + k - 2:off + k - 2 + n]
                            if k == 0:
                                S.op("dve", lambda e: e.tensor_scalar(out=cv[:, 0:n], in0=src, scalar1=self.convw[:, ct, 0:1], scalar2=None, op0=ALU.mult),
                                     reads=rb_reads + [self.b_convw], writes=[bcv])
                            else:
                                S.op("dve", lambda e: e.scalar_tensor_tensor(out=cv[:, 0:n], in0=src, scalar=self.convw[:, ct, k:k + 1], in1=cv[:, 0:n],
                                                                             op0=ALU.mult, op1=ALU.add),
                                     reads=rb_reads + [self.b_convw, bcv], writes=[bcv])
                        if g3 < 2:
                            S.op("act", lambda e: e.activation(out=self.slrow[:, t0:t0 + n], in_=cv[:, 0:n], func=AF.Silu), reads=[bcv], writes=[self.b_slrow.sub(t0)])
                            yield
                        else:
                            o, bo = self.tmo.next()
                            S.op("act", lambda e: e.activation(out=o[:, 0:n], in_=cv[:, 0:n], func=AF.Silu), reads=[bcv], writes=[bo])
                            f, bf = self.fmo.next()
                            self.transpose_out(lambda i: (o[:, i * 128:(i + 1) * 128], bo), ntile, 128, f[:, 0:n], [bf], "va")
                            S.dma("act", scr["VA_TM"][head].rearrange("(t p) c -> p t c", p=128)[:, tile0:tile0 + ntile, :],
                                  f[:, 0:n].rearrange("p (t c) -> p t c", c=128), reads=[bf], writes=[self.db("VA_TM", tile0 + i) for i in range(ntile)])
                            yield
                    if g3 == 2:
                        continue
                    qscale_bias = -0.5 * float(np.log(128.0)) if g3 == 0 else 0.0
                    name = "QA_FM" if g3 == 0 else "KA_FM"
                    for (t0, n, tile0, ntile) in groups:
                        sq, bsq = self.tmo.next()
                        S.op("act", lambda e: e.activation(out=sq[:, 0:n], in_=self.slrow[:, t0:t0 + n], func=AF.Square), reads=[self.b_slrow.sub(t0)], writes=[bsq])
                        ps, bps = S.ps()
                        S.op("pe", lambda e: e.matmul(out=ps[:, 0:n], lhsT=self.ones_b[:], rhs=sq[:, 0:n], start=True, stop=True),
                             reads=[self.b_ones_b, bsq], writes=[bps])
                        ta, bta = self.tmpB.next()
                        S.op("act", lambda e: e.activation(out=ta[:, 0:n], in_=ps[:, 0:n], func=AF.Ln, bias=self.epsb[:, 0:1]), reads=[bps, self.b_epsb], writes=[bta])
                        S.op("act", lambda e: e.activation(out=ta[:, 0:n], in_=ta[:, 0:n], func=AF.Exp, scale=-0.5, bias=self.epsb[:, 1 + g3:2 + g3]),
                             reads=[bta, self.b_epsb], writes=[bta])
                        o, bo = self.fmo.next()
                        S.op("dve", lambda e: e.tensor_tensor(out=o[:, 0:n], in0=self.slrow[:, t0:t0 + n], in1=ta[:, 0:n], op=ALU.mult),
                             reads=[self.b_slrow.sub(t0), bta], writes=[bo])
                        S.dma("act", scr[name][head][:, t0:t0 + n], o[:, 0:n], reads=[bo], writes=[self.db(name, tile0 + i) for i in range(ntile)])
                        if g3 == 1:
                            f, bf = self.zt.next()
                            self.transpose_out(lambda i: (o[:, i * 128:(i + 1) * 128], bo), ntile, 128, f[:, 0:n], [bf], "ka")
                            S.dma("act", scr["KA_TM"][head].rearrange("(t p) c -> p t c", p=128)[:, tile0:tile0 + ntile, :],
                                  f[:, 0:n].rearrange("p (t c) -> p t c", c=128), reads=[bf], writes=[self.db("KA_TM", tile0 + i) for i in range(ntile)])
                        yield


        def merge_gen():
            for g6 in range(6):
                self.marks.append(("  L%d mg%d" % (l, g6), {k: v.count for k, v in S.engs.items()}))
                wb, bwb = wsM.get(O_MG + g6 * 512, 512)
                for cl in range(4):
                    ct = g6 * 4 + cl
                    for (t0, n, tile0, ntile) in groups:
                        if last and t0 == 0:
                            continue
                        ps, bps = S.ps()
                        for kc in range(8):
                            S.op("pe", lambda e: e.matmul(out=ps[:, 0:n], lhsT=wb[:, kc, cl * 128:(cl + 1) * 128], rhs=self.hT[:, kc, t0:t0 + n],
                                                          start=(kc == 0), stop=(kc == 7)),
                                 reads=[self.b_hT[tile0 + i] for i in range(ntile)] + [bwb], writes=[bps])
                        o, bo = self.fmo.next()
                        S.op("act", lambda e: e.activation(out=o[:, 0:n], in_=ps[:, 0:n], func=AF.Sigmoid), reads=[bps], writes=[bo])
                        S.dma("act", scr["GM_FM"][ct * 128:(ct + 1) * 128, t0:t0 + n], o[:, 0:n], reads=[bo],
                              writes=[self.db("GM_FM%d" % ct, tile0 + i) for i in range(ntile)])
                        yield
        self.run_pair(heavy(), merge_gen())
        self.run_pair(afm_gen(), light())
        if self.stop == "p2":
            self.dump_p2()
        S.barrier()
        self.es.close()

    def a_scalars(self, l):
        S = self.S
        A = self.asc
        braw = self.SCR[:, :, 0:8]
        araw = self.SCR[:, :, 8:16]
        rs = [self.b_SCR]
        one = self.epsb[:, 3:4]

        def softplus_parts(x_ap, xb, neg):
            t1, b1 = self.tmp8.next()
            t2, b2 = self.tmp8.next()
            S.op("act", lambda e: e.activation(out=t1[:], in_=x_ap, func=AF.Abs), reads=xb, writes=[b1])
            S.op("act", lambda e: e.activation(out=t1[:], in_=t1[:], func=AF.Exp, scale=-1.0), reads=[b1], writes=[b1])
            S.op("act", lambda e: e.activation(out=t1[:], in_=t1[:], func=AF.Ln, bias=one), reads=[b1, self.b_epsb], writes=[b1])
            S.op("dve", lambda e: e.tensor_scalar(out=t2[:], in0=x_ap, scalar1=(-1.0 if neg else 1.0), scalar2=0.0, op0=ALU.mult, op1=ALU.max),
                 reads=xb, writes=[b2])
            return (t2, b2), (t1, b1)

        (m, bm), (l1, bl1) = softplus_parts(braw, rs, True)
        LNB, bLNB = A["LNB"]
        S.op("dve", lambda e: e.scalar_tensor_tensor(out=LNB[:], in0=m[:], scalar=-1.0, in1=l1[:], op0=ALU.mult, op1=ALU.subtract),
             reads=[bm, bl1], writes=[bLNB])
        BETA, bBETA = A["BETA"]
        S.op("act", lambda e: e.activation(out=BETA[:], in_=LNB[:], func=AF.Exp), reads=[bLNB], writes=[bBETA])
        xa, bxa = self.tmp8.next()
        dtb, bdtb = self.par["a_dt_bias"]
        S.op("dve", lambda e: e.tensor_tensor(out=xa[:], in0=araw, in1=dtb[:].unsqueeze(1).to_broadcast([128, NT, 8]), op=ALU.add),
             reads=rs + [bdtb], writes=[bxa])
        (m2, bm2), (l2, bl2) = softplus_parts(xa[:], [bxa], False)
        S.op("dve", lambda e: e.tensor_tensor(out=m2[:], in0=m2[:], in1=l2[:], op=ALU.add), reads=[bm2, bl2], writes=[bm2])
        alog, balog = self.par["a_log"]
        nea, bnea = self.st8.next()
        S.op("act", lambda e: e.activation(out=nea[:, 0:8], in_=alog[:], func=AF.Exp), reads=[balog], writes=[bnea])
        GRAW, bGRAW = A["GRAW"]
        S.op("dve", lambda e: e.scalar_tensor_tensor(out=GRAW[:], in0=m2[:], scalar=-1.0, in1=nea[:, 0:8].unsqueeze(1).to_broadcast([128, NT, 8]),
                                                     op0=ALU.mult, op1=ALU.mult),
             reads=[bm2, bnea], writes=[bGRAW])
        GC, bGC = A["GC"]
        GT, bGT = A["GT"]
        if self.stop == "p2b2":
            return
        gflat = GRAW[:].rearrange("p t c -> p (t c)")
        res = []
        for lhs, blhs in ((self.tri[:, 0, :], self.b_tri), (self.tri[:, 1, :], self.b_tri), (self.ones_f[:], self.b_ones_f)):
            ps, bps = S.ps()
            S.op("pe", lambda e: e.matmul(out=ps[:, 0:NT * 8], lhsT=lhs, rhs=gflat, start=True, stop=True), reads=[blhs, bGRAW], writes=[bps])
            res.append((ps[:, 0:NT * 8].rearrange("p (t c) -> p t c", c=8), bps))
        S.op("dve", lambda e: e.tensor_copy(out=GC[:, :, 0:4], in_=res[0][0][:, :, 0:4]), reads=[res[0][1]], writes=[bGC])
        S.op("dve", lambda e: e.tensor_copy(out=GC[:, :, 4:8], in_=res[1][0][:, :, 4:8]), reads=[res[1][1]], writes=[bGC])
        S.op("act", lambda e: e.copy(out=GT[:], in_=res[2][0]), reads=[res[2][1]], writes=[bGT])
        NEGG, bNEGG = A["NEGG"]
        S.op("dve", lambda e: e.tensor_scalar(out=NEGG[:], in0=GC[:], scalar1=-1.0, scalar2=None, op0=ALU.mult), reads=[bGC], writes=[bNEGG])
        LNBMG, bLNBMG = A["LNBMG"]
        S.op("dve", lambda e: e.tensor_tensor(out=LNBMG[:], in0=LNB[:], in1=GC[:], op=ALU.subtract), reads=[bLNB, bGC], writes=[bLNBMG])
        NEGEG, bNEGEG = A["NEGEG"]
        S.op("act", lambda e: e.activation(out=NEGEG[:], in_=GC[:], func=AF.Exp), reads=[bGC], writes=[bNEGEG])
        S.op("dve", lambda e: e.tensor_scalar(out=NEGEG[:], in0=NEGEG[:], scalar1=-1.0, scalar2=None, op0=ALU.mult), reads=[bNEGEG], writes=[bNEGEG])
        ETAIL, bETAIL = A["ETAIL"]
        S.op("dve", lambda e: e.tensor_tensor(out=ETAIL[:], in0=GT[:], in1=GC[:], op=ALU.subtract), reads=[bGT, bGC], writes=[bETAIL])
        S.op("act", lambda e: e.activation(out=ETAIL[:], in_=ETAIL[:], func=AF.Exp), reads=[bETAIL], writes=[bETAIL])
        EGL, bEGL = A["EGL"]
        S.op("act", lambda e: e.activation(out=EGL[:], in_=GT[:], func=AF.Exp), reads=[bGT], writes=[bEGL])

    def core_b(self, l, defer=False):
        S, scr, din = self.S, self.scr, self.din
        last = (l == self.nlayers - 1)
        if not defer:
            self.es = ExitStack()
        KBT = self.R1[:, 0:2 * NTOK].rearrange("p (g t) -> p g t", g=2)
        VBR = self.R1[:, 2 * NTOK:2 * NTOK + NT * 130].rearrange("p (t g c) -> p t g c", g=2, c=65)
        bKBT, bVBR = Buf("KBT"), Buf("VBR")
        S.dma("sp", KBT, scr["KB_FM"].rearrange("g r t -> r g t"), reads=self.dball("KB_FM"), writes=[bKBT])
        S.dma("sp", VBR, scr["VB_TM"].rearrange("(t p) g c -> p t g c", p=128), reads=self.dball("VB_TM"), writes=[bVBR])
        if l == 0:
            bst, bbst = self.tsb("bmst", [128, 2, 512])
            S.dma("sp", bst[:], din["k_bmask"].rearrange("a p n -> p a n"), writes=[bbst])
            S.op("pool", lambda e: e.tensor_copy(out=self.bmask[:], in_=bst[:]), reads=[bbst], writes=[self.b_bmask])
        qTr = self.trot("b_qT", [128, 4, 128], BF16, 2)
        pTr = self.trot("b_pT", [128, 5, 512], BF16, 2)
        zbr = self.trot("b_zb", [128, 512], BF16, 2)
        ybr = self.trot("b_yb", [128, 512], BF16, 2)
        obr = self.trot("b_ob", [128, 256], F32, 2)
        str_ = self.trot("b_st", [128, 16], F32, 6)
        fmo = self.trot("b_fmo", [128, 512], BF16, 2)
        qtiles = list(range(2, NT)) if last else list(range(NT))
        def b_gen():
            for qt in qtiles:
                zb, bzb = zbr.next()
                S.dma("sp", zb[:], scr["ZB"][qt * 128:(qt + 1) * 128, :], reads=[self.db("ZB", qt)], writes=[bzb])
                yb, byb = ybr.next()
                for g in range(2):
                    qT, bqT = qTr.next()
                    S.dma("sp", qT[:], scr["QB_FM"][g * 4:(g + 1) * 4].rearrange("h r t -> r h t")[:, :, qt * 128:(qt + 1) * 128],
                          reads=[self.db("QB_FM", qt)], writes=[bqT])
                    keys = [(0, None), (1, None)]
                    if qt >= 2:
                        if qt - 1 >= 2:
                            keys.append((qt - 1, 0))
                        keys.append((qt, None))
                        if qt + 1 < NT:
                            keys.append((qt + 1, 1))
                    pT, bpT = pTr.next()
                    for idx, (kt, m) in enumerate(keys):
                        ps, bps = S.ps()
                        S.op("pe", lambda e: e.matmul(out=ps[:], lhsT=KBT[:, g, kt * 128:(kt + 1) * 128], rhs=qT[:].rearrange("p h t -> p (h t)"),
                                                      start=True, stop=True), reads=[bKBT, bqT], writes=[bps])
                        S.op("act", lambda e: e.activation(out=pT[:, idx, :], in_=ps[:], func=AF.Exp, scale=0.125), reads=[bps], writes=[bpT.sub(idx)])
                        if m is not None:
                            S.op("pool", lambda e: e.tensor_tensor(out=pT[:, idx, :], in0=pT[:, idx, :], in1=self.bmask[:, m, :], op=ALU.mult),
                                 reads=[bpT.sub(idx), self.b_bmask], writes=[bpT.sub(idx)])
                    po, bpo = S.ps()
                    for h in range(4):
                        for idx, (kt, m) in enumerate(keys):
                            S.op("pe", lambda e: e.matmul(out=po[:, h * 65:(h + 1) * 65], lhsT=pT[:, idx, h * 128:(h + 1) * 128], rhs=VBR[:, kt, g, :],
                                                          start=(idx == 0), stop=(idx == len(keys) - 1)), reads=[bpT.sub(idx), bVBR], writes=[bpo])
                    st, bst_ = str_.next()
                    pov = po[:, 0:260].rearrange("p (h c) -> p h c", c=65)
                    S.op("act", lambda e: e.activation(out=st[:, 0:4], in_=self.SH2[:, qt, g * 4:(g + 1) * 4], func=AF.Exp), reads=[self.b_SH2], writes=[bst_])
                    S.op("dve", lambda e: e.tensor_tensor(out=st[:, 4:8], in0=pov[:, :, 64], in1=st[:, 0:4], op=ALU.add), reads=[bpo, bst_], writes=[bst_])
                    S.op("dve", lambda e: e.reciprocal(out=st[:, 8:12], in_=st[:, 4:8]), reads=[bst_], writes=[bst_])
                    ob, bob = obr.next()
                    S.op("dve", lambda e: e.tensor_tensor(out=ob[:].rearrange("p (h c) -> p h c", c=64), in0=pov[:, :, 0:64],
                                                          in1=st[:, 8:12].unsqueeze(2).to_broadcast([128, 4, 64]), op=ALU.mult),
                         reads=[bpo, bst_], writes=[bob])
                    S.op("pool", lambda e: e.tensor_tensor(out=yb[:, g * 256:(g + 1) * 256], in0=ob[:], in1=zb[:, g * 256:(g + 1) * 256], op=ALU.mult),
                         reads=[bob, bzb], writes=[byb.sub(g)])
                    yield
                self.y_out(1, qt, yb, byb, fmo)
                yield
        if defer:
            return b_gen()
        for _ in b_gen():
            pass
        S.barrier()
        self.es.close()

    def core_bc(self, l):
        self.es = ExitStack()
        gb = self.core_b(l, defer=True)
        gc = self.core_c(l, defer=True)
        self.run_pair(gb, gc)
        self.S.barrier()
        self.es.close()

    def y_out(self, br, tt, y, by, fmo, pool=None):
        S = self.S
        f, bf = fmo.next()
        self.transpose_out(lambda i: (y[:, i * 128:(i + 1) * 128], by), 4, 128, f[:], [bf], "y", pool=pool)
        S.dma("act", self.scr["Y_FM"][br].rearrange("(k p) t -> p k t", p=128)[:, :, tt * 128:(tt + 1) * 128],
              f[:].rearrange("p (k t) -> p k t", k=4), reads=[bf], writes=[self.db("Y_FM%d" % br, tt)])

    def core_c(self, l, defer=False):
        S, scr = self.S, self.scr
        last = (l == self.nlayers - 1)
        if not defer:
            self.es = ExitStack()
        one = self.epsb[:, 3:4]
        cd, bcd = self.par["c_decay"]
        c8 = self.trot("c_c8", [128, 8], F32, 6)
        t1, b1 = c8.next()
        t2, b2 = c8.next()
        LG, bLG = c8.next()
        S.op("act", lambda e: e.activation(out=t1[:], in_=cd[:], func=AF.Abs), reads=[bcd], writes=[b1])
        S.op("act", lambda e: e.activation(out=t1[:], in_=t1[:], func=AF.Exp, scale=-1.0), reads=[b1], writes=[b1])
        S.op("act", lambda e: e.activation(out=t1[:], in_=t1[:], func=AF.Ln, bias=one), reads=[b1, self.b_epsb], writes=[b1])
        S.op("dve", lambda e: e.tensor_scalar(out=t2[:], in0=cd[:], scalar1=-1.0, scalar2=0.0, op0=ALU.mult, op1=ALU.max), reads=[bcd], writes=[b2])
        S.op("dve", lambda e: e.scalar_tensor_tensor(out=LG[:], in0=t2[:], scalar=-1.0, in1=t1[:], op0=ALU.mult, op1=ALU.subtract),
             reads=[b1, b2], writes=[bLG])
        GAMC, bGAMC = c8.next()
        S.op("act", lambda e: e.activation(out=GAMC[:], in_=LG[:], func=AF.Exp, scale=float(CH)), reads=[bLG], writes=[bGAMC])
        KDEC, bKDEC = c8.next()
        S.op("dve", lambda e: e.tensor_tensor(out=KDEC[:], in0=LG[:], in1=self.cj[:], op=ALU.mult), reads=[bLG, self.b_cj], writes=[bKDEC])
        S.op("act", lambda e: e.activation(out=KDEC[:], in_=KDEC[:], func=AF.Exp), reads=[bKDEC], writes=[bKDEC])
        DM, bDM = self.tsb("c_DM", [128, 512])
        QDF, bQDF = self.tsb("c_QDF", [128, 512], BF16)
        QDB, bQDB = self.tsb("c_QDB", [128, 512], BF16)
        tm = self.trot("c_tm", [128, 128], F32, 2)
        for h in range(4):
            ta, bta = tm.next()
            tb, btb = tm.next()
            S.op("act", lambda e: e.activation(out=ta[:], in_=self.cm[:, 0, :], func=AF.Exp, scale=LG[:, h:h + 1]), reads=[self.b_cm, bLG], writes=[bta])
            S.op("dve", lambda e: e.tensor_tensor(out=ta[:], in0=ta[:], in1=self.cm[:, 2, :], op=ALU.mult), reads=[bta, self.b_cm], writes=[bta])
            S.op("act", lambda e: e.activation(out=tb[:], in_=self.cm[:, 1, :], func=AF.Exp, scale=LG[:, 4 + h:5 + h]), reads=[self.b_cm, bLG], writes=[btb])
            S.op("dve", lambda e: e.tensor_tensor(out=tb[:], in0=tb[:], in1=self.cm[:, 3, :], op=ALU.mult), reads=[btb, self.b_cm], writes=[btb])
            S.op("dve", lambda e: e.tensor_tensor(out=ta[:], in0=ta[:], in1=tb[:], op=ALU.add), reads=[bta, btb], writes=[bta])
            S.op("dve", lambda e: e.scalar_tensor_tensor(out=DM[:, h * 128:(h + 1) * 128], in0=self.ident_f[:], scalar=2.0, in1=ta[:], op0=ALU.mult, op1=ALU.add),
                 reads=[bta, self.b_ident_f], writes=[bDM])
            S.op("act", lambda e: e.activation(out=QDF[:, h * 128:(h + 1) * 128], in_=self.cm[:, 4, :], func=AF.Exp, scale=LG[:, h:h + 1]),
                 reads=[self.b_cm, bLG], writes=[bQDF])
            S.op("act", lambda e: e.activation(out=QDB[:, h * 128:(h + 1) * 128], in_=self.cm[:, 5, :], func=AF.Exp, scale=LG[:, 4 + h:5 + h]),
                 reads=[self.b_cm, bLG], writes=[bQDB])
        kTMr = [self.trot("c_kTM%d" % d, [128, 512], BF16, 2) for d in range(2)]
        vr = [self.trot("c_v%d" % d, [128, 512], BF16, 2) for d in range(3)]
        kdr = [self.trot("c_kd%d" % d, [128, 512], BF16, 2) for d in range(2)]
        sbfr = [self.trot("c_sbf%d" % d, [128, 512], BF16, 2) for d in range(2)]
        S32 = [self.tsb("c_S32_%d" % d, [128, 512]) for d in range(2)]
        SCN = ["SCF", "SCB"]

        def state_update(d, cc, kTM, bkTM, v, bv):
            kd, bkd = kdr[d].next()
            for h in range(4):
                S.op("act", lambda e: e.activation(out=kd[:, h * 128:(h + 1) * 128], in_=kTM[:, h * 128:(h + 1) * 128], func=AF.Copy,
                                                   scale=KDEC[:, d * 4 + h:d * 4 + h + 1]),
                     reads=[bkTM, bKDEC], writes=[bkd.sub(h)])
            ps, bps = S.ps()
            for h in range(4):
                S.op("pe", lambda e: e.matmul(out=ps[:, h * 128:(h + 1) * 128], lhsT=kd[:, h * 128:(h + 1) * 128], rhs=v[:, h * 128:(h + 1) * 128],
                                              start=True, stop=True), reads=[bkd, bv], writes=[bps])
            s32, bs32 = S32[d]
            for h in range(4):
                S.op("dve", lambda e: e.scalar_tensor_tensor(out=s32[:, h * 128:(h + 1) * 128], in0=s32[:, h * 128:(h + 1) * 128],
                                                             scalar=GAMC[:, d * 4 + h:d * 4 + h + 1], in1=ps[:, h * 128:(h + 1) * 128],
                                                             op0=ALU.mult, op1=ALU.add),
                     reads=[bs32.sub(h), bGAMC, bps], writes=[bs32.sub(h)])

        def state_pass(d):
            S.op("pool", lambda e: e.memset(S32[d][0][:], 0.0), writes=[S32[d][1]])
            order = list(range(NT)) if d == 0 else [1, 0] + list(range(NT - 1, 1, -1))
            for cc in order:
                sbf, bsbf = sbfr[d].next()
                S.op("act", lambda e: e.copy(out=sbf[:], in_=S32[d][0][:]), reads=[S32[d][1]], writes=[bsbf])
                S.dma("act", scr[SCN[d]][cc], sbf[:], reads=[bsbf], writes=[self.db(SCN[d], cc)])
                kTM, bkTM = kTMr[d].next()
                v, bv = vr[d].next()
                S.dma("sp", kTM[:], scr["KC_TM"][cc * 128:(cc + 1) * 128, :], reads=[self.db("KC_TM", cc)], writes=[bkTM])
                S.dma("sp", v[:], scr["VC_TM"][cc * 128:(cc + 1) * 128, :], reads=[self.db("VC_TM", cc)], writes=[bv])
                yield
                state_update(d, cc, kTM, bkTM, v, bv)
                yield

        qTr = self.trot("c_qT", [128, 512], BF16, 2)
        kTr = self.trot("c_kT", [128, 512], BF16, 2)
        sbr = self.trot("c_sb", [128, 512], BF16, 2)
        sfr = self.trot("c_sf", [128, 512], BF16, 2)
        zcr = self.trot("c_zc", [128, 512], BF16, 2)
        qkr = self.trot("c_qk", [128, 512], BF16, 2)
        qdr = self.trot("c_qd", [128, 512], BF16, 2)
        ofr = self.trot("c_of", [128, 512], F32, 2)
        t5r = self.trot("c_t5", [128, 512], F32, 1)
        ycr = self.trot("c_yc", [128, 512], BF16, 2)
        stc = self.trot("c_st", [128, 24], F32, 3)
        fmo = self.trot("c_fmo", [128, 512], BF16, 2)
        cnw, bcnw = self.par["c_norm_w"]
        eps = self.epsb[:, 0:1]
        def out_pass():
            for cc in range(NT):
                need_out = (cc >= 2) or (not last)
                if need_out:
                    v, bv = vr[2].next()
                    S.dma("sp", v[:], scr["VC_TM"][cc * 128:(cc + 1) * 128, :], reads=[self.db("VC_TM", cc)], writes=[bv])
                    qT, bqT = qTr.next()
                    kT, bkT = kTr.next()
                    sb, bsb = sbr.next()
                    zc, bzc = zcr.next()
                    S.dma("sp", qT[:].rearrange("p (h t) -> p h t", h=4), scr["QC_FM"].rearrange("h p t -> p h t")[:, :, cc * 128:(cc + 1) * 128],
                          reads=[self.db("QC_FM", cc)], writes=[bqT])
                    S.dma("sp", kT[:].rearrange("p (h t) -> p h t", h=4), scr["KC_FM"].rearrange("h p t -> p h t")[:, :, cc * 128:(cc + 1) * 128],
                          reads=[self.db("KC_FM", cc)], writes=[bkT])
                    S.dma("sp", sb[:], scr["SCB"][cc], reads=[self.db("SCB", cc)], writes=[bsb])
                    S.dma("sp", zc[:], scr["ZC"][cc * 128:(cc + 1) * 128, :], reads=[self.db("ZC", cc)], writes=[bzc])
                    sbf, bsbf = sfr.next()
                    S.dma("sp", sbf[:], scr["SCF"][cc], reads=[self.db("SCF", cc)], writes=[bsbf])
                    ps1, bps1 = S.ps()
                    for h in range(4):
                        S.op("pe", lambda e: e.matmul(out=ps1[:, h * 128:(h + 1) * 128], lhsT=kT[:, h * 128:(h + 1) * 128], rhs=qT[:, h * 128:(h + 1) * 128],
                                                      start=True, stop=True), reads=[bkT, bqT], writes=[bps1])
                    qk, bqk = qkr.next()
                    S.op("dve", lambda e: e.tensor_tensor(out=qk[:], in0=ps1[:], in1=DM[:], op=ALU.mult), reads=[bps1, bDM], writes=[bqk])
                    qdf, bqdf = qdr.next()
                    qdb, bqdb = qdr.next()
                    S.op("pool", lambda e: e.tensor_tensor(out=qdf[:], in0=qT[:], in1=QDF[:], op=ALU.mult), reads=[bqT, bQDF], writes=[bqdf])
                    S.op("pool", lambda e: e.tensor_tensor(out=qdb[:], in0=qT[:], in1=QDB[:], op=ALU.mult), reads=[bqT, bQDB], writes=[bqdb])
                    po, bpo = S.ps()
                    for h in range(4):
                        sl = slice(h * 128, (h + 1) * 128)
                        S.op("pe", lambda e: e.matmul(out=po[:, sl], lhsT=qk[:, sl], rhs=v[:, sl], start=True, stop=False), reads=[bqk, bv], writes=[bpo])
                        S.op("pe", lambda e: e.matmul(out=po[:, sl], lhsT=qdf[:, sl], rhs=sbf[:, sl], start=False, stop=False), reads=[bqdf, bsbf], writes=[bpo])
                        S.op("pe", lambda e: e.matmul(out=po[:, sl], lhsT=qdb[:, sl], rhs=sb[:, sl], start=False, stop=True), reads=[bqdb, bsb], writes=[bpo])
                    of, bof = ofr.next()
                    t5, bt5 = t5r.next()
                    st, bst = stc.next()
                    S.op("act", lambda e: e.copy(out=of[:], in_=po[:]), reads=[bpo], writes=[bof])
                    S.op("act", lambda e: e.activation(out=t5[:], in_=po[:], func=AF.Square), reads=[bpo], writes=[bt5])
                    S.op("dve", lambda e: e.tensor_reduce(out=st[:, 0:4], in_=of[:].rearrange("p (h c) -> p h c", h=4), axis=AX.X, op=ALU.add), reads=[bof], writes=[bst])
                    S.op("dve", lambda e: e.tensor_reduce(out=st[:, 4:8], in_=t5[:].rearrange("p (h c) -> p h c", h=4), axis=AX.X, op=ALU.add), reads=[bt5], writes=[bst])
                    S.op("dve", lambda e: e.tensor_scalar(out=st[:, 8:12], in0=st[:, 0:4], scalar1=1.0 / 128, scalar2=None, op0=ALU.mult), reads=[bst], writes=[bst])
                    S.op("dve", lambda e: e.tensor_tensor(out=st[:, 12:16], in0=st[:, 8:12], in1=st[:, 8:12], op=ALU.mult), reads=[bst], writes=[bst])
                    S.op("dve", lambda e: e.scalar_tensor_tensor(out=st[:, 16:20], in0=st[:, 4:8], scalar=1.0 / 128, in1=st[:, 12:16], op0=ALU.mult, op1=ALU.subtract),
                         reads=[bst], writes=[bst])
                    S.op("act", lambda e: e.activation(out=st[:, 20:24], in_=st[:, 16:20], func=AF.Ln, bias=eps), reads=[bst, self.b_epsb], writes=[bst])
                    S.op("act", lambda e: e.activation(out=st[:, 20:24], in_=st[:, 20:24], func=AF.Exp, scale=-0.5), reads=[bst], writes=[bst])
                    ofv = of[:].rearrange("p (h c) -> p h c", h=4)
                    S.op("dve", lambda e: e.tensor_tensor(out=ofv, in0=ofv, in1=st[:, 8:12].unsqueeze(2).to_broadcast([128, 4, 128]), op=ALU.subtract),
                         reads=[bof, bst], writes=[bof])
                    S.op("dve", lambda e: e.tensor_tensor(out=ofv, in0=ofv, in1=st[:, 20:24].unsqueeze(2).to_broadcast([128, 4, 128]), op=ALU.mult),
                         reads=[bof, bst], writes=[bof])
                    S.op("pool", lambda e: e.tensor_tensor(out=of[:], in0=of[:], in1=cnw[:], op=ALU.mult), reads=[bof, bcnw], writes=[bof])
                    yc, byc = ycr.next()
                    S.op("pool", lambda e: e.tensor_tensor(out=yc[:], in0=of[:], in1=zc[:], op=ALU.mult), reads=[bof, bzc], writes=[byc])
                    self.y_out(2, cc, yc, byc, fmo)
                yield

        def c_gen():
            gens = [state_pass(0), state_pass(1)]
            while gens:
                for g_ in list(gens):
                    try:
                        next(g_)
                        yield
                    except StopIteration:
                        gens.remove(g_)
            yield from out_pass()
        if defer:
            return c_gen()
        for _ in c_gen():
            pass
        S.barrier()
        self.es.close()

    def core_a(self, l):
        S, scr = self.S, self.scr
        last = (l == self.nlayers - 1)
        self.es = ExitStack()
        A = self.asc
        import os
        KPRE = int(os.environ.get("KPRE", "2"))
        r1_next = [0]

        def mkrot(name, k, use_r1=True):
            items = []
            for i in range(k):
                if use_r1 and r1_next[0] < 68:
                    j = r1_next[0]
                    r1_next[0] += 1
                    items.append((self.R1[:, j * 512:(j + 1) * 512], Buf("%s%d" % (name, i))))
                else:
                    t = self._talloc("a_" + name, [128, 512], BF16)
                    items.append((t[:], Buf("%s%d" % (name, i))))
            r = Rot.__new__(Rot)
            r.items = items
            r.i = 0
            return r

        def R(n, dt=BF16, k=2):
            r = self.trot("a_" + n, [128, 512], dt, k)
            r.items = [(t[:], b) for t, b in r.items]
            return r
        ofr, t5r = R("of", F32, 3), R("t5", F32, 2)
        zar, yar, fmo = R("za"), R("ya"), R("fmo")
        sta = self.trot("a_st", [128, 16], F32, 4)
        nmask, bnmask = self.tsb("a_nmask", [128, 14, 128], BF16)
        nmst, bnmst = self.wst.next()
        nmv = nmst[:].rearrange("p k n -> p (k n)")[:, 0:14 * 128].rearrange("p (a n) -> p a n", a=14)
        S.dma("sp", nmv, self.din["k_nm"].rearrange("a p n -> p a n"), writes=[bnmst])
        S.op("pool", lambda e: e.tensor_copy(out=nmask[:], in_=nmv), reads=[bnmst], writes=[bnmask])
        anw, banw = self.par["a_norm_w"]
        eps = self.epsb[:, 0:1]
        H = [slice(h * 128, (h + 1) * 128) for h in range(4)]
        v4 = lambda t: t[:].rearrange("p (h c) -> p h c", h=4)
        orders = [list(range(NT)), [1, 0] + list(range(NT - 1, 1, -1))]
        oa_written = set()
        from collections import deque
        free_banks = deque(range(8))

        def acq():
            while not free_banks:
                yield
            bk = free_banks.popleft()
            ps, bps = S.psum[bk]
            return ps, bps, bk

        def rel(bk):
            free_banks.append(bk)

        def mm4(lhs, blhs, rhs, brhs):
            ps, bps, bk = yield from acq()
            for h in range(4):
                S.op("pe", lambda e: e.matmul(out=ps[:, H[h]], lhsT=lhs[:, H[h]], rhs=rhs[:, H[h]], start=True, stop=True), reads=[blhs, brhs], writes=[bps])
            return ps, bps, bk

        def tr4(src, bsrc):
            ps, bps, bk = yield from acq()
            psb = ps[:].bitcast(BF16)
            for h in range(4):
                S.op("pe", lambda e: e.transpose(out=psb[:, H[h]], in_=src[:, H[h]], identity=self.ident_b[:]), reads=[bsrc, self.b_ident_b], writes=[bps])
            return psb, bps, bk

        class DirBufs:
            pass
        DB = []
        for d in range(2):
            o = DirBufs()
            for n in ("kT", "kTM", "vTM", "qT", "qk", "qd", "kt", "Pf"):
                setattr(o, n, mkrot("%s_%d" % (n, d), KPRE + 1))
            o.tsets = []
            for ts in range(KPRE):
                tsd = {n: mkrot("%s_%d_%d" % (n, d, ts), 1).items[0] for n in ("eg", "Ma", "Ml", "Pa", "Pb", "W1", "X", "MlmA", "MlmB")}
                for n in ("F0", "F1", "F2"):
                    tsd[n] = (self._talloc("a_%s_%d_%d" % (n, d, ts), [128, 512], F32)[:], Buf("%s_%d_%d" % (n, d, ts)))
                o.tsets.append(tsd)
            for n in ("Y", "vn", "sbf"):
                setattr(o, n, R("%s_%d" % (n, d)))
            o.S32 = self.tsb("a_S32_%d" % d, [128, 512])
            DB.append(o)

        def prep(d, cc, out, ts):
            B = DB[d]
            TS = B.tsets[ts]
            need_out = (cc >= 2) or (not last)
            out["need_out"] = need_out
            col = lambda name, h: A[name][0][:, cc, d * 4 + h:d * 4 + h + 1]
            tok = slice(cc * 128, (cc + 1) * 128)
            m_incl = self.amask[:, 2 * d, :]
            m_strict = self.amask[:, 2 * d + 1, :]
            nmb = lambda lev: nmask[:, d * 7 + lev, :].unsqueeze(1).to_broadcast([128, 4, 128])
            kT, bkT = B.kT.next()
            kTM, bkTM = B.kTM.next()
            vTM, bvTM = B.vTM.next()
            S.dma("sp", kT.rearrange("p (h t) -> p h t", h=4), scr["KA_FM"].rearrange("h p t -> p h t")[:, :, tok], reads=[self.db("KA_FM", cc)], writes=[bkT])
            S.dma("sp", kTM.rearrange("p (h c) -> p h c", h=4), scr["KA_TM"].rearrange("h t c -> t h c")[tok, :, :], reads=[self.db("KA_TM", cc)], writes=[bkTM])
            S.dma("sp", vTM.rearrange("p (h c) -> p h c", h=4), scr["VA_TM"].rearrange("h t c -> t h c")[tok, :, :], reads=[self.db("VA_TM", cc)], writes=[bvTM])
            out.update(kT=(kT, bkT), vTM=(vTM, bvTM))
            if need_out:
                qT, bqT = B.qT.next()
                S.dma("sp", qT.rearrange("p (h t) -> p h t", h=4), scr["QA_FM"].rearrange("h p t -> p h t")[:, :, tok], reads=[self.db("QA_FM", cc)], writes=[bqT])
            yield
            dg, bdg = TS["F0"]
            for h in range(4):
                S.op("dve", lambda e: e.tensor_scalar(out=dg[:, H[h]], in0=self.ident_f[:], scalar1=col("GC", h), scalar2=None, op0=ALU.mult),
                     reads=[self.b_ident_f, A["GC"][1]], writes=[bdg.sub(h)])
            p3, bp3, k3 = yield from acq()
            S.op("pe", lambda e: e.matmul(out=p3[:], lhsT=self.ones_f[:], rhs=dg[:], start=True, stop=True), reads=[self.b_ones_f, bdg], writes=[bp3])
            yield
            Dm2, bDm2 = TS["F1"]
            for h in range(4):
                S.op("dve", lambda e: e.scalar_tensor_tensor(out=Dm2[:, H[h]], in0=p3[:, H[h]], scalar=col("LNBMG", h), in1=m_strict, op0=ALU.add, op1=ALU.add),
                     reads=[bp3, A["LNBMG"][1], self.b_amask], writes=[bDm2.sub(h)])
            if need_out:
                Dm, bDm = TS["F2"]
                for h in range(4):
                    S.op("dve", lambda e: e.scalar_tensor_tensor(out=Dm[:, H[h]], in0=p3[:, H[h]], scalar=col("NEGG", h), in1=m_incl, op0=ALU.add, op1=ALU.add),
                         reads=[bp3, A["NEGG"][1], self.b_amask], writes=[bDm.sub(h)])
                eg, beg = TS["eg"]
                S.op("act", lambda e: e.activation(out=eg, in_=p3[:], func=AF.Exp), reads=[bp3, bDm, bDm2], writes=[beg])
            rel(k3)
            yield
            decb, bdecb = TS["F0"]
            S.op("act", lambda e: e.activation(out=decb[:], in_=Dm2[:], func=AF.Exp), reads=[bDm2], writes=[bdecb])
            p1, bp1, k1 = yield from mm4(kT, bkT, kT, bkT)
            yield
            Ma, bMa = TS["Ma"]
            S.op("dve", lambda e: e.tensor_tensor(out=Ma, in0=p1[:], in1=decb[:], op=ALU.mult), reads=[bp1, bdecb], writes=[bMa])
            rel(k1)
            yield
            psb, bps, kb = yield from tr4(Ma, bMa)
            nml = lambda lev: nmask[:, (1 - d) * 7 + lev, :].unsqueeze(1).to_broadcast([128, 4, 128])
            mlm = lambda lev: TS["MlmA" if lev % 2 else "MlmB"]
            S.op("dve", lambda e: e.tensor_tensor(out=v4(mlm(1)[0]), in0=psb[:, 0:512].rearrange("p (h c) -> p h c", h=4), in1=nml(1), op=ALU.mult),
                 reads=[bps, bnmask], writes=[mlm(1)[1]])
            Ml, bMl = TS["Ml"]
            S.op("act", lambda e: e.copy(out=Ml, in_=psb[:, 0:512]), reads=[bps, mlm(1)[1]], writes=[bMl])
            rel(kb)
            P, bP = TS["Pa"]
            S.op("pool", lambda e: e.tensor_tensor(out=v4(P), in0=v4(Ma), in1=nmb(0), op=ALU.mult), reads=[bMa, bnmask], writes=[bP])
            S.op("pool", lambda e: e.tensor_tensor(out=v4(P), in0=v4(P), in1=self.ident_b[:].unsqueeze(1).to_broadcast([128, 4, 128]), op=ALU.add),
                 reads=[bP, self.b_ident_b], writes=[bP])
            yield
            if need_out:
                dec, bdec = TS["F1"]
                S.op("act", lambda e: e.activation(out=dec[:], in_=Dm[:], func=AF.Exp), reads=[bDm], writes=[bdec])
                p2, bp2, k2 = yield from mm4(kT, bkT, qT, bqT)
                yield
                qk, bqk = B.qk.next()
                S.op("dve", lambda e: e.tensor_tensor(out=qk, in0=p2[:], in1=dec[:], op=ALU.mult), reads=[bp2, bdec], writes=[bqk])
                rel(k2)
                qd, bqd = B.qd.next()
                S.op("pool", lambda e: e.tensor_tensor(out=qd, in0=qT, in1=eg, op=ALU.mult), reads=[bqT, beg], writes=[bqd])
                out.update(qk=(qk, bqk), qd=(qd, bqd))
                yield
            kt, bkt = B.kt.next()
            for h in range(4):
                S.op("act", lambda e: e.activation(out=kt[:, H[h]], in_=kTM[:, H[h]], func=AF.Copy, scale=col("ETAIL", h)),
                     reads=[bkTM, A["ETAIL"][1]], writes=[bkt.sub(h)])
            out.update(kt=(kt, bkt))
            yield
            for lev in range(1, 7):
                cur, bcur = mlm(lev)
                psw, bpsw, kw = yield from mm4(cur, bcur, P, bP)
                psb, bps, kb = yield from tr4(P, bP)
                if lev < 6:
                    nxt_, bnxt_ = mlm(lev + 1)
                    S.op("pool", lambda e: e.tensor_tensor(out=v4(nxt_), in0=v4(Ml), in1=nml(lev + 1), op=ALU.mult), reads=[bMl, bnmask], writes=[bnxt_])
                yield
                W1, bW1 = TS["W1"]
                S.op("act", lambda e: e.copy(out=W1, in_=psw[:]), reads=[bpsw], writes=[bW1])
                rel(kw)
                X, bX = TS["X"]
                S.op("dve", lambda e: e.tensor_copy(out=X, in_=psb[:, 0:512]), reads=[bps], writes=[bX])
                rel(kb)
                yield
                ps2, bps2, k2 = yield from mm4(X, bX, W1, bW1)
                yield
                Pn, bPn = (B.Pf.next() if lev == 6 else TS["Pb" if lev % 2 == 1 else "Pa"])
                S.op("dve", lambda e: e.tensor_tensor(out=Pn, in0=ps2[:], in1=P, op=ALU.add), reads=[bps2, bP], writes=[bPn])
                rel(k2)
                P, bP = Pn, bPn
                yield
            out.update(P=(P, bP))

        def scan(d, cc, ops, st):
            B = DB[d]
            need_out = ops["need_out"]
            col = lambda name, h: A[name][0][:, cc, d * 4 + h:d * 4 + h + 1]
            tok = slice(cc * 128, (cc + 1) * 128)
            kT, bkT = ops["kT"]
            vTM, bvTM = ops["vTM"]
            kt, bkt = ops["kt"]
            P, bP = ops["P"]
            sbf, bsbf = st["sbf"]
            s32, bs32 = B.S32
            px, bpx, kx = yield from mm4(kT, bkT, sbf, bsbf)
            yield
            Y, bY = B.Y.next()
            for h in range(4):
                S.op("dve", lambda e: e.scalar_tensor_tensor(out=Y[:, H[h]], in0=px[:, H[h]], scalar=col("NEGEG", h), in1=vTM[:, H[h]], op0=ALU.mult, op1=ALU.add),
                     reads=[bpx, A["NEGEG"][1], bvTM], writes=[bY.sub(h)])
            rel(kx)
            yield
            pz, bpz, kz = yield from mm4(P, bP, Y, bY)
            yield
            vn, bvn = B.vn.next()
            for h in range(4):
                S.op("act", lambda e: e.activation(out=vn[:, H[h]], in_=pz[:, H[h]], func=AF.Copy, scale=col("BETA", h)), reads=[bpz, A["BETA"][1]], writes=[bvn.sub(h)])
            rel(kz)
            yield
            pS, bpS, kS = yield from mm4(kt, bkt, vn, bvn)
            if need_out:
                qk, bqk = ops["qk"]
                qd, bqd = ops["qd"]
                po, bpo, ko = yield from acq()
                for h in range(4):
                    S.op("pe", lambda e: e.matmul(out=po[:, H[h]], lhsT=qd[:, H[h]], rhs=sbf[:, H[h]], start=True, stop=False), reads=[bqd, bsbf], writes=[bpo])
                    S.op("pe", lambda e: e.matmul(out=po[:, H[h]], lhsT=qk[:, H[h]], rhs=vn[:, H[h]], start=False, stop=True), reads=[bqk, bvn], writes=[bpo])
            yield
            for h in range(4):
                S.op("dve", lambda e: e.scalar_tensor_tensor(out=s32[:, H[h]], in0=s32[:, H[h]], scalar=col("EGL", h), in1=pS[:, H[h]], op0=ALU.mult, op1=ALU.add),
                     reads=[bs32.sub(h), A["EGL"][1], bpS], writes=[bs32.sub(h)])
            rel(kS)
            sbf2, bsbf2 = B.sbf.next()
            S.op("act", lambda e: e.copy(out=sbf2, in_=s32[:]), reads=[bs32], writes=[bsbf2])
            st["sbf"] = (sbf2, bsbf2)
            yield
            if need_out:
                first = cc not in oa_written
                oa_written.add(cc)
                of, bof = ofr.next()
                if first:
                    S.op("act", lambda e: e.copy(out=of[:], in_=po[:]), reads=[bpo], writes=[bof])
                    rel(ko)
                    S.dma("act", scr["OA"][tok, :], of[:], reads=[bof], writes=[self.db("OA", cc)])
                    yield
                else:
                    S.dma("sp", of[:], scr["OA"][tok, :], reads=[self.db("OA", cc)], writes=[bof])
                    za, bza = zar.next()
                    S.dma("sp", za, scr["ZA"][tok, :], reads=[self.db("ZA", cc)], writes=[bza])
                    yield
                    S.op("dve", lambda e: e.tensor_tensor(out=of[:], in0=po[:], in1=of[:], op=ALU.add), reads=[bpo, bof], writes=[bof])
                    rel(ko)
                    t5, bt5 = t5r.next()
                    st_, bst = sta.next()
                    S.op("act", lambda e: e.activation(out=t5[:], in_=of[:], func=AF.Square), reads=[bof], writes=[bt5])
                    yield
                    S.op("dve", lambda e: e.tensor_reduce(out=st_[:, 0:4], in_=t5[:].rearrange("p (h c) -> p h c", h=4), axis=AX.X, op=ALU.add), reads=[bt5], writes=[bst])
                    S.op("act", lambda e: e.activation(out=st_[:, 4:8], in_=st_[:, 0:4], func=AF.Ln, scale=1.0 / 128, bias=eps), reads=[bst, self.b_epsb], writes=[bst])
                    S.op("act", lambda e: e.activation(out=st_[:, 8:12], in_=st_[:, 4:8], func=AF.Exp, scale=-0.5), reads=[bst], writes=[bst])
                    yield
                    ofv = of[:].rearrange("p (h c) -> p h c", h=4)
                    S.op("dve", lambda e: e.tensor_tensor(out=ofv, in0=ofv, in1=st_[:, 8:12].unsqueeze(2).to_broadcast([128, 4, 128]), op=ALU.mult), reads=[bof, bst], writes=[bof])
                    S.op("pool", lambda e: e.tensor_tensor(out=ofv, in0=ofv, in1=anw[:].unsqueeze(1).to_broadcast([128, 4, 128]), op=ALU.mult), reads=[bof, banw], writes=[bof])
                    yield
                    ya, bya = yar.next()
                    S.op("pool", lambda e: e.tensor_tensor(out=ya, in0=of[:], in1=za, op=ALU.mult), reads=[bof, bza], writes=[bya])
                    f, bf = fmo.next()
                    psb, bps, kb = yield from tr4(ya, bya)
                    S.op("act", lambda e: e.copy(out=f, in_=psb[:, 0:512]), reads=[bps], writes=[bf])
                    rel(kb)
                    S.dma("act", scr["Y_FM"][0].rearrange("(k p) t -> p k t", p=128)[:, :, tok], f.rearrange("p (k t) -> p k t", k=4),
                          reads=[bf], writes=[self.db("Y_FM0", cc)])
                    yield

        def chain(d):
            B = DB[d]
            s32, bs32 = B.S32
            S.op("pool", lambda e: e.memset(s32[:], 0.0), writes=[bs32])
            sbf, bsbf = B.sbf.next()
            S.op("pool", lambda e: e.memset(sbf, 0.0), writes=[bsbf])
            st = {"sbf": (sbf, bsbf)}
            order = orders[d]
            n = len(order)
            outs = [dict() for _ in range(n)]
            started = 0
            active = []
            done = set()

            def start_upto(j):
                nonlocal started
                while started <= min(j, n - 1):
                    active.append((started, prep(d, order[started], outs[started], started % KPRE)))
                    started += 1

            def step_preps():
                for item in list(active):
                    try:
                        next(item[1])
                    except StopIteration:
                        active.remove(item)
                        done.add(item[0])
            start_upto(0)
            while 0 not in done:
                step_preps()
                yield
            for i, cc in enumerate(order):
                start_upto(i + KPRE)
                sc = scan(d, cc, outs[i], st)
                sc_done = False
                while not sc_done or (i + 1 < n and (i + 1) not in done):
                    if not sc_done:
                        try:
                            next(sc)
                        except StopIteration:
                            sc_done = True
                    step_preps()
                    yield

        gens = [chain(0), chain(1)]
        while gens:
            for g_ in list(gens):
                try:
                    next(g_)
                except StopIteration:
                    gens.remove(g_)
        S.barrier()
        self.es.close()

    def phase5(self, l):
        S, scr, din = self.S, self.scr, self.din
        last = (l == self.nlayers - 1)
        self.es = ExitStack()
        wbr = self.R1[:, 14336:26624].rearrange("p (r n) -> p r n", n=1024)
        wo = self.R1[:, 26624:34816].rearrange("p (r n) -> p r n", n=1024)
        bwbr, bwo = Buf("wbr"), Buf("wo")

        def load_into(src_view, dst, bdst, nk):
            st, bst = self.wst.next()
            S.dma("sp", st[:, 0:nk, :], src_view, writes=[bst])
            S.op("pool", lambda e: e.tensor_copy(out=dst, in_=st[:, 0:nk, :]), reads=[bst], writes=[bdst])
        wbsrc = din["w_branch"][l].rearrange("b (k p) n -> p (b k) n", p=128)
        for half in range(2):
            for r0, nk in ((0, 8), (8, 4)):
                load_into(wbsrc[:, r0:r0 + nk, half * 512:(half + 1) * 512], wbr[:, r0:r0 + nk, half * 512:(half + 1) * 512], bwbr, nk)
        wosrc = din["w_out"][l].rearrange("(k p) n -> p k n", p=128)
        for half in range(2):
            load_into(wosrc[:, :, half * 512:(half + 1) * 512], wo[:, :, half * 512:(half + 1) * 512], bwo, 8)
        gate_bc, bgate = self.tsb("gate_bc", [128, 2, 1024])
        for s_ in range(2):
            if last and s_ == 1:
                continue
            self.bc_rows(lambda half: gate_bc[:, s_, half * 512:(half + 1) * 512], bgate, lambda kc: self.mod[:, 16 + kc, s_:s_ + 1], self.b_mod, 8)
        if last:
            fnw, bfnw = self.tsb("fnw_bc", [128, 1024])
            S.dma("sp", fnw[:], din["final_norm_w"].partition_broadcast(128), writes=[bfnw])
        yTr = self.trot("p5_yT", [128, 12, 512], BF16, 1)
        gmr = self.trot("p5_gm", [128, 512], BF16, 3)
        accr = self.trot("p5_acc", [128, 512], F32, 2)
        tmr = self.trot("p5_tm", [128, 512], F32, 2)
        mTr = self.trot("p5_mT", [128, 8, 512], BF16, 2)
        xtr = self.trot("p5_xt", [128, 1024], F32, 2)
        t1r = self.trot("p5_t1", [128, 1024], F32, 2)
        sqr, bsqr = self.tsb("p5_sq", [128, 1024])
        st5 = self.trot("p5_st", [128, 4], F32, 3)
        for (t0, n, tile0, ntile) in self.tok_groups():
            if last and t0 == 0:
                continue
            s_ = 1 if t0 == 0 else 0
            yT, byT = yTr.next()
            for br in range(3):
                S.dma("sp", yT[:, br * 4:(br + 1) * 4, 0:n], scr["Y_FM"][br].rearrange("(k p) t -> p k t", p=128)[:, :, t0:t0 + n],
                      reads=[self.db("Y_FM%d" % br, tile0 + i) for i in range(ntile)], writes=[byT.sub(br)])
            mT, bmT = mTr.next()
            for dt in range(8):
                acc, bacc = accr.next()
                for br in range(3):
                    ct = br * 8 + dt
                    gm, bgm = gmr.next()
                    S.dma("sp", gm[:, 0:n], scr["GM_FM"][ct * 128:(ct + 1) * 128, t0:t0 + n],
                          reads=[self.db("GM_FM%d" % ct, tile0 + i) for i in range(ntile)], writes=[bgm])
                    ps, bps = S.ps()
                    for kc in range(4):
                        S.op("pe", lambda e: e.matmul(out=ps[:, 0:n], lhsT=wbr[:, br * 4 + kc, dt * 128:(dt + 1) * 128], rhs=yT[:, br * 4 + kc, 0:n],
                                                      start=(kc == 0), stop=(kc == 3)), reads=[bwbr, byT.sub(br)], writes=[bps])
                    if br == 0:
                        S.op("dve", lambda e: e.tensor_tensor(out=acc[:, 0:n], in0=ps[:, 0:n], in1=gm[:, 0:n], op=ALU.mult), reads=[bps, bgm], writes=[bacc])
                    else:
                        tm, btm = tmr.next()
                        S.op("dve", lambda e: e.tensor_tensor(out=tm[:, 0:n], in0=ps[:, 0:n], in1=gm[:, 0:n], op=ALU.mult), reads=[bps, bgm], writes=[btm])
                        if br == 1:
                            S.op("pool", lambda e: e.tensor_tensor(out=acc[:, 0:n], in0=acc[:, 0:n], in1=tm[:, 0:n], op=ALU.add), reads=[bacc, btm], writes=[bacc])
                        else:
                            S.op("pool", lambda e: e.tensor_tensor(out=mT[:, dt, 0:n], in0=acc[:, 0:n], in1=tm[:, 0:n], op=ALU.add), reads=[bacc, btm], writes=[bmT.sub(dt)])
            for ti in range(ntile):
                tt = tile0 + ti
                xt, bxt = xtr.next()
                if tt < 2:
                    src = (din["ctx"] if l == 0 else scr["CTXS"])[tt * 128:(tt + 1) * 128, :]
                    rd = [] if l == 0 else [self.db("CTXS", tt)]
                else:
                    src = (din["x"] if l == 0 else scr["XS"])[(tt - 2) * 128:(tt - 1) * 128, :]
                    rd = [] if l == 0 else [self.db("XS", tt)]
                S.dma("sp", xt[:], src, reads=rd, writes=[bxt])
                t1, bt1 = t1r.next()
                for cg in range(2):
                    ps, bps = S.ps()
                    for kc in range(8):
                        S.op("pe", lambda e: e.matmul(out=ps[:], lhsT=mT[:, kc, ti * 128:(ti + 1) * 128], rhs=wo[:, kc, cg * 512:(cg + 1) * 512],
                                                      start=(kc == 0), stop=(kc == 7)), reads=[bmT, bwo], writes=[bps])
                    S.op("dve", lambda e: e.tensor_tensor(out=t1[:, cg * 512:(cg + 1) * 512], in0=ps[:], in1=gate_bc[:, s_, cg * 512:(cg + 1) * 512], op=ALU.mult),
                         reads=[bps, bgate], writes=[bt1.sub(cg)])
                S.op("pool", lambda e: e.tensor_tensor(out=t1[:], in0=t1[:], in1=xt[:], op=ALU.add), reads=[bt1, bxt], writes=[bt1])
                if not last:
                    if tt < 2:
                        S.dma("act", scr["CTXS"][tt * 128:(tt + 1) * 128, :], t1[:], reads=[bt1], writes=[self.db("CTXS", tt)])
                    else:
                        S.dma("act", scr["XS"][(tt - 2) * 128:(tt - 1) * 128, :], t1[:], reads=[bt1], writes=[self.db("XS", tt)])
                else:
                    st, bst = st5.next()
                    S.op("act", lambda e: e.activation(out=sqr[:], in_=t1[:], func=AF.Square, accum_out=st[:, 0:1]), reads=[bt1], writes=[bsqr, bst])
                    S.op("dve", lambda e: e.tensor_scalar(out=st[:, 1:2], in0=st[:, 0:1], scalar1=1.0 / D, scalar2=EPS, op0=ALU.mult, op1=ALU.add), reads=[bst], writes=[bst])
                    S.op("act", lambda e: e.activation(out=st[:, 2:3], in_=st[:, 1:2], func=AF.Ln), reads=[bst], writes=[bst])
                    S.op("act", lambda e: e.activation(out=st[:, 3:4], in_=st[:, 2:3], func=AF.Exp, scale=-0.5), reads=[bst], writes=[bst])
                    S.op("dve", lambda e: e.scalar_tensor_tensor(out=xt[:], in0=t1[:], scalar=st[:, 3:4], in1=fnw[:], op0=ALU.mult, op1=ALU.mult),
                         reads=[bt1, bst, bfnw, bxt], writes=[bxt])
                    S.dma("act", self.out[(tt - 2) * 128:(tt - 1) * 128, :], xt[:], reads=[bxt], writes=[self.db("OUT", tt)])
        S.barrier()
        self.es.close()

    def dump(self, name, ap, reads, shape, dtype=F32):
        o = self.nc.dram_tensor("dbg_" + name, shape, dtype, kind="ExternalOutput").ap()
        b = Buf("dbg_" + name)
        self.S.dma("sp", o, ap, reads=reads, writes=[b])
        self._dbgbufs.append(b)

    def dump_p2(self):
        self.dump("mod", self.mod[:], [self.b_mod], [128, 24, 2])
        self.dump("SCR", self.SCR[:], [self.b_SCR], [128, NT, 16])
        for n in self.asc:
            self.dump(n, self.asc[n][0][:], [self.asc[n][1]], [128, NT, 8])
        self.dump("SH2", self.SH2[:], [self.b_SH2], [128, NT, 8])
        self.dump("kmx", self.kmx[:], [self.b_kmx], [128, 4])
        self.dump("hT", self.hT, self.b_hT, [128, 8, NTOK], BF16)

    def program(self):
        S = self.S
        self._dbgbufs = []
        self.marks = []
        mark = lambda n: self.marks.append((n, {k: v.count for k, v in S.engs.items()}))
        for l in range(self.nlayers):
            mark("L%d start" % l)
            self.phase0(l)
            mark("L%d p0 done" % l)
            if self.stop == "p0":
                self.dump("mod", self.mod[:], [self.b_mod], [128, 24, 2])
                self.dump("Afm", self.Afm[:], [self.b_Afm], [128, 8, 2])
                self.dump("convw", self.convw[:], [self.b_convw], [128, 12, 5])
                self.dump("scol", self.scol[:], [self.b_scol], [128, 8, 2])
                break
            self.phase1(l)
            mark("L%d p1 done" % l)
            if self.stop == "p1":
                self.dump("hT", self.hT, self.b_hT, [128, 8, NTOK], BF16)
                break
            self.phase2(l)
            if self.stop is not None and self.stop.startswith("p2"):
                break
            mark("L%d p2 done" % l)
            if self.stop == "b":
                self.core_b(l)
                break
            self.core_bc(l)
            mark("L%d B done" % l)
            mark("L%d C done" % l)
            if self.stop == "c":
                break
            self.core_a(l)
            mark("L%d A done" % l)
            if self.stop == "a":
                break
            self.phase5(l)
            mark("L%d p5 done" % l)
            if self.stop == "p5":
                break
        S.barrier()
        return self.nc


def shard_inputs(inputs, b):
    m = {}
    for n in IN_SHAPES:
        a = np.asarray(inputs[n], dtype=np.float32)
        if n in ("x", "c", "ctx"):
            a = a[b]
        m[n] = np.ascontiguousarray(a)
    return m


_CACHE = {}


def kernel(**inputs):
    if "nc" not in _CACHE:
        _CACHE["nc"] = MK().program()
        _CACHE["consts"] = host_consts()
    nc = _CACHE["nc"]
    in_maps = []
    for b in range(8):
        m = shard_inputs(inputs, b)
        m.update(_CACHE["consts"])
        in_maps.append(m)
    res = run_bass_kernel_spmd(nc, in_maps, core_ids=list(range(8)))
    return np.stack([np.asarray(r["out"], dtype=np.float32) for r in res.results], axis=0)
```

```python
from contextlib import ExitStack
import numpy as np
import concourse.bass as bass
import concourse.mybir as mybir
from concourse.bass_utils import run_bass_kernel_spmd

F32 = mybir.dt.float32
BF16 = mybir.dt.bfloat16
AF = mybir.ActivationFunctionType
ALU = mybir.AluOpType
AX = mybir.AxisListType

T = 4096
LC = 256
D = 1024
NT = 34
NTOK = 4352
INW = 8464
CH = 128
EPS = 1e-6
NEG = -1.0e5
O_AQ, O_AK, O_AV, O_AZ, O_AB, O_BQ, O_BKV, O_BZ, O_CQ, O_CK, O_CV, O_CZ, O_MG = (
    0, 512, 1024, 1536, 2048, 2064, 2576, 2832, 3344, 3856, 4368, 4880, 5392)


class Buf:
    __slots__ = ("name", "w", "r", "parts")

    def __init__(self, name):
        self.name = name
        self.w = None
        self.r = {}
        self.parts = {}

    def sub(self, p):
        return Sub(self, p)


class Sub:
    __slots__ = ("parent", "p", "name")

    def __init__(self, parent, p):
        self.parent = parent
        self.p = p
        self.name = "%s[%s]" % (parent.name, p)

    def _slot(self):
        return self.parent.parts.setdefault(self.p, [None, {}])


class Eng:
    def __init__(self, key, e, sem):
        self.key = key
        self.e = e
        self.sem = sem
        self.count = 0
        self.waited = {}


class Sched:
    def __init__(self, nc, n_dma_sems=40):
        self.nc = nc
        self.sems = {}
        self.engs = {}
        for key, e in (("pe", nc.tensor), ("act", nc.scalar), ("dve", nc.vector), ("pool", nc.gpsimd), ("sp", nc.sync)):
            s = nc.alloc_semaphore("sem_" + key)
            self.sems[key] = s
            self.engs[key] = Eng(key, e, s)
        self.dma_sems = []
        for i in range(n_dma_sems):
            k = "dma%d" % i
            self.sems[k] = nc.alloc_semaphore("sem_" + k)
            self.dma_sems.append([k, 0])
        self.dma_rr = 0
        self.nops = 0
        self.clocks = {}
        self.psum = []
        self.ps_rr = 0
        for i in range(8):
            self.psum.append((nc.alloc_psum_tensor("psb%d" % i, [128, 512], F32), Buf("psb%d" % i)))

    def ps(self, pool=None):
        if pool is not None:
            banks, st = pool
            r = self.psum[banks[st[0] % len(banks)]]
            st[0] += 1
            return r
        r = self.psum[self.ps_rr]
        self.ps_rr = (self.ps_rr + 1) % 8
        return r

    def _deps(self, eng, reads, writes, is_dma):
        deps = {}

        def add(tok, same_ok):
            if tok is None:
                return
            k, v = tok
            if k == eng.key and not same_ok:
                return
            if deps.get(k, 0) < v:
                deps[k] = v

        same = is_dma or eng.key != "pe"
        for b in reads:
            if isinstance(b, Sub):
                add(b.parent.w, True)
                add(b._slot()[0], True)
            else:
                add(b.w, True)
                for pw, pr in b.parts.values():
                    add(pw, True)
        for b in writes:
            if isinstance(b, Sub):
                add(b.parent.w, same)
                for k, v in b.parent.r.items():
                    add((k, v), same)
                pw, pr = b._slot()
                add(pw, same)
                for k, v in pr.items():
                    add((k, v), same)
            else:
                add(b.w, same)
                for k, v in b.r.items():
                    add((k, v), same)
                for pw, pr in b.parts.values():
                    add(pw, same)
                    for k, v in pr.items():
                        add((k, v), same)
        for k, v in sorted(deps.items(), key=lambda kv: -kv[1]):
            self._need(eng, k, v)

    def _record(self, key, val, reads, writes):
        for b in reads:
            r = b._slot()[1] if isinstance(b, Sub) else b.r
            if r.get(key, 0) < val:
                r[key] = val
        for b in writes:
            if isinstance(b, Sub):
                sl = b._slot()
                sl[0] = (key, val)
                sl[1] = {}
            else:
                b.w = (key, val)
                b.r = {}
                b.parts = {}

    def _need(self, eng, k, v):
        if eng.waited.get(k, 0) >= v:
            return
        eng.e.wait_ge(self.sems[k], v)
        eng.waited[k] = v
        clk = self.clocks.get((k, v))
        if clk:
            w = eng.waited
            for k2, v2 in clk.items():
                if w.get(k2, 0) < v2:
                    w[k2] = v2

    def op(self, ek, fn, reads=(), writes=()):
        eng = self.engs[ek]
        self._deps(eng, reads, writes, False)
        ins = fn(eng.e)
        self.nops += 1
        eng.count += 1
        ins.then_inc(eng.sem, 1)
        clk = dict(eng.waited)
        clk.pop(eng.key, None)
        self.clocks[(eng.key, eng.count)] = clk
        self._record(eng.key, eng.count, reads, writes)
        return ins

    def dma(self, ek, out, in_, reads=(), writes=(), **kw):
        eng = self.engs[ek]
        self._deps(eng, reads, writes, True)
        slot = self.dma_sems[self.dma_rr]
        self.dma_rr = (self.dma_rr + 1) % len(self.dma_sems)
        k, uses = slot
        if uses > 0:
            self._need(eng, k, 16 * uses)
        ins = eng.e.dma_start(out=out, in_=in_, **kw)
        self.nops += 1
        slot[1] = uses + 1
        val = 16 * (uses + 1)
        ins.then_inc(self.sems[k], 16)
        clk = dict(eng.waited)
        clk.pop(eng.key, None)
        self.clocks[(k, val)] = clk
        self._record(k, val, reads, writes)
        return ins

    def wait_all(self, ek, bufs):
        eng = self.engs[ek]
        for b in bufs:
            toks = []
            if b.w is not None:
                toks.append(b.w)
            toks.extend(b.r.items())
            for pw, pr in b.parts.values():
                if pw is not None:
                    toks.append(pw)
                toks.extend(pr.items())
            for k, v in toks:
                if eng.waited.get(k, 0) < v:
                    eng.e.wait_ge(self.sems[k], v)
                    eng.waited[k] = v

    def barrier(self):
        for eng in self.engs.values():
            for o in self.engs.values():
                if o.key != eng.key and o.count > 0 and eng.waited.get(o.key, 0) < o.count:
                    eng.e.wait_ge(self.sems[o.key], o.count)
                    eng.waited[o.key] = o.count
            for k, uses in self.dma_sems:
                if uses > 0 and eng.waited.get(k, 0) < 16 * uses:
                    eng.e.wait_ge(self.sems[k], 16 * uses)
                    eng.waited[k] = 16 * uses


class Rot:
    def __init__(self, alloc, name, shape, dtype, n=2):
        self.items = [(alloc("%s%d" % (name, i), shape, dtype), Buf("%s%d" % (name, i))) for i in range(n)]
        self.i = 0

    def next(self):
        r = self.items[self.i]
        self.i = (self.i + 1) % len(self.items)
        return r


def host_consts():
    f = np.float32
    j = np.arange(128)[:, None]
    i = np.arange(128)[None, :]
    c = {}
    c["k_ident"] = np.eye(128, dtype=f)
    c["k_ones"] = np.ones((128, 128), f)
    am = np.zeros((4, 128, 128), f)
    am[0] = np.where(i >= j, 0.0, NEG)
    am[1] = np.where(i > j, 0.0, NEG)
    am[2] = np.where(i <= j, 0.0, NEG)
    am[3] = np.where(i < j, 0.0, NEG)
    c["k_amask"] = am
    tri = np.zeros((2, 128, 128), f)
    tri[0] = (j <= i)
    tri[1] = (j >= i)
    c["k_tri"] = tri
    bm = np.zeros((2, 128, 512), f)
    bm[0] = np.tile((j >= i).astype(f), (1, 4))
    bm[1] = np.tile((j <= i).astype(f), (1, 4))
    c["k_bmask"] = bm
    cm = np.zeros((6, 128, 128), f)
    cm[0] = np.maximum(i - j, 0)
    cm[1] = np.maximum(j - i, 0)
    cm[2] = (i > j)
    cm[3] = (j > i)
    cm[4] = np.broadcast_to(i + 1, (128, 128))
    cm[5] = np.broadcast_to(CH - i, (128, 128))
    c["k_cm"] = cm
    nm = np.zeros((14, 128, 128), f)
    for d in range(2):
        for lev in range(7):
            b = 1 << lev
            same = (j // (2 * b)) == (i // (2 * b))
            if d == 0:
                m = same & ((j % (2 * b)) < b) & ((i % (2 * b)) >= b)
            else:
                m = same & ((i % (2 * b)) < b) & ((j % (2 * b)) >= b)
            nm[d * 7 + lev] = -m.astype(f)
    c["k_nm"] = nm
    cj = np.zeros((128, 8), f)
    cj[:, 0:4] = (CH - 1 - np.arange(128))[:, None]
    cj[:, 4:8] = np.arange(128)[:, None]
    c["k_cj"] = cj
    t = np.arange(T)
    inv16 = (f(10000.0) ** (-np.arange(16, dtype=f) / f(16))).astype(f)
    ar = (t // 64).astype(f)[:, None] * inv16[None, :]
    ac = (t % 64).astype(f)[:, None] * inv16[None, :]
    ab = np.concatenate([ar, ac], axis=1).astype(f)
    c["k_cosb"] = np.tile(np.cos(ab).astype(f), (1, 8))
    c["k_sinb"] = np.tile(np.sin(ab).astype(f), (1, 8))
    inv64 = (f(10000.0) ** (-np.arange(64, dtype=f) / f(64))).astype(f)
    ang = t.astype(f)[:, None] * inv64[None, :]
    c["k_cosc"] = np.cos(ang).astype(f)
    c["k_sinc"] = np.sin(ang).astype(f)
    return c


CONST_SHAPES = {"k_ident": [128, 128], "k_ones": [128, 128], "k_amask": [4, 128, 128], "k_tri": [2, 128, 128],
                "k_bmask": [2, 128, 512], "k_cm": [6, 128, 128], "k_cj": [128, 8], "k_nm": [14, 128, 128],
                "k_cosb": [T, 256], "k_sinb": [T, 256], "k_cosc": [T, 64], "k_sinc": [T, 64]}

IN_SHAPES = {"x": [T, D], "c": [D], "ctx": [LC, D], "c_ctx": [D], "w_ada": [2, D, 3 * D], "b_ada": [2, 3 * D],
             "norm_w": [2, D], "w_in": [2, D, INW], "a_conv_w": [2, 5, 1536], "a_log": [2, 8], "a_dt_bias": [2, 8],
             "a_norm_w": [2, 128], "b_sink": [2, 8], "c_decay": [2, 8], "c_norm_w": [2, 512],
             "w_branch": [2, 3, 512, D], "w_out": [2, D, D], "final_norm_w": [D]}

SCRATCH = {"XS": ([T, D], F32), "CTXS": ([LC, D], F32),
           "QA_FM": ([4, 128, NTOK], BF16), "KA_FM": ([4, 128, NTOK], BF16),
           "KA_TM": ([4, NTOK, 128], BF16), "VA_TM": ([4, NTOK, 128], BF16),
           "ZA": ([NTOK, 512], BF16), "ZB": ([NTOK, 512], BF16), "ZC": ([NTOK, 512], BF16),
           "OA": ([NTOK, 512], F32),
           "QB_FM": ([8, 128, NTOK], BF16),
           "QC_FM": ([4, 128, NTOK], BF16), "KC_FM": ([4, 128, NTOK], BF16),
           "KC_TM": ([NTOK, 512], BF16), "VC_TM": ([NTOK, 512], BF16),
           "SCB": ([NT, 128, 512], BF16), "SCF": ([NT, 128, 512], BF16),
           "KB_FM": ([2, 128, NTOK], BF16), "VB_TM": ([NTOK, 2, 65], BF16),
           "GM_FM": ([3 * D, NTOK], BF16), "Y_FM": ([3, 512, NTOK], BF16)}


class MK:
    def __init__(self, nlayers=2, dbg=(), stop=None):
        nc = bass.Bass("TRN2", target_bir_lowering=False, dynamic_dma_scratch_size=1024)
        self.nc = nc
        self.S = Sched(nc)
        self.nlayers = nlayers
        self.stop = stop
        self.din = {}
        for n, shp in list(IN_SHAPES.items()) + list(CONST_SHAPES.items()):
            self.din[n] = nc.dram_tensor(n, shp, F32, kind="ExternalInput").ap()
        self.out = nc.dram_tensor("out", [T, D], F32, kind="ExternalOutput").ap()
        self.scr = {}
        self._db = {}
        for n, (shp, dt) in SCRATCH.items():
            kind = "ExternalOutput" if n in dbg else "Internal"
            self.scr[n] = nc.dram_tensor(n, shp, dt, kind=kind).ap()
        self.dbg = dbg
        self._tcount = 0
        self.alloc()

    def db(self, name, tt):
        k = (name, tt)
        if k not in self._db:
            self._db[k] = Buf("%s_%d" % k)
        return self._db[k]

    def dball(self, name):
        return [self.db(name, tt) for tt in range(NT)]

    def sb(self, name, shape, dtype=F32):
        return self.nc.alloc_sbuf_tensor(name, shape, dtype), Buf(name)

    def rot(self, name, shape, dtype=F32, n=2):
        return Rot(lambda nm, sh, dt: self.nc.alloc_sbuf_tensor(nm, sh, dt), name, shape, dtype, n)

    def _talloc(self, name, shape, dtype):
        self._tcount += 1
        return self.es.enter_context(self.nc.sbuf_tensor("%s_t%d" % (name, self._tcount), shape, dtype))

    def tsb(self, name, shape, dtype=F32):
        return self._talloc(name, shape, dtype), Buf(name)

    def trot(self, name, shape, dtype=F32, n=2):
        return Rot(self._talloc, name, shape, dtype, n)

    def alloc(self):
        nc, S, din = self.nc, self.S, self.din
        self.ident_f, self.b_ident_f = self.sb("ident_f", [128, 128])
        self.ones_f, self.b_ones_f = self.sb("ones_f", [128, 128])
        self.ident_b, self.b_ident_b = self.sb("ident_b", [128, 128], BF16)
        self.ones_b, self.b_ones_b = self.sb("ones_b", [128, 128], BF16)
        self.amask, self.b_amask = self.sb("amask", [128, 4, 128])
        self.tri, self.b_tri = self.sb("tri", [128, 2, 128])
        self.bmask, self.b_bmask = self.sb("bmask", [128, 2, 512], BF16)
        self.cm, self.b_cm = self.sb("cm", [128, 6, 128])
        self.cj, self.b_cj = self.sb("cj", [128, 8])
        S.dma("sp", self.ident_f[:], din["k_ident"], writes=[self.b_ident_f])
        S.dma("sp", self.ones_f[:], din["k_ones"], writes=[self.b_ones_f])
        S.dma("sp", self.amask[:], din["k_amask"].rearrange("a p n -> p a n"), writes=[self.b_amask])
        S.dma("sp", self.tri[:], din["k_tri"].rearrange("a p n -> p a n"), writes=[self.b_tri])
        S.dma("sp", self.cm[:], din["k_cm"].rearrange("a p n -> p a n"), writes=[self.b_cm])
        S.dma("sp", self.cj[:], din["k_cj"], writes=[self.b_cj])
        S.op("dve", lambda e: e.tensor_copy(out=self.ident_b[:], in_=self.ident_f[:]), reads=[self.b_ident_f], writes=[self.b_ident_b])
        S.op("dve", lambda e: e.tensor_copy(out=self.ones_b[:], in_=self.ones_f[:]), reads=[self.b_ones_f], writes=[self.b_ones_b])
        self.R1, self.b_R1 = self.sb("R1", [128, 8 * NTOK], BF16)
        self.hT = self.R1[:].rearrange("p (k t) -> p k t", k=8)
        self.b_hT = [Buf("hT%d" % i) for i in range(NT)]
        self.wst = self.rot("wst", [128, 8, 512], F32, 1)
        self.wb = self.rot("wb", [128, 8, 512], BF16, 4)
        self.scol, self.b_scol = self.sb("scol", [128, 8, 2])
        self.mod, self.b_mod = self.sb("mod", [128, 24, 2])
        self.nwcol, self.b_nwcol = self.sb("nwcol", [128, 8])
        self.badacol, self.b_badacol = self.sb("badacol", [128, 24])
        self.Afm, self.b_Afm = self.sb("Afm", [128, 8, 2])
        self.convw, self.b_convw = self.sb("convw", [128, 12, 5])
        self.rowtmp = self.rot("rowtmp", [128, 128], F32, 2)
        for it in self.rowtmp.items:
            S.op("pool", lambda e: e.memset(it[0][:], 0.0), writes=[it[1]])
        self.gdiag = self.rot("gdiag", [128, 512], F32, 2)
        self.SCR, self.b_SCR = self.sb("SCR", [128, NT, 16])
        self.asc = {}
        for n in ("BETA", "GC", "NEGG", "LNBMG", "NEGEG", "ETAIL", "EGL"):
            self.asc[n] = self.sb("asc_" + n, [128, NT, 8])
        self.SH2, self.b_SH2 = self.sb("SH2", [128, NT, 8])
        self.kmx, self.b_kmx = self.sb("kmx", [128, 4])
        self.par = {}
        for n, w in (("a_log", 8), ("a_dt_bias", 8), ("b_sink", 8), ("c_decay", 8), ("a_norm_w", 128), ("c_norm_w", 512)):
            self.par[n] = self.sb("par_" + n, [128, w])
        self.epsb, self.b_epsb = self.sb("epsb", [128, 4])
        for col, val in ((0, EPS), (1, -0.5 * float(np.log(128.0))), (2, 0.0), (3, 1.0)):
            S.op("pool", lambda e: e.memset(self.epsb[:, col:col + 1], val), writes=[self.b_epsb])

    def load_cols(self, src_rows, n, dst, bdst, func=None):
        S = self.S
        rt, brt = self.rowtmp.next()
        S.dma("sp", rt[0:n, :], src_rows, writes=[brt])
        if func is not None:
            S.op("act", lambda e: e.activation(out=rt[0:n, :], in_=rt[0:n, :], func=func), reads=[brt], writes=[brt])
        ps, bps = S.ps()
        S.op("pe", lambda e: e.transpose(out=ps[:, 0:128], in_=rt[:, :], identity=self.ident_f[:]),
             reads=[brt, self.b_ident_f], writes=[bps])
        S.op("dve", lambda e: e.tensor_copy(out=dst, in_=ps[:, 0:n]), reads=[bps], writes=[bdst])

    def w_plan(self, src2d, groups):
        self._wsrc = src2d
        self._wgroups = list(groups)
        self._wi = 0
        self._wq = []
        self._w_issue()

    def _w_issue(self):
        if self._wi < len(self._wgroups):
            c0, ncols = self._wgroups[self._wi]
            self._wi += 1
            self._wq.append(((c0, ncols), self._load_w_raw(self._wsrc, c0, ncols)))

    def load_w(self, src2d, c0, ncols, nk=8):
        if getattr(self, "_wq", None):
            key, val = self._wq.pop(0)
            assert key == (c0, ncols), (key, c0, ncols)
            self._w_issue()
            return val
        return self._load_w_raw(src2d, c0, ncols, nk)

    def _load_w_raw(self, src2d, c0, ncols, nk=8):
        S = self.S
        st, bst = self.wst.next()
        wb, bwb = self.wb.next()
        S.dma("sp", st[:, 0:nk, 0:ncols], src2d.rearrange("(k p) n -> p k n", p=128)[:, :, c0:c0 + ncols], writes=[bst])
        S.op("pool", lambda e: e.tensor_copy(out=wb[:, 0:nk, 0:ncols], in_=st[:, 0:nk, 0:ncols]), reads=[bst], writes=[bwb])
        return wb, bwb

    def phase0(self, l):
        S, din = self.S, self.din
        self.es = ExitStack()
        for n in self.par:
            t, b = self.par[n]
            S.dma("sp", t[:], din[n][l].partition_broadcast(128), writes=[b])
        self.load_cols(din["c"].rearrange("(k p) -> k p", p=128), 8, self.scol[:, :, 0], self.b_scol, AF.Silu)
        self.load_cols(din["c_ctx"].rearrange("(k p) -> k p", p=128), 8, self.scol[:, :, 1], self.b_scol, AF.Silu)
        self.load_cols(din["norm_w"][l].rearrange("(k p) -> k p", p=128), 8, self.nwcol[:], self.b_nwcol)
        self.load_cols(din["b_ada"][l].rearrange("(k p) -> k p", p=128), 24, self.badacol[:], self.b_badacol)
        cw, bcw = self.tsb("cwrows", [128, 1536])
        S.op("pool", lambda e: e.memset(cw[:], 0.0), writes=[bcw])
        S.dma("sp", cw[0:5, :], din["a_conv_w"][l], writes=[bcw])
        for ct in range(12):
            ps, bps = S.ps()
            S.op("pe", lambda e: e.transpose(out=ps[:, 0:128], in_=cw[:, ct * 128:(ct + 1) * 128], identity=self.ident_f[:]),
                 reads=[bcw, self.b_ident_f], writes=[bps])
            S.op("dve", lambda e: e.tensor_copy(out=self.convw[:, ct, :], in_=ps[:, 0:5]), reads=[bps], writes=[self.b_convw])
        psm, bpsm = S.ps()
        for g in range(6):
            st, bst = self.wst.next()
            S.dma("sp", st[:], din["w_ada"][l].rearrange("(k p) n -> p k n", p=128)[:, :, g * 512:(g + 1) * 512], writes=[bst])
            for jl in range(4):
                j = g * 4 + jl
                for kc in range(8):
                    S.op("pe", lambda e: e.matmul(out=psm[:, 2 * j:2 * j + 2], lhsT=st[:, kc, jl * 128:(jl + 1) * 128], rhs=self.scol[:, kc, :],
                                                  start=(kc == 0), stop=(kc == 7)),
                         reads=[bst, self.b_scol], writes=[bpsm])
        S.op("dve", lambda e: e.tensor_tensor(out=self.mod[:], in0=psm[:, 0:48].rearrange("p (j s) -> p j s", s=2),
                                              in1=self.badacol[:].unsqueeze(2).to_broadcast([128, 24, 2]), op=ALU.add),
             reads=[bpsm, self.b_badacol], writes=[self.b_mod])
        S.op("dve", lambda e: e.scalar_tensor_tensor(out=self.Afm[:], in0=self.mod[:, 8:16, :], scalar=1.0,
                                                     in1=self.nwcol[:].unsqueeze(2).to_broadcast([128, 8, 2]), op0=ALU.add, op1=ALU.mult),
             reads=[self.b_mod, self.b_nwcol], writes=[self.b_Afm])
        S.barrier()
        self.es.close()

    def bc_rows(self, dst_fn, bdst, col_fn, bcol, nchunks):
        S = self.S
        for half in range(nchunks // 4):
            t, bt = self.gdiag.next()
            for q in range(4):
                kc = half * 4 + q
                S.op("dve", lambda e: e.tensor_scalar(out=t[:, q * 128:(q + 1) * 128], in0=self.ident_f[:], scalar1=col_fn(kc),
                                                      scalar2=None, op0=ALU.mult),
                     reads=[self.b_ident_f, bcol], writes=[bt])
            ps, bps = S.ps()
            S.op("pe", lambda e: e.matmul(out=ps[:], lhsT=self.ones_f[:], rhs=t[:], start=True, stop=True),
                 reads=[self.b_ones_f, bt], writes=[bps])
            S.op("act", lambda e: e.copy(out=dst_fn(half), in_=ps[:]), reads=[bps], writes=[bdst])

    def phase1(self, l):
        S, din = self.S, self.din
        self.es = ExitStack()
        self.p1_xt = self.trot("p1_xt", [128, 1024], F32, 2)
        self.p1_sq, self.b_p1_sq = self.tsb("p1_sq", [128, 1024])
        self.p1_st = self.trot("p1_st", [128, 4], F32, 2)
        for tt in range(NT):
            s = 1 if tt < 2 else 0
            if tt < 2:
                src = (din["ctx"] if l == 0 else self.scr["CTXS"])[tt * 128:(tt + 1) * 128, :]
                rd = [] if l == 0 else [self.db("CTXS", tt)]
            else:
                src = (din["x"] if l == 0 else self.scr["XS"])[(tt - 2) * 128:(tt - 1) * 128, :]
                rd = [] if l == 0 else [self.db("XS", tt)]
            xt, bxt = self.p1_xt.next()
            st, bst = self.p1_st.next()
            S.dma("sp", xt[:], src, reads=rd, writes=[bxt])
            S.op("act", lambda e: e.activation(out=self.p1_sq[:], in_=xt[:], func=AF.Square, accum_out=st[:, 0:1]),
                 reads=[bxt], writes=[self.b_p1_sq, bst])
            S.op("dve", lambda e: e.tensor_scalar(out=st[:, 1:2], in0=st[:, 0:1], scalar1=1.0 / D, scalar2=EPS, op0=ALU.mult, op1=ALU.add),
                 reads=[bst], writes=[bst])
            S.op("act", lambda e: e.activation(out=st[:, 2:3], in_=st[:, 1:2], func=AF.Ln), reads=[bst], writes=[bst])
            S.op("act", lambda e: e.activation(out=st[:, 3:4], in_=st[:, 2:3], func=AF.Exp, scale=-0.5), reads=[bst], writes=[bst])
            S.op("act", lambda e: e.activation(out=xt[:], in_=xt[:], func=AF.Copy, scale=st[:, 3:4]),
                 reads=[bst, bxt], writes=[bxt])
            for half in range(2):
                ps, bps = S.ps()
                for q in range(4):
                    kc = half * 4 + q
                    S.op("pe", lambda e: e.transpose(out=ps[:, q * 128:(q + 1) * 128], in_=xt[:, kc * 128:(kc + 1) * 128], identity=self.ident_f[:]),
                         reads=[bxt, self.b_ident_f], writes=[bps])
                for q in range(4):
                    kc = half * 4 + q
                    dst = self.hT[:, kc, tt * 128:(tt + 1) * 128]
                    if q % 2 == 0:
                        S.op("dve", lambda e: e.tensor_scalar(out=dst, in0=ps[:, q * 128:(q + 1) * 128], scalar1=self.Afm[:, kc, s:s + 1],
                                                              scalar2=self.mod[:, kc, s:s + 1], op0=ALU.mult, op1=ALU.add),
                             reads=[bps, self.b_Afm, self.b_mod], writes=[self.b_hT[tt]])
                    else:
                        S.op("act", lambda e: e.activation(out=dst, in_=ps[:, q * 128:(q + 1) * 128], func=AF.Identity,
                                                           scale=self.Afm[:, kc, s:s + 1], bias=self.mod[:, kc, s:s + 1]),
                             reads=[bps, self.b_Afm, self.b_mod], writes=[self.b_hT[tt]])
        S.barrier()
        self.es.close()

    def tok_groups(self):
        g = [(0, 256, 0, 2)]
        for i in range(8):
            g.append((256 + i * 512, 512, 2 + 4 * i, 4))
        return g

    class WStream:
        def __init__(self, mk, src2d, groups, bufs):
            self.mk, self.src, self.groups, self.bufs = mk, src2d, list(groups), bufs
            self.i = 0
            self.q = []
            self._issue()

        def _issue(self):
            if self.i < len(self.groups):
                c0, ncols = self.groups[self.i]
                wb, bwb = self.bufs[self.i % len(self.bufs)]
                S = self.mk.S
                st, bst = self.mk.wst.next()
                S.dma("sp", st[:, :, 0:ncols], self.src.rearrange("(k p) n -> p k n", p=128)[:, :, c0:c0 + ncols], writes=[bst])
                S.op("pool", lambda e: e.tensor_copy(out=wb[:, :, 0:ncols], in_=st[:, :, 0:ncols]), reads=[bst], writes=[bwb])
                self.q.append(((c0, ncols), (wb, bwb)))
                self.i += 1

        def get(self, c0, ncols):
            key, val = self.q.pop(0)
            assert key == (c0, ncols), (key, c0, ncols)
            self._issue()
            return val

    def proj_tm_gen(self, l, c0, ncols, handler, ws):
        S = self.S
        if hasattr(self, "marks"):
            self.marks.append(("  L%d tm@%d" % (l, c0), {k: v.count for k, v in S.engs.items()}))
        wb, bwb = ws.get(c0, ncols)
        for tt in range(NT):
            ps, bps = S.ps()
            for kc in range(8):
                S.op("pe", lambda e: e.matmul(out=ps[:, 0:ncols], lhsT=self.hT[:, kc, tt * 128:(tt + 1) * 128], rhs=wb[:, kc, 0:ncols],
                                              start=(kc == 0), stop=(kc == 7)),
                     reads=[self.b_hT[tt], bwb], writes=[bps])
            handler(tt, ps, bps)
            yield

    def proj_tm(self, l, c0, ncols, handler, ws):
        for _ in self.proj_tm_gen(l, c0, ncols, handler, ws):
            pass

    @staticmethod
    def run_pair(g1, g2):
        gens = [g1, g2]
        while gens:
            for g_ in list(gens):
                try:
                    next(g_)
                except StopIteration:
                    gens.remove(g_)

    def transpose_out(self, src_fn, n, rows, dst, dst_buf_list, tag, pool=None):
        S = self.S
        ps, bps = S.ps(pool)
        psb = ps[:].bitcast(BF16)
        for i in range(n):
            src, bsrc = src_fn(i)
            S.op("pe", lambda e: e.transpose(out=psb[0:rows, i * 128:(i + 1) * 128], in_=src, identity=self.ident_b[:]),
                 reads=[bsrc, self.b_ident_b], writes=[bps])
        S.op("act", lambda e: e.copy(out=dst, in_=psb[0:rows, 0:n * 128]), reads=[bps], writes=dst_buf_list)

    def phase2(self, l):
        S, din, scr = self.S, self.din, self.scr
        last = (l == self.nlayers - 1)
        self.es = ExitStack()
        self.zt = self.trot("zt", [128, 512], BF16, 2)
        self.tmpA = self.trot("tmpA", [128, 512], F32, 2)
        self.tmpB = self.trot("tmpB", [128, 512], F32, 2)
        self.tmo = self.trot("tmo", [128, 512], BF16, 3)
        self.fmo = self.trot("fmo", [128, 512], BF16, 3)
        self.qa = self.trot("qa", [128, 8, 128], BF16, 2)
        self.ka = self.trot("ka", [128, 2, 128], BF16, 2)
        self.vb = self.trot("vb", [128, 2, 65], BF16, 2)
        self.qaT = self.trot("qaT", [128, 8, 128], BF16, 2)
        self.kaT = self.trot("kaT", [128, 2, 128], BF16, 2)
        self.csc = self.trot("csc", [128, 2, 64], F32, 3)
        self.csb = self.trot("csb", [128, 2, 256], F32, 3)
        self.st8 = self.trot("st8", [128, 24], F32, 4)
        self.tmp8 = self.trot("tmp8", [128, NT, 8], F32, 4)
        for n in ("LNB", "GRAW", "GT"):
            self.asc[n] = self.tsb("asc_" + n, [128, NT, 8])
        self.KM, self.b_KM = self.tsb("KM", [128, 2])
        self.half8, self.b_half8 = self.tsb("half8", [128, 8])
        S.op("pool", lambda e: e.memset(self.half8[:], 0.5), writes=[self.b_half8])
        self.rowbuf, self.b_rowbuf = self.tsb("rowbuf", [128, 4360])
        self.slrow, self.b_slrow = self.tsb("slrow", [128, NTOK])
        S.op("pool", lambda e: e.memset(self.rowbuf[:], 0.0), writes=[self.b_rowbuf])
        for it in self.vb.items:
            S.op("pool", lambda e: e.memset(it[0][:], 1.0), writes=[it[1]])
        for it in self.ka.items + self.qa.items:
            S.op("pool", lambda e: e.memset(it[0][:], 0.0), writes=[it[1]])
        for it in self.ka.items:
            S.op("pool", lambda e: e.memset(it[0][:, :, 64:65], 1.0), writes=[it[1]])
        S.op("pool", lambda e: e.memset(self.KM[:], 0.0), writes=[self.b_KM])

        def silu_out(name):
            def h(tt, ps, bps):
                z, bz = self.zt.next()
                S.op("act", lambda e: e.activation(out=z[:], in_=ps[:], func=AF.Silu), reads=[bps], writes=[bz])
                S.dma("act", scr[name][tt * 128:(tt + 1) * 128, :], z[:], reads=[bz], writes=[self.db(name, tt)])
            return h

        wsrc = din["w_in"][l]
        bufsA, bufsB = self.wb.items[0:2], self.wb.items[2:4]
        ws1 = self.WStream(self, wsrc, [(O_AZ, 512), (O_AB, 16), (O_BKV, 256)], bufsA)
        self.proj_tm(l, O_AZ, 512, silu_out("ZA"), ws1)
        if self.stop == "p2a":
            S.barrier()
            self.es.close()
            return

        def h_ab(tt, ps, bps):
            S.op("act", lambda e: e.copy(out=self.SCR[:, tt, :], in_=ps[:, 0:16]), reads=[bps], writes=[self.b_SCR])
        self.proj_tm(l, O_AB, 16, h_ab, ws1)
        if self.stop == "p2b1":
            S.barrier()
            self.es.close()
            return
        self.a_scalars(l)
        if self.stop in ("p2b", "p2b2"):
            S.barrier()
            self.es.close()
            return

        def load_cs(tt, which):
            rotp, cn, sn, w = (self.csc, "k_cosc", "k_sinc", 64) if which == "c" else (self.csb, "k_cosb", "k_sinb", 256)
            cs, bcs = rotp.next()
            r0 = (tt - 2) * 128
            S.dma("sp", cs[:, 0, :], din[cn][r0:r0 + 128, :], writes=[bcs])
            S.dma("sp", cs[:, 1, :], din[sn][r0:r0 + 128, :], writes=[bcs])
            return cs, bcs

        def rope(x1, x2, cosb, sinb, o1, o2, shape_fn, bps, bcs, bout, scale=None):
            ta, bta = self.tmpA.next()
            tb, btb = self.tmpB.next()
            ta1, ta2 = shape_fn(ta[:, 0:256]), shape_fn(ta[:, 256:512])
            tb1, tb2 = shape_fn(tb[:, 0:256]), shape_fn(tb[:, 256:512])
            if scale is None:
                mul = lambda o, a, b: (lambda e: e.tensor_tensor(out=o, in0=a, in1=b, op=ALU.mult))
            else:
                mul = lambda o, a, b: (lambda e: e.scalar_tensor_tensor(out=o, in0=a, scalar=scale, in1=b, op0=ALU.mult, op1=ALU.mult))
            S.op("dve", mul(ta1, x1, cosb), reads=[bps, bcs], writes=[bta])
            S.op("dve", mul(tb1, x2, sinb), reads=[bps, bcs], writes=[btb])
            S.op("dve", mul(ta2, x1, sinb), reads=[bps, bcs], writes=[bta])
            S.op("dve", mul(tb2, x2, cosb), reads=[bps, bcs], writes=[btb])
            S.op("pool", lambda e: e.tensor_tensor(out=o1, in0=ta1, in1=tb1, op=ALU.subtract), reads=[bta, btb], writes=[bout])
            S.op("pool", lambda e: e.tensor_tensor(out=o2, in0=ta2, in1=tb2, op=ALU.add), reads=[bta, btb], writes=[bout])

        def rope_b(ps_ap, nha, tt, dst, bdst, bps, nh):
            o, bo = self.tmo.next()
            if tt >= 2:
                cs, bcs = load_cs(tt, "b")
                pv = ps_ap.rearrange("p (g f k) -> p g f k", f=2, k=16)
                ov = o[:, 0:nha * 32].rearrange("p (g f k) -> p g f k", f=2, k=16)
                cosb = cs[:, 0, 0:nha * 16].rearrange("p (g k) -> p g k", k=16)
                sinb = cs[:, 1, 0:nha * 16].rearrange("p (g k) -> p g k", k=16)
                rope(pv[:, :, 0, :], pv[:, :, 1, :], cosb, sinb, ov[:, :, 0, :], ov[:, :, 1, :],
                     lambda a: a[:, 0:nha * 16].rearrange("p (g k) -> p g k", k=16), bps, bcs, bo)
                S.op("act", lambda e: e.copy(out=dst[:, :, 0:64], in_=o[:, 0:nha * 32].rearrange("p (h k) -> p h k", h=nh)), reads=[bo], writes=[bdst])
            else:
                S.op("act", lambda e: e.copy(out=dst[:, :, 0:64], in_=ps_ap.rearrange("p (h k) -> p h k", h=nh)), reads=[bps], writes=[bdst])

        def h_bkv(tt, ps, bps):
            ka, bka = self.ka.next()
            rope_b(ps[:, 0:128], 4, tt, ka, bka, bps, 2)
            ta, bta = self.tmpA.next()
            st, bst = self.st8.next()
            S.op("act", lambda e: e.activation(out=ta[:, 0:128], in_=ps[:, 0:128], func=AF.Square), reads=[bps], writes=[bta])
            S.op("dve", lambda e: e.tensor_reduce(out=st[:, 0:2], in_=ta[:, 0:128].rearrange("p (h k) -> p h k", h=2), axis=AX.X, op=ALU.add),
                 reads=[bta], writes=[bst])
            S.op("dve", lambda e: e.tensor_tensor(out=self.KM[:], in0=self.KM[:], in1=st[:, 0:2], op=ALU.max), reads=[bst, self.b_KM], writes=[self.b_KM])
            vb, bvb = self.vb.next()
            S.op("dve", lambda e: e.tensor_copy(out=vb[:, :, 0:64], in_=ps[:, 128:256].rearrange("p (h k) -> p h k", h=2)),
                 reads=[bps], writes=[bvb])
            S.dma("act", scr["VB_TM"][tt * 128:(tt + 1) * 128, :, :], vb[:], reads=[bvb], writes=[self.db("VB_TM", tt)])
            kT, bkT = self.kaT.next()
            self.transpose_out(lambda i: (ka[:, i, :], bka), 2, 128, kT[:].rearrange("r h t -> r (h t)"), [bkT], "kbt")
            S.dma("act", scr["KB_FM"].rearrange("h r t -> r h t")[:, :, tt * 128:(tt + 1) * 128], kT[:], reads=[bkT], writes=[self.db("KB_FM", tt)])
        self.proj_tm(l, O_BKV, 256, h_bkv, ws1)
        if self.stop == "p2c":
            S.barrier()
            self.es.close()
            return
        S.op("dve", lambda e: e.tensor_reduce(out=self.kmx[:, 0:1], in_=self.KM[:], axis=AX.X, op=ALU.max), reads=[self.b_KM], writes=[self.b_kmx])
        dgk, bdgk = self.gdiag.next()
        S.op("dve", lambda e: e.tensor_scalar(out=dgk[:, 0:128], in0=self.ident_f[:], scalar1=self.kmx[:, 0:1], scalar2=None, op0=ALU.mult),
             reads=[self.b_ident_f, self.b_kmx], writes=[bdgk])
        ps, bps = S.ps()
        S.op("pe", lambda e: e.matmul(out=ps[:, 0:128], lhsT=self.ones_f[:], rhs=dgk[:, 0:128], start=True, stop=True),
             reads=[self.b_ones_f, bdgk], writes=[bps])
        kr, bkr = self.tsb("kmrow", [128, 4])
        S.op("dve", lambda e: e.tensor_reduce(out=kr[:, 0:1], in_=ps[:, 0:128], axis=AX.X, op=ALU.max), reads=[bps], writes=[bkr])
        S.op("act", lambda e: e.activation(out=kr[:, 1:2], in_=kr[:, 0:1], func=AF.Ln), reads=[bkr], writes=[bkr])
        S.op("act", lambda e: e.activation(out=kr[:, 2:3], in_=kr[:, 1:2], func=AF.Exp, scale=0.5), reads=[bkr], writes=[bkr])
        S.op("dve", lambda e: e.tensor_scalar(out=self.kmx[:, 1:2], in0=kr[:, 2:3], scalar1=-1.0, scalar2=None, op0=ALU.mult), reads=[bkr], writes=[self.b_kmx])
        S.op("dve", lambda e: e.tensor_scalar(out=self.kmx[:, 2:3], in0=kr[:, 2:3], scalar1=-0.125, scalar2=None, op0=ALU.mult), reads=[bkr], writes=[self.b_kmx])

        def h_bq(tt, ps, bps):
            qa, bqa = self.qa.next()
            ta, bta = self.tmpA.next()
            st, bst = self.st8.next()
            S.op("act", lambda e: e.activation(out=ta[:], in_=ps[:], func=AF.Square), reads=[bps], writes=[bta])
            S.op("dve", lambda e: e.tensor_reduce(out=st[:, 0:8], in_=ta[:].rearrange("p (h k) -> p h k", h=8), axis=AX.X, op=ALU.add),
                 reads=[bta], writes=[bst])
            S.op("pool", lambda e: e.tensor_tensor(out=st[:, 16:24], in0=st[:, 0:8], in1=self.half8[:], op=ALU.pow), reads=[bst, self.b_half8], writes=[bst])
            rope_b(ps[:], 16, tt, qa, bqa, bps, 8)
            S.op("dve", lambda e: e.tensor_scalar(out=qa[:, :, 64], in0=st[:, 16:24], scalar1=self.kmx[:, 1:2], scalar2=None, op0=ALU.mult),
                 reads=[bst, self.b_kmx], writes=[bqa])
            S.op("dve", lambda e: e.scalar_tensor_tensor(out=self.SH2[:, tt, :], in0=st[:, 16:24], scalar=self.kmx[:, 2:3], in1=self.par["b_sink"][0][:],
                                                         op0=ALU.mult, op1=ALU.add),
                 reads=[bst, self.b_kmx, self.par["b_sink"][1]], writes=[self.b_SH2])
            qT, bqT = self.qaT.next()
            self.transpose_out(lambda i: (qa[:, i, :], bqa), 8, 128, qT[:].rearrange("r h t -> r (h t)"), [bqT], "qbt")
            S.dma("act", scr["QB_FM"].rearrange("h r t -> r h t")[:, :, tt * 128:(tt + 1) * 128], qT[:], reads=[bqT], writes=[self.db("QB_FM", tt)])
        if self.stop == "p2d":
            S.barrier()
            self.es.close()
            return

        def h_cqk(name_fm, name_tm, scale):
            def h(tt, ps, bps):
                o, bo = self.tmo.next()
                if tt >= 2:
                    cs, bcs = load_cs(tt, "c")
                    pv = ps[:].rearrange("p (h f k) -> p h f k", h=4, f=2)
                    ov = o[:].rearrange("p (h f k) -> p h f k", h=4, f=2)
                    cosb = cs[:, 0, :].unsqueeze(1).to_broadcast([128, 4, 64])
                    sinb = cs[:, 1, :].unsqueeze(1).to_broadcast([128, 4, 64])
                    rope(pv[:, :, 0, :], pv[:, :, 1, :], cosb, sinb, ov[:, :, 0, :], ov[:, :, 1, :],
                         lambda a: a.rearrange("p (h k) -> p h k", h=4), bps, bcs, bo, scale=scale)
                else:
                    S.op("act", lambda e: e.activation(out=o[:], in_=ps[:], func=AF.Copy, scale=(1.0 if scale is None else scale)), reads=[bps], writes=[bo])
                if name_tm is not None:
                    S.dma("act", scr[name_tm][tt * 128:(tt + 1) * 128, :], o[:], reads=[bo], writes=[self.db(name_tm, tt)])
                f, bf = self.fmo.next()
                self.transpose_out(lambda i: (o[:, i * 128:(i + 1) * 128], bo), 4, 128, f[:], [bf], "cfm")
                S.dma("act", scr[name_fm].rearrange("h p t -> p h t")[:, :, tt * 128:(tt + 1) * 128], f[:].rearrange("p (h t) -> p h t", h=4),
                      reads=[bf], writes=[self.db(name_fm, tt)])
            return h

        def h_cv(tt, ps, bps):
            o, bo = self.tmo.next()
            S.op("act", lambda e: e.copy(out=o[:], in_=ps[:]), reads=[bps], writes=[bo])
            S.dma("act", scr["VC_TM"][tt * 128:(tt + 1) * 128, :], o[:], reads=[bo], writes=[self.db("VC_TM", tt)])
        wsZ = self.WStream(self, wsrc, [(O_BZ, 512), (O_CZ, 512)], bufsB)
        self.proj_tm(l, O_BZ, 512, silu_out("ZB"), wsZ)
        self.proj_tm(l, O_CZ, 512, silu_out("ZC"), wsZ)
        groups = self.tok_groups()
        wsH = self.WStream(self, wsrc, [(O_BQ, 512), (O_CQ, 512), (O_CK, 512)] + [(g * 512, 512) for g in range(3)], bufsA)
        wsL = self.WStream(self, wsrc, [(O_MG + g * 512, 512) for g in range(6)] + [(O_CV, 512)], bufsB)
        wsF = wsH
        wsM = wsL

        def heavy():
            yield from self.proj_tm_gen(l, O_BQ, 512, h_bq, wsH)
            yield from self.proj_tm_gen(l, O_CQ, 512, h_cqk("QC_FM", None, None), wsH)
            yield from self.proj_tm_gen(l, O_CK, 512, h_cqk("KC_FM", "KC_TM", float(CH) ** -0.5), wsH)

        def light():
            yield from self.proj_tm_gen(l, O_CV, 512, h_cv, wsL)
        if self.stop == "p2e":
            S.barrier()
            self.es.close()
            return


        def afm_gen():
            wcur = {}

            def get_w(g3):
                if g3 not in wcur:
                    self.marks.append(("  L%d afm%d" % (l, g3), {k: v.count for k, v in S.engs.items()}))
                    wcur[g3] = wsF.get(g3 * 512, 512)
                return wcur[g3]

            def proj_piece(ct, gi):
                g3, cl = ct // 4, ct % 4
                wb, bwb = get_w(g3)
                (t0, n, tile0, ntile) = groups[gi]
                ps, bps = S.ps()
                for kc in range(8):
                    S.op("pe", lambda e: e.matmul(out=ps[:, 0:n], lhsT=wb[:, kc, cl * 128:(cl + 1) * 128], rhs=self.hT[:, kc, t0:t0 + n],
                                                  start=(kc == 0), stop=(kc == 7)),
                         reads=[self.b_hT[tile0 + i] for i in range(ntile)] + [bwb], writes=[bps])
                off = 2 + t0 if t0 == 0 else 6 + t0
                S.op("act", lambda e: e.copy(out=self.rowbuf[:, off:off + n], in_=ps[:, 0:n]), reads=[bps], writes=[self.b_rowbuf.sub(t0)])

            def pass1_piece(ct, gi):
                g3, head = ct // 4, ct % 4
                (t0, n, tile0, ntile) = groups[gi]
                off = 2 + t0 if t0 == 0 else 6 + t0
                cv, bcv = self.tmpA.next()
                nb = [gi] if gi == 0 else [j for j in (gi - 1, gi, gi + 1) if 1 <= j < len(groups)]
                rb_reads = [self.b_rowbuf.sub(groups[j][0]) for j in nb]
                for k in range(5):
                    src = self.rowbuf[:, off + k - 2:off + k - 2 + n]
                    if k == 0:
                        S.op("dve", lambda e: e.tensor_scalar(out=cv[:, 0:n], in0=src, scalar1=self.convw[:, ct, 0:1], scalar2=None, op0=ALU.mult),
                             reads=rb_reads + [self.b_convw], writes=[bcv])
                    else:
                        S.op("dve", lambda e: e.scalar_tensor_tensor(out=cv[:, 0:n], in0=src, scalar=self.convw[:, ct, k:k + 1], in1=cv[:, 0:n],
                                                                     op0=ALU.mult, op1=ALU.add),
                             reads=rb_reads + [self.b_convw, bcv], writes=[bcv])
                if g3 < 2:
                    S.op("act", lambda e: e.activation(out=self.slrow[:, t0:t0 + n], in_=cv[:, 0:n], func=AF.Silu), reads=[bcv], writes=[self.b_slrow.sub(t0)])
                else:
                    o, bo = self.tmo.next()
                    S.op("act", lambda e: e.activation(out=o[:, 0:n], in_=cv[:, 0:n], func=AF.Silu), reads=[bcv], writes=[bo])
                    f, bf = self.fmo.next()
                    self.transpose_out(lambda i: (o[:, i * 128:(i + 1) * 128], bo), ntile, 128, f[:, 0:n], [bf], "va")
                    S.dma("act", scr["VA_TM"][head].rearrange("(t p) c -> p t c", p=128)[:, tile0:tile0 + ntile, :],
                          f[:, 0:n].rearrange("p (t c) -> p t c", c=128), reads=[bf], writes=[self.db("VA_TM", tile0 + i) for i in range(ntile)])

            def pass2_piece(ct, gi):
                g3, head = ct // 4, ct % 4
                name = "QA_FM" if g3 == 0 else "KA_FM"
                (t0, n, tile0, ntile) = groups[gi]
                sq, bsq = self.tmo.next()
                S.op("act", lambda e: e.activation(out=sq[:, 0:n], in_=self.slrow[:, t0:t0 + n], func=AF.Square), reads=[self.b_slrow.sub(t0)], writes=[bsq])
                ps, bps = S.ps()
                S.op("pe", lambda e: e.matmul(out=ps[:, 0:n], lhsT=self.ones_b[:], rhs=sq[:, 0:n], start=True, stop=True),
                     reads=[self.b_ones_b, bsq], writes=[bps])
                ta, bta = self.tmpB.next()
                S.op("act", lambda e: e.activation(out=ta[:, 0:n], in_=ps[:, 0:n], func=AF.Ln, bias=self.epsb[:, 0:1]), reads=[bps, self.b_epsb], writes=[bta])
                S.op("act", lambda e: e.activation(out=ta[:, 0:n], in_=ta[:, 0:n], func=AF.Exp, scale=-0.5, bias=self.epsb[:, 1 + g3:2 + g3]),
                     reads=[bta, self.b_epsb], writes=[bta])
                o, bo = self.fmo.next()
                S.op("dve", lambda e: e.tensor_tensor(out=o[:, 0:n], in0=self.slrow[:, t0:t0 + n], in1=ta[:, 0:n], op=ALU.mult),
                     reads=[self.b_slrow.sub(t0), bta], writes=[bo])
                S.dma("act", scr[name][head][:, t0:t0 + n], o[:, 0:n], reads=[bo], writes=[self.db(name, tile0 + i) for i in range(ntile)])
                if g3 == 1:
                    f, bf = self.zt.next()
                    self.transpose_out(lambda i: (o[:, i * 128:(i + 1) * 128], bo), ntile, 128, f[:, 0:n], [bf], "ka")
                    S.dma("act", scr["KA_TM"][head].rearrange("(t p) c -> p t c", p=128)[:, tile0:tile0 + ntile, :],
                          f[:, 0:n].rearrange("p (t c) -> p t c", c=128), reads=[bf], writes=[self.db("KA_TM", tile0 + i) for i in range(ntile)])

            ng = len(groups)
            for gi in range(ng):
                proj_piece(0, gi)
                yield
            for ct in range(12):
                for gi in range(ng):
                    pass1_piece(ct, gi)
                    if ct + 1 < 12 and gi >= 1:
                        proj_piece(ct + 1, gi - 1)
                    yield
                if ct + 1 < 12:
                    proj_piece(ct + 1, ng - 1)
                    yield
                if ct < 8:
                    for gi in range(ng):
                        pass2_piece(ct, gi)
                        yield

        def merge_gen():
            for g6 in range(6):
                self.marks.append(("  L%d mg%d" % (l, g6), {k: v.count for k, v in S.engs.items()}))
                wb, bwb = wsM.get(O_MG + g6 * 512, 512)
                for cl in range(4):
                    ct = g6 * 4 + cl
                    for (t0, n, tile0, ntile) in groups:
                        if last and t0 == 0:
                            continue
                        ps, bps = S.ps()
                        for kc in range(8):
                            S.op("pe", lambda e: e.matmul(out=ps[:, 0:n], lhsT=wb[:, kc, cl * 128:(cl + 1) * 128], rhs=self.hT[:, kc, t0:t0 + n],
                                                          start=(kc == 0), stop=(kc == 7)),
                                 reads=[self.b_hT[tile0 + i] for i in range(ntile)] + [bwb], writes=[bps])
                        o, bo = self.fmo.next()
                        S.op("act", lambda e: e.activation(out=o[:, 0:n], in_=ps[:, 0:n], func=AF.Sigmoid), reads=[bps], writes=[bo])
                        S.dma("act", scr["GM_FM"][ct * 128:(ct + 1) * 128, t0:t0 + n], o[:, 0:n], reads=[bo],
                              writes=[self.db("GM_FM%d" % ct, tile0 + i) for i in range(ntile)])
                        yield
        self.run_pair(heavy(), merge_gen())
        self.run_pair(afm_gen(), light())
        if self.stop == "p2":
            self.dump_p2()
        S.barrier()
        self.es.close()

    def a_scalars(self, l):
        S = self.S
        A = self.asc
        braw = self.SCR[:, :, 0:8]
        araw = self.SCR[:, :, 8:16]
        rs = [self.b_SCR]
        one = self.epsb[:, 3:4]

        def softplus_parts(x_ap, xb, neg):
            t1, b1 = self.tmp8.next()
            t2, b2 = self.tmp8.next()
            S.op("act", lambda e: e.activation(out=t1[:], in_=x_ap, func=AF.Abs), reads=xb, writes=[b1])
            S.op("act", lambda e: e.activation(out=t1[:], in_=t1[:], func=AF.Exp, scale=-1.0), reads=[b1], writes=[b1])
            S.op("act", lambda e: e.activation(out=t1[:], in_=t1[:], func=AF.Ln, bias=one), reads=[b1, self.b_epsb], writes=[b1])
            S.op("dve", lambda e: e.tensor_scalar(out=t2[:], in0=x_ap, scalar1=(-1.0 if neg else 1.0), scalar2=0.0, op0=ALU.mult, op1=ALU.max),
                 reads=xb, writes=[b2])
            return (t2, b2), (t1, b1)

        (m, bm), (l1, bl1) = softplus_parts(braw, rs, True)
        LNB, bLNB = A["LNB"]
        S.op("dve", lambda e: e.scalar_tensor_tensor(out=LNB[:], in0=m[:], scalar=-1.0, in1=l1[:], op0=ALU.mult, op1=ALU.subtract),
             reads=[bm, bl1], writes=[bLNB])
        BETA, bBETA = A["BETA"]
        S.op("act", lambda e: e.activation(out=BETA[:], in_=LNB[:], func=AF.Exp), reads=[bLNB], writes=[bBETA])
        xa, bxa = self.tmp8.next()
        dtb, bdtb = self.par["a_dt_bias"]
        S.op("dve", lambda e: e.tensor_tensor(out=xa[:], in0=araw, in1=dtb[:].unsqueeze(1).to_broadcast([128, NT, 8]), op=ALU.add),
             reads=rs + [bdtb], writes=[bxa])
        (m2, bm2), (l2, bl2) = softplus_parts(xa[:], [bxa], False)
        S.op("dve", lambda e: e.tensor_tensor(out=m2[:], in0=m2[:], in1=l2[:], op=ALU.add), reads=[bm2, bl2], writes=[bm2])
        alog, balog = self.par["a_log"]
        nea, bnea = self.st8.next()
        S.op("act", lambda e: e.activation(out=nea[:, 0:8], in_=alog[:], func=AF.Exp), reads=[balog], writes=[bnea])
        GRAW, bGRAW = A["GRAW"]
        S.op("dve", lambda e: e.scalar_tensor_tensor(out=GRAW[:], in0=m2[:], scalar=-1.0, in1=nea[:, 0:8].unsqueeze(1).to_broadcast([128, NT, 8]),
                                                     op0=ALU.mult, op1=ALU.mult),
             reads=[bm2, bnea], writes=[bGRAW])
        GC, bGC = A["GC"]
        GT, bGT = A["GT"]
        if self.stop == "p2b2":
            return
        gflat = GRAW[:].rearrange("p t c -> p (t c)")
        res = []
        for lhs, blhs in ((self.tri[:, 0, :], self.b_tri), (self.tri[:, 1, :], self.b_tri), (self.ones_f[:], self.b_ones_f)):
            ps, bps = S.ps()
            S.op("pe", lambda e: e.matmul(out=ps[:, 0:NT * 8], lhsT=lhs, rhs=gflat, start=True, stop=True), reads=[blhs, bGRAW], writes=[bps])
            res.append((ps[:, 0:NT * 8].rearrange("p (t c) -> p t c", c=8), bps))
        S.op("dve", lambda e: e.tensor_copy(out=GC[:, :, 0:4], in_=res[0][0][:, :, 0:4]), reads=[res[0][1]], writes=[bGC])
        S.op("dve", lambda e: e.tensor_copy(out=GC[:, :, 4:8], in_=res[1][0][:, :, 4:8]), reads=[res[1][1]], writes=[bGC])
        S.op("act", lambda e: e.copy(out=GT[:], in_=res[2][0]), reads=[res[2][1]], writes=[bGT])
        NEGG, bNEGG = A["NEGG"]
        S.op("dve", lambda e: e.tensor_scalar(out=NEGG[:], in0=GC[:], scalar1=-1.0, scalar2=None, op0=ALU.mult), reads=[bGC], writes=[bNEGG])
        LNBMG, bLNBMG = A["LNBMG"]
        S.op("dve", lambda e: e.tensor_tensor(out=LNBMG[:], in0=LNB[:], in1=GC[:], op=ALU.subtract), reads=[bLNB, bGC], writes=[bLNBMG])
        NEGEG, bNEGEG = A["NEGEG"]
        S.op("act", lambda e: e.activation(out=NEGEG[:], in_=GC[:], func=AF.Exp), reads=[bGC], writes=[bNEGEG])
        S.op("dve", lambda e: e.tensor_scalar(out=NEGEG[:], in0=NEGEG[:], scalar1=-1.0, scalar2=None, op0=ALU.mult), reads=[bNEGEG], writes=[bNEGEG])
        ETAIL, bETAIL = A["ETAIL"]
        S.op("dve", lambda e: e.tensor_tensor(out=ETAIL[:], in0=GT[:], in1=GC[:], op=ALU.subtract), reads=[bGT, bGC], writes=[bETAIL])
        S.op("act", lambda e: e.activation(out=ETAIL[:], in_=ETAIL[:], func=AF.Exp), reads=[bETAIL], writes=[bETAIL])
        EGL, bEGL = A["EGL"]
        S.op("act", lambda e: e.activation(out=EGL[:], in_=GT[:], func=AF.Exp), reads=[bGT], writes=[bEGL])

    def core_b(self, l, defer=False):
        S, scr, din = self.S, self.scr, self.din
        last = (l == self.nlayers - 1)
        if not defer:
            self.es = ExitStack()
        KBT = self.R1[:, 0:2 * NTOK].rearrange("p (g t) -> p g t", g=2)
        VBR = self.R1[:, 2 * NTOK:2 * NTOK + NT * 130].rearrange("p (t g c) -> p t g c", g=2, c=65)
        bKBT, bVBR = Buf("KBT"), Buf("VBR")
        S.dma("sp", KBT, scr["KB_FM"].rearrange("g r t -> r g t"), reads=self.dball("KB_FM"), writes=[bKBT])
        S.dma("sp", VBR, scr["VB_TM"].rearrange("(t p) g c -> p t g c", p=128), reads=self.dball("VB_TM"), writes=[bVBR])
        if l == 0:
            bst, bbst = self.tsb("bmst", [128, 2, 512])
            S.dma("sp", bst[:], din["k_bmask"].rearrange("a p n -> p a n"), writes=[bbst])
            S.op("pool", lambda e: e.tensor_copy(out=self.bmask[:], in_=bst[:]), reads=[bbst], writes=[self.b_bmask])
        qTr = self.trot("b_qT", [128, 4, 128], BF16, 2)
        pTr = self.trot("b_pT", [128, 5, 512], BF16, 2)
        zbr = self.trot("b_zb", [128, 512], BF16, 2)
        ybr = self.trot("b_yb", [128, 512], BF16, 2)
        obr = self.trot("b_ob", [128, 256], F32, 2)
        str_ = self.trot("b_st", [128, 16], F32, 6)
        fmo = self.trot("b_fmo", [128, 512], BF16, 2)
        qtiles = list(range(2, NT)) if last else list(range(NT))
        def b_gen():
            for qt in qtiles:
                zb, bzb = zbr.next()
                S.dma("sp", zb[:], scr["ZB"][qt * 128:(qt + 1) * 128, :], reads=[self.db("ZB", qt)], writes=[bzb])
                yb, byb = ybr.next()
                for g in range(2):
                    qT, bqT = qTr.next()
                    S.dma("sp", qT[:], scr["QB_FM"][g * 4:(g + 1) * 4].rearrange("h r t -> r h t")[:, :, qt * 128:(qt + 1) * 128],
                          reads=[self.db("QB_FM", qt)], writes=[bqT])
                    keys = [(0, None), (1, None)]
                    if qt >= 2:
                        if qt - 1 >= 2:
                            keys.append((qt - 1, 0))
                        keys.append((qt, None))
                        if qt + 1 < NT:
                            keys.append((qt + 1, 1))
                    pT, bpT = pTr.next()
                    for idx, (kt, m) in enumerate(keys):
                        ps, bps = S.ps()
                        S.op("pe", lambda e: e.matmul(out=ps[:], lhsT=KBT[:, g, kt * 128:(kt + 1) * 128], rhs=qT[:].rearrange("p h t -> p (h t)"),
                                                      start=True, stop=True), reads=[bKBT, bqT], writes=[bps])
                        S.op("act", lambda e: e.activation(out=pT[:, idx, :], in_=ps[:], func=AF.Exp, scale=0.125), reads=[bps], writes=[bpT.sub(idx)])
                        if m is not None:
                            S.op("pool", lambda e: e.tensor_tensor(out=pT[:, idx, :], in0=pT[:, idx, :], in1=self.bmask[:, m, :], op=ALU.mult),
                                 reads=[bpT.sub(idx), self.b_bmask], writes=[bpT.sub(idx)])
                    po, bpo = S.ps()
                    for h in range(4):
                        for idx, (kt, m) in enumerate(keys):
                            S.op("pe", lambda e: e.matmul(out=po[:, h * 65:(h + 1) * 65], lhsT=pT[:, idx, h * 128:(h + 1) * 128], rhs=VBR[:, kt, g, :],
                                                          start=(idx == 0), stop=(idx == len(keys) - 1)), reads=[bpT.sub(idx), bVBR], writes=[bpo])
                    st, bst_ = str_.next()
                    pov = po[:, 0:260].rearrange("p (h c) -> p h c", c=65)
                    S.op("act", lambda e: e.activation(out=st[:, 0:4], in_=self.SH2[:, qt, g * 4:(g + 1) * 4], func=AF.Exp), reads=[self.b_SH2], writes=[bst_])
                    S.op("dve", lambda e: e.tensor_tensor(out=st[:, 4:8], in0=pov[:, :, 64], in1=st[:, 0:4], op=ALU.add), reads=[bpo, bst_], writes=[bst_])
                    S.op("dve", lambda e: e.reciprocal(out=st[:, 8:12], in_=st[:, 4:8]), reads=[bst_], writes=[bst_])
                    ob, bob = obr.next()
                    S.op("dve", lambda e: e.tensor_tensor(out=ob[:].rearrange("p (h c) -> p h c", c=64), in0=pov[:, :, 0:64],
                                                          in1=st[:, 8:12].unsqueeze(2).to_broadcast([128, 4, 64]), op=ALU.mult),
                         reads=[bpo, bst_], writes=[bob])
                    S.op("pool", lambda e: e.tensor_tensor(out=yb[:, g * 256:(g + 1) * 256], in0=ob[:], in1=zb[:, g * 256:(g + 1) * 256], op=ALU.mult),
                         reads=[bob, bzb], writes=[byb.sub(g)])
                    yield
                self.y_out(1, qt, yb, byb, fmo)
                yield
        if defer:
            return b_gen()
        for _ in b_gen():
            pass
        S.barrier()
        self.es.close()

    def core_bc(self, l):
        self.es = ExitStack()
        gb = self.core_b(l, defer=True)
        gc = self.core_c(l, defer=True)
        self.run_pair(gb, gc)
        self.S.barrier()
        self.es.close()

    def y_out(self, br, tt, y, by, fmo, pool=None):
        S = self.S
        f, bf = fmo.next()
        self.transpose_out(lambda i: (y[:, i * 128:(i + 1) * 128], by), 4, 128, f[:], [bf], "y", pool=pool)
        S.dma("act", self.scr["Y_FM"][br].rearrange("(k p) t -> p k t", p=128)[:, :, tt * 128:(tt + 1) * 128],
              f[:].rearrange("p (k t) -> p k t", k=4), reads=[bf], writes=[self.db("Y_FM%d" % br, tt)])

    def core_c(self, l, defer=False):
        S, scr = self.S, self.scr
        last = (l == self.nlayers - 1)
        if not defer:
            self.es = ExitStack()
        one = self.epsb[:, 3:4]
        cd, bcd = self.par["c_decay"]
        c8 = self.trot("c_c8", [128, 8], F32, 6)
        t1, b1 = c8.next()
        t2, b2 = c8.next()
        LG, bLG = c8.next()
        S.op("act", lambda e: e.activation(out=t1[:], in_=cd[:], func=AF.Abs), reads=[bcd], writes=[b1])
        S.op("act", lambda e: e.activation(out=t1[:], in_=t1[:], func=AF.Exp, scale=-1.0), reads=[b1], writes=[b1])
        S.op("act", lambda e: e.activation(out=t1[:], in_=t1[:], func=AF.Ln, bias=one), reads=[b1, self.b_epsb], writes=[b1])
        S.op("dve", lambda e: e.tensor_scalar(out=t2[:], in0=cd[:], scalar1=-1.0, scalar2=0.0, op0=ALU.mult, op1=ALU.max), reads=[bcd], writes=[b2])
        S.op("dve", lambda e: e.scalar_tensor_tensor(out=LG[:], in0=t2[:], scalar=-1.0, in1=t1[:], op0=ALU.mult, op1=ALU.subtract),
             reads=[b1, b2], writes=[bLG])
        GAMC, bGAMC = c8.next()
        S.op("act", lambda e: e.activation(out=GAMC[:], in_=LG[:], func=AF.Exp, scale=float(CH)), reads=[bLG], writes=[bGAMC])
        KDEC, bKDEC = c8.next()
        S.op("dve", lambda e: e.tensor_tensor(out=KDEC[:], in0=LG[:], in1=self.cj[:], op=ALU.mult), reads=[bLG, self.b_cj], writes=[bKDEC])
        S.op("act", lambda e: e.activation(out=KDEC[:], in_=KDEC[:], func=AF.Exp), reads=[bKDEC], writes=[bKDEC])
        DM, bDM = self.tsb("c_DM", [128, 512])
        QDF, bQDF = self.tsb("c_QDF", [128, 512], BF16)
        QDB, bQDB = self.tsb("c_QDB", [128, 512], BF16)
        tm = self.trot("c_tm", [128, 128], F32, 2)
        for h in range(4):
            ta, bta = tm.next()
            tb, btb = tm.next()
            S.op("act", lambda e: e.activation(out=ta[:], in_=self.cm[:, 0, :], func=AF.Exp, scale=LG[:, h:h + 1]), reads=[self.b_cm, bLG], writes=[bta])
            S.op("dve", lambda e: e.tensor_tensor(out=ta[:], in0=ta[:], in1=self.cm[:, 2, :], op=ALU.mult), reads=[bta, self.b_cm], writes=[bta])
            S.op("act", lambda e: e.activation(out=tb[:], in_=self.cm[:, 1, :], func=AF.Exp, scale=LG[:, 4 + h:5 + h]), reads=[self.b_cm, bLG], writes=[btb])
            S.op("dve", lambda e: e.tensor_tensor(out=tb[:], in0=tb[:], in1=self.cm[:, 3, :], op=ALU.mult), reads=[btb, self.b_cm], writes=[btb])
            S.op("dve", lambda e: e.tensor_tensor(out=ta[:], in0=ta[:], in1=tb[:], op=ALU.add), reads=[bta, btb], writes=[bta])
            S.op("dve", lambda e: e.scalar_tensor_tensor(out=DM[:, h * 128:(h + 1) * 128], in0=self.ident_f[:], scalar=2.0, in1=ta[:], op0=ALU.mult, op1=ALU.add),
                 reads=[bta, self.b_ident_f], writes=[bDM])
            S.op("act", lambda e: e.activation(out=QDF[:, h * 128:(h + 1) * 128], in_=self.cm[:, 4, :], func=AF.Exp, scale=LG[:, h:h + 1]),
                 reads=[self.b_cm, bLG], writes=[bQDF])
            S.op("act", lambda e: e.activation(out=QDB[:, h * 128:(h + 1) * 128], in_=self.cm[:, 5, :], func=AF.Exp, scale=LG[:, 4 + h:5 + h]),
                 reads=[self.b_cm, bLG], writes=[bQDB])
        kTMr = [self.trot("c_kTM%d" % d, [128, 512], BF16, 2) for d in range(2)]
        vr = [self.trot("c_v%d" % d, [128, 512], BF16, 2) for d in range(3)]
        kdr = [self.trot("c_kd%d" % d, [128, 512], BF16, 2) for d in range(2)]
        sbfr = [self.trot("c_sbf%d" % d, [128, 512], BF16, 2) for d in range(2)]
        S32 = [self.tsb("c_S32_%d" % d, [128, 512]) for d in range(2)]
        SCN = ["SCF", "SCB"]

        def state_update(d, cc, kTM, bkTM, v, bv):
            kd, bkd = kdr[d].next()
            for h in range(4):
                S.op("act", lambda e: e.activation(out=kd[:, h * 128:(h + 1) * 128], in_=kTM[:, h * 128:(h + 1) * 128], func=AF.Copy,
                                                   scale=KDEC[:, d * 4 + h:d * 4 + h + 1]),
                     reads=[bkTM, bKDEC], writes=[bkd.sub(h)])
            ps, bps = S.ps()
            for h in range(4):
                S.op("pe", lambda e: e.matmul(out=ps[:, h * 128:(h + 1) * 128], lhsT=kd[:, h * 128:(h + 1) * 128], rhs=v[:, h * 128:(h + 1) * 128],
                                              start=True, stop=True), reads=[bkd, bv], writes=[bps])
            s32, bs32 = S32[d]
            for h in range(4):
                S.op("dve", lambda e: e.scalar_tensor_tensor(out=s32[:, h * 128:(h + 1) * 128], in0=s32[:, h * 128:(h + 1) * 128],
                                                             scalar=GAMC[:, d * 4 + h:d * 4 + h + 1], in1=ps[:, h * 128:(h + 1) * 128],
                                                             op0=ALU.mult, op1=ALU.add),
                     reads=[bs32.sub(h), bGAMC, bps], writes=[bs32.sub(h)])

        def state_pass(d):
            S.op("pool", lambda e: e.memset(S32[d][0][:], 0.0), writes=[S32[d][1]])
            order = list(range(NT)) if d == 0 else [1, 0] + list(range(NT - 1, 1, -1))
            for cc in order:
                sbf, bsbf = sbfr[d].next()
                S.op("act", lambda e: e.copy(out=sbf[:], in_=S32[d][0][:]), reads=[S32[d][1]], writes=[bsbf])
                S.dma("act", scr[SCN[d]][cc], sbf[:], reads=[bsbf], writes=[self.db(SCN[d], cc)])
                kTM, bkTM = kTMr[d].next()
                v, bv = vr[d].next()
                S.dma("sp", kTM[:], scr["KC_TM"][cc * 128:(cc + 1) * 128, :], reads=[self.db("KC_TM", cc)], writes=[bkTM])
                S.dma("sp", v[:], scr["VC_TM"][cc * 128:(cc + 1) * 128, :], reads=[self.db("VC_TM", cc)], writes=[bv])
                yield
                state_update(d, cc, kTM, bkTM, v, bv)
                yield

        qTr = self.trot("c_qT", [128, 512], BF16, 2)
        kTr = self.trot("c_kT", [128, 512], BF16, 2)
        sbr = self.trot("c_sb", [128, 512], BF16, 2)
        sfr = self.trot("c_sf", [128, 512], BF16, 2)
        zcr = self.trot("c_zc", [128, 512], BF16, 2)
        qkr = self.trot("c_qk", [128, 512], BF16, 2)
        qdr = self.trot("c_qd", [128, 512], BF16, 2)
        ofr = self.trot("c_of", [128, 512], F32, 2)
        t5r = self.trot("c_t5", [128, 512], F32, 1)
        ycr = self.trot("c_yc", [128, 512], BF16, 2)
        stc = self.trot("c_st", [128, 24], F32, 3)
        fmo = self.trot("c_fmo", [128, 512], BF16, 2)
        cnw, bcnw = self.par["c_norm_w"]
        eps = self.epsb[:, 0:1]
        def out_pass():
            for cc in range(NT):
                need_out = (cc >= 2) or (not last)
                if need_out:
                    v, bv = vr[2].next()
                    S.dma("sp", v[:], scr["VC_TM"][cc * 128:(cc + 1) * 128, :], reads=[self.db("VC_TM", cc)], writes=[bv])
                    qT, bqT = qTr.next()
                    kT, bkT = kTr.next()
                    sb, bsb = sbr.next()
                    zc, bzc = zcr.next()
                    S.dma("sp", qT[:].rearrange("p (h t) -> p h t", h=4), scr["QC_FM"].rearrange("h p t -> p h t")[:, :, cc * 128:(cc + 1) * 128],
                          reads=[self.db("QC_FM", cc)], writes=[bqT])
                    S.dma("sp", kT[:].rearrange("p (h t) -> p h t", h=4), scr["KC_FM"].rearrange("h p t -> p h t")[:, :, cc * 128:(cc + 1) * 128],
                          reads=[self.db("KC_FM", cc)], writes=[bkT])
                    S.dma("sp", sb[:], scr["SCB"][cc], reads=[self.db("SCB", cc)], writes=[bsb])
                    S.dma("sp", zc[:], scr["ZC"][cc * 128:(cc + 1) * 128, :], reads=[self.db("ZC", cc)], writes=[bzc])
                    sbf, bsbf = sfr.next()
                    S.dma("sp", sbf[:], scr["SCF"][cc], reads=[self.db("SCF", cc)], writes=[bsbf])
                    ps1, bps1 = S.ps()
                    for h in range(4):
                        S.op("pe", lambda e: e.matmul(out=ps1[:, h * 128:(h + 1) * 128], lhsT=kT[:, h * 128:(h + 1) * 128], rhs=qT[:, h * 128:(h + 1) * 128],
                                                      start=True, stop=True), reads=[bkT, bqT], writes=[bps1])
                    qk, bqk = qkr.next()
                    S.op("dve", lambda e: e.tensor_tensor(out=qk[:], in0=ps1[:], in1=DM[:], op=ALU.mult), reads=[bps1, bDM], writes=[bqk])
                    qdf, bqdf = qdr.next()
                    qdb, bqdb = qdr.next()
                    S.op("pool", lambda e: e.tensor_tensor(out=qdf[:], in0=qT[:], in1=QDF[:], op=ALU.mult), reads=[bqT, bQDF], writes=[bqdf])
                    S.op("pool", lambda e: e.tensor_tensor(out=qdb[:], in0=qT[:], in1=QDB[:], op=ALU.mult), reads=[bqT, bQDB], writes=[bqdb])
                    po, bpo = S.ps()
                    for h in range(4):
                        sl = slice(h * 128, (h + 1) * 128)
                        S.op("pe", lambda e: e.matmul(out=po[:, sl], lhsT=qk[:, sl], rhs=v[:, sl], start=True, stop=False), reads=[bqk, bv], writes=[bpo])
                        S.op("pe", lambda e: e.matmul(out=po[:, sl], lhsT=qdf[:, sl], rhs=sbf[:, sl], start=False, stop=False), reads=[bqdf, bsbf], writes=[bpo])
                        S.op("pe", lambda e: e.matmul(out=po[:, sl], lhsT=qdb[:, sl], rhs=sb[:, sl], start=False, stop=True), reads=[bqdb, bsb], writes=[bpo])
                    of, bof = ofr.next()
                    t5, bt5 = t5r.next()
                    st, bst = stc.next()
                    S.op("act", lambda e: e.copy(out=of[:], in_=po[:]), reads=[bpo], writes=[bof])
                    S.op("act", lambda e: e.activation(out=t5[:], in_=po[:], func=AF.Square), reads=[bpo], writes=[bt5])
                    S.op("dve", lambda e: e.tensor_reduce(out=st[:, 0:4], in_=of[:].rearrange("p (h c) -> p h c", h=4), axis=AX.X, op=ALU.add), reads=[bof], writes=[bst])
                    S.op("dve", lambda e: e.tensor_reduce(out=st[:, 4:8], in_=t5[:].rearrange("p (h c) -> p h c", h=4), axis=AX.X, op=ALU.add), reads=[bt5], writes=[bst])
                    S.op("dve", lambda e: e.tensor_scalar(out=st[:, 8:12], in0=st[:, 0:4], scalar1=1.0 / 128, scalar2=None, op0=ALU.mult), reads=[bst], writes=[bst])
                    S.op("dve", lambda e: e.tensor_tensor(out=st[:, 12:16], in0=st[:, 8:12], in1=st[:, 8:12], op=ALU.mult), reads=[bst], writes=[bst])
                    S.op("dve", lambda e: e.scalar_tensor_tensor(out=st[:, 16:20], in0=st[:, 4:8], scalar=1.0 / 128, in1=st[:, 12:16], op0=ALU.mult, op1=ALU.subtract),
                         reads=[bst], writes=[bst])
                    S.op("act", lambda e: e.activation(out=st[:, 20:24], in_=st[:, 16:20], func=AF.Ln, bias=eps), reads=[bst, self.b_epsb], writes=[bst])
                    S.op("act", lambda e: e.activation(out=st[:, 20:24], in_=st[:, 20:24], func=AF.Exp, scale=-0.5), reads=[bst], writes=[bst])
                    ofv = of[:].rearrange("p (h c) -> p h c", h=4)
                    S.op("dve", lambda e: e.tensor_tensor(out=ofv, in0=ofv, in1=st[:, 8:12].unsqueeze(2).to_broadcast([128, 4, 128]), op=ALU.subtract),
                         reads=[bof, bst], writes=[bof])
                    S.op("dve", lambda e: e.tensor_tensor(out=ofv, in0=ofv, in1=st[:, 20:24].unsqueeze(2).to_broadcast([128, 4, 128]), op=ALU.mult),
                         reads=[bof, bst], writes=[bof])
                    S.op("pool", lambda e: e.tensor_tensor(out=of[:], in0=of[:], in1=cnw[:], op=ALU.mult), reads=[bof, bcnw], writes=[bof])
                    yc, byc = ycr.next()
                    S.op("pool", lambda e: e.tensor_tensor(out=yc[:], in0=of[:], in1=zc[:], op=ALU.mult), reads=[bof, bzc], writes=[byc])
                    self.y_out(2, cc, yc, byc, fmo)
                yield

        def c_gen():
            gens = [state_pass(0), state_pass(1)]
            while gens:
                for g_ in list(gens):
                    try:
                        next(g_)
                        yield
                    except StopIteration:
                        gens.remove(g_)
            yield from out_pass()
        if defer:
            return c_gen()
        for _ in c_gen():
            pass
        S.barrier()
        self.es.close()

    def core_a(self, l):
        S, scr = self.S, self.scr
        last = (l == self.nlayers - 1)
        self.es = ExitStack()
        A = self.asc
        import os
        KPRE = int(os.environ.get("KPRE", "2"))
        r1_next = [0]

        def mkrot(name, k, use_r1=True):
            items = []
            for i in range(k):
                if use_r1 and r1_next[0] < 68:
                    j = r1_next[0]
                    r1_next[0] += 1
                    items.append((self.R1[:, j * 512:(j + 1) * 512], Buf("%s%d" % (name, i))))
                else:
                    t = self._talloc("a_" + name, [128, 512], BF16)
                    items.append((t[:], Buf("%s%d" % (name, i))))
            r = Rot.__new__(Rot)
            r.items = items
            r.i = 0
            return r

        def R(n, dt=BF16, k=2):
            r = self.trot("a_" + n, [128, 512], dt, k)
            r.items = [(t[:], b) for t, b in r.items]
            return r
        ofr, t5r = R("of", F32, 3), R("t5", F32, 2)
        zar, yar, fmo = R("za"), R("ya"), R("fmo")
        sta = self.trot("a_st", [128, 16], F32, 4)
        nmask, bnmask = self.tsb("a_nmask", [128, 14, 128], BF16)
        nmst, bnmst = self.wst.next()
        nmv = nmst[:].rearrange("p k n -> p (k n)")[:, 0:14 * 128].rearrange("p (a n) -> p a n", a=14)
        S.dma("sp", nmv, self.din["k_nm"].rearrange("a p n -> p a n"), writes=[bnmst])
        S.op("pool", lambda e: e.tensor_copy(out=nmask[:], in_=nmv), reads=[bnmst], writes=[bnmask])
        anw, banw = self.par["a_norm_w"]
        eps = self.epsb[:, 0:1]
        H = [slice(h * 128, (h + 1) * 128) for h in range(4)]
        v4 = lambda t: t[:].rearrange("p (h c) -> p h c", h=4)
        orders = [list(range(NT)), [1, 0] + list(range(NT - 1, 1, -1))]
        oa_written = set()
        from collections import deque
        free_banks = deque(range(8))

        def acq():
            while not free_banks:
                yield
            bk = free_banks.popleft()
            ps, bps = S.psum[bk]
            return ps, bps, bk

        def rel(bk):
            free_banks.append(bk)

        def mm4(lhs, blhs, rhs, brhs):
            ps, bps, bk = yield from acq()
            for h in range(4):
                S.op("pe", lambda e: e.matmul(out=ps[:, H[h]], lhsT=lhs[:, H[h]], rhs=rhs[:, H[h]], start=True, stop=True), reads=[blhs, brhs], writes=[bps])
            return ps, bps, bk

        def tr4(src, bsrc):
            ps, bps, bk = yield from acq()
            psb = ps[:].bitcast(BF16)
            for h in range(4):
                S.op("pe", lambda e: e.transpose(out=psb[:, H[h]], in_=src[:, H[h]], identity=self.ident_b[:]), reads=[bsrc, self.b_ident_b], writes=[bps])
            return psb, bps, bk

        class DirBufs:
            pass
        DB = []
        for d in range(2):
            o = DirBufs()
            for n in ("kT", "kTM", "vTM", "qT", "qk", "qd", "kt", "Pf"):
                setattr(o, n, mkrot("%s_%d" % (n, d), KPRE + 1))
            o.tsets = []
            for ts in range(KPRE):
                tsd = {n: mkrot("%s_%d_%d" % (n, d, ts), 1).items[0] for n in ("eg", "Ma", "Ml", "Pa", "Pb", "W1", "X", "MlmA", "MlmB")}
                for n in ("F0", "F1", "F2"):
                    tsd[n] = (self._talloc("a_%s_%d_%d" % (n, d, ts), [128, 512], F32)[:], Buf("%s_%d_%d" % (n, d, ts)))
                o.tsets.append(tsd)
            for n in ("Y", "vn", "sbf"):
                setattr(o, n, R("%s_%d" % (n, d)))
            o.S32 = self.tsb("a_S32_%d" % d, [128, 512])
            DB.append(o)

        def prep(d, cc, out, ts):
            B = DB[d]
            TS = B.tsets[ts]
            need_out = (cc >= 2) or (not last)
            out["need_out"] = need_out
            col = lambda name, h: A[name][0][:, cc, d * 4 + h:d * 4 + h + 1]
            tok = slice(cc * 128, (cc + 1) * 128)
            m_incl = self.amask[:, 2 * d, :]
            m_strict = self.amask[:, 2 * d + 1, :]
            nmb = lambda lev: nmask[:, d * 7 + lev, :].unsqueeze(1).to_broadcast([128, 4, 128])
            kT, bkT = B.kT.next()
            kTM, bkTM = B.kTM.next()
            vTM, bvTM = B.vTM.next()
            S.dma("sp", kT.rearrange("p (h t) -> p h t", h=4), scr["KA_FM"].rearrange("h p t -> p h t")[:, :, tok], reads=[self.db("KA_FM", cc)], writes=[bkT])
            S.dma("sp", kTM.rearrange("p (h c) -> p h c", h=4), scr["KA_TM"].rearrange("h t c -> t h c")[tok, :, :], reads=[self.db("KA_TM", cc)], writes=[bkTM])
            S.dma("sp", vTM.rearrange("p (h c) -> p h c", h=4), scr["VA_TM"].rearrange("h t c -> t h c")[tok, :, :], reads=[self.db("VA_TM", cc)], writes=[bvTM])
            out.update(kT=(kT, bkT), vTM=(vTM, bvTM))
            if need_out:
                qT, bqT = B.qT.next()
                S.dma("sp", qT.rearrange("p (h t) -> p h t", h=4), scr["QA_FM"].rearrange("h p t -> p h t")[:, :, tok], reads=[self.db("QA_FM", cc)], writes=[bqT])
            yield
            dg, bdg = TS["F0"]
            for h in range(4):
                S.op("dve", lambda e: e.tensor_scalar(out=dg[:, H[h]], in0=self.ident_f[:], scalar1=col("GC", h), scalar2=None, op0=ALU.mult),
                     reads=[self.b_ident_f, A["GC"][1]], writes=[bdg.sub(h)])
            p3, bp3, k3 = yield from acq()
            S.op("pe", lambda e: e.matmul(out=p3[:], lhsT=self.ones_f[:], rhs=dg[:], start=True, stop=True), reads=[self.b_ones_f, bdg], writes=[bp3])
            yield
            Dm2, bDm2 = TS["F1"]
            for h in range(4):
                S.op("dve", lambda e: e.scalar_tensor_tensor(out=Dm2[:, H[h]], in0=p3[:, H[h]], scalar=col("LNBMG", h), in1=m_strict, op0=ALU.add, op1=ALU.add),
                     reads=[bp3, A["LNBMG"][1], self.b_amask], writes=[bDm2.sub(h)])
            if need_out:
                Dm, bDm = TS["F2"]
                for h in range(4):
                    S.op("dve", lambda e: e.scalar_tensor_tensor(out=Dm[:, H[h]], in0=p3[:, H[h]], scalar=col("NEGG", h), in1=m_incl, op0=ALU.add, op1=ALU.add),
                         reads=[bp3, A["NEGG"][1], self.b_amask], writes=[bDm.sub(h)])
                eg, beg = TS["eg"]
                S.op("act", lambda e: e.activation(out=eg, in_=p3[:], func=AF.Exp), reads=[bp3, bDm, bDm2], writes=[beg])
            rel(k3)
            yield
            decb, bdecb = TS["F0"]
            S.op("act", lambda e: e.activation(out=decb[:], in_=Dm2[:], func=AF.Exp), reads=[bDm2], writes=[bdecb])
            p1, bp1, k1 = yield from mm4(kT, bkT, kT, bkT)
            yield
            Ma, bMa = TS["Ma"]
            S.op("dve", lambda e: e.tensor_tensor(out=Ma, in0=p1[:], in1=decb[:], op=ALU.mult), reads=[bp1, bdecb], writes=[bMa])
            rel(k1)
            yield
            psb, bps, kb = yield from tr4(Ma, bMa)
            nml = lambda lev: nmask[:, (1 - d) * 7 + lev, :].unsqueeze(1).to_broadcast([128, 4, 128])
            mlm = lambda lev: TS["MlmA" if lev % 2 else "MlmB"]
            S.op("dve", lambda e: e.tensor_tensor(out=v4(mlm(1)[0]), in0=psb[:, 0:512].rearrange("p (h c) -> p h c", h=4), in1=nml(1), op=ALU.mult),
                 reads=[bps, bnmask], writes=[mlm(1)[1]])
            Ml, bMl = TS["Ml"]
            S.op("act", lambda e: e.copy(out=Ml, in_=psb[:, 0:512]), reads=[bps, mlm(1)[1]], writes=[bMl])
            rel(kb)
            P, bP = TS["Pa"]
            S.op("pool", lambda e: e.tensor_tensor(out=v4(P), in0=v4(Ma), in1=nmb(0), op=ALU.mult), reads=[bMa, bnmask], writes=[bP])
            S.op("pool", lambda e: e.tensor_tensor(out=v4(P), in0=v4(P), in1=self.ident_b[:].unsqueeze(1).to_broadcast([128, 4, 128]), op=ALU.add),
                 reads=[bP, self.b_ident_b], writes=[bP])
            yield
            if need_out:
                dec, bdec = TS["F1"]
                S.op("act", lambda e: e.activation(out=dec[:], in_=Dm[:], func=AF.Exp), reads=[bDm], writes=[bdec])
                p2, bp2, k2 = yield from mm4(kT, bkT, qT, bqT)
                yield
                qk, bqk = B.qk.next()
                S.op("dve", lambda e: e.tensor_tensor(out=qk, in0=p2[:], in1=dec[:], op=ALU.mult), reads=[bp2, bdec], writes=[bqk])
                rel(k2)
                qd, bqd = B.qd.next()
                S.op("pool", lambda e: e.tensor_tensor(out=qd, in0=qT, in1=eg, op=ALU.mult), reads=[bqT, beg], writes=[bqd])
                out.update(qk=(qk, bqk), qd=(qd, bqd))
                yield
            kt, bkt = B.kt.next()
            for h in range(4):
                S.op("act", lambda e: e.activation(out=kt[:, H[h]], in_=kTM[:, H[h]], func=AF.Copy, scale=col("ETAIL", h)),
                     reads=[bkTM, A["ETAIL"][1]], writes=[bkt.sub(h)])
            out.update(kt=(kt, bkt))
            yield
            for lev in range(1, 7):
                cur, bcur = mlm(lev)
                psw, bpsw, kw = yield from mm4(cur, bcur, P, bP)
                psb, bps, kb = yield from tr4(P, bP)
                if lev < 6:
                    nxt_, bnxt_ = mlm(lev + 1)
                    S.op("pool", lambda e: e.tensor_tensor(out=v4(nxt_), in0=v4(Ml), in1=nml(lev + 1), op=ALU.mult), reads=[bMl, bnmask], writes=[bnxt_])
                yield
                W1, bW1 = TS["W1"]
                S.op("act", lambda e: e.copy(out=W1, in_=psw[:]), reads=[bpsw], writes=[bW1])
                rel(kw)
                X, bX = TS["X"]
                S.op("dve", lambda e: e.tensor_copy(out=X, in_=psb[:, 0:512]), reads=[bps], writes=[bX])
                rel(kb)
                yield
                ps2, bps2, k2 = yield from mm4(X, bX, W1, bW1)
                yield
                Pn, bPn = (B.Pf.next() if lev == 6 else TS["Pb" if lev % 2 == 1 else "Pa"])
                S.op("dve", lambda e: e.tensor_tensor(out=Pn, in0=ps2[:], in1=P, op=ALU.add), reads=[bps2, bP], writes=[bPn])
                rel(k2)
                P, bP = Pn, bPn
                yield
            out.update(P=(P, bP))

        def scan(d, cc, ops, st):
            B = DB[d]
            need_out = ops["need_out"]
            col = lambda name, h: A[name][0][:, cc, d * 4 + h:d * 4 + h + 1]
            tok = slice(cc * 128, (cc + 1) * 128)
            kT, bkT = ops["kT"]
            vTM, bvTM = ops["vTM"]
            kt, bkt = ops["kt"]
            P, bP = ops["P"]
            sbf, bsbf = st["sbf"]
            s32, bs32 = B.S32
            px, bpx, kx = yield from mm4(kT, bkT, sbf, bsbf)
            yield
            Y, bY = B.Y.next()
            for h in range(4):
                S.op("dve", lambda e: e.scalar_tensor_tensor(out=Y[:, H[h]], in0=px[:, H[h]], scalar=col("NEGEG", h), in1=vTM[:, H[h]], op0=ALU.mult, op1=ALU.add),
                     reads=[bpx, A["NEGEG"][1], bvTM], writes=[bY.sub(h)])
            rel(kx)
            yield
            pz, bpz, kz = yield from mm4(P, bP, Y, bY)
            yield
            vn, bvn = B.vn.next()
            for h in range(4):
                S.op("act", lambda e: e.activation(out=vn[:, H[h]], in_=pz[:, H[h]], func=AF.Copy, scale=col("BETA", h)), reads=[bpz, A["BETA"][1]], writes=[bvn.sub(h)])
            rel(kz)
            yield
            pS, bpS, kS = yield from mm4(kt, bkt, vn, bvn)
            if need_out:
                qk, bqk = ops["qk"]
                qd, bqd = ops["qd"]
                po, bpo, ko = yield from acq()
                for h in range(4):
                    S.op("pe", lambda e: e.matmul(out=po[:, H[h]], lhsT=qd[:, H[h]], rhs=sbf[:, H[h]], start=True, stop=False), reads=[bqd, bsbf], writes=[bpo])
                    S.op("pe", lambda e: e.matmul(out=po[:, H[h]], lhsT=qk[:, H[h]], rhs=vn[:, H[h]], start=False, stop=True), reads=[bqk, bvn], writes=[bpo])
            yield
            for h in range(4):
                S.op("dve", lambda e: e.scalar_tensor_tensor(out=s32[:, H[h]], in0=s32[:, H[h]], scalar=col("EGL", h), in1=pS[:, H[h]], op0=ALU.mult, op1=ALU.add),
                     reads=[bs32.sub(h), A["EGL"][1], bpS], writes=[bs32.sub(h)])
            rel(kS)
            sbf2, bsbf2 = B.sbf.next()
            S.op("act", lambda e: e.copy(out=sbf2, in_=s32[:]), reads=[bs32], writes=[bsbf2])
            st["sbf"] = (sbf2, bsbf2)
            yield
            if need_out:
                first = cc not in oa_written
                oa_written.add(cc)
                of, bof = ofr.next()
                if first:
                    S.op("act", lambda e: e.copy(out=of[:], in_=po[:]), reads=[bpo], writes=[bof])
                    rel(ko)
                    S.dma("act", scr["OA"][tok, :], of[:], reads=[bof], writes=[self.db("OA", cc)])
                    yield
                else:
                    S.dma("sp", of[:], scr["OA"][tok, :], reads=[self.db("OA", cc)], writes=[bof])
                    za, bza = zar.next()
                    S.dma("sp", za, scr["ZA"][tok, :], reads=[self.db("ZA", cc)], writes=[bza])
                    yield
                    S.op("dve", lambda e: e.tensor_tensor(out=of[:], in0=po[:], in1=of[:], op=ALU.add), reads=[bpo, bof], writes=[bof])
                    rel(ko)
                    t5, bt5 = t5r.next()
                    st_, bst = sta.next()
                    S.op("act", lambda e: e.activation(out=t5[:], in_=of[:], func=AF.Square), reads=[bof], writes=[bt5])
                    yield
                    S.op("dve", lambda e: e.tensor_reduce(out=st_[:, 0:4], in_=t5[:].rearrange("p (h c) -> p h c", h=4), axis=AX.X, op=ALU.add), reads=[bt5], writes=[bst])
                    S.op("act", lambda e: e.activation(out=st_[:, 4:8], in_=st_[:, 0:4], func=AF.Ln, scale=1.0 / 128, bias=eps), reads=[bst, self.b_epsb], writes=[bst])
                    S.op("act", lambda e: e.activation(out=st_[:, 8:12], in_=st_[:, 4:8], func=AF.Exp, scale=-0.5), reads=[bst], writes=[bst])
                    yield
                    ofv = of[:].rearrange("p (h c) -> p h c", h=4)
                    S.op("dve", lambda e: e.tensor_tensor(out=ofv, in0=ofv, in1=st_[:, 8:12].unsqueeze(2).to_broadcast([128, 4, 128]), op=ALU.mult), reads=[bof, bst], writes=[bof])
                    S.op("pool", lambda e: e.tensor_tensor(out=ofv, in0=ofv, in1=anw[:].unsqueeze(1).to_broadcast([128, 4, 128]), op=ALU.mult), reads=[bof, banw], writes=[bof])
                    yield
                    ya, bya = yar.next()
                    S.op("pool", lambda e: e.tensor_tensor(out=ya, in0=of[:], in1=za, op=ALU.mult), reads=[bof, bza], writes=[bya])
                    f, bf = fmo.next()
                    psb, bps, kb = yield from tr4(ya, bya)
                    S.op("act", lambda e: e.copy(out=f, in_=psb[:, 0:512]), reads=[bps], writes=[bf])
                    rel(kb)
                    S.dma("act", scr["Y_FM"][0].rearrange("(k p) t -> p k t", p=128)[:, :, tok], f.rearrange("p (k t) -> p k t", k=4),
                          reads=[bf], writes=[self.db("Y_FM0", cc)])
                    yield

        def chain(d):
            B = DB[d]
            s32, bs32 = B.S32
            S.op("pool", lambda e: e.memset(s32[:], 0.0), writes=[bs32])
            sbf, bsbf = B.sbf.next()
            S.op("pool", lambda e: e.memset(sbf, 0.0), writes=[bsbf])
            st = {"sbf": (sbf, bsbf)}
            order = orders[d]
            n = len(order)
            outs = [dict() for _ in range(n)]
            started = 0
            active = []
            done = set()

            def start_upto(j):
                nonlocal started
                while started <= min(j, n - 1):
                    active.append((started, prep(d, order[started], outs[started], started % KPRE)))
                    started += 1

            def step_preps():
                for item in list(active):
                    try:
                        next(item[1])
                    except StopIteration:
                        active.remove(item)
                        done.add(item[0])
            start_upto(0)
            while 0 not in done:
                step_preps()
                yield
            for i, cc in enumerate(order):
                start_upto(i + KPRE)
                sc = scan(d, cc, outs[i], st)
                sc_done = False
                while not sc_done or (i + 1 < n and (i + 1) not in done):
                    if not sc_done:
                        try:
                            next(sc)
                        except StopIteration:
                            sc_done = True
                    step_preps()
                    yield

        gens = [chain(0), chain(1)]
        while gens:
            for g_ in list(gens):
                try:
                    next(g_)
                except StopIteration:
                    gens.remove(g_)
        S.barrier()
        self.es.close()

    def phase5(self, l):
        S, scr, din = self.S, self.scr, self.din
        last = (l == self.nlayers - 1)
        self.es = ExitStack()
        wbr = self.R1[:, 14336:26624].rearrange("p (r n) -> p r n", n=1024)
        wo = self.R1[:, 26624:34816].rearrange("p (r n) -> p r n", n=1024)
        bwbr, bwo = Buf("wbr"), Buf("wo")

        def load_into(src_view, dst, bdst, nk):
            st, bst = self.wst.next()
            S.dma("sp", st[:, 0:nk, :], src_view, writes=[bst])
            S.op("pool", lambda e: e.tensor_copy(out=dst, in_=st[:, 0:nk, :]), reads=[bst], writes=[bdst])
        wbsrc = din["w_branch"][l].rearrange("b (k p) n -> p (b k) n", p=128)
        for half in range(2):
            for r0, nk in ((0, 8), (8, 4)):
                load_into(wbsrc[:, r0:r0 + nk, half * 512:(half + 1) * 512], wbr[:, r0:r0 + nk, half * 512:(half + 1) * 512], bwbr, nk)
        wosrc = din["w_out"][l].rearrange("(k p) n -> p k n", p=128)
        for half in range(2):
            load_into(wosrc[:, :, half * 512:(half + 1) * 512], wo[:, :, half * 512:(half + 1) * 512], bwo, 8)
        gate_bc, bgate = self.tsb("gate_bc", [128, 2, 1024])
        for s_ in range(2):
            if last and s_ == 1:
                continue
            self.bc_rows(lambda half: gate_bc[:, s_, half * 512:(half + 1) * 512], bgate, lambda kc: self.mod[:, 16 + kc, s_:s_ + 1], self.b_mod, 8)
        if last:
            fnw, bfnw = self.tsb("fnw_bc", [128, 1024])
            S.dma("sp", fnw[:], din["final_norm_w"].partition_broadcast(128), writes=[bfnw])
        yTr = self.trot("p5_yT", [128, 12, 512], BF16, 1)
        gmr = self.trot("p5_gm", [128, 512], BF16, 3)
        accr = self.trot("p5_acc", [128, 512], F32, 2)
        tmr = self.trot("p5_tm", [128, 512], F32, 2)
        mTr = self.trot("p5_mT", [128, 8, 512], BF16, 2)
        xtr = self.trot("p5_xt", [128, 1024], F32, 2)
        t1r = self.trot("p5_t1", [128, 1024], F32, 2)
        sqr, bsqr = self.tsb("p5_sq", [128, 1024])
        st5 = self.trot("p5_st", [128, 4], F32, 3)
        for (t0, n, tile0, ntile) in self.tok_groups():
            if last and t0 == 0:
                continue
            s_ = 1 if t0 == 0 else 0
            yT, byT = yTr.next()
            for br in range(3):
                S.dma("sp", yT[:, br * 4:(br + 1) * 4, 0:n], scr["Y_FM"][br].rearrange("(k p) t -> p k t", p=128)[:, :, t0:t0 + n],
                      reads=[self.db("Y_FM%d" % br, tile0 + i) for i in range(ntile)], writes=[byT.sub(br)])
            mT, bmT = mTr.next()
            for dt in range(8):
                acc, bacc = accr.next()
                for br in range(3):
                    ct = br * 8 + dt
                    gm, bgm = gmr.next()
                    S.dma("sp", gm[:, 0:n], scr["GM_FM"][ct * 128:(ct + 1) * 128, t0:t0 + n],
                          reads=[self.db("GM_FM%d" % ct, tile0 + i) for i in range(ntile)], writes=[bgm])
                    ps, bps = S.ps()
                    for kc in range(4):
                        S.op("pe", lambda e: e.matmul(out=ps[:, 0:n], lhsT=wbr[:, br * 4 + kc, dt * 128:(dt + 1) * 128], rhs=yT[:, br * 4 + kc, 0:n],
                                                      start=(kc == 0), stop=(kc == 3)), reads=[bwbr, byT.sub(br)], writes=[bps])
                    if br == 0:
                        S.op("dve", lambda e: e.tensor_tensor(out=acc[:, 0:n], in0=ps[:, 0:n], in1=gm[:, 0:n], op=ALU.mult), reads=[bps, bgm], writes=[bacc])
                    else:
                        tm, btm = tmr.next()
                        S.op("dve", lambda e: e.tensor_tensor(out=tm[:, 0:n], in0=ps[:, 0:n], in1=gm[:, 0:n], op=ALU.mult), reads=[bps, bgm], writes=[btm])
                        if br == 1:
                            S.op("pool", lambda e: e.tensor_tensor(out=acc[:, 0:n], in0=acc[:, 0:n], in1=tm[:, 0:n], op=ALU.add), reads=[bacc, btm], writes=[bacc])
                        else:
                            S.op("pool", lambda e: e.tensor_tensor(out=mT[:, dt, 0:n], in0=acc[:, 0:n], in1=tm[:, 0:n], op=ALU.add), reads=[bacc, btm], writes=[bmT.sub(dt)])
            for ti in range(ntile):
                tt = tile0 + ti
                xt, bxt = xtr.next()
                if tt < 2:
                    src = (din["ctx"] if l == 0 else scr["CTXS"])[tt * 128:(tt + 1) * 128, :]
                    rd = [] if l == 0 else [self.db("CTXS", tt)]
                else:
                    src = (din["x"] if l == 0 else scr["XS"])[(tt - 2) * 128:(tt - 1) * 128, :]
                    rd = [] if l == 0 else [self.db("XS", tt)]
                S.dma("sp", xt[:], src, reads=rd, writes=[bxt])
                t1, bt1 = t1r.next()
                for cg in range(2):
                    ps, bps = S.ps()
                    for kc in range(8):
                        S.op("pe", lambda e: e.matmul(out=ps[:], lhsT=mT[:, kc, ti * 128:(ti + 1) * 128], rhs=wo[:, kc, cg * 512:(cg + 1) * 512],
                                                      start=(kc == 0), stop=(kc == 7)), reads=[bmT, bwo], writes=[bps])
                    S.op("dve", lambda e: e.tensor_tensor(out=t1[:, cg * 512:(cg + 1) * 512], in0=ps[:], in1=gate_bc[:, s_, cg * 512:(cg + 1) * 512], op=ALU.mult),
                         reads=[bps, bgate], writes=[bt1.sub(cg)])
                S.op("pool", lambda e: e.tensor_tensor(out=t1[:], in0=t1[:], in1=xt[:], op=ALU.add), reads=[bt1, bxt], writes=[bt1])
                if not last:
                    if tt < 2:
                        S.dma("act", scr["CTXS"][tt * 128:(tt + 1) * 128, :], t1[:], reads=[bt1], writes=[self.db("CTXS", tt)])
                    else:
                        S.dma("act", scr["XS"][(tt - 2) * 128:(tt - 1) * 128, :], t1[:], reads=[bt1], writes=[self.db("XS", tt)])
                else:
                    st, bst = st5.next()
                    S.op("act", lambda e: e.activation(out=sqr[:], in_=t1[:], func=AF.Square, accum_out=st[:, 0:1]), reads=[bt1], writes=[bsqr, bst])
                    S.op("dve", lambda e: e.tensor_scalar(out=st[:, 1:2], in0=st[:, 0:1], scalar1=1.0 / D, scalar2=EPS, op0=ALU.mult, op1=ALU.add), reads=[bst], writes=[bst])
                    S.op("act", lambda e: e.activation(out=st[:, 2:3], in_=st[:, 1:2], func=AF.Ln), reads=[bst], writes=[bst])
                    S.op("act", lambda e: e.activation(out=st[:, 3:4], in_=st[:, 2:3], func=AF.Exp, scale=-0.5), reads=[bst], writes=[bst])
                    S.op("dve", lambda e: e.scalar_tensor_tensor(out=xt[:], in0=t1[:], scalar=st[:, 3:4], in1=fnw[:], op0=ALU.mult, op1=ALU.mult),
                         reads=[bt1, bst, bfnw, bxt], writes=[bxt])
                    S.dma("act", self.out[(tt - 2) * 128:(tt - 1) * 128, :], xt[:], reads=[bxt], writes=[self.db("OUT", tt)])
        S.barrier()
        self.es.close()

    def dump(self, name, ap, reads, shape, dtype=F32):
        o = self.nc.dram_tensor("dbg_" + name, shape, dtype, kind="ExternalOutput").ap()
        b = Buf("dbg_" + name)
        self.S.dma("sp", o, ap, reads=reads, writes=[b])
        self._dbgbufs.append(b)

    def dump_p2(self):
        self.dump("mod", self.mod[:], [self.b_mod], [128, 24, 2])
        self.dump("SCR", self.SCR[:], [self.b_SCR], [128, NT, 16])
        for n in self.asc:
            self.dump(n, self.asc[n][0][:], [self.asc[n][1]], [128, NT, 8])
        self.dump("SH2", self.SH2[:], [self.b_SH2], [128, NT, 8])
        self.dump("kmx", self.kmx[:], [self.b_kmx], [128, 4])
        self.dump("hT", self.hT, self.b_hT, [128, 8, NTOK], BF16)

    def program(self):
        S = self.S
        self._dbgbufs = []
        self.marks = []
        mark = lambda n: self.marks.append((n, {k: v.count for k, v in S.engs.items()}))
        for l in range(self.nlayers):
            mark("L%d start" % l)
            self.phase0(l)
            mark("L%d p0 done" % l)
            if self.stop == "p0":
                self.dump("mod", self.mod[:], [self.b_mod], [128, 24, 2])
                self.dump("Afm", self.Afm[:], [self.b_Afm], [128, 8, 2])
                self.dump("convw", self.convw[:], [self.b_convw], [128, 12, 5])
                self.dump("scol", self.scol[:], [self.b_scol], [128, 8, 2])
                break
            self.phase1(l)
            mark("L%d p1 done" % l)
            if self.stop == "p1":
                self.dump("hT", self.hT, self.b_hT, [128, 8, NTOK], BF16)
                break
            self.phase2(l)
            if self.stop is not None and self.stop.startswith("p2"):
                break
            mark("L%d p2 done" % l)
            if self.stop == "b":
                self.core_b(l)
                break
            self.core_bc(l)
            mark("L%d B done" % l)
            mark("L%d C done" % l)
            if self.stop == "c":
                break
            self.core_a(l)
            mark("L%d A done" % l)
            if self.stop == "a":
                break
            self.phase5(l)
            mark("L%d p5 done" % l)
            if self.stop == "p5":
                break
        S.barrier()
        return self.nc


def shard_inputs(inputs, b):
    m = {}
    for n in IN_SHAPES:
        a = np.asarray(inputs[n], dtype=np.float32)
        if n in ("x", "c", "ctx"):
            a = a[b]
        m[n] = np.ascontiguousarray(a)
    return m


_CACHE = {}


def kernel(**inputs):
    if "nc" not in _CACHE:
        _CACHE["nc"] = MK().program()
        _CACHE["consts"] = host_consts()
    nc = _CACHE["nc"]
    in_maps = []
    for b in range(8):
        m = shard_inputs(inputs, b)
        m.update(_CACHE["consts"])
        in_maps.append(m)
    res = run_bass_kernel_spmd(nc, in_maps, core_ids=list(range(8)))
    return np.stack([np.asarray(r["out"], dtype=np.float32) for r in res.results], axis=0)
```

```python
from contextlib import ExitStack
import numpy as np
import concourse.bass as bass
import concourse.mybir as mybir
from concourse.bass_utils import run_bass_kernel_spmd

F32 = mybir.dt.float32
BF16 = mybir.dt.bfloat16
AF = mybir.ActivationFunctionType
ALU = mybir.AluOpType
AX = mybir.AxisListType

T = 4096
LC = 256
D = 1024
NT = 34
NTOK = 4352
INW = 8464
CH = 128
EPS = 1e-6
NEG = -1.0e5
O_AQ, O_AK, O_AV, O_AZ, O_AB, O_BQ, O_BKV, O_BZ, O_CQ, O_CK, O_CV, O_CZ, O_MG = (
    0, 512, 1024, 1536, 2048, 2064, 2576, 2832, 3344, 3856, 4368, 4880, 5392)


class Buf:
    __slots__ = ("name", "w", "r", "parts")

    def __init__(self, name):
        self.name = name
        self.w = None
        self.r = {}
        self.parts = {}

    def sub(self, p):
        return Sub(self, p)


class Sub:
    __slots__ = ("parent", "p", "name")

    def __init__(self, parent, p):
        self.parent = parent
        self.p = p
        self.name = "%s[%s]" % (parent.name, p)

    def _slot(self):
        return self.parent.parts.setdefault(self.p, [None, {}])


class Eng:
    def __init__(self, key, e, sem):
        self.key = key
        self.e = e
        self.sem = sem
        self.count = 0
        self.waited = {}


class Sched:
    def __init__(self, nc, n_dma_sems=40):
        self.nc = nc
        self.sems = {}
        self.engs = {}
        for key, e in (("pe", nc.tensor), ("act", nc.scalar), ("dve", nc.vector), ("pool", nc.gpsimd), ("sp", nc.sync)):
            s = nc.alloc_semaphore("sem_" + key)
            self.sems[key] = s
            self.engs[key] = Eng(key, e, s)
        self.dma_sems = []
        for i in range(n_dma_sems):
            k = "dma%d" % i
            self.sems[k] = nc.alloc_semaphore("sem_" + k)
            self.dma_sems.append([k, 0])
        self.dma_rr = 0
        self.nops = 0
        self.clocks = {}
        self.psum = []
        self.ps_rr = 0
        for i in range(8):
            self.psum.append((nc.alloc_psum_tensor("psb%d" % i, [128, 512], F32), Buf("psb%d" % i)))

    def ps(self, pool=None):
        if pool is not None:
            banks, st = pool
            r = self.psum[banks[st[0] % len(banks)]]
            st[0] += 1
            return r
        r = self.psum[self.ps_rr]
        self.ps_rr = (self.ps_rr + 1) % 8
        return r

    def _deps(self, eng, reads, writes, is_dma):
        deps = {}

        def add(tok, same_ok):
            if tok is None:
                return
            k, v = tok
            if k == eng.key and not same_ok:
                return
            if deps.get(k, 0) < v:
                deps[k] = v

        same = is_dma or eng.key != "pe"
        for b in reads:
            if isinstance(b, Sub):
                add(b.parent.w, True)
                add(b._slot()[0], True)
            else:
                add(b.w, True)
                for pw, pr in b.parts.values():
                    add(pw, True)
        for b in writes:
            if isinstance(b, Sub):
                add(b.parent.w, same)
                for k, v in b.parent.r.items():
                    add((k, v), same)
                pw, pr = b._slot()
                add(pw, same)
                for k, v in pr.items():
                    add((k, v), same)
            else:
                add(b.w, same)
                for k, v in b.r.items():
                    add((k, v), same)
                for pw, pr in b.parts.values():
                    add(pw, same)
                    for k, v in pr.items():
                        add((k, v), same)
        for k, v in sorted(deps.items(), key=lambda kv: -kv[1]):
            self._need(eng, k, v)

    def _record(self, key, val, reads, writes):
        for b in reads:
            r = b._slot()[1] if isinstance(b, Sub) else b.r
            if r.get(key, 0) < val:
                r[key] = val
        for b in writes:
            if isinstance(b, Sub):
                sl = b._slot()
                sl[0] = (key, val)
                sl[1] = {}
            else:
                b.w = (key, val)
                b.r = {}
                b.parts = {}

    def _need(self, eng, k, v):
        if eng.waited.get(k, 0) >= v:
            return
        eng.e.wait_ge(self.sems[k], v)
        eng.waited[k] = v
        clk = self.clocks.get((k, v))
        if clk:
            w = eng.waited
            for k2, v2 in clk.items():
                if w.get(k2, 0) < v2:
                    w[k2] = v2

    def op(self, ek, fn, reads=(), writes=()):
        eng = self.engs[ek]
        self._deps(eng, reads, writes, False)
        ins = fn(eng.e)
        self.nops += 1
        eng.count += 1
        ins.then_inc(eng.sem, 1)
        clk = dict(eng.waited)
        clk.pop(eng.key, None)
        self.clocks[(eng.key, eng.count)] = clk
        self._record(eng.key, eng.count, reads, writes)
        return ins

    def dma(self, ek, out, in_, reads=(), writes=(), **kw):
        eng = self.engs[ek]
        self._deps(eng, reads, writes, True)
        slot = self.dma_sems[self.dma_rr]
        self.dma_rr = (self.dma_rr + 1) % len(self.dma_sems)
        k, uses = slot
        if uses > 0:
            self._need(eng, k, 16 * uses)
        ins = eng.e.dma_start(out=out, in_=in_, **kw)
        self.nops += 1
        slot[1] = uses + 1
        val = 16 * (uses + 1)
        ins.then_inc(self.sems[k], 16)
        clk = dict(eng.waited)
        clk.pop(eng.key, None)
        self.clocks[(k, val)] = clk
        self._record(k, val, reads, writes)
        return ins

    def wait_all(self, ek, bufs):
        eng = self.engs[ek]
        for b in bufs:
            toks = []
            if b.w is not None:
                toks.append(b.w)
            toks.extend(b.r.items())
            for pw, pr in b.parts.values():
                if pw is not None:
                    toks.append(pw)
                toks.extend(pr.items())
            for k, v in toks:
                if eng.waited.get(k, 0) < v:
                    eng.e.wait_ge(self.sems[k], v)
                    eng.waited[k] = v

    def barrier(self):
        for eng in self.engs.values():
            for o in self.engs.values():
                if o.key != eng.key and o.count > 0 and eng.waited.get(o.key, 0) < o.count:
                    eng.e.wait_ge(self.sems[o.key], o.count)
                    eng.waited[o.key] = o.count
            for k, uses in self.dma_sems:
                if uses > 0 and eng.waited.get(k, 0) < 16 * uses:
                    eng.e.wait_ge(self.sems[k], 16 * uses)
                    eng.waited[k] = 16 * uses


class Rot:
    def __init__(self, alloc, name, shape, dtype, n=2):
        self.items = [(alloc("%s%d" % (name, i), shape, dtype), Buf("%s%d" % (name, i))) for i in range(n)]
        self.i = 0

    def next(self):
        r = self.items[self.i]
        self.i = (self.i + 1) % len(self.items)
        return r


def host_consts():
    f = np.float32
    j = np.arange(128)[:, None]
    i = np.arange(128)[None, :]
    c = {}
    c["k_ident"] = np.eye(128, dtype=f)
    c["k_ones"] = np.ones((128, 128), f)
    am = np.zeros((4, 128, 128), f)
    am[0] = np.where(i >= j, 0.0, NEG)
    am[1] = np.where(i > j, 0.0, NEG)
    am[2] = np.where(i <= j, 0.0, NEG)
    am[3] = np.where(i < j, 0.0, NEG)
    c["k_amask"] = am
    tri = np.zeros((2, 128, 128), f)
    tri[0] = (j <= i)
    tri[1] = (j >= i)
    c["k_tri"] = tri
    bm = np.zeros((2, 128, 512), f)
    bm[0] = np.tile((j >= i).astype(f), (1, 4))
    bm[1] = np.tile((j <= i).astype(f), (1, 4))
    c["k_bmask"] = bm
    cm = np.zeros((6, 128, 128), f)
    cm[0] = np.maximum(i - j, 0)
    cm[1] = np.maximum(j - i, 0)
    cm[2] = (i > j)
    cm[3] = (j > i)
    cm[4] = np.broadcast_to(i + 1, (128, 128))
    cm[5] = np.broadcast_to(CH - i, (128, 128))
    c["k_cm"] = cm
    nm = np.zeros((14, 128, 128), f)
    for d in range(2):
        for lev in range(7):
            b = 1 << lev
            same = (j // (2 * b)) == (i // (2 * b))
            if d == 0:
                m = same & ((j % (2 * b)) < b) & ((i % (2 * b)) >= b)
            else:
                m = same & ((i % (2 * b)) < b) & ((j % (2 * b)) >= b)
            nm[d * 7 + lev] = -m.astype(f)
    c["k_nm"] = nm
    cj = np.zeros((128, 8), f)
    cj[:, 0:4] = (CH - 1 - np.arange(128))[:, None]
    cj[:, 4:8] = np.arange(128)[:, None]
    c["k_cj"] = cj
    t = np.arange(T)
    inv16 = (f(10000.0) ** (-np.arange(16, dtype=f) / f(16))).astype(f)
    ar = (t // 64).astype(f)[:, None] * inv16[None, :]
    ac = (t % 64).astype(f)[:, None] * inv16[None, :]
    ab = np.concatenate([ar, ac], axis=1).astype(f)
    c["k_cosb"] = np.tile(np.cos(ab).astype(f), (1, 8))
    c["k_sinb"] = np.tile(np.sin(ab).astype(f), (1, 8))
    inv64 = (f(10000.0) ** (-np.arange(64, dtype=f) / f(64))).astype(f)
    ang = t.astype(f)[:, None] * inv64[None, :]
    c["k_cosc"] = np.cos(ang).astype(f)
    c["k_sinc"] = np.sin(ang).astype(f)
    return c


CONST_SHAPES = {"k_ident": [128, 128], "k_ones": [128, 128], "k_amask": [4, 128, 128], "k_tri": [2, 128, 128],
                "k_bmask": [2, 128, 512], "k_cm": [6, 128, 128], "k_cj": [128, 8], "k_nm": [14, 128, 128],
                "k_cosb": [T, 256], "k_sinb": [T, 256], "k_cosc": [T, 64], "k_sinc": [T, 64]}

IN_SHAPES = {"x": [T, D], "c": [D], "ctx": [LC, D], "c_ctx": [D], "w_ada": [2, D, 3 * D], "b_ada": [2, 3 * D],
             "norm_w": [2, D], "w_in": [2, D, INW], "a_conv_w": [2, 5, 1536], "a_log": [2, 8], "a_dt_bias": [2, 8],
             "a_norm_w": [2, 128], "b_sink": [2, 8], "c_decay": [2, 8], "c_norm_w": [2, 512],
             "w_branch": [2, 3, 512, D], "w_out": [2, D, D], "final_norm_w": [D]}

SCRATCH = {"XS": ([T, D], F32), "CTXS": ([LC, D], F32),
           "QA_FM": ([4, 128, NTOK], BF16), "KA_FM": ([4, 128, NTOK], BF16),
           "KA_TM": ([4, NTOK, 128], BF16), "VA_TM": ([4, NTOK, 128], BF16),
           "ZA": ([NTOK, 512], BF16), "ZB": ([NTOK, 512], BF16), "ZC": ([NTOK, 512], BF16),
           "OA": ([NTOK, 512], F32),
           "QB_FM": ([8, 128, NTOK], BF16),
           "QC_FM": ([4, 128, NTOK], BF16), "KC_FM": ([4, 128, NTOK], BF16),
           "KC_TM": ([NTOK, 512], BF16), "VC_TM": ([NTOK, 512], BF16),
           "SCB": ([NT, 128, 512], BF16), "SCF": ([NT, 128, 512], BF16),
           "KB_FM": ([2, 128, NTOK], BF16), "VB_TM": ([NTOK, 2, 65], BF16),
           "GM_FM": ([3 * D, NTOK], BF16), "Y_FM": ([3, 512, NTOK], BF16)}


class MK:
    def __init__(self, nlayers=2, dbg=(), stop=None):
        nc = bass.Bass("TRN2", target_bir_lowering=False, dynamic_dma_scratch_size=1024)
        self.nc = nc
        self.S = Sched(nc)
        self.nlayers = nlayers
        self.stop = stop
        self.din = {}
        for n, shp in list(IN_SHAPES.items()) + list(CONST_SHAPES.items()):
            self.din[n] = nc.dram_tensor(n, shp, F32, kind="ExternalInput").ap()
        self.out = nc.dram_tensor("out", [T, D], F32, kind="ExternalOutput").ap()
        self.scr = {}
        self._db = {}
        for n, (shp, dt) in SCRATCH.items():
            kind = "ExternalOutput" if n in dbg else "Internal"
            self.scr[n] = nc.dram_tensor(n, shp, dt, kind=kind).ap()
        self.dbg = dbg
        self._tcount = 0
        self.alloc()

    def db(self, name, tt):
        k = (name, tt)
        if k not in self._db:
            self._db[k] = Buf("%s_%d" % k)
        return self._db[k]

    def dball(self, name):
        return [self.db(name, tt) for tt in range(NT)]

    def sb(self, name, shape, dtype=F32):
        return self.nc.alloc_sbuf_tensor(name, shape, dtype), Buf(name)

    def rot(self, name, shape, dtype=F32, n=2):
        return Rot(lambda nm, sh, dt: self.nc.alloc_sbuf_tensor(nm, sh, dt), name, shape, dtype, n)

    def _talloc(self, name, shape, dtype):
        self._tcount += 1
        return self.es.enter_context(self.nc.sbuf_tensor("%s_t%d" % (name, self._tcount), shape, dtype))

    def tsb(self, name, shape, dtype=F32):
        return self._talloc(name, shape, dtype), Buf(name)

    def trot(self, name, shape, dtype=F32, n=2):
        return Rot(self._talloc, name, shape, dtype, n)

    def alloc(self):
        nc, S, din = self.nc, self.S, self.din
        self.ident_f, self.b_ident_f = self.sb("ident_f", [128, 128])
        self.ones_f, self.b_ones_f = self.sb("ones_f", [128, 128])
        self.ident_b, self.b_ident_b = self.sb("ident_b", [128, 128], BF16)
        self.ones_b, self.b_ones_b = self.sb("ones_b", [128, 128], BF16)
        self.amask, self.b_amask = self.sb("amask", [128, 4, 128])
        self.tri, self.b_tri = self.sb("tri", [128, 2, 128])
        self.bmask, self.b_bmask = self.sb("bmask", [128, 2, 512], BF16)
        self.cm, self.b_cm = self.sb("cm", [128, 6, 128])
        self.cj, self.b_cj = self.sb("cj", [128, 8])
        S.dma("sp", self.ident_f[:], din["k_ident"], writes=[self.b_ident_f])
        S.dma("sp", self.ones_f[:], din["k_ones"], writes=[self.b_ones_f])
        S.dma("sp", self.amask[:], din["k_amask"].rearrange("a p n -> p a n"), writes=[self.b_amask])
        S.dma("sp", self.tri[:], din["k_tri"].rearrange("a p n -> p a n"), writes=[self.b_tri])
        S.dma("sp", self.cm[:], din["k_cm"].rearrange("a p n -> p a n"), writes=[self.b_cm])
        S.dma("sp", self.cj[:], din["k_cj"], writes=[self.b_cj])
        S.op("dve", lambda e: e.tensor_copy(out=self.ident_b[:], in_=self.ident_f[:]), reads=[self.b_ident_f], writes=[self.b_ident_b])
        S.op("dve", lambda e: e.tensor_copy(out=self.ones_b[:], in_=self.ones_f[:]), reads=[self.b_ones_f], writes=[self.b_ones_b])
        self.R1, self.b_R1 = self.sb("R1", [128, 8 * NTOK], BF16)
        self.hT = self.R1[:].rearrange("p (k t) -> p k t", k=8)
        self.b_hT = [Buf("hT%d" % i) for i in range(NT)]
        self.wst = self.rot("wst", [128, 8, 512], F32, 1)
        self.wb = self.rot("wb", [128, 8, 512], BF16, 4)
        self.scol, self.b_scol = self.sb("scol", [128, 8, 2])
        self.mod, self.b_mod = self.sb("mod", [128, 24, 2])
        self.nwcol, self.b_nwcol = self.sb("nwcol", [128, 8])
        self.badacol, self.b_badacol = self.sb("badacol", [128, 24])
        self.Afm, self.b_Afm = self.sb("Afm", [128, 8, 2])
        self.convw, self.b_convw = self.sb("convw", [128, 12, 5])
        self.rowtmp = self.rot("rowtmp", [128, 128], F32, 2)
        for it in self.rowtmp.items:
            S.op("pool", lambda e: e.memset(it[0][:], 0.0), writes=[it[1]])
        self.gdiag = self.rot("gdiag", [128, 512], F32, 2)
        self.SCR, self.b_SCR = self.sb("SCR", [128, NT, 16])
        self.asc = {}
        for n in ("BETA", "GC", "NEGG", "LNBMG", "NEGEG", "ETAIL", "EGL"):
            self.asc[n] = self.sb("asc_" + n, [128, NT, 8])
        self.SH2, self.b_SH2 = self.sb("SH2", [128, NT, 8])
        self.kmx, self.b_kmx = self.sb("kmx", [128, 4])
        self.par = {}
        for n, w in (("a_log", 8), ("a_dt_bias", 8), ("b_sink", 8), ("c_decay", 8), ("a_norm_w", 128), ("c_norm_w", 512)):
            self.par[n] = self.sb("par_" + n, [128, w])
        self.epsb, self.b_epsb = self.sb("epsb", [128, 4])
        for col, val in ((0, EPS), (1, -0.5 * float(np.log(128.0))), (2, 0.0), (3, 1.0)):
            S.op("pool", lambda e: e.memset(self.epsb[:, col:col + 1], val), writes=[self.b_epsb])

    def load_cols(self, src_rows, n, dst, bdst, func=None):
        S = self.S
        rt, brt = self.rowtmp.next()
        S.dma("sp", rt[0:n, :], src_rows, writes=[brt])
        if func is not None:
            S.op("act", lambda e: e.activation(out=rt[0:n, :], in_=rt[0:n, :], func=func), reads=[brt], writes=[brt])
        ps, bps = S.ps()
        S.op("pe", lambda e: e.transpose(out=ps[:, 0:128], in_=rt[:, :], identity=self.ident_f[:]),
             reads=[brt, self.b_ident_f], writes=[bps])
        S.op("dve", lambda e: e.tensor_copy(out=dst, in_=ps[:, 0:n]), reads=[bps], writes=[bdst])

    def w_plan(self, src2d, groups):
        self._wsrc = src2d
        self._wgroups = list(groups)
        self._wi = 0
        self._wq = []
        self._w_issue()

    def _w_issue(self):
        if self._wi < len(self._wgroups):
            c0, ncols = self._wgroups[self._wi]
            self._wi += 1
            self._wq.append(((c0, ncols), self._load_w_raw(self._wsrc, c0, ncols)))

    def load_w(self, src2d, c0, ncols, nk=8):
        if getattr(self, "_wq", None):
            key, val = self._wq.pop(0)
            assert key == (c0, ncols), (key, c0, ncols)
            self._w_issue()
            return val
        return self._load_w_raw(src2d, c0, ncols, nk)

    def _load_w_raw(self, src2d, c0, ncols, nk=8):
        S = self.S
        st, bst = self.wst.next()
        wb, bwb = self.wb.next()
        S.dma("sp", st[:, 0:nk, 0:ncols], src2d.rearrange("(k p) n -> p k n", p=128)[:, :, c0:c0 + ncols], writes=[bst])
        S.op("pool", lambda e: e.tensor_copy(out=wb[:, 0:nk, 0:ncols], in_=st[:, 0:nk, 0:ncols]), reads=[bst], writes=[bwb])
        return wb, bwb

    def phase0(self, l):
        S, din = self.S, self.din
        self.es = ExitStack()
        for n in self.par:
            t, b = self.par[n]
            S.dma("sp", t[:], din[n][l].partition_broadcast(128), writes=[b])
        self.load_cols(din["c"].rearrange("(k p) -> k p", p=128), 8, self.scol[:, :, 0], self.b_scol, AF.Silu)
        self.load_cols(din["c_ctx"].rearrange("(k p) -> k p", p=128), 8, self.scol[:, :, 1], self.b_scol, AF.Silu)
        self.load_cols(din["norm_w"][l].rearrange("(k p) -> k p", p=128), 8, self.nwcol[:], self.b_nwcol)
        self.load_cols(din["b_ada"][l].rearrange("(k p) -> k p", p=128), 24, self.badacol[:], self.b_badacol)
        cw, bcw = self.tsb("cwrows", [128, 1536])
        S.op("pool", lambda e: e.memset(cw[:], 0.0), writes=[bcw])
        S.dma("sp", cw[0:5, :], din["a_conv_w"][l], writes=[bcw])
        for ct in range(12):
            ps, bps = S.ps()
            S.op("pe", lambda e: e.transpose(out=ps[:, 0:128], in_=cw[:, ct * 128:(ct + 1) * 128], identity=self.ident_f[:]),
                 reads=[bcw, self.b_ident_f], writes=[bps])
            S.op("dve", lambda e: e.tensor_copy(out=self.convw[:, ct, :], in_=ps[:, 0:5]), reads=[bps], writes=[self.b_convw])
        psm, bpsm = S.ps()
        for g in range(6):
            st, bst = self.wst.next()
            S.dma("sp", st[:], din["w_ada"][l].rearrange("(k p) n -> p k n", p=128)[:, :, g * 512:(g + 1) * 512], writes=[bst])
            for jl in range(4):
                j = g * 4 + jl
                for kc in range(8):
                    S.op("pe", lambda e: e.matmul(out=psm[:, 2 * j:2 * j + 2], lhsT=st[:, kc, jl * 128:(jl + 1) * 128], rhs=self.scol[:, kc, :],
                                                  start=(kc == 0), stop=(kc == 7)),
                         reads=[bst, self.b_scol], writes=[bpsm])
        S.op("dve", lambda e: e.tensor_tensor(out=self.mod[:], in0=psm[:, 0:48].rearrange("p (j s) -> p j s", s=2),
                                              in1=self.badacol[:].unsqueeze(2).to_broadcast([128, 24, 2]), op=ALU.add),
             reads=[bpsm, self.b_badacol], writes=[self.b_mod])
        S.op("dve", lambda e: e.scalar_tensor_tensor(out=self.Afm[:], in0=self.mod[:, 8:16, :], scalar=1.0,
                                                     in1=self.nwcol[:].unsqueeze(2).to_broadcast([128, 8, 2]), op0=ALU.add, op1=ALU.mult),
             reads=[self.b_mod, self.b_nwcol], writes=[self.b_Afm])
        S.barrier()
        self.es.close()

    def bc_rows(self, dst_fn, bdst, col_fn, bcol, nchunks):
        S = self.S
        for half in range(nchunks // 4):
            t, bt = self.gdiag.next()
            for q in range(4):
                kc = half * 4 + q
                S.op("dve", lambda e: e.tensor_scalar(out=t[:, q * 128:(q + 1) * 128], in0=self.ident_f[:], scalar1=col_fn(kc),
                                                      scalar2=None, op0=ALU.mult),
                     reads=[self.b_ident_f, bcol], writes=[bt])
            ps, bps = S.ps()
            S.op("pe", lambda e: e.matmul(out=ps[:], lhsT=self.ones_f[:], rhs=t[:], start=True, stop=True),
                 reads=[self.b_ones_f, bt], writes=[bps])
            S.op("act", lambda e: e.copy(out=dst_fn(half), in_=ps[:]), reads=[bps], writes=[bdst])

    def phase1(self, l):
        S, din = self.S, self.din
        self.es = ExitStack()
        self.p1_xt = self.trot("p1_xt", [128, 1024], F32, 2)
        self.p1_sq, self.b_p1_sq = self.tsb("p1_sq", [128, 1024])
        self.p1_st = self.trot("p1_st", [128, 4], F32, 2)
        for tt in range(NT):
            s = 1 if tt < 2 else 0
            if tt < 2:
                src = (din["ctx"] if l == 0 else self.scr["CTXS"])[tt * 128:(tt + 1) * 128, :]
                rd = [] if l == 0 else [self.db("CTXS", tt)]
            else:
                src = (din["x"] if l == 0 else self.scr["XS"])[(tt - 2) * 128:(tt - 1) * 128, :]
                rd = [] if l == 0 else [self.db("XS", tt)]
            xt, bxt = self.p1_xt.next()
            st, bst = self.p1_st.next()
            S.dma("sp", xt[:], src, reads=rd, writes=[bxt])
            S.op("act", lambda e: e.activation(out=self.p1_sq[:], in_=xt[:], func=AF.Square, accum_out=st[:, 0:1]),
                 reads=[bxt], writes=[self.b_p1_sq, bst])
            S.op("dve", lambda e: e.tensor_scalar(out=st[:, 1:2], in0=st[:, 0:1], scalar1=1.0 / D, scalar2=EPS, op0=ALU.mult, op1=ALU.add),
                 reads=[bst], writes=[bst])
            S.op("act", lambda e: e.activation(out=st[:, 2:3], in_=st[:, 1:2], func=AF.Ln), reads=[bst], writes=[bst])
            S.op("act", lambda e: e.activation(out=st[:, 3:4], in_=st[:, 2:3], func=AF.Exp, scale=-0.5), reads=[bst], writes=[bst])
            S.op("act", lambda e: e.activation(out=xt[:], in_=xt[:], func=AF.Copy, scale=st[:, 3:4]),
                 reads=[bst, bxt], writes=[bxt])
            for half in range(2):
                ps, bps = S.ps()
                for q in range(4):
                    kc = half * 4 + q
                    S.op("pe", lambda e: e.transpose(out=ps[:, q * 128:(q + 1) * 128], in_=xt[:, kc * 128:(kc + 1) * 128], identity=self.ident_f[:]),
                         reads=[bxt, self.b_ident_f], writes=[bps])
                for q in range(4):
                    kc = half * 4 + q
                    dst = self.hT[:, kc, tt * 128:(tt + 1) * 128]
                    if q % 2 == 0:
                        S.op("dve", lambda e: e.tensor_scalar(out=dst, in0=ps[:, q * 128:(q + 1) * 128], scalar1=self.Afm[:, kc, s:s + 1],
                                                              scalar2=self.mod[:, kc, s:s + 1], op0=ALU.mult, op1=ALU.add),
                             reads=[bps, self.b_Afm, self.b_mod], writes=[self.b_hT[tt]])
                    else:
                        S.op("act", lambda e: e.activation(out=dst, in_=ps[:, q * 128:(q + 1) * 128], func=AF.Identity,
                                                           scale=self.Afm[:, kc, s:s + 1], bias=self.mod[:, kc, s:s + 1]),
                             reads=[bps, self.b_Afm, self.b_mod], writes=[self.b_hT[tt]])
        S.barrier()
        self.es.close()

    def tok_groups(self):
        g = [(0, 256, 0, 2)]
        for i in range(8):
            g.append((256 + i * 512, 512, 2 + 4 * i, 4))
        return g

    class WStream:
        def __init__(self, mk, src2d, groups, bufs):
            self.mk, self.src, self.groups, self.bufs = mk, src2d, list(groups), bufs
            self.i = 0
            self.q = []
            self._issue()

        def _issue(self):
            if self.i < len(self.groups):
                c0, ncols = self.groups[self.i]
                wb, bwb = self.bufs[self.i % len(self.bufs)]
                S = self.mk.S
                st, bst = self.mk.wst.next()
                S.dma("sp", st[:, :, 0:ncols], self.src.rearrange("(k p) n -> p k n", p=128)[:, :, c0:c0 + ncols], writes=[bst])
                S.op("pool", lambda e: e.tensor_copy(out=wb[:, :, 0:ncols], in_=st[:, :, 0:ncols]), reads=[bst], writes=[bwb])
                self.q.append(((c0, ncols), (wb, bwb)))
                self.i += 1

        def get(self, c0, ncols):
            key, val = self.q.pop(0)
            assert key == (c0, ncols), (key, c0, ncols)
            self._issue()
            return val

    def proj_tm_gen(self, l, c0, ncols, handler, ws):
        S = self.S
        if hasattr(self, "marks"):
            self.marks.append(("  L%d tm@%d" % (l, c0), {k: v.count for k, v in S.engs.items()}))
        wb, bwb = ws.get(c0, ncols)
        prev = None
        for tt in range(NT):
            ps, bps = S.ps()
            for kc in range(8):
                S.op("pe", lambda e: e.matmul(out=ps[:, 0:ncols], lhsT=self.hT[:, kc, tt * 128:(tt + 1) * 128], rhs=wb[:, kc, 0:ncols],
                                              start=(kc == 0), stop=(kc == 7)),
                     reads=[self.b_hT[tt], bwb], writes=[bps])
            if prev is not None:
                handler(*prev)
            prev = (tt, ps, bps)
            yield
        handler(*prev)
        yield

    def proj_tm(self, l, c0, ncols, handler, ws):
        for _ in self.proj_tm_gen(l, c0, ncols, handler, ws):
            pass

    @staticmethod
    def run_pair(g1, g2):
        gens = [g1, g2]
        while gens:
            for g_ in list(gens):
                try:
                    next(g_)
                except StopIteration:
                    gens.remove(g_)

    def transpose_out(self, src_fn, n, rows, dst, dst_buf_list, tag, pool=None):
        S = self.S
        ps, bps = S.ps(pool)
        psb = ps[:].bitcast(BF16)
        for i in range(n):
            src, bsrc = src_fn(i)
            S.op("pe", lambda e: e.transpose(out=psb[0:rows, i * 128:(i + 1) * 128], in_=src, identity=self.ident_b[:]),
                 reads=[bsrc, self.b_ident_b], writes=[bps])
        S.op("act", lambda e: e.copy(out=dst, in_=psb[0:rows, 0:n * 128]), reads=[bps], writes=dst_buf_list)

    def phase2(self, l):
        S, din, scr = self.S, self.din, self.scr
        last = (l == self.nlayers - 1)
        self.es = ExitStack()
        self.zt = self.trot("zt", [128, 512], BF16, 2)
        self.tmpA = self.trot("tmpA", [128, 512], F32, 2)
        self.tmpB = self.trot("tmpB", [128, 512], F32, 2)
        self.tmo = self.trot("tmo", [128, 512], BF16, 3)
        self.fmo = self.trot("fmo", [128, 512], BF16, 3)
        self.qa = self.trot("qa", [128, 8, 128], BF16, 2)
        self.ka = self.trot("ka", [128, 2, 128], BF16, 2)
        self.vb = self.trot("vb", [128, 2, 65], BF16, 2)
        self.qaT = self.trot("qaT", [128, 8, 128], BF16, 2)
        self.kaT = self.trot("kaT", [128, 2, 128], BF16, 2)
        self.csc = self.trot("csc", [128, 2, 64], F32, 3)
        self.csb = self.trot("csb", [128, 2, 256], F32, 3)
        self.st8 = self.trot("st8", [128, 24], F32, 4)
        self.tmp8 = self.trot("tmp8", [128, NT, 8], F32, 4)
        for n in ("LNB", "GRAW", "GT"):
            self.asc[n] = self.tsb("asc_" + n, [128, NT, 8])
        self.KM, self.b_KM = self.tsb("KM", [128, 2])
        self.half8, self.b_half8 = self.tsb("half8", [128, 8])
        S.op("pool", lambda e: e.memset(self.half8[:], 0.5), writes=[self.b_half8])
        self.rowbuf, self.b_rowbuf = self.tsb("rowbuf", [128, 4360])
        self.slrow, self.b_slrow = self.tsb("slrow", [128, NTOK])
        S.op("pool", lambda e: e.memset(self.rowbuf[:], 0.0), writes=[self.b_rowbuf])
        for it in self.vb.items:
            S.op("pool", lambda e: e.memset(it[0][:], 1.0), writes=[it[1]])
        for it in self.ka.items + self.qa.items:
            S.op("pool", lambda e: e.memset(it[0][:], 0.0), writes=[it[1]])
        for it in self.ka.items:
            S.op("pool", lambda e: e.memset(it[0][:, :, 64:65], 1.0), writes=[it[1]])
        S.op("pool", lambda e: e.memset(self.KM[:], 0.0), writes=[self.b_KM])

        def silu_out(name):
            def h(tt, ps, bps):
                z, bz = self.zt.next()
                S.op("act", lambda e: e.activation(out=z[:], in_=ps[:], func=AF.Silu), reads=[bps], writes=[bz])
                S.dma("act", scr[name][tt * 128:(tt + 1) * 128, :], z[:], reads=[bz], writes=[self.db(name, tt)])
            return h

        wsrc = din["w_in"][l]
        bufsA, bufsB = self.wb.items[0:2], self.wb.items[2:4]
        ws1 = self.WStream(self, wsrc, [(O_AZ, 512), (O_AB, 16), (O_BKV, 256)], bufsA)
        self.proj_tm(l, O_AZ, 512, silu_out("ZA"), ws1)
        if self.stop == "p2a":
            S.barrier()
            self.es.close()
            return

        def h_ab(tt, ps, bps):
            S.op("act", lambda e: e.copy(out=self.SCR[:, tt, :], in_=ps[:, 0:16]), reads=[bps], writes=[self.b_SCR])
        self.proj_tm(l, O_AB, 16, h_ab, ws1)
        if self.stop == "p2b1":
            S.barrier()
            self.es.close()
            return
        self.a_scalars(l)
        if self.stop in ("p2b", "p2b2"):
            S.barrier()
            self.es.close()
            return

        def load_cs(tt, which):
            rotp, cn, sn, w = (self.csc, "k_cosc", "k_sinc", 64) if which == "c" else (self.csb, "k_cosb", "k_sinb", 256)
            cs, bcs = rotp.next()
            r0 = (tt - 2) * 128
            S.dma("sp", cs[:, 0, :], din[cn][r0:r0 + 128, :], writes=[bcs])
            S.dma("sp", cs[:, 1, :], din[sn][r0:r0 + 128, :], writes=[bcs])
            return cs, bcs

        def rope(x1, x2, cosb, sinb, o1, o2, shape_fn, bps, bcs, bout, scale=None):
            ta, bta = self.tmpA.next()
            tb, btb = self.tmpB.next()
            ta1, ta2 = shape_fn(ta[:, 0:256]), shape_fn(ta[:, 256:512])
            tb1, tb2 = shape_fn(tb[:, 0:256]), shape_fn(tb[:, 256:512])
            if scale is None:
                mul = lambda o, a, b: (lambda e: e.tensor_tensor(out=o, in0=a, in1=b, op=ALU.mult))
            else:
                mul = lambda o, a, b: (lambda e: e.scalar_tensor_tensor(out=o, in0=a, scalar=scale, in1=b, op0=ALU.mult, op1=ALU.mult))
            S.op("dve", mul(ta1, x1, cosb), reads=[bps, bcs], writes=[bta])
            S.op("dve", mul(tb1, x2, sinb), reads=[bps, bcs], writes=[btb])
            S.op("dve", mul(ta2, x1, sinb), reads=[bps, bcs], writes=[bta])
            S.op("dve", mul(tb2, x2, cosb), reads=[bps, bcs], writes=[btb])
            S.op("pool", lambda e: e.tensor_tensor(out=o1, in0=ta1, in1=tb1, op=ALU.subtract), reads=[bta, btb], writes=[bout])
            S.op("pool", lambda e: e.tensor_tensor(out=o2, in0=ta2, in1=tb2, op=ALU.add), reads=[bta, btb], writes=[bout])

        def rope_b(ps_ap, nha, tt, dst, bdst, bps, nh):
            o, bo = self.tmo.next()
            if tt >= 2:
                cs, bcs = load_cs(tt, "b")
                pv = ps_ap.rearrange("p (g f k) -> p g f k", f=2, k=16)
                ov = o[:, 0:nha * 32].rearrange("p (g f k) -> p g f k", f=2, k=16)
                cosb = cs[:, 0, 0:nha * 16].rearrange("p (g k) -> p g k", k=16)
                sinb = cs[:, 1, 0:nha * 16].rearrange("p (g k) -> p g k", k=16)
                rope(pv[:, :, 0, :], pv[:, :, 1, :], cosb, sinb, ov[:, :, 0, :], ov[:, :, 1, :],
                     lambda a: a[:, 0:nha * 16].rearrange("p (g k) -> p g k", k=16), bps, bcs, bo)
                S.op("act", lambda e: e.copy(out=dst[:, :, 0:64], in_=o[:, 0:nha * 32].rearrange("p (h k) -> p h k", h=nh)), reads=[bo], writes=[bdst])
            else:
                S.op("act", lambda e: e.copy(out=dst[:, :, 0:64], in_=ps_ap.rearrange("p (h k) -> p h k", h=nh)), reads=[bps], writes=[bdst])

        def h_bkv(tt, ps, bps):
            ka, bka = self.ka.next()
            rope_b(ps[:, 0:128], 4, tt, ka, bka, bps, 2)
            ta, bta = self.tmpA.next()
            st, bst = self.st8.next()
            S.op("act", lambda e: e.activation(out=ta[:, 0:128], in_=ps[:, 0:128], func=AF.Square), reads=[bps], writes=[bta])
            S.op("dve", lambda e: e.tensor_reduce(out=st[:, 0:2], in_=ta[:, 0:128].rearrange("p (h k) -> p h k", h=2), axis=AX.X, op=ALU.add),
                 reads=[bta], writes=[bst])
            S.op("dve", lambda e: e.tensor_tensor(out=self.KM[:], in0=self.KM[:], in1=st[:, 0:2], op=ALU.max), reads=[bst, self.b_KM], writes=[self.b_KM])
            vb, bvb = self.vb.next()
            S.op("dve", lambda e: e.tensor_copy(out=vb[:, :, 0:64], in_=ps[:, 128:256].rearrange("p (h k) -> p h k", h=2)),
                 reads=[bps], writes=[bvb])
            S.dma("act", scr["VB_TM"][tt * 128:(tt + 1) * 128, :, :], vb[:], reads=[bvb], writes=[self.db("VB_TM", tt)])
            kT, bkT = self.kaT.next()
            self.transpose_out(lambda i: (ka[:, i, :], bka), 2, 128, kT[:].rearrange("r h t -> r (h t)"), [bkT], "kbt")
            S.dma("act", scr["KB_FM"].rearrange("h r t -> r h t")[:, :, tt * 128:(tt + 1) * 128], kT[:], reads=[bkT], writes=[self.db("KB_FM", tt)])
        self.proj_tm(l, O_BKV, 256, h_bkv, ws1)
        if self.stop == "p2c":
            S.barrier()
            self.es.close()
            return
        S.op("dve", lambda e: e.tensor_reduce(out=self.kmx[:, 0:1], in_=self.KM[:], axis=AX.X, op=ALU.max), reads=[self.b_KM], writes=[self.b_kmx])
        dgk, bdgk = self.gdiag.next()
        S.op("dve", lambda e: e.tensor_scalar(out=dgk[:, 0:128], in0=self.ident_f[:], scalar1=self.kmx[:, 0:1], scalar2=None, op0=ALU.mult),
             reads=[self.b_ident_f, self.b_kmx], writes=[bdgk])
        ps, bps = S.ps()
        S.op("pe", lambda e: e.matmul(out=ps[:, 0:128], lhsT=self.ones_f[:], rhs=dgk[:, 0:128], start=True, stop=True),
             reads=[self.b_ones_f, bdgk], writes=[bps])
        kr, bkr = self.tsb("kmrow", [128, 4])
        S.op("dve", lambda e: e.tensor_reduce(out=kr[:, 0:1], in_=ps[:, 0:128], axis=AX.X, op=ALU.max), reads=[bps], writes=[bkr])
        S.op("act", lambda e: e.activation(out=kr[:, 1:2], in_=kr[:, 0:1], func=AF.Ln), reads=[bkr], writes=[bkr])
        S.op("act", lambda e: e.activation(out=kr[:, 2:3], in_=kr[:, 1:2], func=AF.Exp, scale=0.5), reads=[bkr], writes=[bkr])
        S.op("dve", lambda e: e.tensor_scalar(out=self.kmx[:, 1:2], in0=kr[:, 2:3], scalar1=-1.0, scalar2=None, op0=ALU.mult), reads=[bkr], writes=[self.b_kmx])
        S.op("dve", lambda e: e.tensor_scalar(out=self.kmx[:, 2:3], in0=kr[:, 2:3], scalar1=-0.125, scalar2=None, op0=ALU.mult), reads=[bkr], writes=[self.b_kmx])

        def h_bq(tt, ps, bps):
            qa, bqa = self.qa.next()
            ta, bta = self.tmpA.next()
            st, bst = self.st8.next()
            S.op("act", lambda e: e.activation(out=ta[:], in_=ps[:], func=AF.Square), reads=[bps], writes=[bta])
            S.op("dve", lambda e: e.tensor_reduce(out=st[:, 0:8], in_=ta[:].rearrange("p (h k) -> p h k", h=8), axis=AX.X, op=ALU.add),
                 reads=[bta], writes=[bst])
            S.op("pool", lambda e: e.tensor_tensor(out=st[:, 16:24], in0=st[:, 0:8], in1=self.half8[:], op=ALU.pow), reads=[bst, self.b_half8], writes=[bst])
            rope_b(ps[:], 16, tt, qa, bqa, bps, 8)
            S.op("dve", lambda e: e.tensor_scalar(out=qa[:, :, 64], in0=st[:, 16:24], scalar1=self.kmx[:, 1:2], scalar2=None, op0=ALU.mult),
                 reads=[bst, self.b_kmx], writes=[bqa])
            S.op("dve", lambda e: e.scalar_tensor_tensor(out=self.SH2[:, tt, :], in0=st[:, 16:24], scalar=self.kmx[:, 2:3], in1=self.par["b_sink"][0][:],
                                                         op0=ALU.mult, op1=ALU.add),
                 reads=[bst, self.b_kmx, self.par["b_sink"][1]], writes=[self.b_SH2])
            qT, bqT = self.qaT.next()
            self.transpose_out(lambda i: (qa[:, i, :], bqa), 8, 128, qT[:].rearrange("r h t -> r (h t)"), [bqT], "qbt")
            S.dma("act", scr["QB_FM"].rearrange("h r t -> r h t")[:, :, tt * 128:(tt + 1) * 128], qT[:], reads=[bqT], writes=[self.db("QB_FM", tt)])
        if self.stop == "p2d":
            S.barrier()
            self.es.close()
            return

        def h_cqk(name_fm, name_tm, scale):
            def h(tt, ps, bps):
                o, bo = self.tmo.next()
                if tt >= 2:
                    cs, bcs = load_cs(tt, "c")
                    pv = ps[:].rearrange("p (h f k) -> p h f k", h=4, f=2)
                    ov = o[:].rearrange("p (h f k) -> p h f k", h=4, f=2)
                    cosb = cs[:, 0, :].unsqueeze(1).to_broadcast([128, 4, 64])
                    sinb = cs[:, 1, :].unsqueeze(1).to_broadcast([128, 4, 64])
                    rope(pv[:, :, 0, :], pv[:, :, 1, :], cosb, sinb, ov[:, :, 0, :], ov[:, :, 1, :],
                         lambda a: a.rearrange("p (h k) -> p h k", h=4), bps, bcs, bo, scale=scale)
                else:
                    S.op("act", lambda e: e.activation(out=o[:], in_=ps[:], func=AF.Copy, scale=(1.0 if scale is None else scale)), reads=[bps], writes=[bo])
                if name_tm is not None:
                    S.dma("act", scr[name_tm][tt * 128:(tt + 1) * 128, :], o[:], reads=[bo], writes=[self.db(name_tm, tt)])
                f, bf = self.fmo.next()
                self.transpose_out(lambda i: (o[:, i * 128:(i + 1) * 128], bo), 4, 128, f[:], [bf], "cfm")
                S.dma("act", scr[name_fm].rearrange("h p t -> p h t")[:, :, tt * 128:(tt + 1) * 128], f[:].rearrange("p (h t) -> p h t", h=4),
                      reads=[bf], writes=[self.db(name_fm, tt)])
            return h

        def h_cv(tt, ps, bps):
            o, bo = self.tmo.next()
            S.op("act", lambda e: e.copy(out=o[:], in_=ps[:]), reads=[bps], writes=[bo])
            S.dma("act", scr["VC_TM"][tt * 128:(tt + 1) * 128, :], o[:], reads=[bo], writes=[self.db("VC_TM", tt)])
        wsZ = self.WStream(self, wsrc, [(O_BZ, 512), (O_CZ, 512)], bufsB)
        self.proj_tm(l, O_BZ, 512, silu_out("ZB"), wsZ)
        self.proj_tm(l, O_CZ, 512, silu_out("ZC"), wsZ)
        groups = self.tok_groups()
        wsH = self.WStream(self, wsrc, [(O_BQ, 512), (O_CQ, 512), (O_CK, 512)] + [(g * 512, 512) for g in range(3)], bufsA)
        wsL = self.WStream(self, wsrc, [(O_MG + g * 512, 512) for g in range(6)] + [(O_CV, 512)], bufsB)
        wsF = wsH
        wsM = wsL

        def heavy():
            yield from self.proj_tm_gen(l, O_BQ, 512, h_bq, wsH)
            yield from self.proj_tm_gen(l, O_CQ, 512, h_cqk("QC_FM", None, None), wsH)
            yield from self.proj_tm_gen(l, O_CK, 512, h_cqk("KC_FM", "KC_TM", float(CH) ** -0.5), wsH)

        def light():
            yield from self.proj_tm_gen(l, O_CV, 512, h_cv, wsL)
        if self.stop == "p2e":
            S.barrier()
            self.es.close()
            return


        def afm_gen():
            wcur = {}

            def get_w(g3):
                if g3 not in wcur:
                    self.marks.append(("  L%d afm%d" % (l, g3), {k: v.count for k, v in S.engs.items()}))
                    wcur[g3] = wsF.get(g3 * 512, 512)
                return wcur[g3]

            def proj_piece(ct, gi):
                g3, cl = ct // 4, ct % 4
                wb, bwb = get_w(g3)
                (t0, n, tile0, ntile) = groups[gi]
                ps, bps = S.ps()
                for kc in range(8):
                    S.op("pe", lambda e: e.matmul(out=ps[:, 0:n], lhsT=wb[:, kc, cl * 128:(cl + 1) * 128], rhs=self.hT[:, kc, t0:t0 + n],
                                                  start=(kc == 0), stop=(kc == 7)),
                         reads=[self.b_hT[tile0 + i] for i in range(ntile)] + [bwb], writes=[bps])
                off = 2 + t0 if t0 == 0 else 6 + t0
                S.op("act", lambda e: e.copy(out=self.rowbuf[:, off:off + n], in_=ps[:, 0:n]), reads=[bps], writes=[self.b_rowbuf.sub(t0)])

            def pass1_piece(ct, gi):
                g3, head = ct // 4, ct % 4
                (t0, n, tile0, ntile) = groups[gi]
                off = 2 + t0 if t0 == 0 else 6 + t0
                cv, bcv = self.tmpA.next()
                nb = [gi] if gi == 0 else [j for j in (gi - 1, gi, gi + 1) if 1 <= j < len(groups)]
                rb_reads = [self.b_rowbuf.sub(groups[j][0]) for j in nb]
                for k in range(5):
                    src = self.rowbuf[:, off + k - 2:off + k - 2 + n]
                    if k == 0:
                        S.op("dve", lambda e: e.tensor_scalar(out=cv[:, 0:n], in0=src, scalar1=self.convw[:, ct, 0:1], scalar2=None, op0=ALU.mult),
                             reads=rb_reads + [self.b_convw], writes=[bcv])
                    else:
                        S.op("dve", lambda e: e.scalar_tensor_tensor(out=cv[:, 0:n], in0=src, scalar=self.convw[:, ct, k:k + 1], in1=cv[:, 0:n],
                                                                     op0=ALU.mult, op1=ALU.add),
                             reads=rb_reads + [self.b_convw, bcv], writes=[bcv])
                if g3 < 2:
                    S.op("act", lambda e: e.activation(out=self.slrow[:, t0:t0 + n], in_=cv[:, 0:n], func=AF.Silu), reads=[bcv], writes=[self.b_slrow.sub(t0)])
                else:
                    o, bo = self.tmo.next()
                    S.op("act", lambda e: e.activation(out=o[:, 0:n], in_=cv[:, 0:n], func=AF.Silu), reads=[bcv], writes=[bo])
                    f, bf = self.fmo.next()
                    self.transpose_out(lambda i: (o[:, i * 128:(i + 1) * 128], bo), ntile, 128, f[:, 0:n], [bf], "va")
                    S.dma("act", scr["VA_TM"][head].rearrange("(t p) c -> p t c", p=128)[:, tile0:tile0 + ntile, :],
                          f[:, 0:n].rearrange("p (t c) -> p t c", c=128), reads=[bf], writes=[self.db("VA_TM", tile0 + i) for i in range(ntile)])

            def pass2_piece(ct, gi):
                g3, head = ct // 4, ct % 4
                name = "QA_FM" if g3 == 0 else "KA_FM"
                (t0, n, tile0, ntile) = groups[gi]
                sq, bsq = self.tmo.next()
                S.op("act", lambda e: e.activation(out=sq[:, 0:n], in_=self.slrow[:, t0:t0 + n], func=AF.Square), reads=[self.b_slrow.sub(t0)], writes=[bsq])
                ps, bps = S.ps()
                S.op("pe", lambda e: e.matmul(out=ps[:, 0:n], lhsT=self.ones_b[:], rhs=sq[:, 0:n], start=True, stop=True),
                     reads=[self.b_ones_b, bsq], writes=[bps])
                ta, bta = self.tmpB.next()
                S.op("act", lambda e: e.activation(out=ta[:, 0:n], in_=ps[:, 0:n], func=AF.Ln, bias=self.epsb[:, 0:1]), reads=[bps, self.b_epsb], writes=[bta])
                S.op("act", lambda e: e.activation(out=ta[:, 0:n], in_=ta[:, 0:n], func=AF.Exp, scale=-0.5, bias=self.epsb[:, 1 + g3:2 + g3]),
                     reads=[bta, self.b_epsb], writes=[bta])
                o, bo = self.fmo.next()
                S.op("dve", lambda e: e.tensor_tensor(out=o[:, 0:n], in0=self.slrow[:, t0:t0 + n], in1=ta[:, 0:n], op=ALU.mult),
                     reads=[self.b_slrow.sub(t0), bta], writes=[bo])
                S.dma("act", scr[name][head][:, t0:t0 + n], o[:, 0:n], reads=[bo], writes=[self.db(name, tile0 + i) for i in range(ntile)])
                if g3 == 1:
                    f, bf = self.zt.next()
                    self.transpose_out(lambda i: (o[:, i * 128:(i + 1) * 128], bo), ntile, 128, f[:, 0:n], [bf], "ka")
                    S.dma("act", scr["KA_TM"][head].rearrange("(t p) c -> p t c", p=128)[:, tile0:tile0 + ntile, :],
                          f[:, 0:n].rearrange("p (t c) -> p t c", c=128), reads=[bf], writes=[self.db("KA_TM", tile0 + i) for i in range(ntile)])

            ng = len(groups)
            for gi in range(ng):
                proj_piece(0, gi)
                yield
            for ct in range(12):
                for gi in range(ng):
                    pass1_piece(ct, gi)
                    if ct + 1 < 12 and gi >= 1:
                        proj_piece(ct + 1, gi - 1)
                    yield
                if ct + 1 < 12:
                    proj_piece(ct + 1, ng - 1)
                    yield
                if ct < 8:
                    for gi in range(ng):
                        pass2_piece(ct, gi)
                        yield

        def merge_gen():
            for g6 in range(6):
                self.marks.append(("  L%d mg%d" % (l, g6), {k: v.count for k, v in S.engs.items()}))
                wb, bwb = wsM.get(O_MG + g6 * 512, 512)
                for cl in range(4):
                    ct = g6 * 4 + cl
                    for (t0, n, tile0, ntile) in groups:
                        if last and t0 == 0:
                            continue
                        ps, bps = S.ps()
                        for kc in range(8):
                            S.op("pe", lambda e: e.matmul(out=ps[:, 0:n], lhsT=wb[:, kc, cl * 128:(cl + 1) * 128], rhs=self.hT[:, kc, t0:t0 + n],
                                                          start=(kc == 0), stop=(kc == 7)),
                                 reads=[self.b_hT[tile0 + i] for i in range(ntile)] + [bwb], writes=[bps])
                        o, bo = self.fmo.next()
                        S.op("act", lambda e: e.activation(out=o[:, 0:n], in_=ps[:, 0:n], func=AF.Sigmoid), reads=[bps], writes=[bo])
                        S.dma("act", scr["GM_FM"][ct * 128:(ct + 1) * 128, t0:t0 + n], o[:, 0:n], reads=[bo],
                              writes=[self.db("GM_FM%d" % ct, tile0 + i) for i in range(ntile)])
                        yield
        self.run_pair(heavy(), merge_gen())
        self.run_pair(afm_gen(), light())
        if self.stop == "p2":
            self.dump_p2()
        S.barrier()
        self.es.close()

    def a_scalars(self, l):
        S = self.S
        A = self.asc
        braw = self.SCR[:, :, 0:8]
        araw = self.SCR[:, :, 8:16]
        rs = [self.b_SCR]
        one = self.epsb[:, 3:4]

        def softplus_parts(x_ap, xb, neg):
            t1, b1 = self.tmp8.next()
            t2, b2 = self.tmp8.next()
            S.op("act", lambda e: e.activation(out=t1[:], in_=x_ap, func=AF.Abs), reads=xb, writes=[b1])
            S.op("act", lambda e: e.activation(out=t1[:], in_=t1[:], func=AF.Exp, scale=-1.0), reads=[b1], writes=[b1])
            S.op("act", lambda e: e.activation(out=t1[:], in_=t1[:], func=AF.Ln, bias=one), reads=[b1, self.b_epsb], writes=[b1])
            S.op("dve", lambda e: e.tensor_scalar(out=t2[:], in0=x_ap, scalar1=(-1.0 if neg else 1.0), scalar2=0.0, op0=ALU.mult, op1=ALU.max),
                 reads=xb, writes=[b2])
            return (t2, b2), (t1, b1)

        (m, bm), (l1, bl1) = softplus_parts(braw, rs, True)
        LNB, bLNB = A["LNB"]
        S.op("dve", lambda e: e.scalar_tensor_tensor(out=LNB[:], in0=m[:], scalar=-1.0, in1=l1[:], op0=ALU.mult, op1=ALU.subtract),
             reads=[bm, bl1], writes=[bLNB])
        BETA, bBETA = A["BETA"]
        S.op("act", lambda e: e.activation(out=BETA[:], in_=LNB[:], func=AF.Exp), reads=[bLNB], writes=[bBETA])
        xa, bxa = self.tmp8.next()
        dtb, bdtb = self.par["a_dt_bias"]
        S.op("dve", lambda e: e.tensor_tensor(out=xa[:], in0=araw, in1=dtb[:].unsqueeze(1).to_broadcast([128, NT, 8]), op=ALU.add),
             reads=rs + [bdtb], writes=[bxa])
        (m2, bm2), (l2, bl2) = softplus_parts(xa[:], [bxa], False)
        S.op("dve", lambda e: e.tensor_tensor(out=m2[:], in0=m2[:], in1=l2[:], op=ALU.add), reads=[bm2, bl2], writes=[bm2])
        alog, balog = self.par["a_log"]
        nea, bnea = self.st8.next()
        S.op("act", lambda e: e.activation(out=nea[:, 0:8], in_=alog[:], func=AF.Exp), reads=[balog], writes=[bnea])
        GRAW, bGRAW = A["GRAW"]
        S.op("dve", lambda e: e.scalar_tensor_tensor(out=GRAW[:], in0=m2[:], scalar=-1.0, in1=nea[:, 0:8].unsqueeze(1).to_broadcast([128, NT, 8]),
                                                     op0=ALU.mult, op1=ALU.mult),
             reads=[bm2, bnea], writes=[bGRAW])
        GC, bGC = A["GC"]
        GT, bGT = A["GT"]
        if self.stop == "p2b2":
            return
        gflat = GRAW[:].rearrange("p t c -> p (t c)")
        res = []
        for lhs, blhs in ((self.tri[:, 0, :], self.b_tri), (self.tri[:, 1, :], self.b_tri), (self.ones_f[:], self.b_ones_f)):
            ps, bps = S.ps()
            S.op("pe", lambda e: e.matmul(out=ps[:, 0:NT * 8], lhsT=lhs, rhs=gflat, start=True, stop=True), reads=[blhs, bGRAW], writes=[bps])
            res.append((ps[:, 0:NT * 8].rearrange("p (t c) -> p t c", c=8), bps))
        S.op("dve", lambda e: e.tensor_copy(out=GC[:, :, 0:4], in_=res[0][0][:, :, 0:4]), reads=[res[0][1]], writes=[bGC])
        S.op("dve", lambda e: e.tensor_copy(out=GC[:, :, 4:8], in_=res[1][0][:, :, 4:8]), reads=[res[1][1]], writes=[bGC])
        S.op("act", lambda e: e.copy(out=GT[:], in_=res[2][0]), reads=[res[2][1]], writes=[bGT])
        NEGG, bNEGG = A["NEGG"]
        S.op("dve", lambda e: e.tensor_scalar(out=NEGG[:], in0=GC[:], scalar1=-1.0, scalar2=None, op0=ALU.mult), reads=[bGC], writes=[bNEGG])
        LNBMG, bLNBMG = A["LNBMG"]
        S.op("dve", lambda e: e.tensor_tensor(out=LNBMG[:], in0=LNB[:], in1=GC[:], op=ALU.subtract), reads=[bLNB, bGC], writes=[bLNBMG])
        NEGEG, bNEGEG = A["NEGEG"]
        S.op("act", lambda e: e.activation(out=NEGEG[:], in_=GC[:], func=AF.Exp), reads=[bGC], writes=[bNEGEG])
        S.op("dve", lambda e: e.tensor_scalar(out=NEGEG[:], in0=NEGEG[:], scalar1=-1.0, scalar2=None, op0=ALU.mult), reads=[bNEGEG], writes=[bNEGEG])
        ETAIL, bETAIL = A["ETAIL"]
        S.op("dve", lambda e: e.tensor_tensor(out=ETAIL[:], in0=GT[:], in1=GC[:], op=ALU.subtract), reads=[bGT, bGC], writes=[bETAIL])
        S.op("act", lambda e: e.activation(out=ETAIL[:], in_=ETAIL[:], func=AF.Exp), reads=[bETAIL], writes=[bETAIL])
        EGL, bEGL = A["EGL"]
        S.op("act", lambda e: e.activation(out=EGL[:], in_=GT[:], func=AF.Exp), reads=[bGT], writes=[bEGL])

    def core_b(self, l, defer=False):
        S, scr, din = self.S, self.scr, self.din
        last = (l == self.nlayers - 1)
        if not defer:
            self.es = ExitStack()
        KBT = self.R1[:, 0:2 * NTOK].rearrange("p (g t) -> p g t", g=2)
        VBR = self.R1[:, 2 * NTOK:2 * NTOK + NT * 130].rearrange("p (t g c) -> p t g c", g=2, c=65)
        bKBT, bVBR = Buf("KBT"), Buf("VBR")
        S.dma("sp", KBT, scr["KB_FM"].rearrange("g r t -> r g t"), reads=self.dball("KB_FM"), writes=[bKBT])
        S.dma("sp", VBR, scr["VB_TM"].rearrange("(t p) g c -> p t g c", p=128), reads=self.dball("VB_TM"), writes=[bVBR])
        if l == 0:
            bst, bbst = self.tsb("bmst", [128, 2, 512])
            S.dma("sp", bst[:], din["k_bmask"].rearrange("a p n -> p a n"), writes=[bbst])
            S.op("pool", lambda e: e.tensor_copy(out=self.bmask[:], in_=bst[:]), reads=[bbst], writes=[self.b_bmask])
        qTr = self.trot("b_qT", [128, 4, 128], BF16, 2)
        pTr = self.trot("b_pT", [128, 5, 512], BF16, 2)
        zbr = self.trot("b_zb", [128, 512], BF16, 2)
        ybr = self.trot("b_yb", [128, 512], BF16, 2)
        obr = self.trot("b_ob", [128, 256], F32, 2)
        str_ = self.trot("b_st", [128, 16], F32, 6)
        fmo = self.trot("b_fmo", [128, 512], BF16, 2)
        qtiles = list(range(2, NT)) if last else list(range(NT))
        def b_gen():
            for qt in qtiles:
                zb, bzb = zbr.next()
                S.dma("sp", zb[:], scr["ZB"][qt * 128:(qt + 1) * 128, :], reads=[self.db("ZB", qt)], writes=[bzb])
                yb, byb = ybr.next()
                for g in range(2):
                    qT, bqT = qTr.next()
                    S.dma("sp", qT[:], scr["QB_FM"][g * 4:(g + 1) * 4].rearrange("h r t -> r h t")[:, :, qt * 128:(qt + 1) * 128],
                          reads=[self.db("QB_FM", qt)], writes=[bqT])
                    keys = [(0, None), (1, None)]
                    if qt >= 2:
                        if qt - 1 >= 2:
                            keys.append((qt - 1, 0))
                        keys.append((qt, None))
                        if qt + 1 < NT:
                            keys.append((qt + 1, 1))
                    pT, bpT = pTr.next()
                    for idx, (kt, m) in enumerate(keys):
                        ps, bps = S.ps()
                        S.op("pe", lambda e: e.matmul(out=ps[:], lhsT=KBT[:, g, kt * 128:(kt + 1) * 128], rhs=qT[:].rearrange("p h t -> p (h t)"),
                                                      start=True, stop=True), reads=[bKBT, bqT], writes=[bps])
                        S.op("act", lambda e: e.activation(out=pT[:, idx, :], in_=ps[:], func=AF.Exp, scale=0.125), reads=[bps], writes=[bpT.sub(idx)])
                        if m is not None:
                            S.op("pool", lambda e: e.tensor_tensor(out=pT[:, idx, :], in0=pT[:, idx, :], in1=self.bmask[:, m, :], op=ALU.mult),
                                 reads=[bpT.sub(idx), self.b_bmask], writes=[bpT.sub(idx)])
                    po, bpo = S.ps()
                    for h in range(4):
                        for idx, (kt, m) in enumerate(keys):
                            S.op("pe", lambda e: e.matmul(out=po[:, h * 65:(h + 1) * 65], lhsT=pT[:, idx, h * 128:(h + 1) * 128], rhs=VBR[:, kt, g, :],
                                                          start=(idx == 0), stop=(idx == len(keys) - 1)), reads=[bpT.sub(idx), bVBR], writes=[bpo])
                    st, bst_ = str_.next()
                    pov = po[:, 0:260].rearrange("p (h c) -> p h c", c=65)
                    S.op("act", lambda e: e.activation(out=st[:, 0:4], in_=self.SH2[:, qt, g * 4:(g + 1) * 4], func=AF.Exp), reads=[self.b_SH2], writes=[bst_])
                    S.op("dve", lambda e: e.tensor_tensor(out=st[:, 4:8], in0=pov[:, :, 64], in1=st[:, 0:4], op=ALU.add), reads=[bpo, bst_], writes=[bst_])
                    S.op("dve", lambda e: e.reciprocal(out=st[:, 8:12], in_=st[:, 4:8]), reads=[bst_], writes=[bst_])
                    ob, bob = obr.next()
                    S.op("dve", lambda e: e.tensor_tensor(out=ob[:].rearrange("p (h c) -> p h c", c=64), in0=pov[:, :, 0:64],
                                                          in1=st[:, 8:12].unsqueeze(2).to_broadcast([128, 4, 64]), op=ALU.mult),
                         reads=[bpo, bst_], writes=[bob])
                    S.op("pool", lambda e: e.tensor_tensor(out=yb[:, g * 256:(g + 1) * 256], in0=ob[:], in1=zb[:, g * 256:(g + 1) * 256], op=ALU.mult),
                         reads=[bob, bzb], writes=[byb.sub(g)])
                    yield
                self.y_out(1, qt, yb, byb, fmo)
                yield
        if defer:
            return b_gen()
        for _ in b_gen():
            pass
        S.barrier()
        self.es.close()

    def core_bc(self, l):
        self.es = ExitStack()
        gb = self.core_b(l, defer=True)
        gc = self.core_c(l, defer=True)
        self.run_pair(gb, gc)
        self.S.barrier()
        self.es.close()

    def y_out(self, br, tt, y, by, fmo, pool=None):
        S = self.S
        f, bf = fmo.next()
        self.transpose_out(lambda i: (y[:, i * 128:(i + 1) * 128], by), 4, 128, f[:], [bf], "y", pool=pool)
        S.dma("act", self.scr["Y_FM"][br].rearrange("(k p) t -> p k t", p=128)[:, :, tt * 128:(tt + 1) * 128],
              f[:].rearrange("p (k t) -> p k t", k=4), reads=[bf], writes=[self.db("Y_FM%d" % br, tt)])

    def core_c(self, l, defer=False):
        S, scr = self.S, self.scr
        last = (l == self.nlayers - 1)
        if not defer:
            self.es = ExitStack()
        one = self.epsb[:, 3:4]
        cd, bcd = self.par["c_decay"]
        c8 = self.trot("c_c8", [128, 8], F32, 6)
        t1, b1 = c8.next()
        t2, b2 = c8.next()
        LG, bLG = c8.next()
        S.op("act", lambda e: e.activation(out=t1[:], in_=cd[:], func=AF.Abs), reads=[bcd], writes=[b1])
        S.op("act", lambda e: e.activation(out=t1[:], in_=t1[:], func=AF.Exp, scale=-1.0), reads=[b1], writes=[b1])
        S.op("act", lambda e: e.activation(out=t1[:], in_=t1[:], func=AF.Ln, bias=one), reads=[b1, self.b_epsb], writes=[b1])
        S.op("dve", lambda e: e.tensor_scalar(out=t2[:], in0=cd[:], scalar1=-1.0, scalar2=0.0, op0=ALU.mult, op1=ALU.max), reads=[bcd], writes=[b2])
        S.op("dve", lambda e: e.scalar_tensor_tensor(out=LG[:], in0=t2[:], scalar=-1.0, in1=t1[:], op0=ALU.mult, op1=ALU.subtract),
             reads=[b1, b2], writes=[bLG])
        GAMC, bGAMC = c8.next()
        S.op("act", lambda e: e.activation(out=GAMC[:], in_=LG[:], func=AF.Exp, scale=float(CH)), reads=[bLG], writes=[bGAMC])
        KDEC, bKDEC = c8.next()
        S.op("dve", lambda e: e.tensor_tensor(out=KDEC[:], in0=LG[:], in1=self.cj[:], op=ALU.mult), reads=[bLG, self.b_cj], writes=[bKDEC])
        S.op("act", lambda e: e.activation(out=KDEC[:], in_=KDEC[:], func=AF.Exp), reads=[bKDEC], writes=[bKDEC])
        DM, bDM = self.tsb("c_DM", [128, 512])
        QDF, bQDF = self.tsb("c_QDF", [128, 512], BF16)
        QDB, bQDB = self.tsb("c_QDB", [128, 512], BF16)
        tm = self.trot("c_tm", [128, 128], F32, 2)
        for h in range(4):
            ta, bta = tm.next()
            tb, btb = tm.next()
            S.op("act", lambda e: e.activation(out=ta[:], in_=self.cm[:, 0, :], func=AF.Exp, scale=LG[:, h:h + 1]), reads=[self.b_cm, bLG], writes=[bta])
            S.op("dve", lambda e: e.tensor_tensor(out=ta[:], in0=ta[:], in1=self.cm[:, 2, :], op=ALU.mult), reads=[bta, self.b_cm], writes=[bta])
            S.op("act", lambda e: e.activation(out=tb[:], in_=self.cm[:, 1, :], func=AF.Exp, scale=LG[:, 4 + h:5 + h]), reads=[self.b_cm, bLG], writes=[btb])
            S.op("dve", lambda e: e.tensor_tensor(out=tb[:], in0=tb[:], in1=self.cm[:, 3, :], op=ALU.mult), reads=[btb, self.b_cm], writes=[btb])
            S.op("dve", lambda e: e.tensor_tensor(out=ta[:], in0=ta[:], in1=tb[:], op=ALU.add), reads=[bta, btb], writes=[bta])
            S.op("dve", lambda e: e.scalar_tensor_tensor(out=DM[:, h * 128:(h + 1) * 128], in0=self.ident_f[:], scalar=2.0, in1=ta[:], op0=ALU.mult, op1=ALU.add),
                 reads=[bta, self.b_ident_f], writes=[bDM])
            S.op("act", lambda e: e.activation(out=QDF[:, h * 128:(h + 1) * 128], in_=self.cm[:, 4, :], func=AF.Exp, scale=LG[:, h:h + 1]),
                 reads=[self.b_cm, bLG], writes=[bQDF])
            S.op("act", lambda e: e.activation(out=QDB[:, h * 128:(h + 1) * 128], in_=self.cm[:, 5, :], func=AF.Exp, scale=LG[:, 4 + h:5 + h]),
                 reads=[self.b_cm, bLG], writes=[bQDB])
        kTMr = [self.trot("c_kTM%d" % d, [128, 512], BF16, 2) for d in range(2)]
        vr = [self.trot("c_v%d" % d, [128, 512], BF16, 2) for d in range(3)]
        kdr = [self.trot("c_kd%d" % d, [128, 512], BF16, 2) for d in range(2)]
        sbfr = [self.trot("c_sbf%d" % d, [128, 512], BF16, 2) for d in range(2)]
        S32 = [self.tsb("c_S32_%d" % d, [128, 512]) for d in range(2)]
        SCN = ["SCF", "SCB"]

        def state_update(d, cc, kTM, bkTM, v, bv):
            kd, bkd = kdr[d].next()
            for h in range(4):
                S.op("act", lambda e: e.activation(out=kd[:, h * 128:(h + 1) * 128], in_=kTM[:, h * 128:(h + 1) * 128], func=AF.Copy,
                                                   scale=KDEC[:, d * 4 + h:d * 4 + h + 1]),
                     reads=[bkTM, bKDEC], writes=[bkd.sub(h)])
            ps, bps = S.ps()
            for h in range(4):
                S.op("pe", lambda e: e.matmul(out=ps[:, h * 128:(h + 1) * 128], lhsT=kd[:, h * 128:(h + 1) * 128], rhs=v[:, h * 128:(h + 1) * 128],
                                              start=True, stop=True), reads=[bkd, bv], writes=[bps])
            s32, bs32 = S32[d]
            for h in range(4):
                S.op("dve", lambda e: e.scalar_tensor_tensor(out=s32[:, h * 128:(h + 1) * 128], in0=s32[:, h * 128:(h + 1) * 128],
                                                             scalar=GAMC[:, d * 4 + h:d * 4 + h + 1], in1=ps[:, h * 128:(h + 1) * 128],
                                                             op0=ALU.mult, op1=ALU.add),
                     reads=[bs32.sub(h), bGAMC, bps], writes=[bs32.sub(h)])

        def state_pass(d):
            S.op("pool", lambda e: e.memset(S32[d][0][:], 0.0), writes=[S32[d][1]])
            order = list(range(NT)) if d == 0 else [1, 0] + list(range(NT - 1, 1, -1))
            for cc in order:
                sbf, bsbf = sbfr[d].next()
                S.op("act", lambda e: e.copy(out=sbf[:], in_=S32[d][0][:]), reads=[S32[d][1]], writes=[bsbf])
                S.dma("act", scr[SCN[d]][cc], sbf[:], reads=[bsbf], writes=[self.db(SCN[d], cc)])
                kTM, bkTM = kTMr[d].next()
                v, bv = vr[d].next()
                S.dma("sp", kTM[:], scr["KC_TM"][cc * 128:(cc + 1) * 128, :], reads=[self.db("KC_TM", cc)], writes=[bkTM])
                S.dma("sp", v[:], scr["VC_TM"][cc * 128:(cc + 1) * 128, :], reads=[self.db("VC_TM", cc)], writes=[bv])
                yield
                state_update(d, cc, kTM, bkTM, v, bv)
                yield

        qTr = self.trot("c_qT", [128, 512], BF16, 2)
        kTr = self.trot("c_kT", [128, 512], BF16, 2)
        sbr = self.trot("c_sb", [128, 512], BF16, 2)
        sfr = self.trot("c_sf", [128, 512], BF16, 2)
        zcr = self.trot("c_zc", [128, 512], BF16, 2)
        qkr = self.trot("c_qk", [128, 512], BF16, 2)
        qdr = self.trot("c_qd", [128, 512], BF16, 2)
        ofr = self.trot("c_of", [128, 512], F32, 2)
        t5r = self.trot("c_t5", [128, 512], F32, 1)
        ycr = self.trot("c_yc", [128, 512], BF16, 2)
        stc = self.trot("c_st", [128, 24], F32, 3)
        fmo = self.trot("c_fmo", [128, 512], BF16, 2)
        cnw, bcnw = self.par["c_norm_w"]
        eps = self.epsb[:, 0:1]
        def out_pass():
            for cc in range(NT):
                need_out = (cc >= 2) or (not last)
                if need_out:
                    v, bv = vr[2].next()
                    S.dma("sp", v[:], scr["VC_TM"][cc * 128:(cc + 1) * 128, :], reads=[self.db("VC_TM", cc)], writes=[bv])
                    qT, bqT = qTr.next()
                    kT, bkT = kTr.next()
                    sb, bsb = sbr.next()
                    zc, bzc = zcr.next()
                    S.dma("sp", qT[:].rearrange("p (h t) -> p h t", h=4), scr["QC_FM"].rearrange("h p t -> p h t")[:, :, cc * 128:(cc + 1) * 128],
                          reads=[self.db("QC_FM", cc)], writes=[bqT])
                    S.dma("sp", kT[:].rearrange("p (h t) -> p h t", h=4), scr["KC_FM"].rearrange("h p t -> p h t")[:, :, cc * 128:(cc + 1) * 128],
                          reads=[self.db("KC_FM", cc)], writes=[bkT])
                    S.dma("sp", sb[:], scr["SCB"][cc], reads=[self.db("SCB", cc)], writes=[bsb])
                    S.dma("sp", zc[:], scr["ZC"][cc * 128:(cc + 1) * 128, :], reads=[self.db("ZC", cc)], writes=[bzc])
                    sbf, bsbf = sfr.next()
                    S.dma("sp", sbf[:], scr["SCF"][cc], reads=[self.db("SCF", cc)], writes=[bsbf])
                    ps1, bps1 = S.ps()
                    for h in range(4):
                        S.op("pe", lambda e: e.matmul(out=ps1[:, h * 128:(h + 1) * 128], lhsT=kT[:, h * 128:(h + 1) * 128], rhs=qT[:, h * 128:(h + 1) * 128],
                                                      start=True, stop=True), reads=[bkT, bqT], writes=[bps1])
                    qk, bqk = qkr.next()
                    S.op("dve", lambda e: e.tensor_tensor(out=qk[:], in0=ps1[:], in1=DM[:], op=ALU.mult), reads=[bps1, bDM], writes=[bqk])
                    qdf, bqdf = qdr.next()
                    qdb, bqdb = qdr.next()
                    S.op("pool", lambda e: e.tensor_tensor(out=qdf[:], in0=qT[:], in1=QDF[:], op=ALU.mult), reads=[bqT, bQDF], writes=[bqdf])
                    S.op("pool", lambda e: e.tensor_tensor(out=qdb[:], in0=qT[:], in1=QDB[:], op=ALU.mult), reads=[bqT, bQDB], writes=[bqdb])
                    po, bpo = S.ps()
                    for h in range(4):
                        sl = slice(h * 128, (h + 1) * 128)
                        S.op("pe", lambda e: e.matmul(out=po[:, sl], lhsT=qk[:, sl], rhs=v[:, sl], start=True, stop=False), reads=[bqk, bv], writes=[bpo])
                        S.op("pe", lambda e: e.matmul(out=po[:, sl], lhsT=qdf[:, sl], rhs=sbf[:, sl], start=False, stop=False), reads=[bqdf, bsbf], writes=[bpo])
                        S.op("pe", lambda e: e.matmul(out=po[:, sl], lhsT=qdb[:, sl], rhs=sb[:, sl], start=False, stop=True), reads=[bqdb, bsb], writes=[bpo])
                    of, bof = ofr.next()
                    t5, bt5 = t5r.next()
                    st, bst = stc.next()
                    S.op("act", lambda e: e.copy(out=of[:], in_=po[:]), reads=[bpo], writes=[bof])
                    S.op("act", lambda e: e.activation(out=t5[:], in_=po[:], func=AF.Square), reads=[bpo], writes=[bt5])
                    S.op("dve", lambda e: e.tensor_reduce(out=st[:, 0:4], in_=of[:].rearrange("p (h c) -> p h c", h=4), axis=AX.X, op=ALU.add), reads=[bof], writes=[bst])
                    S.op("dve", lambda e: e.tensor_reduce(out=st[:, 4:8], in_=t5[:].rearrange("p (h c) -> p h c", h=4), axis=AX.X, op=ALU.add), reads=[bt5], writes=[bst])
                    S.op("dve", lambda e: e.tensor_scalar(out=st[:, 8:12], in0=st[:, 0:4], scalar1=1.0 / 128, scalar2=None, op0=ALU.mult), reads=[bst], writes=[bst])
                    S.op("dve", lambda e: e.tensor_tensor(out=st[:, 12:16], in0=st[:, 8:12], in1=st[:, 8:12], op=ALU.mult), reads=[bst], writes=[bst])
                    S.op("dve", lambda e: e.scalar_tensor_tensor(out=st[:, 16:20], in0=st[:, 4:8], scalar=1.0 / 128, in1=st[:, 12:16], op0=ALU.mult, op1=ALU.subtract),
                         reads=[bst], writes=[bst])
                    S.op("act", lambda e: e.activation(out=st[:, 20:24], in_=st[:, 16:20], func=AF.Ln, bias=eps), reads=[bst, self.b_epsb], writes=[bst])
                    S.op("act", lambda e: e.activation(out=st[:, 20:24], in_=st[:, 20:24], func=AF.Exp, scale=-0.5), reads=[bst], writes=[bst])
                    ofv = of[:].rearrange("p (h c) -> p h c", h=4)
                    S.op("dve", lambda e: e.tensor_tensor(out=ofv, in0=ofv, in1=st[:, 8:12].unsqueeze(2).to_broadcast([128, 4, 128]), op=ALU.subtract),
                         reads=[bof, bst], writes=[bof])
                    S.op("dve", lambda e: e.tensor_tensor(out=ofv, in0=ofv, in1=st[:, 20:24].unsqueeze(2).to_broadcast([128, 4, 128]), op=ALU.mult),
                         reads=[bof, bst], writes=[bof])
                    S.op("pool", lambda e: e.tensor_tensor(out=of[:], in0=of[:], in1=cnw[:], op=ALU.mult), reads=[bof, bcnw], writes=[bof])
                    yc, byc = ycr.next()
                    S.op("pool", lambda e: e.tensor_tensor(out=yc[:], in0=of[:], in1=zc[:], op=ALU.mult), reads=[bof, bzc], writes=[byc])
                    self.y_out(2, cc, yc, byc, fmo)
                yield

        def c_gen():
            gens = [state_pass(0), state_pass(1)]
            while gens:
                for g_ in list(gens):
                    try:
                        next(g_)
                        yield
                    except StopIteration:
                        gens.remove(g_)
            yield from out_pass()
        if defer:
            return c_gen()
        for _ in c_gen():
            pass
        S.barrier()
        self.es.close()

    def core_a(self, l):
        S, scr = self.S, self.scr
        last = (l == self.nlayers - 1)
        self.es = ExitStack()
        A = self.asc
        import os
        KPRE = int(os.environ.get("KPRE", "2"))
        r1_next = [0]

        def mkrot(name, k, use_r1=True):
            items = []
            for i in range(k):
                if use_r1 and r1_next[0] < 68:
                    j = r1_next[0]
                    r1_next[0] += 1
                    items.append((self.R1[:, j * 512:(j + 1) * 512], Buf("%s%d" % (name, i))))
                else:
                    t = self._talloc("a_" + name, [128, 512], BF16)
                    items.append((t[:], Buf("%s%d" % (name, i))))
            r = Rot.__new__(Rot)
            r.items = items
            r.i = 0
            return r

        def R(n, dt=BF16, k=2):
            r = self.trot("a_" + n, [128, 512], dt, k)
            r.items = [(t[:], b) for t, b in r.items]
            return r
        ofr, t5r = R("of", F32, 3), R("t5", F32, 2)
        zar, yar, fmo = R("za"), R("ya"), R("fmo")
        sta = self.trot("a_st", [128, 16], F32, 4)
        nmask, bnmask = self.tsb("a_nmask", [128, 14, 128], BF16)
        nmst, bnmst = self.wst.next()
        nmv = nmst[:].rearrange("p k n -> p (k n)")[:, 0:14 * 128].rearrange("p (a n) -> p a n", a=14)
        S.dma("sp", nmv, self.din["k_nm"].rearrange("a p n -> p a n"), writes=[bnmst])
        S.op("pool", lambda e: e.tensor_copy(out=nmask[:], in_=nmv), reads=[bnmst], writes=[bnmask])
        anw, banw = self.par["a_norm_w"]
        eps = self.epsb[:, 0:1]
        H = [slice(h * 128, (h + 1) * 128) for h in range(4)]
        v4 = lambda t: t[:].rearrange("p (h c) -> p h c", h=4)
        orders = [list(range(NT)), [1, 0] + list(range(NT - 1, 1, -1))]
        oa_written = set()
        from collections import deque
        free_banks = deque(range(8))

        def acq():
            while not free_banks:
                yield
            bk = free_banks.popleft()
            ps, bps = S.psum[bk]
            return ps, bps, bk

        def rel(bk):
            free_banks.append(bk)

        def mm4(lhs, blhs, rhs, brhs):
            ps, bps, bk = yield from acq()
            for h in range(4):
                S.op("pe", lambda e: e.matmul(out=ps[:, H[h]], lhsT=lhs[:, H[h]], rhs=rhs[:, H[h]], start=True, stop=True), reads=[blhs, brhs], writes=[bps])
            return ps, bps, bk

        def tr4(src, bsrc):
            ps, bps, bk = yield from acq()
            psb = ps[:].bitcast(BF16)
            for h in range(4):
                S.op("pe", lambda e: e.transpose(out=psb[:, H[h]], in_=src[:, H[h]], identity=self.ident_b[:]), reads=[bsrc, self.b_ident_b], writes=[bps])
            return psb, bps, bk

        class DirBufs:
            pass
        DB = []
        for d in range(2):
            o = DirBufs()
            for n in ("kT", "kTM", "vTM", "qT", "qk", "qd", "kt", "Pf"):
                setattr(o, n, mkrot("%s_%d" % (n, d), KPRE + 1))
            o.tsets = []
            for ts in range(KPRE):
                tsd = {n: mkrot("%s_%d_%d" % (n, d, ts), 1).items[0] for n in ("eg", "Ma", "Ml", "Pa", "Pb", "W1", "X", "MlmA", "MlmB")}
                for n in ("F0", "F1", "F2"):
                    tsd[n] = (self._talloc("a_%s_%d_%d" % (n, d, ts), [128, 512], F32)[:], Buf("%s_%d_%d" % (n, d, ts)))
                o.tsets.append(tsd)
            for n in ("Y", "vn", "sbf"):
                setattr(o, n, R("%s_%d" % (n, d)))
            o.S32 = self.tsb("a_S32_%d" % d, [128, 512])
            DB.append(o)

        def prep(d, cc, out, ts):
            B = DB[d]
            TS = B.tsets[ts]
            need_out = (cc >= 2) or (not last)
            out["need_out"] = need_out
            col = lambda name, h: A[name][0][:, cc, d * 4 + h:d * 4 + h + 1]
            tok = slice(cc * 128, (cc + 1) * 128)
            m_incl = self.amask[:, 2 * d, :]
            m_strict = self.amask[:, 2 * d + 1, :]
            nmb = lambda lev: nmask[:, d * 7 + lev, :].unsqueeze(1).to_broadcast([128, 4, 128])
            kT, bkT = B.kT.next()
            kTM, bkTM = B.kTM.next()
            vTM, bvTM = B.vTM.next()
            S.dma("sp", kT.rearrange("p (h t) -> p h t", h=4), scr["KA_FM"].rearrange("h p t -> p h t")[:, :, tok], reads=[self.db("KA_FM", cc)], writes=[bkT])
            S.dma("sp", kTM.rearrange("p (h c) -> p h c", h=4), scr["KA_TM"].rearrange("h t c -> t h c")[tok, :, :], reads=[self.db("KA_TM", cc)], writes=[bkTM])
            S.dma("sp", vTM.rearrange("p (h c) -> p h c", h=4), scr["VA_TM"].rearrange("h t c -> t h c")[tok, :, :], reads=[self.db("VA_TM", cc)], writes=[bvTM])
            out.update(kT=(kT, bkT), vTM=(vTM, bvTM))
            if need_out:
                qT, bqT = B.qT.next()
                S.dma("sp", qT.rearrange("p (h t) -> p h t", h=4), scr["QA_FM"].rearrange("h p t -> p h t")[:, :, tok], reads=[self.db("QA_FM", cc)], writes=[bqT])
            yield
            dg, bdg = TS["F0"]
            for h in range(4):
                S.op("dve", lambda e: e.tensor_scalar(out=dg[:, H[h]], in0=self.ident_f[:], scalar1=col("GC", h), scalar2=None, op0=ALU.mult),
                     reads=[self.b_ident_f, A["GC"][1]], writes=[bdg.sub(h)])
            p3, bp3, k3 = yield from acq()
            S.op("pe", lambda e: e.matmul(out=p3[:], lhsT=self.ones_f[:], rhs=dg[:], start=True, stop=True), reads=[self.b_ones_f, bdg], writes=[bp3])
            yield
            Dm2, bDm2 = TS["F1"]
            for h in range(4):
                S.op("dve", lambda e: e.scalar_tensor_tensor(out=Dm2[:, H[h]], in0=p3[:, H[h]], scalar=col("LNBMG", h), in1=m_strict, op0=ALU.add, op1=ALU.add),
                     reads=[bp3, A["LNBMG"][1], self.b_amask], writes=[bDm2.sub(h)])
            if need_out:
                Dm, bDm = TS["F2"]
                for h in range(4):
                    S.op("dve", lambda e: e.scalar_tensor_tensor(out=Dm[:, H[h]], in0=p3[:, H[h]], scalar=col("NEGG", h), in1=m_incl, op0=ALU.add, op1=ALU.add),
                         reads=[bp3, A["NEGG"][1], self.b_amask], writes=[bDm.sub(h)])
                eg, beg = TS["eg"]
                S.op("act", lambda e: e.activation(out=eg, in_=p3[:], func=AF.Exp), reads=[bp3, bDm, bDm2], writes=[beg])
            rel(k3)
            yield
            decb, bdecb = TS["F0"]
            S.op("act", lambda e: e.activation(out=decb[:], in_=Dm2[:], func=AF.Exp), reads=[bDm2], writes=[bdecb])
            p1, bp1, k1 = yield from mm4(kT, bkT, kT, bkT)
            yield
            Ma, bMa = TS["Ma"]
            S.op("dve", lambda e: e.tensor_tensor(out=Ma, in0=p1[:], in1=decb[:], op=ALU.mult), reads=[bp1, bdecb], writes=[bMa])
            rel(k1)
            yield
            psb, bps, kb = yield from tr4(Ma, bMa)
            nml = lambda lev: nmask[:, (1 - d) * 7 + lev, :].unsqueeze(1).to_broadcast([128, 4, 128])
            mlm = lambda lev: TS["MlmA" if lev % 2 else "MlmB"]
            S.op("dve", lambda e: e.tensor_tensor(out=v4(mlm(1)[0]), in0=psb[:, 0:512].rearrange("p (h c) -> p h c", h=4), in1=nml(1), op=ALU.mult),
                 reads=[bps, bnmask], writes=[mlm(1)[1]])
            Ml, bMl = TS["Ml"]
            S.op("act", lambda e: e.copy(out=Ml, in_=psb[:, 0:512]), reads=[bps, mlm(1)[1]], writes=[bMl])
            rel(kb)
            P, bP = TS["Pa"]
            S.op("pool", lambda e: e.tensor_tensor(out=v4(P), in0=v4(Ma), in1=nmb(0), op=ALU.mult), reads=[bMa, bnmask], writes=[bP])
            S.op("pool", lambda e: e.tensor_tensor(out=v4(P), in0=v4(P), in1=self.ident_b[:].unsqueeze(1).to_broadcast([128, 4, 128]), op=ALU.add),
                 reads=[bP, self.b_ident_b], writes=[bP])
            yield
            if need_out:
                dec, bdec = TS["F1"]
                S.op("act", lambda e: e.activation(out=dec[:], in_=Dm[:], func=AF.Exp), reads=[bDm], writes=[bdec])
                p2, bp2, k2 = yield from mm4(kT, bkT, qT, bqT)
                yield
                qk, bqk = B.qk.next()
                S.op("dve", lambda e: e.tensor_tensor(out=qk, in0=p2[:], in1=dec[:], op=ALU.mult), reads=[bp2, bdec], writes=[bqk])
                rel(k2)
                qd, bqd = B.qd.next()
                S.op("pool", lambda e: e.tensor_tensor(out=qd, in0=qT, in1=eg, op=ALU.mult), reads=[bqT, beg], writes=[bqd])
                out.update(qk=(qk, bqk), qd=(qd, bqd))
                yield
            kt, bkt = B.kt.next()
            for h in range(4):
                S.op("act", lambda e: e.activation(out=kt[:, H[h]], in_=kTM[:, H[h]], func=AF.Copy, scale=col("ETAIL", h)),
                     reads=[bkTM, A["ETAIL"][1]], writes=[bkt.sub(h)])
            out.update(kt=(kt, bkt))
            yield
            for lev in range(1, 7):
                cur, bcur = mlm(lev)
                psw, bpsw, kw = yield from mm4(cur, bcur, P, bP)
                psb, bps, kb = yield from tr4(P, bP)
                if lev < 6:
                    nxt_, bnxt_ = mlm(lev + 1)
                    S.op("pool", lambda e: e.tensor_tensor(out=v4(nxt_), in0=v4(Ml), in1=nml(lev + 1), op=ALU.mult), reads=[bMl, bnmask], writes=[bnxt_])
                yield
                W1, bW1 = TS["W1"]
                S.op("act", lambda e: e.copy(out=W1, in_=psw[:]), reads=[bpsw], writes=[bW1])
                rel(kw)
                X, bX = TS["X"]
                S.op("dve", lambda e: e.tensor_copy(out=X, in_=psb[:, 0:512]), reads=[bps], writes=[bX])
                rel(kb)
                yield
                ps2, bps2, k2 = yield from mm4(X, bX, W1, bW1)
                yield
                Pn, bPn = (B.Pf.next() if lev == 6 else TS["Pb" if lev % 2 == 1 else "Pa"])
                S.op("dve", lambda e: e.tensor_tensor(out=Pn, in0=ps2[:], in1=P, op=ALU.add), reads=[bps2, bP], writes=[bPn])
                rel(k2)
                P, bP = Pn, bPn
                yield
            out.update(P=(P, bP))

        def scan(d, cc, ops, st):
            B = DB[d]
            need_out = ops["need_out"]
            col = lambda name, h: A[name][0][:, cc, d * 4 + h:d * 4 + h + 1]
            tok = slice(cc * 128, (cc + 1) * 128)
            kT, bkT = ops["kT"]
            vTM, bvTM = ops["vTM"]
            kt, bkt = ops["kt"]
            P, bP = ops["P"]
            sbf, bsbf = st["sbf"]
            s32, bs32 = B.S32
            px, bpx, kx = yield from mm4(kT, bkT, sbf, bsbf)
            yield
            Y, bY = B.Y.next()
            for h in range(4):
                S.op("dve", lambda e: e.scalar_tensor_tensor(out=Y[:, H[h]], in0=px[:, H[h]], scalar=col("NEGEG", h), in1=vTM[:, H[h]], op0=ALU.mult, op1=ALU.add),
                     reads=[bpx, A["NEGEG"][1], bvTM], writes=[bY.sub(h)])
            rel(kx)
            yield
            pz, bpz, kz = yield from mm4(P, bP, Y, bY)
            yield
            vn, bvn = B.vn.next()
            for h in range(4):
                S.op("act", lambda e: e.activation(out=vn[:, H[h]], in_=pz[:, H[h]], func=AF.Copy, scale=col("BETA", h)), reads=[bpz, A["BETA"][1]], writes=[bvn.sub(h)])
            rel(kz)
            yield
            pS, bpS, kS = yield from mm4(kt, bkt, vn, bvn)
            if need_out:
                qk, bqk = ops["qk"]
                qd, bqd = ops["qd"]
                po, bpo, ko = yield from acq()
                for h in range(4):
                    S.op("pe", lambda e: e.matmul(out=po[:, H[h]], lhsT=qd[:, H[h]], rhs=sbf[:, H[h]], start=True, stop=False), reads=[bqd, bsbf], writes=[bpo])
                    S.op("pe", lambda e: e.matmul(out=po[:, H[h]], lhsT=qk[:, H[h]], rhs=vn[:, H[h]], start=False, stop=True), reads=[bqk, bvn], writes=[bpo])
            yield
            for h in range(4):
                S.op("dve", lambda e: e.scalar_tensor_tensor(out=s32[:, H[h]], in0=s32[:, H[h]], scalar=col("EGL", h), in1=pS[:, H[h]], op0=ALU.mult, op1=ALU.add),
                     reads=[bs32.sub(h), A["EGL"][1], bpS], writes=[bs32.sub(h)])
            rel(kS)
            sbf2, bsbf2 = B.sbf.next()
            S.op("act", lambda e: e.copy(out=sbf2, in_=s32[:]), reads=[bs32], writes=[bsbf2])
            st["sbf"] = (sbf2, bsbf2)
            yield
            if need_out:
                first = cc not in oa_written
                oa_written.add(cc)
                of, bof = ofr.next()
                if first:
                    S.op("act", lambda e: e.copy(out=of[:], in_=po[:]), reads=[bpo], writes=[bof])
                    rel(ko)
                    S.dma("act", scr["OA"][tok, :], of[:], reads=[bof], writes=[self.db("OA", cc)])
                    yield
                else:
                    S.dma("sp", of[:], scr["OA"][tok, :], reads=[self.db("OA", cc)], writes=[bof])
                    za, bza = zar.next()
                    S.dma("sp", za, scr["ZA"][tok, :], reads=[self.db("ZA", cc)], writes=[bza])
                    yield
                    S.op("dve", lambda e: e.tensor_tensor(out=of[:], in0=po[:], in1=of[:], op=ALU.add), reads=[bpo, bof], writes=[bof])
                    rel(ko)
                    t5, bt5 = t5r.next()
                    st_, bst = sta.next()
                    S.op("act", lambda e: e.activation(out=t5[:], in_=of[:], func=AF.Square), reads=[bof], writes=[bt5])
                    yield
                    S.op("dve", lambda e: e.tensor_reduce(out=st_[:, 0:4], in_=t5[:].rearrange("p (h c) -> p h c", h=4), axis=AX.X, op=ALU.add), reads=[bt5], writes=[bst])
                    S.op("act", lambda e: e.activation(out=st_[:, 4:8], in_=st_[:, 0:4], func=AF.Ln, scale=1.0 / 128, bias=eps), reads=[bst, self.b_epsb], writes=[bst])
                    S.op("act", lambda e: e.activation(out=st_[:, 8:12], in_=st_[:, 4:8], func=AF.Exp, scale=-0.5), reads=[bst], writes=[bst])
                    yield
                    ofv = of[:].rearrange("p (h c) -> p h c", h=4)
                    S.op("dve", lambda e: e.tensor_tensor(out=ofv, in0=ofv, in1=st_[:, 8:12].unsqueeze(2).to_broadcast([128, 4, 128]), op=ALU.mult), reads=[bof, bst], writes=[bof])
                    S.op("pool", lambda e: e.tensor_tensor(out=ofv, in0=ofv, in1=anw[:].unsqueeze(1).to_broadcast([128, 4, 128]), op=ALU.mult), reads=[bof, banw], writes=[bof])
                    yield
                    ya, bya = yar.next()
                    S.op("pool", lambda e: e.tensor_tensor(out=ya, in0=of[:], in1=za, op=ALU.mult), reads=[bof, bza], writes=[bya])
                    f, bf = fmo.next()
                    psb, bps, kb = yield from tr4(ya, bya)
                    S.op("act", lambda e: e.copy(out=f, in_=psb[:, 0:512]), reads=[bps], writes=[bf])
                    rel(kb)
                    S.dma("act", scr["Y_FM"][0].rearrange("(k p) t -> p k t", p=128)[:, :, tok], f.rearrange("p (k t) -> p k t", k=4),
                          reads=[bf], writes=[self.db("Y_FM0", cc)])
                    yield

        def chain(d):
            B = DB[d]
            s32, bs32 = B.S32
            S.op("pool", lambda e: e.memset(s32[:], 0.0), writes=[bs32])
            sbf, bsbf = B.sbf.next()
            S.op("pool", lambda e: e.memset(sbf, 0.0), writes=[bsbf])
            st = {"sbf": (sbf, bsbf)}
            order = orders[d]
            n = len(order)
            outs = [dict() for _ in range(n)]
            started = 0
            active = []
            done = set()

            def start_upto(j):
                nonlocal started
                while started <= min(j, n - 1):
                    active.append((started, prep(d, order[started], outs[started], started % KPRE)))
                    started += 1

            def step_preps():
                for item in list(active):
                    try:
                        next(item[1])
                    except StopIteration:
                        active.remove(item)
                        done.add(item[0])
            start_upto(0)
            while 0 not in done:
                step_preps()
                yield
            for i, cc in enumerate(order):
                start_upto(i + KPRE)
                sc = scan(d, cc, outs[i], st)
                sc_done = False
                while not sc_done or (i + 1 < n and (i + 1) not in done):
                    if not sc_done:
                        try:
                            next(sc)
                        except StopIteration:
                            sc_done = True
                    step_preps()
                    yield

        gens = [chain(0), chain(1)]
        while gens:
            for g_ in list(gens):
                try:
                    next(g_)
                except StopIteration:
                    gens.remove(g_)
        S.barrier()
        self.es.close()

    def phase5(self, l):
        S, scr, din = self.S, self.scr, self.din
        last = (l == self.nlayers - 1)
        self.es = ExitStack()
        wbr = self.R1[:, 14336:26624].rearrange("p (r n) -> p r n", n=1024)
        wo = self.R1[:, 26624:34816].rearrange("p (r n) -> p r n", n=1024)
        bwbr, bwo = Buf("wbr"), Buf("wo")

        def load_into(src_view, dst, bdst, nk):
            st, bst = self.wst.next()
            S.dma("sp", st[:, 0:nk, :], src_view, writes=[bst])
            S.op("pool", lambda e: e.tensor_copy(out=dst, in_=st[:, 0:nk, :]), reads=[bst], writes=[bdst])
        wbsrc = din["w_branch"][l].rearrange("b (k p) n -> p (b k) n", p=128)
        for half in range(2):
            for r0, nk in ((0, 8), (8, 4)):
                load_into(wbsrc[:, r0:r0 + nk, half * 512:(half + 1) * 512], wbr[:, r0:r0 + nk, half * 512:(half + 1) * 512], bwbr, nk)
        wosrc = din["w_out"][l].rearrange("(k p) n -> p k n", p=128)
        for half in range(2):
            load_into(wosrc[:, :, half * 512:(half + 1) * 512], wo[:, :, half * 512:(half + 1) * 512], bwo, 8)
        gate_bc, bgate = self.tsb("gate_bc", [128, 2, 1024])
        for s_ in range(2):
            if last and s_ == 1:
                continue
            self.bc_rows(lambda half: gate_bc[:, s_, half * 512:(half + 1) * 512], bgate, lambda kc: self.mod[:, 16 + kc, s_:s_ + 1], self.b_mod, 8)
        if last:
            fnw, bfnw = self.tsb("fnw_bc", [128, 1024])
            S.dma("sp", fnw[:], din["final_norm_w"].partition_broadcast(128), writes=[bfnw])
        yTr = self.trot("p5_yT", [128, 12, 512], BF16, 1)
        gmr = self.trot("p5_gm", [128, 512], BF16, 3)
        accr = self.trot("p5_acc", [128, 512], F32, 2)
        tmr = self.trot("p5_tm", [128, 512], F32, 2)
        mTr = self.trot("p5_mT", [128, 8, 512], BF16, 2)
        xtr = self.trot("p5_xt", [128, 1024], F32, 2)
        t1r = self.trot("p5_t1", [128, 1024], F32, 2)
        sqr, bsqr = self.tsb("p5_sq", [128, 1024])
        st5 = self.trot("p5_st", [128, 4], F32, 3)
        for (t0, n, tile0, ntile) in self.tok_groups():
            if last and t0 == 0:
                continue
            s_ = 1 if t0 == 0 else 0
            yT, byT = yTr.next()
            for br in range(3):
                S.dma("sp", yT[:, br * 4:(br + 1) * 4, 0:n], scr["Y_FM"][br].rearrange("(k p) t -> p k t", p=128)[:, :, t0:t0 + n],
                      reads=[self.db("Y_FM%d" % br, tile0 + i) for i in range(ntile)], writes=[byT.sub(br)])
            mT, bmT = mTr.next()
            for dt in range(8):
                acc, bacc = accr.next()
                for br in range(3):
                    ct = br * 8 + dt
                    gm, bgm = gmr.next()
                    S.dma("sp", gm[:, 0:n], scr["GM_FM"][ct * 128:(ct + 1) * 128, t0:t0 + n],
                          reads=[self.db("GM_FM%d" % ct, tile0 + i) for i in range(ntile)], writes=[bgm])
                    ps, bps = S.ps()
                    for kc in range(4):
                        S.op("pe", lambda e: e.matmul(out=ps[:, 0:n], lhsT=wbr[:, br * 4 + kc, dt * 128:(dt + 1) * 128], rhs=yT[:, br * 4 + kc, 0:n],
                                                      start=(kc == 0), stop=(kc == 3)), reads=[bwbr, byT.sub(br)], writes=[bps])
                    if br == 0:
                        S.op("dve", lambda e: e.tensor_tensor(out=acc[:, 0:n], in0=ps[:, 0:n], in1=gm[:, 0:n], op=ALU.mult), reads=[bps, bgm], writes=[bacc])
                    else:
                        tm, btm = tmr.next()
                        S.op("dve", lambda e: e.tensor_tensor(out=tm[:, 0:n], in0=ps[:, 0:n], in1=gm[:, 0:n], op=ALU.mult), reads=[bps, bgm], writes=[btm])
                        if br == 1:
                            S.op("pool", lambda e: e.tensor_tensor(out=acc[:, 0:n], in0=acc[:, 0:n], in1=tm[:, 0:n], op=ALU.add), reads=[bacc, btm], writes=[bacc])
                        else:
                            S.op("pool", lambda e: e.tensor_tensor(out=mT[:, dt, 0:n], in0=acc[:, 0:n], in1=tm[:, 0:n], op=ALU.add), reads=[bacc, btm], writes=[bmT.sub(dt)])
            for ti in range(ntile):
                tt = tile0 + ti
                xt, bxt = xtr.next()
                if tt < 2:
                    src = (din["ctx"] if l == 0 else scr["CTXS"])[tt * 128:(tt + 1) * 128, :]
                    rd = [] if l == 0 else [self.db("CTXS", tt)]
                else:
                    src = (din["x"] if l == 0 else scr["XS"])[(tt - 2) * 128:(tt - 1) * 128, :]
                    rd = [] if l == 0 else [self.db("XS", tt)]
                S.dma("sp", xt[:], src, reads=rd, writes=[bxt])
                t1, bt1 = t1r.next()
                for cg in range(2):
                    ps, bps = S.ps()
                    for kc in range(8):
                        S.op("pe", lambda e: e.matmul(out=ps[:], lhsT=mT[:, kc, ti * 128:(ti + 1) * 128], rhs=wo[:, kc, cg * 512:(cg + 1) * 512],
                                                      start=(kc == 0), stop=(kc == 7)), reads=[bmT, bwo], writes=[bps])
                    S.op("dve", lambda e: e.tensor_tensor(out=t1[:, cg * 512:(cg + 1) * 512], in0=ps[:], in1=gate_bc[:, s_, cg * 512:(cg + 1) * 512], op=ALU.mult),
                         reads=[bps, bgate], writes=[bt1.sub(cg)])
                S.op("pool", lambda e: e.tensor_tensor(out=t1[:], in0=t1[:], in1=xt[:], op=ALU.add), reads=[bt1, bxt], writes=[bt1])
                if not last:
                    if tt < 2:
                        S.dma("act", scr["CTXS"][tt * 128:(tt + 1) * 128, :], t1[:], reads=[bt1], writes=[self.db("CTXS", tt)])
                    else:
                        S.dma("act", scr["XS"][(tt - 2) * 128:(tt - 1) * 128, :], t1[:], reads=[bt1], writes=[self.db("XS", tt)])
                else:
                    st, bst = st5.next()
                    S.op("act", lambda e: e.activation(out=sqr[:], in_=t1[:], func=AF.Square, accum_out=st[:, 0:1]), reads=[bt1], writes=[bsqr, bst])
                    S.op("dve", lambda e: e.tensor_scalar(out=st[:, 1:2], in0=st[:, 0:1], scalar1=1.0 / D, scalar2=EPS, op0=ALU.mult, op1=ALU.add), reads=[bst], writes=[bst])
                    S.op("act", lambda e: e.activation(out=st[:, 2:3], in_=st[:, 1:2], func=AF.Ln), reads=[bst], writes=[bst])
                    S.op("act", lambda e: e.activation(out=st[:, 3:4], in_=st[:, 2:3], func=AF.Exp, scale=-0.5), reads=[bst], writes=[bst])
                    S.op("dve", lambda e: e.scalar_tensor_tensor(out=xt[:], in0=t1[:], scalar=st[:, 3:4], in1=fnw[:], op0=ALU.mult, op1=ALU.mult),
                         reads=[bt1, bst, bfnw, bxt], writes=[bxt])
                    S.dma("act", self.out[(tt - 2) * 128:(tt - 1) * 128, :], xt[:], reads=[bxt], writes=[self.db("OUT", tt)])
        S.barrier()
        self.es.close()

    def dump(self, name, ap, reads, shape, dtype=F32):
        o = self.nc.dram_tensor("dbg_" + name, shape, dtype, kind="ExternalOutput").ap()
        b = Buf("dbg_" + name)
        self.S.dma("sp", o, ap, reads=reads, writes=[b])
        self._dbgbufs.append(b)

    def dump_p2(self):
        self.dump("mod", self.mod[:], [self.b_mod], [128, 24, 2])
        self.dump("SCR", self.SCR[:], [self.b_SCR], [128, NT, 16])
        for n in self.asc:
            self.dump(n, self.asc[n][0][:], [self.asc[n][1]], [128, NT, 8])
        self.dump("SH2", self.SH2[:], [self.b_SH2], [128, NT, 8])
        self.dump("kmx", self.kmx[:], [self.b_kmx], [128, 4])
        self.dump("hT", self.hT, self.b_hT, [128, 8, NTOK], BF16)

    def program(self):
        S = self.S
        self._dbgbufs = []
        self.marks = []
        mark = lambda n: self.marks.append((n, {k: v.count for k, v in S.engs.items()}))
        for l in range(self.nlayers):
            mark("L%d start" % l)
            self.phase0(l)
            mark("L%d p0 done" % l)
            if self.stop == "p0":
                self.dump("mod", self.mod[:], [self.b_mod], [128, 24, 2])
                self.dump("Afm", self.Afm[:], [self.b_Afm], [128, 8, 2])
                self.dump("convw", self.convw[:], [self.b_convw], [128, 12, 5])
                self.dump("scol", self.scol[:], [self.b_scol], [128, 8, 2])
                break
            self.phase1(l)
            mark("L%d p1 done" % l)
            if self.stop == "p1":
                self.dump("hT", self.hT, self.b_hT, [128, 8, NTOK], BF16)
                break
            self.phase2(l)
            if self.stop is not None and self.stop.startswith("p2"):
                break
            mark("L%d p2 done" % l)
            if self.stop == "b":
                self.core_b(l)
                break
            self.core_bc(l)
            mark("L%d B done" % l)
            mark("L%d C done" % l)
            if self.stop == "c":
                break
            self.core_a(l)
            mark("L%d A done" % l)
            if self.stop == "a":
                break
            self.phase5(l)
            mark("L%d p5 done" % l)
            if self.stop == "p5":
                break
        S.barrier()
        return self.nc


def shard_inputs(inputs, b):
    m = {}
    for n in IN_SHAPES:
        a = np.asarray(inputs[n], dtype=np.float32)
        if n in ("x", "c", "ctx"):
            a = a[b]
        m[n] = np.ascontiguousarray(a)
    return m


_CACHE = {}


def kernel(**inputs):
    if "nc" not in _CACHE:
        _CACHE["nc"] = MK().program()
        _CACHE["consts"] = host_consts()
    nc = _CACHE["nc"]
    in_maps = []
    for b in range(8):
        m = shard_inputs(inputs, b)
        m.update(_CACHE["consts"])
        in_maps.append(m)
    res = run_bass_kernel_spmd(nc, in_maps, core_ids=list(range(8)))
    return np.stack([np.asarray(r["out"], dtype=np.float32) for r in res.results], axis=0)
```

```python
from contextlib import ExitStack
import numpy as np
import concourse.bass as bass
import concourse.mybir as mybir
from concourse.bass_utils import run_bass_kernel_spmd

F32 = mybir.dt.float32
BF16 = mybir.dt.bfloat16
AF = mybir.ActivationFunctionType
ALU = mybir.AluOpType
AX = mybir.AxisListType

T = 4096
LC = 256
D = 1024
NT = 34
NTOK = 4352
INW = 8464
CH = 128
EPS = 1e-6
NEG = -1.0e5
O_AQ, O_AK, O_AV, O_AZ, O_AB, O_BQ, O_BKV, O_BZ, O_CQ, O_CK, O_CV, O_CZ, O_MG = (
    0, 512, 1024, 1536, 2048, 2064, 2576, 2832, 3344, 3856, 4368, 4880, 5392)


class Buf:
    __slots__ = ("name", "w", "r", "parts")

    def __init__(self, name):
        self.name = name
        self.w = None
        self.r = {}
        self.parts = {}

    def sub(self, p):
        return Sub(self, p)


class Sub:
    __slots__ = ("parent", "p", "name")

    def __init__(self, parent, p):
        self.parent = parent
        self.p = p
        self.name = "%s[%s]" % (parent.name, p)

    def _slot(self):
        return self.parent.parts.setdefault(self.p, [None, {}])


class Eng:
    def __init__(self, key, e, sem):
        self.key = key
        self.e = e
        self.sem = sem
        self.count = 0
        self.waited = {}


class Sched:
    def __init__(self, nc, n_dma_sems=40):
        self.nc = nc
        self.sems = {}
        self.engs = {}
        for key, e in (("pe", nc.tensor), ("act", nc.scalar), ("dve", nc.vector), ("pool", nc.gpsimd), ("sp", nc.sync)):
            s = nc.alloc_semaphore("sem_" + key)
            self.sems[key] = s
            self.engs[key] = Eng(key, e, s)
        self.dma_sems = []
        for i in range(n_dma_sems):
            k = "dma%d" % i
            self.sems[k] = nc.alloc_semaphore("sem_" + k)
            self.dma_sems.append([k, 0])
        self.dma_rr = 0
        self.nops = 0
        self.clocks = {}
        self.psum = []
        self.ps_rr = 0
        for i in range(8):
            self.psum.append((nc.alloc_psum_tensor("psb%d" % i, [128, 512], F32), Buf("psb%d" % i)))

    def ps(self, pool=None):
        if pool is not None:
            banks, st = pool
            r = self.psum[banks[st[0] % len(banks)]]
            st[0] += 1
            return r
        r = self.psum[self.ps_rr]
        self.ps_rr = (self.ps_rr + 1) % 8
        return r

    def _deps(self, eng, reads, writes, is_dma):
        deps = {}

        def add(tok, same_ok):
            if tok is None:
                return
            k, v = tok
            if k == eng.key and not same_ok:
                return
            if deps.get(k, 0) < v:
                deps[k] = v

        same = is_dma or eng.key != "pe"
        for b in reads:
            if isinstance(b, Sub):
                add(b.parent.w, True)
                add(b._slot()[0], True)
            else:
                add(b.w, True)
                for pw, pr in b.parts.values():
                    add(pw, True)
        for b in writes:
            if isinstance(b, Sub):
                add(b.parent.w, same)
                for k, v in b.parent.r.items():
                    add((k, v), same)
                pw, pr = b._slot()
                add(pw, same)
                for k, v in pr.items():
                    add((k, v), same)
            else:
                add(b.w, same)
                for k, v in b.r.items():
                    add((k, v), same)
                for pw, pr in b.parts.values():
                    add(pw, same)
                    for k, v in pr.items():
                        add((k, v), same)
        for k, v in sorted(deps.items(), key=lambda kv: -kv[1]):
            self._need(eng, k, v)

    def _record(self, key, val, reads, writes):
        for b in reads:
            r = b._slot()[1] if isinstance(b, Sub) else b.r
            if r.get(key, 0) < val:
                r[key] = val
        for b in writes:
            if isinstance(b, Sub):
                sl = b._slot()
                sl[0] = (key, val)
                sl[1] = {}
            else:
                b.w = (key, val)
                b.r = {}
                b.parts = {}

    def _need(self, eng, k, v):
        if eng.waited.get(k, 0) >= v:
            return
        eng.e.wait_ge(self.sems[k], v)
        eng.waited[k] = v
        clk = self.clocks.get((k, v))
        if clk:
            w = eng.waited
            for k2, v2 in clk.items():
                if w.get(k2, 0) < v2:
                    w[k2] = v2

    def op(self, ek, fn, reads=(), writes=()):
        eng = self.engs[ek]
        self._deps(eng, reads, writes, False)
        ins = fn(eng.e)
        self.nops += 1
        eng.count += 1
        ins.then_inc(eng.sem, 1)
        clk = dict(eng.waited)
        clk.pop(eng.key, None)
        self.clocks[(eng.key, eng.count)] = clk
        self._record(eng.key, eng.count, reads, writes)
        return ins

    def dma(self, ek, out, in_, reads=(), writes=(), **kw):
        eng = self.engs[ek]
        self._deps(eng, reads, writes, True)
        slot = self.dma_sems[self.dma_rr]
        self.dma_rr = (self.dma_rr + 1) % len(self.dma_sems)
        k, uses = slot
        if uses > 0:
            self._need(eng, k, 16 * uses)
        ins = eng.e.dma_start(out=out, in_=in_, **kw)
        self.nops += 1
        slot[1] = uses + 1
        val = 16 * (uses + 1)
        ins.then_inc(self.sems[k], 16)
        clk = dict(eng.waited)
        clk.pop(eng.key, None)
        self.clocks[(k, val)] = clk
        self._record(k, val, reads, writes)
        return ins

    def wait_all(self, ek, bufs):
        eng = self.engs[ek]
        for b in bufs:
            toks = []
            if b.w is not None:
                toks.append(b.w)
            toks.extend(b.r.items())
            for pw, pr in b.parts.values():
                if pw is not None:
                    toks.append(pw)
                toks.extend(pr.items())
            for k, v in toks:
                if eng.waited.get(k, 0) < v:
                    eng.e.wait_ge(self.sems[k], v)
                    eng.waited[k] = v

    def barrier(self):
        for eng in self.engs.values():
            for o in self.engs.values():
                if o.key != eng.key and o.count > 0 and eng.waited.get(o.key, 0) < o.count:
                    eng.e.wait_ge(self.sems[o.key], o.count)
                    eng.waited[o.key] = o.count
            for k, uses in self.dma_sems:
                if uses > 0 and eng.waited.get(k, 0) < 16 * uses:
                    eng.e.wait_ge(self.sems[k], 16 * uses)
                    eng.waited[k] = 16 * uses


class Rot:
    def __init__(self, alloc, name, shape, dtype, n=2):
        self.items = [(alloc("%s%d" % (name, i), shape, dtype), Buf("%s%d" % (name, i))) for i in range(n)]
        self.i = 0

    def next(self):
        r = self.items[self.i]
        self.i = (self.i + 1) % len(self.items)
        return r


def host_consts():
    f = np.float32
    j = np.arange(128)[:, None]
    i = np.arange(128)[None, :]
    c = {}
    c["k_ident"] = np.eye(128, dtype=f)
    c["k_ones"] = np.ones((128, 128), f)
    am = np.zeros((4, 128, 128), f)
    am[0] = np.where(i >= j, 0.0, NEG)
    am[1] = np.where(i > j, 0.0, NEG)
    am[2] = np.where(i <= j, 0.0, NEG)
    am[3] = np.where(i < j, 0.0, NEG)
    c["k_amask"] = am
    tri = np.zeros((2, 128, 128), f)
    tri[0] = (j <= i)
    tri[1] = (j >= i)
    c["k_tri"] = tri
    bm = np.zeros((2, 128, 512), f)
    bm[0] = np.tile((j >= i).astype(f), (1, 4))
    bm[1] = np.tile((j <= i).astype(f), (1, 4))
    c["k_bmask"] = bm
    cm = np.zeros((6, 128, 128), f)
    cm[0] = np.maximum(i - j, 0)
    cm[1] = np.maximum(j - i, 0)
    cm[2] = (i > j)
    cm[3] = (j > i)
    cm[4] = np.broadcast_to(i + 1, (128, 128))
    cm[5] = np.broadcast_to(CH - i, (128, 128))
    c["k_cm"] = cm
    nm = np.zeros((14, 128, 128), f)
    for d in range(2):
        for lev in range(7):
            b = 1 << lev
            same = (j // (2 * b)) == (i // (2 * b))
            if d == 0:
                m = same & ((j % (2 * b)) < b) & ((i % (2 * b)) >= b)
            else:
                m = same & ((i % (2 * b)) < b) & ((j % (2 * b)) >= b)
            nm[d * 7 + lev] = -m.astype(f)
    c["k_nm"] = nm
    cj = np.zeros((128, 8), f)
    cj[:, 0:4] = (CH - 1 - np.arange(128))[:, None]
    cj[:, 4:8] = np.arange(128)[:, None]
    c["k_cj"] = cj
    t = np.arange(T)
    inv16 = (f(10000.0) ** (-np.arange(16, dtype=f) / f(16))).astype(f)
    ar = (t // 64).astype(f)[:, None] * inv16[None, :]
    ac = (t % 64).astype(f)[:, None] * inv16[None, :]
    ab = np.concatenate([ar, ac], axis=1).astype(f)
    c["k_cosb"] = np.tile(np.cos(ab).astype(f), (1, 8))
    c["k_sinb"] = np.tile(np.sin(ab).astype(f), (1, 8))
    inv64 = (f(10000.0) ** (-np.arange(64, dtype=f) / f(64))).astype(f)
    ang = t.astype(f)[:, None] * inv64[None, :]
    c["k_cosc"] = np.cos(ang).astype(f)
    c["k_sinc"] = np.sin(ang).astype(f)
    return c


CONST_SHAPES = {"k_ident": [128, 128], "k_ones": [128, 128], "k_amask": [4, 128, 128], "k_tri": [2, 128, 128],
                "k_bmask": [2, 128, 512], "k_cm": [6, 128, 128], "k_cj": [128, 8], "k_nm": [14, 128, 128],
                "k_cosb": [T, 256], "k_sinb": [T, 256], "k_cosc": [T, 64], "k_sinc": [T, 64]}

IN_SHAPES = {"x": [T, D], "c": [D], "ctx": [LC, D], "c_ctx": [D], "w_ada": [2, D, 3 * D], "b_ada": [2, 3 * D],
             "norm_w": [2, D], "w_in": [2, D, INW], "a_conv_w": [2, 5, 1536], "a_log": [2, 8], "a_dt_bias": [2, 8],
             "a_norm_w": [2, 128], "b_sink": [2, 8], "c_decay": [2, 8], "c_norm_w": [2, 512],
             "w_branch": [2, 3, 512, D], "w_out": [2, D, D], "final_norm_w": [D]}

SCRATCH = {"XS": ([T, D], F32), "CTXS": ([LC, D], F32),
           "QA_FM": ([4, 128, NTOK], BF16), "KA_FM": ([4, 128, NTOK], BF16),
           "KA_TM": ([4, NTOK, 128], BF16), "VA_TM": ([4, NTOK, 128], BF16),
           "ZA": ([NTOK, 512], BF16), "ZB": ([NTOK, 512], BF16), "ZC": ([NTOK, 512], BF16),
           "OA": ([NTOK, 512], F32),
           "QB_FM": ([8, 128, NTOK], BF16),
           "QC_FM": ([4, 128, NTOK], BF16), "KC_FM": ([4, 128, NTOK], BF16),
           "KC_TM": ([NTOK, 512], BF16), "VC_TM": ([NTOK, 512], BF16),
           "SCB": ([NT, 128, 512], BF16), "SCF": ([NT, 128, 512], BF16),
           "KB_FM": ([2, 128, NTOK], BF16), "VB_TM": ([NTOK, 2, 65], BF16),
           "GM_FM": ([3 * D, NTOK], BF16), "Y_FM": ([3, 512, NTOK], BF16)}


class MK:
    def __init__(self, nlayers=2, dbg=(), stop=None):
        nc = bass.Bass("TRN2", target_bir_lowering=False, dynamic_dma_scratch_size=1024)
        self.nc = nc
        self.S = Sched(nc)
        self.nlayers = nlayers
        self.stop = stop
        self.din = {}
        for n, shp in list(IN_SHAPES.items()) + list(CONST_SHAPES.items()):
            self.din[n] = nc.dram_tensor(n, shp, F32, kind="ExternalInput").ap()
        self.out = nc.dram_tensor("out", [T, D], F32, kind="ExternalOutput").ap()
        self.scr = {}
        self._db = {}
        for n, (shp, dt) in SCRATCH.items():
            kind = "ExternalOutput" if n in dbg else "Internal"
            self.scr[n] = nc.dram_tensor(n, shp, dt, kind=kind).ap()
        self.dbg = dbg
        self._tcount = 0
        self.alloc()

    def db(self, name, tt):
        k = (name, tt)
        if k not in self._db:
            self._db[k] = Buf("%s_%d" % k)
        return self._db[k]

    def dball(self, name):
        return [self.db(name, tt) for tt in range(NT)]

    def sb(self, name, shape, dtype=F32):
        return self.nc.alloc_sbuf_tensor(name, shape, dtype), Buf(name)

    def rot(self, name, shape, dtype=F32, n=2):
        return Rot(lambda nm, sh, dt: self.nc.alloc_sbuf_tensor(nm, sh, dt), name, shape, dtype, n)

    def _talloc(self, name, shape, dtype):
        self._tcount += 1
        return self.es.enter_context(self.nc.sbuf_tensor("%s_t%d" % (name, self._tcount), shape, dtype))

    def tsb(self, name, shape, dtype=F32):
        return self._talloc(name, shape, dtype), Buf(name)

    def trot(self, name, shape, dtype=F32, n=2):
        return Rot(self._talloc, name, shape, dtype, n)

    def alloc(self):
        nc, S, din = self.nc, self.S, self.din
        self.ident_f, self.b_ident_f = self.sb("ident_f", [128, 128])
        self.ones_f, self.b_ones_f = self.sb("ones_f", [128, 128])
        self.ident_b, self.b_ident_b = self.sb("ident_b", [128, 128], BF16)
        self.ones_b, self.b_ones_b = self.sb("ones_b", [128, 128], BF16)
        self.amask, self.b_amask = self.sb("amask", [128, 4, 128])
        self.tri, self.b_tri = self.sb("tri", [128, 2, 128])
        self.bmask, self.b_bmask = self.sb("bmask", [128, 2, 512], BF16)
        self.cm, self.b_cm = self.sb("cm", [128, 6, 128])
        self.cj, self.b_cj = self.sb("cj", [128, 8])
        S.dma("sp", self.ident_f[:], din["k_ident"], writes=[self.b_ident_f])
        S.dma("sp", self.ones_f[:], din["k_ones"], writes=[self.b_ones_f])
        S.dma("sp", self.amask[:], din["k_amask"].rearrange("a p n -> p a n"), writes=[self.b_amask])
        S.dma("sp", self.tri[:], din["k_tri"].rearrange("a p n -> p a n"), writes=[self.b_tri])
        S.dma("sp", self.cm[:], din["k_cm"].rearrange("a p n -> p a n"), writes=[self.b_cm])
        S.dma("sp", self.cj[:], din["k_cj"], writes=[self.b_cj])
        S.op("dve", lambda e: e.tensor_copy(out=self.ident_b[:], in_=self.ident_f[:]), reads=[self.b_ident_f], writes=[self.b_ident_b])
        S.op("dve", lambda e: e.tensor_copy(out=self.ones_b[:], in_=self.ones_f[:]), reads=[self.b_ones_f], writes=[self.b_ones_b])
        self.R1, self.b_R1 = self.sb("R1", [128, 8 * NTOK], BF16)
        self.hT = self.R1[:].rearrange("p (k t) -> p k t", k=8)
        self.b_hT = [Buf("hT%d" % i) for i in range(NT)]
        self.wst = self.rot("wst", [128, 8, 512], F32, 1)
        self.wb = self.rot("wb", [128, 8, 512], BF16, 4)
        self.scol, self.b_scol = self.sb("scol", [128, 8, 2])
        self.mod, self.b_mod = self.sb("mod", [128, 24, 2])
        self.nwcol, self.b_nwcol = self.sb("nwcol", [128, 8])
        self.badacol, self.b_badacol = self.sb("badacol", [128, 24])
        self.Afm, self.b_Afm = self.sb("Afm", [128, 8, 2])
        self.convw, self.b_convw = self.sb("convw", [128, 12, 5])
        self.rowtmp = self.rot("rowtmp", [128, 128], F32, 2)
        for it in self.rowtmp.items:
            S.op("pool", lambda e: e.memset(it[0][:], 0.0), writes=[it[1]])
        self.gdiag = self.rot("gdiag", [128, 512], F32, 2)
        self.SCR, self.b_SCR = self.sb("SCR", [128, NT, 16])
        self.asc = {}
        for n in ("BETA", "GC", "NEGG", "LNBMG", "NEGEG", "ETAIL", "EGL"):
            self.asc[n] = self.sb("asc_" + n, [128, NT, 8])
        self.SH2, self.b_SH2 = self.sb("SH2", [128, NT, 8])
        self.kmx, self.b_kmx = self.sb("kmx", [128, 4])
        self.par = {}
        for n, w in (("a_log", 8), ("a_dt_bias", 8), ("b_sink", 8), ("c_decay", 8), ("a_norm_w", 128), ("c_norm_w", 512)):
            self.par[n] = self.sb("par_" + n, [128, w])
        self.epsb, self.b_epsb = self.sb("epsb", [128, 4])
        for col, val in ((0, EPS), (1, -0.5 * float(np.log(128.0))), (2, 0.0), (3, 1.0)):
            S.op("pool", lambda e: e.memset(self.epsb[:, col:col + 1], val), writes=[self.b_epsb])

    def load_cols(self, src_rows, n, dst, bdst, func=None):
        S = self.S
        rt, brt = self.rowtmp.next()
        S.dma("sp", rt[0:n, :], src_rows, writes=[brt])
        if func is not None:
            S.op("act", lambda e: e.activation(out=rt[0:n, :], in_=rt[0:n, :], func=func), reads=[brt], writes=[brt])
        ps, bps = S.ps()
        S.op("pe", lambda e: e.transpose(out=ps[:, 0:128], in_=rt[:, :], identity=self.ident_f[:]),
             reads=[brt, self.b_ident_f], writes=[bps])
        S.op("dve", lambda e: e.tensor_copy(out=dst, in_=ps[:, 0:n]), reads=[bps], writes=[bdst])

    def w_plan(self, src2d, groups):
        self._wsrc = src2d
        self._wgroups = list(groups)
        self._wi = 0
        self._wq = []
        self._w_issue()

    def _w_issue(self):
        if self._wi < len(self._wgroups):
            c0, ncols = self._wgroups[self._wi]
            self._wi += 1
            self._wq.append(((c0, ncols), self._load_w_raw(self._wsrc, c0, ncols)))

    def load_w(self, src2d, c0, ncols, nk=8):
        if getattr(self, "_wq", None):
            key, val = self._wq.pop(0)
            assert key == (c0, ncols), (key, c0, ncols)
            self._w_issue()
            return val
        return self._load_w_raw(src2d, c0, ncols, nk)

    def _load_w_raw(self, src2d, c0, ncols, nk=8):
        S = self.S
        st, bst = self.wst.next()
        wb, bwb = self.wb.next()
        S.dma("sp", st[:, 0:nk, 0:ncols], src2d.rearrange("(k p) n -> p k n", p=128)[:, :, c0:c0 + ncols], writes=[bst])
        S.op("pool", lambda e: e.tensor_copy(out=wb[:, 0:nk, 0:ncols], in_=st[:, 0:nk, 0:ncols]), reads=[bst], writes=[bwb])
        return wb, bwb

    def phase0(self, l):
        S, din = self.S, self.din
        self.es = ExitStack()
        for n in self.par:
            t, b = self.par[n]
            S.dma("sp", t[:], din[n][l].partition_broadcast(128), writes=[b])
        self.load_cols(din["c"].rearrange("(k p) -> k p", p=128), 8, self.scol[:, :, 0], self.b_scol, AF.Silu)
        self.load_cols(din["c_ctx"].rearrange("(k p) -> k p", p=128), 8, self.scol[:, :, 1], self.b_scol, AF.Silu)
        self.load_cols(din["norm_w"][l].rearrange("(k p) -> k p", p=128), 8, self.nwcol[:], self.b_nwcol)
        self.load_cols(din["b_ada"][l].rearrange("(k p) -> k p", p=128), 24, self.badacol[:], self.b_badacol)
        cw, bcw = self.tsb("cwrows", [128, 1536])
        S.op("pool", lambda e: e.memset(cw[:], 0.0), writes=[bcw])
        S.dma("sp", cw[0:5, :], din["a_conv_w"][l], writes=[bcw])
        for ct in range(12):
            ps, bps = S.ps()
            S.op("pe", lambda e: e.transpose(out=ps[:, 0:128], in_=cw[:, ct * 128:(ct + 1) * 128], identity=self.ident_f[:]),
                 reads=[bcw, self.b_ident_f], writes=[bps])
            S.op("dve", lambda e: e.tensor_copy(out=self.convw[:, ct, :], in_=ps[:, 0:5]), reads=[bps], writes=[self.b_convw])
        psm, bpsm = S.ps()
        for g in range(6):
            st, bst = self.wst.next()
            S.dma("sp", st[:], din["w_ada"][l].rearrange("(k p) n -> p k n", p=128)[:, :, g * 512:(g + 1) * 512], writes=[bst])
            for jl in range(4):
                j = g * 4 + jl
                for kc in range(8):
                    S.op("pe", lambda e: e.matmul(out=psm[:, 2 * j:2 * j + 2], lhsT=st[:, kc, jl * 128:(jl + 1) * 128], rhs=self.scol[:, kc, :],
                                                  start=(kc == 0), stop=(kc == 7)),
                         reads=[bst, self.b_scol], writes=[bpsm])
        S.op("dve", lambda e: e.tensor_tensor(out=self.mod[:], in0=psm[:, 0:48].rearrange("p (j s) -> p j s", s=2),
                                              in1=self.badacol[:].unsqueeze(2).to_broadcast([128, 24, 2]), op=ALU.add),
             reads=[bpsm, self.b_badacol], writes=[self.b_mod])
        S.op("dve", lambda e: e.scalar_tensor_tensor(out=self.Afm[:], in0=self.mod[:, 8:16, :], scalar=1.0,
                                                     in1=self.nwcol[:].unsqueeze(2).to_broadcast([128, 8, 2]), op0=ALU.add, op1=ALU.mult),
             reads=[self.b_mod, self.b_nwcol], writes=[self.b_Afm])
        S.barrier()
        self.es.close()

    def bc_rows(self, dst_fn, bdst, col_fn, bcol, nchunks):
        S = self.S
        for half in range(nchunks // 4):
            t, bt = self.gdiag.next()
            for q in range(4):
                kc = half * 4 + q
                S.op("dve", lambda e: e.tensor_scalar(out=t[:, q * 128:(q + 1) * 128], in0=self.ident_f[:], scalar1=col_fn(kc),
                                                      scalar2=None, op0=ALU.mult),
                     reads=[self.b_ident_f, bcol], writes=[bt])
            ps, bps = S.ps()
            S.op("pe", lambda e: e.matmul(out=ps[:], lhsT=self.ones_f[:], rhs=t[:], start=True, stop=True),
                 reads=[self.b_ones_f, bt], writes=[bps])
            S.op("act", lambda e: e.copy(out=dst_fn(half), in_=ps[:]), reads=[bps], writes=[bdst])

    def phase1(self, l):
        S, din = self.S, self.din
        self.es = ExitStack()
        self.p1_xt = self.trot("p1_xt", [128, 1024], F32, 2)
        self.p1_sq, self.b_p1_sq = self.tsb("p1_sq", [128, 1024])
        self.p1_st = self.trot("p1_st", [128, 4], F32, 2)
        for tt in range(NT):
            s = 1 if tt < 2 else 0
            if tt < 2:
                src = (din["ctx"] if l == 0 else self.scr["CTXS"])[tt * 128:(tt + 1) * 128, :]
                rd = [] if l == 0 else [self.db("CTXS", tt)]
            else:
                src = (din["x"] if l == 0 else self.scr["XS"])[(tt - 2) * 128:(tt - 1) * 128, :]
                rd = [] if l == 0 else [self.db("XS", tt)]
            xt, bxt = self.p1_xt.next()
            st, bst = self.p1_st.next()
            S.dma("sp", xt[:], src, reads=rd, writes=[bxt])
            S.op("act", lambda e: e.activation(out=self.p1_sq[:], in_=xt[:], func=AF.Square, accum_out=st[:, 0:1]),
                 reads=[bxt], writes=[self.b_p1_sq, bst])
            S.op("dve", lambda e: e.tensor_scalar(out=st[:, 1:2], in0=st[:, 0:1], scalar1=1.0 / D, scalar2=EPS, op0=ALU.mult, op1=ALU.add),
                 reads=[bst], writes=[bst])
            S.op("act", lambda e: e.activation(out=st[:, 2:3], in_=st[:, 1:2], func=AF.Ln), reads=[bst], writes=[bst])
            S.op("act", lambda e: e.activation(out=st[:, 3:4], in_=st[:, 2:3], func=AF.Exp, scale=-0.5), reads=[bst], writes=[bst])
            S.op("act", lambda e: e.activation(out=xt[:], in_=xt[:], func=AF.Copy, scale=st[:, 3:4]),
                 reads=[bst, bxt], writes=[bxt])
            for half in range(2):
                ps, bps = S.ps()
                for q in range(4):
                    kc = half * 4 + q
                    S.op("pe", lambda e: e.transpose(out=ps[:, q * 128:(q + 1) * 128], in_=xt[:, kc * 128:(kc + 1) * 128], identity=self.ident_f[:]),
                         reads=[bxt, self.b_ident_f], writes=[bps])
                for q in range(4):
                    kc = half * 4 + q
                    dst = self.hT[:, kc, tt * 128:(tt + 1) * 128]
                    if q % 2 == 0:
                        S.op("dve", lambda e: e.tensor_scalar(out=dst, in0=ps[:, q * 128:(q + 1) * 128], scalar1=self.Afm[:, kc, s:s + 1],
                                                              scalar2=self.mod[:, kc, s:s + 1], op0=ALU.mult, op1=ALU.add),
                             reads=[bps, self.b_Afm, self.b_mod], writes=[self.b_hT[tt]])
                    else:
                        S.op("act", lambda e: e.activation(out=dst, in_=ps[:, q * 128:(q + 1) * 128], func=AF.Identity,
                                                           scale=self.Afm[:, kc, s:s + 1], bias=self.mod[:, kc, s:s + 1]),
                             reads=[bps, self.b_Afm, self.b_mod], writes=[self.b_hT[tt]])
        S.barrier()
        self.es.close()

    def tok_groups(self):
        g = [(0, 256, 0, 2)]
        for i in range(8):
            g.append((256 + i * 512, 512, 2 + 4 * i, 4))
        return g

    class WStream:
        def __init__(self, mk, src2d, groups, bufs):
            self.mk, self.src, self.groups, self.bufs = mk, src2d, list(groups), bufs
            self.i = 0
            self.q = []
            self._issue()

        def _issue(self):
            if self.i < len(self.groups):
                c0, ncols = self.groups[self.i]
                wb, bwb = self.bufs[self.i % len(self.bufs)]
                S = self.mk.S
                st, bst = self.mk.wst.next()
                S.dma("sp", st[:, :, 0:ncols], self.src.rearrange("(k p) n -> p k n", p=128)[:, :, c0:c0 + ncols], writes=[bst])
                S.op("pool", lambda e: e.tensor_copy(out=wb[:, :, 0:ncols], in_=st[:, :, 0:ncols]), reads=[bst], writes=[bwb])
                self.q.append(((c0, ncols), (wb, bwb)))
                self.i += 1

        def get(self, c0, ncols):
            key, val = self.q.pop(0)
            assert key == (c0, ncols), (key, c0, ncols)
            self._issue()
            return val

    def proj_tm_gen(self, l, c0, ncols, handler, ws):
        S = self.S
        if hasattr(self, "marks"):
            self.marks.append(("  L%d tm@%d" % (l, c0), {k: v.count for k, v in S.engs.items()}))
        wb, bwb = ws.get(c0, ncols)
        prev = None
        for tt in range(NT):
            ps, bps = S.ps()
            for kc in range(8):
                S.op("pe", lambda e: e.matmul(out=ps[:, 0:ncols], lhsT=self.hT[:, kc, tt * 128:(tt + 1) * 128], rhs=wb[:, kc, 0:ncols],
                                              start=(kc == 0), stop=(kc == 7)),
                     reads=[self.b_hT[tt], bwb], writes=[bps])
            if prev is not None:
                handler(*prev)
            prev = (tt, ps, bps)
            yield
        handler(*prev)
        yield

    def proj_tm(self, l, c0, ncols, handler, ws):
        for _ in self.proj_tm_gen(l, c0, ncols, handler, ws):
            pass

    @staticmethod
    def run_pair(g1, g2):
        gens = [g1, g2]
        while gens:
            for g_ in list(gens):
                try:
                    next(g_)
                except StopIteration:
                    gens.remove(g_)

    def transpose_out(self, src_fn, n, rows, dst, dst_buf_list, tag, pool=None):
        S = self.S
        ps, bps = S.ps(pool)
        psb = ps[:].bitcast(BF16)
        for i in range(n):
            src, bsrc = src_fn(i)
            S.op("pe", lambda e: e.transpose(out=psb[0:rows, i * 128:(i + 1) * 128], in_=src, identity=self.ident_b[:]),
                 reads=[bsrc, self.b_ident_b], writes=[bps])
        S.op("act", lambda e: e.copy(out=dst, in_=psb[0:rows, 0:n * 128]), reads=[bps], writes=dst_buf_list)

    def phase2(self, l):
        S, din, scr = self.S, self.din, self.scr
        last = (l == self.nlayers - 1)
        self.es = ExitStack()
        self.zt = self.trot("zt", [128, 512], BF16, 2)
        self.tmpA = self.trot("tmpA", [128, 512], F32, 2)
        self.tmpB = self.trot("tmpB", [128, 512], F32, 2)
        self.tmo = self.trot("tmo", [128, 512], BF16, 3)
        self.fmo = self.trot("fmo", [128, 512], BF16, 3)
        self.qa = self.trot("qa", [128, 8, 128], BF16, 2)
        self.ka = self.trot("ka", [128, 2, 128], BF16, 2)
        self.vb = self.trot("vb", [128, 2, 65], BF16, 2)
        self.qaT = self.trot("qaT", [128, 8, 128], BF16, 2)
        self.kaT = self.trot("kaT", [128, 2, 128], BF16, 2)
        self.csc = self.trot("csc", [128, 2, 64], F32, 3)
        self.csb = self.trot("csb", [128, 2, 256], F32, 3)
        self.st8 = self.trot("st8", [128, 24], F32, 4)
        self.tmp8 = self.trot("tmp8", [128, NT, 8], F32, 4)
        for n in ("LNB", "GRAW", "GT"):
            self.asc[n] = self.tsb("asc_" + n, [128, NT, 8])
        self.KM, self.b_KM = self.tsb("KM", [128, 2])
        self.half8, self.b_half8 = self.tsb("half8", [128, 8])
        S.op("pool", lambda e: e.memset(self.half8[:], 0.5), writes=[self.b_half8])
        self.rowbuf, self.b_rowbuf = self.tsb("rowbuf", [128, 4360])
        self.slrow, self.b_slrow = self.tsb("slrow", [128, NTOK])
        S.op("pool", lambda e: e.memset(self.rowbuf[:], 0.0), writes=[self.b_rowbuf])
        for it in self.vb.items:
            S.op("pool", lambda e: e.memset(it[0][:], 1.0), writes=[it[1]])
        for it in self.ka.items + self.qa.items:
            S.op("pool", lambda e: e.memset(it[0][:], 0.0), writes=[it[1]])
        for it in self.ka.items:
            S.op("pool", lambda e: e.memset(it[0][:, :, 64:65], 1.0), writes=[it[1]])
        S.op("pool", lambda e: e.memset(self.KM[:], 0.0), writes=[self.b_KM])

        def silu_out(name):
            def h(tt, ps, bps):
                z, bz = self.zt.next()
                S.op("act", lambda e: e.activation(out=z[:], in_=ps[:], func=AF.Silu), reads=[bps], writes=[bz])
                S.dma("act", scr[name][tt * 128:(tt + 1) * 128, :], z[:], reads=[bz], writes=[self.db(name, tt)])
            return h

        wsrc = din["w_in"][l]
        bufsA, bufsB = self.wb.items[0:2], self.wb.items[2:4]
        ws1 = self.WStream(self, wsrc, [(O_AZ, 512), (O_AB, 16), (O_BKV, 256)], bufsA)
        self.proj_tm(l, O_AZ, 512, silu_out("ZA"), ws1)
        if self.stop == "p2a":
            S.barrier()
            self.es.close()
            return

        def h_ab(tt, ps, bps):
            S.op("act", lambda e: e.copy(out=self.SCR[:, tt, :], in_=ps[:, 0:16]), reads=[bps], writes=[self.b_SCR])
        self.proj_tm(l, O_AB, 16, h_ab, ws1)
        if self.stop == "p2b1":
            S.barrier()
            self.es.close()
            return
        self.a_scalars(l)
        if self.stop in ("p2b", "p2b2"):
            S.barrier()
            self.es.close()
            return

        def load_cs(tt, which):
            rotp, cn, sn, w = (self.csc, "k_cosc", "k_sinc", 64) if which == "c" else (self.csb, "k_cosb", "k_sinb", 256)
            cs, bcs = rotp.next()
            r0 = (tt - 2) * 128
            S.dma("sp", cs[:, 0, :], din[cn][r0:r0 + 128, :], writes=[bcs])
            S.dma("sp", cs[:, 1, :], din[sn][r0:r0 + 128, :], writes=[bcs])
            return cs, bcs

        def rope(x1, x2, cosb, sinb, o1, o2, shape_fn, bps, bcs, bout, scale=None):
            ta, bta = self.tmpA.next()
            tb, btb = self.tmpB.next()
            ta1, ta2 = shape_fn(ta[:, 0:256]), shape_fn(ta[:, 256:512])
            tb1, tb2 = shape_fn(tb[:, 0:256]), shape_fn(tb[:, 256:512])
            if scale is None:
                mul = lambda o, a, b: (lambda e: e.tensor_tensor(out=o, in0=a, in1=b, op=ALU.mult))
            else:
                mul = lambda o, a, b: (lambda e: e.scalar_tensor_tensor(out=o, in0=a, scalar=scale, in1=b, op0=ALU.mult, op1=ALU.mult))
            S.op("dve", mul(ta1, x1, cosb), reads=[bps, bcs], writes=[bta])
            S.op("dve", mul(tb1, x2, sinb), reads=[bps, bcs], writes=[btb])
            S.op("dve", mul(ta2, x1, sinb), reads=[bps, bcs], writes=[bta])
            S.op("dve", mul(tb2, x2, cosb), reads=[bps, bcs], writes=[btb])
            S.op("pool", lambda e: e.tensor_tensor(out=o1, in0=ta1, in1=tb1, op=ALU.subtract), reads=[bta, btb], writes=[bout])
            S.op("pool", lambda e: e.tensor_tensor(out=o2, in0=ta2, in1=tb2, op=ALU.add), reads=[bta, btb], writes=[bout])

        def rope_b(ps_ap, nha, tt, dst, bdst, bps, nh):
            o, bo = self.tmo.next()
            if tt >= 2:
                cs, bcs = load_cs(tt, "b")
                pv = ps_ap.rearrange("p (g f k) -> p g f k", f=2, k=16)
                ov = o[:, 0:nha * 32].rearrange("p (g f k) -> p g f k", f=2, k=16)
                cosb = cs[:, 0, 0:nha * 16].rearrange("p (g k) -> p g k", k=16)
                sinb = cs[:, 1, 0:nha * 16].rearrange("p (g k) -> p g k", k=16)
                rope(pv[:, :, 0, :], pv[:, :, 1, :], cosb, sinb, ov[:, :, 0, :], ov[:, :, 1, :],
                     lambda a: a[:, 0:nha * 16].rearrange("p (g k) -> p g k", k=16), bps, bcs, bo)
                S.op("act", lambda e: e.copy(out=dst[:, :, 0:64], in_=o[:, 0:nha * 32].rearrange("p (h k) -> p h k", h=nh)), reads=[bo], writes=[bdst])
            else:
                S.op("act", lambda e: e.copy(out=dst[:, :, 0:64], in_=ps_ap.rearrange("p (h k) -> p h k", h=nh)), reads=[bps], writes=[bdst])

        def h_bkv(tt, ps, bps):
            ka, bka = self.ka.next()
            rope_b(ps[:, 0:128], 4, tt, ka, bka, bps, 2)
            ta, bta = self.tmpA.next()
            st, bst = self.st8.next()
            S.op("act", lambda e: e.activation(out=ta[:, 0:128], in_=ps[:, 0:128], func=AF.Square), reads=[bps], writes=[bta])
            S.op("dve", lambda e: e.tensor_reduce(out=st[:, 0:2], in_=ta[:, 0:128].rearrange("p (h k) -> p h k", h=2), axis=AX.X, op=ALU.add),
                 reads=[bta], writes=[bst])
            S.op("dve", lambda e: e.tensor_tensor(out=self.KM[:], in0=self.KM[:], in1=st[:, 0:2], op=ALU.max), reads=[bst, self.b_KM], writes=[self.b_KM])
            vb, bvb = self.vb.next()
            S.op("dve", lambda e: e.tensor_copy(out=vb[:, :, 0:64], in_=ps[:, 128:256].rearrange("p (h k) -> p h k", h=2)),
                 reads=[bps], writes=[bvb])
            S.dma("act", scr["VB_TM"][tt * 128:(tt + 1) * 128, :, :], vb[:], reads=[bvb], writes=[self.db("VB_TM", tt)])
            kT, bkT = self.kaT.next()
            self.transpose_out(lambda i: (ka[:, i, :], bka), 2, 128, kT[:].rearrange("r h t -> r (h t)"), [bkT], "kbt")
            S.dma("act", scr["KB_FM"].rearrange("h r t -> r h t")[:, :, tt * 128:(tt + 1) * 128], kT[:], reads=[bkT], writes=[self.db("KB_FM", tt)])
        self.proj_tm(l, O_BKV, 256, h_bkv, ws1)
        if self.stop == "p2c":
            S.barrier()
            self.es.close()
            return
        S.op("dve", lambda e: e.tensor_reduce(out=self.kmx[:, 0:1], in_=self.KM[:], axis=AX.X, op=ALU.max), reads=[self.b_KM], writes=[self.b_kmx])
        dgk, bdgk = self.gdiag.next()
        S.op("dve", lambda e: e.tensor_scalar(out=dgk[:, 0:128], in0=self.ident_f[:], scalar1=self.kmx[:, 0:1], scalar2=None, op0=ALU.mult),
             reads=[self.b_ident_f, self.b_kmx], writes=[bdgk])
        ps, bps = S.ps()
        S.op("pe", lambda e: e.matmul(out=ps[:, 0:128], lhsT=self.ones_f[:], rhs=dgk[:, 0:128], start=True, stop=True),
             reads=[self.b_ones_f, bdgk], writes=[bps])
        kr, bkr = self.tsb("kmrow", [128, 4])
        S.op("dve", lambda e: e.tensor_reduce(out=kr[:, 0:1], in_=ps[:, 0:128], axis=AX.X, op=ALU.max), reads=[bps], writes=[bkr])
        S.op("act", lambda e: e.activation(out=kr[:, 1:2], in_=kr[:, 0:1], func=AF.Ln), reads=[bkr], writes=[bkr])
        S.op("act", lambda e: e.activation(out=kr[:, 2:3], in_=kr[:, 1:2], func=AF.Exp, scale=0.5), reads=[bkr], writes=[bkr])
        S.op("dve", lambda e: e.tensor_scalar(out=self.kmx[:, 1:2], in0=kr[:, 2:3], scalar1=-1.0, scalar2=None, op0=ALU.mult), reads=[bkr], writes=[self.b_kmx])
        S.op("dve", lambda e: e.tensor_scalar(out=self.kmx[:, 2:3], in0=kr[:, 2:3], scalar1=-0.125, scalar2=None, op0=ALU.mult), reads=[bkr], writes=[self.b_kmx])

        def h_bq(tt, ps, bps):
            qa, bqa = self.qa.next()
            ta, bta = self.tmpA.next()
            st, bst = self.st8.next()
            S.op("act", lambda e: e.activation(out=ta[:], in_=ps[:], func=AF.Square), reads=[bps], writes=[bta])
            S.op("dve", lambda e: e.tensor_reduce(out=st[:, 0:8], in_=ta[:].rearrange("p (h k) -> p h k", h=8), axis=AX.X, op=ALU.add),
                 reads=[bta], writes=[bst])
            S.op("pool", lambda e: e.tensor_tensor(out=st[:, 16:24], in0=st[:, 0:8], in1=self.half8[:], op=ALU.pow), reads=[bst, self.b_half8], writes=[bst])
            rope_b(ps[:], 16, tt, qa, bqa, bps, 8)
            S.op("dve", lambda e: e.tensor_scalar(out=qa[:, :, 64], in0=st[:, 16:24], scalar1=self.kmx[:, 1:2], scalar2=None, op0=ALU.mult),
                 reads=[bst, self.b_kmx], writes=[bqa])
            S.op("dve", lambda e: e.scalar_tensor_tensor(out=self.SH2[:, tt, :], in0=st[:, 16:24], scalar=self.kmx[:, 2:3], in1=self.par["b_sink"][0][:],
                                                         op0=ALU.mult, op1=ALU.add),
                 reads=[bst, self.b_kmx, self.par["b_sink"][1]], writes=[self.b_SH2])
            qT, bqT = self.qaT.next()
            self.transpose_out(lambda i: (qa[:, i, :], bqa), 8, 128, qT[:].rearrange("r h t -> r (h t)"), [bqT], "qbt")
            S.dma("act", scr["QB_FM"].rearrange("h r t -> r h t")[:, :, tt * 128:(tt + 1) * 128], qT[:], reads=[bqT], writes=[self.db("QB_FM", tt)])
        if self.stop == "p2d":
            S.barrier()
            self.es.close()
            return

        def h_cqk(name_fm, name_tm, scale):
            def h(tt, ps, bps):
                o, bo = self.tmo.next()
                if tt >= 2:
                    cs, bcs = load_cs(tt, "c")
                    pv = ps[:].rearrange("p (h f k) -> p h f k", h=4, f=2)
                    ov = o[:].rearrange("p (h f k) -> p h f k", h=4, f=2)
                    cosb = cs[:, 0, :].unsqueeze(1).to_broadcast([128, 4, 64])
                    sinb = cs[:, 1, :].unsqueeze(1).to_broadcast([128, 4, 64])
                    rope(pv[:, :, 0, :], pv[:, :, 1, :], cosb, sinb, ov[:, :, 0, :], ov[:, :, 1, :],
                         lambda a: a.rearrange("p (h k) -> p h k", h=4), bps, bcs, bo, scale=scale)
                else:
                    S.op("act", lambda e: e.activation(out=o[:], in_=ps[:], func=AF.Copy, scale=(1.0 if scale is None else scale)), reads=[bps], writes=[bo])
                if name_tm is not None:
                    S.dma("act", scr[name_tm][tt * 128:(tt + 1) * 128, :], o[:], reads=[bo], writes=[self.db(name_tm, tt)])
                f, bf = self.fmo.next()
                self.transpose_out(lambda i: (o[:, i * 128:(i + 1) * 128], bo), 4, 128, f[:], [bf], "cfm")
                S.dma("act", scr[name_fm].rearrange("h p t -> p h t")[:, :, tt * 128:(tt + 1) * 128], f[:].rearrange("p (h t) -> p h t", h=4),
                      reads=[bf], writes=[self.db(name_fm, tt)])
            return h

        def h_cv(tt, ps, bps):
            o, bo = self.tmo.next()
            S.op("act", lambda e: e.copy(out=o[:], in_=ps[:]), reads=[bps], writes=[bo])
            S.dma("act", scr["VC_TM"][tt * 128:(tt + 1) * 128, :], o[:], reads=[bo], writes=[self.db("VC_TM", tt)])
        wsZ = self.WStream(self, wsrc, [(O_BZ, 512), (O_CZ, 512)], bufsB)
        self.proj_tm(l, O_BZ, 512, silu_out("ZB"), wsZ)
        self.proj_tm(l, O_CZ, 512, silu_out("ZC"), wsZ)
        groups = self.tok_groups()
        wsH = self.WStream(self, wsrc, [(O_BQ, 512), (O_CQ, 512), (O_CK, 512)] + [(g * 512, 512) for g in range(3)], bufsA)
        wsL = self.WStream(self, wsrc, [(O_MG + g * 512, 512) for g in range(6)] + [(O_CV, 512)], bufsB)
        wsF = wsH
        wsM = wsL

        def heavy():
            yield from self.proj_tm_gen(l, O_BQ, 512, h_bq, wsH)
            yield from self.proj_tm_gen(l, O_CQ, 512, h_cqk("QC_FM", None, None), wsH)
            yield from self.proj_tm_gen(l, O_CK, 512, h_cqk("KC_FM", "KC_TM", float(CH) ** -0.5), wsH)

        def light():
            yield from self.proj_tm_gen(l, O_CV, 512, h_cv, wsL)
        if self.stop == "p2e":
            S.barrier()
            self.es.close()
            return


        def afm_gen():
            wcur = {}

            def get_w(g3):
                if g3 not in wcur:
                    self.marks.append(("  L%d afm%d" % (l, g3), {k: v.count for k, v in S.engs.items()}))
                    wcur[g3] = wsF.get(g3 * 512, 512)
                return wcur[g3]

            def proj_piece(ct, gi):
                g3, cl = ct // 4, ct % 4
                wb, bwb = get_w(g3)
                (t0, n, tile0, ntile) = groups[gi]
                ps, bps = S.ps()
                for kc in range(8):
                    S.op("pe", lambda e: e.matmul(out=ps[:, 0:n], lhsT=wb[:, kc, cl * 128:(cl + 1) * 128], rhs=self.hT[:, kc, t0:t0 + n],
                                                  start=(kc == 0), stop=(kc == 7)),
                         reads=[self.b_hT[tile0 + i] for i in range(ntile)] + [bwb], writes=[bps])
                off = 2 + t0 if t0 == 0 else 6 + t0
                S.op("act", lambda e: e.copy(out=self.rowbuf[:, off:off + n], in_=ps[:, 0:n]), reads=[bps], writes=[self.b_rowbuf.sub(t0)])

            def pass1_piece(ct, gi):
                g3, head = ct // 4, ct % 4
                (t0, n, tile0, ntile) = groups[gi]
                off = 2 + t0 if t0 == 0 else 6 + t0
                cv, bcv = self.tmpA.next()
                nb = [gi] if gi == 0 else [j for j in (gi - 1, gi, gi + 1) if 1 <= j < len(groups)]
                rb_reads = [self.b_rowbuf.sub(groups[j][0]) for j in nb]
                for k in range(5):
                    src = self.rowbuf[:, off + k - 2:off + k - 2 + n]
                    if k == 0:
                        S.op("dve", lambda e: e.tensor_scalar(out=cv[:, 0:n], in0=src, scalar1=self.convw[:, ct, 0:1], scalar2=None, op0=ALU.mult),
                             reads=rb_reads + [self.b_convw], writes=[bcv])
                    else:
                        S.op("dve", lambda e: e.scalar_tensor_tensor(out=cv[:, 0:n], in0=src, scalar=self.convw[:, ct, k:k + 1], in1=cv[:, 0:n],
                                                                     op0=ALU.mult, op1=ALU.add),
                             reads=rb_reads + [self.b_convw, bcv], writes=[bcv])
                if g3 < 2:
                    S.op("act", lambda e: e.activation(out=self.slrow[:, t0:t0 + n], in_=cv[:, 0:n], func=AF.Silu), reads=[bcv], writes=[self.b_slrow.sub(t0)])
                else:
                    o, bo = self.tmo.next()
                    S.op("act", lambda e: e.activation(out=o[:, 0:n], in_=cv[:, 0:n], func=AF.Silu), reads=[bcv], writes=[bo])
                    f, bf = self.fmo.next()
                    self.transpose_out(lambda i: (o[:, i * 128:(i + 1) * 128], bo), ntile, 128, f[:, 0:n], [bf], "va")
                    S.dma("act", scr["VA_TM"][head].rearrange("(t p) c -> p t c", p=128)[:, tile0:tile0 + ntile, :],
                          f[:, 0:n].rearrange("p (t c) -> p t c", c=128), reads=[bf], writes=[self.db("VA_TM", tile0 + i) for i in range(ntile)])

            def pass2_piece(ct, gi):
                g3, head = ct // 4, ct % 4
                name = "QA_FM" if g3 == 0 else "KA_FM"
                (t0, n, tile0, ntile) = groups[gi]
                sq, bsq = self.tmo.next()
                S.op("act", lambda e: e.activation(out=sq[:, 0:n], in_=self.slrow[:, t0:t0 + n], func=AF.Square), reads=[self.b_slrow.sub(t0)], writes=[bsq])
                ps, bps = S.ps()
                S.op("pe", lambda e: e.matmul(out=ps[:, 0:n], lhsT=self.ones_b[:], rhs=sq[:, 0:n], start=True, stop=True),
                     reads=[self.b_ones_b, bsq], writes=[bps])
                ta, bta = self.tmpB.next()
                S.op("act", lambda e: e.activation(out=ta[:, 0:n], in_=ps[:, 0:n], func=AF.Ln, bias=self.epsb[:, 0:1]), reads=[bps, self.b_epsb], writes=[bta])
                S.op("act", lambda e: e.activation(out=ta[:, 0:n], in_=ta[:, 0:n], func=AF.Exp, scale=-0.5, bias=self.epsb[:, 1 + g3:2 + g3]),
                     reads=[bta, self.b_epsb], writes=[bta])
                o, bo = self.fmo.next()
                S.op("dve", lambda e: e.tensor_tensor(out=o[:, 0:n], in0=self.slrow[:, t0:t0 + n], in1=ta[:, 0:n], op=ALU.mult),
                     reads=[self.b_slrow.sub(t0), bta], writes=[bo])
                S.dma("act", scr[name][head][:, t0:t0 + n], o[:, 0:n], reads=[bo], writes=[self.db(name, tile0 + i) for i in range(ntile)])
                if g3 == 1:
                    f, bf = self.zt.next()
                    self.transpose_out(lambda i: (o[:, i * 128:(i + 1) * 128], bo), ntile, 128, f[:, 0:n], [bf], "ka")
                    S.dma("act", scr["KA_TM"][head].rearrange("(t p) c -> p t c", p=128)[:, tile0:tile0 + ntile, :],
                          f[:, 0:n].rearrange("p (t c) -> p t c", c=128), reads=[bf], writes=[self.db("KA_TM", tile0 + i) for i in range(ntile)])

            ng = len(groups)
            for gi in range(ng):
                proj_piece(0, gi)
                yield
            for ct in range(12):
                for gi in range(ng):
                    pass1_piece(ct, gi)
                    if ct + 1 < 12 and gi >= 1:
                        proj_piece(ct + 1, gi - 1)
                    yield
                if ct + 1 < 12:
                    proj_piece(ct + 1, ng - 1)
                    yield
                if ct < 8:
                    for gi in range(ng):
                        pass2_piece(ct, gi)
                        yield

        def merge_gen():
            for g6 in range(6):
                self.marks.append(("  L%d mg%d" % (l, g6), {k: v.count for k, v in S.engs.items()}))
                wb, bwb = wsM.get(O_MG + g6 * 512, 512)
                for cl in range(4):
                    ct = g6 * 4 + cl
                    for (t0, n, tile0, ntile) in groups:
                        if last and t0 == 0:
                            continue
                        ps, bps = S.ps()
                        for kc in range(8):
                            S.op("pe", lambda e: e.matmul(out=ps[:, 0:n], lhsT=wb[:, kc, cl * 128:(cl + 1) * 128], rhs=self.hT[:, kc, t0:t0 + n],
                                                          start=(kc == 0), stop=(kc == 7)),
                                 reads=[self.b_hT[tile0 + i] for i in range(ntile)] + [bwb], writes=[bps])
                        o, bo = self.fmo.next()
                        S.op("act", lambda e: e.activation(out=o[:, 0:n], in_=ps[:, 0:n], func=AF.Sigmoid), reads=[bps], writes=[bo])
                        S.dma("act", scr["GM_FM"][ct * 128:(ct + 1) * 128, t0:t0 + n], o[:, 0:n], reads=[bo],
                              writes=[self.db("GM_FM%d" % ct, tile0 + i) for i in range(ntile)])
                        yield
        self.run_pair(heavy(), merge_gen())
        self.run_pair(afm_gen(), light())
        if self.stop == "p2":
            self.dump_p2()
        S.barrier()
        self.es.close()

    def a_scalars(self, l):
        S = self.S
        A = self.asc
        braw = self.SCR[:, :, 0:8]
        araw = self.SCR[:, :, 8:16]
        rs = [self.b_SCR]
        one = self.epsb[:, 3:4]

        def softplus_parts(x_ap, xb, neg):
            t1, b1 = self.tmp8.next()
            t2, b2 = self.tmp8.next()
            S.op("act", lambda e: e.activation(out=t1[:], in_=x_ap, func=AF.Abs), reads=xb, writes=[b1])
            S.op("act", lambda e: e.activation(out=t1[:], in_=t1[:], func=AF.Exp, scale=-1.0), reads=[b1], writes=[b1])
            S.op("act", lambda e: e.activation(out=t1[:], in_=t1[:], func=AF.Ln, bias=one), reads=[b1, self.b_epsb], writes=[b1])
            S.op("dve", lambda e: e.tensor_scalar(out=t2[:], in0=x_ap, scalar1=(-1.0 if neg else 1.0), scalar2=0.0, op0=ALU.mult, op1=ALU.max),
                 reads=xb, writes=[b2])
            return (t2, b2), (t1, b1)

        (m, bm), (l1, bl1) = softplus_parts(braw, rs, True)
        LNB, bLNB = A["LNB"]
        S.op("dve", lambda e: e.scalar_tensor_tensor(out=LNB[:], in0=m[:], scalar=-1.0, in1=l1[:], op0=ALU.mult, op1=ALU.subtract),
             reads=[bm, bl1], writes=[bLNB])
        BETA, bBETA = A["BETA"]
        S.op("act", lambda e: e.activation(out=BETA[:], in_=LNB[:], func=AF.Exp), reads=[bLNB], writes=[bBETA])
        xa, bxa = self.tmp8.next()
        dtb, bdtb = self.par["a_dt_bias"]
        S.op("dve", lambda e: e.tensor_tensor(out=xa[:], in0=araw, in1=dtb[:].unsqueeze(1).to_broadcast([128, NT, 8]), op=ALU.add),
             reads=rs + [bdtb], writes=[bxa])
        (m2, bm2), (l2, bl2) = softplus_parts(xa[:], [bxa], False)
        S.op("dve", lambda e: e.tensor_tensor(out=m2[:], in0=m2[:], in1=l2[:], op=ALU.add), reads=[bm2, bl2], writes=[bm2])
        alog, balog = self.par["a_log"]
        nea, bnea = self.st8.next()
        S.op("act", lambda e: e.activation(out=nea[:, 0:8], in_=alog[:], func=AF.Exp), reads=[balog], writes=[bnea])
        GRAW, bGRAW = A["GRAW"]
        S.op("dve", lambda e: e.scalar_tensor_tensor(out=GRAW[:], in0=m2[:], scalar=-1.0, in1=nea[:, 0:8].unsqueeze(1).to_broadcast([128, NT, 8]),
                                                     op0=ALU.mult, op1=ALU.mult),
             reads=[bm2, bnea], writes=[bGRAW])
        GC, bGC = A["GC"]
        GT, bGT = A["GT"]
        if self.stop == "p2b2":
            return
        gflat = GRAW[:].rearrange("p t c -> p (t c)")
        res = []
        for lhs, blhs in ((self.tri[:, 0, :], self.b_tri), (self.tri[:, 1, :], self.b_tri), (self.ones_f[:], self.b_ones_f)):
            ps, bps = S.ps()
            S.op("pe", lambda e: e.matmul(out=ps[:, 0:NT * 8], lhsT=lhs, rhs=gflat, start=True, stop=True), reads=[blhs, bGRAW], writes=[bps])
            res.append((ps[:, 0:NT * 8].rearrange("p (t c) -> p t c", c=8), bps))
        S.op("dve", lambda e: e.tensor_copy(out=GC[:, :, 0:4], in_=res[0][0][:, :, 0:4]), reads=[res[0][1]], writes=[bGC])
        S.op("dve", lambda e: e.tensor_copy(out=GC[:, :, 4:8], in_=res[1][0][:, :, 4:8]), reads=[res[1][1]], writes=[bGC])
        S.op("act", lambda e: e.copy(out=GT[:], in_=res[2][0]), reads=[res[2][1]], writes=[bGT])
        NEGG, bNEGG = A["NEGG"]
        S.op("dve", lambda e: e.tensor_scalar(out=NEGG[:], in0=GC[:], scalar1=-1.0, scalar2=None, op0=ALU.mult), reads=[bGC], writes=[bNEGG])
        LNBMG, bLNBMG = A["LNBMG"]
        S.op("dve", lambda e: e.tensor_tensor(out=LNBMG[:], in0=LNB[:], in1=GC[:], op=ALU.subtract), reads=[bLNB, bGC], writes=[bLNBMG])
        NEGEG, bNEGEG = A["NEGEG"]
        S.op("act", lambda e: e.activation(out=NEGEG[:], in_=GC[:], func=AF.Exp), reads=[bGC], writes=[bNEGEG])
        S.op("dve", lambda e: e.tensor_scalar(out=NEGEG[:], in0=NEGEG[:], scalar1=-1.0, scalar2=None, op0=ALU.mult), reads=[bNEGEG], writes=[bNEGEG])
        ETAIL, bETAIL = A["ETAIL"]
        S.op("dve", lambda e: e.tensor_tensor(out=ETAIL[:], in0=GT[:], in1=GC[:], op=ALU.subtract), reads=[bGT, bGC], writes=[bETAIL])
        S.op("act", lambda e: e.activation(out=ETAIL[:], in_=ETAIL[:], func=AF.Exp), reads=[bETAIL], writes=[bETAIL])
        EGL, bEGL = A["EGL"]
        S.op("act", lambda e: e.activation(out=EGL[:], in_=GT[:], func=AF.Exp), reads=[bGT], writes=[bEGL])

    def core_b(self, l, defer=False):
        S, scr, din = self.S, self.scr, self.din
        last = (l == self.nlayers - 1)
        if not defer:
            self.es = ExitStack()
        KBT = self.R1[:, 0:2 * NTOK].rearrange("p (g t) -> p g t", g=2)
        VBR = self.R1[:, 2 * NTOK:2 * NTOK + NT * 130].rearrange("p (t g c) -> p t g c", g=2, c=65)
        bKBT, bVBR = Buf("KBT"), Buf("VBR")
        S.dma("sp", KBT, scr["KB_FM"].rearrange("g r t -> r g t"), reads=self.dball("KB_FM"), writes=[bKBT])
        S.dma("sp", VBR, scr["VB_TM"].rearrange("(t p) g c -> p t g c", p=128), reads=self.dball("VB_TM"), writes=[bVBR])
        if l == 0:
            bst, bbst = self.tsb("bmst", [128, 2, 512])
            S.dma("sp", bst[:], din["k_bmask"].rearrange("a p n -> p a n"), writes=[bbst])
            S.op("pool", lambda e: e.tensor_copy(out=self.bmask[:], in_=bst[:]), reads=[bbst], writes=[self.b_bmask])
        qTr = self.trot("b_qT", [128, 4, 128], BF16, 2)
        pTr = self.trot("b_pT", [128, 5, 512], BF16, 2)
        zbr = self.trot("b_zb", [128, 512], BF16, 2)
        ybr = self.trot("b_yb", [128, 512], BF16, 2)
        obr = self.trot("b_ob", [128, 256], F32, 2)
        str_ = self.trot("b_st", [128, 16], F32, 6)
        fmo = self.trot("b_fmo", [128, 512], BF16, 2)
        qtiles = list(range(2, NT)) if last else list(range(NT))
        def b_gen():
            for qt in qtiles:
                zb, bzb = zbr.next()
                S.dma("sp", zb[:], scr["ZB"][qt * 128:(qt + 1) * 128, :], reads=[self.db("ZB", qt)], writes=[bzb])
                yb, byb = ybr.next()
                for g in range(2):
                    qT, bqT = qTr.next()
                    S.dma("sp", qT[:], scr["QB_FM"][g * 4:(g + 1) * 4].rearrange("h r t -> r h t")[:, :, qt * 128:(qt + 1) * 128],
                          reads=[self.db("QB_FM", qt)], writes=[bqT])
                    keys = [(0, None), (1, None)]
                    if qt >= 2:
                        if qt - 1 >= 2:
                            keys.append((qt - 1, 0))
                        keys.append((qt, None))
                        if qt + 1 < NT:
                            keys.append((qt + 1, 1))
                    pT, bpT = pTr.next()
                    for idx, (kt, m) in enumerate(keys):
                        ps, bps = S.ps()
                        S.op("pe", lambda e: e.matmul(out=ps[:], lhsT=KBT[:, g, kt * 128:(kt + 1) * 128], rhs=qT[:].rearrange("p h t -> p (h t)"),
                                                      start=True, stop=True), reads=[bKBT, bqT], writes=[bps])
                        S.op("act", lambda e: e.activation(out=pT[:, idx, :], in_=ps[:], func=AF.Exp, scale=0.125), reads=[bps], writes=[bpT.sub(idx)])
                        if m is not None:
                            S.op("pool", lambda e: e.tensor_tensor(out=pT[:, idx, :], in0=pT[:, idx, :], in1=self.bmask[:, m, :], op=ALU.mult),
                                 reads=[bpT.sub(idx), self.b_bmask], writes=[bpT.sub(idx)])
                    po, bpo = S.ps()
                    for h in range(4):
                        for idx, (kt, m) in enumerate(keys):
                            S.op("pe", lambda e: e.matmul(out=po[:, h * 65:(h + 1) * 65], lhsT=pT[:, idx, h * 128:(h + 1) * 128], rhs=VBR[:, kt, g, :],
                                                          start=(idx == 0), stop=(idx == len(keys) - 1)), reads=[bpT.sub(idx), bVBR], writes=[bpo])
                    st, bst_ = str_.next()
                    pov = po[:, 0:260].rearrange("p (h c) -> p h c", c=65)
                    S.op("act", lambda e: e.activation(out=st[:, 0:4], in_=self.SH2[:, qt, g * 4:(g + 1) * 4], func=AF.Exp), reads=[self.b_SH2], writes=[bst_])
                    S.op("dve", lambda e: e.tensor_tensor(out=st[:, 4:8], in0=pov[:, :, 64], in1=st[:, 0:4], op=ALU.add), reads=[bpo, bst_], writes=[bst_])
                    S.op("dve", lambda e: e.reciprocal(out=st[:, 8:12], in_=st[:, 4:8]), reads=[bst_], writes=[bst_])
                    ob, bob = obr.next()
                    S.op("dve", lambda e: e.tensor_tensor(out=ob[:].rearrange("p (h c) -> p h c", c=64), in0=pov[:, :, 0:64],
                                                          in1=st[:, 8:12].unsqueeze(2).to_broadcast([128, 4, 64]), op=ALU.mult),
                         reads=[bpo, bst_], writes=[bob])
                    S.op("pool", lambda e: e.tensor_tensor(out=yb[:, g * 256:(g + 1) * 256], in0=ob[:], in1=zb[:, g * 256:(g + 1) * 256], op=ALU.mult),
                         reads=[bob, bzb], writes=[byb.sub(g)])
                    yield
                self.y_out(1, qt, yb, byb, fmo)
                yield
        if defer:
            return b_gen()
        for _ in b_gen():
            pass
        S.barrier()
        self.es.close()

    def core_bc(self, l):
        self.es = ExitStack()
        gb = self.core_b(l, defer=True)
        gc = self.core_c(l, defer=True)
        self.run_pair(gb, gc)
        self.S.barrier()
        self.es.close()

    def y_out(self, br, tt, y, by, fmo, pool=None):
        S = self.S
        f, bf = fmo.next()
        self.transpose_out(lambda i: (y[:, i * 128:(i + 1) * 128], by), 4, 128, f[:], [bf], "y", pool=pool)
        S.dma("act", self.scr["Y_FM"][br].rearrange("(k p) t -> p k t", p=128)[:, :, tt * 128:(tt + 1) * 128],
              f[:].rearrange("p (k t) -> p k t", k=4), reads=[bf], writes=[self.db("Y_FM%d" % br, tt)])

    def core_c(self, l, defer=False):
        S, scr = self.S, self.scr
        last = (l == self.nlayers - 1)
        if not defer:
            self.es = ExitStack()
        one = self.epsb[:, 3:4]
        cd, bcd = self.par["c_decay"]
        c8 = self.trot("c_c8", [128, 8], F32, 6)
        t1, b1 = c8.next()
        t2, b2 = c8.next()
        LG, bLG = c8.next()
        S.op("act", lambda e: e.activation(out=t1[:], in_=cd[:], func=AF.Abs), reads=[bcd], writes=[b1])
        S.op("act", lambda e: e.activation(out=t1[:], in_=t1[:], func=AF.Exp, scale=-1.0), reads=[b1], writes=[b1])
        S.op("act", lambda e: e.activation(out=t1[:], in_=t1[:], func=AF.Ln, bias=one), reads=[b1, self.b_epsb], writes=[b1])
        S.op("dve", lambda e: e.tensor_scalar(out=t2[:], in0=cd[:], scalar1=-1.0, scalar2=0.0, op0=ALU.mult, op1=ALU.max), reads=[bcd], writes=[b2])
        S.op("dve", lambda e: e.scalar_tensor_tensor(out=LG[:], in0=t2[:], scalar=-1.0, in1=t1[:], op0=ALU.mult, op1=ALU.subtract),
             reads=[b1, b2], writes=[bLG])
        GAMC, bGAMC = c8.next()
        S.op("act", lambda e: e.activation(out=GAMC[:], in_=LG[:], func=AF.Exp, scale=float(CH)), reads=[bLG], writes=[bGAMC])
        KDEC, bKDEC = c8.next()
        S.op("dve", lambda e: e.tensor_tensor(out=KDEC[:], in0=LG[:], in1=self.cj[:], op=ALU.mult), reads=[bLG, self.b_cj], writes=[bKDEC])
        S.op("act", lambda e: e.activation(out=KDEC[:], in_=KDEC[:], func=AF.Exp), reads=[bKDEC], writes=[bKDEC])
        DM, bDM = self.tsb("c_DM", [128, 512])
        QDF, bQDF = self.tsb("c_QDF", [128, 512], BF16)
        QDB, bQDB = self.tsb("c_QDB", [128, 512], BF16)
        tm = self.trot("c_tm", [128, 128], F32, 2)
        for h in range(4):
            ta, bta = tm.next()
            tb, btb = tm.next()
            S.op("act", lambda e: e.activation(out=ta[:], in_=self.cm[:, 0, :], func=AF.Exp, scale=LG[:, h:h + 1]), reads=[self.b_cm, bLG], writes=[bta])
            S.op("dve", lambda e: e.tensor_tensor(out=ta[:], in0=ta[:], in1=self.cm[:, 2, :], op=ALU.mult), reads=[bta, self.b_cm], writes=[bta])
            S.op("act", lambda e: e.activation(out=tb[:], in_=self.cm[:, 1, :], func=AF.Exp, scale=LG[:, 4 + h:5 + h]), reads=[self.b_cm, bLG], writes=[btb])
            S.op("dve", lambda e: e.tensor_tensor(out=tb[:], in0=tb[:], in1=self.cm[:, 3, :], op=ALU.mult), reads=[btb, self.b_cm], writes=[btb])
            S.op("dve", lambda e: e.tensor_tensor(out=ta[:], in0=ta[:], in1=tb[:], op=ALU.add), reads=[bta, btb], writes=[bta])
            S.op("dve", lambda e: e.scalar_tensor_tensor(out=DM[:, h * 128:(h + 1) * 128], in0=self.ident_f[:], scalar=2.0, in1=ta[:], op0=ALU.mult, op1=ALU.add),
                 reads=[bta, self.b_ident_f], writes=[bDM])
            S.op("act", lambda e: e.activation(out=QDF[:, h * 128:(h + 1) * 128], in_=self.cm[:, 4, :], func=AF.Exp, scale=LG[:, h:h + 1]),
                 reads=[self.b_cm, bLG], writes=[bQDF])
            S.op("act", lambda e: e.activation(out=QDB[:, h * 128:(h + 1) * 128], in_=self.cm[:, 5, :], func=AF.Exp, scale=LG[:, 4 + h:5 + h]),
                 reads=[self.b_cm, bLG], writes=[bQDB])
        kTMr = [self.trot("c_kTM%d" % d, [128, 512], BF16, 2) for d in range(2)]
        vr = [self.trot("c_v%d" % d, [128, 512], BF16, 2) for d in range(3)]
        kdr = [self.trot("c_kd%d" % d, [128, 512], BF16, 2) for d in range(2)]
        sbfr = [self.trot("c_sbf%d" % d, [128, 512], BF16, 2) for d in range(2)]
        S32 = [self.tsb("c_S32_%d" % d, [128, 512]) for d in range(2)]
        SCN = ["SCF", "SCB"]

        def state_update(d, cc, kTM, bkTM, v, bv):
            kd, bkd = kdr[d].next()
            for h in range(4):
                S.op("act", lambda e: e.activation(out=kd[:, h * 128:(h + 1) * 128], in_=kTM[:, h * 128:(h + 1) * 128], func=AF.Copy,
                                                   scale=KDEC[:, d * 4 + h:d * 4 + h + 1]),
                     reads=[bkTM, bKDEC], writes=[bkd.sub(h)])
            ps, bps = S.ps()
            for h in range(4):
                S.op("pe", lambda e: e.matmul(out=ps[:, h * 128:(h + 1) * 128], lhsT=kd[:, h * 128:(h + 1) * 128], rhs=v[:, h * 128:(h + 1) * 128],
                                              start=True, stop=True), reads=[bkd, bv], writes=[bps])
            s32, bs32 = S32[d]
            for h in range(4):
                S.op("dve", lambda e: e.scalar_tensor_tensor(out=s32[:, h * 128:(h + 1) * 128], in0=s32[:, h * 128:(h + 1) * 128],
                                                             scalar=GAMC[:, d * 4 + h:d * 4 + h + 1], in1=ps[:, h * 128:(h + 1) * 128],
                                                             op0=ALU.mult, op1=ALU.add),
                     reads=[bs32.sub(h), bGAMC, bps], writes=[bs32.sub(h)])

        def state_pass(d):
            S.op("pool", lambda e: e.memset(S32[d][0][:], 0.0), writes=[S32[d][1]])
            order = list(range(NT)) if d == 0 else [1, 0] + list(range(NT - 1, 1, -1))
            for cc in order:
                sbf, bsbf = sbfr[d].next()
                S.op("act", lambda e: e.copy(out=sbf[:], in_=S32[d][0][:]), reads=[S32[d][1]], writes=[bsbf])
                S.dma("act", scr[SCN[d]][cc], sbf[:], reads=[bsbf], writes=[self.db(SCN[d], cc)])
                kTM, bkTM = kTMr[d].next()
                v, bv = vr[d].next()
                S.dma("sp", kTM[:], scr["KC_TM"][cc * 128:(cc + 1) * 128, :], reads=[self.db("KC_TM", cc)], writes=[bkTM])
                S.dma("sp", v[:], scr["VC_TM"][cc * 128:(cc + 1) * 128, :], reads=[self.db("VC_TM", cc)], writes=[bv])
                yield
                state_update(d, cc, kTM, bkTM, v, bv)
                yield

        qTr = self.trot("c_qT", [128, 512], BF16, 2)
        kTr = self.trot("c_kT", [128, 512], BF16, 2)
        sbr = self.trot("c_sb", [128, 512], BF16, 2)
        sfr = self.trot("c_sf", [128, 512], BF16, 2)
        zcr = self.trot("c_zc", [128, 512], BF16, 2)
        qkr = self.trot("c_qk", [128, 512], BF16, 2)
        qdr = self.trot("c_qd", [128, 512], BF16, 2)
        ofr = self.trot("c_of", [128, 512], F32, 2)
        t5r = self.trot("c_t5", [128, 512], F32, 1)
        ycr = self.trot("c_yc", [128, 512], BF16, 2)
        stc = self.trot("c_st", [128, 24], F32, 3)
        fmo = self.trot("c_fmo", [128, 512], BF16, 2)
        cnw, bcnw = self.par["c_norm_w"]
        eps = self.epsb[:, 0:1]
        def out_pass():
            for cc in range(NT):
                need_out = (cc >= 2) or (not last)
                if need_out:
                    v, bv = vr[2].next()
                    S.dma("sp", v[:], scr["VC_TM"][cc * 128:(cc + 1) * 128, :], reads=[self.db("VC_TM", cc)], writes=[bv])
                    qT, bqT = qTr.next()
                    kT, bkT = kTr.next()
                    sb, bsb = sbr.next()
                    zc, bzc = zcr.next()
                    S.dma("sp", qT[:].rearrange("p (h t) -> p h t", h=4), scr["QC_FM"].rearrange("h p t -> p h t")[:, :, cc * 128:(cc + 1) * 128],
                          reads=[self.db("QC_FM", cc)], writes=[bqT])
                    S.dma("sp", kT[:].rearrange("p (h t) -> p h t", h=4), scr["KC_FM"].rearrange("h p t -> p h t")[:, :, cc * 128:(cc + 1) * 128],
                          reads=[self.db("KC_FM", cc)], writes=[bkT])
                    S.dma("sp", sb[:], scr["SCB"][cc], reads=[self.db("SCB", cc)], writes=[bsb])
                    S.dma("sp", zc[:], scr["ZC"][cc * 128:(cc + 1) * 128, :], reads=[self.db("ZC", cc)], writes=[bzc])
                    sbf, bsbf = sfr.next()
                    S.dma("sp", sbf[:], scr["SCF"][cc], reads=[self.db("SCF", cc)], writes=[bsbf])
                    ps1, bps1 = S.ps()
                    for h in range(4):
                        S.op("pe", lambda e: e.matmul(out=ps1[:, h * 128:(h + 1) * 128], lhsT=kT[:, h * 128:(h + 1) * 128], rhs=qT[:, h * 128:(h + 1) * 128],
                                                      start=True, stop=True), reads=[bkT, bqT], writes=[bps1])
                    qk, bqk = qkr.next()
                    S.op("dve", lambda e: e.tensor_tensor(out=qk[:], in0=ps1[:], in1=DM[:], op=ALU.mult), reads=[bps1, bDM], writes=[bqk])
                    qdf, bqdf = qdr.next()
                    qdb, bqdb = qdr.next()
                    S.op("pool", lambda e: e.tensor_tensor(out=qdf[:], in0=qT[:], in1=QDF[:], op=ALU.mult), reads=[bqT, bQDF], writes=[bqdf])
                    S.op("pool", lambda e: e.tensor_tensor(out=qdb[:], in0=qT[:], in1=QDB[:], op=ALU.mult), reads=[bqT, bQDB], writes=[bqdb])
                    po, bpo = S.ps()
                    for h in range(4):
                        sl = slice(h * 128, (h + 1) * 128)
                        S.op("pe", lambda e: e.matmul(out=po[:, sl], lhsT=qk[:, sl], rhs=v[:, sl], start=True, stop=False), reads=[bqk, bv], writes=[bpo])
                        S.op("pe", lambda e: e.matmul(out=po[:, sl], lhsT=qdf[:, sl], rhs=sbf[:, sl], start=False, stop=False), reads=[bqdf, bsbf], writes=[bpo])
                        S.op("pe", lambda e: e.matmul(out=po[:, sl], lhsT=qdb[:, sl], rhs=sb[:, sl], start=False, stop=True), reads=[bqdb, bsb], writes=[bpo])
                    of, bof = ofr.next()
                    t5, bt5 = t5r.next()
                    st, bst = stc.next()
                    S.op("act", lambda e: e.copy(out=of[:], in_=po[:]), reads=[bpo], writes=[bof])
                    S.op("act", lambda e: e.activation(out=t5[:], in_=po[:], func=AF.Square), reads=[bpo], writes=[bt5])
                    S.op("dve", lambda e: e.tensor_reduce(out=st[:, 0:4], in_=of[:].rearrange("p (h c) -> p h c", h=4), axis=AX.X, op=ALU.add), reads=[bof], writes=[bst])
                    S.op("dve", lambda e: e.tensor_reduce(out=st[:, 4:8], in_=t5[:].rearrange("p (h c) -> p h c", h=4), axis=AX.X, op=ALU.add), reads=[bt5], writes=[bst])
                    S.op("dve", lambda e: e.tensor_scalar(out=st[:, 8:12], in0=st[:, 0:4], scalar1=1.0 / 128, scalar2=None, op0=ALU.mult), reads=[bst], writes=[bst])
                    S.op("dve", lambda e: e.tensor_tensor(out=st[:, 12:16], in0=st[:, 8:12], in1=st[:, 8:12], op=ALU.mult), reads=[bst], writes=[bst])
                    S.op("dve", lambda e: e.scalar_tensor_tensor(out=st[:, 16:20], in0=st[:, 4:8], scalar=1.0 / 128, in1=st[:, 12:16], op0=ALU.mult, op1=ALU.subtract),
                         reads=[bst], writes=[bst])
                    S.op("act", lambda e: e.activation(out=st[:, 20:24], in_=st[:, 16:20], func=AF.Ln, bias=eps), reads=[bst, self.b_epsb], writes=[bst])
                    S.op("act", lambda e: e.activation(out=st[:, 20:24], in_=st[:, 20:24], func=AF.Exp, scale=-0.5), reads=[bst], writes=[bst])
                    ofv = of[:].rearrange("p (h c) -> p h c", h=4)
                    S.op("dve", lambda e: e.tensor_tensor(out=ofv, in0=ofv, in1=st[:, 8:12].unsqueeze(2).to_broadcast([128, 4, 128]), op=ALU.subtract),
                         reads=[bof, bst], writes=[bof])
                    S.op("dve", lambda e: e.tensor_tensor(out=ofv, in0=ofv, in1=st[:, 20:24].unsqueeze(2).to_broadcast([128, 4, 128]), op=ALU.mult),
                         reads=[bof, bst], writes=[bof])
                    S.op("pool", lambda e: e.tensor_tensor(out=of[:], in0=of[:], in1=cnw[:], op=ALU.mult), reads=[bof, bcnw], writes=[bof])
                    yc, byc = ycr.next()
                    S.op("pool", lambda e: e.tensor_tensor(out=yc[:], in0=of[:], in1=zc[:], op=ALU.mult), reads=[bof, bzc], writes=[byc])
                    self.y_out(2, cc, yc, byc, fmo)
                yield

        def c_gen():
            gens = [state_pass(0), state_pass(1)]
            while gens:
                for g_ in list(gens):
                    try:
                        next(g_)
                        yield
                    except StopIteration:
                        gens.remove(g_)
            yield from out_pass()
        if defer:
            return c_gen()
        for _ in c_gen():
            pass
        S.barrier()
        self.es.close()

    def core_a(self, l):
        S, scr = self.S, self.scr
        last = (l == self.nlayers - 1)
        self.es = ExitStack()
        A = self.asc
        import os
        KPRE = int(os.environ.get("KPRE", "2"))
        r1_next = [0]

        def mkrot(name, k, use_r1=True):
            items = []
            for i in range(k):
                if use_r1 and r1_next[0] < 68:
                    j = r1_next[0]
                    r1_next[0] += 1
                    items.append((self.R1[:, j * 512:(j + 1) * 512], Buf("%s%d" % (name, i))))
                else:
                    t = self._talloc("a_" + name, [128, 512], BF16)
                    items.append((t[:], Buf("%s%d" % (name, i))))
            r = Rot.__new__(Rot)
            r.items = items
            r.i = 0
            return r

        def R(n, dt=BF16, k=2):
            r = self.trot("a_" + n, [128, 512], dt, k)
            r.items = [(t[:], b) for t, b in r.items]
            return r
        ofr, t5r = R("of", F32, 3), R("t5", F32, 2)
        zar, yar, fmo = R("za"), R("ya"), R("fmo")
        sta = self.trot("a_st", [128, 16], F32, 4)
        nmask, bnmask = self.tsb("a_nmask", [128, 14, 128], BF16)
        nmst, bnmst = self.wst.next()
        nmv = nmst[:].rearrange("p k n -> p (k n)")[:, 0:14 * 128].rearrange("p (a n) -> p a n", a=14)
        S.dma("sp", nmv, self.din["k_nm"].rearrange("a p n -> p a n"), writes=[bnmst])
        S.op("pool", lambda e: e.tensor_copy(out=nmask[:], in_=nmv), reads=[bnmst], writes=[bnmask])
        anw, banw = self.par["a_norm_w"]
        eps = self.epsb[:, 0:1]
        H = [slice(h * 128, (h + 1) * 128) for h in range(4)]
        v4 = lambda t: t[:].rearrange("p (h c) -> p h c", h=4)
        orders = [list(range(NT)), [1, 0] + list(range(NT - 1, 1, -1))]
        oa_written = set()
        from collections import deque
        free_banks = deque(range(8))

        def acq():
            while not free_banks:
                yield
            bk = free_banks.popleft()
            ps, bps = S.psum[bk]
            return ps, bps, bk

        def rel(bk):
            free_banks.append(bk)

        def mm4(lhs, blhs, rhs, brhs):
            ps, bps, bk = yield from acq()
            for h in range(4):
                S.op("pe", lambda e: e.matmul(out=ps[:, H[h]], lhsT=lhs[:, H[h]], rhs=rhs[:, H[h]], start=True, stop=True), reads=[blhs, brhs], writes=[bps])
            return ps, bps, bk

        def tr4(src, bsrc):
            ps, bps, bk = yield from acq()
            psb = ps[:].bitcast(BF16)
            for h in range(4):
                S.op("pe", lambda e: e.transpose(out=psb[:, H[h]], in_=src[:, H[h]], identity=self.ident_b[:]), reads=[bsrc, self.b_ident_b], writes=[bps])
            return psb, bps, bk

        class DirBufs:
            pass
        DB = []
        for d in range(2):
            o = DirBufs()
            for n in ("kT", "kTM", "vTM", "qT", "qk", "qd", "kt", "Pf"):
                setattr(o, n, mkrot("%s_%d" % (n, d), KPRE + 1))
            o.tsets = []
            for ts in range(KPRE):
                tsd = {n: mkrot("%s_%d_%d" % (n, d, ts), 1).items[0] for n in ("eg", "Ma", "Ml", "Pa", "Pb", "W1", "X", "MlmA", "MlmB")}
                for n in ("F0", "F1", "F2"):
                    tsd[n] = (self._talloc("a_%s_%d_%d" % (n, d, ts), [128, 512], F32)[:], Buf("%s_%d_%d" % (n, d, ts)))
                o.tsets.append(tsd)
            for n in ("Y", "vn", "sbf"):
                setattr(o, n, R("%s_%d" % (n, d)))
            o.S32 = self.tsb("a_S32_%d" % d, [128, 512])
            DB.append(o)

        def prep(d, cc, out, ts):
            B = DB[d]
            TS = B.tsets[ts]
            need_out = (cc >= 2) or (not last)
            out["need_out"] = need_out
            col = lambda name, h: A[name][0][:, cc, d * 4 + h:d * 4 + h + 1]
            tok = slice(cc * 128, (cc + 1) * 128)
            m_incl = self.amask[:, 2 * d, :]
            m_strict = self.amask[:, 2 * d + 1, :]
            nmb = lambda lev: nmask[:, d * 7 + lev, :].unsqueeze(1).to_broadcast([128, 4, 128])
            kT, bkT = B.kT.next()
            kTM, bkTM = B.kTM.next()
            vTM, bvTM = B.vTM.next()
            S.dma("sp", kT.rearrange("p (h t) -> p h t", h=4), scr["KA_FM"].rearrange("h p t -> p h t")[:, :, tok], reads=[self.db("KA_FM", cc)], writes=[bkT])
            S.dma("sp", kTM.rearrange("p (h c) -> p h c", h=4), scr["KA_TM"].rearrange("h t c -> t h c")[tok, :, :], reads=[self.db("KA_TM", cc)], writes=[bkTM])
            S.dma("sp", vTM.rearrange("p (h c) -> p h c", h=4), scr["VA_TM"].rearrange("h t c -> t h c")[tok, :, :], reads=[self.db("VA_TM", cc)], writes=[bvTM])
            out.update(kT=(kT, bkT), vTM=(vTM, bvTM))
            if need_out:
                qT, bqT = B.qT.next()
                S.dma("sp", qT.rearrange("p (h t) -> p h t", h=4), scr["QA_FM"].rearrange("h p t -> p h t")[:, :, tok], reads=[self.db("QA_FM", cc)], writes=[bqT])
            yield
            dg, bdg = TS["F0"]
            for h in range(4):
                S.op("dve", lambda e: e.tensor_scalar(out=dg[:, H[h]], in0=self.ident_f[:], scalar1=col("GC", h), scalar2=None, op0=ALU.mult),
                     reads=[self.b_ident_f, A["GC"][1]], writes=[bdg.sub(h)])
            p3, bp3, k3 = yield from acq()
            S.op("pe", lambda e: e.matmul(out=p3[:], lhsT=self.ones_f[:], rhs=dg[:], start=True, stop=True), reads=[self.b_ones_f, bdg], writes=[bp3])
            yield
            Dm2, bDm2 = TS["F1"]
            for h in range(4):
                S.op("dve", lambda e: e.scalar_tensor_tensor(out=Dm2[:, H[h]], in0=p3[:, H[h]], scalar=col("LNBMG", h), in1=m_strict, op0=ALU.add, op1=ALU.add),
                     reads=[bp3, A["LNBMG"][1], self.b_amask], writes=[bDm2.sub(h)])
            if need_out:
                Dm, bDm = TS["F2"]
                for h in range(4):
                    S.op("dve", lambda e: e.scalar_tensor_tensor(out=Dm[:, H[h]], in0=p3[:, H[h]], scalar=col("NEGG", h), in1=m_incl, op0=ALU.add, op1=ALU.add),
                         reads=[bp3, A["NEGG"][1], self.b_amask], writes=[bDm.sub(h)])
                eg, beg = TS["eg"]
                S.op("act", lambda e: e.activation(out=eg, in_=p3[:], func=AF.Exp), reads=[bp3, bDm, bDm2], writes=[beg])
            rel(k3)
            yield
            decb, bdecb = TS["F0"]
            S.op("act", lambda e: e.activation(out=decb[:], in_=Dm2[:], func=AF.Exp), reads=[bDm2], writes=[bdecb])
            p1, bp1, k1 = yield from mm4(kT, bkT, kT, bkT)
            yield
            Ma, bMa = TS["Ma"]
            S.op("dve", lambda e: e.tensor_tensor(out=Ma, in0=p1[:], in1=decb[:], op=ALU.mult), reads=[bp1, bdecb], writes=[bMa])
            rel(k1)
            yield
            psb, bps, kb = yield from tr4(Ma, bMa)
            nml = lambda lev: nmask[:, (1 - d) * 7 + lev, :].unsqueeze(1).to_broadcast([128, 4, 128])
            mlm = lambda lev: TS["MlmA" if lev % 2 else "MlmB"]
            S.op("dve", lambda e: e.tensor_tensor(out=v4(mlm(1)[0]), in0=psb[:, 0:512].rearrange("p (h c) -> p h c", h=4), in1=nml(1), op=ALU.mult),
                 reads=[bps, bnmask], writes=[mlm(1)[1]])
            Ml, bMl = TS["Ml"]
            S.op("act", lambda e: e.copy(out=Ml, in_=psb[:, 0:512]), reads=[bps, mlm(1)[1]], writes=[bMl])
            rel(kb)
            P, bP = TS["Pa"]
            S.op("pool", lambda e: e.tensor_tensor(out=v4(P), in0=v4(Ma), in1=nmb(0), op=ALU.mult), reads=[bMa, bnmask], writes=[bP])
            S.op("pool", lambda e: e.tensor_tensor(out=v4(P), in0=v4(P), in1=self.ident_b[:].unsqueeze(1).to_broadcast([128, 4, 128]), op=ALU.add),
                 reads=[bP, self.b_ident_b], writes=[bP])
            yield
            if need_out:
                dec, bdec = TS["F1"]
                S.op("act", lambda e: e.activation(out=dec[:], in_=Dm[:], func=AF.Exp), reads=[bDm], writes=[bdec])
                p2, bp2, k2 = yield from mm4(kT, bkT, qT, bqT)
                yield
                qk, bqk = B.qk.next()
                S.op("dve", lambda e: e.tensor_tensor(out=qk, in0=p2[:], in1=dec[:], op=ALU.mult), reads=[bp2, bdec], writes=[bqk])
                rel(k2)
                qd, bqd = B.qd.next()
                S.op("pool", lambda e: e.tensor_tensor(out=qd, in0=qT, in1=eg, op=ALU.mult), reads=[bqT, beg], writes=[bqd])
                out.update(qk=(qk, bqk), qd=(qd, bqd))
                yield
            kt, bkt = B.kt.next()
            for h in range(4):
                S.op("act", lambda e: e.activation(out=kt[:, H[h]], in_=kTM[:, H[h]], func=AF.Copy, scale=col("ETAIL", h)),
                     reads=[bkTM, A["ETAIL"][1]], writes=[bkt.sub(h)])
            out.update(kt=(kt, bkt))
            yield
            for lev in range(1, 7):
                cur, bcur = mlm(lev)
                psw, bpsw, kw = yield from mm4(cur, bcur, P, bP)
                psb, bps, kb = yield from tr4(P, bP)
                if lev < 6:
                    nxt_, bnxt_ = mlm(lev + 1)
                    S.op("pool", lambda e: e.tensor_tensor(out=v4(nxt_), in0=v4(Ml), in1=nml(lev + 1), op=ALU.mult), reads=[bMl, bnmask], writes=[bnxt_])
                yield
                W1, bW1 = TS["W1"]
                S.op("act", lambda e: e.copy(out=W1, in_=psw[:]), reads=[bpsw], writes=[bW1])
                rel(kw)
                X, bX = TS["X"]
                S.op("dve", lambda e: e.tensor_copy(out=X, in_=psb[:, 0:512]), reads=[bps], writes=[bX])
                rel(kb)
                yield
                ps2, bps2, k2 = yield from mm4(X, bX, W1, bW1)
                yield
                Pn, bPn = (B.Pf.next() if lev == 6 else TS["Pb" if lev % 2 == 1 else "Pa"])
                S.op("dve", lambda e: e.tensor_tensor(out=Pn, in0=ps2[:], in1=P, op=ALU.add), reads=[bps2, bP], writes=[bPn])
                rel(k2)
                P, bP = Pn, bPn
                yield
            out.update(P=(P, bP))

        def scan(d, cc, ops, st):
            B = DB[d]
            need_out = ops["need_out"]
            col = lambda name, h: A[name][0][:, cc, d * 4 + h:d * 4 + h + 1]
            tok = slice(cc * 128, (cc + 1) * 128)
            kT, bkT = ops["kT"]
            vTM, bvTM = ops["vTM"]
            kt, bkt = ops["kt"]
            P, bP = ops["P"]
            sbf, bsbf = st["sbf"]
            s32, bs32 = B.S32
            px, bpx, kx = yield from mm4(kT, bkT, sbf, bsbf)
            yield
            Y, bY = B.Y.next()
            for h in range(4):
                S.op("dve", lambda e: e.scalar_tensor_tensor(out=Y[:, H[h]], in0=px[:, H[h]], scalar=col("NEGEG", h), in1=vTM[:, H[h]], op0=ALU.mult, op1=ALU.add),
                     reads=[bpx, A["NEGEG"][1], bvTM], writes=[bY.sub(h)])
            rel(kx)
            yield
            pz, bpz, kz = yield from mm4(P, bP, Y, bY)
            yield
            vn, bvn = B.vn.next()
            for h in range(4):
                S.op("act", lambda e: e.activation(out=vn[:, H[h]], in_=pz[:, H[h]], func=AF.Copy, scale=col("BETA", h)), reads=[bpz, A["BETA"][1]], writes=[bvn.sub(h)])
            rel(kz)
            yield
            pS, bpS, kS = yield from mm4(kt, bkt, vn, bvn)
            if need_out:
                qk, bqk = ops["qk"]
                qd, bqd = ops["qd"]
                po, bpo, ko = yield from acq()
                for h in range(4):
                    S.op("pe", lambda e: e.matmul(out=po[:, H[h]], lhsT=qd[:, H[h]], rhs=sbf[:, H[h]], start=True, stop=False), reads=[bqd, bsbf], writes=[bpo])
                    S.op("pe", lambda e: e.matmul(out=po[:, H[h]], lhsT=qk[:, H[h]], rhs=vn[:, H[h]], start=False, stop=True), reads=[bqk, bvn], writes=[bpo])
            yield
            for h in range(4):
                S.op("dve", lambda e: e.scalar_tensor_tensor(out=s32[:, H[h]], in0=s32[:, H[h]], scalar=col("EGL", h), in1=pS[:, H[h]], op0=ALU.mult, op1=ALU.add),
                     reads=[bs32.sub(h), A["EGL"][1], bpS], writes=[bs32.sub(h)])
            rel(kS)
            sbf2, bsbf2 = B.sbf.next()
            S.op("act", lambda e: e.copy(out=sbf2, in_=s32[:]), reads=[bs32], writes=[bsbf2])
            st["sbf"] = (sbf2, bsbf2)
            yield
            if need_out:
                first = cc not in oa_written
                oa_written.add(cc)
                of, bof = ofr.next()
                if first:
                    S.op("act", lambda e: e.copy(out=of[:], in_=po[:]), reads=[bpo], writes=[bof])
                    rel(ko)
                    S.dma("act", scr["OA"][tok, :], of[:], reads=[bof], writes=[self.db("OA", cc)])
                    yield
                else:
                    S.dma("sp", of[:], scr["OA"][tok, :], reads=[self.db("OA", cc)], writes=[bof])
                    za, bza = zar.next()
                    S.dma("sp", za, scr["ZA"][tok, :], reads=[self.db("ZA", cc)], writes=[bza])
                    yield
                    S.op("dve", lambda e: e.tensor_tensor(out=of[:], in0=po[:], in1=of[:], op=ALU.add), reads=[bpo, bof], writes=[bof])
                    rel(ko)
                    t5, bt5 = t5r.next()
                    st_, bst = sta.next()
                    S.op("act", lambda e: e.activation(out=t5[:], in_=of[:], func=AF.Square), reads=[bof], writes=[bt5])
                    yield
                    S.op("dve", lambda e: e.tensor_reduce(out=st_[:, 0:4], in_=t5[:].rearrange("p (h c) -> p h c", h=4), axis=AX.X, op=ALU.add), reads=[bt5], writes=[bst])
                    S.op("act", lambda e: e.activation(out=st_[:, 4:8], in_=st_[:, 0:4], func=AF.Ln, scale=1.0 / 128, bias=eps), reads=[bst, self.b_epsb], writes=[bst])
                    S.op("act", lambda e: e.activation(out=st_[:, 8:12], in_=st_[:, 4:8], func=AF.Exp, scale=-0.5), reads=[bst], writes=[bst])
                    yield
                    ofv = of[:].rearrange("p (h c) -> p h c", h=4)
                    S.op("dve", lambda e: e.tensor_tensor(out=ofv, in0=ofv, in1=st_[:, 8:12].unsqueeze(2).to_broadcast([128, 4, 128]), op=ALU.mult), reads=[bof, bst], writes=[bof])
                    S.op("pool", lambda e: e.tensor_tensor(out=ofv, in0=ofv, in1=anw[:].unsqueeze(1).to_broadcast([128, 4, 128]), op=ALU.mult), reads=[bof, banw], writes=[bof])
                    yield
                    ya, bya = yar.next()
                    S.op("pool", lambda e: e.tensor_tensor(out=ya, in0=of[:], in1=za, op=ALU.mult), reads=[bof, bza], writes=[bya])
                    f, bf = fmo.next()
                    psb, bps, kb = yield from tr4(ya, bya)
                    S.op("act", lambda e: e.copy(out=f, in_=psb[:, 0:512]), reads=[bps], writes=[bf])
                    rel(kb)
                    S.dma("act", scr["Y_FM"][0].rearrange("(k p) t -> p k t", p=128)[:, :, tok], f.rearrange("p (k t) -> p k t", k=4),
                          reads=[bf], writes=[self.db("Y_FM0", cc)])
                    yield

        def chain(d):
            B = DB[d]
            s32, bs32 = B.S32
            S.op("pool", lambda e: e.memset(s32[:], 0.0), writes=[bs32])
            sbf, bsbf = B.sbf.next()
            S.op("pool", lambda e: e.memset(sbf, 0.0), writes=[bsbf])
            st = {"sbf": (sbf, bsbf)}
            order = orders[d]
            n = len(order)
            outs = [dict() for _ in range(n)]
            started = 0
            active = []
            done = set()

            def start_upto(j):
                nonlocal started
                while started <= min(j, n - 1):
                    active.append((started, prep(d, order[started], outs[started], started % KPRE)))
                    started += 1

            def step_preps():
                for item in list(active):
                    try:
                        next(item[1])
                    except StopIteration:
                        active.remove(item)
                        done.add(item[0])
            start_upto(0)
            while 0 not in done:
                step_preps()
                yield
            for i, cc in enumerate(order):
                start_upto(i + KPRE)
                sc = scan(d, cc, outs[i], st)
                sc_done = False
                while not sc_done or (i + 1 < n and (i + 1) not in done):
                    if not sc_done:
                        try:
                            next(sc)
                        except StopIteration:
                            sc_done = True
                    step_preps()
                    yield

        gens = [chain(0), chain(1)]
        while gens:
            for g_ in list(gens):
                try:
                    next(g_)
                except StopIteration:
                    gens.remove(g_)
        S.barrier()
        self.es.close()

    def phase5(self, l):
        S, scr, din = self.S, self.scr, self.din
        last = (l == self.nlayers - 1)
        self.es = ExitStack()
        wbr = self.R1[:, 14336:26624].rearrange("p (r n) -> p r n", n=1024)
        wo = self.R1[:, 26624:34816].rearrange("p (r n) -> p r n", n=1024)
        bwbr, bwo = Buf("wbr"), Buf("wo")

        def load_into(src_view, dst, bdst, nk):
            st, bst = self.wst.next()
            S.dma("sp", st[:, 0:nk, :], src_view, writes=[bst])
            S.op("pool", lambda e: e.tensor_copy(out=dst, in_=st[:, 0:nk, :]), reads=[bst], writes=[bdst])
        wbsrc = din["w_branch"][l].rearrange("b (k p) n -> p (b k) n", p=128)
        for half in range(2):
            for r0, nk in ((0, 8), (8, 4)):
                load_into(wbsrc[:, r0:r0 + nk, half * 512:(half + 1) * 512], wbr[:, r0:r0 + nk, half * 512:(half + 1) * 512], bwbr, nk)
        wosrc = din["w_out"][l].rearrange("(k p) n -> p k n", p=128)
        for half in range(2):
            load_into(wosrc[:, :, half * 512:(half + 1) * 512], wo[:, :, half * 512:(half + 1) * 512], bwo, 8)
        gate_bc, bgate = self.tsb("gate_bc", [128, 2, 1024])
        for s_ in range(2):
            if last and s_ == 1:
                continue
            self.bc_rows(lambda half: gate_bc[:, s_, half * 512:(half + 1) * 512], bgate, lambda kc: self.mod[:, 16 + kc, s_:s_ + 1], self.b_mod, 8)
        if last:
            fnw, bfnw = self.tsb("fnw_bc", [128, 1024])
            S.dma("sp", fnw[:], din["final_norm_w"].partition_broadcast(128), writes=[bfnw])
        yTr = self.trot("p5_yT", [128, 12, 512], BF16, 1)
        gmr = self.trot("p5_gm", [128, 512], BF16, 4)
        accr = self.trot("p5_acc", [128, 512], F32, 2)
        tmr = self.trot("p5_tm", [128, 512], F32, 2)
        mTr = self.trot("p5_mT", [128, 8, 512], BF16, 2)
        xtr = self.trot("p5_xt", [128, 1024], F32, 3)
        t1r = self.trot("p5_t1", [128, 1024], F32, 2)
        sqr, bsqr = self.tsb("p5_sq", [128, 1024])
        st5 = self.trot("p5_st", [128, 4], F32, 3)
        for (t0, n, tile0, ntile) in self.tok_groups():
            if last and t0 == 0:
                continue
            s_ = 1 if t0 == 0 else 0
            yT, byT = yTr.next()
            for br in range(3):
                S.dma("sp", yT[:, br * 4:(br + 1) * 4, 0:n], scr["Y_FM"][br].rearrange("(k p) t -> p k t", p=128)[:, :, t0:t0 + n],
                      reads=[self.db("Y_FM%d" % br, tile0 + i) for i in range(ntile)], writes=[byT.sub(br)])
            mT, bmT = mTr.next()
            pend = []

            def flush():
                while pend:
                    pend.pop(0)()
            acc_of = {}
            for dt in range(8):
                acc_of[dt] = accr.next()
                for br in range(3):
                    ct = br * 8 + dt
                    gm, bgm = gmr.next()
                    S.dma("sp", gm[:, 0:n], scr["GM_FM"][ct * 128:(ct + 1) * 128, t0:t0 + n],
                          reads=[self.db("GM_FM%d" % ct, tile0 + i) for i in range(ntile)], writes=[bgm])
                    ps, bps = S.ps()
                    for kc in range(4):
                        S.op("pe", lambda e: e.matmul(out=ps[:, 0:n], lhsT=wbr[:, br * 4 + kc, dt * 128:(dt + 1) * 128], rhs=yT[:, br * 4 + kc, 0:n],
                                                      start=(kc == 0), stop=(kc == 3)), reads=[bwbr, byT.sub(br)], writes=[bps])
                    flush()

                    def cons(dt=dt, br=br, ps=ps, bps=bps, gm=gm, bgm=bgm):
                        acc, bacc = acc_of[dt]
                        if br == 0:
                            S.op("dve", lambda e: e.tensor_tensor(out=acc[:, 0:n], in0=ps[:, 0:n], in1=gm[:, 0:n], op=ALU.mult), reads=[bps, bgm], writes=[bacc])
                        else:
                            tm, btm = tmr.next()
                            S.op("dve", lambda e: e.tensor_tensor(out=tm[:, 0:n], in0=ps[:, 0:n], in1=gm[:, 0:n], op=ALU.mult), reads=[bps, bgm], writes=[btm])
                            if br == 1:
                                S.op("pool", lambda e: e.tensor_tensor(out=acc[:, 0:n], in0=acc[:, 0:n], in1=tm[:, 0:n], op=ALU.add), reads=[bacc, btm], writes=[bacc])
                            else:
                                S.op("pool", lambda e: e.tensor_tensor(out=mT[:, dt, 0:n], in0=acc[:, 0:n], in1=tm[:, 0:n], op=ALU.add), reads=[bacc, btm], writes=[bmT.sub(dt)])
                    pend.append(cons)
            flush()
            for ti in range(ntile):
                tt = tile0 + ti
                xt, bxt = xtr.next()
                if tt < 2:
                    src = (din["ctx"] if l == 0 else scr["CTXS"])[tt * 128:(tt + 1) * 128, :]
                    rd = [] if l == 0 else [self.db("CTXS", tt)]
                else:
                    src = (din["x"] if l == 0 else scr["XS"])[(tt - 2) * 128:(tt - 1) * 128, :]
                    rd = [] if l == 0 else [self.db("XS", tt)]
                S.dma("sp", xt[:], src, reads=rd, writes=[bxt])
                pss = []
                for cg in range(2):
                    ps, bps = S.ps()
                    for kc in range(8):
                        S.op("pe", lambda e: e.matmul(out=ps[:], lhsT=mT[:, kc, ti * 128:(ti + 1) * 128], rhs=wo[:, kc, cg * 512:(cg + 1) * 512],
                                                      start=(kc == 0), stop=(kc == 7)), reads=[bmT, bwo], writes=[bps])
                    pss.append((ps, bps))
                flush()

                def cons2(tt=tt, xt=xt, bxt=bxt, pss=pss):
                    t1, bt1 = t1r.next()
                    for cg in range(2):
                        ps, bps = pss[cg]
                        S.op("dve", lambda e: e.tensor_tensor(out=t1[:, cg * 512:(cg + 1) * 512], in0=ps[:], in1=gate_bc[:, s_, cg * 512:(cg + 1) * 512], op=ALU.mult),
                             reads=[bps, bgate], writes=[bt1.sub(cg)])
                    S.op("pool", lambda e: e.tensor_tensor(out=t1[:], in0=t1[:], in1=xt[:], op=ALU.add), reads=[bt1, bxt], writes=[bt1])
                    if not last:
                        if tt < 2:
                            S.dma("act", scr["CTXS"][tt * 128:(tt + 1) * 128, :], t1[:], reads=[bt1], writes=[self.db("CTXS", tt)])
                        else:
                            S.dma("act", scr["XS"][(tt - 2) * 128:(tt - 1) * 128, :], t1[:], reads=[bt1], writes=[self.db("XS", tt)])
                    else:
                        st, bst = st5.next()
                        S.op("act", lambda e: e.activation(out=sqr[:], in_=t1[:], func=AF.Square, accum_out=st[:, 0:1]), reads=[bt1], writes=[bsqr, bst])
                        S.op("dve", lambda e: e.tensor_scalar(out=st[:, 1:2], in0=st[:, 0:1], scalar1=1.0 / D, scalar2=EPS, op0=ALU.mult, op1=ALU.add), reads=[bst], writes=[bst])
                        S.op("act", lambda e: e.activation(out=st[:, 2:3], in_=st[:, 1:2], func=AF.Ln), reads=[bst], writes=[bst])
                        S.op("act", lambda e: e.activation(out=st[:, 3:4], in_=st[:, 2:3], func=AF.Exp, scale=-0.5), reads=[bst], writes=[bst])
                        S.op("dve", lambda e: e.scalar_tensor_tensor(out=xt[:], in0=t1[:], scalar=st[:, 3:4], in1=fnw[:], op0=ALU.mult, op1=ALU.mult),
                             reads=[bt1, bst, bfnw, bxt], writes=[bxt])
                        S.dma("act", self.out[(tt - 2) * 128:(tt - 1) * 128, :], xt[:], reads=[bxt], writes=[self.db("OUT", tt)])
                pend.append(cons2)
            flush()
        S.barrier()
        self.es.close()

    def dump(self, name, ap, reads, shape, dtype=F32):
        o = self.nc.dram_tensor("dbg_" + name, shape, dtype, kind="ExternalOutput").ap()
        b = Buf("dbg_" + name)
        self.S.dma("sp", o, ap, reads=reads, writes=[b])
        self._dbgbufs.append(b)

    def dump_p2(self):
        self.dump("mod", self.mod[:], [self.b_mod], [128, 24, 2])
        self.dump("SCR", self.SCR[:], [self.b_SCR], [128, NT, 16])
        for n in self.asc:
            self.dump(n, self.asc[n][0][:], [self.asc[n][1]], [128, NT, 8])
        self.dump("SH2", self.SH2[:], [self.b_SH2], [128, NT, 8])
        self.dump("kmx", self.kmx[:], [self.b_kmx], [128, 4])
        self.dump("hT", self.hT, self.b_hT, [128, 8, NTOK], BF16)

    def program(self):
        S = self.S
        self._dbgbufs = []
        self.marks = []
        mark = lambda n: self.marks.append((n, {k: v.count for k, v in S.engs.items()}))
        for l in range(self.nlayers):
            mark("L%d start" % l)
            self.phase0(l)
            mark("L%d p0 done" % l)
            if self.stop == "p0":
                self.dump("mod", self.mod[:], [self.b_mod], [128, 24, 2])
                self.dump("Afm", self.Afm[:], [self.b_Afm], [128, 8, 2])
                self.dump("convw", self.convw[:], [self.b_convw], [128, 12, 5])
                self.dump("scol", self.scol[:], [self.b_scol], [128, 8, 2])
                break
            self.phase1(l)
            mark("L%d p1 done" % l)
            if self.stop == "p1":
                self.dump("hT", self.hT, self.b_hT, [128, 8, NTOK], BF16)
                break
            self.phase2(l)
            if self.stop is not None and self.stop.startswith("p2"):
                break
            mark("L%d p2 done" % l)
            if self.stop == "b":
                self.core_b(l)
                break
            self.core_bc(l)
            mark("L%d B done" % l)
            mark("L%d C done" % l)
            if self.stop == "c":
                break
            self.core_a(l)
            mark("L%d A done" % l)
            if self.stop == "a":
                break
            self.phase5(l)
            mark("L%d p5 done" % l)
            if self.stop == "p5":
                break
        S.barrier()
        return self.nc


def shard_inputs(inputs, b):
    m = {}
    for n in IN_SHAPES:
        a = np.asarray(inputs[n], dtype=np.float32)
        if n in ("x", "c", "ctx"):
            a = a[b]
        m[n] = np.ascontiguousarray(a)
    return m


_CACHE = {}


def kernel(**inputs):
    if "nc" not in _CACHE:
        _CACHE["nc"] = MK().program()
        _CACHE["consts"] = host_consts()
    nc = _CACHE["nc"]
    in_maps = []
    for b in range(8):
        m = shard_inputs(inputs, b)
        m.update(_CACHE["consts"])
        in_maps.append(m)
    res = run_bass_kernel_spmd(nc, in_maps, core_ids=list(range(8)))
    return np.stack([np.asarray(r["out"], dtype=np.float32) for r in res.results], axis=0)
```

```python
from contextlib import ExitStack
import numpy as np
import concourse.bass as bass
import concourse.mybir as mybir
from concourse.bass_utils import run_bass_kernel_spmd

F32 = mybir.dt.float32
BF16 = mybir.dt.bfloat16
AF = mybir.ActivationFunctionType
ALU = mybir.AluOpType
AX = mybir.AxisListType

T = 4096
LC = 256
D = 1024
NT = 34
NTOK = 4352
INW = 8464
CH = 128
EPS = 1e-6
NEG = -1.0e5
O_AQ, O_AK, O_AV, O_AZ, O_AB, O_BQ, O_BKV, O_BZ, O_CQ, O_CK, O_CV, O_CZ, O_MG = (
    0, 512, 1024, 1536, 2048, 2064, 2576, 2832, 3344, 3856, 4368, 4880, 5392)


class Buf:
    __slots__ = ("name", "w", "r", "parts")

    def __init__(self, name):
        self.name = name
        self.w = None
        self.r = {}
        self.parts = {}

    def sub(self, p):
        return Sub(self, p)


class Sub:
    __slots__ = ("parent", "p", "name")

    def __init__(self, parent, p):
        self.parent = parent
        self.p = p
        self.name = "%s[%s]" % (parent.name, p)

    def _slot(self):
        return self.parent.parts.setdefault(self.p, [None, {}])


class Eng:
    def __init__(self, key, e, sem):
        self.key = key
        self.e = e
        self.sem = sem
        self.count = 0
        self.waited = {}


class Sched:
    def __init__(self, nc, n_dma_sems=40):
        self.nc = nc
        self.sems = {}
        self.engs = {}
        for key, e in (("pe", nc.tensor), ("act", nc.scalar), ("dve", nc.vector), ("pool", nc.gpsimd), ("sp", nc.sync)):
            s = nc.alloc_semaphore("sem_" + key)
            self.sems[key] = s
            self.engs[key] = Eng(key, e, s)
        self.dma_sems = []
        for i in range(n_dma_sems):
            k = "dma%d" % i
            self.sems[k] = nc.alloc_semaphore("sem_" + k)
            self.dma_sems.append([k, 0])
        self.dma_rr = 0
        self.nops = 0
        self.clocks = {}
        self.psum = []
        self.ps_rr = 0
        for i in range(8):
            self.psum.append((nc.alloc_psum_tensor("psb%d" % i, [128, 512], F32), Buf("psb%d" % i)))

    def ps(self, pool=None):
        if pool is not None:
            banks, st = pool
            r = self.psum[banks[st[0] % len(banks)]]
            st[0] += 1
            return r
        r = self.psum[self.ps_rr]
        self.ps_rr = (self.ps_rr + 1) % 8
        return r

    def _deps(self, eng, reads, writes, is_dma):
        deps = {}

        def add(tok, same_ok):
            if tok is None:
                return
            k, v = tok
            if k == eng.key and not same_ok:
                return
            if deps.get(k, 0) < v:
                deps[k] = v

        same = is_dma or eng.key != "pe"
        for b in reads:
            if isinstance(b, Sub):
                add(b.parent.w, True)
                add(b._slot()[0], True)
            else:
                add(b.w, True)
                for pw, pr in b.parts.values():
                    add(pw, True)
        for b in writes:
            if isinstance(b, Sub):
                add(b.parent.w, same)
                for k, v in b.parent.r.items():
                    add((k, v), same)
                pw, pr = b._slot()
                add(pw, same)
                for k, v in pr.items():
                    add((k, v), same)
            else:
                add(b.w, same)
                for k, v in b.r.items():
                    add((k, v), same)
                for pw, pr in b.parts.values():
                    add(pw, same)
                    for k, v in pr.items():
                        add((k, v), same)
        for k, v in sorted(deps.items(), key=lambda kv: -kv[1]):
            self._need(eng, k, v)

    def _record(self, key, val, reads, writes):
        for b in reads:
            r = b._slot()[1] if isinstance(b, Sub) else b.r
            if r.get(key, 0) < val:
                r[key] = val
        for b in writes:
            if isinstance(b, Sub):
                sl = b._slot()
                sl[0] = (key, val)
                sl[1] = {}
            else:
                b.w = (key, val)
                b.r = {}
                b.parts = {}

    def _need(self, eng, k, v):
        if eng.waited.get(k, 0) >= v:
            return
        eng.e.wait_ge(self.sems[k], v)
        eng.waited[k] = v
        clk = self.clocks.get((k, v))
        if clk:
            w = eng.waited
            for k2, v2 in clk.items():
                if w.get(k2, 0) < v2:
                    w[k2] = v2

    def op(self, ek, fn, reads=(), writes=()):
        eng = self.engs[ek]
        self._deps(eng, reads, writes, False)
        ins = fn(eng.e)
        self.nops += 1
        eng.count += 1
        ins.then_inc(eng.sem, 1)
        clk = dict(eng.waited)
        clk.pop(eng.key, None)
        self.clocks[(eng.key, eng.count)] = clk
        self._record(eng.key, eng.count, reads, writes)
        return ins

    def dma(self, ek, out, in_, reads=(), writes=(), **kw):
        eng = self.engs[ek]
        self._deps(eng, reads, writes, True)
        slot = self.dma_sems[self.dma_rr]
        self.dma_rr = (self.dma_rr + 1) % len(self.dma_sems)
        k, uses = slot
        if uses > 0:
            self._need(eng, k, 16 * uses)
        ins = eng.e.dma_start(out=out, in_=in_, **kw)
        self.nops += 1
        slot[1] = uses + 1
        val = 16 * (uses + 1)
        ins.then_inc(self.sems[k], 16)
        clk = dict(eng.waited)
        clk.pop(eng.key, None)
        self.clocks[(k, val)] = clk
        self._record(k, val, reads, writes)
        return ins

    def wait_all(self, ek, bufs):
        eng = self.engs[ek]
        for b in bufs:
            toks = []
            if b.w is not None:
                toks.append(b.w)
            toks.extend(b.r.items())
            for pw, pr in b.parts.values():
                if pw is not None:
                    toks.append(pw)
                toks.extend(pr.items())
            for k, v in toks:
                if eng.waited.get(k, 0) < v:
                    eng.e.wait_ge(self.sems[k], v)
                    eng.waited[k] = v

    def barrier(self):
        for eng in self.engs.values():
            for o in self.engs.values():
                if o.key != eng.key and o.count > 0 and eng.waited.get(o.key, 0) < o.count:
                    eng.e.wait_ge(self.sems[o.key], o.count)
                    eng.waited[o.key] = o.count
            for k, uses in self.dma_sems:
                if uses > 0 and eng.waited.get(k, 0) < 16 * uses:
                    eng.e.wait_ge(self.sems[k], 16 * uses)
                    eng.waited[k] = 16 * uses


class Rot:
    def __init__(self, alloc, name, shape, dtype, n=2):
        self.items = [(alloc("%s%d" % (name, i), shape, dtype), Buf("%s%d" % (name, i))) for i in range(n)]
        self.i = 0

    def next(self):
        r = self.items[self.i]
        self.i = (self.i + 1) % len(self.items)
        return r


def host_consts():
    f = np.float32
    j = np.arange(128)[:, None]
    i = np.arange(128)[None, :]
    c = {}
    c["k_ident"] = np.eye(128, dtype=f)
    c["k_ones"] = np.ones((128, 128), f)
    am = np.zeros((4, 128, 128), f)
    am[0] = np.where(i >= j, 0.0, NEG)
    am[1] = np.where(i > j, 0.0, NEG)
    am[2] = np.where(i <= j, 0.0, NEG)
    am[3] = np.where(i < j, 0.0, NEG)
    c["k_amask"] = am
    tri = np.zeros((2, 128, 128), f)
    tri[0] = (j <= i)
    tri[1] = (j >= i)
    c["k_tri"] = tri
    bm = np.zeros((2, 128, 512), f)
    bm[0] = np.tile((j >= i).astype(f), (1, 4))
    bm[1] = np.tile((j <= i).astype(f), (1, 4))
    c["k_bmask"] = bm
    cm = np.zeros((6, 128, 128), f)
    cm[0] = np.maximum(i - j, 0)
    cm[1] = np.maximum(j - i, 0)
    cm[2] = (i > j)
    cm[3] = (j > i)
    cm[4] = np.broadcast_to(i + 1, (128, 128))
    cm[5] = np.broadcast_to(CH - i, (128, 128))
    c["k_cm"] = cm
    nm = np.zeros((14, 128, 128), f)
    for d in range(2):
        for lev in range(7):
            b = 1 << lev
            same = (j // (2 * b)) == (i // (2 * b))
            if d == 0:
                m = same & ((j % (2 * b)) < b) & ((i % (2 * b)) >= b)
            else:
                m = same & ((i % (2 * b)) < b) & ((j % (2 * b)) >= b)
            nm[d * 7 + lev] = -m.astype(f)
    c["k_nm"] = nm
    cj = np.zeros((128, 8), f)
    cj[:, 0:4] = (CH - 1 - np.arange(128))[:, None]
    cj[:, 4:8] = np.arange(128)[:, None]
    c["k_cj"] = cj
    t = np.arange(T)
    inv16 = (f(10000.0) ** (-np.arange(16, dtype=f) / f(16))).astype(f)
    ar = (t // 64).astype(f)[:, None] * inv16[None, :]
    ac = (t % 64).astype(f)[:, None] * inv16[None, :]
    ab = np.concatenate([ar, ac], axis=1).astype(f)
    c["k_cosb"] = np.tile(np.cos(ab).astype(f), (1, 8))
    c["k_sinb"] = np.tile(np.sin(ab).astype(f), (1, 8))
    inv64 = (f(10000.0) ** (-np.arange(64, dtype=f) / f(64))).astype(f)
    ang = t.astype(f)[:, None] * inv64[None, :]
    c["k_cosc"] = np.cos(ang).astype(f)
    c["k_sinc"] = np.sin(ang).astype(f)
    return c


CONST_SHAPES = {"k_ident": [128, 128], "k_ones": [128, 128], "k_amask": [4, 128, 128], "k_tri": [2, 128, 128],
                "k_bmask": [2, 128, 512], "k_cm": [6, 128, 128], "k_cj": [128, 8], "k_nm": [14, 128, 128],
                "k_cosb": [T, 256], "k_sinb": [T, 256], "k_cosc": [T, 64], "k_sinc": [T, 64]}

IN_SHAPES = {"x": [T, D], "c": [D], "ctx": [LC, D], "c_ctx": [D], "w_ada": [2, D, 3 * D], "b_ada": [2, 3 * D],
             "norm_w": [2, D], "w_in": [2, D, INW], "a_conv_w": [2, 5, 1536], "a_log": [2, 8], "a_dt_bias": [2, 8],
             "a_norm_w": [2, 128], "b_sink": [2, 8], "c_decay": [2, 8], "c_norm_w": [2, 512],
             "w_branch": [2, 3, 512, D], "w_out": [2, D, D], "final_norm_w": [D]}

SCRATCH = {"XS": ([T, D], F32), "CTXS": ([LC, D], F32),
           "QA_FM": ([4, 128, NTOK], BF16), "KA_FM": ([4, 128, NTOK], BF16),
           "KA_TM": ([4, NTOK, 128], BF16), "VA_TM": ([4, NTOK, 128], BF16),
           "ZA": ([NTOK, 512], BF16), "ZB": ([NTOK, 512], BF16), "ZC": ([NTOK, 512], BF16),
           "OA": ([NTOK, 512], F32),
           "QB_FM": ([8, 128, NTOK], BF16),
           "QC_FM": ([4, 128, NTOK], BF16), "KC_FM": ([4, 128, NTOK], BF16),
           "KC_TM": ([NTOK, 512], BF16), "VC_TM": ([NTOK, 512], BF16),
           "SCB": ([NT, 128, 512], BF16), "SCF": ([NT, 128, 512], BF16),
           "KB_FM": ([2, 128, NTOK], BF16), "VB_TM": ([NTOK, 2, 65], BF16),
           "GM_FM": ([3 * D, NTOK], BF16), "Y_FM": ([3, 512, NTOK], BF16)}


class MK:
    def __init__(self, nlayers=2, dbg=(), stop=None):
        nc = bass.Bass("TRN2", target_bir_lowering=False, dynamic_dma_scratch_size=1024)
        self.nc = nc
        self.S = Sched(nc)
        self.nlayers = nlayers
        self.stop = stop
        self.din = {}
        for n, shp in list(IN_SHAPES.items()) + list(CONST_SHAPES.items()):
            self.din[n] = nc.dram_tensor(n, shp, F32, kind="ExternalInput").ap()
        self.out = nc.dram_tensor("out", [T, D], F32, kind="ExternalOutput").ap()
        self.scr = {}
        self._db = {}
        for n, (shp, dt) in SCRATCH.items():
            kind = "ExternalOutput" if n in dbg else "Internal"
            self.scr[n] = nc.dram_tensor(n, shp, dt, kind=kind).ap()
        self.dbg = dbg
        self._tcount = 0
        self.alloc()

    def db(self, name, tt):
        k = (name, tt)
        if k not in self._db:
            self._db[k] = Buf("%s_%d" % k)
        return self._db[k]

    def dball(self, name):
        return [self.db(name, tt) for tt in range(NT)]

    def sb(self, name, shape, dtype=F32):
        return self.nc.alloc_sbuf_tensor(name, shape, dtype), Buf(name)

    def rot(self, name, shape, dtype=F32, n=2):
        return Rot(lambda nm, sh, dt: self.nc.alloc_sbuf_tensor(nm, sh, dt), name, shape, dtype, n)

    def _talloc(self, name, shape, dtype):
        self._tcount += 1
        return self.es.enter_context(self.nc.sbuf_tensor("%s_t%d" % (name, self._tcount), shape, dtype))

    def tsb(self, name, shape, dtype=F32):
        return self._talloc(name, shape, dtype), Buf(name)

    def trot(self, name, shape, dtype=F32, n=2):
        return Rot(self._talloc, name, shape, dtype, n)

    def alloc(self):
        nc, S, din = self.nc, self.S, self.din
        self.ident_f, self.b_ident_f = self.sb("ident_f", [128, 128])
        self.ones_f, self.b_ones_f = self.sb("ones_f", [128, 128])
        self.ident_b, self.b_ident_b = self.sb("ident_b", [128, 128], BF16)
        self.ones_b, self.b_ones_b = self.sb("ones_b", [128, 128], BF16)
        self.amask, self.b_amask = self.sb("amask", [128, 4, 128])
        self.tri, self.b_tri = self.sb("tri", [128, 2, 128])
        self.bmask, self.b_bmask = self.sb("bmask", [128, 2, 512], BF16)
        self.cm, self.b_cm = self.sb("cm", [128, 6, 128])
        self.cj, self.b_cj = self.sb("cj", [128, 8])
        S.dma("sp", self.ident_f[:], din["k_ident"], writes=[self.b_ident_f])
        S.dma("sp", self.ones_f[:], din["k_ones"], writes=[self.b_ones_f])
        S.dma("sp", self.amask[:], din["k_amask"].rearrange("a p n -> p a n"), writes=[self.b_amask])
        S.dma("sp", self.tri[:], din["k_tri"].rearrange("a p n -> p a n"), writes=[self.b_tri])
        S.dma("sp", self.cm[:], din["k_cm"].rearrange("a p n -> p a n"), writes=[self.b_cm])
        S.dma("sp", self.cj[:], din["k_cj"], writes=[self.b_cj])
        S.op("dve", lambda e: e.tensor_copy(out=self.ident_b[:], in_=self.ident_f[:]), reads=[self.b_ident_f], writes=[self.b_ident_b])
        S.op("dve", lambda e: e.tensor_copy(out=self.ones_b[:], in_=self.ones_f[:]), reads=[self.b_ones_f], writes=[self.b_ones_b])
        self.R1, self.b_R1 = self.sb("R1", [128, 8 * NTOK], BF16)
        self.hT = self.R1[:].rearrange("p (k t) -> p k t", k=8)
        self.b_hT = [Buf("hT%d" % i) for i in range(NT)]
        self.wst = self.rot("wst", [128, 8, 512], F32, 1)
        self.wb = self.rot("wb", [128, 8, 512], BF16, 4)
        self.scol, self.b_scol = self.sb("scol", [128, 8, 2])
        self.mod, self.b_mod = self.sb("mod", [128, 24, 2])
        self.nwcol, self.b_nwcol = self.sb("nwcol", [128, 8])
        self.badacol, self.b_badacol = self.sb("badacol", [128, 24])
        self.Afm, self.b_Afm = self.sb("Afm", [128, 8, 2])
        self.convw, self.b_convw = self.sb("convw", [128, 12, 5])
        self.rowtmp = self.rot("rowtmp", [128, 128], F32, 2)
        for it in self.rowtmp.items:
            S.op("pool", lambda e: e.memset(it[0][:], 0.0), writes=[it[1]])
        self.gdiag = self.rot("gdiag", [128, 512], F32, 2)
        self.SCR, self.b_SCR = self.sb("SCR", [128, NT, 16])
        self.asc = {}
        for n in ("BETA", "GC", "NEGG", "LNBMG", "NEGEG", "ETAIL", "EGL"):
            self.asc[n] = self.sb("asc_" + n, [128, NT, 8])
        self.SH2, self.b_SH2 = self.sb("SH2", [128, NT, 8])
        self.kmx, self.b_kmx = self.sb("kmx", [128, 4])
        self.par = {}
        for n, w in (("a_log", 8), ("a_dt_bias", 8), ("b_sink", 8), ("c_decay", 8), ("a_norm_w", 128), ("c_norm_w", 512)):
            self.par[n] = self.sb("par_" + n, [128, w])
        self.epsb, self.b_epsb = self.sb("epsb", [128, 4])
        for col, val in ((0, EPS), (1, -0.5 * float(np.log(128.0))), (2, 0.0), (3, 1.0)):
            S.op("pool", lambda e: e.memset(self.epsb[:, col:col + 1], val), writes=[self.b_epsb])

    def load_cols(self, src_rows, n, dst, bdst, func=None):
        S = self.S
        rt, brt = self.rowtmp.next()
        S.dma("sp", rt[0:n, :], src_rows, writes=[brt])
        if func is not None:
            S.op("act", lambda e: e.activation(out=rt[0:n, :], in_=rt[0:n, :], func=func), reads=[brt], writes=[brt])
        ps, bps = S.ps()
        S.op("pe", lambda e: e.transpose(out=ps[:, 0:128], in_=rt[:, :], identity=self.ident_f[:]),
             reads=[brt, self.b_ident_f], writes=[bps])
        S.op("dve", lambda e: e.tensor_copy(out=dst, in_=ps[:, 0:n]), reads=[bps], writes=[bdst])

    def w_plan(self, src2d, groups):
        self._wsrc = src2d
        self._wgroups = list(groups)
        self._wi = 0
        self._wq = []
        self._w_issue()

    def _w_issue(self):
        if self._wi < len(self._wgroups):
            c0, ncols = self._wgroups[self._wi]
            self._wi += 1
            self._wq.append(((c0, ncols), self._load_w_raw(self._wsrc, c0, ncols)))

    def load_w(self, src2d, c0, ncols, nk=8):
        if getattr(self, "_wq", None):
            key, val = self._wq.pop(0)
            assert key == (c0, ncols), (key, c0, ncols)
            self._w_issue()
            return val
        return self._load_w_raw(src2d, c0, ncols, nk)

    def _load_w_raw(self, src2d, c0, ncols, nk=8):
        S = self.S
        st, bst = self.wst.next()
        wb, bwb = self.wb.next()
        S.dma("sp", st[:, 0:nk, 0:ncols], src2d.rearrange("(k p) n -> p k n", p=128)[:, :, c0:c0 + ncols], writes=[bst])
        S.op("pool", lambda e: e.tensor_copy(out=wb[:, 0:nk, 0:ncols], in_=st[:, 0:nk, 0:ncols]), reads=[bst], writes=[bwb])
        return wb, bwb

    def phase0(self, l):
        S, din = self.S, self.din
        self.es = ExitStack()
        for n in self.par:
            t, b = self.par[n]
            S.dma("sp", t[:], din[n][l].partition_broadcast(128), writes=[b])
        self.load_cols(din["c"].rearrange("(k p) -> k p", p=128), 8, self.scol[:, :, 0], self.b_scol, AF.Silu)
        self.load_cols(din["c_ctx"].rearrange("(k p) -> k p", p=128), 8, self.scol[:, :, 1], self.b_scol, AF.Silu)
        self.load_cols(din["norm_w"][l].rearrange("(k p) -> k p", p=128), 8, self.nwcol[:], self.b_nwcol)
        self.load_cols(din["b_ada"][l].rearrange("(k p) -> k p", p=128), 24, self.badacol[:], self.b_badacol)
        cw, bcw = self.tsb("cwrows", [128, 1536])
        S.op("pool", lambda e: e.memset(cw[:], 0.0), writes=[bcw])
        S.dma("sp", cw[0:5, :], din["a_conv_w"][l], writes=[bcw])
        for ct in range(12):
            ps, bps = S.ps()
            S.op("pe", lambda e: e.transpose(out=ps[:, 0:128], in_=cw[:, ct * 128:(ct + 1) * 128], identity=self.ident_f[:]),
                 reads=[bcw, self.b_ident_f], writes=[bps])
            S.op("dve", lambda e: e.tensor_copy(out=self.convw[:, ct, :], in_=ps[:, 0:5]), reads=[bps], writes=[self.b_convw])
        psm, bpsm = S.ps()
        for g in range(6):
            st, bst = self.wst.next()
            S.dma("sp", st[:], din["w_ada"][l].rearrange("(k p) n -> p k n", p=128)[:, :, g * 512:(g + 1) * 512], writes=[bst])
            for jl in range(4):
                j = g * 4 + jl
                for kc in range(8):
                    S.op("pe", lambda e: e.matmul(out=psm[:, 2 * j:2 * j + 2], lhsT=st[:, kc, jl * 128:(jl + 1) * 128], rhs=self.scol[:, kc, :],
                                                  start=(kc == 0), stop=(kc == 7)),
                         reads=[bst, self.b_scol], writes=[bpsm])
        S.op("dve", lambda e: e.tensor_tensor(out=self.mod[:], in0=psm[:, 0:48].rearrange("p (j s) -> p j s", s=2),
                                              in1=self.badacol[:].unsqueeze(2).to_broadcast([128, 24, 2]), op=ALU.add),
             reads=[bpsm, self.b_badacol], writes=[self.b_mod])
        S.op("dve", lambda e: e.scalar_tensor_tensor(out=self.Afm[:], in0=self.mod[:, 8:16, :], scalar=1.0,
                                                     in1=self.nwcol[:].unsqueeze(2).to_broadcast([128, 8, 2]), op0=ALU.add, op1=ALU.mult),
             reads=[self.b_mod, self.b_nwcol], writes=[self.b_Afm])
        S.barrier()
        self.es.close()

    def bc_rows(self, dst_fn, bdst, col_fn, bcol, nchunks):
        S = self.S
        for half in range(nchunks // 4):
            t, bt = self.gdiag.next()
            for q in range(4):
                kc = half * 4 + q
                S.op("dve", lambda e: e.tensor_scalar(out=t[:, q * 128:(q + 1) * 128], in0=self.ident_f[:], scalar1=col_fn(kc),
                                                      scalar2=None, op0=ALU.mult),
                     reads=[self.b_ident_f, bcol], writes=[bt])
            ps, bps = S.ps()
            S.op("pe", lambda e: e.matmul(out=ps[:], lhsT=self.ones_f[:], rhs=t[:], start=True, stop=True),
                 reads=[self.b_ones_f, bt], writes=[bps])
            S.op("act", lambda e: e.copy(out=dst_fn(half), in_=ps[:]), reads=[bps], writes=[bdst])

    def phase1(self, l):
        S, din = self.S, self.din
        self.es = ExitStack()
        self.p1_xt = self.trot("p1_xt", [128, 1024], F32, 2)
        self.p1_sq, self.b_p1_sq = self.tsb("p1_sq", [128, 1024])
        self.p1_st = self.trot("p1_st", [128, 4], F32, 2)
        for tt in range(NT):
            s = 1 if tt < 2 else 0
            if tt < 2:
                src = (din["ctx"] if l == 0 else self.scr["CTXS"])[tt * 128:(tt + 1) * 128, :]
                rd = [] if l == 0 else [self.db("CTXS", tt)]
            else:
                src = (din["x"] if l == 0 else self.scr["XS"])[(tt - 2) * 128:(tt - 1) * 128, :]
                rd = [] if l == 0 else [self.db("XS", tt)]
            xt, bxt = self.p1_xt.next()
            st, bst = self.p1_st.next()
            S.dma("sp", xt[:], src, reads=rd, writes=[bxt])
            S.op("act", lambda e: e.activation(out=self.p1_sq[:], in_=xt[:], func=AF.Square, accum_out=st[:, 0:1]),
                 reads=[bxt], writes=[self.b_p1_sq, bst])
            S.op("dve", lambda e: e.tensor_scalar(out=st[:, 1:2], in0=st[:, 0:1], scalar1=1.0 / D, scalar2=EPS, op0=ALU.mult, op1=ALU.add),
                 reads=[bst], writes=[bst])
            S.op("act", lambda e: e.activation(out=st[:, 2:3], in_=st[:, 1:2], func=AF.Ln), reads=[bst], writes=[bst])
            S.op("act", lambda e: e.activation(out=st[:, 3:4], in_=st[:, 2:3], func=AF.Exp, scale=-0.5), reads=[bst], writes=[bst])
            S.op("act", lambda e: e.activation(out=xt[:], in_=xt[:], func=AF.Copy, scale=st[:, 3:4]),
                 reads=[bst, bxt], writes=[bxt])
            for half in range(2):
                ps, bps = S.ps()
                for q in range(4):
                    kc = half * 4 + q
                    S.op("pe", lambda e: e.transpose(out=ps[:, q * 128:(q + 1) * 128], in_=xt[:, kc * 128:(kc + 1) * 128], identity=self.ident_f[:]),
                         reads=[bxt, self.b_ident_f], writes=[bps])
                for q in range(4):
                    kc = half * 4 + q
                    dst = self.hT[:, kc, tt * 128:(tt + 1) * 128]
                    if half == 0:
                        S.op("dve", lambda e: e.tensor_scalar(out=dst, in0=ps[:, q * 128:(q + 1) * 128], scalar1=self.Afm[:, kc, s:s + 1],
                                                              scalar2=self.mod[:, kc, s:s + 1], op0=ALU.mult, op1=ALU.add),
                             reads=[bps, self.b_Afm, self.b_mod], writes=[self.b_hT[tt].sub(kc)])
                    else:
                        S.op("act", lambda e: e.activation(out=dst, in_=ps[:, q * 128:(q + 1) * 128], func=AF.Identity,
                                                           scale=self.Afm[:, kc, s:s + 1], bias=self.mod[:, kc, s:s + 1]),
                             reads=[bps, self.b_Afm, self.b_mod], writes=[self.b_hT[tt].sub(kc)])
        S.barrier()
        self.es.close()

    def tok_groups(self):
        g = [(0, 256, 0, 2)]
        for i in range(8):
            g.append((256 + i * 512, 512, 2 + 4 * i, 4))
        return g

    class WStream:
        def __init__(self, mk, src2d, groups, bufs):
            self.mk, self.src, self.groups, self.bufs = mk, src2d, list(groups), bufs
            self.i = 0
            self.q = []
            self._issue()

        def _issue(self):
            if self.i < len(self.groups):
                c0, ncols = self.groups[self.i]
                wb, bwb = self.bufs[self.i % len(self.bufs)]
                S = self.mk.S
                st, bst = self.mk.wst.next()
                S.dma("sp", st[:, :, 0:ncols], self.src.rearrange("(k p) n -> p k n", p=128)[:, :, c0:c0 + ncols], writes=[bst])
                S.op("pool", lambda e: e.tensor_copy(out=wb[:, :, 0:ncols], in_=st[:, :, 0:ncols]), reads=[bst], writes=[bwb])
                self.q.append(((c0, ncols), (wb, bwb)))
                self.i += 1

        def get(self, c0, ncols):
            key, val = self.q.pop(0)
            assert key == (c0, ncols), (key, c0, ncols)
            self._issue()
            return val

    def proj_tm_gen(self, l, c0, ncols, handler, ws):
        S = self.S
        if hasattr(self, "marks"):
            self.marks.append(("  L%d tm@%d" % (l, c0), {k: v.count for k, v in S.engs.items()}))
        wb, bwb = ws.get(c0, ncols)
        prev = None
        for tt in range(NT):
            ps, bps = S.ps()
            for kc in range(8):
                S.op("pe", lambda e: e.matmul(out=ps[:, 0:ncols], lhsT=self.hT[:, kc, tt * 128:(tt + 1) * 128], rhs=wb[:, kc, 0:ncols],
                                              start=(kc == 0), stop=(kc == 7)),
                     reads=[self.b_hT[tt], bwb], writes=[bps])
            if prev is not None:
                handler(*prev)
            prev = (tt, ps, bps)
            yield
        handler(*prev)
        yield

    def proj_tm(self, l, c0, ncols, handler, ws):
        for _ in self.proj_tm_gen(l, c0, ncols, handler, ws):
            pass

    @staticmethod
    def run_pair(g1, g2):
        gens = [g1, g2]
        while gens:
            for g_ in list(gens):
                try:
                    next(g_)
                except StopIteration:
                    gens.remove(g_)

    def transpose_out(self, src_fn, n, rows, dst, dst_buf_list, tag, pool=None):
        S = self.S
        ps, bps = S.ps(pool)
        psb = ps[:].bitcast(BF16)
        for i in range(n):
            src, bsrc = src_fn(i)
            S.op("pe", lambda e: e.transpose(out=psb[0:rows, i * 128:(i + 1) * 128], in_=src, identity=self.ident_b[:]),
                 reads=[bsrc, self.b_ident_b], writes=[bps])
        S.op("act", lambda e: e.copy(out=dst, in_=psb[0:rows, 0:n * 128]), reads=[bps], writes=dst_buf_list)

    def phase2(self, l):
        S, din, scr = self.S, self.din, self.scr
        last = (l == self.nlayers - 1)
        self.es = ExitStack()
        self.zt = self.trot("zt", [128, 512], BF16, 2)
        self.tmpA = self.trot("tmpA", [128, 512], F32, 2)
        self.tmpB = self.trot("tmpB", [128, 512], F32, 2)
        self.tmo = self.trot("tmo", [128, 512], BF16, 3)
        self.fmo = self.trot("fmo", [128, 512], BF16, 3)
        self.qa = self.trot("qa", [128, 8, 128], BF16, 2)
        self.ka = self.trot("ka", [128, 2, 128], BF16, 2)
        self.vb = self.trot("vb", [128, 2, 65], BF16, 2)
        self.qaT = self.trot("qaT", [128, 8, 128], BF16, 2)
        self.kaT = self.trot("kaT", [128, 2, 128], BF16, 2)
        self.csc = self.trot("csc", [128, 2, 64], F32, 3)
        self.csb = self.trot("csb", [128, 2, 256], F32, 3)
        self.st8 = self.trot("st8", [128, 24], F32, 4)
        self.tmp8 = self.trot("tmp8", [128, NT, 8], F32, 4)
        for n in ("LNB", "GRAW", "GT"):
            self.asc[n] = self.tsb("asc_" + n, [128, NT, 8])
        self.KM, self.b_KM = self.tsb("KM", [128, 2])
        self.half8, self.b_half8 = self.tsb("half8", [128, 8])
        S.op("pool", lambda e: e.memset(self.half8[:], 0.5), writes=[self.b_half8])
        self.rowbuf, self.b_rowbuf = self.tsb("rowbuf", [128, 4360])
        self.slrow, self.b_slrow = self.tsb("slrow", [128, NTOK])
        S.op("pool", lambda e: e.memset(self.rowbuf[:], 0.0), writes=[self.b_rowbuf])
        for it in self.vb.items:
            S.op("pool", lambda e: e.memset(it[0][:], 1.0), writes=[it[1]])
        for it in self.ka.items + self.qa.items:
            S.op("pool", lambda e: e.memset(it[0][:], 0.0), writes=[it[1]])
        for it in self.ka.items:
            S.op("pool", lambda e: e.memset(it[0][:, :, 64:65], 1.0), writes=[it[1]])
        S.op("pool", lambda e: e.memset(self.KM[:], 0.0), writes=[self.b_KM])

        def silu_out(name):
            def h(tt, ps, bps):
                z, bz = self.zt.next()
                S.op("act", lambda e: e.activation(out=z[:], in_=ps[:], func=AF.Silu), reads=[bps], writes=[bz])
                S.dma("act", scr[name][tt * 128:(tt + 1) * 128, :], z[:], reads=[bz], writes=[self.db(name, tt)])
            return h

        wsrc = din["w_in"][l]
        bufsA, bufsB = self.wb.items[0:2], self.wb.items[2:4]
        ws1 = self.WStream(self, wsrc, [(O_AZ, 512), (O_AB, 16), (O_BKV, 256)], bufsA)
        self.proj_tm(l, O_AZ, 512, silu_out("ZA"), ws1)
        if self.stop == "p2a":
            S.barrier()
            self.es.close()
            return

        def h_ab(tt, ps, bps):
            S.op("act", lambda e: e.copy(out=self.SCR[:, tt, :], in_=ps[:, 0:16]), reads=[bps], writes=[self.b_SCR])
        self.proj_tm(l, O_AB, 16, h_ab, ws1)
        if self.stop == "p2b1":
            S.barrier()
            self.es.close()
            return
        self.a_scalars(l)
        if self.stop in ("p2b", "p2b2"):
            S.barrier()
            self.es.close()
            return

        def load_cs(tt, which):
            rotp, cn, sn, w = (self.csc, "k_cosc", "k_sinc", 64) if which == "c" else (self.csb, "k_cosb", "k_sinb", 256)
            cs, bcs = rotp.next()
            r0 = (tt - 2) * 128
            S.dma("sp", cs[:, 0, :], din[cn][r0:r0 + 128, :], writes=[bcs])
            S.dma("sp", cs[:, 1, :], din[sn][r0:r0 + 128, :], writes=[bcs])
            return cs, bcs

        def rope(x1, x2, cosb, sinb, o1, o2, shape_fn, bps, bcs, bout, scale=None):
            ta, bta = self.tmpA.next()
            tb, btb = self.tmpB.next()
            ta1, ta2 = shape_fn(ta[:, 0:256]), shape_fn(ta[:, 256:512])
            tb1, tb2 = shape_fn(tb[:, 0:256]), shape_fn(tb[:, 256:512])
            if scale is None:
                mul = lambda o, a, b: (lambda e: e.tensor_tensor(out=o, in0=a, in1=b, op=ALU.mult))
            else:
                mul = lambda o, a, b: (lambda e: e.scalar_tensor_tensor(out=o, in0=a, scalar=scale, in1=b, op0=ALU.mult, op1=ALU.mult))
            S.op("dve", mul(ta1, x1, cosb), reads=[bps, bcs], writes=[bta])
            S.op("dve", mul(tb1, x2, sinb), reads=[bps, bcs], writes=[btb])
            S.op("dve", mul(ta2, x1, sinb), reads=[bps, bcs], writes=[bta])
            S.op("dve", mul(tb2, x2, cosb), reads=[bps, bcs], writes=[btb])
            S.op("pool", lambda e: e.tensor_tensor(out=o1, in0=ta1, in1=tb1, op=ALU.subtract), reads=[bta, btb], writes=[bout])
            S.op("pool", lambda e: e.tensor_tensor(out=o2, in0=ta2, in1=tb2, op=ALU.add), reads=[bta, btb], writes=[bout])

        def rope_b(ps_ap, nha, tt, dst, bdst, bps, nh):
            o, bo = self.tmo.next()
            if tt >= 2:
                cs, bcs = load_cs(tt, "b")
                pv = ps_ap.rearrange("p (g f k) -> p g f k", f=2, k=16)
                ov = o[:, 0:nha * 32].rearrange("p (g f k) -> p g f k", f=2, k=16)
                cosb = cs[:, 0, 0:nha * 16].rearrange("p (g k) -> p g k", k=16)
                sinb = cs[:, 1, 0:nha * 16].rearrange("p (g k) -> p g k", k=16)
                rope(pv[:, :, 0, :], pv[:, :, 1, :], cosb, sinb, ov[:, :, 0, :], ov[:, :, 1, :],
                     lambda a: a[:, 0:nha * 16].rearrange("p (g k) -> p g k", k=16), bps, bcs, bo)
                S.op("act", lambda e: e.copy(out=dst[:, :, 0:64], in_=o[:, 0:nha * 32].rearrange("p (h k) -> p h k", h=nh)), reads=[bo], writes=[bdst])
            else:
                S.op("act", lambda e: e.copy(out=dst[:, :, 0:64], in_=ps_ap.rearrange("p (h k) -> p h k", h=nh)), reads=[bps], writes=[bdst])

        def h_bkv(tt, ps, bps):
            ka, bka = self.ka.next()
            rope_b(ps[:, 0:128], 4, tt, ka, bka, bps, 2)
            ta, bta = self.tmpA.next()
            st, bst = self.st8.next()
            S.op("act", lambda e: e.activation(out=ta[:, 0:128], in_=ps[:, 0:128], func=AF.Square), reads=[bps], writes=[bta])
            S.op("dve", lambda e: e.tensor_reduce(out=st[:, 0:2], in_=ta[:, 0:128].rearrange("p (h k) -> p h k", h=2), axis=AX.X, op=ALU.add),
                 reads=[bta], writes=[bst])
            S.op("dve", lambda e: e.tensor_tensor(out=self.KM[:], in0=self.KM[:], in1=st[:, 0:2], op=ALU.max), reads=[bst, self.b_KM], writes=[self.b_KM])
            vb, bvb = self.vb.next()
            S.op("dve", lambda e: e.tensor_copy(out=vb[:, :, 0:64], in_=ps[:, 128:256].rearrange("p (h k) -> p h k", h=2)),
                 reads=[bps], writes=[bvb])
            S.dma("act", scr["VB_TM"][tt * 128:(tt + 1) * 128, :, :], vb[:], reads=[bvb], writes=[self.db("VB_TM", tt)])
            kT, bkT = self.kaT.next()
            self.transpose_out(lambda i: (ka[:, i, :], bka), 2, 128, kT[:].rearrange("r h t -> r (h t)"), [bkT], "kbt")
            S.dma("act", scr["KB_FM"].rearrange("h r t -> r h t")[:, :, tt * 128:(tt + 1) * 128], kT[:], reads=[bkT], writes=[self.db("KB_FM", tt)])
        self.proj_tm(l, O_BKV, 256, h_bkv, ws1)
        if self.stop == "p2c":
            S.barrier()
            self.es.close()
            return
        S.op("dve", lambda e: e.tensor_reduce(out=self.kmx[:, 0:1], in_=self.KM[:], axis=AX.X, op=ALU.max), reads=[self.b_KM], writes=[self.b_kmx])
        dgk, bdgk = self.gdiag.next()
        S.op("dve", lambda e: e.tensor_scalar(out=dgk[:, 0:128], in0=self.ident_f[:], scalar1=self.kmx[:, 0:1], scalar2=None, op0=ALU.mult),
             reads=[self.b_ident_f, self.b_kmx], writes=[bdgk])
        ps, bps = S.ps()
        S.op("pe", lambda e: e.matmul(out=ps[:, 0:128], lhsT=self.ones_f[:], rhs=dgk[:, 0:128], start=True, stop=True),
             reads=[self.b_ones_f, bdgk], writes=[bps])
        kr, bkr = self.tsb("kmrow", [128, 4])
        S.op("dve", lambda e: e.tensor_reduce(out=kr[:, 0:1], in_=ps[:, 0:128], axis=AX.X, op=ALU.max), reads=[bps], writes=[bkr])
        S.op("act", lambda e: e.activation(out=kr[:, 1:2], in_=kr[:, 0:1], func=AF.Ln), reads=[bkr], writes=[bkr])
        S.op("act", lambda e: e.activation(out=kr[:, 2:3], in_=kr[:, 1:2], func=AF.Exp, scale=0.5), reads=[bkr], writes=[bkr])
        S.op("dve", lambda e: e.tensor_scalar(out=self.kmx[:, 1:2], in0=kr[:, 2:3], scalar1=-1.0, scalar2=None, op0=ALU.mult), reads=[bkr], writes=[self.b_kmx])
        S.op("dve", lambda e: e.tensor_scalar(out=self.kmx[:, 2:3], in0=kr[:, 2:3], scalar1=-0.125, scalar2=None, op0=ALU.mult), reads=[bkr], writes=[self.b_kmx])

        def h_bq(tt, ps, bps):
            qa, bqa = self.qa.next()
            ta, bta = self.tmpA.next()
            st, bst = self.st8.next()
            S.op("act", lambda e: e.activation(out=ta[:], in_=ps[:], func=AF.Square), reads=[bps], writes=[bta])
            S.op("dve", lambda e: e.tensor_reduce(out=st[:, 0:8], in_=ta[:].rearrange("p (h k) -> p h k", h=8), axis=AX.X, op=ALU.add),
                 reads=[bta], writes=[bst])
            S.op("pool", lambda e: e.tensor_tensor(out=st[:, 16:24], in0=st[:, 0:8], in1=self.half8[:], op=ALU.pow), reads=[bst, self.b_half8], writes=[bst])
            rope_b(ps[:], 16, tt, qa, bqa, bps, 8)
            S.op("dve", lambda e: e.tensor_scalar(out=qa[:, :, 64], in0=st[:, 16:24], scalar1=self.kmx[:, 1:2], scalar2=None, op0=ALU.mult),
                 reads=[bst, self.b_kmx], writes=[bqa])
            S.op("dve", lambda e: e.scalar_tensor_tensor(out=self.SH2[:, tt, :], in0=st[:, 16:24], scalar=self.kmx[:, 2:3], in1=self.par["b_sink"][0][:],
                                                         op0=ALU.mult, op1=ALU.add),
                 reads=[bst, self.b_kmx, self.par["b_sink"][1]], writes=[self.b_SH2])
            qT, bqT = self.qaT.next()
            self.transpose_out(lambda i: (qa[:, i, :], bqa), 8, 128, qT[:].rearrange("r h t -> r (h t)"), [bqT], "qbt")
            S.dma("act", scr["QB_FM"].rearrange("h r t -> r h t")[:, :, tt * 128:(tt + 1) * 128], qT[:], reads=[bqT], writes=[self.db("QB_FM", tt)])
        if self.stop == "p2d":
            S.barrier()
            self.es.close()
            return

        def h_cqk(name_fm, name_tm, scale):
            def h(tt, ps, bps):
                o, bo = self.tmo.next()
                if tt >= 2:
                    cs, bcs = load_cs(tt, "c")
                    pv = ps[:].rearrange("p (h f k) -> p h f k", h=4, f=2)
                    ov = o[:].rearrange("p (h f k) -> p h f k", h=4, f=2)
                    cosb = cs[:, 0, :].unsqueeze(1).to_broadcast([128, 4, 64])
                    sinb = cs[:, 1, :].unsqueeze(1).to_broadcast([128, 4, 64])
                    rope(pv[:, :, 0, :], pv[:, :, 1, :], cosb, sinb, ov[:, :, 0, :], ov[:, :, 1, :],
                         lambda a: a.rearrange("p (h k) -> p h k", h=4), bps, bcs, bo, scale=scale)
                else:
                    S.op("act", lambda e: e.activation(out=o[:], in_=ps[:], func=AF.Copy, scale=(1.0 if scale is None else scale)), reads=[bps], writes=[bo])
                if name_tm is not None:
                    S.dma("act", scr[name_tm][tt * 128:(tt + 1) * 128, :], o[:], reads=[bo], writes=[self.db(name_tm, tt)])
                f, bf = self.fmo.next()
                self.transpose_out(lambda i: (o[:, i * 128:(i + 1) * 128], bo), 4, 128, f[:], [bf], "cfm")
                S.dma("act", scr[name_fm].rearrange("h p t -> p h t")[:, :, tt * 128:(tt + 1) * 128], f[:].rearrange("p (h t) -> p h t", h=4),
                      reads=[bf], writes=[self.db(name_fm, tt)])
            return h

        def h_cv(tt, ps, bps):
            o, bo = self.tmo.next()
            S.op("act", lambda e: e.copy(out=o[:], in_=ps[:]), reads=[bps], writes=[bo])
            S.dma("act", scr["VC_TM"][tt * 128:(tt + 1) * 128, :], o[:], reads=[bo], writes=[self.db("VC_TM", tt)])
        wsZ = self.WStream(self, wsrc, [(O_BZ, 512), (O_CZ, 512)], bufsB)
        self.proj_tm(l, O_BZ, 512, silu_out("ZB"), wsZ)
        self.proj_tm(l, O_CZ, 512, silu_out("ZC"), wsZ)
        groups = self.tok_groups()
        wsH = self.WStream(self, wsrc, [(O_BQ, 512), (O_CQ, 512), (O_CK, 512)] + [(g * 512, 512) for g in range(3)], bufsA)
        wsL = self.WStream(self, wsrc, [(O_MG + g * 512, 512) for g in range(6)] + [(O_CV, 512)], bufsB)
        wsF = wsH
        wsM = wsL

        def heavy():
            yield from self.proj_tm_gen(l, O_BQ, 512, h_bq, wsH)
            yield from self.proj_tm_gen(l, O_CQ, 512, h_cqk("QC_FM", None, None), wsH)
            yield from self.proj_tm_gen(l, O_CK, 512, h_cqk("KC_FM", "KC_TM", float(CH) ** -0.5), wsH)

        def light():
            yield from self.proj_tm_gen(l, O_CV, 512, h_cv, wsL)
        if self.stop == "p2e":
            S.barrier()
            self.es.close()
            return


        def afm_gen():
            wcur = {}

            def get_w(g3):
                if g3 not in wcur:
                    self.marks.append(("  L%d afm%d" % (l, g3), {k: v.count for k, v in S.engs.items()}))
                    wcur[g3] = wsF.get(g3 * 512, 512)
                return wcur[g3]

            def proj_piece(ct, gi):
                g3, cl = ct // 4, ct % 4
                wb, bwb = get_w(g3)
                (t0, n, tile0, ntile) = groups[gi]
                ps, bps = S.ps()
                for kc in range(8):
                    S.op("pe", lambda e: e.matmul(out=ps[:, 0:n], lhsT=wb[:, kc, cl * 128:(cl + 1) * 128], rhs=self.hT[:, kc, t0:t0 + n],
                                                  start=(kc == 0), stop=(kc == 7)),
                         reads=[self.b_hT[tile0 + i] for i in range(ntile)] + [bwb], writes=[bps])
                off = 2 + t0 if t0 == 0 else 6 + t0
                S.op("act", lambda e: e.copy(out=self.rowbuf[:, off:off + n], in_=ps[:, 0:n]), reads=[bps], writes=[self.b_rowbuf.sub(t0)])

            def pass1_piece(ct, gi):
                g3, head = ct // 4, ct % 4
                (t0, n, tile0, ntile) = groups[gi]
                off = 2 + t0 if t0 == 0 else 6 + t0
                cv, bcv = self.tmpA.next()
                nb = [gi] if gi == 0 else [j for j in (gi - 1, gi, gi + 1) if 1 <= j < len(groups)]
                rb_reads = [self.b_rowbuf.sub(groups[j][0]) for j in nb]
                for k in range(5):
                    src = self.rowbuf[:, off + k - 2:off + k - 2 + n]
                    if k == 0:
                        S.op("dve", lambda e: e.tensor_scalar(out=cv[:, 0:n], in0=src, scalar1=self.convw[:, ct, 0:1], scalar2=None, op0=ALU.mult),
                             reads=rb_reads + [self.b_convw], writes=[bcv])
                    else:
                        S.op("dve", lambda e: e.scalar_tensor_tensor(out=cv[:, 0:n], in0=src, scalar=self.convw[:, ct, k:k + 1], in1=cv[:, 0:n],
                                                                     op0=ALU.mult, op1=ALU.add),
                             reads=rb_reads + [self.b_convw, bcv], writes=[bcv])
                if g3 < 2:
                    S.op("act", lambda e: e.activation(out=self.slrow[:, t0:t0 + n], in_=cv[:, 0:n], func=AF.Silu), reads=[bcv], writes=[self.b_slrow.sub(t0)])
                else:
                    o, bo = self.tmo.next()
                    S.op("act", lambda e: e.activation(out=o[:, 0:n], in_=cv[:, 0:n], func=AF.Silu), reads=[bcv], writes=[bo])
                    f, bf = self.fmo.next()
                    self.transpose_out(lambda i: (o[:, i * 128:(i + 1) * 128], bo), ntile, 128, f[:, 0:n], [bf], "va")
                    S.dma("act", scr["VA_TM"][head].rearrange("(t p) c -> p t c", p=128)[:, tile0:tile0 + ntile, :],
                          f[:, 0:n].rearrange("p (t c) -> p t c", c=128), reads=[bf], writes=[self.db("VA_TM", tile0 + i) for i in range(ntile)])

            def pass2_piece(ct, gi):
                g3, head = ct // 4, ct % 4
                name = "QA_FM" if g3 == 0 else "KA_FM"
                (t0, n, tile0, ntile) = groups[gi]
                sq, bsq = self.tmo.next()
                S.op("act", lambda e: e.activation(out=sq[:, 0:n], in_=self.slrow[:, t0:t0 + n], func=AF.Square), reads=[self.b_slrow.sub(t0)], writes=[bsq])
                ps, bps = S.ps()
                S.op("pe", lambda e: e.matmul(out=ps[:, 0:n], lhsT=self.ones_b[:], rhs=sq[:, 0:n], start=True, stop=True),
                     reads=[self.b_ones_b, bsq], writes=[bps])
                ta, bta = self.tmpB.next()
                S.op("act", lambda e: e.activation(out=ta[:, 0:n], in_=ps[:, 0:n], func=AF.Ln, bias=self.epsb[:, 0:1]), reads=[bps, self.b_epsb], writes=[bta])
                S.op("act", lambda e: e.activation(out=ta[:, 0:n], in_=ta[:, 0:n], func=AF.Exp, scale=-0.5, bias=self.epsb[:, 1 + g3:2 + g3]),
                     reads=[bta, self.b_epsb], writes=[bta])
                o, bo = self.fmo.next()
                S.op("dve", lambda e: e.tensor_tensor(out=o[:, 0:n], in0=self.slrow[:, t0:t0 + n], in1=ta[:, 0:n], op=ALU.mult),
                     reads=[self.b_slrow.sub(t0), bta], writes=[bo])
                S.dma("act", scr[name][head][:, t0:t0 + n], o[:, 0:n], reads=[bo], writes=[self.db(name, tile0 + i) for i in range(ntile)])
                if g3 == 1:
                    f, bf = self.zt.next()
                    self.transpose_out(lambda i: (o[:, i * 128:(i + 1) * 128], bo), ntile, 128, f[:, 0:n], [bf], "ka")
                    S.dma("act", scr["KA_TM"][head].rearrange("(t p) c -> p t c", p=128)[:, tile0:tile0 + ntile, :],
                          f[:, 0:n].rearrange("p (t c) -> p t c", c=128), reads=[bf], writes=[self.db("KA_TM", tile0 + i) for i in range(ntile)])

            ng = len(groups)
            for gi in range(ng):
                proj_piece(0, gi)
                yield
            for ct in range(12):
                for gi in range(ng):
                    pass1_piece(ct, gi)
                    if ct + 1 < 12 and gi >= 1:
                        proj_piece(ct + 1, gi - 1)
                    yield
                if ct + 1 < 12:
                    proj_piece(ct + 1, ng - 1)
                    yield
                if ct < 8:
                    for gi in range(ng):
                        pass2_piece(ct, gi)
                        yield

        def merge_gen():
            for g6 in range(6):
                self.marks.append(("  L%d mg%d" % (l, g6), {k: v.count for k, v in S.engs.items()}))
                wb, bwb = wsM.get(O_MG + g6 * 512, 512)
                for cl in range(4):
                    ct = g6 * 4 + cl
                    for (t0, n, tile0, ntile) in groups:
                        if last and t0 == 0:
                            continue
                        ps, bps = S.ps()
                        for kc in range(8):
                            S.op("pe", lambda e: e.matmul(out=ps[:, 0:n], lhsT=wb[:, kc, cl * 128:(cl + 1) * 128], rhs=self.hT[:, kc, t0:t0 + n],
                                                          start=(kc == 0), stop=(kc == 7)),
                                 reads=[self.b_hT[tile0 + i] for i in range(ntile)] + [bwb], writes=[bps])
                        o, bo = self.fmo.next()
                        S.op("act", lambda e: e.activation(out=o[:, 0:n], in_=ps[:, 0:n], func=AF.Sigmoid), reads=[bps], writes=[bo])
                        S.dma("act", scr["GM_FM"][ct * 128:(ct + 1) * 128, t0:t0 + n], o[:, 0:n], reads=[bo],
                              writes=[self.db("GM_FM%d" % ct, tile0 + i) for i in range(ntile)])
                        yield
        self.run_pair(heavy(), merge_gen())
        self.run_pair(afm_gen(), light())
        if self.stop == "p2":
            self.dump_p2()
        S.barrier()
        self.es.close()

    def a_scalars(self, l):
        S = self.S
        A = self.asc
        braw = self.SCR[:, :, 0:8]
        araw = self.SCR[:, :, 8:16]
        rs = [self.b_SCR]
        one = self.epsb[:, 3:4]

        def softplus_parts(x_ap, xb, neg):
            t1, b1 = self.tmp8.next()
            t2, b2 = self.tmp8.next()
            S.op("act", lambda e: e.activation(out=t1[:], in_=x_ap, func=AF.Abs), reads=xb, writes=[b1])
            S.op("act", lambda e: e.activation(out=t1[:], in_=t1[:], func=AF.Exp, scale=-1.0), reads=[b1], writes=[b1])
            S.op("act", lambda e: e.activation(out=t1[:], in_=t1[:], func=AF.Ln, bias=one), reads=[b1, self.b_epsb], writes=[b1])
            S.op("dve", lambda e: e.tensor_scalar(out=t2[:], in0=x_ap, scalar1=(-1.0 if neg else 1.0), scalar2=0.0, op0=ALU.mult, op1=ALU.max),
                 reads=xb, writes=[b2])
            return (t2, b2), (t1, b1)

        (m, bm), (l1, bl1) = softplus_parts(braw, rs, True)
        LNB, bLNB = A["LNB"]
        S.op("dve", lambda e: e.scalar_tensor_tensor(out=LNB[:], in0=m[:], scalar=-1.0, in1=l1[:], op0=ALU.mult, op1=ALU.subtract),
             reads=[bm, bl1], writes=[bLNB])
        BETA, bBETA = A["BETA"]
        S.op("act", lambda e: e.activation(out=BETA[:], in_=LNB[:], func=AF.Exp), reads=[bLNB], writes=[bBETA])
        xa, bxa = self.tmp8.next()
        dtb, bdtb = self.par["a_dt_bias"]
        S.op("dve", lambda e: e.tensor_tensor(out=xa[:], in0=araw, in1=dtb[:].unsqueeze(1).to_broadcast([128, NT, 8]), op=ALU.add),
             reads=rs + [bdtb], writes=[bxa])
        (m2, bm2), (l2, bl2) = softplus_parts(xa[:], [bxa], False)
        S.op("dve", lambda e: e.tensor_tensor(out=m2[:], in0=m2[:], in1=l2[:], op=ALU.add), reads=[bm2, bl2], writes=[bm2])
        alog, balog = self.par["a_log"]
        nea, bnea = self.st8.next()
        S.op("act", lambda e: e.activation(out=nea[:, 0:8], in_=alog[:], func=AF.Exp), reads=[balog], writes=[bnea])
        GRAW, bGRAW = A["GRAW"]
        S.op("dve", lambda e: e.scalar_tensor_tensor(out=GRAW[:], in0=m2[:], scalar=-1.0, in1=nea[:, 0:8].unsqueeze(1).to_broadcast([128, NT, 8]),
                                                     op0=ALU.mult, op1=ALU.mult),
             reads=[bm2, bnea], writes=[bGRAW])
        GC, bGC = A["GC"]
        GT, bGT = A["GT"]
        if self.stop == "p2b2":
            return
        gflat = GRAW[:].rearrange("p t c -> p (t c)")
        res = []
        for lhs, blhs in ((self.tri[:, 0, :], self.b_tri), (self.tri[:, 1, :], self.b_tri), (self.ones_f[:], self.b_ones_f)):
            ps, bps = S.ps()
            S.op("pe", lambda e: e.matmul(out=ps[:, 0:NT * 8], lhsT=lhs, rhs=gflat, start=True, stop=True), reads=[blhs, bGRAW], writes=[bps])
            res.append((ps[:, 0:NT * 8].rearrange("p (t c) -> p t c", c=8), bps))
        S.op("dve", lambda e: e.tensor_copy(out=GC[:, :, 0:4], in_=res[0][0][:, :, 0:4]), reads=[res[0][1]], writes=[bGC])
        S.op("dve", lambda e: e.tensor_copy(out=GC[:, :, 4:8], in_=res[1][0][:, :, 4:8]), reads=[res[1][1]], writes=[bGC])
        S.op("act", lambda e: e.copy(out=GT[:], in_=res[2][0]), reads=[res[2][1]], writes=[bGT])
        NEGG, bNEGG = A["NEGG"]
        S.op("dve", lambda e: e.tensor_scalar(out=NEGG[:], in0=GC[:], scalar1=-1.0, scalar2=None, op0=ALU.mult), reads=[bGC], writes=[bNEGG])
        LNBMG, bLNBMG = A["LNBMG"]
        S.op("dve", lambda e: e.tensor_tensor(out=LNBMG[:], in0=LNB[:], in1=GC[:], op=ALU.subtract), reads=[bLNB, bGC], writes=[bLNBMG])
        NEGEG, bNEGEG = A["NEGEG"]
        S.op("act", lambda e: e.activation(out=NEGEG[:], in_=GC[:], func=AF.Exp), reads=[bGC], writes=[bNEGEG])
        S.op("dve", lambda e: e.tensor_scalar(out=NEGEG[:], in0=NEGEG[:], scalar1=-1.0, scalar2=None, op0=ALU.mult), reads=[bNEGEG], writes=[bNEGEG])
        ETAIL, bETAIL = A["ETAIL"]
        S.op("dve", lambda e: e.tensor_tensor(out=ETAIL[:], in0=GT[:], in1=GC[:], op=ALU.subtract), reads=[bGT, bGC], writes=[bETAIL])
        S.op("act", lambda e: e.activation(out=ETAIL[:], in_=ETAIL[:], func=AF.Exp), reads=[bETAIL], writes=[bETAIL])
        EGL, bEGL = A["EGL"]
        S.op("act", lambda e: e.activation(out=EGL[:], in_=GT[:], func=AF.Exp), reads=[bGT], writes=[bEGL])

    def core_b(self, l, defer=False):
        S, scr, din = self.S, self.scr, self.din
        last = (l == self.nlayers - 1)
        if not defer:
            self.es = ExitStack()
        KBT = self.R1[:, 0:2 * NTOK].rearrange("p (g t) -> p g t", g=2)
        VBR = self.R1[:, 2 * NTOK:2 * NTOK + NT * 130].rearrange("p (t g c) -> p t g c", g=2, c=65)
        bKBT, bVBR = Buf("KBT"), Buf("VBR")
        S.dma("sp", KBT, scr["KB_FM"].rearrange("g r t -> r g t"), reads=self.dball("KB_FM"), writes=[bKBT])
        S.dma("sp", VBR, scr["VB_TM"].rearrange("(t p) g c -> p t g c", p=128), reads=self.dball("VB_TM"), writes=[bVBR])
        if l == 0:
            bst, bbst = self.tsb("bmst", [128, 2, 512])
            S.dma("sp", bst[:], din["k_bmask"].rearrange("a p n -> p a n"), writes=[bbst])
            S.op("pool", lambda e: e.tensor_copy(out=self.bmask[:], in_=bst[:]), reads=[bbst], writes=[self.b_bmask])
        qTr = self.trot("b_qT", [128, 4, 128], BF16, 2)
        pTr = self.trot("b_pT", [128, 5, 512], BF16, 2)
        zbr = self.trot("b_zb", [128, 512], BF16, 2)
        ybr = self.trot("b_yb", [128, 512], BF16, 2)
        obr = self.trot("b_ob", [128, 256], F32, 2)
        str_ = self.trot("b_st", [128, 16], F32, 6)
        fmo = self.trot("b_fmo", [128, 512], BF16, 2)
        qtiles = list(range(2, NT)) if last else list(range(NT))
        def b_gen():
            for qt in qtiles:
                zb, bzb = zbr.next()
                S.dma("sp", zb[:], scr["ZB"][qt * 128:(qt + 1) * 128, :], reads=[self.db("ZB", qt)], writes=[bzb])
                yb, byb = ybr.next()
                for g in range(2):
                    qT, bqT = qTr.next()
                    S.dma("sp", qT[:], scr["QB_FM"][g * 4:(g + 1) * 4].rearrange("h r t -> r h t")[:, :, qt * 128:(qt + 1) * 128],
                          reads=[self.db("QB_FM", qt)], writes=[bqT])
                    keys = [(0, None), (1, None)]
                    if qt >= 2:
                        if qt - 1 >= 2:
                            keys.append((qt - 1, 0))
                        keys.append((qt, None))
                        if qt + 1 < NT:
                            keys.append((qt + 1, 1))
                    pT, bpT = pTr.next()
                    for idx, (kt, m) in enumerate(keys):
                        ps, bps = S.ps()
                        S.op("pe", lambda e: e.matmul(out=ps[:], lhsT=KBT[:, g, kt * 128:(kt + 1) * 128], rhs=qT[:].rearrange("p h t -> p (h t)"),
                                                      start=True, stop=True), reads=[bKBT, bqT], writes=[bps])
                        S.op("act", lambda e: e.activation(out=pT[:, idx, :], in_=ps[:], func=AF.Exp, scale=0.125), reads=[bps], writes=[bpT.sub(idx)])
                        if m is not None:
                            S.op("pool", lambda e: e.tensor_tensor(out=pT[:, idx, :], in0=pT[:, idx, :], in1=self.bmask[:, m, :], op=ALU.mult),
                                 reads=[bpT.sub(idx), self.b_bmask], writes=[bpT.sub(idx)])
                    po, bpo = S.ps()
                    for h in range(4):
                        for idx, (kt, m) in enumerate(keys):
                            S.op("pe", lambda e: e.matmul(out=po[:, h * 65:(h + 1) * 65], lhsT=pT[:, idx, h * 128:(h + 1) * 128], rhs=VBR[:, kt, g, :],
                                                          start=(idx == 0), stop=(idx == len(keys) - 1)), reads=[bpT.sub(idx), bVBR], writes=[bpo])
                    st, bst_ = str_.next()
                    pov = po[:, 0:260].rearrange("p (h c) -> p h c", c=65)
                    S.op("act", lambda e: e.activation(out=st[:, 0:4], in_=self.SH2[:, qt, g * 4:(g + 1) * 4], func=AF.Exp), reads=[self.b_SH2], writes=[bst_])
                    S.op("dve", lambda e: e.tensor_tensor(out=st[:, 4:8], in0=pov[:, :, 64], in1=st[:, 0:4], op=ALU.add), reads=[bpo, bst_], writes=[bst_])
                    S.op("dve", lambda e: e.reciprocal(out=st[:, 8:12], in_=st[:, 4:8]), reads=[bst_], writes=[bst_])
                    ob, bob = obr.next()
                    S.op("dve", lambda e: e.tensor_tensor(out=ob[:].rearrange("p (h c) -> p h c", c=64), in0=pov[:, :, 0:64],
                                                          in1=st[:, 8:12].unsqueeze(2).to_broadcast([128, 4, 64]), op=ALU.mult),
                         reads=[bpo, bst_], writes=[bob])
                    S.op("pool", lambda e: e.tensor_tensor(out=yb[:, g * 256:(g + 1) * 256], in0=ob[:], in1=zb[:, g * 256:(g + 1) * 256], op=ALU.mult),
                         reads=[bob, bzb], writes=[byb.sub(g)])
                    yield
                self.y_out(1, qt, yb, byb, fmo)
                yield
        if defer:
            return b_gen()
        for _ in b_gen():
            pass
        S.barrier()
        self.es.close()

    def core_bc(self, l):
        self.es = ExitStack()
        gb = self.core_b(l, defer=True)
        gc = self.core_c(l, defer=True)
        self.run_pair(gb, gc)
        self.S.barrier()
        self.es.close()

    def y_out(self, br, tt, y, by, fmo, pool=None):
        S = self.S
        f, bf = fmo.next()
        self.transpose_out(lambda i: (y[:, i * 128:(i + 1) * 128], by), 4, 128, f[:], [bf], "y", pool=pool)
        S.dma("act", self.scr["Y_FM"][br].rearrange("(k p) t -> p k t", p=128)[:, :, tt * 128:(tt + 1) * 128],
              f[:].rearrange("p (k t) -> p k t", k=4), reads=[bf], writes=[self.db("Y_FM%d" % br, tt)])

    def core_c(self, l, defer=False):
        S, scr = self.S, self.scr
        last = (l == self.nlayers - 1)
        if not defer:
            self.es = ExitStack()
        one = self.epsb[:, 3:4]
        cd, bcd = self.par["c_decay"]
        c8 = self.trot("c_c8", [128, 8], F32, 6)
        t1, b1 = c8.next()
        t2, b2 = c8.next()
        LG, bLG = c8.next()
        S.op("act", lambda e: e.activation(out=t1[:], in_=cd[:], func=AF.Abs), reads=[bcd], writes=[b1])
        S.op("act", lambda e: e.activation(out=t1[:], in_=t1[:], func=AF.Exp, scale=-1.0), reads=[b1], writes=[b1])
        S.op("act", lambda e: e.activation(out=t1[:], in_=t1[:], func=AF.Ln, bias=one), reads=[b1, self.b_epsb], writes=[b1])
        S.op("dve", lambda e: e.tensor_scalar(out=t2[:], in0=cd[:], scalar1=-1.0, scalar2=0.0, op0=ALU.mult, op1=ALU.max), reads=[bcd], writes=[b2])
        S.op("dve", lambda e: e.scalar_tensor_tensor(out=LG[:], in0=t2[:], scalar=-1.0, in1=t1[:], op0=ALU.mult, op1=ALU.subtract),
             reads=[b1, b2], writes=[bLG])
        GAMC, bGAMC = c8.next()
        S.op("act", lambda e: e.activation(out=GAMC[:], in_=LG[:], func=AF.Exp, scale=float(CH)), reads=[bLG], writes=[bGAMC])
        KDEC, bKDEC = c8.next()
        S.op("dve", lambda e: e.tensor_tensor(out=KDEC[:], in0=LG[:], in1=self.cj[:], op=ALU.mult), reads=[bLG, self.b_cj], writes=[bKDEC])
        S.op("act", lambda e: e.activation(out=KDEC[:], in_=KDEC[:], func=AF.Exp), reads=[bKDEC], writes=[bKDEC])
        DM, bDM = self.tsb("c_DM", [128, 512])
        QDF, bQDF = self.tsb("c_QDF", [128, 512], BF16)
        QDB, bQDB = self.tsb("c_QDB", [128, 512], BF16)
        tm = self.trot("c_tm", [128, 128], F32, 2)
        for h in range(4):
            ta, bta = tm.next()
            tb, btb = tm.next()
            S.op("act", lambda e: e.activation(out=ta[:], in_=self.cm[:, 0, :], func=AF.Exp, scale=LG[:, h:h + 1]), reads=[self.b_cm, bLG], writes=[bta])
            S.op("dve", lambda e: e.tensor_tensor(out=ta[:], in0=ta[:], in1=self.cm[:, 2, :], op=ALU.mult), reads=[bta, self.b_cm], writes=[bta])
            S.op("act", lambda e: e.activation(out=tb[:], in_=self.cm[:, 1, :], func=AF.Exp, scale=LG[:, 4 + h:5 + h]), reads=[self.b_cm, bLG], writes=[btb])
            S.op("dve", lambda e: e.tensor_tensor(out=tb[:], in0=tb[:], in1=self.cm[:, 3, :], op=ALU.mult), reads=[btb, self.b_cm], writes=[btb])
            S.op("dve", lambda e: e.tensor_tensor(out=ta[:], in0=ta[:], in1=tb[:], op=ALU.add), reads=[bta, btb], writes=[bta])
            S.op("dve", lambda e: e.scalar_tensor_tensor(out=DM[:, h * 128:(h + 1) * 128], in0=self.ident_f[:], scalar=2.0, in1=ta[:], op0=ALU.mult, op1=ALU.add),
                 reads=[bta, self.b_ident_f], writes=[bDM])
            S.op("act", lambda e: e.activation(out=QDF[:, h * 128:(h + 1) * 128], in_=self.cm[:, 4, :], func=AF.Exp, scale=LG[:, h:h + 1]),
                 reads=[self.b_cm, bLG], writes=[bQDF])
            S.op("act", lambda e: e.activation(out=QDB[:, h * 128:(h + 1) * 128], in_=self.cm[:, 5, :], func=AF.Exp, scale=LG[:, 4 + h:5 + h]),
                 reads=[self.b_cm, bLG], writes=[bQDB])
        kTMr = [self.trot("c_kTM%d" % d, [128, 512], BF16, 2) for d in range(2)]
        vr = [self.trot("c_v%d" % d, [128, 512], BF16, 2) for d in range(3)]
        kdr = [self.trot("c_kd%d" % d, [128, 512], BF16, 2) for d in range(2)]
        sbfr = [self.trot("c_sbf%d" % d, [128, 512], BF16, 2) for d in range(2)]
        S32 = [self.tsb("c_S32_%d" % d, [128, 512]) for d in range(2)]
        SCN = ["SCF", "SCB"]

        def state_update(d, cc, kTM, bkTM, v, bv):
            kd, bkd = kdr[d].next()
            for h in range(4):
                S.op("act", lambda e: e.activation(out=kd[:, h * 128:(h + 1) * 128], in_=kTM[:, h * 128:(h + 1) * 128], func=AF.Copy,
                                                   scale=KDEC[:, d * 4 + h:d * 4 + h + 1]),
                     reads=[bkTM, bKDEC], writes=[bkd.sub(h)])
            ps, bps = S.ps()
            for h in range(4):
                S.op("pe", lambda e: e.matmul(out=ps[:, h * 128:(h + 1) * 128], lhsT=kd[:, h * 128:(h + 1) * 128], rhs=v[:, h * 128:(h + 1) * 128],
                                              start=True, stop=True), reads=[bkd, bv], writes=[bps])
            s32, bs32 = S32[d]
            for h in range(4):
                S.op("dve", lambda e: e.scalar_tensor_tensor(out=s32[:, h * 128:(h + 1) * 128], in0=s32[:, h * 128:(h + 1) * 128],
                                                             scalar=GAMC[:, d * 4 + h:d * 4 + h + 1], in1=ps[:, h * 128:(h + 1) * 128],
                                                             op0=ALU.mult, op1=ALU.add),
                     reads=[bs32.sub(h), bGAMC, bps], writes=[bs32.sub(h)])

        def state_pass(d):
            S.op("pool", lambda e: e.memset(S32[d][0][:], 0.0), writes=[S32[d][1]])
            order = list(range(NT)) if d == 0 else [1, 0] + list(range(NT - 1, 1, -1))
            for cc in order:
                sbf, bsbf = sbfr[d].next()
                S.op("act", lambda e: e.copy(out=sbf[:], in_=S32[d][0][:]), reads=[S32[d][1]], writes=[bsbf])
                S.dma("act", scr[SCN[d]][cc], sbf[:], reads=[bsbf], writes=[self.db(SCN[d], cc)])
                kTM, bkTM = kTMr[d].next()
                v, bv = vr[d].next()
                S.dma("sp", kTM[:], scr["KC_TM"][cc * 128:(cc + 1) * 128, :], reads=[self.db("KC_TM", cc)], writes=[bkTM])
                S.dma("sp", v[:], scr["VC_TM"][cc * 128:(cc + 1) * 128, :], reads=[self.db("VC_TM", cc)], writes=[bv])
                yield
                state_update(d, cc, kTM, bkTM, v, bv)
                yield

        qTr = self.trot("c_qT", [128, 512], BF16, 2)
        kTr = self.trot("c_kT", [128, 512], BF16, 2)
        sbr = self.trot("c_sb", [128, 512], BF16, 2)
        sfr = self.trot("c_sf", [128, 512], BF16, 2)
        zcr = self.trot("c_zc", [128, 512], BF16, 2)
        qkr = self.trot("c_qk", [128, 512], BF16, 2)
        qdr = self.trot("c_qd", [128, 512], BF16, 2)
        ofr = self.trot("c_of", [128, 512], F32, 2)
        t5r = self.trot("c_t5", [128, 512], F32, 1)
        ycr = self.trot("c_yc", [128, 512], BF16, 2)
        stc = self.trot("c_st", [128, 24], F32, 3)
        fmo = self.trot("c_fmo", [128, 512], BF16, 2)
        cnw, bcnw = self.par["c_norm_w"]
        eps = self.epsb[:, 0:1]
        def out_pass():
            for cc in range(NT):
                need_out = (cc >= 2) or (not last)
                if need_out:
                    v, bv = vr[2].next()
                    S.dma("sp", v[:], scr["VC_TM"][cc * 128:(cc + 1) * 128, :], reads=[self.db("VC_TM", cc)], writes=[bv])
                    qT, bqT = qTr.next()
                    kT, bkT = kTr.next()
                    sb, bsb = sbr.next()
                    zc, bzc = zcr.next()
                    S.dma("sp", qT[:].rearrange("p (h t) -> p h t", h=4), scr["QC_FM"].rearrange("h p t -> p h t")[:, :, cc * 128:(cc + 1) * 128],
                          reads=[self.db("QC_FM", cc)], writes=[bqT])
                    S.dma("sp", kT[:].rearrange("p (h t) -> p h t", h=4), scr["KC_FM"].rearrange("h p t -> p h t")[:, :, cc * 128:(cc + 1) * 128],
                          reads=[self.db("KC_FM", cc)], writes=[bkT])
                    S.dma("sp", sb[:], scr["SCB"][cc], reads=[self.db("SCB", cc)], writes=[bsb])
                    S.dma("sp", zc[:], scr["ZC"][cc * 128:(cc + 1) * 128, :], reads=[self.db("ZC", cc)], writes=[bzc])
                    sbf, bsbf = sfr.next()
                    S.dma("sp", sbf[:], scr["SCF"][cc], reads=[self.db("SCF", cc)], writes=[bsbf])
                    ps1, bps1 = S.ps()
                    for h in range(4):
                        S.op("pe", lambda e: e.matmul(out=ps1[:, h * 128:(h + 1) * 128], lhsT=kT[:, h * 128:(h + 1) * 128], rhs=qT[:, h * 128:(h + 1) * 128],
                                                      start=True, stop=True), reads=[bkT, bqT], writes=[bps1])
                    qk, bqk = qkr.next()
                    S.op("dve", lambda e: e.tensor_tensor(out=qk[:], in0=ps1[:], in1=DM[:], op=ALU.mult), reads=[bps1, bDM], writes=[bqk])
                    qdf, bqdf = qdr.next()
                    qdb, bqdb = qdr.next()
                    S.op("pool", lambda e: e.tensor_tensor(out=qdf[:], in0=qT[:], in1=QDF[:], op=ALU.mult), reads=[bqT, bQDF], writes=[bqdf])
                    S.op("pool", lambda e: e.tensor_tensor(out=qdb[:], in0=qT[:], in1=QDB[:], op=ALU.mult), reads=[bqT, bQDB], writes=[bqdb])
                    po, bpo = S.ps()
                    for h in range(4):
                        sl = slice(h * 128, (h + 1) * 128)
                        S.op("pe", lambda e: e.matmul(out=po[:, sl], lhsT=qk[:, sl], rhs=v[:, sl], start=True, stop=False), reads=[bqk, bv], writes=[bpo])
                        S.op("pe", lambda e: e.matmul(out=po[:, sl], lhsT=qdf[:, sl], rhs=sbf[:, sl], start=False, stop=False), reads=[bqdf, bsbf], writes=[bpo])
                        S.op("pe", lambda e: e.matmul(out=po[:, sl], lhsT=qdb[:, sl], rhs=sb[:, sl], start=False, stop=True), reads=[bqdb, bsb], writes=[bpo])
                    of, bof = ofr.next()
                    t5, bt5 = t5r.next()
                    st, bst = stc.next()
                    S.op("act", lambda e: e.copy(out=of[:], in_=po[:]), reads=[bpo], writes=[bof])
                    S.op("act", lambda e: e.activation(out=t5[:], in_=po[:], func=AF.Square), reads=[bpo], writes=[bt5])
                    S.op("dve", lambda e: e.tensor_reduce(out=st[:, 0:4], in_=of[:].rearrange("p (h c) -> p h c", h=4), axis=AX.X, op=ALU.add), reads=[bof], writes=[bst])
                    S.op("dve", lambda e: e.tensor_reduce(out=st[:, 4:8], in_=t5[:].rearrange("p (h c) -> p h c", h=4), axis=AX.X, op=ALU.add), reads=[bt5], writes=[bst])
                    S.op("dve", lambda e: e.tensor_scalar(out=st[:, 8:12], in0=st[:, 0:4], scalar1=1.0 / 128, scalar2=None, op0=ALU.mult), reads=[bst], writes=[bst])
                    S.op("dve", lambda e: e.tensor_tensor(out=st[:, 12:16], in0=st[:, 8:12], in1=st[:, 8:12], op=ALU.mult), reads=[bst], writes=[bst])
                    S.op("dve", lambda e: e.scalar_tensor_tensor(out=st[:, 16:20], in0=st[:, 4:8], scalar=1.0 / 128, in1=st[:, 12:16], op0=ALU.mult, op1=ALU.subtract),
                         reads=[bst], writes=[bst])
                    S.op("act", lambda e: e.activation(out=st[:, 20:24], in_=st[:, 16:20], func=AF.Ln, bias=eps), reads=[bst, self.b_epsb], writes=[bst])
                    S.op("act", lambda e: e.activation(out=st[:, 20:24], in_=st[:, 20:24], func=AF.Exp, scale=-0.5), reads=[bst], writes=[bst])
                    ofv = of[:].rearrange("p (h c) -> p h c", h=4)
                    S.op("dve", lambda e: e.tensor_tensor(out=ofv, in0=ofv, in1=st[:, 8:12].unsqueeze(2).to_broadcast([128, 4, 128]), op=ALU.subtract),
                         reads=[bof, bst], writes=[bof])
                    S.op("dve", lambda e: e.tensor_tensor(out=ofv, in0=ofv, in1=st[:, 20:24].unsqueeze(2).to_broadcast([128, 4, 128]), op=ALU.mult),
                         reads=[bof, bst], writes=[bof])
                    S.op("pool", lambda e: e.tensor_tensor(out=of[:], in0=of[:], in1=cnw[:], op=ALU.mult), reads=[bof, bcnw], writes=[bof])
                    yc, byc = ycr.next()
                    S.op("pool", lambda e: e.tensor_tensor(out=yc[:], in0=of[:], in1=zc[:], op=ALU.mult), reads=[bof, bzc], writes=[byc])
                    self.y_out(2, cc, yc, byc, fmo)
                yield

        def c_gen():
            gens = [state_pass(0), state_pass(1)]
            while gens:
                for g_ in list(gens):
                    try:
                        next(g_)
                        yield
                    except StopIteration:
                        gens.remove(g_)
            yield from out_pass()
        if defer:
            return c_gen()
        for _ in c_gen():
            pass
        S.barrier()
        self.es.close()

    def core_a(self, l):
        S, scr = self.S, self.scr
        last = (l == self.nlayers - 1)
        self.es = ExitStack()
        A = self.asc
        import os
        KPRE = int(os.environ.get("KPRE", "2"))
        r1_next = [0]

        def mkrot(name, k, use_r1=True):
            items = []
            for i in range(k):
                if use_r1 and r1_next[0] < 68:
                    j = r1_next[0]
                    r1_next[0] += 1
                    items.append((self.R1[:, j * 512:(j + 1) * 512], Buf("%s%d" % (name, i))))
                else:
                    t = self._talloc("a_" + name, [128, 512], BF16)
                    items.append((t[:], Buf("%s%d" % (name, i))))
            r = Rot.__new__(Rot)
            r.items = items
            r.i = 0
            return r

        def R(n, dt=BF16, k=2):
            r = self.trot("a_" + n, [128, 512], dt, k)
            r.items = [(t[:], b) for t, b in r.items]
            return r
        ofr, t5r = R("of", F32, 3), R("t5", F32, 2)
        zar, yar, fmo = R("za"), R("ya"), R("fmo")
        sta = self.trot("a_st", [128, 16], F32, 4)
        nmask, bnmask = self.tsb("a_nmask", [128, 14, 128], BF16)
        nmst, bnmst = self.wst.next()
        nmv = nmst[:].rearrange("p k n -> p (k n)")[:, 0:14 * 128].rearrange("p (a n) -> p a n", a=14)
        S.dma("sp", nmv, self.din["k_nm"].rearrange("a p n -> p a n"), writes=[bnmst])
        S.op("pool", lambda e: e.tensor_copy(out=nmask[:], in_=nmv), reads=[bnmst], writes=[bnmask])
        anw, banw = self.par["a_norm_w"]
        eps = self.epsb[:, 0:1]
        H = [slice(h * 128, (h + 1) * 128) for h in range(4)]
        v4 = lambda t: t[:].rearrange("p (h c) -> p h c", h=4)
        orders = [list(range(NT)), [1, 0] + list(range(NT - 1, 1, -1))]
        oa_written = set()
        from collections import deque
        free_banks = deque(range(8))

        def acq():
            while not free_banks:
                yield
            bk = free_banks.popleft()
            ps, bps = S.psum[bk]
            return ps, bps, bk

        def rel(bk):
            free_banks.append(bk)

        def mm4(lhs, blhs, rhs, brhs):
            ps, bps, bk = yield from acq()
            for h in range(4):
                S.op("pe", lambda e: e.matmul(out=ps[:, H[h]], lhsT=lhs[:, H[h]], rhs=rhs[:, H[h]], start=True, stop=True), reads=[blhs, brhs], writes=[bps])
            return ps, bps, bk

        def tr4(src, bsrc):
            ps, bps, bk = yield from acq()
            psb = ps[:].bitcast(BF16)
            for h in range(4):
                S.op("pe", lambda e: e.transpose(out=psb[:, H[h]], in_=src[:, H[h]], identity=self.ident_b[:]), reads=[bsrc, self.b_ident_b], writes=[bps])
            return psb, bps, bk

        class DirBufs:
            pass
        DB = []
        for d in range(2):
            o = DirBufs()
            for n in ("kT", "kTM", "vTM", "qT", "qk", "qd", "kt", "Pf"):
                setattr(o, n, mkrot("%s_%d" % (n, d), KPRE + 1))
            o.tsets = []
            for ts in range(KPRE):
                tsd = {n: mkrot("%s_%d_%d" % (n, d, ts), 1).items[0] for n in ("eg", "Ma", "Ml", "Pa", "Pb", "W1", "X", "MlmA", "MlmB")}
                for n in ("F0", "F1", "F2"):
                    tsd[n] = (self._talloc("a_%s_%d_%d" % (n, d, ts), [128, 512], F32)[:], Buf("%s_%d_%d" % (n, d, ts)))
                o.tsets.append(tsd)
            for n in ("Y", "vn", "sbf"):
                setattr(o, n, R("%s_%d" % (n, d)))
            o.S32 = self.tsb("a_S32_%d" % d, [128, 512])
            DB.append(o)

        def prep(d, cc, out, ts):
            B = DB[d]
            TS = B.tsets[ts]
            need_out = (cc >= 2) or (not last)
            out["need_out"] = need_out
            col = lambda name, h: A[name][0][:, cc, d * 4 + h:d * 4 + h + 1]
            tok = slice(cc * 128, (cc + 1) * 128)
            m_incl = self.amask[:, 2 * d, :]
            m_strict = self.amask[:, 2 * d + 1, :]
            nmb = lambda lev: nmask[:, d * 7 + lev, :].unsqueeze(1).to_broadcast([128, 4, 128])
            kT, bkT = B.kT.next()
            kTM, bkTM = B.kTM.next()
            vTM, bvTM = B.vTM.next()
            S.dma("sp", kT.rearrange("p (h t) -> p h t", h=4), scr["KA_FM"].rearrange("h p t -> p h t")[:, :, tok], reads=[self.db("KA_FM", cc)], writes=[bkT])
            S.dma("sp", kTM.rearrange("p (h c) -> p h c", h=4), scr["KA_TM"].rearrange("h t c -> t h c")[tok, :, :], reads=[self.db("KA_TM", cc)], writes=[bkTM])
            S.dma("sp", vTM.rearrange("p (h c) -> p h c", h=4), scr["VA_TM"].rearrange("h t c -> t h c")[tok, :, :], reads=[self.db("VA_TM", cc)], writes=[bvTM])
            out.update(kT=(kT, bkT), vTM=(vTM, bvTM))
            if need_out:
                qT, bqT = B.qT.next()
                S.dma("sp", qT.rearrange("p (h t) -> p h t", h=4), scr["QA_FM"].rearrange("h p t -> p h t")[:, :, tok], reads=[self.db("QA_FM", cc)], writes=[bqT])
            yield
            dg, bdg = TS["F0"]
            for h in range(4):
                S.op("dve", lambda e: e.tensor_scalar(out=dg[:, H[h]], in0=self.ident_f[:], scalar1=col("GC", h), scalar2=None, op0=ALU.mult),
                     reads=[self.b_ident_f, A["GC"][1]], writes=[bdg.sub(h)])
            p3, bp3, k3 = yield from acq()
            S.op("pe", lambda e: e.matmul(out=p3[:], lhsT=self.ones_f[:], rhs=dg[:], start=True, stop=True), reads=[self.b_ones_f, bdg], writes=[bp3])
            yield
            Dm2, bDm2 = TS["F1"]
            for h in range(4):
                S.op("dve", lambda e: e.scalar_tensor_tensor(out=Dm2[:, H[h]], in0=p3[:, H[h]], scalar=col("LNBMG", h), in1=m_strict, op0=ALU.add, op1=ALU.add),
                     reads=[bp3, A["LNBMG"][1], self.b_amask], writes=[bDm2.sub(h)])
            if need_out:
                Dm, bDm = TS["F2"]
                for h in range(4):
                    S.op("dve", lambda e: e.scalar_tensor_tensor(out=Dm[:, H[h]], in0=p3[:, H[h]], scalar=col("NEGG", h), in1=m_incl, op0=ALU.add, op1=ALU.add),
                         reads=[bp3, A["NEGG"][1], self.b_amask], writes=[bDm.sub(h)])
                eg, beg = TS["eg"]
                S.op("act", lambda e: e.activation(out=eg, in_=p3[:], func=AF.Exp), reads=[bp3, bDm, bDm2], writes=[beg])
            rel(k3)
            yield
            decb, bdecb = TS["F0"]
            S.op("act", lambda e: e.activation(out=decb[:], in_=Dm2[:], func=AF.Exp), reads=[bDm2], writes=[bdecb])
            p1, bp1, k1 = yield from mm4(kT, bkT, kT, bkT)
            yield
            Ma, bMa = TS["Ma"]
            S.op("dve", lambda e: e.tensor_tensor(out=Ma, in0=p1[:], in1=decb[:], op=ALU.mult), reads=[bp1, bdecb], writes=[bMa])
            rel(k1)
            yield
            psb, bps, kb = yield from tr4(Ma, bMa)
            nml = lambda lev: nmask[:, (1 - d) * 7 + lev, :].unsqueeze(1).to_broadcast([128, 4, 128])
            mlm = lambda lev: TS["MlmA" if lev % 2 else "MlmB"]
            S.op("dve", lambda e: e.tensor_tensor(out=v4(mlm(1)[0]), in0=psb[:, 0:512].rearrange("p (h c) -> p h c", h=4), in1=nml(1), op=ALU.mult),
                 reads=[bps, bnmask], writes=[mlm(1)[1]])
            Ml, bMl = TS["Ml"]
            S.op("act", lambda e: e.copy(out=Ml, in_=psb[:, 0:512]), reads=[bps, mlm(1)[1]], writes=[bMl])
            rel(kb)
            P, bP = TS["Pa"]
            S.op("pool", lambda e: e.tensor_tensor(out=v4(P), in0=v4(Ma), in1=nmb(0), op=ALU.mult), reads=[bMa, bnmask], writes=[bP])
            S.op("pool", lambda e: e.tensor_tensor(out=v4(P), in0=v4(P), in1=self.ident_b[:].unsqueeze(1).to_broadcast([128, 4, 128]), op=ALU.add),
                 reads=[bP, self.b_ident_b], writes=[bP])
            yield
            if need_out:
                dec, bdec = TS["F1"]
                S.op("act", lambda e: e.activation(out=dec[:], in_=Dm[:], func=AF.Exp), reads=[bDm], writes=[bdec])
                p2, bp2, k2 = yield from mm4(kT, bkT, qT, bqT)
                yield
                qk, bqk = B.qk.next()
                S.op("dve", lambda e: e.tensor_tensor(out=qk, in0=p2[:], in1=dec[:], op=ALU.mult), reads=[bp2, bdec], writes=[bqk])
                rel(k2)
                qd, bqd = B.qd.next()
                S.op("pool", lambda e: e.tensor_tensor(out=qd, in0=qT, in1=eg, op=ALU.mult), reads=[bqT, beg], writes=[bqd])
                out.update(qk=(qk, bqk), qd=(qd, bqd))
                yield
            kt, bkt = B.kt.next()
            for h in range(4):
                S.op("act", lambda e: e.activation(out=kt[:, H[h]], in_=kTM[:, H[h]], func=AF.Copy, scale=col("ETAIL", h)),
                     reads=[bkTM, A["ETAIL"][1]], writes=[bkt.sub(h)])
            out.update(kt=(kt, bkt))
            yield
            for lev in range(1, 7):
                cur, bcur = mlm(lev)
                psw, bpsw, kw = yield from mm4(cur, bcur, P, bP)
                psb, bps, kb = yield from tr4(P, bP)
                if lev < 6:
                    nxt_, bnxt_ = mlm(lev + 1)
                    S.op("pool", lambda e: e.tensor_tensor(out=v4(nxt_), in0=v4(Ml), in1=nml(lev + 1), op=ALU.mult), reads=[bMl, bnmask], writes=[bnxt_])
                yield
                W1, bW1 = TS["W1"]
                S.op("act", lambda e: e.copy(out=W1, in_=psw[:]), reads=[bpsw], writes=[bW1])
                rel(kw)
                X, bX = TS["X"]
                S.op("dve", lambda e: e.tensor_copy(out=X, in_=psb[:, 0:512]), reads=[bps], writes=[bX])
                rel(kb)
                yield
                ps2, bps2, k2 = yield from mm4(X, bX, W1, bW1)
                yield
                Pn, bPn = (B.Pf.next() if lev == 6 else TS["Pb" if lev % 2 == 1 else "Pa"])
                S.op("dve", lambda e: e.tensor_tensor(out=Pn, in0=ps2[:], in1=P, op=ALU.add), reads=[bps2, bP], writes=[bPn])
                rel(k2)
                P, bP = Pn, bPn
                yield
            out.update(P=(P, bP))

        def scan(d, cc, ops, st):
            B = DB[d]
            need_out = ops["need_out"]
            col = lambda name, h: A[name][0][:, cc, d * 4 + h:d * 4 + h + 1]
            tok = slice(cc * 128, (cc + 1) * 128)
            kT, bkT = ops["kT"]
            vTM, bvTM = ops["vTM"]
            kt, bkt = ops["kt"]
            P, bP = ops["P"]
            sbf, bsbf = st["sbf"]
            s32, bs32 = B.S32
            px, bpx, kx = yield from mm4(kT, bkT, sbf, bsbf)
            yield
            Y, bY = B.Y.next()
            for h in range(4):
                S.op("dve", lambda e: e.scalar_tensor_tensor(out=Y[:, H[h]], in0=px[:, H[h]], scalar=col("NEGEG", h), in1=vTM[:, H[h]], op0=ALU.mult, op1=ALU.add),
                     reads=[bpx, A["NEGEG"][1], bvTM], writes=[bY.sub(h)])
            rel(kx)
            yield
            pz, bpz, kz = yield from mm4(P, bP, Y, bY)
            yield
            vn, bvn = B.vn.next()
            for h in range(4):
                S.op("act", lambda e: e.activation(out=vn[:, H[h]], in_=pz[:, H[h]], func=AF.Copy, scale=col("BETA", h)), reads=[bpz, A["BETA"][1]], writes=[bvn.sub(h)])
            rel(kz)
            yield
            pS, bpS, kS = yield from mm4(kt, bkt, vn, bvn)
            if need_out:
                qk, bqk = ops["qk"]
                qd, bqd = ops["qd"]
                po, bpo, ko = yield from acq()
                for h in range(4):
                    S.op("pe", lambda e: e.matmul(out=po[:, H[h]], lhsT=qd[:, H[h]], rhs=sbf[:, H[h]], start=True, stop=False), reads=[bqd, bsbf], writes=[bpo])
                    S.op("pe", lambda e: e.matmul(out=po[:, H[h]], lhsT=qk[:, H[h]], rhs=vn[:, H[h]], start=False, stop=True), reads=[bqk, bvn], writes=[bpo])
            yield
            for h in range(4):
                S.op("dve", lambda e: e.scalar_tensor_tensor(out=s32[:, H[h]], in0=s32[:, H[h]], scalar=col("EGL", h), in1=pS[:, H[h]], op0=ALU.mult, op1=ALU.add),
                     reads=[bs32.sub(h), A["EGL"][1], bpS], writes=[bs32.sub(h)])
            rel(kS)
            sbf2, bsbf2 = B.sbf.next()
            S.op("act", lambda e: e.copy(out=sbf2, in_=s32[:]), reads=[bs32], writes=[bsbf2])
            st["sbf"] = (sbf2, bsbf2)
            yield
            if need_out:
                first = cc not in oa_written
                oa_written.add(cc)
                of, bof = ofr.next()
                if first:
                    S.op("act", lambda e: e.copy(out=of[:], in_=po[:]), reads=[bpo], writes=[bof])
                    rel(ko)
                    S.dma("act", scr["OA"][tok, :], of[:], reads=[bof], writes=[self.db("OA", cc)])
                    yield
                else:
                    S.dma("sp", of[:], scr["OA"][tok, :], reads=[self.db("OA", cc)], writes=[bof])
                    za, bza = zar.next()
                    S.dma("sp", za, scr["ZA"][tok, :], reads=[self.db("ZA", cc)], writes=[bza])
                    yield
                    S.op("dve", lambda e: e.tensor_tensor(out=of[:], in0=po[:], in1=of[:], op=ALU.add), reads=[bpo, bof], writes=[bof])
                    rel(ko)
                    t5, bt5 = t5r.next()
                    st_, bst = sta.next()
                    S.op("act", lambda e: e.activation(out=t5[:], in_=of[:], func=AF.Square), reads=[bof], writes=[bt5])
                    yield
                    S.op("dve", lambda e: e.tensor_reduce(out=st_[:, 0:4], in_=t5[:].rearrange("p (h c) -> p h c", h=4), axis=AX.X, op=ALU.add), reads=[bt5], writes=[bst])
                    S.op("act", lambda e: e.activation(out=st_[:, 4:8], in_=st_[:, 0:4], func=AF.Ln, scale=1.0 / 128, bias=eps), reads=[bst, self.b_epsb], writes=[bst])
                    S.op("act", lambda e: e.activation(out=st_[:, 8:12], in_=st_[:, 4:8], func=AF.Exp, scale=-0.5), reads=[bst], writes=[bst])
                    yield
                    ofv = of[:].rearrange("p (h c) -> p h c", h=4)
                    S.op("dve", lambda e: e.tensor_tensor(out=ofv, in0=ofv, in1=st_[:, 8:12].unsqueeze(2).to_broadcast([128, 4, 128]), op=ALU.mult), reads=[bof, bst], writes=[bof])
                    S.op("pool", lambda e: e.tensor_tensor(out=ofv, in0=ofv, in1=anw[:].unsqueeze(1).to_broadcast([128, 4, 128]), op=ALU.mult), reads=[bof, banw], writes=[bof])
                    yield
                    ya, bya = yar.next()
                    S.op("pool", lambda e: e.tensor_tensor(out=ya, in0=of[:], in1=za, op=ALU.mult), reads=[bof, bza], writes=[bya])
                    f, bf = fmo.next()
                    psb, bps, kb = yield from tr4(ya, bya)
                    S.op("act", lambda e: e.copy(out=f, in_=psb[:, 0:512]), reads=[bps], writes=[bf])
                    rel(kb)
                    S.dma("act", scr["Y_FM"][0].rearrange("(k p) t -> p k t", p=128)[:, :, tok], f.rearrange("p (k t) -> p k t", k=4),
                          reads=[bf], writes=[self.db("Y_FM0", cc)])
                    yield

        def chain(d):
            B = DB[d]
            s32, bs32 = B.S32
            S.op("pool", lambda e: e.memset(s32[:], 0.0), writes=[bs32])
            sbf, bsbf = B.sbf.next()
            S.op("pool", lambda e: e.memset(sbf, 0.0), writes=[bsbf])
            st = {"sbf": (sbf, bsbf)}
            order = orders[d]
            n = len(order)
            outs = [dict() for _ in range(n)]
            started = 0
            active = []
            done = set()

            def start_upto(j):
                nonlocal started
                while started <= min(j, n - 1):
                    active.append((started, prep(d, order[started], outs[started], started % KPRE)))
                    started += 1

            def step_preps():
                for item in list(active):
                    try:
                        next(item[1])
                    except StopIteration:
                        active.remove(item)
                        done.add(item[0])
            start_upto(0)
            while 0 not in done:
                step_preps()
                yield
            for i, cc in enumerate(order):
                start_upto(i + KPRE)
                sc = scan(d, cc, outs[i], st)
                sc_done = False
                while not sc_done or (i + 1 < n and (i + 1) not in done):
                    if not sc_done:
                        try:
                            next(sc)
                        except StopIteration:
                            sc_done = True
                    step_preps()
                    yield

        gens = [chain(0), chain(1)]
        while gens:
            for g_ in list(gens):
                try:
                    next(g_)
                except StopIteration:
                    gens.remove(g_)
        S.barrier()
        self.es.close()

    def phase5(self, l):
        S, scr, din = self.S, self.scr, self.din
        last = (l == self.nlayers - 1)
        self.es = ExitStack()
        wbr = self.R1[:, 14336:26624].rearrange("p (r n) -> p r n", n=1024)
        wo = self.R1[:, 26624:34816].rearrange("p (r n) -> p r n", n=1024)
        bwbr, bwo = Buf("wbr"), Buf("wo")

        def load_into(src_view, dst, bdst, nk):
            st, bst = self.wst.next()
            S.dma("sp", st[:, 0:nk, :], src_view, writes=[bst])
            S.op("pool", lambda e: e.tensor_copy(out=dst, in_=st[:, 0:nk, :]), reads=[bst], writes=[bdst])
        wbsrc = din["w_branch"][l].rearrange("b (k p) n -> p (b k) n", p=128)
        for half in range(2):
            for r0, nk in ((0, 8), (8, 4)):
                load_into(wbsrc[:, r0:r0 + nk, half * 512:(half + 1) * 512], wbr[:, r0:r0 + nk, half * 512:(half + 1) * 512], bwbr, nk)
        wosrc = din["w_out"][l].rearrange("(k p) n -> p k n", p=128)
        for half in range(2):
            load_into(wosrc[:, :, half * 512:(half + 1) * 512], wo[:, :, half * 512:(half + 1) * 512], bwo, 8)
        gate_bc, bgate = self.tsb("gate_bc", [128, 2, 1024])
        for s_ in range(2):
            if last and s_ == 1:
                continue
            self.bc_rows(lambda half: gate_bc[:, s_, half * 512:(half + 1) * 512], bgate, lambda kc: self.mod[:, 16 + kc, s_:s_ + 1], self.b_mod, 8)
        if last:
            fnw, bfnw = self.tsb("fnw_bc", [128, 1024])
            S.dma("sp", fnw[:], din["final_norm_w"].partition_broadcast(128), writes=[bfnw])
        yTr = self.trot("p5_yT", [128, 12, 512], BF16, 1)
        gmr = self.trot("p5_gm", [128, 512], BF16, 4)
        accr = self.trot("p5_acc", [128, 512], F32, 2)
        tmr = self.trot("p5_tm", [128, 512], F32, 2)
        mTr = self.trot("p5_mT", [128, 8, 512], BF16, 2)
        xtr = self.trot("p5_xt", [128, 1024], F32, 3)
        t1r = self.trot("p5_t1", [128, 1024], F32, 2)
        sqr, bsqr = self.tsb("p5_sq", [128, 1024])
        st5 = self.trot("p5_st", [128, 4], F32, 3)
        for (t0, n, tile0, ntile) in self.tok_groups():
            if last and t0 == 0:
                continue
            s_ = 1 if t0 == 0 else 0
            yT, byT = yTr.next()
            for br in range(3):
                S.dma("sp", yT[:, br * 4:(br + 1) * 4, 0:n], scr["Y_FM"][br].rearrange("(k p) t -> p k t", p=128)[:, :, t0:t0 + n],
                      reads=[self.db("Y_FM%d" % br, tile0 + i) for i in range(ntile)], writes=[byT.sub(br)])
            mT, bmT = mTr.next()
            pend = []

            def flush():
                while pend:
                    pend.pop(0)()
            acc_of = {}
            for dt in range(8):
                acc_of[dt] = accr.next()
                for br in range(3):
                    ct = br * 8 + dt
                    gm, bgm = gmr.next()
                    S.dma("sp", gm[:, 0:n], scr["GM_FM"][ct * 128:(ct + 1) * 128, t0:t0 + n],
                          reads=[self.db("GM_FM%d" % ct, tile0 + i) for i in range(ntile)], writes=[bgm])
                    ps, bps = S.ps()
                    for kc in range(4):
                        S.op("pe", lambda e: e.matmul(out=ps[:, 0:n], lhsT=wbr[:, br * 4 + kc, dt * 128:(dt + 1) * 128], rhs=yT[:, br * 4 + kc, 0:n],
                                                      start=(kc == 0), stop=(kc == 3)), reads=[bwbr, byT.sub(br)], writes=[bps])
                    flush()

                    def cons(dt=dt, br=br, ps=ps, bps=bps, gm=gm, bgm=bgm):
                        acc, bacc = acc_of[dt]
                        if br == 0:
                            S.op("dve", lambda e: e.tensor_tensor(out=acc[:, 0:n], in0=ps[:, 0:n], in1=gm[:, 0:n], op=ALU.mult), reads=[bps, bgm], writes=[bacc])
                        else:
                            tm, btm = tmr.next()
                            S.op("dve", lambda e: e.tensor_tensor(out=tm[:, 0:n], in0=ps[:, 0:n], in1=gm[:, 0:n], op=ALU.mult), reads=[bps, bgm], writes=[btm])
                            if br == 1:
                                S.op("pool", lambda e: e.tensor_tensor(out=acc[:, 0:n], in0=acc[:, 0:n], in1=tm[:, 0:n], op=ALU.add), reads=[bacc, btm], writes=[bacc])
                            else:
                                S.op("pool", lambda e: e.tensor_tensor(out=mT[:, dt, 0:n], in0=acc[:, 0:n], in1=tm[:, 0:n], op=ALU.add), reads=[bacc, btm], writes=[bmT.sub(dt)])
                    pend.append(cons)
            flush()
            for ti in range(ntile):
                tt = tile0 + ti
                xt, bxt = xtr.next()
                if tt < 2:
                    src = (din["ctx"] if l == 0 else scr["CTXS"])[tt * 128:(tt + 1) * 128, :]
                    rd = [] if l == 0 else [self.db("CTXS", tt)]
                else:
                    src = (din["x"] if l == 0 else scr["XS"])[(tt - 2) * 128:(tt - 1) * 128, :]
                    rd = [] if l == 0 else [self.db("XS", tt)]
                S.dma("sp", xt[:], src, reads=rd, writes=[bxt])
                pss = []
                for cg in range(2):
                    ps, bps = S.ps()
                    for kc in range(8):
                        S.op("pe", lambda e: e.matmul(out=ps[:], lhsT=mT[:, kc, ti * 128:(ti + 1) * 128], rhs=wo[:, kc, cg * 512:(cg + 1) * 512],
                                                      start=(kc == 0), stop=(kc == 7)), reads=[bmT, bwo], writes=[bps])
                    pss.append((ps, bps))
                flush()

                def cons2(tt=tt, xt=xt, bxt=bxt, pss=pss):
                    t1, bt1 = t1r.next()
                    for cg in range(2):
                        ps, bps = pss[cg]
                        S.op("dve", lambda e: e.tensor_tensor(out=t1[:, cg * 512:(cg + 1) * 512], in0=ps[:], in1=gate_bc[:, s_, cg * 512:(cg + 1) * 512], op=ALU.mult),
                             reads=[bps, bgate], writes=[bt1.sub(cg)])
                    S.op("pool", lambda e: e.tensor_tensor(out=t1[:], in0=t1[:], in1=xt[:], op=ALU.add), reads=[bt1, bxt], writes=[bt1])
                    if not last:
                        if tt < 2:
                            S.dma("act", scr["CTXS"][tt * 128:(tt + 1) * 128, :], t1[:], reads=[bt1], writes=[self.db("CTXS", tt)])
                        else:
                            S.dma("act", scr["XS"][(tt - 2) * 128:(tt - 1) * 128, :], t1[:], reads=[bt1], writes=[self.db("XS", tt)])
                    else:
                        st, bst = st5.next()
                        S.op("act", lambda e: e.activation(out=sqr[:], in_=t1[:], func=AF.Square, accum_out=st[:, 0:1]), reads=[bt1], writes=[bsqr, bst])
                        S.op("dve", lambda e: e.tensor_scalar(out=st[:, 1:2], in0=st[:, 0:1], scalar1=1.0 / D, scalar2=EPS, op0=ALU.mult, op1=ALU.add), reads=[bst], writes=[bst])
                        S.op("act", lambda e: e.activation(out=st[:, 2:3], in_=st[:, 1:2], func=AF.Ln), reads=[bst], writes=[bst])
                        S.op("act", lambda e: e.activation(out=st[:, 3:4], in_=st[:, 2:3], func=AF.Exp, scale=-0.5), reads=[bst], writes=[bst])
                        S.op("dve", lambda e: e.scalar_tensor_tensor(out=xt[:], in0=t1[:], scalar=st[:, 3:4], in1=fnw[:], op0=ALU.mult, op1=ALU.mult),
                             reads=[bt1, bst, bfnw, bxt], writes=[bxt])
                        S.dma("act", self.out[(tt - 2) * 128:(tt - 1) * 128, :], xt[:], reads=[bxt], writes=[self.db("OUT", tt)])
                pend.append(cons2)
            flush()
        S.barrier()
        self.es.close()

    def dump(self, name, ap, reads, shape, dtype=F32):
        o = self.nc.dram_tensor("dbg_" + name, shape, dtype, kind="ExternalOutput").ap()
        b = Buf("dbg_" + name)
        self.S.dma("sp", o, ap, reads=reads, writes=[b])
        self._dbgbufs.append(b)

    def dump_p2(self):
        self.dump("mod", self.mod[:], [self.b_mod], [128, 24, 2])
        self.dump("SCR", self.SCR[:], [self.b_SCR], [128, NT, 16])
        for n in self.asc:
            self.dump(n, self.asc[n][0][:], [self.asc[n][1]], [128, NT, 8])
        self.dump("SH2", self.SH2[:], [self.b_SH2], [128, NT, 8])
        self.dump("kmx", self.kmx[:], [self.b_kmx], [128, 4])
        self.dump("hT", self.hT, self.b_hT, [128, 8, NTOK], BF16)

    def program(self):
        S = self.S
        self._dbgbufs = []
        self.marks = []
        mark = lambda n: self.marks.append((n, {k: v.count for k, v in S.engs.items()}))
        for l in range(self.nlayers):
            mark("L%d start" % l)
            self.phase0(l)
            mark("L%d p0 done" % l)
            if self.stop == "p0":
                self.dump("mod", self.mod[:], [self.b_mod], [128, 24, 2])
                self.dump("Afm", self.Afm[:], [self.b_Afm], [128, 8, 2])
                self.dump("convw", self.convw[:], [self.b_convw], [128, 12, 5])
                self.dump("scol", self.scol[:], [self.b_scol], [128, 8, 2])
                break
            self.phase1(l)
            mark("L%d p1 done" % l)
            if self.stop == "p1":
                self.dump("hT", self.hT, self.b_hT, [128, 8, NTOK], BF16)
                break
            self.phase2(l)
            if self.stop is not None and self.stop.startswith("p2"):
                break
            mark("L%d p2 done" % l)
            if self.stop == "b":
                self.core_b(l)
                break
            self.core_bc(l)
            mark("L%d B done" % l)
            mark("L%d C done" % l)
            if self.stop == "c":
                break
            self.core_a(l)
            mark("L%d A done" % l)
            if self.stop == "a":
                break
            self.phase5(l)
            mark("L%d p5 done" % l)
            if self.stop == "p5":
                break
        S.barrier()
        return self.nc


def shard_inputs(inputs, b):
    m = {}
    for n in IN_SHAPES:
        a = np.asarray(inputs[n], dtype=np.float32)
        if n in ("x", "c", "ctx"):
            a = a[b]
        m[n] = np.ascontiguousarray(a)
    return m


_CACHE = {}


def kernel(**inputs):
    if "nc" not in _CACHE:
        _CACHE["nc"] = MK().program()
        _CACHE["consts"] = host_consts()
    nc = _CACHE["nc"]
    in_maps = []
    for b in range(8):
        m = shard_inputs(inputs, b)
        m.update(_CACHE["consts"])
        in_maps.append(m)
    res = run_bass_kernel_spmd(nc, in_maps, core_ids=list(range(8)))
    return np.stack([np.asarray(r["out"], dtype=np.float32) for r in res.results], axis=0)
```

```python
from contextlib import ExitStack
import numpy as np
import concourse.bass as bass
import concourse.mybir as mybir
from concourse.bass_utils import run_bass_kernel_spmd

F32 = mybir.dt.float32
BF16 = mybir.dt.bfloat16
AF = mybir.ActivationFunctionType
ALU = mybir.AluOpType
AX = mybir.AxisListType

T = 4096
LC = 256
D = 1024
NT = 34
NTOK = 4352
INW = 8464
CH = 128
EPS = 1e-6
NEG = -1.0e5
O_AQ, O_AK, O_AV, O_AZ, O_AB, O_BQ, O_BKV, O_BZ, O_CQ, O_CK, O_CV, O_CZ, O_MG = (
    0, 512, 1024, 1536, 2048, 2064, 2576, 2832, 3344, 3856, 4368, 4880, 5392)


class Buf:
    __slots__ = ("name", "w", "r", "parts")

    def __init__(self, name):
        self.name = name
        self.w = None
        self.r = {}
        self.parts = {}

    def sub(self, p):
        return Sub(self, p)


class Sub:
    __slots__ = ("parent", "p", "name")

    def __init__(self, parent, p):
        self.parent = parent
        self.p = p
        self.name = "%s[%s]" % (parent.name, p)

    def _slot(self):
        return self.parent.parts.setdefault(self.p, [None, {}])


class Eng:
    def __init__(self, key, e, sem):
        self.key = key
        self.e = e
        self.sem = sem
        self.count = 0
        self.waited = {}


class Sched:
    def __init__(self, nc, n_dma_sems=40):
        self.nc = nc
        self.sems = {}
        self.engs = {}
        for key, e in (("pe", nc.tensor), ("act", nc.scalar), ("dve", nc.vector), ("pool", nc.gpsimd), ("sp", nc.sync)):
            s = nc.alloc_semaphore("sem_" + key)
            self.sems[key] = s
            self.engs[key] = Eng(key, e, s)
        self.dma_sems = []
        for i in range(n_dma_sems):
            k = "dma%d" % i
            self.sems[k] = nc.alloc_semaphore("sem_" + k)
            self.dma_sems.append([k, 0])
        self.dma_rr = 0
        self.nops = 0
        self.clocks = {}
        self.psum = []
        self.ps_rr = 0
        for i in range(8):
            self.psum.append((nc.alloc_psum_tensor("psb%d" % i, [128, 512], F32), Buf("psb%d" % i)))

    def ps(self, pool=None):
        if pool is not None:
            banks, st = pool
            r = self.psum[banks[st[0] % len(banks)]]
            st[0] += 1
            return r
        r = self.psum[self.ps_rr]
        self.ps_rr = (self.ps_rr + 1) % 8
        return r

    def _deps(self, eng, reads, writes, is_dma):
        deps = {}

        def add(tok, same_ok):
            if tok is None:
                return
            k, v = tok
            if k == eng.key and not same_ok:
                return
            if deps.get(k, 0) < v:
                deps[k] = v

        same = is_dma or eng.key != "pe"
        for b in reads:
            if isinstance(b, Sub):
                add(b.parent.w, True)
                add(b._slot()[0], True)
            else:
                add(b.w, True)
                for pw, pr in b.parts.values():
                    add(pw, True)
        for b in writes:
            if isinstance(b, Sub):
                add(b.parent.w, same)
                for k, v in b.parent.r.items():
                    add((k, v), same)
                pw, pr = b._slot()
                add(pw, same)
                for k, v in pr.items():
                    add((k, v), same)
            else:
                add(b.w, same)
                for k, v in b.r.items():
                    add((k, v), same)
                for pw, pr in b.parts.values():
                    add(pw, same)
                    for k, v in pr.items():
                        add((k, v), same)
        for k, v in sorted(deps.items(), key=lambda kv: -kv[1]):
            self._need(eng, k, v)

    def _record(self, key, val, reads, writes):
        for b in reads:
            r = b._slot()[1] if isinstance(b, Sub) else b.r
            if r.get(key, 0) < val:
                r[key] = val
        for b in writes:
            if isinstance(b, Sub):
                sl = b._slot()
                sl[0] = (key, val)
                sl[1] = {}
            else:
                b.w = (key, val)
                b.r = {}
                b.parts = {}

    def _need(self, eng, k, v):
        if eng.waited.get(k, 0) >= v:
            return
        eng.e.wait_ge(self.sems[k], v)
        eng.waited[k] = v
        clk = self.clocks.get((k, v))
        if clk:
            w = eng.waited
            for k2, v2 in clk.items():
                if w.get(k2, 0) < v2:
                    w[k2] = v2

    def op(self, ek, fn, reads=(), writes=()):
        eng = self.engs[ek]
        self._deps(eng, reads, writes, False)
        ins = fn(eng.e)
        self.nops += 1
        eng.count += 1
        ins.then_inc(eng.sem, 1)
        clk = dict(eng.waited)
        clk.pop(eng.key, None)
        self.clocks[(eng.key, eng.count)] = clk
        self._record(eng.key, eng.count, reads, writes)
        return ins

    def dma(self, ek, out, in_, reads=(), writes=(), **kw):
        eng = self.engs[ek]
        self._deps(eng, reads, writes, True)
        slot = self.dma_sems[self.dma_rr]
        self.dma_rr = (self.dma_rr + 1) % len(self.dma_sems)
        k, uses = slot
        if uses > 0:
            self._need(eng, k, 16 * uses)
        ins = eng.e.dma_start(out=out, in_=in_, **kw)
        self.nops += 1
        slot[1] = uses + 1
        val = 16 * (uses + 1)
        ins.then_inc(self.sems[k], 16)
        clk = dict(eng.waited)
        clk.pop(eng.key, None)
        self.clocks[(k, val)] = clk
        self._record(k, val, reads, writes)
        return ins

    def wait_all(self, ek, bufs):
        eng = self.engs[ek]
        for b in bufs:
            toks = []
            if b.w is not None:
                toks.append(b.w)
            toks.extend(b.r.items())
            for pw, pr in b.parts.values():
                if pw is not None:
                    toks.append(pw)
                toks.extend(pr.items())
            for k, v in toks:
                if eng.waited.get(k, 0) < v:
                    eng.e.wait_ge(self.sems[k], v)
                    eng.waited[k] = v

    def barrier(self):
        for eng in self.engs.values():
            for o in self.engs.values():
                if o.key != eng.key and o.count > 0 and eng.waited.get(o.key, 0) < o.count:
                    eng.e.wait_ge(self.sems[o.key], o.count)
                    eng.waited[o.key] = o.count
            for k, uses in self.dma_sems:
                if uses > 0 and eng.waited.get(k, 0) < 16 * uses:
                    eng.e.wait_ge(self.sems[k], 16 * uses)
                    eng.waited[k] = 16 * uses


class Rot:
    def __init__(self, alloc, name, shape, dtype, n=2):
        self.items = [(alloc("%s%d" % (name, i), shape, dtype), Buf("%s%d" % (name, i))) for i in range(n)]
        self.i = 0

    def next(self):
        r = self.items[self.i]
        self.i = (self.i + 1) % len(self.items)
        return r


def host_consts():
    f = np.float32
    j = np.arange(128)[:, None]
    i = np.arange(128)[None, :]
    c = {}
    c["k_ident"] = np.eye(128, dtype=f)
    c["k_ones"] = np.ones((128, 128), f)
    am = np.zeros((4, 128, 128), f)
    am[0] = np.where(i >= j, 0.0, NEG)
    am[1] = np.where(i > j, 0.0, NEG)
    am[2] = np.where(i <= j, 0.0, NEG)
    am[3] = np.where(i < j, 0.0, NEG)
    c["k_amask"] = am
    tri = np.zeros((2, 128, 128), f)
    tri[0] = (j <= i)
    tri[1] = (j >= i)
    c["k_tri"] = tri
    bm = np.zeros((2, 128, 512), f)
    bm[0] = np.tile((j >= i).astype(f), (1, 4))
    bm[1] = np.tile((j <= i).astype(f), (1, 4))
    c["k_bmask"] = bm
    cm = np.zeros((6, 128, 128), f)
    cm[0] = np.maximum(i - j, 0)
    cm[1] = np.maximum(j - i, 0)
    cm[2] = (i > j)
    cm[3] = (j > i)
    cm[4] = np.broadcast_to(i + 1, (128, 128))
    cm[5] = np.broadcast_to(CH - i, (128, 128))
    c["k_cm"] = cm
    nm = np.zeros((14, 128, 128), f)
    for d in range(2):
        for lev in range(7):
            b = 1 << lev
            same = (j // (2 * b)) == (i // (2 * b))
            if d == 0:
                m = same & ((j % (2 * b)) < b) & ((i % (2 * b)) >= b)
            else:
                m = same & ((i % (2 * b)) < b) & ((j % (2 * b)) >= b)
            nm[d * 7 + lev] = -m.astype(f)
    c["k_nm"] = nm
    cj = np.zeros((128, 8), f)
    cj[:, 0:4] = (CH - 1 - np.arange(128))[:, None]
    cj[:, 4:8] = np.arange(128)[:, None]
    c["k_cj"] = cj
    t = np.arange(T)
    inv16 = (f(10000.0) ** (-np.arange(16, dtype=f) / f(16))).astype(f)
    ar = (t // 64).astype(f)[:, None] * inv16[None, :]
    ac = (t % 64).astype(f)[:, None] * inv16[None, :]
    ab = np.concatenate([ar, ac], axis=1).astype(f)
    c["k_cosb"] = np.tile(np.cos(ab).astype(f), (1, 8))
    c["k_sinb"] = np.tile(np.sin(ab).astype(f), (1, 8))
    inv64 = (f(10000.0) ** (-np.arange(64, dtype=f) / f(64))).astype(f)
    ang = t.astype(f)[:, None] * inv64[None, :]
    c["k_cosc"] = np.cos(ang).astype(f)
    c["k_sinc"] = np.sin(ang).astype(f)
    return c


CONST_SHAPES = {"k_ident": [128, 128], "k_ones": [128, 128], "k_amask": [4, 128, 128], "k_tri": [2, 128, 128],
                "k_bmask": [2, 128, 512], "k_cm": [6, 128, 128], "k_cj": [128, 8], "k_nm": [14, 128, 128],
                "k_cosb": [T, 256], "k_sinb": [T, 256], "k_cosc": [T, 64], "k_sinc": [T, 64]}

IN_SHAPES = {"x": [T, D], "c": [D], "ctx": [LC, D], "c_ctx": [D], "w_ada": [2, D, 3 * D], "b_ada": [2, 3 * D],
             "norm_w": [2, D], "w_in": [2, D, INW], "a_conv_w": [2, 5, 1536], "a_log": [2, 8], "a_dt_bias": [2, 8],
             "a_norm_w": [2, 128], "b_sink": [2, 8], "c_decay": [2, 8], "c_norm_w": [2, 512],
             "w_branch": [2, 3, 512, D], "w_out": [2, D, D], "final_norm_w": [D]}

SCRATCH = {"XS": ([T, D], F32), "CTXS": ([LC, D], F32),
           "QA_FM": ([4, 128, NTOK], BF16), "KA_FM": ([4, 128, NTOK], BF16),
           "KA_TM": ([4, NTOK, 128], BF16), "VA_TM": ([4, NTOK, 128], BF16),
           "ZA": ([NTOK, 512], BF16), "ZB": ([NTOK, 512], BF16), "ZC": ([NTOK, 512], BF16),
           "OA": ([NTOK, 512], F32),
           "QB_FM": ([8, 128, NTOK], BF16),
           "QC_FM": ([4, 128, NTOK], BF16), "KC_FM": ([4, 128, NTOK], BF16),
           "KC_TM": ([NTOK, 512], BF16), "VC_TM": ([NTOK, 512], BF16),
           "SCB": ([NT, 128, 512], BF16), "SCF": ([NT, 128, 512], BF16),
           "KB_FM": ([2, 128, NTOK], BF16), "VB_TM": ([NTOK, 2, 65], BF16),
           "GM_FM": ([3 * D, NTOK], BF16), "Y_FM": ([3, 512, NTOK], BF16)}


class MK:
    def __init__(self, nlayers=2, dbg=(), stop=None):
        nc = bass.Bass("TRN2", target_bir_lowering=False, dynamic_dma_scratch_size=1024)
        self.nc = nc
        self.S = Sched(nc)
        self.nlayers = nlayers
        self.stop = stop
        self.din = {}
        for n, shp in list(IN_SHAPES.items()) + list(CONST_SHAPES.items()):
            self.din[n] = nc.dram_tensor(n, shp, F32, kind="ExternalInput").ap()
        self.out = nc.dram_tensor("out", [T, D], F32, kind="ExternalOutput").ap()
        self.scr = {}
        self._db = {}
        for n, (shp, dt) in SCRATCH.items():
            kind = "ExternalOutput" if n in dbg else "Internal"
            self.scr[n] = nc.dram_tensor(n, shp, dt, kind=kind).ap()
        self.dbg = dbg
        self._tcount = 0
        self.alloc()

    def db(self, name, tt):
        k = (name, tt)
        if k not in self._db:
            self._db[k] = Buf("%s_%d" % k)
        return self._db[k]

    def dball(self, name):
        return [self.db(name, tt) for tt in range(NT)]

    def sb(self, name, shape, dtype=F32):
        return self.nc.alloc_sbuf_tensor(name, shape, dtype), Buf(name)

    def rot(self, name, shape, dtype=F32, n=2):
        return Rot(lambda nm, sh, dt: self.nc.alloc_sbuf_tensor(nm, sh, dt), name, shape, dtype, n)

    def _talloc(self, name, shape, dtype):
        self._tcount += 1
        return self.es.enter_context(self.nc.sbuf_tensor("%s_t%d" % (name, self._tcount), shape, dtype))

    def tsb(self, name, shape, dtype=F32):
        return self._talloc(name, shape, dtype), Buf(name)

    def trot(self, name, shape, dtype=F32, n=2):
        return Rot(self._talloc, name, shape, dtype, n)

    def alloc(self):
        nc, S, din = self.nc, self.S, self.din
        self.ident_f, self.b_ident_f = self.sb("ident_f", [128, 128])
        self.ones_f, self.b_ones_f = self.sb("ones_f", [128, 128])
        self.ident_b, self.b_ident_b = self.sb("ident_b", [128, 128], BF16)
        self.ones_b, self.b_ones_b = self.sb("ones_b", [128, 128], BF16)
        self.amask, self.b_amask = self.sb("amask", [128, 4, 128])
        self.tri, self.b_tri = self.sb("tri", [128, 2, 128])
        self.bmask, self.b_bmask = self.sb("bmask", [128, 2, 512], BF16)
        self.cm, self.b_cm = self.sb("cm", [128, 6, 128])
        self.cj, self.b_cj = self.sb("cj", [128, 8])
        S.dma("sp", self.ident_f[:], din["k_ident"], writes=[self.b_ident_f])
        S.dma("sp", self.ones_f[:], din["k_ones"], writes=[self.b_ones_f])
        S.dma("sp", self.amask[:], din["k_amask"].rearrange("a p n -> p a n"), writes=[self.b_amask])
        S.dma("sp", self.tri[:], din["k_tri"].rearrange("a p n -> p a n"), writes=[self.b_tri])
        S.dma("sp", self.cm[:], din["k_cm"].rearrange("a p n -> p a n"), writes=[self.b_cm])
        S.dma("sp", self.cj[:], din["k_cj"], writes=[self.b_cj])
        S.op("dve", lambda e: e.tensor_copy(out=self.ident_b[:], in_=self.ident_f[:]), reads=[self.b_ident_f], writes=[self.b_ident_b])
        S.op("dve", lambda e: e.tensor_copy(out=self.ones_b[:], in_=self.ones_f[:]), reads=[self.b_ones_f], writes=[self.b_ones_b])
        self.R1, self.b_R1 = self.sb("R1", [128, 8 * NTOK], BF16)
        self.hT = self.R1[:].rearrange("p (k t) -> p k t", k=8)
        self.b_hT = [Buf("hT%d" % i) for i in range(NT)]
        self.wst = self.rot("wst", [128, 8, 512], F32, 1)
        self.wb = self.rot("wb", [128, 8, 512], BF16, 4)
        self.scol, self.b_scol = self.sb("scol", [128, 8, 2])
        self.mod, self.b_mod = self.sb("mod", [128, 24, 2])
        self.nwcol, self.b_nwcol = self.sb("nwcol", [128, 8])
        self.badacol, self.b_badacol = self.sb("badacol", [128, 24])
        self.Afm, self.b_Afm = self.sb("Afm", [128, 8, 2])
        self.convw, self.b_convw = self.sb("convw", [128, 12, 5])
        self.rowtmp = self.rot("rowtmp", [128, 128], F32, 2)
        for it in self.rowtmp.items:
            S.op("pool", lambda e: e.memset(it[0][:], 0.0), writes=[it[1]])
        self.gdiag = self.rot("gdiag", [128, 512], F32, 2)
        self.SCR, self.b_SCR = self.sb("SCR", [128, NT, 16])
        self.asc = {}
        for n in ("BETA", "GC", "NEGG", "LNBMG", "NEGEG", "ETAIL", "EGL"):
            self.asc[n] = self.sb("asc_" + n, [128, NT, 8])
        self.SH2, self.b_SH2 = self.sb("SH2", [128, NT, 8])
        self.kmx, self.b_kmx = self.sb("kmx", [128, 4])
        self.par = {}
        for n, w in (("a_log", 8), ("a_dt_bias", 8), ("b_sink", 8), ("c_decay", 8), ("a_norm_w", 128), ("c_norm_w", 512)):
            self.par[n] = self.sb("par_" + n, [128, w])
        self.epsb, self.b_epsb = self.sb("epsb", [128, 4])
        for col, val in ((0, EPS), (1, -0.5 * float(np.log(128.0))), (2, 0.0), (3, 1.0)):
            S.op("pool", lambda e: e.memset(self.epsb[:, col:col + 1], val), writes=[self.b_epsb])

    def load_cols(self, src_rows, n, dst, bdst, func=None):
        S = self.S
        rt, brt = self.rowtmp.next()
        S.dma("sp", rt[0:n, :], src_rows, writes=[brt])
        if func is not None:
            S.op("act", lambda e: e.activation(out=rt[0:n, :], in_=rt[0:n, :], func=func), reads=[brt], writes=[brt])
        ps, bps = S.ps()
        S.op("pe", lambda e: e.transpose(out=ps[:, 0:128], in_=rt[:, :], identity=self.ident_f[:]),
             reads=[brt, self.b_ident_f], writes=[bps])
        S.op("dve", lambda e: e.tensor_copy(out=dst, in_=ps[:, 0:n]), reads=[bps], writes=[bdst])

    def w_plan(self, src2d, groups):
        self._wsrc = src2d
        self._wgroups = list(groups)
        self._wi = 0
        self._wq = []
        self._w_issue()

    def _w_issue(self):
        if self._wi < len(self._wgroups):
            c0, ncols = self._wgroups[self._wi]
            self._wi += 1
            self._wq.append(((c0, ncols), self._load_w_raw(self._wsrc, c0, ncols)))

    def load_w(self, src2d, c0, ncols, nk=8):
        if getattr(self, "_wq", None):
            key, val = self._wq.pop(0)
            assert key == (c0, ncols), (key, c0, ncols)
            self._w_issue()
            return val
        return self._load_w_raw(src2d, c0, ncols, nk)

    def _load_w_raw(self, src2d, c0, ncols, nk=8):
        S = self.S
        st, bst = self.wst.next()
        wb, bwb = self.wb.next()
        S.dma("sp", st[:, 0:nk, 0:ncols], src2d.rearrange("(k p) n -> p k n", p=128)[:, :, c0:c0 + ncols], writes=[bst])
        S.op("pool", lambda e: e.tensor_copy(out=wb[:, 0:nk, 0:ncols], in_=st[:, 0:nk, 0:ncols]), reads=[bst], writes=[bwb])
        return wb, bwb

    def phase0(self, l):
        S, din = self.S, self.din
        self.es = ExitStack()
        for n in self.par:
            t, b = self.par[n]
            S.dma("sp", t[:], din[n][l].partition_broadcast(128), writes=[b])
        self.load_cols(din["c"].rearrange("(k p) -> k p", p=128), 8, self.scol[:, :, 0], self.b_scol, AF.Silu)
        self.load_cols(din["c_ctx"].rearrange("(k p) -> k p", p=128), 8, self.scol[:, :, 1], self.b_scol, AF.Silu)
        self.load_cols(din["norm_w"][l].rearrange("(k p) -> k p", p=128), 8, self.nwcol[:], self.b_nwcol)
        self.load_cols(din["b_ada"][l].rearrange("(k p) -> k p", p=128), 24, self.badacol[:], self.b_badacol)
        cw, bcw = self.tsb("cwrows", [128, 1536])
        S.op("pool", lambda e: e.memset(cw[:], 0.0), writes=[bcw])
        S.dma("sp", cw[0:5, :], din["a_conv_w"][l], writes=[bcw])
        for ct in range(12):
            ps, bps = S.ps()
            S.op("pe", lambda e: e.transpose(out=ps[:, 0:128], in_=cw[:, ct * 128:(ct + 1) * 128], identity=self.ident_f[:]),
                 reads=[bcw, self.b_ident_f], writes=[bps])
            S.op("dve", lambda e: e.tensor_copy(out=self.convw[:, ct, :], in_=ps[:, 0:5]), reads=[bps], writes=[self.b_convw])
        psm, bpsm = S.ps()
        for g in range(6):
            st, bst = self.wst.next()
            S.dma("sp", st[:], din["w_ada"][l].rearrange("(k p) n -> p k n", p=128)[:, :, g * 512:(g + 1) * 512], writes=[bst])
            for jl in range(4):
                j = g * 4 + jl
                for kc in range(8):
                    S.op("pe", lambda e: e.matmul(out=psm[:, 2 * j:2 * j + 2], lhsT=st[:, kc, jl * 128:(jl + 1) * 128], rhs=self.scol[:, kc, :],
                                                  start=(kc == 0), stop=(kc == 7)),
                         reads=[bst, self.b_scol], writes=[bpsm])
        S.op("dve", lambda e: e.tensor_tensor(out=self.mod[:], in0=psm[:, 0:48].rearrange("p (j s) -> p j s", s=2),
                                              in1=self.badacol[:].unsqueeze(2).to_broadcast([128, 24, 2]), op=ALU.add),
             reads=[bpsm, self.b_badacol], writes=[self.b_mod])
        S.op("dve", lambda e: e.scalar_tensor_tensor(out=self.Afm[:], in0=self.mod[:, 8:16, :], scalar=1.0,
                                                     in1=self.nwcol[:].unsqueeze(2).to_broadcast([128, 8, 2]), op0=ALU.add, op1=ALU.mult),
             reads=[self.b_mod, self.b_nwcol], writes=[self.b_Afm])
        S.barrier()
        self.es.close()

    def bc_rows(self, dst_fn, bdst, col_fn, bcol, nchunks):
        S = self.S
        for half in range(nchunks // 4):
            t, bt = self.gdiag.next()
            for q in range(4):
                kc = half * 4 + q
                S.op("dve", lambda e: e.tensor_scalar(out=t[:, q * 128:(q + 1) * 128], in0=self.ident_f[:], scalar1=col_fn(kc),
                                                      scalar2=None, op0=ALU.mult),
                     reads=[self.b_ident_f, bcol], writes=[bt])
            ps, bps = S.ps()
            S.op("pe", lambda e: e.matmul(out=ps[:], lhsT=self.ones_f[:], rhs=t[:], start=True, stop=True),
                 reads=[self.b_ones_f, bt], writes=[bps])
            S.op("act", lambda e: e.copy(out=dst_fn(half), in_=ps[:]), reads=[bps], writes=[bdst])

    def phase1(self, l):
        S, din = self.S, self.din
        self.es = ExitStack()
        self.p1_xt = self.trot("p1_xt", [128, 1024], F32, 2)
        self.p1_sq, self.b_p1_sq = self.tsb("p1_sq", [128, 1024])
        self.p1_st = self.trot("p1_st", [128, 4], F32, 2)
        for tt in range(NT):
            s = 1 if tt < 2 else 0
            if tt < 2:
                src = (din["ctx"] if l == 0 else self.scr["CTXS"])[tt * 128:(tt + 1) * 128, :]
                rd = [] if l == 0 else [self.db("CTXS", tt)]
            else:
                src = (din["x"] if l == 0 else self.scr["XS"])[(tt - 2) * 128:(tt - 1) * 128, :]
                rd = [] if l == 0 else [self.db("XS", tt)]
            xt, bxt = self.p1_xt.next()
            st, bst = self.p1_st.next()
            S.dma("sp", xt[:], src, reads=rd, writes=[bxt])
            S.op("act", lambda e: e.activation(out=self.p1_sq[:], in_=xt[:], func=AF.Square, accum_out=st[:, 0:1]),
                 reads=[bxt], writes=[self.b_p1_sq, bst])
            S.op("dve", lambda e: e.tensor_scalar(out=st[:, 1:2], in0=st[:, 0:1], scalar1=1.0 / D, scalar2=EPS, op0=ALU.mult, op1=ALU.add),
                 reads=[bst], writes=[bst])
            S.op("act", lambda e: e.activation(out=st[:, 2:3], in_=st[:, 1:2], func=AF.Ln), reads=[bst], writes=[bst])
            S.op("act", lambda e: e.activation(out=st[:, 3:4], in_=st[:, 2:3], func=AF.Exp, scale=-0.5), reads=[bst], writes=[bst])
            S.op("act", lambda e: e.activation(out=xt[:], in_=xt[:], func=AF.Copy, scale=st[:, 3:4]),
                 reads=[bst, bxt], writes=[bxt])
            for half in range(2):
                ps, bps = S.ps()
                for q in range(4):
                    kc = half * 4 + q
                    S.op("pe", lambda e: e.transpose(out=ps[:, q * 128:(q + 1) * 128], in_=xt[:, kc * 128:(kc + 1) * 128], identity=self.ident_f[:]),
                         reads=[bxt, self.b_ident_f], writes=[bps])
                for q in range(4):
                    kc = half * 4 + q
                    dst = self.hT[:, kc, tt * 128:(tt + 1) * 128]
                    if half == 0:
                        S.op("dve", lambda e: e.tensor_scalar(out=dst, in0=ps[:, q * 128:(q + 1) * 128], scalar1=self.Afm[:, kc, s:s + 1],
                                                              scalar2=self.mod[:, kc, s:s + 1], op0=ALU.mult, op1=ALU.add),
                             reads=[bps, self.b_Afm, self.b_mod], writes=[self.b_hT[tt].sub(kc)])
                    else:
                        S.op("act", lambda e: e.activation(out=dst, in_=ps[:, q * 128:(q + 1) * 128], func=AF.Identity,
                                                           scale=self.Afm[:, kc, s:s + 1], bias=self.mod[:, kc, s:s + 1]),
                             reads=[bps, self.b_Afm, self.b_mod], writes=[self.b_hT[tt].sub(kc)])
        S.barrier()
        self.es.close()

    def tok_groups(self):
        g = [(0, 256, 0, 2)]
        for i in range(8):
            g.append((256 + i * 512, 512, 2 + 4 * i, 4))
        return g

    class WStream:
        def __init__(self, mk, src2d, groups, bufs):
            self.mk, self.src, self.groups, self.bufs = mk, src2d, list(groups), bufs
            self.i = 0
            self.q = []
            self._issue()

        def _issue(self):
            if self.i < len(self.groups):
                c0, ncols = self.groups[self.i]
                wb, bwb = self.bufs[self.i % len(self.bufs)]
                S = self.mk.S
                st, bst = self.mk.wst.next()
                S.dma("sp", st[:, :, 0:ncols], self.src.rearrange("(k p) n -> p k n", p=128)[:, :, c0:c0 + ncols], writes=[bst])
                S.op("pool", lambda e: e.tensor_copy(out=wb[:, :, 0:ncols], in_=st[:, :, 0:ncols]), reads=[bst], writes=[bwb])
                self.q.append(((c0, ncols), (wb, bwb)))
                self.i += 1

        def get(self, c0, ncols):
            key, val = self.q.pop(0)
            assert key == (c0, ncols), (key, c0, ncols)
            self._issue()
            return val

    def proj_tm_gen(self, l, c0, ncols, handler, ws):
        S = self.S
        if hasattr(self, "marks"):
            self.marks.append(("  L%d tm@%d" % (l, c0), {k: v.count for k, v in S.engs.items()}))
        wb, bwb = ws.get(c0, ncols)
        prev = None
        for tt in range(NT):
            ps, bps = S.ps()
            for kc in range(8):
                S.op("pe", lambda e: e.matmul(out=ps[:, 0:ncols], lhsT=self.hT[:, kc, tt * 128:(tt + 1) * 128], rhs=wb[:, kc, 0:ncols],
                                              start=(kc == 0), stop=(kc == 7)),
                     reads=[self.b_hT[tt], bwb], writes=[bps])
            if prev is not None:
                handler(*prev)
            prev = (tt, ps, bps)
            yield
        handler(*prev)
        yield

    def proj_tm(self, l, c0, ncols, handler, ws):
        for _ in self.proj_tm_gen(l, c0, ncols, handler, ws):
            pass

    @staticmethod
    def run_pair(g1, g2):
        gens = [g1, g2]
        while gens:
            for g_ in list(gens):
                try:
                    next(g_)
                except StopIteration:
                    gens.remove(g_)

    def transpose_out(self, src_fn, n, rows, dst, dst_buf_list, tag, pool=None):
        S = self.S
        ps, bps = S.ps(pool)
        psb = ps[:].bitcast(BF16)
        for i in range(n):
            src, bsrc = src_fn(i)
            S.op("pe", lambda e: e.transpose(out=psb[0:rows, i * 128:(i + 1) * 128], in_=src, identity=self.ident_b[:]),
                 reads=[bsrc, self.b_ident_b], writes=[bps])
        S.op("act", lambda e: e.copy(out=dst, in_=psb[0:rows, 0:n * 128]), reads=[bps], writes=dst_buf_list)

    def phase2(self, l):
        S, din, scr = self.S, self.din, self.scr
        last = (l == self.nlayers - 1)
        self.es = ExitStack()
        self.zt = self.trot("zt", [128, 512], BF16, 2)
        self.tmpA = self.trot("tmpA", [128, 512], F32, 2)
        self.tmpB = self.trot("tmpB", [128, 512], F32, 2)
        self.tmo = self.trot("tmo", [128, 512], BF16, 3)
        self.fmo = self.trot("fmo", [128, 512], BF16, 3)
        self.qa = self.trot("qa", [128, 8, 128], BF16, 2)
        self.ka = self.trot("ka", [128, 2, 128], BF16, 2)
        self.vb = self.trot("vb", [128, 2, 65], BF16, 2)
        self.qaT = self.trot("qaT", [128, 8, 128], BF16, 2)
        self.kaT = self.trot("kaT", [128, 2, 128], BF16, 2)
        self.csc = self.trot("csc", [128, 2, 64], F32, 3)
        self.csb = self.trot("csb", [128, 2, 256], F32, 3)
        self.st8 = self.trot("st8", [128, 24], F32, 4)
        self.tmp8 = self.trot("tmp8", [128, NT, 8], F32, 4)
        for n in ("LNB", "GRAW", "GT"):
            self.asc[n] = self.tsb("asc_" + n, [128, NT, 8])
        self.KM, self.b_KM = self.tsb("KM", [128, 2])
        self.half8, self.b_half8 = self.tsb("half8", [128, 8])
        S.op("pool", lambda e: e.memset(self.half8[:], 0.5), writes=[self.b_half8])
        self.rowbuf, self.b_rowbuf = self.tsb("rowbuf", [128, 4360])
        self.slrow, self.b_slrow = self.tsb("slrow", [128, NTOK])
        S.op("pool", lambda e: e.memset(self.rowbuf[:], 0.0), writes=[self.b_rowbuf])
        for it in self.vb.items:
            S.op("pool", lambda e: e.memset(it[0][:], 1.0), writes=[it[1]])
        for it in self.ka.items + self.qa.items:
            S.op("pool", lambda e: e.memset(it[0][:], 0.0), writes=[it[1]])
        for it in self.ka.items:
            S.op("pool", lambda e: e.memset(it[0][:, :, 64:65], 1.0), writes=[it[1]])
        S.op("pool", lambda e: e.memset(self.KM[:], 0.0), writes=[self.b_KM])

        def silu_out(name):
            def h(tt, ps, bps):
                z, bz = self.zt.next()
                S.op("act", lambda e: e.activation(out=z[:], in_=ps[:], func=AF.Silu), reads=[bps], writes=[bz])
                S.dma("act", scr[name][tt * 128:(tt + 1) * 128, :], z[:], reads=[bz], writes=[self.db(name, tt)])
            return h

        wsrc = din["w_in"][l]
        bufsA, bufsB = self.wb.items[0:2], self.wb.items[2:4]
        ws1 = self.WStream(self, wsrc, [(O_AZ, 512), (O_AB, 16), (O_BKV, 256)], bufsA)
        self.proj_tm(l, O_AZ, 512, silu_out("ZA"), ws1)
        if self.stop == "p2a":
            S.barrier()
            self.es.close()
            return

        def h_ab(tt, ps, bps):
            S.op("act", lambda e: e.copy(out=self.SCR[:, tt, :], in_=ps[:, 0:16]), reads=[bps], writes=[self.b_SCR])
        self.proj_tm(l, O_AB, 16, h_ab, ws1)
        if self.stop == "p2b1":
            S.barrier()
            self.es.close()
            return
        self.a_scalars(l)
        if self.stop in ("p2b", "p2b2"):
            S.barrier()
            self.es.close()
            return

        def load_cs(tt, which):
            rotp, cn, sn, w = (self.csc, "k_cosc", "k_sinc", 64) if which == "c" else (self.csb, "k_cosb", "k_sinb", 256)
            cs, bcs = rotp.next()
            r0 = (tt - 2) * 128
            S.dma("sp", cs[:, 0, :], din[cn][r0:r0 + 128, :], writes=[bcs])
            S.dma("sp", cs[:, 1, :], din[sn][r0:r0 + 128, :], writes=[bcs])
            return cs, bcs

        def rope(x1, x2, cosb, sinb, o1, o2, shape_fn, bps, bcs, bout, scale=None):
            ta, bta = self.tmpA.next()
            tb, btb = self.tmpB.next()
            ta1, ta2 = shape_fn(ta[:, 0:256]), shape_fn(ta[:, 256:512])
            tb1, tb2 = shape_fn(tb[:, 0:256]), shape_fn(tb[:, 256:512])
            if scale is None:
                mul = lambda o, a, b: (lambda e: e.tensor_tensor(out=o, in0=a, in1=b, op=ALU.mult))
            else:
                mul = lambda o, a, b: (lambda e: e.scalar_tensor_tensor(out=o, in0=a, scalar=scale, in1=b, op0=ALU.mult, op1=ALU.mult))
            S.op("dve", mul(ta1, x1, cosb), reads=[bps, bcs], writes=[bta.sub(0)])
            S.op("dve", mul(tb1, x2, sinb), reads=[bps, bcs], writes=[btb.sub(0)])
            S.op("dve", mul(ta2, x1, sinb), reads=[bps, bcs], writes=[bta.sub(1)])
            S.op("dve", mul(tb2, x2, cosb), reads=[bps, bcs], writes=[btb.sub(1)])
            S.op("pool", lambda e: e.tensor_tensor(out=o1, in0=ta1, in1=tb1, op=ALU.subtract), reads=[bta.sub(0), btb.sub(0)], writes=[bout.sub("o1")])
            S.op("pool", lambda e: e.tensor_tensor(out=o2, in0=ta2, in1=tb2, op=ALU.add), reads=[bta.sub(1), btb.sub(1)], writes=[bout.sub("o2")])

        def rope_b(ps_ap, nha, tt, dst, bdst, bps, nh):
            o, bo = self.tmo.next()
            if tt >= 2:
                cs, bcs = load_cs(tt, "b")
                pv = ps_ap.rearrange("p (g f k) -> p g f k", f=2, k=16)
                ov = o[:, 0:nha * 32].rearrange("p (g f k) -> p g f k", f=2, k=16)
                cosb = cs[:, 0, 0:nha * 16].rearrange("p (g k) -> p g k", k=16)
                sinb = cs[:, 1, 0:nha * 16].rearrange("p (g k) -> p g k", k=16)
                rope(pv[:, :, 0, :], pv[:, :, 1, :], cosb, sinb, ov[:, :, 0, :], ov[:, :, 1, :],
                     lambda a: a[:, 0:nha * 16].rearrange("p (g k) -> p g k", k=16), bps, bcs, bo)
                S.op("act", lambda e: e.copy(out=dst[:, :, 0:64], in_=o[:, 0:nha * 32].rearrange("p (h k) -> p h k", h=nh)), reads=[bo], writes=[bdst])
            else:
                S.op("act", lambda e: e.copy(out=dst[:, :, 0:64], in_=ps_ap.rearrange("p (h k) -> p h k", h=nh)), reads=[bps], writes=[bdst])

        def h_bkv(tt, ps, bps):
            ka, bka = self.ka.next()
            rope_b(ps[:, 0:128], 4, tt, ka, bka, bps, 2)
            ta, bta = self.tmpA.next()
            st, bst = self.st8.next()
            S.op("act", lambda e: e.activation(out=ta[:, 0:128], in_=ps[:, 0:128], func=AF.Square), reads=[bps], writes=[bta])
            S.op("dve", lambda e: e.tensor_reduce(out=st[:, 0:2], in_=ta[:, 0:128].rearrange("p (h k) -> p h k", h=2), axis=AX.X, op=ALU.add),
                 reads=[bta], writes=[bst])
            S.op("dve", lambda e: e.tensor_tensor(out=self.KM[:], in0=self.KM[:], in1=st[:, 0:2], op=ALU.max), reads=[bst, self.b_KM], writes=[self.b_KM])
            vb, bvb = self.vb.next()
            S.op("dve", lambda e: e.tensor_copy(out=vb[:, :, 0:64], in_=ps[:, 128:256].rearrange("p (h k) -> p h k", h=2)),
                 reads=[bps], writes=[bvb])
            S.dma("act", scr["VB_TM"][tt * 128:(tt + 1) * 128, :, :], vb[:], reads=[bvb], writes=[self.db("VB_TM", tt)])
            kT, bkT = self.kaT.next()
            self.transpose_out(lambda i: (ka[:, i, :], bka), 2, 128, kT[:].rearrange("r h t -> r (h t)"), [bkT], "kbt")
            S.dma("act", scr["KB_FM"].rearrange("h r t -> r h t")[:, :, tt * 128:(tt + 1) * 128], kT[:], reads=[bkT], writes=[self.db("KB_FM", tt)])
        self.proj_tm(l, O_BKV, 256, h_bkv, ws1)
        if self.stop == "p2c":
            S.barrier()
            self.es.close()
            return
        S.op("dve", lambda e: e.tensor_reduce(out=self.kmx[:, 0:1], in_=self.KM[:], axis=AX.X, op=ALU.max), reads=[self.b_KM], writes=[self.b_kmx])
        dgk, bdgk = self.gdiag.next()
        S.op("dve", lambda e: e.tensor_scalar(out=dgk[:, 0:128], in0=self.ident_f[:], scalar1=self.kmx[:, 0:1], scalar2=None, op0=ALU.mult),
             reads=[self.b_ident_f, self.b_kmx], writes=[bdgk])
        ps, bps = S.ps()
        S.op("pe", lambda e: e.matmul(out=ps[:, 0:128], lhsT=self.ones_f[:], rhs=dgk[:, 0:128], start=True, stop=True),
             reads=[self.b_ones_f, bdgk], writes=[bps])
        kr, bkr = self.tsb("kmrow", [128, 4])
        S.op("dve", lambda e: e.tensor_reduce(out=kr[:, 0:1], in_=ps[:, 0:128], axis=AX.X, op=ALU.max), reads=[bps], writes=[bkr])
        S.op("act", lambda e: e.activation(out=kr[:, 1:2], in_=kr[:, 0:1], func=AF.Ln), reads=[bkr], writes=[bkr])
        S.op("act", lambda e: e.activation(out=kr[:, 2:3], in_=kr[:, 1:2], func=AF.Exp, scale=0.5), reads=[bkr], writes=[bkr])
        S.op("dve", lambda e: e.tensor_scalar(out=self.kmx[:, 1:2], in0=kr[:, 2:3], scalar1=-1.0, scalar2=None, op0=ALU.mult), reads=[bkr], writes=[self.b_kmx])
        S.op("dve", lambda e: e.tensor_scalar(out=self.kmx[:, 2:3], in0=kr[:, 2:3], scalar1=-0.125, scalar2=None, op0=ALU.mult), reads=[bkr], writes=[self.b_kmx])

        def h_bq(tt, ps, bps):
            qa, bqa = self.qa.next()
            ta, bta = self.tmpA.next()
            st, bst = self.st8.next()
            S.op("act", lambda e: e.activation(out=ta[:], in_=ps[:], func=AF.Square), reads=[bps], writes=[bta])
            S.op("dve", lambda e: e.tensor_reduce(out=st[:, 0:8], in_=ta[:].rearrange("p (h k) -> p h k", h=8), axis=AX.X, op=ALU.add),
                 reads=[bta], writes=[bst])
            S.op("pool", lambda e: e.tensor_tensor(out=st[:, 16:24], in0=st[:, 0:8], in1=self.half8[:], op=ALU.pow), reads=[bst, self.b_half8], writes=[bst])
            rope_b(ps[:], 16, tt, qa, bqa, bps, 8)
            S.op("dve", lambda e: e.tensor_scalar(out=qa[:, :, 64], in0=st[:, 16:24], scalar1=self.kmx[:, 1:2], scalar2=None, op0=ALU.mult),
                 reads=[bst, self.b_kmx], writes=[bqa])
            S.op("dve", lambda e: e.scalar_tensor_tensor(out=self.SH2[:, tt, :], in0=st[:, 16:24], scalar=self.kmx[:, 2:3], in1=self.par["b_sink"][0][:],
                                                         op0=ALU.mult, op1=ALU.add),
                 reads=[bst, self.b_kmx, self.par["b_sink"][1]], writes=[self.b_SH2])
            qT, bqT = self.qaT.next()
            self.transpose_out(lambda i: (qa[:, i, :], bqa), 8, 128, qT[:].rearrange("r h t -> r (h t)"), [bqT], "qbt")
            S.dma("act", scr["QB_FM"].rearrange("h r t -> r h t")[:, :, tt * 128:(tt + 1) * 128], qT[:], reads=[bqT], writes=[self.db("QB_FM", tt)])
        if self.stop == "p2d":
            S.barrier()
            self.es.close()
            return

        def h_cqk(name_fm, name_tm, scale):
            def h(tt, ps, bps):
                o, bo = self.tmo.next()
                if tt >= 2:
                    cs, bcs = load_cs(tt, "c")
                    pv = ps[:].rearrange("p (h f k) -> p h f k", h=4, f=2)
                    ov = o[:].rearrange("p (h f k) -> p h f k", h=4, f=2)
                    cosb = cs[:, 0, :].unsqueeze(1).to_broadcast([128, 4, 64])
                    sinb = cs[:, 1, :].unsqueeze(1).to_broadcast([128, 4, 64])
                    rope(pv[:, :, 0, :], pv[:, :, 1, :], cosb, sinb, ov[:, :, 0, :], ov[:, :, 1, :],
                         lambda a: a.rearrange("p (h k) -> p h k", h=4), bps, bcs, bo, scale=scale)
                else:
                    S.op("act", lambda e: e.activation(out=o[:], in_=ps[:], func=AF.Copy, scale=(1.0 if scale is None else scale)), reads=[bps], writes=[bo])
                if name_tm is not None:
                    S.dma("act", scr[name_tm][tt * 128:(tt + 1) * 128, :], o[:], reads=[bo], writes=[self.db(name_tm, tt)])
                f, bf = self.fmo.next()
                self.transpose_out(lambda i: (o[:, i * 128:(i + 1) * 128], bo), 4, 128, f[:], [bf], "cfm")
                S.dma("act", scr[name_fm].rearrange("h p t -> p h t")[:, :, tt * 128:(tt + 1) * 128], f[:].rearrange("p (h t) -> p h t", h=4),
                      reads=[bf], writes=[self.db(name_fm, tt)])
            return h

        def h_cv(tt, ps, bps):
            o, bo = self.tmo.next()
            S.op("act", lambda e: e.copy(out=o[:], in_=ps[:]), reads=[bps], writes=[bo])
            S.dma("act", scr["VC_TM"][tt * 128:(tt + 1) * 128, :], o[:], reads=[bo], writes=[self.db("VC_TM", tt)])
        wsZ = self.WStream(self, wsrc, [(O_BZ, 512), (O_CZ, 512)], bufsB)
        self.proj_tm(l, O_BZ, 512, silu_out("ZB"), wsZ)
        self.proj_tm(l, O_CZ, 512, silu_out("ZC"), wsZ)
        groups = self.tok_groups()
        wsH = self.WStream(self, wsrc, [(O_BQ, 512), (O_CQ, 512), (O_CK, 512)] + [(g * 512, 512) for g in range(3)], bufsA)
        wsL = self.WStream(self, wsrc, [(O_MG + g * 512, 512) for g in range(6)] + [(O_CV, 512)], bufsB)
        wsF = wsH
        wsM = wsL

        def heavy():
            yield from self.proj_tm_gen(l, O_BQ, 512, h_bq, wsH)
            yield from self.proj_tm_gen(l, O_CQ, 512, h_cqk("QC_FM", None, None), wsH)
            yield from self.proj_tm_gen(l, O_CK, 512, h_cqk("KC_FM", "KC_TM", float(CH) ** -0.5), wsH)

        def light():
            yield from self.proj_tm_gen(l, O_CV, 512, h_cv, wsL)
        if self.stop == "p2e":
            S.barrier()
            self.es.close()
            return


        def afm_gen():
            wcur = {}

            def get_w(g3):
                if g3 not in wcur:
                    self.marks.append(("  L%d afm%d" % (l, g3), {k: v.count for k, v in S.engs.items()}))
                    wcur[g3] = wsF.get(g3 * 512, 512)
                return wcur[g3]

            def proj_piece(ct, gi):
                g3, cl = ct // 4, ct % 4
                wb, bwb = get_w(g3)
                (t0, n, tile0, ntile) = groups[gi]
                ps, bps = S.ps()
                for kc in range(8):
                    S.op("pe", lambda e: e.matmul(out=ps[:, 0:n], lhsT=wb[:, kc, cl * 128:(cl + 1) * 128], rhs=self.hT[:, kc, t0:t0 + n],
                                                  start=(kc == 0), stop=(kc == 7)),
                         reads=[self.b_hT[tile0 + i] for i in range(ntile)] + [bwb], writes=[bps])
                off = 2 + t0 if t0 == 0 else 6 + t0
                S.op("act", lambda e: e.copy(out=self.rowbuf[:, off:off + n], in_=ps[:, 0:n]), reads=[bps], writes=[self.b_rowbuf.sub(t0)])

            def pass1_piece(ct, gi):
                g3, head = ct // 4, ct % 4
                (t0, n, tile0, ntile) = groups[gi]
                off = 2 + t0 if t0 == 0 else 6 + t0
                cv, bcv = self.tmpA.next()
                nb = [gi] if gi == 0 else [j for j in (gi - 1, gi, gi + 1) if 1 <= j < len(groups)]
                rb_reads = [self.b_rowbuf.sub(groups[j][0]) for j in nb]
                for k in range(5):
                    src = self.rowbuf[:, off + k - 2:off + k - 2 + n]
                    if k == 0:
                        S.op("dve", lambda e: e.tensor_scalar(out=cv[:, 0:n], in0=src, scalar1=self.convw[:, ct, 0:1], scalar2=None, op0=ALU.mult),
                             reads=rb_reads + [self.b_convw], writes=[bcv])
                    else:
                        S.op("dve", lambda e: e.scalar_tensor_tensor(out=cv[:, 0:n], in0=src, scalar=self.convw[:, ct, k:k + 1], in1=cv[:, 0:n],
                                                                     op0=ALU.mult, op1=ALU.add),
                             reads=rb_reads + [self.b_convw, bcv], writes=[bcv])
                if g3 < 2:
                    S.op("act", lambda e: e.activation(out=self.slrow[:, t0:t0 + n], in_=cv[:, 0:n], func=AF.Silu), reads=[bcv], writes=[self.b_slrow.sub(t0)])
                else:
                    o, bo = self.tmo.next()
                    S.op("act", lambda e: e.activation(out=o[:, 0:n], in_=cv[:, 0:n], func=AF.Silu), reads=[bcv], writes=[bo])
                    f, bf = self.fmo.next()
                    self.transpose_out(lambda i: (o[:, i * 128:(i + 1) * 128], bo), ntile, 128, f[:, 0:n], [bf], "va")
                    S.dma("act", scr["VA_TM"][head].rearrange("(t p) c -> p t c", p=128)[:, tile0:tile0 + ntile, :],
                          f[:, 0:n].rearrange("p (t c) -> p t c", c=128), reads=[bf], writes=[self.db("VA_TM", tile0 + i) for i in range(ntile)])

            def pass2_piece(ct, gi):
                g3, head = ct // 4, ct % 4
                name = "QA_FM" if g3 == 0 else "KA_FM"
                (t0, n, tile0, ntile) = groups[gi]
                sq, bsq = self.tmo.next()
                S.op("act", lambda e: e.activation(out=sq[:, 0:n], in_=self.slrow[:, t0:t0 + n], func=AF.Square), reads=[self.b_slrow.sub(t0)], writes=[bsq])
                ps, bps = S.ps()
                S.op("pe", lambda e: e.matmul(out=ps[:, 0:n], lhsT=self.ones_b[:], rhs=sq[:, 0:n], start=True, stop=True),
                     reads=[self.b_ones_b, bsq], writes=[bps])
                ta, bta = self.tmpB.next()
                S.op("act", lambda e: e.activation(out=ta[:, 0:n], in_=ps[:, 0:n], func=AF.Ln, bias=self.epsb[:, 0:1]), reads=[bps, self.b_epsb], writes=[bta])
                S.op("act", lambda e: e.activation(out=ta[:, 0:n], in_=ta[:, 0:n], func=AF.Exp, scale=-0.5, bias=self.epsb[:, 1 + g3:2 + g3]),
                     reads=[bta, self.b_epsb], writes=[bta])
                o, bo = self.fmo.next()
                S.op("dve", lambda e: e.tensor_tensor(out=o[:, 0:n], in0=self.slrow[:, t0:t0 + n], in1=ta[:, 0:n], op=ALU.mult),
                     reads=[self.b_slrow.sub(t0), bta], writes=[bo])
                S.dma("act", scr[name][head][:, t0:t0 + n], o[:, 0:n], reads=[bo], writes=[self.db(name, tile0 + i) for i in range(ntile)])
                if g3 == 1:
                    f, bf = self.zt.next()
                    self.transpose_out(lambda i: (o[:, i * 128:(i + 1) * 128], bo), ntile, 128, f[:, 0:n], [bf], "ka")
                    S.dma("act", scr["KA_TM"][head].rearrange("(t p) c -> p t c", p=128)[:, tile0:tile0 + ntile, :],
                          f[:, 0:n].rearrange("p (t c) -> p t c", c=128), reads=[bf], writes=[self.db("KA_TM", tile0 + i) for i in range(ntile)])

            ng = len(groups)
            for gi in range(ng):
                proj_piece(0, gi)
                yield
            for ct in range(12):
                for gi in range(ng):
                    pass1_piece(ct, gi)
                    if ct + 1 < 12 and gi >= 1:
                        proj_piece(ct + 1, gi - 1)
                    yield
                if ct + 1 < 12:
                    proj_piece(ct + 1, ng - 1)
                    yield
                if ct < 8:
                    for gi in range(ng):
                        pass2_piece(ct, gi)
                        yield

        def merge_gen():
            for g6 in range(6):
                self.marks.append(("  L%d mg%d" % (l, g6), {k: v.count for k, v in S.engs.items()}))
                wb, bwb = wsM.get(O_MG + g6 * 512, 512)
                for cl in range(4):
                    ct = g6 * 4 + cl
                    for (t0, n, tile0, ntile) in groups:
                        if last and t0 == 0:
                            continue
                        ps, bps = S.ps()
                        for kc in range(8):
                            S.op("pe", lambda e: e.matmul(out=ps[:, 0:n], lhsT=wb[:, kc, cl * 128:(cl + 1) * 128], rhs=self.hT[:, kc, t0:t0 + n],
                                                          start=(kc == 0), stop=(kc == 7)),
                                 reads=[self.b_hT[tile0 + i] for i in range(ntile)] + [bwb], writes=[bps])
                        o, bo = self.fmo.next()
                        S.op("act", lambda e: e.activation(out=o[:, 0:n], in_=ps[:, 0:n], func=AF.Sigmoid), reads=[bps], writes=[bo])
                        S.dma("act", scr["GM_FM"][ct * 128:(ct + 1) * 128, t0:t0 + n], o[:, 0:n], reads=[bo],
                              writes=[self.db("GM_FM%d" % ct, tile0 + i) for i in range(ntile)])
                        yield
        self.run_pair(heavy(), merge_gen())
        self.run_pair(afm_gen(), light())
        if self.stop == "p2":
            self.dump_p2()
        S.barrier()
        self.es.close()

    def a_scalars(self, l):
        S = self.S
        A = self.asc
        braw = self.SCR[:, :, 0:8]
        araw = self.SCR[:, :, 8:16]
        rs = [self.b_SCR]
        one = self.epsb[:, 3:4]

        def softplus_parts(x_ap, xb, neg):
            t1, b1 = self.tmp8.next()
            t2, b2 = self.tmp8.next()
            S.op("act", lambda e: e.activation(out=t1[:], in_=x_ap, func=AF.Abs), reads=xb, writes=[b1])
            S.op("act", lambda e: e.activation(out=t1[:], in_=t1[:], func=AF.Exp, scale=-1.0), reads=[b1], writes=[b1])
            S.op("act", lambda e: e.activation(out=t1[:], in_=t1[:], func=AF.Ln, bias=one), reads=[b1, self.b_epsb], writes=[b1])
            S.op("dve", lambda e: e.tensor_scalar(out=t2[:], in0=x_ap, scalar1=(-1.0 if neg else 1.0), scalar2=0.0, op0=ALU.mult, op1=ALU.max),
                 reads=xb, writes=[b2])
            return (t2, b2), (t1, b1)

        (m, bm), (l1, bl1) = softplus_parts(braw, rs, True)
        LNB, bLNB = A["LNB"]
        S.op("dve", lambda e: e.scalar_tensor_tensor(out=LNB[:], in0=m[:], scalar=-1.0, in1=l1[:], op0=ALU.mult, op1=ALU.subtract),
             reads=[bm, bl1], writes=[bLNB])
        BETA, bBETA = A["BETA"]
        S.op("act", lambda e: e.activation(out=BETA[:], in_=LNB[:], func=AF.Exp), reads=[bLNB], writes=[bBETA])
        xa, bxa = self.tmp8.next()
        dtb, bdtb = self.par["a_dt_bias"]
        S.op("dve", lambda e: e.tensor_tensor(out=xa[:], in0=araw, in1=dtb[:].unsqueeze(1).to_broadcast([128, NT, 8]), op=ALU.add),
             reads=rs + [bdtb], writes=[bxa])
        (m2, bm2), (l2, bl2) = softplus_parts(xa[:], [bxa], False)
        S.op("dve", lambda e: e.tensor_tensor(out=m2[:], in0=m2[:], in1=l2[:], op=ALU.add), reads=[bm2, bl2], writes=[bm2])
        alog, balog = self.par["a_log"]
        nea, bnea = self.st8.next()
        S.op("act", lambda e: e.activation(out=nea[:, 0:8], in_=alog[:], func=AF.Exp), reads=[balog], writes=[bnea])
        GRAW, bGRAW = A["GRAW"]
        S.op("dve", lambda e: e.scalar_tensor_tensor(out=GRAW[:], in0=m2[:], scalar=-1.0, in1=nea[:, 0:8].unsqueeze(1).to_broadcast([128, NT, 8]),
                                                     op0=ALU.mult, op1=ALU.mult),
             reads=[bm2, bnea], writes=[bGRAW])
        GC, bGC = A["GC"]
        GT, bGT = A["GT"]
        if self.stop == "p2b2":
            return
        gflat = GRAW[:].rearrange("p t c -> p (t c)")
        res = []
        for lhs, blhs in ((self.tri[:, 0, :], self.b_tri), (self.tri[:, 1, :], self.b_tri), (self.ones_f[:], self.b_ones_f)):
            ps, bps = S.ps()
            S.op("pe", lambda e: e.matmul(out=ps[:, 0:NT * 8], lhsT=lhs, rhs=gflat, start=True, stop=True), reads=[blhs, bGRAW], writes=[bps])
            res.append((ps[:, 0:NT * 8].rearrange("p (t c) -> p t c", c=8), bps))
        S.op("dve", lambda e: e.tensor_copy(out=GC[:, :, 0:4], in_=res[0][0][:, :, 0:4]), reads=[res[0][1]], writes=[bGC])
        S.op("dve", lambda e: e.tensor_copy(out=GC[:, :, 4:8], in_=res[1][0][:, :, 4:8]), reads=[res[1][1]], writes=[bGC])
        S.op("act", lambda e: e.copy(out=GT[:], in_=res[2][0]), reads=[res[2][1]], writes=[bGT])
        NEGG, bNEGG = A["NEGG"]
        S.op("dve", lambda e: e.tensor_scalar(out=NEGG[:], in0=GC[:], scalar1=-1.0, scalar2=None, op0=ALU.mult), reads=[bGC], writes=[bNEGG])
        LNBMG, bLNBMG = A["LNBMG"]
        S.op("dve", lambda e: e.tensor_tensor(out=LNBMG[:], in0=LNB[:], in1=GC[:], op=ALU.subtract), reads=[bLNB, bGC], writes=[bLNBMG])
        NEGEG, bNEGEG = A["NEGEG"]
        S.op("act", lambda e: e.activation(out=NEGEG[:], in_=GC[:], func=AF.Exp), reads=[bGC], writes=[bNEGEG])
        S.op("dve", lambda e: e.tensor_scalar(out=NEGEG[:], in0=NEGEG[:], scalar1=-1.0, scalar2=None, op0=ALU.mult), reads=[bNEGEG], writes=[bNEGEG])
        ETAIL, bETAIL = A["ETAIL"]
        S.op("dve", lambda e: e.tensor_tensor(out=ETAIL[:], in0=GT[:], in1=GC[:], op=ALU.subtract), reads=[bGT, bGC], writes=[bETAIL])
        S.op("act", lambda e: e.activation(out=ETAIL[:], in_=ETAIL[:], func=AF.Exp), reads=[bETAIL], writes=[bETAIL])
        EGL, bEGL = A["EGL"]
        S.op("act", lambda e: e.activation(out=EGL[:], in_=GT[:], func=AF.Exp), reads=[bGT], writes=[bEGL])

    def core_b(self, l, defer=False):
        S, scr, din = self.S, self.scr, self.din
        last = (l == self.nlayers - 1)
        if not defer:
            self.es = ExitStack()
        KBT = self.R1[:, 0:2 * NTOK].rearrange("p (g t) -> p g t", g=2)
        VBR = self.R1[:, 2 * NTOK:2 * NTOK + NT * 130].rearrange("p (t g c) -> p t g c", g=2, c=65)
        bKBT, bVBR = Buf("KBT"), Buf("VBR")
        S.dma("sp", KBT, scr["KB_FM"].rearrange("g r t -> r g t"), reads=self.dball("KB_FM"), writes=[bKBT])
        S.dma("sp", VBR, scr["VB_TM"].rearrange("(t p) g c -> p t g c", p=128), reads=self.dball("VB_TM"), writes=[bVBR])
        if l == 0:
            bst, bbst = self.tsb("bmst", [128, 2, 512])
            S.dma("sp", bst[:], din["k_bmask"].rearrange("a p n -> p a n"), writes=[bbst])
            S.op("pool", lambda e: e.tensor_copy(out=self.bmask[:], in_=bst[:]), reads=[bbst], writes=[self.b_bmask])
        qTr = self.trot("b_qT", [128, 4, 128], BF16, 2)
        pTr = self.trot("b_pT", [128, 5, 512], BF16, 2)
        zbr = self.trot("b_zb", [128, 512], BF16, 2)
        ybr = self.trot("b_yb", [128, 512], BF16, 2)
        obr = self.trot("b_ob", [128, 256], F32, 2)
        str_ = self.trot("b_st", [128, 16], F32, 6)
        fmo = self.trot("b_fmo", [128, 512], BF16, 2)
        qtiles = list(range(2, NT)) if last else list(range(NT))
        def b_gen():
            for qt in qtiles:
                zb, bzb = zbr.next()
                S.dma("sp", zb[:], scr["ZB"][qt * 128:(qt + 1) * 128, :], reads=[self.db("ZB", qt)], writes=[bzb])
                yb, byb = ybr.next()
                for g in range(2):
                    qT, bqT = qTr.next()
                    S.dma("sp", qT[:], scr["QB_FM"][g * 4:(g + 1) * 4].rearrange("h r t -> r h t")[:, :, qt * 128:(qt + 1) * 128],
                          reads=[self.db("QB_FM", qt)], writes=[bqT])
                    keys = [(0, None), (1, None)]
                    if qt >= 2:
                        if qt - 1 >= 2:
                            keys.append((qt - 1, 0))
                        keys.append((qt, None))
                        if qt + 1 < NT:
                            keys.append((qt + 1, 1))
                    pT, bpT = pTr.next()
                    for idx, (kt, m) in enumerate(keys):
                        ps, bps = S.ps()
                        S.op("pe", lambda e: e.matmul(out=ps[:], lhsT=KBT[:, g, kt * 128:(kt + 1) * 128], rhs=qT[:].rearrange("p h t -> p (h t)"),
                                                      start=True, stop=True), reads=[bKBT, bqT], writes=[bps])
                        S.op("act", lambda e: e.activation(out=pT[:, idx, :], in_=ps[:], func=AF.Exp, scale=0.125), reads=[bps], writes=[bpT.sub(idx)])
                        if m is not None:
                            S.op("pool", lambda e: e.tensor_tensor(out=pT[:, idx, :], in0=pT[:, idx, :], in1=self.bmask[:, m, :], op=ALU.mult),
                                 reads=[bpT.sub(idx), self.b_bmask], writes=[bpT.sub(idx)])
                    po, bpo = S.ps()
                    for h in range(4):
                        for idx, (kt, m) in enumerate(keys):
                            S.op("pe", lambda e: e.matmul(out=po[:, h * 65:(h + 1) * 65], lhsT=pT[:, idx, h * 128:(h + 1) * 128], rhs=VBR[:, kt, g, :],
                                                          start=(idx == 0), stop=(idx == len(keys) - 1)), reads=[bpT.sub(idx), bVBR], writes=[bpo])
                    st, bst_ = str_.next()
                    pov = po[:, 0:260].rearrange("p (h c) -> p h c", c=65)
                    S.op("act", lambda e: e.activation(out=st[:, 0:4], in_=self.SH2[:, qt, g * 4:(g + 1) * 4], func=AF.Exp), reads=[self.b_SH2], writes=[bst_])
                    S.op("dve", lambda e: e.tensor_tensor(out=st[:, 4:8], in0=pov[:, :, 64], in1=st[:, 0:4], op=ALU.add), reads=[bpo, bst_], writes=[bst_])
                    S.op("dve", lambda e: e.reciprocal(out=st[:, 8:12], in_=st[:, 4:8]), reads=[bst_], writes=[bst_])
                    ob, bob = obr.next()
                    S.op("dve", lambda e: e.tensor_tensor(out=ob[:].rearrange("p (h c) -> p h c", c=64), in0=pov[:, :, 0:64],
                                                          in1=st[:, 8:12].unsqueeze(2).to_broadcast([128, 4, 64]), op=ALU.mult),
                         reads=[bpo, bst_], writes=[bob])
                    S.op("pool", lambda e: e.tensor_tensor(out=yb[:, g * 256:(g + 1) * 256], in0=ob[:], in1=zb[:, g * 256:(g + 1) * 256], op=ALU.mult),
                         reads=[bob, bzb], writes=[byb.sub(g)])
                    yield
                self.y_out(1, qt, yb, byb, fmo)
                yield
        if defer:
            return b_gen()
        for _ in b_gen():
            pass
        S.barrier()
        self.es.close()

    def core_bc(self, l):
        self.es = ExitStack()
        gb = self.core_b(l, defer=True)
        gc = self.core_c(l, defer=True)
        self.run_pair(gb, gc)
        self.S.barrier()
        self.es.close()

    def y_out(self, br, tt, y, by, fmo, pool=None):
        S = self.S
        f, bf = fmo.next()
        self.transpose_out(lambda i: (y[:, i * 128:(i + 1) * 128], by), 4, 128, f[:], [bf], "y", pool=pool)
        S.dma("act", self.scr["Y_FM"][br].rearrange("(k p) t -> p k t", p=128)[:, :, tt * 128:(tt + 1) * 128],
              f[:].rearrange("p (k t) -> p k t", k=4), reads=[bf], writes=[self.db("Y_FM%d" % br, tt)])

    def core_c(self, l, defer=False):
        S, scr = self.S, self.scr
        last = (l == self.nlayers - 1)
        if not defer:
            self.es = ExitStack()
        one = self.epsb[:, 3:4]
        cd, bcd = self.par["c_decay"]
        c8 = self.trot("c_c8", [128, 8], F32, 6)
        t1, b1 = c8.next()
        t2, b2 = c8.next()
        LG, bLG = c8.next()
        S.op("act", lambda e: e.activation(out=t1[:], in_=cd[:], func=AF.Abs), reads=[bcd], writes=[b1])
        S.op("act", lambda e: e.activation(out=t1[:], in_=t1[:], func=AF.Exp, scale=-1.0), reads=[b1], writes=[b1])
        S.op("act", lambda e: e.activation(out=t1[:], in_=t1[:], func=AF.Ln, bias=one), reads=[b1, self.b_epsb], writes=[b1])
        S.op("dve", lambda e: e.tensor_scalar(out=t2[:], in0=cd[:], scalar1=-1.0, scalar2=0.0, op0=ALU.mult, op1=ALU.max), reads=[bcd], writes=[b2])
        S.op("dve", lambda e: e.scalar_tensor_tensor(out=LG[:], in0=t2[:], scalar=-1.0, in1=t1[:], op0=ALU.mult, op1=ALU.subtract),
             reads=[b1, b2], writes=[bLG])
        GAMC, bGAMC = c8.next()
        S.op("act", lambda e: e.activation(out=GAMC[:], in_=LG[:], func=AF.Exp, scale=float(CH)), reads=[bLG], writes=[bGAMC])
        KDEC, bKDEC = c8.next()
        S.op("dve", lambda e: e.tensor_tensor(out=KDEC[:], in0=LG[:], in1=self.cj[:], op=ALU.mult), reads=[bLG, self.b_cj], writes=[bKDEC])
        S.op("act", lambda e: e.activation(out=KDEC[:], in_=KDEC[:], func=AF.Exp), reads=[bKDEC], writes=[bKDEC])
        DM, bDM = self.tsb("c_DM", [128, 512])
        QDF, bQDF = self.tsb("c_QDF", [128, 512], BF16)
        QDB, bQDB = self.tsb("c_QDB", [128, 512], BF16)
        tm = self.trot("c_tm", [128, 128], F32, 2)
        for h in range(4):
            ta, bta = tm.next()
            tb, btb = tm.next()
            S.op("act", lambda e: e.activation(out=ta[:], in_=self.cm[:, 0, :], func=AF.Exp, scale=LG[:, h:h + 1]), reads=[self.b_cm, bLG], writes=[bta])
            S.op("dve", lambda e: e.tensor_tensor(out=ta[:], in0=ta[:], in1=self.cm[:, 2, :], op=ALU.mult), reads=[bta, self.b_cm], writes=[bta])
            S.op("act", lambda e: e.activation(out=tb[:], in_=self.cm[:, 1, :], func=AF.Exp, scale=LG[:, 4 + h:5 + h]), reads=[self.b_cm, bLG], writes=[btb])
            S.op("dve", lambda e: e.tensor_tensor(out=tb[:], in0=tb[:], in1=self.cm[:, 3, :], op=ALU.mult), reads=[btb, self.b_cm], writes=[btb])
            S.op("dve", lambda e: e.tensor_tensor(out=ta[:], in0=ta[:], in1=tb[:], op=ALU.add), reads=[bta, btb], writes=[bta])
            S.op("dve", lambda e: e.scalar_tensor_tensor(out=DM[:, h * 128:(h + 1) * 128], in0=self.ident_f[:], scalar=2.0, in1=ta[:], op0=ALU.mult, op1=ALU.add),
                 reads=[bta, self.b_ident_f], writes=[bDM])
            S.op("act", lambda e: e.activation(out=QDF[:, h * 128:(h + 1) * 128], in_=self.cm[:, 4, :], func=AF.Exp, scale=LG[:, h:h + 1]),
                 reads=[self.b_cm, bLG], writes=[bQDF])
            S.op("act", lambda e: e.activation(out=QDB[:, h * 128:(h + 1) * 128], in_=self.cm[:, 5, :], func=AF.Exp, scale=LG[:, 4 + h:5 + h]),
                 reads=[self.b_cm, bLG], writes=[bQDB])
        kTMr = [self.trot("c_kTM%d" % d, [128, 512], BF16, 2) for d in range(2)]
        vr = [self.trot("c_v%d" % d, [128, 512], BF16, 2) for d in range(3)]
        kdr = [self.trot("c_kd%d" % d, [128, 512], BF16, 2) for d in range(2)]
        sbfr = [self.trot("c_sbf%d" % d, [128, 512], BF16, 2) for d in range(2)]
        S32 = [self.tsb("c_S32_%d" % d, [128, 512]) for d in range(2)]
        SCN = ["SCF", "SCB"]

        def state_update(d, cc, kTM, bkTM, v, bv):
            kd, bkd = kdr[d].next()
            for h in range(4):
                S.op("act", lambda e: e.activation(out=kd[:, h * 128:(h + 1) * 128], in_=kTM[:, h * 128:(h + 1) * 128], func=AF.Copy,
                                                   scale=KDEC[:, d * 4 + h:d * 4 + h + 1]),
                     reads=[bkTM, bKDEC], writes=[bkd.sub(h)])
            ps, bps = S.ps()
            for h in range(4):
                S.op("pe", lambda e: e.matmul(out=ps[:, h * 128:(h + 1) * 128], lhsT=kd[:, h * 128:(h + 1) * 128], rhs=v[:, h * 128:(h + 1) * 128],
                                              start=True, stop=True), reads=[bkd, bv], writes=[bps])
            s32, bs32 = S32[d]
            for h in range(4):
                S.op("dve", lambda e: e.scalar_tensor_tensor(out=s32[:, h * 128:(h + 1) * 128], in0=s32[:, h * 128:(h + 1) * 128],
                                                             scalar=GAMC[:, d * 4 + h:d * 4 + h + 1], in1=ps[:, h * 128:(h + 1) * 128],
                                                             op0=ALU.mult, op1=ALU.add),
                     reads=[bs32.sub(h), bGAMC, bps], writes=[bs32.sub(h)])

        def state_pass(d):
            S.op("pool", lambda e: e.memset(S32[d][0][:], 0.0), writes=[S32[d][1]])
            order = list(range(NT)) if d == 0 else [1, 0] + list(range(NT - 1, 1, -1))
            for cc in order:
                sbf, bsbf = sbfr[d].next()
                S.op("act", lambda e: e.copy(out=sbf[:], in_=S32[d][0][:]), reads=[S32[d][1]], writes=[bsbf])
                S.dma("act", scr[SCN[d]][cc], sbf[:], reads=[bsbf], writes=[self.db(SCN[d], cc)])
                kTM, bkTM = kTMr[d].next()
                v, bv = vr[d].next()
                S.dma("sp", kTM[:], scr["KC_TM"][cc * 128:(cc + 1) * 128, :], reads=[self.db("KC_TM", cc)], writes=[bkTM])
                S.dma("sp", v[:], scr["VC_TM"][cc * 128:(cc + 1) * 128, :], reads=[self.db("VC_TM", cc)], writes=[bv])
                yield
                state_update(d, cc, kTM, bkTM, v, bv)
                yield

        qTr = self.trot("c_qT", [128, 512], BF16, 2)
        kTr = self.trot("c_kT", [128, 512], BF16, 2)
        sbr = self.trot("c_sb", [128, 512], BF16, 2)
        sfr = self.trot("c_sf", [128, 512], BF16, 2)
        zcr = self.trot("c_zc", [128, 512], BF16, 2)
        qkr = self.trot("c_qk", [128, 512], BF16, 2)
        qdr = self.trot("c_qd", [128, 512], BF16, 2)
        ofr = self.trot("c_of", [128, 512], F32, 2)
        t5r = self.trot("c_t5", [128, 512], F32, 1)
        ycr = self.trot("c_yc", [128, 512], BF16, 2)
        stc = self.trot("c_st", [128, 24], F32, 3)
        fmo = self.trot("c_fmo", [128, 512], BF16, 2)
        cnw, bcnw = self.par["c_norm_w"]
        eps = self.epsb[:, 0:1]
        def out_pass():
            for cc in range(NT):
                need_out = (cc >= 2) or (not last)
                if need_out:
                    v, bv = vr[2].next()
                    S.dma("sp", v[:], scr["VC_TM"][cc * 128:(cc + 1) * 128, :], reads=[self.db("VC_TM", cc)], writes=[bv])
                    qT, bqT = qTr.next()
                    kT, bkT = kTr.next()
                    sb, bsb = sbr.next()
                    zc, bzc = zcr.next()
                    S.dma("sp", qT[:].rearrange("p (h t) -> p h t", h=4), scr["QC_FM"].rearrange("h p t -> p h t")[:, :, cc * 128:(cc + 1) * 128],
                          reads=[self.db("QC_FM", cc)], writes=[bqT])
                    S.dma("sp", kT[:].rearrange("p (h t) -> p h t", h=4), scr["KC_FM"].rearrange("h p t -> p h t")[:, :, cc * 128:(cc + 1) * 128],
                          reads=[self.db("KC_FM", cc)], writes=[bkT])
                    S.dma("sp", sb[:], scr["SCB"][cc], reads=[self.db("SCB", cc)], writes=[bsb])
                    S.dma("sp", zc[:], scr["ZC"][cc * 128:(cc + 1) * 128, :], reads=[self.db("ZC", cc)], writes=[bzc])
                    sbf, bsbf = sfr.next()
                    S.dma("sp", sbf[:], scr["SCF"][cc], reads=[self.db("SCF", cc)], writes=[bsbf])
                    ps1, bps1 = S.ps()
                    for h in range(4):
                        S.op("pe", lambda e: e.matmul(out=ps1[:, h * 128:(h + 1) * 128], lhsT=kT[:, h * 128:(h + 1) * 128], rhs=qT[:, h * 128:(h + 1) * 128],
                                                      start=True, stop=True), reads=[bkT, bqT], writes=[bps1])
                    qk, bqk = qkr.next()
                    S.op("dve", lambda e: e.tensor_tensor(out=qk[:], in0=ps1[:], in1=DM[:], op=ALU.mult), reads=[bps1, bDM], writes=[bqk])
                    qdf, bqdf = qdr.next()
                    qdb, bqdb = qdr.next()
                    S.op("pool", lambda e: e.tensor_tensor(out=qdf[:], in0=qT[:], in1=QDF[:], op=ALU.mult), reads=[bqT, bQDF], writes=[bqdf])
                    S.op("pool", lambda e: e.tensor_tensor(out=qdb[:], in0=qT[:], in1=QDB[:], op=ALU.mult), reads=[bqT, bQDB], writes=[bqdb])
                    po, bpo = S.ps()
                    for h in range(4):
                        sl = slice(h * 128, (h + 1) * 128)
                        S.op("pe", lambda e: e.matmul(out=po[:, sl], lhsT=qk[:, sl], rhs=v[:, sl], start=True, stop=False), reads=[bqk, bv], writes=[bpo])
                        S.op("pe", lambda e: e.matmul(out=po[:, sl], lhsT=qdf[:, sl], rhs=sbf[:, sl], start=False, stop=False), reads=[bqdf, bsbf], writes=[bpo])
                        S.op("pe", lambda e: e.matmul(out=po[:, sl], lhsT=qdb[:, sl], rhs=sb[:, sl], start=False, stop=True), reads=[bqdb, bsb], writes=[bpo])
                    of, bof = ofr.next()
                    t5, bt5 = t5r.next()
                    st, bst = stc.next()
                    S.op("act", lambda e: e.copy(out=of[:], in_=po[:]), reads=[bpo], writes=[bof])
                    S.op("act", lambda e: e.activation(out=t5[:], in_=po[:], func=AF.Square), reads=[bpo], writes=[bt5])
                    S.op("dve", lambda e: e.tensor_reduce(out=st[:, 0:4], in_=of[:].rearrange("p (h c) -> p h c", h=4), axis=AX.X, op=ALU.add), reads=[bof], writes=[bst])
                    S.op("dve", lambda e: e.tensor_reduce(out=st[:, 4:8], in_=t5[:].rearrange("p (h c) -> p h c", h=4), axis=AX.X, op=ALU.add), reads=[bt5], writes=[bst])
                    S.op("dve", lambda e: e.tensor_scalar(out=st[:, 8:12], in0=st[:, 0:4], scalar1=1.0 / 128, scalar2=None, op0=ALU.mult), reads=[bst], writes=[bst])
                    S.op("dve", lambda e: e.tensor_tensor(out=st[:, 12:16], in0=st[:, 8:12], in1=st[:, 8:12], op=ALU.mult), reads=[bst], writes=[bst])
                    S.op("dve", lambda e: e.scalar_tensor_tensor(out=st[:, 16:20], in0=st[:, 4:8], scalar=1.0 / 128, in1=st[:, 12:16], op0=ALU.mult, op1=ALU.subtract),
                         reads=[bst], writes=[bst])
                    S.op("act", lambda e: e.activation(out=st[:, 20:24], in_=st[:, 16:20], func=AF.Ln, bias=eps), reads=[bst, self.b_epsb], writes=[bst])
                    S.op("act", lambda e: e.activation(out=st[:, 20:24], in_=st[:, 20:24], func=AF.Exp, scale=-0.5), reads=[bst], writes=[bst])
                    ofv = of[:].rearrange("p (h c) -> p h c", h=4)
                    S.op("dve", lambda e: e.tensor_tensor(out=ofv, in0=ofv, in1=st[:, 8:12].unsqueeze(2).to_broadcast([128, 4, 128]), op=ALU.subtract),
                         reads=[bof, bst], writes=[bof])
                    S.op("dve", lambda e: e.tensor_tensor(out=ofv, in0=ofv, in1=st[:, 20:24].unsqueeze(2).to_broadcast([128, 4, 128]), op=ALU.mult),
                         reads=[bof, bst], writes=[bof])
                    S.op("pool", lambda e: e.tensor_tensor(out=of[:], in0=of[:], in1=cnw[:], op=ALU.mult), reads=[bof, bcnw], writes=[bof])
                    yc, byc = ycr.next()
                    S.op("pool", lambda e: e.tensor_tensor(out=yc[:], in0=of[:], in1=zc[:], op=ALU.mult), reads=[bof, bzc], writes=[byc])
                    self.y_out(2, cc, yc, byc, fmo)
                yield

        def c_gen():
            gens = [state_pass(0), state_pass(1)]
            while gens:
                for g_ in list(gens):
                    try:
                        next(g_)
                        yield
                    except StopIteration:
                        gens.remove(g_)
            yield from out_pass()
        if defer:
            return c_gen()
        for _ in c_gen():
            pass
        S.barrier()
        self.es.close()

    def core_a(self, l):
        S, scr = self.S, self.scr
        last = (l == self.nlayers - 1)
        self.es = ExitStack()
        A = self.asc
        import os
        KPRE = int(os.environ.get("KPRE", "2"))
        r1_next = [0]

        def mkrot(name, k, use_r1=True):
            items = []
            for i in range(k):
                if use_r1 and r1_next[0] < 68:
                    j = r1_next[0]
                    r1_next[0] += 1
                    items.append((self.R1[:, j * 512:(j + 1) * 512], Buf("%s%d" % (name, i))))
                else:
                    t = self._talloc("a_" + name, [128, 512], BF16)
                    items.append((t[:], Buf("%s%d" % (name, i))))
            r = Rot.__new__(Rot)
            r.items = items
            r.i = 0
            return r

        def R(n, dt=BF16, k=2):
            r = self.trot("a_" + n, [128, 512], dt, k)
            r.items = [(t[:], b) for t, b in r.items]
            return r
        ofr, t5r = R("of", F32, 3), R("t5", F32, 2)
        zar, yar, fmo = R("za"), R("ya"), R("fmo")
        sta = self.trot("a_st", [128, 16], F32, 4)
        nmask, bnmask = self.tsb("a_nmask", [128, 14, 128], BF16)
        nmst, bnmst = self.wst.next()
        nmv = nmst[:].rearrange("p k n -> p (k n)")[:, 0:14 * 128].rearrange("p (a n) -> p a n", a=14)
        S.dma("sp", nmv, self.din["k_nm"].rearrange("a p n -> p a n"), writes=[bnmst])
        S.op("pool", lambda e: e.tensor_copy(out=nmask[:], in_=nmv), reads=[bnmst], writes=[bnmask])
        anw, banw = self.par["a_norm_w"]
        eps = self.epsb[:, 0:1]
        H = [slice(h * 128, (h + 1) * 128) for h in range(4)]
        v4 = lambda t: t[:].rearrange("p (h c) -> p h c", h=4)
        orders = [list(range(NT)), [1, 0] + list(range(NT - 1, 1, -1))]
        oa_written = set()
        from collections import deque
        free_banks = deque(range(8))

        def acq():
            while not free_banks:
                yield
            bk = free_banks.popleft()
            ps, bps = S.psum[bk]
            return ps, bps, bk

        def rel(bk):
            free_banks.append(bk)

        def mm4(lhs, blhs, rhs, brhs):
            ps, bps, bk = yield from acq()
            for h in range(4):
                S.op("pe", lambda e: e.matmul(out=ps[:, H[h]], lhsT=lhs[:, H[h]], rhs=rhs[:, H[h]], start=True, stop=True), reads=[blhs, brhs], writes=[bps])
            return ps, bps, bk

        def tr4(src, bsrc):
            ps, bps, bk = yield from acq()
            psb = ps[:].bitcast(BF16)
            for h in range(4):
                S.op("pe", lambda e: e.transpose(out=psb[:, H[h]], in_=src[:, H[h]], identity=self.ident_b[:]), reads=[bsrc, self.b_ident_b], writes=[bps])
            return psb, bps, bk

        class DirBufs:
            pass
        DB = []
        for d in range(2):
            o = DirBufs()
            for n in ("kT", "kTM", "vTM", "qT", "qk", "qd", "kt", "Pf"):
                setattr(o, n, mkrot("%s_%d" % (n, d), KPRE + 1))
            o.tsets = []
            for ts in range(KPRE):
                tsd = {n: mkrot("%s_%d_%d" % (n, d, ts), 1).items[0] for n in ("eg", "Ma", "Ml", "Pa", "Pb", "W1", "X", "MlmA", "MlmB")}
                for n in ("F0", "F1", "F2"):
                    tsd[n] = (self._talloc("a_%s_%d_%d" % (n, d, ts), [128, 512], F32)[:], Buf("%s_%d_%d" % (n, d, ts)))
                o.tsets.append(tsd)
            for n in ("Y", "vn", "sbf"):
                setattr(o, n, R("%s_%d" % (n, d)))
            o.S32 = self.tsb("a_S32_%d" % d, [128, 512])
            DB.append(o)

        def prep(d, cc, out, ts):
            B = DB[d]
            TS = B.tsets[ts]
            need_out = (cc >= 2) or (not last)
            out["need_out"] = need_out
            col = lambda name, h: A[name][0][:, cc, d * 4 + h:d * 4 + h + 1]
            tok = slice(cc * 128, (cc + 1) * 128)
            m_incl = self.amask[:, 2 * d, :]
            m_strict = self.amask[:, 2 * d + 1, :]
            nmb = lambda lev: nmask[:, d * 7 + lev, :].unsqueeze(1).to_broadcast([128, 4, 128])
            kT, bkT = B.kT.next()
            kTM, bkTM = B.kTM.next()
            vTM, bvTM = B.vTM.next()
            S.dma("sp", kT.rearrange("p (h t) -> p h t", h=4), scr["KA_FM"].rearrange("h p t -> p h t")[:, :, tok], reads=[self.db("KA_FM", cc)], writes=[bkT])
            S.dma("sp", kTM.rearrange("p (h c) -> p h c", h=4), scr["KA_TM"].rearrange("h t c -> t h c")[tok, :, :], reads=[self.db("KA_TM", cc)], writes=[bkTM])
            S.dma("sp", vTM.rearrange("p (h c) -> p h c", h=4), scr["VA_TM"].rearrange("h t c -> t h c")[tok, :, :], reads=[self.db("VA_TM", cc)], writes=[bvTM])
            out.update(kT=(kT, bkT), vTM=(vTM, bvTM))
            if need_out:
                qT, bqT = B.qT.next()
                S.dma("sp", qT.rearrange("p (h t) -> p h t", h=4), scr["QA_FM"].rearrange("h p t -> p h t")[:, :, tok], reads=[self.db("QA_FM", cc)], writes=[bqT])
            yield
            dg, bdg = TS["F0"]
            for h in range(4):
                S.op("dve", lambda e: e.tensor_scalar(out=dg[:, H[h]], in0=self.ident_f[:], scalar1=col("GC", h), scalar2=None, op0=ALU.mult),
                     reads=[self.b_ident_f, A["GC"][1]], writes=[bdg.sub(h)])
            p3, bp3, k3 = yield from acq()
            S.op("pe", lambda e: e.matmul(out=p3[:], lhsT=self.ones_f[:], rhs=dg[:], start=True, stop=True), reads=[self.b_ones_f, bdg], writes=[bp3])
            yield
            Dm2, bDm2 = TS["F1"]
            for h in range(4):
                S.op("dve", lambda e: e.scalar_tensor_tensor(out=Dm2[:, H[h]], in0=p3[:, H[h]], scalar=col("LNBMG", h), in1=m_strict, op0=ALU.add, op1=ALU.add),
                     reads=[bp3, A["LNBMG"][1], self.b_amask], writes=[bDm2.sub(h)])
            if need_out:
                Dm, bDm = TS["F2"]
                for h in range(4):
                    S.op("dve", lambda e: e.scalar_tensor_tensor(out=Dm[:, H[h]], in0=p3[:, H[h]], scalar=col("NEGG", h), in1=m_incl, op0=ALU.add, op1=ALU.add),
                         reads=[bp3, A["NEGG"][1], self.b_amask], writes=[bDm.sub(h)])
                eg, beg = TS["eg"]
                S.op("act", lambda e: e.activation(out=eg, in_=p3[:], func=AF.Exp), reads=[bp3, bDm, bDm2], writes=[beg])
            rel(k3)
            yield
            decb, bdecb = TS["F0"]
            S.op("act", lambda e: e.activation(out=decb[:], in_=Dm2[:], func=AF.Exp), reads=[bDm2], writes=[bdecb])
            p1, bp1, k1 = yield from mm4(kT, bkT, kT, bkT)
            yield
            Ma, bMa = TS["Ma"]
            S.op("dve", lambda e: e.tensor_tensor(out=Ma, in0=p1[:], in1=decb[:], op=ALU.mult), reads=[bp1, bdecb], writes=[bMa])
            rel(k1)
            yield
            psb, bps, kb = yield from tr4(Ma, bMa)
            nml = lambda lev: nmask[:, (1 - d) * 7 + lev, :].unsqueeze(1).to_broadcast([128, 4, 128])
            mlm = lambda lev: TS["MlmA" if lev % 2 else "MlmB"]
            S.op("dve", lambda e: e.tensor_tensor(out=v4(mlm(1)[0]), in0=psb[:, 0:512].rearrange("p (h c) -> p h c", h=4), in1=nml(1), op=ALU.mult),
                 reads=[bps, bnmask], writes=[mlm(1)[1]])
            Ml, bMl = TS["Ml"]
            S.op("act", lambda e: e.copy(out=Ml, in_=psb[:, 0:512]), reads=[bps, mlm(1)[1]], writes=[bMl])
            rel(kb)
            P, bP = TS["Pa"]
            S.op("pool", lambda e: e.tensor_tensor(out=v4(P), in0=v4(Ma), in1=nmb(0), op=ALU.mult), reads=[bMa, bnmask], writes=[bP])
            S.op("pool", lambda e: e.tensor_tensor(out=v4(P), in0=v4(P), in1=self.ident_b[:].unsqueeze(1).to_broadcast([128, 4, 128]), op=ALU.add),
                 reads=[bP, self.b_ident_b], writes=[bP])
            yield
            if need_out:
                dec, bdec = TS["F1"]
                S.op("act", lambda e: e.activation(out=dec[:], in_=Dm[:], func=AF.Exp), reads=[bDm], writes=[bdec])
                p2, bp2, k2 = yield from mm4(kT, bkT, qT, bqT)
                yield
                qk, bqk = B.qk.next()
                S.op("dve", lambda e: e.tensor_tensor(out=qk, in0=p2[:], in1=dec[:], op=ALU.mult), reads=[bp2, bdec], writes=[bqk])
                rel(k2)
                qd, bqd = B.qd.next()
                S.op("pool", lambda e: e.tensor_tensor(out=qd, in0=qT, in1=eg, op=ALU.mult), reads=[bqT, beg], writes=[bqd])
                out.update(qk=(qk, bqk), qd=(qd, bqd))
                yield
            kt, bkt = B.kt.next()
            for h in range(4):
                S.op("act", lambda e: e.activation(out=kt[:, H[h]], in_=kTM[:, H[h]], func=AF.Copy, scale=col("ETAIL", h)),
                     reads=[bkTM, A["ETAIL"][1]], writes=[bkt.sub(h)])
            out.update(kt=(kt, bkt))
            yield
            for lev in range(1, 7):
                cur, bcur = mlm(lev)
                psw, bpsw, kw = yield from mm4(cur, bcur, P, bP)
                psb, bps, kb = yield from tr4(P, bP)
                if lev < 6:
                    nxt_, bnxt_ = mlm(lev + 1)
                    S.op("pool", lambda e: e.tensor_tensor(out=v4(nxt_), in0=v4(Ml), in1=nml(lev + 1), op=ALU.mult), reads=[bMl, bnmask], writes=[bnxt_])
                yield
                W1, bW1 = TS["W1"]
                S.op("act", lambda e: e.copy(out=W1, in_=psw[:]), reads=[bpsw], writes=[bW1])
                rel(kw)
                X, bX = TS["X"]
                S.op("dve", lambda e: e.tensor_copy(out=X, in_=psb[:, 0:512]), reads=[bps], writes=[bX])
                rel(kb)
                yield
                ps2, bps2, k2 = yield from mm4(X, bX, W1, bW1)
                yield
                Pn, bPn = (B.Pf.next() if lev == 6 else TS["Pb" if lev % 2 == 1 else "Pa"])
                S.op("dve", lambda e: e.tensor_tensor(out=Pn, in0=ps2[:], in1=P, op=ALU.add), reads=[bps2, bP], writes=[bPn])
                rel(k2)
                P, bP = Pn, bPn
                yield
            out.update(P=(P, bP))

        def scan(d, cc, ops, st):
            B = DB[d]
            need_out = ops["need_out"]
            col = lambda name, h: A[name][0][:, cc, d * 4 + h:d * 4 + h + 1]
            tok = slice(cc * 128, (cc + 1) * 128)
            kT, bkT = ops["kT"]
            vTM, bvTM = ops["vTM"]
            kt, bkt = ops["kt"]
            P, bP = ops["P"]
            sbf, bsbf = st["sbf"]
            s32, bs32 = B.S32
            px, bpx, kx = yield from mm4(kT, bkT, sbf, bsbf)
            yield
            Y, bY = B.Y.next()
            for h in range(4):
                S.op("dve", lambda e: e.scalar_tensor_tensor(out=Y[:, H[h]], in0=px[:, H[h]], scalar=col("NEGEG", h), in1=vTM[:, H[h]], op0=ALU.mult, op1=ALU.add),
                     reads=[bpx, A["NEGEG"][1], bvTM], writes=[bY.sub(h)])
            rel(kx)
            yield
            pz, bpz, kz = yield from mm4(P, bP, Y, bY)
            yield
            vn, bvn = B.vn.next()
            for h in range(4):
                S.op("act", lambda e: e.activation(out=vn[:, H[h]], in_=pz[:, H[h]], func=AF.Copy, scale=col("BETA", h)), reads=[bpz, A["BETA"][1]], writes=[bvn.sub(h)])
            rel(kz)
            yield
            pS, bpS, kS = yield from mm4(kt, bkt, vn, bvn)
            if need_out:
                qk, bqk = ops["qk"]
                qd, bqd = ops["qd"]
                po, bpo, ko = yield from acq()
                for h in range(4):
                    S.op("pe", lambda e: e.matmul(out=po[:, H[h]], lhsT=qd[:, H[h]], rhs=sbf[:, H[h]], start=True, stop=False), reads=[bqd, bsbf], writes=[bpo])
                    S.op("pe", lambda e: e.matmul(out=po[:, H[h]], lhsT=qk[:, H[h]], rhs=vn[:, H[h]], start=False, stop=True), reads=[bqk, bvn], writes=[bpo])
            yield
            for h in range(4):
                S.op("dve", lambda e: e.scalar_tensor_tensor(out=s32[:, H[h]], in0=s32[:, H[h]], scalar=col("EGL", h), in1=pS[:, H[h]], op0=ALU.mult, op1=ALU.add),
                     reads=[bs32.sub(h), A["EGL"][1], bpS], writes=[bs32.sub(h)])
            rel(kS)
            sbf2, bsbf2 = B.sbf.next()
            S.op("act", lambda e: e.copy(out=sbf2, in_=s32[:]), reads=[bs32], writes=[bsbf2])
            st["sbf"] = (sbf2, bsbf2)
            yield
            if need_out:
                first = cc not in oa_written
                oa_written.add(cc)
                of, bof = ofr.next()
                if first:
                    S.op("act", lambda e: e.copy(out=of[:], in_=po[:]), reads=[bpo], writes=[bof])
                    rel(ko)
                    S.dma("act", scr["OA"][tok, :], of[:], reads=[bof], writes=[self.db("OA", cc)])
                    yield
                else:
                    S.dma("sp", of[:], scr["OA"][tok, :], reads=[self.db("OA", cc)], writes=[bof])
                    za, bza = zar.next()
                    S.dma("sp", za, scr["ZA"][tok, :], reads=[self.db("ZA", cc)], writes=[bza])
                    yield
                    S.op("dve", lambda e: e.tensor_tensor(out=of[:], in0=po[:], in1=of[:], op=ALU.add), reads=[bpo, bof], writes=[bof])
                    rel(ko)
                    t5, bt5 = t5r.next()
                    st_, bst = sta.next()
                    S.op("act", lambda e: e.activation(out=t5[:], in_=of[:], func=AF.Square), reads=[bof], writes=[bt5])
                    yield
                    S.op("dve", lambda e: e.tensor_reduce(out=st_[:, 0:4], in_=t5[:].rearrange("p (h c) -> p h c", h=4), axis=AX.X, op=ALU.add), reads=[bt5], writes=[bst])
                    S.op("act", lambda e: e.activation(out=st_[:, 4:8], in_=st_[:, 0:4], func=AF.Ln, scale=1.0 / 128, bias=eps), reads=[bst, self.b_epsb], writes=[bst])
                    S.op("act", lambda e: e.activation(out=st_[:, 8:12], in_=st_[:, 4:8], func=AF.Exp, scale=-0.5), reads=[bst], writes=[bst])
                    yield
                    ofv = of[:].rearrange("p (h c) -> p h c", h=4)
                    S.op("dve", lambda e: e.tensor_tensor(out=ofv, in0=ofv, in1=st_[:, 8:12].unsqueeze(2).to_broadcast([128, 4, 128]), op=ALU.mult), reads=[bof, bst], writes=[bof])
                    S.op("pool", lambda e: e.tensor_tensor(out=ofv, in0=ofv, in1=anw[:].unsqueeze(1).to_broadcast([128, 4, 128]), op=ALU.mult), reads=[bof, banw], writes=[bof])
                    yield
                    ya, bya = yar.next()
                    S.op("pool", lambda e: e.tensor_tensor(out=ya, in0=of[:], in1=za, op=ALU.mult), reads=[bof, bza], writes=[bya])
                    f, bf = fmo.next()
                    psb, bps, kb = yield from tr4(ya, bya)
                    S.op("act", lambda e: e.copy(out=f, in_=psb[:, 0:512]), reads=[bps], writes=[bf])
                    rel(kb)
                    S.dma("act", scr["Y_FM"][0].rearrange("(k p) t -> p k t", p=128)[:, :, tok], f.rearrange("p (k t) -> p k t", k=4),
                          reads=[bf], writes=[self.db("Y_FM0", cc)])
                    yield

        def chain(d):
            B = DB[d]
            s32, bs32 = B.S32
            S.op("pool", lambda e: e.memset(s32[:], 0.0), writes=[bs32])
            sbf, bsbf = B.sbf.next()
            S.op("pool", lambda e: e.memset(sbf, 0.0), writes=[bsbf])
            st = {"sbf": (sbf, bsbf)}
            order = orders[d]
            n = len(order)
            outs = [dict() for _ in range(n)]
            started = 0
            active = []
            done = set()

            def start_upto(j):
                nonlocal started
                while started <= min(j, n - 1):
                    active.append((started, prep(d, order[started], outs[started], started % KPRE)))
                    started += 1

            def step_preps():
                for item in list(active):
                    try:
                        next(item[1])
                    except StopIteration:
                        active.remove(item)
                        done.add(item[0])
            start_upto(0)
            while 0 not in done:
                step_preps()
                yield
            for i, cc in enumerate(order):
                start_upto(i + KPRE)
                sc = scan(d, cc, outs[i], st)
                sc_done = False
                while not sc_done or (i + 1 < n and (i + 1) not in done):
                    if not sc_done:
                        try:
                            next(sc)
                        except StopIteration:
                            sc_done = True
                    step_preps()
                    yield

        gens = [chain(0), chain(1)]
        while gens:
            for g_ in list(gens):
                try:
                    next(g_)
                except StopIteration:
                    gens.remove(g_)
        S.barrier()
        self.es.close()

    def phase5(self, l):
        S, scr, din = self.S, self.scr, self.din
        last = (l == self.nlayers - 1)
        self.es = ExitStack()
        wbr = self.R1[:, 14336:26624].rearrange("p (r n) -> p r n", n=1024)
        wo = self.R1[:, 26624:34816].rearrange("p (r n) -> p r n", n=1024)
        bwbr, bwo = Buf("wbr"), Buf("wo")

        def load_into(src_view, dst, bdst, nk):
            st, bst = self.wst.next()
            S.dma("sp", st[:, 0:nk, :], src_view, writes=[bst])
            S.op("pool", lambda e: e.tensor_copy(out=dst, in_=st[:, 0:nk, :]), reads=[bst], writes=[bdst])
        wbsrc = din["w_branch"][l].rearrange("b (k p) n -> p (b k) n", p=128)
        for half in range(2):
            for r0, nk in ((0, 8), (8, 4)):
                load_into(wbsrc[:, r0:r0 + nk, half * 512:(half + 1) * 512], wbr[:, r0:r0 + nk, half * 512:(half + 1) * 512], bwbr, nk)
        wosrc = din["w_out"][l].rearrange("(k p) n -> p k n", p=128)
        for half in range(2):
            load_into(wosrc[:, :, half * 512:(half + 1) * 512], wo[:, :, half * 512:(half + 1) * 512], bwo, 8)
        gate_bc, bgate = self.tsb("gate_bc", [128, 2, 1024])
        for s_ in range(2):
            if last and s_ == 1:
                continue
            self.bc_rows(lambda half: gate_bc[:, s_, half * 512:(half + 1) * 512], bgate, lambda kc: self.mod[:, 16 + kc, s_:s_ + 1], self.b_mod, 8)
        if last:
            fnw, bfnw = self.tsb("fnw_bc", [128, 1024])
            S.dma("sp", fnw[:], din["final_norm_w"].partition_broadcast(128), writes=[bfnw])
        yTr = self.trot("p5_yT", [128, 12, 512], BF16, 1)
        gmr = self.trot("p5_gm", [128, 512], BF16, 4)
        accr = self.trot("p5_acc", [128, 512], F32, 2)
        tmr = self.trot("p5_tm", [128, 512], F32, 2)
        mTr = self.trot("p5_mT", [128, 8, 512], BF16, 2)
        xtr = self.trot("p5_xt", [128, 1024], F32, 3)
        t1r = self.trot("p5_t1", [128, 1024], F32, 2)
        sqr, bsqr = self.tsb("p5_sq", [128, 1024])
        st5 = self.trot("p5_st", [128, 4], F32, 3)
        for (t0, n, tile0, ntile) in self.tok_groups():
            if last and t0 == 0:
                continue
            s_ = 1 if t0 == 0 else 0
            yT, byT = yTr.next()
            for br in range(3):
                S.dma("sp", yT[:, br * 4:(br + 1) * 4, 0:n], scr["Y_FM"][br].rearrange("(k p) t -> p k t", p=128)[:, :, t0:t0 + n],
                      reads=[self.db("Y_FM%d" % br, tile0 + i) for i in range(ntile)], writes=[byT.sub(br)])
            mT, bmT = mTr.next()
            pend = []

            def flush():
                while pend:
                    pend.pop(0)()
            acc_of = {}
            for dt in range(8):
                acc_of[dt] = accr.next()
                for br in range(3):
                    ct = br * 8 + dt
                    gm, bgm = gmr.next()
                    S.dma("sp", gm[:, 0:n], scr["GM_FM"][ct * 128:(ct + 1) * 128, t0:t0 + n],
                          reads=[self.db("GM_FM%d" % ct, tile0 + i) for i in range(ntile)], writes=[bgm])
                    ps, bps = S.ps()
                    for kc in range(4):
                        S.op("pe", lambda e: e.matmul(out=ps[:, 0:n], lhsT=wbr[:, br * 4 + kc, dt * 128:(dt + 1) * 128], rhs=yT[:, br * 4 + kc, 0:n],
                                                      start=(kc == 0), stop=(kc == 3)), reads=[bwbr, byT.sub(br)], writes=[bps])
                    flush()

                    def cons(dt=dt, br=br, ps=ps, bps=bps, gm=gm, bgm=bgm):
                        acc, bacc = acc_of[dt]
                        if br == 0:
                            S.op("dve", lambda e: e.tensor_tensor(out=acc[:, 0:n], in0=ps[:, 0:n], in1=gm[:, 0:n], op=ALU.mult), reads=[bps, bgm], writes=[bacc])
                        else:
                            tm, btm = tmr.next()
                            S.op("dve", lambda e: e.tensor_tensor(out=tm[:, 0:n], in0=ps[:, 0:n], in1=gm[:, 0:n], op=ALU.mult), reads=[bps, bgm], writes=[btm])
                            if br == 1:
                                S.op("pool", lambda e: e.tensor_tensor(out=acc[:, 0:n], in0=acc[:, 0:n], in1=tm[:, 0:n], op=ALU.add), reads=[bacc, btm], writes=[bacc])
                            else:
                                S.op("pool", lambda e: e.tensor_tensor(out=mT[:, dt, 0:n], in0=acc[:, 0:n], in1=tm[:, 0:n], op=ALU.add), reads=[bacc, btm], writes=[bmT.sub(dt)])
                    pend.append(cons)
            flush()
            for ti in range(ntile):
                tt = tile0 + ti
                xt, bxt = xtr.next()
                if tt < 2:
                    src = (din["ctx"] if l == 0 else scr["CTXS"])[tt * 128:(tt + 1) * 128, :]
                    rd = [] if l == 0 else [self.db("CTXS", tt)]
                else:
                    src = (din["x"] if l == 0 else scr["XS"])[(tt - 2) * 128:(tt - 1) * 128, :]
                    rd = [] if l == 0 else [self.db("XS", tt)]
                S.dma("sp", xt[:], src, reads=rd, writes=[bxt])
                pss = []
                for cg in range(2):
                    ps, bps = S.ps()
                    for kc in range(8):
                        S.op("pe", lambda e: e.matmul(out=ps[:], lhsT=mT[:, kc, ti * 128:(ti + 1) * 128], rhs=wo[:, kc, cg * 512:(cg + 1) * 512],
                                                      start=(kc == 0), stop=(kc == 7)), reads=[bmT, bwo], writes=[bps])
                    pss.append((ps, bps))
                flush()

                def cons2(tt=tt, xt=xt, bxt=bxt, pss=pss):
                    t1, bt1 = t1r.next()
                    for cg in range(2):
                        ps, bps = pss[cg]
                        S.op("dve", lambda e: e.tensor_tensor(out=t1[:, cg * 512:(cg + 1) * 512], in0=ps[:], in1=gate_bc[:, s_, cg * 512:(cg + 1) * 512], op=ALU.mult),
                             reads=[bps, bgate], writes=[bt1.sub(cg)])
                    S.op("pool", lambda e: e.tensor_tensor(out=t1[:], in0=t1[:], in1=xt[:], op=ALU.add), reads=[bt1, bxt], writes=[bt1])
                    if not last:
                        if tt < 2:
                            S.dma("act", scr["CTXS"][tt * 128:(tt + 1) * 128, :], t1[:], reads=[bt1], writes=[self.db("CTXS", tt)])
                        else:
                            S.dma("act", scr["XS"][(tt - 2) * 128:(tt - 1) * 128, :], t1[:], reads=[bt1], writes=[self.db("XS", tt)])
                    else:
                        st, bst = st5.next()
                        S.op("act", lambda e: e.activation(out=sqr[:], in_=t1[:], func=AF.Square, accum_out=st[:, 0:1]), reads=[bt1], writes=[bsqr, bst])
                        S.op("dve", lambda e: e.tensor_scalar(out=st[:, 1:2], in0=st[:, 0:1], scalar1=1.0 / D, scalar2=EPS, op0=ALU.mult, op1=ALU.add), reads=[bst], writes=[bst])
                        S.op("act", lambda e: e.activation(out=st[:, 2:3], in_=st[:, 1:2], func=AF.Ln), reads=[bst], writes=[bst])
                        S.op("act", lambda e: e.activation(out=st[:, 3:4], in_=st[:, 2:3], func=AF.Exp, scale=-0.5), reads=[bst], writes=[bst])
                        S.op("dve", lambda e: e.scalar_tensor_tensor(out=xt[:], in0=t1[:], scalar=st[:, 3:4], in1=fnw[:], op0=ALU.mult, op1=ALU.mult),
                             reads=[bt1, bst, bfnw, bxt], writes=[bxt])
                        S.dma("act", self.out[(tt - 2) * 128:(tt - 1) * 128, :], xt[:], reads=[bxt], writes=[self.db("OUT", tt)])
                pend.append(cons2)
            flush()
        S.barrier()
        self.es.close()

    def dump(self, name, ap, reads, shape, dtype=F32):
        o = self.nc.dram_tensor("dbg_" + name, shape, dtype, kind="ExternalOutput").ap()
        b = Buf("dbg_" + name)
        self.S.dma("sp", o, ap, reads=reads, writes=[b])
        self._dbgbufs.append(b)

    def dump_p2(self):
        self.dump("mod", self.mod[:], [self.b_mod], [128, 24, 2])
        self.dump("SCR", self.SCR[:], [self.b_SCR], [128, NT, 16])
        for n in self.asc:
            self.dump(n, self.asc[n][0][:], [self.asc[n][1]], [128, NT, 8])
        self.dump("SH2", self.SH2[:], [self.b_SH2], [128, NT, 8])
        self.dump("kmx", self.kmx[:], [self.b_kmx], [128, 4])
        self.dump("hT", self.hT, self.b_hT, [128, 8, NTOK], BF16)

    def program(self):
        S = self.S
        self._dbgbufs = []
        self.marks = []
        mark = lambda n: self.marks.append((n, {k: v.count for k, v in S.engs.items()}))
        for l in range(self.nlayers):
            mark("L%d start" % l)
            self.phase0(l)
            mark("L%d p0 done" % l)
            if self.stop == "p0":
                self.dump("mod", self.mod[:], [self.b_mod], [128, 24, 2])
                self.dump("Afm", self.Afm[:], [self.b_Afm], [128, 8, 2])
                self.dump("convw", self.convw[:], [self.b_convw], [128, 12, 5])
                self.dump("scol", self.scol[:], [self.b_scol], [128, 8, 2])
                break
            self.phase1(l)
            mark("L%d p1 done" % l)
            if self.stop == "p1":
                self.dump("hT", self.hT, self.b_hT, [128, 8, NTOK], BF16)
                break
            self.phase2(l)
            if self.stop is not None and self.stop.startswith("p2"):
                break
            mark("L%d p2 done" % l)
            if self.stop == "b":
                self.core_b(l)
                break
            self.core_bc(l)
            mark("L%d B done" % l)
            mark("L%d C done" % l)
            if self.stop == "c":
                break
            self.core_a(l)
            mark("L%d A done" % l)
            if self.stop == "a":
                break
            self.phase5(l)
            mark("L%d p5 done" % l)
            if self.stop == "p5":
                break
        S.barrier()
        return self.nc


def shard_inputs(inputs, b):
    m = {}
    for n in IN_SHAPES:
        a = np.asarray(inputs[n], dtype=np.float32)
        if n in ("x", "c", "ctx"):
            a = a[b]
        m[n] = np.ascontiguousarray(a)
    return m


_CACHE = {}


def kernel(**inputs):
    if "nc" not in _CACHE:
        _CACHE["nc"] = MK().program()
        _CACHE["consts"] = host_consts()
    nc = _CACHE["nc"]
    in_maps = []
    for b in range(8):
        m = shard_inputs(inputs, b)
        m.update(_CACHE["consts"])
        in_maps.append(m)
    res = run_bass_kernel_spmd(nc, in_maps, core_ids=list(range(8)))
    return np.stack([np.asarray(r["out"], dtype=np.float32) for r in res.results], axis=0)
```
